# Optimizing a Trainium2 kernel written in Bass

```python
import math
import jax, jax.numpy as jnp
from jax import lax
import numpy as np

D_MODEL = 1024
BATCH = 8
SEQ = 2048
DEPTH = 1
DEC_BATCH = 128
DEC_SEQ = 8
PAST_LEN = 16384
PAGE_SIZE = 128

M_HEADS = 4
M_HEAD_DIM = 128
M_WIDTH = M_HEADS * M_HEAD_DIM
M_CONV = 4
M_CHUNK = 64
R_HEADS = 8
R_HEAD_DIM = 64
R_WIDTH = R_HEADS * R_HEAD_DIM
R_DECAY_LORA = 64
R_A_LORA = 64
R_GATE_LORA = 128
R_COLS = 3 * R_WIDTH + R_DECAY_LORA + R_A_LORA + R_GATE_LORA
D_FF = 2816
F_CONV = 3
PLE_DIM = 256
EPS = 1e-6
GN_EPS = 64e-5

M_QK = 0
M_V = 2 * M_WIDTH
M_O = 3 * M_WIDTH
M_I = 4 * M_WIDTH
M_F = M_I + M_HEADS
M_END = M_F + M_HEADS
R_START = M_END
R_END = R_START + R_COLS
G_START = R_END
N_IN = G_START + 2 * D_MODEL
RC_R = 0
RC_K = R_WIDTH
RC_V = 2 * R_WIDTH
RC_W = 3 * R_WIDTH
RC_A = RC_W + R_DECAY_LORA
RC_G = RC_A + R_A_LORA

kernel_name = "mlstm_rwkv7_gated_hybrid_step"


def rmsnorm(x, g):
    xf = x.astype(jnp.float32)
    y = xf * lax.rsqrt(jnp.mean(xf * xf, -1, keepdims=True) + EPS)
    return (y * g.astype(jnp.float32)).astype(x.dtype)


def head_layernorm(h, eps):
    B, T, H, N = h.shape
    hf = h.astype(jnp.float32)
    mu = jnp.mean(hf, -1, keepdims=True)
    d = hf - mu
    y = d * lax.rsqrt(jnp.mean(d * d, -1, keepdims=True) + eps)
    return y.reshape(B, T, H * N)


def causal_dwconv(buf, u, w, b):
    W = w.shape[0]
    T = u.shape[1]
    full = jnp.concatenate([buf.astype(u.dtype), u], axis=1)
    y = b + w[W - 1] * full[:, W - 1:W - 1 + T]
    for j in range(W - 1):
        y = y + w[j] * full[:, j:j + T]
    return y, full[:, -(W - 1):]


def mlstm_chunked(q, k, v, li, lf, C0, n0, m0):
    B, T, H, dk = q.shape
    L = math.gcd(T, M_CHUNK)
    nc = T // L

    def to_chunks(a):
        a = a.reshape((B, nc, L) + a.shape[2:])
        return jnp.moveaxis(jnp.moveaxis(a, 3, 2), 1, 0)

    causal = jnp.tril(jnp.ones((L, L), bool))

    def step(carry, inp):
        C, n, m = carry
        qb, kb, vb, ib, fb = inp
        bcum = jnp.cumsum(fb, axis=-1)
        Dm = bcum[..., :, None] - bcum[..., None, :] + ib[..., None, :]
        Dm = jnp.where(causal, Dm, -jnp.inf)
        inter = bcum + m[..., None]
        mt = jnp.maximum(inter, jnp.max(Dm, -1))
        P = jnp.exp(Dm - mt[..., None]) * jnp.einsum('bhld,bhsd->bhls', qb, kb)
        sc = jnp.exp(inter - mt)
        num = sc[..., None] * jnp.einsum('bhld,bhde->bhle', qb, C) + jnp.einsum('bhls,bhse->bhle', P, vb)
        den = sc * jnp.einsum('bhld,bhd->bhl', qb, n) + jnp.sum(P, -1)
        h = num / jnp.maximum(jnp.abs(den), jnp.exp(-mt))[..., None]
        btot = bcum[..., -1]
        wlog = btot[..., None] - bcum + ib
        m_new = jnp.maximum(btot + m, jnp.max(wlog, -1))
        s0 = jnp.exp(btot + m - m_new)
        ws = jnp.exp(wlog - m_new[..., None])
        C_new = s0[..., None, None] * C + jnp.einsum('bhl,bhld,bhle->bhde', ws, kb, vb)
        n_new = s0[..., None] * n + jnp.einsum('bhl,bhld->bhd', ws, kb)
        return (C_new, n_new, m_new), h

    xs = (to_chunks(q), to_chunks(k), to_chunks(v), to_chunks(li), to_chunks(lf))
    (C, n, m), hs = lax.scan(step, (C0, n0, m0), xs)
    hs = jnp.moveaxis(jnp.moveaxis(hs, 0, 1), 2, 3).reshape(B, T, H, -1)
    return hs, C, n, m


def rwkv7_scan(r, w, k, v, a, b, S0):
    def step(S, inp):
        rt, wt, kt, vt, at, bt = inp
        sa = jnp.einsum('bhij,bhj->bhi', S, at)
        S = S * wt[:, :, None, :] + sa[..., None] * bt[:, :, None, :] + vt[..., None] * kt[:, :, None, :]
        return S, jnp.einsum('bhij,bhj->bhi', S, rt)
    xs = tuple(jnp.moveaxis(t, 1, 0) for t in (r, w, k, v, a, b))
    S, ys = lax.scan(step, S0, xs)
    return jnp.moveaxis(ys, 0, 1), S


def decoder_layer(x, pe, states, lp):
    mconv, mC, mn, mm, rshift, rS, fconv = states
    B, T, _ = x.shape
    f32 = jnp.float32
    h = rmsnorm(x, lp['norm1_g'])
    z = h @ lp['w_in']

    qk, mconv_new = causal_dwconv(mconv, z[..., M_QK:M_V], lp['m_conv_w'], lp['m_conv_b'])
    qk = jax.nn.silu(qk)
    q = qk[..., :M_WIDTH].reshape(B, T, M_HEADS, M_HEAD_DIM)
    k = qk[..., M_WIDTH:].reshape(B, T, M_HEADS, M_HEAD_DIM) * (M_HEAD_DIM ** -0.5)
    v = z[..., M_V:M_O].reshape(B, T, M_HEADS, M_HEAD_DIM)
    li = (z[..., M_I:M_F] + lp['m_i_bias']).astype(f32)
    lf = jax.nn.log_sigmoid((z[..., M_F:M_END] + lp['m_f_bias']).astype(f32))
    hm, mC_new, mn_new, mm_new = mlstm_chunked(q.astype(f32), k.astype(f32), v.astype(f32), li, lf,
                                               mC.astype(f32), mn.astype(f32), mm.astype(f32))
    hm = (head_layernorm(hm, EPS) * lp['m_norm_g']).astype(x.dtype)
    hm = jax.nn.sigmoid(z[..., M_O:M_I]) * hm
    y_a = hm @ lp['w_branch_m']

    zr = z[..., R_START:R_END]
    prev = jnp.concatenate([rshift[:, None].astype(zr.dtype), zr[:, :-1]], axis=1)
    xm = zr + (prev - zr) * lp['r_mix']
    rshift_new = zr[:, -1]
    r = xm[..., RC_R:RC_K]
    kr = xm[..., RC_K:RC_V]
    vr = xm[..., RC_V:RC_W]
    wd = xm[..., RC_W:RC_A]
    ad = xm[..., RC_A:RC_G]
    gd = xm[..., RC_G:]
    wlog = -jax.nn.softplus(-(lp['r_w0'] + jnp.tanh(wd) @ lp['r_w2'])) - 0.5
    decay = jnp.exp(-jnp.exp(wlog.astype(f32)))
    a = jax.nn.sigmoid(lp['r_a0'] + ad @ lp['r_a2'])
    g = jax.nn.sigmoid(gd) @ lp['r_g2']

    def heads(t):
        return t.reshape(B, T, R_HEADS, R_HEAD_DIM).astype(f32)

    kk = heads(kr * lp['r_kk'])
    kk = kk / jnp.maximum(jnp.sqrt(jnp.sum(kk * kk, -1, keepdims=True)), 1e-12)
    kr = kr * (1.0 + (a - 1.0) * lp['r_ka'])
    rh, kh, vh = heads(r), heads(kr), heads(vr)
    yr, rS_new = rwkv7_scan(rh, heads(decay), kh, vh, -kk, kk * heads(a), rS.astype(f32))
    yr = head_layernorm(yr, GN_EPS) * lp['r_ln_g'] + lp['r_ln_b']
    bonus = jnp.sum(rh * kh * lp['r_rk'], -1, keepdims=True) * vh
    yr = (yr + bonus.reshape(B, T, R_WIDTH)).astype(x.dtype) * g
    y_b = yr @ lp['w_branch_r']

    gates = jax.nn.sigmoid(z[..., G_START:])
    merged = gates[..., :D_MODEL] * y_a + gates[..., D_MODEL:] * y_b
    x = x + merged @ lp['w_out']

    h2 = rmsnorm(x, lp['norm2_g'])
    u = h2 @ lp['f_up']
    c, fconv_new = causal_dwconv(fconv, u[..., :D_FF], lp['f_conv_w'], lp['f_conv_b'])
    x = x + (jax.nn.gelu(c) * u[..., D_FF:]) @ lp['f_down']

    gate = jax.nn.sigmoid(rmsnorm(x, lp['ple_norm_g']) @ lp['ple_gate_w'])
    x = x + gate * (pe.astype(x.dtype) @ lp['ple_proj'])

    new_states = (mconv_new.astype(mconv.dtype), mC_new.astype(mC.dtype), mn_new.astype(mn.dtype),
                  mm_new.astype(mm.dtype), rshift_new.astype(rshift.dtype), rS_new.astype(rS.dtype),
                  fconv_new.astype(fconv.dtype))
    return x, new_states


def setup_inputs(seed: int = 0) -> dict:
    key = jax.random.key(seed)
    ks = iter(jax.random.split(key, 64))
    L = DEPTH

    def nrm(shape, scale):
        return jax.random.normal(next(ks), shape, jnp.float32) * scale

    def gain(shape):
        return 1.0 + nrm(shape, 0.02)

    def unif(shape, lo, hi):
        return jax.random.uniform(next(ks), shape, jnp.float32, lo, hi)

    return {
        'x_prompt': nrm((BATCH, SEQ, D_MODEL), 1.0),
        'x_sample': nrm((DEC_BATCH, DEC_SEQ, D_MODEL), 1.0),
        'state_mlstm_conv': nrm((L, DEC_BATCH, M_CONV - 1, 2 * M_WIDTH), 1.0),
        'state_mlstm_C': nrm((L, DEC_BATCH, M_HEADS, M_HEAD_DIM, M_HEAD_DIM), 0.05),
        'state_mlstm_n': nrm((L, DEC_BATCH, M_HEADS, M_HEAD_DIM), 0.05),
        'state_mlstm_m': 1.0 + nrm((L, DEC_BATCH, M_HEADS), 0.5),
        'state_rwkv_shift': nrm((L, DEC_BATCH, R_COLS), 1.0),
        'state_rwkv_S': nrm((L, DEC_BATCH, R_HEADS, R_HEAD_DIM, R_HEAD_DIM), 0.1),
        'state_ffn_conv': nrm((L, DEC_BATCH, F_CONV - 1, D_FF), 0.5),
        'p_prompt': nrm((L, BATCH, SEQ, PLE_DIM), 1.0),
        'p_sample': nrm((L, DEC_BATCH, DEC_SEQ, PLE_DIM), 1.0),
        'norm1_g': gain((L, D_MODEL)),
        'w_in': nrm((L, D_MODEL, N_IN), D_MODEL ** -0.5),
        'm_conv_w': nrm((L, M_CONV, 2 * M_WIDTH), 0.5),
        'm_conv_b': nrm((L, 2 * M_WIDTH), 0.01),
        'm_i_bias': nrm((L, M_HEADS), 0.1),
        'm_f_bias': jnp.linspace(3.0, 6.0, M_HEADS, dtype=jnp.float32) + nrm((L, M_HEADS), 0.1),
        'm_norm_g': gain((L, M_WIDTH)),
        'w_branch_m': nrm((L, M_WIDTH, D_MODEL), M_WIDTH ** -0.5),
        'r_mix': unif((L, R_COLS), 0.0, 1.0),
        'r_w0': unif((L, R_WIDTH), -2.0, 1.0),
        'r_w2': nrm((L, R_DECAY_LORA, R_WIDTH), 0.1),
        'r_a0': nrm((L, R_WIDTH), 0.1),
        'r_a2': nrm((L, R_A_LORA, R_WIDTH), 0.5 * R_A_LORA ** -0.5),
        'r_g2': nrm((L, R_GATE_LORA, R_WIDTH), R_GATE_LORA ** -0.5),
        'r_kk': 0.85 + nrm((L, R_WIDTH), 0.05),
        'r_ka': 1.0 + nrm((L, R_WIDTH), 0.05),
        'r_rk': nrm((L, R_HEADS, R_HEAD_DIM), 0.1),
        'r_ln_g': gain((L, R_WIDTH)),
        'r_ln_b': nrm((L, R_WIDTH), 0.01),
        'w_branch_r': nrm((L, R_WIDTH, D_MODEL), R_WIDTH ** -0.5),
        'w_out': nrm((L, D_MODEL, D_MODEL), D_MODEL ** -0.5),
        'norm2_g': gain((L, D_MODEL)),
        'f_up': nrm((L, D_MODEL, 2 * D_FF), D_MODEL ** -0.5),
        'f_conv_w': nrm((L, F_CONV, D_FF), 0.5),
        'f_conv_b': nrm((L, D_FF), 0.01),
        'f_down': nrm((L, D_FF, D_MODEL), D_FF ** -0.5),
        'ple_norm_g': gain((L, D_MODEL)),
        'ple_gate_w': nrm((L, D_MODEL, D_MODEL), D_MODEL ** -0.5),
        'ple_proj': nrm((L, PLE_DIM, D_MODEL), 0.5 * PLE_DIM ** -0.5),
        'final_norm_g': gain((D_MODEL,)),
    }


def reference(x_prompt, x_sample, state_mlstm_conv, state_mlstm_C, state_mlstm_n, state_mlstm_m,
              state_rwkv_shift, state_rwkv_S, state_ffn_conv, p_prompt, p_sample,
              norm1_g, w_in, m_conv_w, m_conv_b, m_i_bias, m_f_bias, m_norm_g, w_branch_m,
              r_mix, r_w0, r_w2, r_a0, r_a2, r_g2, r_kk, r_ka, r_rk, r_ln_g, r_ln_b, w_branch_r,
              w_out, norm2_g, f_up, f_conv_w, f_conv_b, f_down, ple_norm_g, ple_gate_w, ple_proj,
              final_norm_g):
    y_p, y_s = x_prompt, x_sample
    Bp = x_prompt.shape[0]
    dt = x_prompt.dtype
    new_p = [[] for _ in range(7)]
    new_s = [[] for _ in range(7)]
    for i in range(DEPTH):
        lp = dict(norm1_g=norm1_g[i], w_in=w_in[i], m_conv_w=m_conv_w[i], m_conv_b=m_conv_b[i],
                  m_i_bias=m_i_bias[i], m_f_bias=m_f_bias[i], m_norm_g=m_norm_g[i],
                  w_branch_m=w_branch_m[i], r_mix=r_mix[i], r_w0=r_w0[i], r_w2=r_w2[i],
                  r_a0=r_a0[i], r_a2=r_a2[i], r_g2=r_g2[i], r_kk=r_kk[i], r_ka=r_ka[i],
                  r_rk=r_rk[i], r_ln_g=r_ln_g[i], r_ln_b=r_ln_b[i], w_branch_r=w_branch_r[i],
                  w_out=w_out[i], norm2_g=norm2_g[i], f_up=f_up[i], f_conv_w=f_conv_w[i],
                  f_conv_b=f_conv_b[i], f_down=f_down[i], ple_norm_g=ple_norm_g[i],
                  ple_gate_w=ple_gate_w[i], ple_proj=ple_proj[i])
        init_p = (jnp.zeros((Bp, M_CONV - 1, 2 * M_WIDTH), dt),
                  jnp.zeros((Bp, M_HEADS, M_HEAD_DIM, M_HEAD_DIM), dt),
                  jnp.zeros((Bp, M_HEADS, M_HEAD_DIM), dt),
                  jnp.zeros((Bp, M_HEADS), dt),
                  jnp.zeros((Bp, R_COLS), dt),
                  jnp.zeros((Bp, R_HEADS, R_HEAD_DIM, R_HEAD_DIM), dt),
                  jnp.zeros((Bp, F_CONV - 1, D_FF), dt))
        y_p, st_p = decoder_layer(y_p, p_prompt[i], init_p, lp)
        past_s = (state_mlstm_conv[i], state_mlstm_C[i], state_mlstm_n[i], state_mlstm_m[i],
                  state_rwkv_shift[i], state_rwkv_S[i], state_ffn_conv[i])
        y_s, st_s = decoder_layer(y_s, p_sample[i], past_s, lp)
        for j in range(7):
            new_p[j].append(st_p[j])
            new_s[j].append(st_s[j])
    y_prompt = rmsnorm(y_p, final_norm_g)
    y_sample = rmsnorm(y_s, final_norm_g)
    p_conv, p_C, p_n, p_m, p_shift, p_S, p_fconv = [jnp.stack(t, 0) for t in new_p]
    s_conv, s_C, s_n, s_m, s_shift, s_S, s_fconv = [jnp.stack(t, 0) for t in new_s]
    return (y_prompt, y_sample, p_conv, p_C, p_n, p_m, p_shift, p_S, p_fconv,
            s_conv, s_C, s_n, s_m, s_shift, s_S, s_fconv)
```

```python
import math
from contextlib import ExitStack

import numpy as np
import concourse.bass as bass
import concourse.mybir as mybir
from concourse.bass_utils import run_bass_kernel_spmd

F32 = mybir.dt.float32
BF16 = mybir.dt.bfloat16
AF = mybir.ActivationFunctionType
ALU = mybir.AluOpType
AX = mybir.AxisListType

ENGS = ("pe", "act", "dve", "pool", "sp")
N_DMA_SEMS = 8
SAME_ENG_DIST = 2

D = 1024
SEQ = 2048
NCORES = 8
NTP = SEQ // 128
NSB = 16
ST = 8
MW = 512
MH = 4
RW = 512
RH = 8
RN = 64
RCOLS = 1792
DFF = 2816
NFC = DFF // 128
PLE = 256
N_IN = 5896
C_QK, C_V, C_O, C_I, C_F, C_R, C_G = 0, 1024, 1536, 2048, 2052, 2056, 3848
EPS = 1e-6
GN_EPS = 64e-5
KSCALE = 128 ** -0.5
WSCALE = -math.exp(-0.5)


class _Trk:
    __slots__ = ("w", "r")

    def __init__(self):
        self.w = None
        self.r = []


class Buf:
    def __init__(self, t, name):
        self.t = t
        self.name = name
        self.whole = _Trk()
        self.subs = {}

    def __getitem__(self, idx):
        return self.t[idx]

    def view(self, ap, name=None):
        b = Buf(ap, name or self.name + "_v")
        b.whole = self.whole
        b.subs = self.subs
        return b


class _Op:
    __slots__ = ("eng", "fn", "deps", "needs_inc", "is_dma", "sem", "val", "pos", "force")


class K:
    def __init__(self, nc):
        self.nc = nc
        self.es = ExitStack()
        self.streams = {e: [] for e in ENGS}
        self.dma_rr = {e: 0 for e in ENGS}
        self.dma_last = {}
        self.nbuf = 0
        self.ops = []

    def _init_arena(self):
        nbytes = (int(self.nc.sbuf_bytes_remaining) - 512) // 64 * 64
        self.arena_bytes = nbytes
        self.arena = self.es.enter_context(self.nc.sbuf_tensor("arena", [128, nbytes // 2], BF16))
        self.bot = 0
        self.top = nbytes
        self.hiwater = 0

    def _view(self, off, shape, dtype):
        n = 1
        for d in shape[1:]:
            n *= d
        esz = 4 if dtype == F32 else 2
        v = self.arena[:, off // 2:(off + n * esz) // 2]
        if dtype == F32:
            v = v.bitcast(F32)
        if len(shape) > 2:
            names = " ".join(f"d{i}" for i in range(len(shape) - 1))
            v = v.rearrange(f"p ({names}) -> p {names}", **{f"d{i}": shape[i + 1] for i in range(len(shape) - 1)})
        if shape[0] < 128:
            v = v[0:shape[0]]
        return v, n * esz

    def sbuf(self, shape, dtype, name=None, top=False):
        if not hasattr(self, "arena"):
            self._init_arena()
        self.nbuf += 1
        name = name or f"sb{self.nbuf}"
        n = 1
        for d in shape[1:]:
            n *= d
        nb = (n * (4 if dtype == F32 else 2) + 63) // 64 * 64
        if top:
            self.top -= nb
            off = self.top
        else:
            off = self.bot
            self.bot += nb
        assert self.bot <= self.top, f"SBUF arena overflow allocating {name}: bot={self.bot} top={self.top}"
        self.hiwater = max(self.hiwater, self.bot + (self.arena_bytes - self.top))
        v, _ = self._view(off, list(shape), dtype)
        return Buf(v, name)

    def pe_fence(self):
        st = self.streams["pe"]
        if not st:
            return
        last = st[-1]
        o = self.op("pe", lambda h: h.nop(), (), ())
        o.deps.add(last)
        o.force = {last}
        if getattr(self, "fence_mm", None) is not None:
            fb, fi = self.fence_mm
            self.tr((fb, fb[:, 0:128]), (fi, fi[:]), (fi, fi[:]))
            last = self.streams["pe"][-1]
            o = self.op("pe", lambda h: h.nop(), (), ())
            o.deps.add(last)
            o.force = {last}

    def barrier(self):
        lasts = [st[-1] for st in self.streams.values() if st]
        lasts += list(self.dma_last.values())
        for e in ENGS:
            o = self.op(e, lambda h: h.nop(), (), ())
            o.deps.update(x for x in lasts if x is not o)

    def psum(self, shape, dtype, name=None):
        self.nbuf += 1
        name = "ps_" + (name or f"{self.nbuf}")
        t = self.es.enter_context(self.nc.psum_tensor(name, list(shape), dtype))
        return Buf(t, name)

    def dram(self, name, shape, dtype, kind="Internal"):
        t = self.nc.dram_tensor(name, list(shape), dtype, kind=kind)
        return Buf(t.ap(), name)

    def _touch(self, op, item, is_write):
        if isinstance(item, tuple):
            buf, key = item
        else:
            buf, key = item, None
        if key is None:
            trks = [buf.whole] + list(buf.subs.values())
        else:
            if key not in buf.subs:
                buf.subs[key] = _Trk()
            trks = [buf.whole, buf.subs[key]]
        for t in trks:
            if t.w is not None:
                op.deps.add(t.w)
            if is_write:
                op.deps.update(t.r)
        return buf, key

    def _commit(self, op, buf, key, is_write):
        if key is None:
            if is_write:
                buf.whole.w = op
                buf.whole.r = []
                buf.subs.clear()
            else:
                self._add_reader(buf.whole, op)
        else:
            t = buf.subs[key]
            if is_write:
                t.w = op
                t.r = []
            else:
                self._add_reader(t, op)

    @staticmethod
    def _add_reader(t, op):
        if not op.is_dma:
            t.r = [o for o in t.r if o.is_dma or o.eng != op.eng]
        t.r.append(op)

    def op(self, eng, fn, reads=(), writes=(), dma=False):
        o = _Op()
        o.eng = eng
        o.fn = fn
        o.deps = set()
        o.needs_inc = False
        o.is_dma = dma
        o.sem = None
        o.val = None
        o.force = None
        touched = []
        for it in reads:
            touched.append(self._touch(o, it, False) + (False,))
        for it in writes:
            touched.append(self._touch(o, it, True) + (True,))
        o.deps.discard(o)
        for buf, key, w in touched:
            self._commit(o, buf, key, w)
        if dma:
            kk = (eng, self.dma_rr[eng] % N_DMA_SEMS)
            self.dma_rr[eng] += 1
            prev = self.dma_last.get(kk)
            if prev is not None:
                o.deps.add(prev)
            self.dma_last[kk] = o
            o.sem = kk
            o.needs_inc = True
        o.pos = len(self.streams[eng])
        self.streams[eng].append(o)
        self.ops.append(o)
        return o

    def dma(self, eng, out, in_, reads=(), writes=(), **kw):
        return self.op(eng, lambda e: e.dma_start(out=out, in_=in_, **kw), reads, writes, dma=True)

    def emit(self):
        nc = self.nc
        for o in self.ops:
            real = []
            for d in o.deps:
                if (not d.is_dma) and (not o.is_dma) and d.eng == o.eng and o.eng == "pe":
                    if not (o.force and d in o.force):
                        continue
                d.needs_inc = True
                real.append(d)
            o.deps = real
        for e in ENGS:
            cs = [o for o in self.streams[e] if not o.is_dma]
            if cs:
                cs[-1].needs_inc = True
        es = self.es
        esem = {e: es.enter_context(nc.semaphore(f"s_{e}")) for e in ENGS}
        dsem = {}
        for e in ENGS:
            for i in range(min(N_DMA_SEMS, self.dma_rr[e])):
                dsem[(e, i)] = es.enter_context(nc.semaphore(f"d_{e}{i}"))
        dcount = {kk: 0 for kk in dsem}
        for e in ENGS:
            c = 0
            for o in self.streams[e]:
                if o.is_dma:
                    dcount[o.sem] += 16
                    o.val = dcount[o.sem]
                    o.sem = dsem[o.sem]
                elif o.needs_inc:
                    c += 1
                    o.val = c
                    o.sem = esem[e]
        final_waits = [(s, dcount[kk]) for kk, s in dsem.items() if dcount[kk] > 0]
        for e in ENGS:
            if e == "sp":
                continue
            cs = [o for o in self.streams[e] if not o.is_dma and o.needs_inc]
            if cs:
                final_waits.append((esem[e], cs[-1].val))
        streams = self.streams
        nwaits = [0]

        def run(e, handle):
            waited = {}
            for o in streams[e]:
                need = {}
                for d in o.deps:
                    if need.get(d.sem, (None, 0))[1] < d.val:
                        need[d.sem] = (d.sem, d.val)
                for s, v in need.values():
                    if waited.get(s, 0) >= v:
                        continue
                    handle.wait_ge(s, v)
                    nwaits[0] += 1
                    waited[s] = v
                ins = o.fn(handle)
                if o.is_dma:
                    ins.then_inc(o.sem, 16)
                elif o.needs_inc:
                    ins.then_inc(o.sem, 1)
            if e == "sp":
                for s, v in final_waits:
                    handle.wait_ge(s, v)

        with nc.Block() as block:
            @block.tensor
            def _(h):
                run("pe", h)

            @block.scalar
            def _(h):
                run("act", h)

            @block.vector
            def _(h):
                run("dve", h)

            @block.gpsimd
            def _(h):
                run("pool", h)

            @block.sync
            def _(h):
                run("sp", h)
        self.stats = dict(n_ops={e: len(streams[e]) for e in ENGS}, n_waits=nwaits[0])
        self.es.close()

    @staticmethod
    def _it(x):
        return (x[0], x[2]) if len(x) > 2 else x[0]

    def mm(self, out, lhsT, rhs, start=True, stop=True):
        return self.op("pe", lambda e: e.matmul(out[1], lhsT=lhsT[1], rhs=rhs[1], start=start, stop=stop),
                       reads=[self._it(lhsT), self._it(rhs)], writes=[self._it(out)])

    def tr(self, out, in_, ident):
        return self.op("pe", lambda e: e.transpose(out[1], in_[1], ident[1]),
                       reads=[self._it(in_), self._it(ident)], writes=[self._it(out)])

    def act(self, out, in_, func, bias=None, scale=None, accum=None, eng="act"):
        reads = [self._it(in_)]
        kw = {}
        if bias is not None:
            if isinstance(bias, tuple):
                reads.append(self._it(bias))
                kw["bias"] = bias[1]
            else:
                kw["bias"] = bias
        if scale is not None:
            if isinstance(scale, tuple):
                reads.append(self._it(scale))
                kw["scale"] = scale[1]
            else:
                kw["scale"] = scale
        writes = [self._it(out)]
        if accum is not None:
            writes.append(self._it(accum))
            kw["accum_out"] = accum[1]
        return self.op(eng, lambda e: e.activation(out=out[1], in_=in_[1], func=func, **kw), reads, writes)

    def tt(self, eng, out, in0, in1, op):
        return self.op(eng, lambda e: e.tensor_tensor(out=out[1], in0=in0[1], in1=in1[1], op=op),
                       reads=[self._it(in0), self._it(in1)], writes=[self._it(out)])

    def ts(self, eng, out, in0, s1, s2=None, op0=ALU.mult, op1=None, accum=None):
        reads = [self._it(in0)]
        a1 = s1
        a2 = s2
        if isinstance(s1, tuple):
            reads.append(self._it(s1))
            a1 = s1[1]
        if isinstance(s2, tuple):
            reads.append(self._it(s2))
            a2 = s2[1]
        kw = {}
        if op1 is not None:
            kw["op1"] = op1
        writes = [self._it(out)]
        if accum is not None:
            writes.append(self._it(accum))
            kw["accum_out"] = accum[1]
        return self.op(eng, lambda e: e.tensor_scalar(out=out[1], in0=in0[1], scalar1=a1, scalar2=a2, op0=op0, **kw),
                       reads, writes)

    def stt(self, eng, out, in0, scalar, in1, op0, op1):
        reads = [self._it(in0), self._it(in1)]
        a = scalar
        if isinstance(scalar, tuple):
            reads.append(self._it(scalar))
            a = scalar[1]
        return self.op(eng, lambda e: e.scalar_tensor_tensor(out=out[1], in0=in0[1], scalar=a, in1=in1[1], op0=op0, op1=op1),
                       reads, [self._it(out)])

    def copy(self, eng, out, in_):
        if eng == "act":
            return self.op(eng, lambda e: e.activation(out=out[1], in_=in_[1], func=AF.Copy),
                           reads=[self._it(in_)], writes=[self._it(out)])
        return self.op(eng, lambda e: e.tensor_copy(out=out[1], in_=in_[1]),
                       reads=[self._it(in_)], writes=[self._it(out)])

    def red(self, eng, out, in_, op, axis=AX.X):
        return self.op(eng, lambda e: e.tensor_reduce(out=out[1], in_=in_[1], axis=axis, op=op),
                       reads=[self._it(in_)], writes=[self._it(out)])

    def memset(self, eng, out, val):
        return self.op(eng, lambda e: e.memset(out[1], val), reads=[], writes=[self._it(out)])

    def scan(self, eng, out, d0, d1, init, op0, op1):
        return self.op(eng, lambda e: e.tensor_tensor_scan(out=out[1], data0=d0[1], data1=d1[1], initial=init, op0=op0, op1=op1),
                       reads=[self._it(d0), self._it(d1)], writes=[self._it(out)])


def build_program(stage=99, dbg=False):
    import os as _os
    nc = bass.Bass("TRN2", target_bir_lowering=False)
    k = K(nc)
    NT = NTP + 1

    def din(name, shape):
        return k.dram(name, shape, F32, "ExternalInput")

    def dout(name, shape):
        return k.dram(name, shape, F32, "ExternalOutput")

    xp = din("xp", [SEQ, D]); xs = din("xs", [128, D])
    pp = din("pp", [SEQ, PLE]); psm = din("psm", [128, PLE])
    st_mconv = din("st_mconv", [NSB * 3, 2 * MW])
    st_mC = din("st_mC", [NSB, MH, 128, 128])
    st_mn = din("st_mn", [NSB, MH, 128])
    st_mm = din("st_mm", [NSB, MH])
    st_rshift = din("st_rshift", [NSB, RCOLS])
    st_rS = din("st_rS", [NSB * RH, RN * RN])
    st_fconv = din("st_fconv", [NSB * 2, DFF])
    d_w_in = din("w_in", [128, 8, N_IN])
    d_w_bm = din("w_bm", [128, 4, D]); d_w_br = din("w_br", [128, 4, D])
    d_w_out = din("w_out", [128, 8, D])
    d_f_up = din("f_up", [128, 8, 2 * DFF]); d_f_down = din("f_down", [128, NFC, D])
    d_pgw = din("ple_gate_w", [128, 8, D]); d_ppj = din("ple_proj", [128, 2, D])
    d_rw2 = din("r_w2", [64, RW]); d_ra2 = din("r_a2", [128, RW]); d_rg2 = din("r_g2", [128, RW])
    d_g1 = din("norm1_g", [128, D]); d_g2 = din("norm2_g", [128, D])
    d_g3 = din("ple_norm_g", [128, D]); d_g4 = din("final_norm_g", [128, D])
    d_ptab = din("ptab", [128, 128])
    d_fctab = din("fctab", [128, 4 * NFC])
    d_gb = din("gate_bias", [4, 2])
    y_p = dout("y_p", [SEQ, D]); y_s = dout("y_s", [128, D])
    o_pconv = dout("p_conv", [3, 2 * MW]); o_pC = dout("p_C", [MH, 128, 128]); o_pn = dout("p_n", [MH, 128])
    o_pm = dout("p_m", [1, MH]); o_pshift = dout("p_shift", [1, RCOLS]); o_pS = dout("p_S", [RH * RN, RN])
    o_pfconv = dout("p_fconv", [2, DFF])
    o_sconv = dout("s_conv", [NSB * 3, 2 * MW]); o_sC = dout("s_C", [NSB, MH, 128, 128]); o_sn = dout("s_n", [NSB, MH, 128])
    o_sm = dout("s_m", [NSB, MH]); o_sshift = dout("s_shift", [NSB, RCOLS]); o_sS = dout("s_S", [NSB * RH, RN * RN])
    o_sfconv = dout("s_fconv", [NSB * 2, DFF])
    x1s = k.dram("x1_scratch", [NT * 128, D], F32)
    dbgs = {}

    def dbg_out(name, src_buf, src_ap, shape):
        if not dbg:
            return
        t = dout("dbg_" + name, shape)
        dbgs[name] = t
        k.dma("sp", t[:], src_ap, reads=[src_buf], writes=[t])

    identf = k.sbuf([128, 128], F32, "identf")
    identb = k.sbuf([128, 128], BF16, "identb")
    mark_phase = k.bot
    mU_in = [k.sbuf([128, 128], F32, f"mUin{i}") for i in range(2)]
    mU_st = [k.sbuf([128, 128], F32, f"mUst{i}") for i in range(2)]
    mL_st = [k.sbuf([128, 128], F32, f"mLst{i}") for i in range(2)]
    resets = [k.sbuf([128, 512], F32, f"resets{i}") for i in range(2)]
    ones4 = k.sbuf([4, 128], F32, "ones4")

    def aff(out_buf, out_ap, pattern, cm, base, op=ALU.is_ge):
        k.op("pool", lambda e: e.affine_select(out=out_ap, in_=out_ap, pattern=pattern, compare_op=op,
                                               fill=0.0, base=base, channel_multiplier=cm),
             reads=[out_buf], writes=[out_buf])

    k.memset("pool", (identf, identf[:]), 1.0)
    aff(identf, identf[:], [[-1, 128]], 1, 0)
    aff(identf, identf[:], [[1, 128]], -1, 0)
    k.copy("pool", (identb, identb[:]), (identf, identf[:]))
    for i in range(2):
        k.memset("pool", (mU_in[i], mU_in[i][:]), 1.0)
        aff(mU_in[i], mU_in[i][:], [[1, 128]], -1, 0)
        k.memset("pool", (mU_st[i], mU_st[i][:]), 1.0)
        aff(mU_st[i], mU_st[i][:], [[1, 128]], -1, -1)
        k.memset("pool", (mL_st[i], mL_st[i][:]), 1.0)
        aff(mL_st[i], mL_st[i][:], [[-1, 128]], 1, -1)
        k.memset("pool", (resets[i], resets[i][:]), 1.0)
    v3 = lambda b: b[:].rearrange("p (a c) -> p a c", c=ST)
    aff(mU_in[1], v3(mU_in[1]), [[-ST, 16], [0, ST]], 1, 0)
    aff(mU_st[1], v3(mU_st[1]), [[-ST, 16], [0, ST]], 1, 0)
    aff(mL_st[1], v3(mL_st[1]), [[ST, 16], [0, ST]], -1, ST - 1)
    k.memset("pool", (resets[0], resets[0][:].rearrange("p (a c) -> p a c", c=128)[:, :, 0:1]), 0.0)
    k.memset("pool", (resets[1], resets[1][:].rearrange("p (a c) -> p a c", c=ST)[:, :, 0:1]), 0.0)
    k.memset("pool", (ones4, ones4[:]), 1.0)
    mask2 = [k.sbuf([128, 256], F32, f"mask2_{i}") for i in range(2)]
    for i in range(2):
        k.copy("pool", (mask2[i], mask2[i][:, 0:128]), (mU_st[i], mU_st[i][:]))
        k.copy("pool", (mask2[i], mask2[i][:, 128:256]), (mU_in[i], mU_in[i][:]))
    I2 = k.sbuf([128, 64], F32, "I2")
    k.tt("pool", (I2, I2[:]), (identf, identf[:, 0:64]), (identf, identf[:, 64:128]), ALU.add)
    bones = k.sbuf([128, 128], F32, "bones")
    k.memset("pool", (bones, bones[:]), 0.0)
    k.memset("pool", (bones, bones[0:64, 0:64]), 1.0)
    k.memset("pool", (bones, bones[64:128, 64:128]), 1.0)

    ptab = k.sbuf([128, 128], F32, "ptab")
    k.dma("sp", ptab[:], d_ptab[:], reads=[d_ptab], writes=[ptab])
    PT_MCW, PT_MCB, PT_RMIX, PT_RW0, PT_RA0, PT_RKK, PT_RKA, PT_RRK, PT_RLNG, PT_RLNB, PT_MNG = 0, 32, 40, 54, 58, 62, 66, 70, 74, 78, 82
    pcol = lambda c: (ptab, ptab[:, c:c + 1])
    gb = k.sbuf([4, 2], F32, "gb")
    k.dma("sp", gb[:], d_gb[:], reads=[d_gb], writes=[gb])
    nbf = k.sbuf([4, 1], F32, "nbf")
    k.ts("dve", (nbf, nbf[:]), (gb, gb[:, 1:2]), -1.0, None, op0=ALU.mult)
    g1bc = k.sbuf([128, D], F32, "g1bc")
    k.dma("sp", g1bc[:], d_g1[:], reads=[d_g1], writes=[g1bc])

    NA = C_G
    hmT_all = k.sbuf([128, NT, 4, 128], BF16, "hmT_all")
    yrgT_all = k.sbuf([128, NT, 4, 128], BF16, "yrgT_all")
    mark_1a = k.bot
    W_in = k.sbuf([128, 8, NA], BF16, "W_in")
    Wl_w2 = k.sbuf([64, RW], BF16, "Wl_w2")
    Wl_a2 = k.sbuf([128, RW], BF16, "Wl_a2")
    Wl_g2 = k.sbuf([128, RW], BF16, "Wl_g2")
    GRP = {"g0": (0, 1024), "g1": (1024, 2056), "g2": (2056, 3848)}
    for g in ("g0", "g1", "g2"):
        a, b = GRP[g]
        for kh in range(2):
            k.dma("pool", W_in[:, 4 * kh:4 * kh + 4, a:b], d_w_in[:, 4 * kh:4 * kh + 4, a:b], reads=[d_w_in], writes=[(W_in, g)])
        if g == "g1":
            k.dma("pool", Wl_w2[:], d_rw2[:], reads=[d_rw2], writes=[Wl_w2])
            k.dma("pool", Wl_a2[:], d_ra2[:], reads=[d_ra2], writes=[Wl_a2])
            k.dma("pool", Wl_g2[:], d_rg2[:], reads=[d_rg2], writes=[Wl_g2])

    def wgrp(col):
        for g, (a, b) in GRP.items():
            if a <= col < b:
                return g

    pF = [k.psum([128, 512], F32, f"pF{i}") for i in range(2)]
    pR = [k.psum([128, 512], F32, f"pR{i}") for i in range(2)]
    pT = k.psum([128, 1024], BF16, "pT")
    pM = [k.psum([128, 512], F32, f"pM{i}") for i in range(3)]

    xt = [k.sbuf([128, D], F32, "xt0")] * 2
    hb = k.sbuf([128, D], BF16, "hb")
    hT = k.sbuf([128, 8, 128], BF16, "hT")
    ss = k.sbuf([128, 1], F32, "ss")
    rs = k.sbuf([128, 1], F32, "rs")
    ext_q = k.sbuf([128, 8, 131], F32, "ext_q")
    cq = k.sbuf([128, 8, 3], F32, "cq")
    _eqf = ext_q[:].rearrange("p a b -> p (a b)")
    cv = k.sbuf([128, 8, 128], F32, "cv")
    qkT = k.sbuf([128, 8, 128], BF16, "qkT")
    soT = k.sbuf([128, 4, 128], F32, "soT")
    vaug = k.sbuf([128, 4, 130], BF16, "vaug")
    Cst = k.sbuf([128, 4, 129], F32, "Cst")
    Cb = k.sbuf([128, 4, 130], BF16, "Cb")
    gsm = [k.sbuf([4, 128], F32, f"gsm{i}") for i in range(8)]
    gpk = k.sbuf([4, 3, 128], F32, "gpk")
    mst = k.sbuf([4, 16], F32, "mst")
    mnew = k.sbuf([4, 16], F32, "mnew")
    gt = [k.sbuf([4, 16], F32, f"gt{i}") for i in range(4)]
    s0d = k.sbuf([4, 4, 16], F32, "s0d")
    tokS = k.sbuf([128, 12], F32, "tokS")
    s0bc = k.sbuf([128, 64], F32, "s0bc")
    _pk = _eqf[:, 512:1024].bitcast(BF16).rearrange("p (a b c) -> p a b c", a=2, b=4)
    PTm = ext_q.view(_pk[:, 0, :, :], "PTm")
    ktm = ext_q.view(_pk[:, 1, :, :], "ktm")
    dn = k.sbuf([128, 4], F32, "dn")
    hm = ext_q.view(_eqf[:, 0:512].rearrange("p (a b) -> p a b", a=4), "hm")
    hn = hm
    bst = k.sbuf([128, 4, 6], F32, "bst")
    bag = k.sbuf([128, 4, 2], F32, "bag")
    zq_tm = cv.view(cv[:].rearrange("p a b -> p (a b)"), "zq_tm")

    ext_r = k.sbuf([128, 14, 129], F32, "ext_r")
    cr = k.sbuf([128, 14, 1], F32, "cr")
    _erf = ext_r[:].rearrange("p a b -> p (a b)")
    xm = k.sbuf([128, 14, 128], F32, "xm")
    thad = k.sbuf([128, 128], BF16, "thad")
    sgd = k.sbuf([128, 128], BF16, "sgd")
    bst8 = k.sbuf([128, 8, 6], F32, "bst8")
    bag8 = k.sbuf([128, 8, 2], F32, "bag8")
    mark_rw = k.bot
    rt = [k.sbuf([128, 4, 128], F32, f"rt{i}") for i in range(7)]
    rt.append(cv.view(cv[:, 0:4, :], "rt7"))
    rt.append(cv.view(cv[:, 4:8, :], "rt8"))
    gTs = ext_r.view(_erf[:, 0:512].rearrange("p (a b) -> p a b", a=4), "gTs")
    bonT = ext_r.view(_erf[:, 512:1024].rearrange("p (a b) -> p a b", a=4), "bonT")
    ART = k.sbuf([128, 4, 2, 128], BF16, "ART")
    BTb = k.sbuf([128, 4, 128], BF16, "BTb")
    KTb = k.sbuf([128, 4, 128], BF16, "KTb")
    VTb = k.sbuf([128, 4, 128], BF16, "VTb")
    AB_tm = k.sbuf([128, 2, 512], BF16, "AB_tm")
    KV_tm = k.sbuf([128, 2, 512], BF16, "KV_tm")
    GBm = k.sbuf([128, 4, 256], BF16, "GBm")
    GKm = k.sbuf([128, 4, 256], BF16, "GKm")
    Nn = k.sbuf([128, 4, 128], BF16, "Nn")
    GBm_b = k.sbuf([128, 4, 256], BF16, "GBm_b")
    GKm_b = k.sbuf([128, 4, 256], BF16, "GKm_b")
    Nn_b = k.sbuf([128, 4, 128], BF16, "Nn_b")
    PP_b = [k.sbuf([128, 4, 256], BF16, f"PPb{i}") for i in range(2)]
    XX_b = [k.sbuf([128, 4, 128], BF16, f"XXb{i}") for i in range(2)]
    PP = [k.sbuf([128, 4, 256], BF16, f"PP{i}") for i in range(2)]
    XX = [k.sbuf([128, 4, 128], BF16, f"XX{i}") for i in range(2)]
    QT = k.sbuf([128, 2, 128], BF16, "QT")
    IE = k.sbuf([128, 4, 64], F32, "IE")
    STf = k.sbuf([128, 4, 64], F32, "STf")
    STb = k.sbuf([128, 4, 64], BF16, "STb")
    yn = ext_r.view(_erf[:, 1024:1536].rearrange("p (a b) -> p a b", a=8), "yn")
    k.memset("pool", (STf, STf[:]), 0.0)
    k.memset("pool", (STb, STb[:]), 0.0)
    k.memset("pool", (cr, cr[:]), 0.0)
    k.memset("pool", (vaug, vaug[:]), 1.0)
    k.memset("pool", (Cst, Cst[:]), 0.0)
    k.memset("pool", (Cb, Cb[:]), 0.0)
    k.memset("pool", (mst, mst[:]), 0.0)
    k.memset("pool", (cq, cq[:]), 0.0)
    LNK = math.log(KSCALE)

    def x_rows(ti):
        if ti < NTP:
            return xp, xp[ti * 128:(ti + 1) * 128, :]
        return xs, xs[:, :]

    def norm_to_hT(xbuf, gbc):
        k.act((hb, hb[:]), (xbuf, xbuf[:]), AF.Square, accum=(ss, ss[:]))
        k.ts("dve", (rs, rs[:]), (ss, ss[:]), 1.0 / D, EPS, op0=ALU.mult, op1=ALU.add)
        k.act((rs, rs[:]), (rs, rs[:]), AF.Ln)
        k.act((rs, rs[:]), (rs, rs[:]), AF.Exp, scale=-0.5)
        k.stt("dve", (hb, hb[:]), (xbuf, xbuf[:]), (rs, rs[:, 0:1]), (gbc, gbc[:]), ALU.mult, ALU.mult)
        for kc in range(8):
            k.tr((pT, pT[:, kc * 128:(kc + 1) * 128]), (hb, hb[:, kc * 128:(kc + 1) * 128]), (identb, identb[:]))
        k.copy("act", (hT, hT[:].rearrange("p a b -> p (a b)")), (pT, pT[:, :]))

    def proj_fm(ps, ps_ap, col, M=128):
        g = wgrp(col)
        for kc in range(8):
            k.mm((ps, ps_ap), (W_in, W_in[:, kc, col:col + M], g), (hT, hT[:, kc, :]), start=(kc == 0), stop=(kc == 7))

    def proj_tm(ps, ps_ap, col, N):
        g = wgrp(col)
        for kc in range(8):
            k.mm((ps, ps_ap), (hT, hT[:, kc, :]), (W_in, W_in[:, kc, col:col + N], g), start=(kc == 0), stop=(kc == 7))

    def mixer_tile(ti):
        smp = ti == NTP
        mi = 1 if smp else 0
        NB = NSB if smp else 1
        LB = ST if smp else 128
        xb = xt[ti % 2]
        xd, xap = x_rows(ti)
        if smp:
            k.barrier()
            k.bot = mark_rw
            Cs = k.sbuf([128, NSB, 129], F32, "Cs")
            Csb = k.sbuf([128, NSB, 130], BF16, "Csb")
            qTm = k.sbuf([128, NSB, 128], BF16, "qTm")
            ktmb = k.sbuf([128, NSB, 128], BF16, "ktmb")
            blkF = k.sbuf([128, NSB, 128], BF16, "blkF")
            rowm = k.sbuf([128, NSB], F32, "rowm")
            k.memset("pool", (blkF, blkF[:]), 1.0)
            aff(blkF, blkF[:], [[-ST, NSB], [1, 128]], 0, 0)
            aff(blkF, blkF[:], [[ST, NSB], [-1, 128]], 0, ST - 1)
            k.memset("pool", (rowm, rowm[:]), 1.0)
            aff(rowm, rowm[:], [[-ST, NSB]], 1, 0)
            aff(rowm, rowm[:], [[ST, NSB]], -1, ST - 1)
            smc = cv.view(cv[:].rearrange("p a b -> p (a b)")[0:NSB * 3, :], "smc")
            ext_s = xm.view(xm[:].rearrange("p a b -> p (a b)")[:, 0:8 * NSB * 11].rearrange("p (c b t) -> p c b t", c=8, b=NSB), "ext_s")
            k.dma("sp", smc[:], st_mconv[:, :], reads=[st_mconv], writes=[smc])
            for c in range(8):
                k.tr((pM[0], pM[0][:, c * 48:(c + 1) * 48]), (smc, smc[:, c * 128:(c + 1) * 128]), (identf, identf[0:48, 0:48]))
            k.copy("act", (ext_s, ext_s[:, :, :, 0:3]), (pM[0], pM[0][:, 0:384].rearrange("p (c b j) -> p c b j", c=8, b=NSB)))
            k.dma("sp", mst[:, 0:NSB], st_mm[:, :].rearrange("b h -> h b"), reads=[st_mm], writes=[mst], allow_slow_non_contiguous=True)
        k.dma("sp", xb[:], xap, reads=[xd], writes=[xb])
        norm_to_hT(xb, g1bc)

        if smp:
            for g in range(2):
                for c in range(4):
                    proj_fm(pF[g], pF[g][:, c * 128:(c + 1) * 128], C_QK + (4 * g + c) * 128)
                k.copy("act", (ext_s, ext_s[:, 4 * g:4 * g + 4, :, 3:11]), (pF[g], pF[g][:].rearrange("p (c b t) -> p c b t", c=4, b=NSB)))
            for c in range(8):
                cvv = cv[:, c, :].rearrange("p (b t) -> p b t", t=ST)
                k.ts("dve", (cv, cvv), (ext_s, ext_s[:, c, :, 3:11]), pcol(PT_MCW + 3 * 8 + c), pcol(PT_MCB + c),
                     op0=ALU.mult, op1=ALU.add)
                for j in range(3):
                    k.stt("dve", (cv, cvv), (ext_s, ext_s[:, c, :, j:j + ST]), pcol(PT_MCW + j * 8 + c), (cv, cvv),
                          ALU.mult, ALU.add)
        if not smp:
            k.copy("pool", (ext_q, ext_q[:, :, 0:3]), (cq, cq[:]))
            for g in range(2):
                for c in range(4):
                    proj_fm(pF[g], pF[g][:, c * 128:(c + 1) * 128], C_QK + (4 * g + c) * 128)
                k.copy("act", (ext_q, ext_q[:, 4 * g:4 * g + 4, 3:131]), (pF[g], pF[g][:].rearrange("p (c t) -> p c t", c=4)))
            k.copy("pool", (cq, cq[:]), (ext_q, ext_q[:, :, 128:131]))
            for c in range(8):
                k.ts("dve", (cv, cv[:, c, :]), (ext_q, ext_q[:, c, 3:131]), pcol(PT_MCW + 3 * 8 + c), pcol(PT_MCB + c),
                     op0=ALU.mult, op1=ALU.add)
                for j in range(3):
                    k.stt("dve", (cv, cv[:, c, :]), (ext_q, ext_q[:, c, j:j + 128]), pcol(PT_MCW + j * 8 + c), (cv, cv[:, c, :]),
                          ALU.mult, ALU.add)
        k.act((qkT, qkT[:].rearrange("p a b -> p (a b)")), (cv, cv[:].rearrange("p a b -> p (a b)")), AF.Silu)

        proj_tm(pR[0], pR[0][:, :], C_V, 512)
        k.copy("act", (vaug, vaug[:, :, 0:128]), (pR[0], pR[0][:].rearrange("p (h c) -> p h c", h=4)))
        for c in range(4):
            proj_fm(pF[0], pF[0][:, c * 128:(c + 1) * 128], C_O + c * 128)
        k.act((soT, soT[:].rearrange("p a b -> p (a b)")), (pF[0], pF[0][:, :]), AF.Sigmoid)
        proj_fm(pM[0], pM[0][0:4, 0:128], C_I, M=4)
        proj_fm(pM[0], pM[0][0:4, 128:256], C_F, M=4)
        liT, nlf, ncum, gT_, t0, t1 = gsm[0], gsm[1], gsm[2], gsm[3], gsm[4], gsm[5]
        k.ts("dve", (liT, liT[:]), (pM[0], pM[0][0:4, 0:128]), (gb, gb[:, 0:1]), None, op0=ALU.add)
        k.act((t0, t0[:]), (pM[0], pM[0][0:4, 128:256]), AF.Exp, bias=(nbf, nbf[:, 0:1]), scale=-1.0)
        k.act((nlf, nlf[:]), (t0, t0[:]), AF.Ln, bias=1.0)
        k.scan("dve", (ncum, ncum[:]), (resets[mi], resets[mi][0:4, 0:128]), (nlf, nlf[:]), 0.0, ALU.mult, ALU.add)
        k.tt("dve", (gT_, gT_[:]), (liT, liT[:]), (ncum, ncum[:]), ALU.add)
        b3 = lambda buf: buf[:].rearrange("p (b l) -> p b l", l=LB)
        mcb_ = mst[:, 0:NB].unsqueeze(2).to_broadcast([4, NB, LB])
        nlast = ncum[:].rearrange("p (b l) -> p b l", l=LB)[:, :, LB - 1:LB]
        k.stt("dve", (t0, b3(t0)), (gT_, b3(gT_)), LNK, (mst, mcb_), ALU.add, ALU.subtract)
        k.act((gpk, gpk[:, 0, :]), (t0, t0[:]), AF.Exp)
        k.tt("dve", (t1, b3(t1)), (ncum, b3(ncum)), (mst, mcb_), ALU.subtract)
        k.act((gpk, gpk[:, 1, :]), (t1, t1[:]), AF.Exp)
        k.tt("dve", (t1, b3(t1)), (gT_, b3(gT_)), (ncum, nlast.to_broadcast([4, NB, LB])), ALU.subtract)
        k.red("dve", (gt[0], gt[0][:, 0:NB]), (t1, b3(t1)), ALU.max)
        k.tt("dve", (gt[1], gt[1][:, 0:NB]), (mst, mst[:, 0:NB]), (ncum, nlast.rearrange("p b o -> p (b o)")), ALU.subtract)
        k.tt("dve", (mnew, mnew[:, 0:NB]), (gt[1], gt[1][:, 0:NB]), (gt[0], gt[0][:, 0:NB]), ALU.max)
        k.tt("dve", (gt[2], gt[2][:, 0:NB]), (gt[1], gt[1][:, 0:NB]), (mnew, mnew[:, 0:NB]), ALU.subtract)
        k.act((gt[3], gt[3][:, 0:NB]), (gt[2], gt[2][:, 0:NB]), AF.Exp)
        k.tt("dve", (gpk, gpk[:, 2, :].rearrange("p (b l) -> p b l", l=LB)), (gpk, gpk[:, 0, :].rearrange("p (b l) -> p b l", l=LB)),
             (gt[3], gt[3][:, 0:NB].unsqueeze(2).to_broadcast([4, NB, LB])), ALU.mult)
        for j in range(3):
            k.tr((pM[1], pM[1][:, 4 * j:4 * j + 4]), (gpk, gpk[:, j, :]), (identf, identf[0:4, 0:4]))
        k.copy("dve", (tokS, tokS[:]), (pM[1], pM[1][:, 0:12]))
        k.tt("dve", (s0d, s0d[:, :, 0:NB]), (identf, identf[0:4, 0:4].unsqueeze(2).to_broadcast([4, 4, NB])),
             (gt[3], gt[3][:, 0:NB].unsqueeze(1).to_broadcast([4, 4, NB])), ALU.mult)
        k.mm((pM[1], pM[1][:, 16:16 + 4 * NB]), (ones4, ones4[:]), (s0d, s0d[:, :, 0:NB].rearrange("p a b -> p (a b)")))
        k.copy("dve", (s0bc, s0bc[:, 0:4 * NB]), (pM[1], pM[1][:, 16:16 + 4 * NB]))

        for h in range(4):
            k.mm((pM[0], pM[0][:, h * 128:(h + 1) * 128]), (qkT, qkT[:, 4 + h, :]), (qkT, qkT[:, h, :]))
        for h in range(4):
            k.stt("dve", (PTm, PTm[:, h, :]), (pM[0], pM[0][:, h * 128:(h + 1) * 128]), (tokS, tokS[:, h:h + 1]),
                  (mU_in[mi], mU_in[mi][:]), ALU.mult, ALU.mult)
        pO = [pM[1], pM[2]]
        oap = lambda h: pO[h // 2][:, 256 * (h % 2):256 * (h % 2) + 129]
        if not smp:
            for h in range(4):
                k.mm((pO[h // 2], oap(h)), (qkT, qkT[:, h, :]), (Cb, Cb[:, h, 0:129]), start=True, stop=False)
                k.mm((pO[h // 2], oap(h)), (PTm, PTm[:, h, :]), (vaug, vaug[:, h, 0:129]), start=False, stop=True)
        else:
            for h in range(4):
                k.tr((pT, pT[:, h * 128:(h + 1) * 128]), (qkT, qkT[:, 4 + h, :]), (identb, identb[:]))
            for h in range(4):
                k.ts("dve", (ktm, ktm[:, h, :]), (pT, pT[:, h * 128:(h + 1) * 128]), (tokS, tokS[:, 8 + h:9 + h]), None, op0=ALU.mult)
            for h in range(4):
                k.dma("sp", Cs[:, :, 0:128], st_mC[:, h, :, :].rearrange("b d v -> d b v"), reads=[st_mC], writes=[Cs])
                k.dma("sp", Cs[:, :, 128], st_mn[:, h, :].rearrange("b d -> d b"), reads=[st_mn], writes=[Cs], allow_slow_non_contiguous=True)
                k.copy("act", (Csb, Csb[:, :, 0:129]), (Cs, Cs[:]))
                k.tt("dve", (qTm, qTm[:]), (qkT, qkT[:, h, :].unsqueeze(1).to_broadcast([128, NSB, 128])), (blkF, blkF[:]), ALU.mult)
                for b in range(NSB):
                    k.mm((pO[h // 2], oap(h)), (qTm, qTm[:, b, :]), (Csb, Csb[:, b, 0:129]), start=(b == 0), stop=False)
                k.mm((pO[h // 2], oap(h)), (PTm, PTm[:, h, :]), (vaug, vaug[:, h, 0:129]), start=False, stop=True)
                k.tt("dve", (ktmb, ktmb[:]), (ktm, ktm[:, h, :].unsqueeze(1).to_broadcast([128, NSB, 128])),
                     (rowm, rowm[:].unsqueeze(2).to_broadcast([128, NSB, 128])), ALU.mult)
                for grp in range(4):
                    bank = pF[grp % 2]
                    for bi in range(4):
                        b = 4 * grp + bi
                        k.mm((bank, bank[:, bi * 128:(bi + 1) * 128]), (ktmb, ktmb[:, b, :]), (vaug, vaug[:, h, 0:128]))
                    for bi in range(4):
                        b = 4 * grp + bi
                        k.stt("dve", (Cs, Cs[:, b, 0:128]), (Cs, Cs[:, b, 0:128]), (s0bc, s0bc[:, h * NSB + b:h * NSB + b + 1]),
                              (bank, bank[:, bi * 128:(bi + 1) * 128]), ALU.mult, ALU.add)
                for b in range(NSB):
                    k.mm((pR[0], pR[0][:, b:b + 1]), (ktmb, ktmb[:, b, :]), (vaug, vaug[:, h, 128:129]))
                k.tt("dve", (Cs, Cs[:, :, 128]), (Cs, Cs[:, :, 128]), (s0bc, s0bc[:, h * NSB:(h + 1) * NSB]), ALU.mult)
                k.tt("dve", (Cs, Cs[:, :, 128]), (Cs, Cs[:, :, 128]), (pR[0], pR[0][:, 0:NSB]), ALU.add)
                k.dma("sp", o_sC[:, h, :, :].rearrange("b d v -> d b v"), Cs[:, :, 0:128], reads=[Cs], writes=[o_sC])
                k.dma("sp", o_sn[:, h, :].rearrange("b d -> d b"), Cs[:, :, 128], reads=[Cs], writes=[o_sn], allow_slow_non_contiguous=True)
        for h in range(4):
            k.copy("act", (dn, dn[:, h:h + 1]), (pO[h // 2], oap(h)[:, 128:129]))
        k.stt("dve", (dn, dn[:]), (dn, dn[:]), -1.0, (dn, dn[:]), ALU.mult, ALU.max)
        k.tt("dve", (dn, dn[:]), (dn, dn[:]), (tokS, tokS[:, 4:8]), ALU.max)
        k.op("dve", lambda e: e.reciprocal(out=dn[:], in_=dn[:]), reads=[dn], writes=[dn])
        for h in range(4):
            k.act((hm, hm[:, h, :]), (pO[h // 2], oap(h)[:, 0:128]), AF.Copy, scale=(dn, dn[:, h:h + 1]))
        for h in range(4):
            k.op("dve", lambda e, h=h: e.bn_stats(out=bst[:, h, :], in_=hm[:, h, :]), reads=[hm], writes=[(bst, h)])
        for h in range(4):
            k.op("dve", lambda e, h=h: e.bn_aggr(out=bag[:, h, :], in_=bst[:, h, :]), reads=[(bst, h)], writes=[(bag, h)])
        k.act((bag, bag[:, :, 1:2]), (bag, bag[:, :, 1:2]), AF.Ln, bias=EPS)
        k.act((bag, bag[:, :, 1:2]), (bag, bag[:, :, 1:2]), AF.Exp, scale=-0.5)
        for h in range(4):
            k.ts("dve", (hn, hn[:, h, :]), (hm, hm[:, h, :]), (bag, bag[:, h, 0:1]), (bag, bag[:, h, 1:2]),
                 op0=ALU.subtract, op1=ALU.mult)
        for h in range(4):
            k.tr((pM[0], pM[0][:, h * 128:(h + 1) * 128]), (hn, hn[:, h, :]), (identf, identf[:]))
        for h in range(4):
            k.stt("dve", (hmT_all, hmT_all[:, ti, h, :], ti), (pM[0], pM[0][:, h * 128:(h + 1) * 128]), pcol(PT_MNG + h),
                  (soT, soT[:, h, :]), ALU.mult, ALU.mult)
        if not smp:
            for h in range(4):
                k.tr((pT, pT[:, h * 128:(h + 1) * 128]), (qkT, qkT[:, 4 + h, :]), (identb, identb[:]))
            for h in range(4):
                k.ts("dve", (ktm, ktm[:, h, :]), (pT, pT[:, h * 128:(h + 1) * 128]), (tokS, tokS[:, 8 + h:9 + h]), None, op0=ALU.mult)
            for h in range(4):
                k.mm((pO[h // 2], oap(h)), (ktm, ktm[:, h, :]), (vaug, vaug[:, h, 0:129]))
            for h in range(4):
                k.stt("dve", (Cst, Cst[:, h, :]), (Cst, Cst[:, h, :]), (s0bc, s0bc[:, h:h + 1]), (pO[h // 2], oap(h)),
                      ALU.mult, ALU.add)
            k.copy("act", (Cb, Cb[:, :, 0:129]), (Cst, Cst[:]))
            k.copy("dve", (mst, mst[:, 0:1]), (mnew, mnew[:, 0:1]))
        if ti == NTP - 1:
            for h in range(4):
                k.dma("sp", o_pC[h], Cst[:, h, 0:128], reads=[Cst], writes=[o_pC])
            k.dma("sp", o_pn[:].rearrange("h d -> d h"), Cst[:, :, 128], reads=[Cst], writes=[o_pn], allow_slow_non_contiguous=True)
            k.dma("sp", o_pm[:].rearrange("o h -> h o"), mnew[:, 0:1], reads=[mnew], writes=[o_pm], allow_slow_non_contiguous=True)
            for blk in range(2):
                proj_tm(pR[blk], pR[blk][:, :], C_QK + blk * 512, 512)
                k.copy("act", (zq_tm, zq_tm[:, blk * 512:(blk + 1) * 512]), (pR[blk], pR[blk][:, :]))
            k.dma("sp", o_pconv[:], zq_tm[125:128, :], reads=[zq_tm], writes=[o_pconv])
        if smp:
            k.dma("sp", o_sm[:, :].rearrange("b h -> h b"), mnew[:, 0:NSB], reads=[mnew], writes=[o_sm], allow_slow_non_contiguous=True)
            for blk in range(2):
                proj_tm(pR[blk], pR[blk][:, :], C_QK + blk * 512, 512)
                k.copy("act", (zq_tm, zq_tm[:, blk * 512:(blk + 1) * 512]), (pR[blk], pR[blk][:, :]))
            for b in range(NSB):
                k.dma("sp", o_sconv[3 * b:3 * b + 3, :], zq_tm[ST * b + 5:ST * b + 8, :], reads=[zq_tm], writes=[o_sconv])

    k.fence_mm = (pT, identb)
    BK = [pM[0], pM[1], pF[0], pF[1], pR[0], pR[1]]

    _rw_stop = int(_os.environ.get("KDBG_RW", "99"))

    def rwkv_tile(ti):
        smp = ti == NTP
        mi = 0
        NLV = 7
        rtl = rt
        if not smp:
            k.copy("pool", (ext_r, ext_r[:, :, 0:1]), (cr, cr[:]))
            for g in range(4):
                n = min(4, 14 - 4 * g)
                for c in range(n):
                    proj_fm(pF[g % 2], pF[g % 2][:, c * 128:(c + 1) * 128], C_R + (4 * g + c) * 128)
                k.copy("act", (ext_r, ext_r[:, 4 * g:4 * g + n, 1:129]),
                       (pF[g % 2], pF[g % 2][:, 0:n * 128].rearrange("p (c t) -> p c t", c=n)))
            k.copy("pool", (cr, cr[:]), (ext_r, ext_r[:, :, 128:129]))
            k.tt("pool", (xm, xm[:]), (ext_r, ext_r[:, :, 0:128]), (ext_r, ext_r[:, :, 1:129]), ALU.subtract)
            for c in range(14):
                k.stt("dve", (xm, xm[:, c, :]), (xm, xm[:, c, :]), pcol(PT_RMIX + c), (ext_r, ext_r[:, c, 1:129]), ALU.mult, ALU.add)
        else:
            k.barrier()
            k.bot = mark_rw
            ext_rs = k.sbuf([128, 14, NSB, ST + 1], F32, "ext_rs")
            rtl = [k.sbuf([128, 4, 128], F32, f"rts{i}") for i in range(7)] + [rt[7], rt[8]]
            stg = k.sbuf([128, 512], F32, "stg")
            srs = xm.view(xm[:].rearrange("p a b -> p (a b)")[0:NSB, :], "srs")
            k.dma("sp", srs[:], st_rshift[:, :], reads=[st_rshift], writes=[srs])
            for c in range(14):
                k.tr((pM[0], pM[0][:, c * NSB:(c + 1) * NSB]), (srs, srs[:, c * 128:(c + 1) * 128]), (identf, identf[0:NSB, 0:NSB]))
            k.copy("act", (ext_rs, ext_rs[:, :, :, 0]), (pM[0], pM[0][:, 0:14 * NSB].rearrange("p (c b) -> p c b", c=14)))
            for g in range(4):
                n = min(4, 14 - 4 * g)
                for c in range(n):
                    proj_fm(pF[g % 2], pF[g % 2][:, c * 128:(c + 1) * 128], C_R + (4 * g + c) * 128)
                k.copy("act", (ext_rs, ext_rs[:, 4 * g:4 * g + n, :, 1:ST + 1]),
                       (pF[g % 2], pF[g % 2][:, 0:n * 128].rearrange("p (c b t) -> p c b t", c=n, b=NSB)))
            xm4 = xm[:].rearrange("p c (b t) -> p c b t", t=ST)
            k.tt("pool", (xm, xm4), (ext_rs, ext_rs[:, :, :, 0:ST]), (ext_rs, ext_rs[:, :, :, 1:ST + 1]), ALU.subtract)
            for c in range(14):
                k.stt("dve", (xm, xm4[:, c]), (xm, xm4[:, c]), pcol(PT_RMIX + c), (ext_rs, ext_rs[:, c, :, 1:ST + 1]), ALU.mult, ALU.add)
        rT, krT, vrT = xm[:, 0:4, :], xm[:, 4:8, :], xm[:, 8:12, :]
        sig, cums, gam, ginv, gexc, a_, kk, tmp, kr2 = rtl
        if _rw_stop <= 1:
            return
        k.act((thad, thad[0:64, :]), (xm, xm[0:64, 12, :]), AF.Tanh)
        k.copy("act", (thad, thad[64:128, :]), (xm, xm[64:128, 12, :]))
        k.act((sgd, sgd[:]), (xm, xm[:, 13, :]), AF.Sigmoid)
        for c in range(4):
            k.mm((pM[0], pM[0][:, c * 128:(c + 1) * 128]), (Wl_w2, Wl_w2[0:64, c * 128:(c + 1) * 128]), (thad, thad[0:64, :]))
        for c in range(4):
            k.act((sig, sig[:, c, :]), (pM[0], pM[0][:, c * 128:(c + 1) * 128]), AF.Sigmoid, bias=pcol(PT_RW0 + c))
        k.pe_fence()
        for c in range(4):
            k.mm((pM[1], pM[1][:, c * 128:(c + 1) * 128]), (Wl_a2, Wl_a2[64:128, c * 128:(c + 1) * 128]), (thad, thad[64:128, :]))
        k.pe_fence()
        for c in range(4):
            k.act((a_, a_[:, c, :]), (pM[1], pM[1][:, c * 128:(c + 1) * 128]), AF.Sigmoid, bias=pcol(PT_RA0 + c))
        for c in range(4):
            k.mm((pM[2], pM[2][:, c * 128:(c + 1) * 128]), (Wl_g2, Wl_g2[:, c * 128:(c + 1) * 128]), (sgd, sgd[:]))
        k.copy("act", (gTs, gTs[:].rearrange("p a b -> p (a b)")), (pM[2], pM[2][:, :]))
        if _rw_stop <= 2:
            return
        fl = lambda b: b[:].rearrange("p a b -> p (a b)")
        if not smp:
            k.scan("dve", (cums, fl(cums)), (resets[mi], resets[mi][:]), (sig, fl(sig)), 0.0, ALU.mult, ALU.add)
            k.act((gam, fl(gam)), (cums, fl(cums)), AF.Exp, scale=WSCALE)
            k.act((ginv, fl(ginv)), (cums, fl(cums)), AF.Exp, scale=-WSCALE)
            k.tt("pool", (tmp, tmp[:]), (cums, cums[:]), (sig, sig[:]), ALU.subtract)
            k.act((gexc, fl(gexc)), (tmp, fl(tmp)), AF.Exp, scale=WSCALE)
        else:
            k.act((gam, fl(gam)), (sig, fl(sig)), AF.Exp, scale=WSCALE)
        if _rw_stop <= 3:
            return
        for c in range(4):
            k.ts("dve", (kk, kk[:, c, :]), (xm, xm[:, 4 + c, :]), pcol(PT_RKK + c), None, op0=ALU.mult)
        k.tt("pool", (tmp, tmp[:]), (kk, kk[:]), (kk, kk[:]), ALU.mult)
        for c in range(4):
            k.mm((pM[0], pM[0][:, c * 128:(c + 1) * 128]), (bones, bones[:]), (tmp, tmp[:, c, :]))
        k.ts("dve", (tmp, fl(tmp)), (pM[0], pM[0][:, :]), 1e-24, None, op0=ALU.max)
        k.act((tmp, fl(tmp)), (tmp, fl(tmp)), AF.Ln)
        k.act((tmp, fl(tmp)), (tmp, fl(tmp)), AF.Exp, scale=-0.5)
        k.tt("dve", (kk, kk[:]), (kk, kk[:]), (tmp, tmp[:]), ALU.mult)
        for c in range(4):
            k.ts("dve", (tmp, tmp[:, c, :]), (a_, a_[:, c, :]), -1.0, pcol(PT_RKA + c), op0=ALU.add, op1=ALU.mult)
        k.stt("dve", (kr2, kr2[:]), (tmp, tmp[:]), 1.0, (xm, krT), ALU.add, ALU.mult)
        k.tt("pool", (tmp, tmp[:]), (xm, rT), (kr2, kr2[:]), ALU.mult)
        for c in range(4):
            k.ts("dve", (tmp, tmp[:, c, :]), (tmp, tmp[:, c, :]), pcol(PT_RRK + c), None, op0=ALU.mult)
        for c in range(4):
            k.mm((pM[1], pM[1][:, c * 128:(c + 1) * 128]), (bones, bones[:]), (tmp, tmp[:, c, :]))
        k.tt("dve", (bonT, fl(bonT)), (pM[1], pM[1][:, :]), (xm, vrT.rearrange("p a b -> p (a b)") if False else xm[:, 8:12, :].rearrange("p a b -> p (a b)")), ALU.mult)
        if _rw_stop <= 4:
            return
        if smp:
            rwkv_sample_core(xm, gam, kr2, kk, a_, tmp, stg, gTs, bonT)
            return
        k.stt("dve", (ART, ART[:, :, 0, :]), (kk, kk[:]), -1.0, (gexc, gexc[:]), ALU.mult, ALU.mult)
        k.tt("pool", (ART, ART[:, :, 1, :]), (xm, rT), (gam, gam[:]), ALU.mult)
        k.tt("pool", (tmp, tmp[:]), (kk, kk[:]), (a_, a_[:]), ALU.mult)
        k.tt("dve", (BTb, BTb[:]), (tmp, tmp[:]), (ginv, ginv[:]), ALU.mult)
        k.tt("pool", (KTb, KTb[:]), (kr2, kr2[:]), (ginv, ginv[:]), ALU.mult)
        k.copy("act", (VTb, VTb[:]), (xm, vrT))
        if _rw_stop <= 5:
            return
        for c in range(4):
            k.tr((pT, pT[:, c * 128:(c + 1) * 128]), (ART, ART[:, c, 0, :]), (identb, identb[:]))
            k.tr((pT, pT[:, 512 + c * 128:512 + (c + 1) * 128]), (BTb, BTb[:, c, :]), (identb, identb[:]))
        k.copy("act", (AB_tm, AB_tm[:].rearrange("p a b -> p (a b)")), (pT, pT[:, :]))
        for c in range(4):
            k.tr((pT, pT[:, c * 128:(c + 1) * 128]), (KTb, KTb[:, c, :]), (identb, identb[:]))
            k.tr((pT, pT[:, 512 + c * 128:512 + (c + 1) * 128]), (VTb, VTb[:, c, :]), (identb, identb[:]))
        k.copy("dve", (KV_tm, KV_tm[:].rearrange("p a b -> p (a b)")), (pT, pT[:, :]))
        if _rw_stop <= 6:
            return
        A_tm = lambda h: (AB_tm, AB_tm[:, 0, h * 64:(h + 1) * 64])
        B_tm = lambda h: (AB_tm, AB_tm[:, 1, h * 64:(h + 1) * 64])
        K_tm = lambda h: (KV_tm, KV_tm[:, 0, h * 64:(h + 1) * 64])
        V_tm = lambda h: (KV_tm, KV_tm[:, 1, h * 64:(h + 1) * 64])
        m2b = mask2[mi][:].unsqueeze(1).to_broadcast([128, 2, 256])
        GB2, GK2, Nn2, PP2, XX2 = [GBm, GBm_b], [GKm, GKm_b], [Nn, Nn_b], [PP, PP_b], [XX, XX_b]
        LB3 = [[pM[0], pM[1], pR[0]], [pF[0], pF[1], pR[1]]]
        for g in range(2):
            GBm_, GKm_, Nn_, XX_ = GB2[g], GK2[g], Nn2[g], XX2[g]
            heads = [4 * g + i for i in range(4)]
            HO = [(pbs, [(i, h) for i, h in enumerate(heads) if 64 * (h % 2) == pbs]) for pbs in (0, 64)]
            for pbs, hl in HO:
                for i, h in hl:
                    c, pb = h // 2, 64 * (h % 2)
                    off = (i % 2) * 256
                    rAR = (ART, ART[pb:pb + 64, c, :, :].rearrange("p a t -> p (a t)"))
                    k.mm((BK[i // 2], BK[i // 2][:, off:off + 256]), (BTb, BTb[pb:pb + 64, c, :]), rAR)
                    k.mm((BK[2 + i // 2], BK[2 + i // 2][:, off:off + 256]), (KTb, KTb[pb:pb + 64, c, :]), rAR)
                    k.mm((BK[4], BK[4][:, i * 128:(i + 1) * 128]), (ART, ART[pb:pb + 64, c, 0, :]), (BTb, BTb[pb:pb + 64, c, :]))
                k.pe_fence()
            for hf in range(2):
                k.tt("dve", (GBm_, GBm_[:, 2 * hf:2 * hf + 2, :]), (BK[hf], BK[hf][:].rearrange("p (a b) -> p a b", a=2)), (mask2[mi], m2b), ALU.mult)
                k.tt("dve", (GKm_, GKm_[:, 2 * hf:2 * hf + 2, :]), (BK[2 + hf], BK[2 + hf][:].rearrange("p (a b) -> p a b", a=2)), (mask2[mi], m2b), ALU.mult)
            k.tt("dve", (Nn_, Nn_[:]), (BK[4], BK[4][:].rearrange("p (a b) -> p a b", a=4)),
                 (mL_st[mi], mL_st[mi][:].unsqueeze(1).to_broadcast([128, 4, 128])), ALU.mult)
            for i, h in enumerate(heads):
                k.mm((BK[5], BK[5][:, i * 64:(i + 1) * 64]), (GKm_, GKm_[:, i, 0:128]), V_tm(h))
            k.copy("act", (XX_[0], XX_[0][:, :, 64:128]), (BK[5], BK[5][:, 0:256].rearrange("p (a b) -> p a b", a=4)))
            k.copy("pool", (XX_[0], XX_[0][:, :, 0:64]), (AB_tm, AB_tm[:, 0, 256 * g:256 * g + 256].rearrange("p (a b) -> p a b", a=4)))

        xfinal = [None, None]

        def levels_gen(g):
            GBm_, Nn_, PP_, XX_ = GB2[g], Nn2[g], PP2[g], XX2[g]
            bP, bQ, bX = LB3[g]
            Pc = lambda i: (Nn_, Nn_[:, i, :])
            PTc = lambda i: (GBm_, GBm_[:, i, 0:128])
            xi = 0
            for lvl in range(NLV):
                Xc, Xn = XX_[xi], XX_[1 - xi]
                for i in range(4):
                    o = (bX, bX[:, i * 128:(i + 1) * 128])
                    k.mm(o, (identb, identb[:]), (Xc, Xc[:, i, :]), start=True, stop=False)
                    k.mm(o, PTc(i), (Xc, Xc[:, i, :]), start=False, stop=True)
                k.copy("act", (Xn, Xn[:].rearrange("p a b -> p (a b)")), (bX, bX[:, :]))
                xi = 1 - xi
                yield
                if lvl < NLV - 1:
                    bb = [bP, bQ]
                    for i in range(4):
                        off = (i % 2) * 256
                        if lvl < NLV - 2:
                            k.mm((bb[i // 2], bb[i // 2][:, off:off + 128]), PTc(i), Pc(i))
                        k.mm((bb[i // 2], bb[i // 2][:, off + 128:off + 256]), Pc(i), PTc(i))
                    PPn = PP_[lvl % 2]
                    for hf in range(2):
                        if lvl < NLV - 2:
                            k.copy("dve", (PPn, PPn[:, 2 * hf:2 * hf + 2, :]), (bb[hf], bb[hf][:].rearrange("p (a b) -> p a b", a=2)))
                        else:
                            k.copy("dve", (PPn, PPn[:, 2 * hf:2 * hf + 2, 128:256]),
                                   (bb[hf], bb[hf][:].rearrange("p (a b) -> p a b", a=2)[:, :, 128:256]))
                    Pc = lambda i, PPn=PPn: (PPn, PPn[:, i, 0:128])
                    PTc = lambda i, PPn=PPn: (PPn, PPn[:, i, 128:256])
                    yield
            xfinal[g] = XX_[xi]

        gens = [levels_gen(0), levels_gen(1)]
        while gens:
            for g_ in list(gens):
                try:
                    next(g_)
                except StopIteration:
                    gens.remove(g_)

        for g in range(2):
            heads = [4 * g + i for i in range(4)]
            HO = [(pbs, [(i, h) for i, h in enumerate(heads) if 64 * (h % 2) == pbs]) for pbs in (0, 64)]
            GBt, GKt = GB2[g], GK2[g]
            Xf = xfinal[g]
            if _rw_stop <= 8:
                continue
            k.pe_fence()
            for pbs, hl in HO:
                for i, h in hl:
                    c, pb = h // 2, 64 * (h % 2)
                    ci = i // 2
                    o = (BK[0], BK[0][pb:pb + 64, ci * 128:(ci + 1) * 128])
                    k.mm(o, (Xf, Xf[:, i, 0:64]), (GBt, GBt[:, i, 128:256]), start=True, stop=False)
                    k.pe_fence()
                    k.mm(o, (identb, identb[pb:pb + 64, pb:pb + 64]), (ART, ART[pb:pb + 64, c, 1, :]), start=False, stop=True)
                    k.pe_fence()
            k.copy("act", (QT, QT[:].rearrange("p a b -> p (a b)")), (BK[0], BK[0][:, 0:256]))
            for pbs, hl in HO:
                for i, h in hl:
                    c, pb = h // 2, 64 * (h % 2)
                    ci = i // 2
                    o = (pM[2], pM[2][:, h * 64:(h + 1) * 64])
                    k.mm(o, (QT, QT[pb:pb + 64, ci, :]), (STb, STb[pb:pb + 64, c, :]), start=True, stop=False)
                    k.pe_fence()
                    k.mm(o, (GBt, GBt[:, i, 128:256]), (Xf, Xf[:, i, 64:128]), start=False, stop=False)
                    k.mm(o, (GKt, GKt[:, i, 128:256]), V_tm(h), start=False, stop=True)
                    k.pe_fence()
            if _rw_stop <= 9:
                continue
            for pbs, hl in HO:
                for i, h in hl:
                    c, pb = h // 2, 64 * (h % 2)
                    ci = i // 2
                    k.mm((BK[1], BK[1][pb:pb + 64, ci * 64:(ci + 1) * 64]), (Xf, Xf[:, i, 0:64]), B_tm(h))
                k.pe_fence()
            k.tt("dve", (IE, IE[:, 2 * g:2 * g + 2, :]), (BK[1], BK[1][:, 0:128].rearrange("p (a b) -> p a b", a=2)),
                 (I2, I2[:].unsqueeze(1).to_broadcast([128, 2, 64])), ALU.add)
            for pbs, hl in HO:
                for i, h in hl:
                    c, pb = h // 2, 64 * (h % 2)
                    ci = i // 2
                    o = (BK[2], BK[2][pb:pb + 64, ci * 64:(ci + 1) * 64])
                    k.mm(o, (IE, IE[pb:pb + 64, c, :]), (STf, STf[pb:pb + 64, c, :]), start=True, stop=False)
                    k.pe_fence()
                    k.mm(o, B_tm(h), (Xf, Xf[:, i, 64:128]), start=False, stop=False)
                    k.mm(o, K_tm(h), V_tm(h), start=False, stop=True)
                    k.pe_fence()
            for ci in range(2):
                c = 2 * g + ci
                k.ts("dve", (STf, STf[:, c, :]), (BK[2], BK[2][:, ci * 64:(ci + 1) * 64]), (gam, gam[:, c, 127:128]), None, op0=ALU.mult)
            k.copy("act", (STb, STb[:, 2 * g:2 * g + 2, :]), (STf, STf[:, 2 * g:2 * g + 2, :]))
        if _rw_stop <= 10:
            return
        rwkv_epilogue(ti, pM[2], tmp)
        if ti == NTP - 1:
            for c in range(4):
                k.tr((pM[0], pM[0][0:64, c * 128:(c + 1) * 128]), (STf, STf[:, c, :]), (identf, identf[:]))
            k.copy("act", (rt[0], rt[0][0:64, :, :]), (pM[0], pM[0][0:64, :].rearrange("p (a b) -> p a b", a=4)))
            k.dma("sp", o_pS[:].rearrange("(h i) j -> i h j", h=8), rt[0][0:64, :, :].rearrange("p c (f j) -> p (c f) j", f=2),
                  reads=[rt[0]], writes=[o_pS])

    def rwkv_epilogue(ti, Yb, tmp):
        pM2 = [None, None, Yb]
        for h in range(8):
            k.op("dve", lambda e, h=h: e.bn_stats(out=bst8[:, h, :], in_=Yb[:, h * 64:(h + 1) * 64]), reads=[Yb], writes=[(bst8, h)])
        for h in range(8):
            k.op("dve", lambda e, h=h: e.bn_aggr(out=bag8[:, h, :], in_=bst8[:, h, :]), reads=[(bst8, h)], writes=[(bag8, h)])
        k.act((bag8, bag8[:, :, 1:2]), (bag8, bag8[:, :, 1:2]), AF.Ln, bias=GN_EPS)
        k.act((bag8, bag8[:, :, 1:2]), (bag8, bag8[:, :, 1:2]), AF.Exp, scale=-0.5)
        for h in range(8):
            k.ts("dve", (yn, yn[:, h, :]), (Yb, Yb[:, h * 64:(h + 1) * 64]), (bag8, bag8[:, h, 0:1]), (bag8, bag8[:, h, 1:2]),
                 op0=ALU.subtract, op1=ALU.mult)
        for c in range(4):
            k.tr((pM[0], pM[0][:, c * 128:(c + 1) * 128]), (yn, yn[:, 2 * c:2 * c + 2, :].rearrange("p a b -> p (a b)")), (identf, identf[:]))
        for c in range(4):
            k.ts("dve", (tmp, tmp[:, c, :]), (pM[0], pM[0][:, c * 128:(c + 1) * 128]), pcol(PT_RLNG + c), pcol(PT_RLNB + c),
                 op0=ALU.mult, op1=ALU.add)
        k.tt("pool", (tmp, tmp[:]), (tmp, tmp[:]), (bonT, bonT[:]), ALU.add)
        k.tt("dve", (yrgT_all, yrgT_all[:, ti, :, :], ti), (tmp, tmp[:]), (gTs, gTs[:]), ALU.mult)

    rsc = k.dram("rw_scratch", [6, 128, RW], F32)
    ysc = k.dram("ry_scratch", [128, RW], F32)

    def rwkv_sample_core(xm, dec, kr2, kk, a_, tmp, stg, gTs, bonT):
        ti = NTP
        srcs = []
        srcs.append((xm, lambda c: xm[:, c, :]))
        srcs.append((dec, lambda c: dec[:, c, :]))
        srcs.append((kr2, lambda c: kr2[:, c, :]))
        srcs.append((xm, lambda c: xm[:, 8 + c, :]))
        for q in range(6):
            if q == 4:
                k.ts("dve", (tmp, tmp[:]), (kk, kk[:]), -1.0, None, op0=ALU.mult)
                sb_, fn = tmp, (lambda c: tmp[:, c, :])
            elif q == 5:
                k.tt("dve", (tmp, tmp[:]), (kk, kk[:]), (a_, a_[:]), ALU.mult)
                sb_, fn = tmp, (lambda c: tmp[:, c, :])
            else:
                sb_, fn = srcs[q]
            pb_ = pM[q % 2]
            for c in range(4):
                k.tr((pb_, pb_[:, c * 128:(c + 1) * 128]), (sb_, fn(c)), (identf, identf[:]))
            k.copy("act", (stg, stg[:]), (pb_, pb_[:, :]))
            k.dma("sp", rsc[q], stg[:], reads=[stg], writes=[(rsc, q)])
        for blk, (c0, n) in enumerate(((0, 512), (512, 512), (1024, 512), (1536, 256))):
            proj_tm(pR[blk % 2], pR[blk % 2][:, 0:n], C_R + c0, n)
            k.copy("act", (stg, stg[:, 0:n]), (pR[blk % 2], pR[blk % 2][:, 0:n]))
            for b in range(NSB):
                k.dma("sp", o_sshift[b:b + 1, c0:c0 + n], stg[ST * b + ST - 1:ST * b + ST, 0:n], reads=[stg], writes=[o_sshift])
        k.barrier()
        k.bot = mark_rw
        vec6 = k.sbuf([128, 6, ST, RN], F32, "vec6")
        Ssb = k.sbuf([128, RN, RN], F32, "Ssb")
        tmpS = k.sbuf([128, RN, RN], F32, "tmpS")
        sa = k.sbuf([128, RN], F32, "sa")
        ys = k.sbuf([128, ST, RN], F32, "ys")
        Ytm = k.sbuf([128, RW], F32, "Ytm")
        k.dma("sp", Ssb[:].rearrange("p a b -> p (a b)"), st_rS[:, :], reads=[st_rS], writes=[Ssb])
        for q in range(6):
            for b in range(NSB):
                k.dma("sp", vec6[RH * b:RH * b + RH, q, :, :], rsc[q, ST * b:ST * b + ST, :].rearrange("t (h j) -> h t j", h=RH),
                      reads=[(rsc, q)], writes=[(vec6, q)])
        bc_i = lambda q, t: vec6[:, q, t, :].unsqueeze(1).to_broadcast([128, RN, RN])
        for t in range(ST):
            k.tt("dve", (tmpS, tmpS[:]), (Ssb, Ssb[:]), (vec6, bc_i(4, t)), ALU.mult)
            k.red("dve", (sa, sa[:]), (tmpS, tmpS[:]), ALU.add)
            k.tt("pool", (Ssb, Ssb[:]), (Ssb, Ssb[:]), (vec6, bc_i(1, t)), ALU.mult)
            k.tt("dve", (tmpS, tmpS[:]), (sa, sa[:].unsqueeze(2).to_broadcast([128, RN, RN])), (vec6, bc_i(5, t)), ALU.mult)
            k.tt("dve", (Ssb, Ssb[:]), (Ssb, Ssb[:]), (tmpS, tmpS[:]), ALU.add)
            k.tt("pool", (tmpS, tmpS[:]), (vec6, vec6[:, 3, t, :].unsqueeze(2).to_broadcast([128, RN, RN])), (vec6, bc_i(2, t)), ALU.mult)
            k.tt("dve", (Ssb, Ssb[:]), (Ssb, Ssb[:]), (tmpS, tmpS[:]), ALU.add)
            k.tt("pool", (tmpS, tmpS[:]), (Ssb, Ssb[:]), (vec6, bc_i(0, t)), ALU.mult)
            k.red("dve", (ys, ys[:, t, :]), (tmpS, tmpS[:]), ALU.add)
        k.dma("sp", o_sS[:, :], Ssb[:].rearrange("p a b -> p (a b)"), reads=[Ssb], writes=[o_sS])
        k.dma("sp", ysc[:, :], ys[:].rearrange("p a b -> p (a b)"), reads=[ys], writes=[ysc])
        for b in range(NSB):
            k.dma("sp", Ytm[ST * b:ST * b + ST, :].rearrange("t (h i) -> t h i", h=RH),
                  ysc[RH * b:RH * b + RH, :].rearrange("h (t i) -> t h i", t=ST), reads=[ysc], writes=[Ytm])
        rwkv_epilogue(ti, Ytm, rt[7])

    def tail_rows(ti):
        if ti != NTP - 1:
            return
        for blk, (c0, n) in enumerate(((0, 512), (512, 512), (1024, 512), (1536, 256))):
            proj_tm(pR[blk % 2], pR[blk % 2][:, 0:n], C_R + c0, n)
            k.copy("act", (rt[1], rt[1][96:128, :, :].rearrange("p a b -> p (a b)")[:, 0:n]), (pR[blk % 2], pR[blk % 2][96:128, 0:n]))
            k.dma("sp", o_pshift[0:1, c0:c0 + n], rt[1][127:128, :, :].rearrange("p a b -> p (a b)")[:, 0:n], reads=[rt[1]], writes=[o_pshift])

    tiles_all = list(range(NT)) if stage >= 5 else list(range(NTP))

    def phase_1b(k):
        k.barrier()
        k.bot = mark_1a
        W_g = k.sbuf([128, 8, 2048], BF16, "W_g")
        W_bm = k.sbuf([128, 4, D], BF16, "W_bm")
        W_br = k.sbuf([128, 4, D], BF16, "W_br")
        W_out = k.sbuf([128, 8, D], BF16, "W_out")
        k.dma("pool", W_bm[:], d_w_bm[:], reads=[d_w_bm], writes=[W_bm])
        for kh in range(2):
            k.dma("pool", W_g[:, 4 * kh:4 * kh + 4, 0:1024], d_w_in[:, 4 * kh:4 * kh + 4, C_G:C_G + 1024], reads=[d_w_in], writes=[(W_g, "a")])
        k.dma("pool", W_br[:], d_w_br[:], reads=[d_w_br], writes=[W_br])
        for kh in range(2):
            k.dma("pool", W_g[:, 4 * kh:4 * kh + 4, 1024:2048], d_w_in[:, 4 * kh:4 * kh + 4, C_G + 1024:C_G + 2048], reads=[d_w_in], writes=[(W_g, "b")])
        k.dma("pool", W_out[:], d_w_out[:], reads=[d_w_out], writes=[W_out])
        xtb = [k.sbuf([128, D], F32, f"xtb{i}") for i in range(2)]
        hb2 = k.sbuf([128, D], BF16, "hb2")
        hT2 = k.sbuf([128, 8, 128], BF16, "hT2")
        ss2 = k.sbuf([128, 1], F32, "ss2")
        rs2 = k.sbuf([128, 1], F32, "rs2")
        sgb = [k.sbuf([128, 512], F32, f"sgb{i}") for i in range(2)]
        yab = k.sbuf([128, D], F32, "yab")
        mg = k.sbuf([128, D], BF16, "mg")
        mT = k.sbuf([128, 8, 128], BF16, "mT")
        for ti in tiles_all:
            xb = xtb[ti % 2]
            xd, xap = x_rows(ti)
            k.dma("sp", xb[:], xap, reads=[xd], writes=[xb])
            norm_generic(xb, g1bc, hb2, hT2, ss2, rs2)
            for half, (Wb, src, key) in enumerate(((W_bm, hmT_all, "a"), (W_br, yrgT_all, "b"))):
                for blk in range(2):
                    for kc in range(4):
                        k.mm((pR[blk], pR[blk][:, :]), (src, src[:, ti, kc, :], ti), (Wb, Wb[:, kc, blk * 512:(blk + 1) * 512]),
                             start=(kc == 0), stop=(kc == 3))
                    col = half * 1024 + blk * 512
                    for kc in range(8):
                        k.mm((pF[blk], pF[blk][:, :]), (hT2, hT2[:, kc, :]), (W_g, W_g[:, kc, col:col + 512], key),
                             start=(kc == 0), stop=(kc == 7))
                    k.act((sgb[blk], sgb[blk][:]), (pF[blk], pF[blk][:, :]), AF.Sigmoid)
                    if half == 0:
                        k.tt("dve", (yab, yab[:, blk * 512:(blk + 1) * 512]), (sgb[blk], sgb[blk][:]), (pR[blk], pR[blk][:, :]), ALU.mult)
                    else:
                        k.tt("dve", (sgb[blk], sgb[blk][:]), (sgb[blk], sgb[blk][:]), (pR[blk], pR[blk][:, :]), ALU.mult)
                        k.tt("pool", (mg, mg[:, blk * 512:(blk + 1) * 512]), (sgb[blk], sgb[blk][:]), (yab, yab[:, blk * 512:(blk + 1) * 512]), ALU.add)
            for kc in range(8):
                k.tr((pT, pT[:, kc * 128:(kc + 1) * 128]), (mg, mg[:, kc * 128:(kc + 1) * 128]), (identb, identb[:]))
            k.copy("act", (mT, mT[:].rearrange("p a b -> p (a b)")), (pT, pT[:, :]))
            for blk in range(2):
                for kc in range(8):
                    k.mm((pM[blk], pM[blk][:, :]), (mT, mT[:, kc, :]), (W_out, W_out[:, kc, blk * 512:(blk + 1) * 512]),
                         start=(kc == 0), stop=(kc == 7))
                k.tt("dve", (xb, xb[:, blk * 512:(blk + 1) * 512]), (xb, xb[:, blk * 512:(blk + 1) * 512]), (pM[blk], pM[blk][:, :]), ALU.add)
            k.dma("sp", x1s[ti * 128:(ti + 1) * 128, :], xb[:], reads=[xb], writes=[(x1s, ti)])

    def norm_generic(xbuf, gbc, hb_, hT_, ss_, rs_):
        k.act((hb_, hb_[:]), (xbuf, xbuf[:]), AF.Square, accum=(ss_, ss_[:]))
        k.ts("dve", (rs_, rs_[:]), (ss_, ss_[:]), 1.0 / D, EPS, op0=ALU.mult, op1=ALU.add)
        k.act((rs_, rs_[:]), (rs_, rs_[:]), AF.Ln)
        k.act((rs_, rs_[:]), (rs_, rs_[:]), AF.Exp, scale=-0.5)
        k.stt("dve", (hb_, hb_[:]), (xbuf, xbuf[:]), (rs_, rs_[:, 0:1]), (gbc, gbc[:]), ALU.mult, ALU.mult)
        for kc in range(8):
            k.tr((pT, pT[:, kc * 128:(kc + 1) * 128]), (hb_, hb_[:, kc * 128:(kc + 1) * 128]), (identb, identb[:]))
        k.copy("act", (hT_, hT_[:].rearrange("p a b -> p (a b)")), (pT, pT[:, :]))

    def phase_2(k):
        k.barrier()
        k.bot = mark_phase
        F_up = k.sbuf([128, 8, 2 * DFF], BF16, "F_up")
        F_dn = k.sbuf([128, NFC, D], BF16, "F_dn")
        PGW = k.sbuf([128, 8, D], BF16, "PGW")
        PPJ = k.sbuf([128, 2, D], BF16, "PPJ")
        NG = 4
        CW = DFF // NG
        for g in range(NG):
            for part in range(2):
                k.dma("pool", F_up[:, :, part * DFF + g * CW:part * DFF + (g + 1) * CW], d_f_up[:, :, part * DFF + g * CW:part * DFF + (g + 1) * CW],
                      reads=[d_f_up], writes=[(F_up, g)])
        for g in range(2):
            k.dma("pool", F_dn[:, 11 * g:11 * g + 11, :], d_f_down[:, 11 * g:11 * g + 11, :], reads=[d_f_down], writes=[(F_dn, g)])
        k.dma("pool", PGW[:], d_pgw[:], reads=[d_pgw], writes=[PGW])
        k.dma("pool", PPJ[:], d_ppj[:], reads=[d_ppj], writes=[PPJ])
        g2bc = k.sbuf([128, D], F32, "g2bc")
        g3bc = k.sbuf([128, D], F32, "g3bc")
        g4bc = k.sbuf([128, D], F32, "g4bc")
        fct = k.sbuf([128, 4 * NFC], F32, "fct")
        k.dma("sp", g2bc[:], d_g2[:], reads=[d_g2], writes=[g2bc])
        k.dma("sp", g3bc[:], d_g3[:], reads=[d_g3], writes=[g3bc])
        k.dma("sp", g4bc[:], d_g4[:], reads=[d_g4], writes=[g4bc])
        k.dma("sp", fct[:], d_fctab[:], reads=[d_fctab], writes=[fct])
        fcol = lambda c: (fct, fct[:, c:c + 1])
        xq = [k.sbuf([128, D], F32, f"xq{i}") for i in range(2)]
        hb3 = k.sbuf([128, D], BF16, "hb3")
        hT3s = [k.sbuf([128, 8, 128], BF16, f"hT3a{i}") for i in range(2)]
        ss3 = k.sbuf([128, 1], F32, "ss3")
        rs3 = k.sbuf([128, 1], F32, "rs3")
        gT = k.sbuf([128, NFC, 128], BF16, "gT")
        cf = k.sbuf([128, NFC, 2], F32, "cf")
        GS = 4
        EXW = NSB * (ST + 2)
        ex4 = [k.sbuf([128, GS, EXW], F32, f"ex4_{i}") for i in range(2)]
        cc4 = [k.sbuf([128, GS, 128], F32, f"cc4_{i}") for i in range(2)]
        t14 = [k.sbuf([128, GS, 128], F32, "t14_0")] * 2
        up4 = [k.sbuf([128, GS, 128], F32, f"up4_{i}") for i in range(2)]
        sg3 = [k.sbuf([128, 512], F32, "sg30")] * 2
        ppt = k.sbuf([128, PLE], F32, "ppt")
        ppb = k.sbuf([128, PLE], BF16, "ppb")
        peT = k.sbuf([128, 2, 128], BF16, "peT")
        utm = sg3[0]
        k.memset("pool", (cf, cf[:]), 0.0)
        cfs = k.sbuf([128, NFC, 2 * NSB], F32, "cfs")
        GC = 1.5957691216057308
        BLK6 = ((0, 512), (512, 512), (1024, 512), (1536, 512), (2048, 512), (2560, 256))
        groups = [list(range(g0, min(g0 + GS, NFC))) for g0 in range(0, NFC, GS)]
        gbank = [pF[0], pF[1]]
        ubank = [pM[0], pM[1]]

        def fup_w(col):
            g = col // CW
            g_hi = (col + 127) // CW
            return g, g_hi

        def stage_A(ti, gi, hT3):
            smp = ti == NTP
            p = gi % 2
            chunks = groups[gi]
            n = len(chunks)
            c0 = chunks[0]
            for part, bank in ((0, gbank[p]), (1, ubank[p])):
                for ci, c in enumerate(chunks):
                    g, g_hi = fup_w(c * 128)
                    for kc in range(8):
                        k.mm((bank, bank[:, ci * 128:(ci + 1) * 128]), (F_up, F_up[:, kc, part * DFF + c * 128:part * DFF + (c + 1) * 128], g),
                             (hT3, hT3[:, kc, :]), start=(kc == 0), stop=(kc == 7))
                        if g_hi != g and g_hi in F_up.subs and F_up.subs[g_hi].w is not None:
                            k.streams["pe"][-1].deps.add(F_up.subs[g_hi].w)
            ex = ex4[p]
            if not smp:
                k.copy("pool", (ex, ex[:, 0:n, 0:2]), (cf, cf[:, c0:c0 + n, :]))
                k.copy("act", (ex, ex[:, 0:n, 2:130]), (gbank[p], gbank[p][:, 0:n * 128].rearrange("p (c t) -> p c t", c=n)))
                k.copy("pool", (cf, cf[:, c0:c0 + n, :]), (ex, ex[:, 0:n, 128:130]))
            else:
                exs = ex[:, 0:n, :].rearrange("p c (b t) -> p c b t", t=ST + 2)
                k.copy("pool", (ex, exs[:, :, :, 0:2]), (cfs, cfs[:, c0:c0 + n, :].rearrange("p c (b j) -> p c b j", j=2)))
                k.copy("act", (ex, exs[:, :, :, 2:ST + 2]), (gbank[p], gbank[p][:, 0:n * 128].rearrange("p (c b t) -> p c b t", c=n, b=NSB)))
            k.copy("act", (up4[p], up4[p][:, 0:n, :]), (ubank[p], ubank[p][:, 0:n * 128].rearrange("p (c t) -> p c t", c=n)))
            for ci, c in enumerate(chunks):
                if not smp:
                    tap = lambda j: ex[:, ci, j:j + 128]
                    ccv = cc4[p][:, ci, :]
                else:
                    e3 = ex[:, ci, :].rearrange("p (b t) -> p b t", t=ST + 2)
                    tap = lambda j, e3=e3: e3[:, :, j:j + ST]
                    ccv = cc4[p][:, ci, :].rearrange("p (b t) -> p b t", t=ST)
                cb = cc4[p]
                k.ts("dve", (cb, ccv), (ex, tap(2)), fcol(2 * NFC + c), fcol(3 * NFC + c), op0=ALU.mult, op1=ALU.add)
                k.stt("dve", (cb, ccv), (ex, tap(1)), fcol(1 * NFC + c), (cb, ccv), ALU.mult, ALU.add)
                k.stt("dve", (cb, ccv), (ex, tap(0)), fcol(0 * NFC + c), (cb, ccv), ALU.mult, ALU.add)

        def stage_B(ti, gi):
            p = gi % 2
            chunks = groups[gi]
            n = len(chunks)
            c0 = chunks[0]
            cb = (cc4[p], cc4[p][:, 0:n, :])
            ta = (t14[p], t14[p][:, 0:n, :])
            k.tt("pool", ta, cb, cb, ALU.mult)
            k.ts("dve", ta, ta, 0.044715, 1.0, op0=ALU.mult, op1=ALU.add)
            k.tt("pool", ta, ta, cb, ALU.mult)
            k.act(ta, ta, AF.Sigmoid, scale=GC)
            k.tt("pool", ta, ta, cb, ALU.mult)
            k.tt("dve", (gT, gT[:, c0:c0 + n, :], ("g", gi)), ta, (up4[p], up4[p][:, 0:n, :]), ALU.mult)

        def load_norm2(ti):
            xb = xq[ti % 2]
            k.dma("sp", xb[:], x1s[ti * 128:(ti + 1) * 128, :], reads=[(x1s, ti)], writes=[xb])
            if ti == NTP:
                for b6, (c0, n) in enumerate(BLK6):
                    k.dma("sp", utm[0:2 * NSB, 0:n], st_fconv[:, c0:c0 + n], reads=[st_fconv], writes=[utm])
                    nch = n // 128
                    for ci in range(nch):
                        k.tr((pR[b6 % 2], pR[b6 % 2][:, ci * 32:(ci + 1) * 32]), (utm, utm[0:2 * NSB, ci * 128:(ci + 1) * 128]),
                             (identf, identf[0:2 * NSB, 0:2 * NSB]))
                    k.copy("act", (cfs, cfs[:, 4 * b6:4 * b6 + nch, :]), (pR[b6 % 2], pR[b6 % 2][:, 0:nch * 32].rearrange("p (c x) -> p c x", c=nch)))
            norm_generic(xb, g2bc, hb3, hT3s[ti % 2], ss3, rs3)

        hb3b = hb3

        def groups_gen(ti):
            hT3 = hT3s[ti % 2]
            for gi in range(len(groups) + 1):
                if gi < len(groups):
                    stage_A(ti, gi, hT3)
                    yield
                if gi >= 1:
                    stage_B(ti, gi - 1)
                    yield

        def tail_gen(ti):
            smp = ti == NTP
            xb = xq[ti % 2]
            hT3 = hT3s[ti % 2]
            for blk in range(2):
                for c in range(NFC):
                    k.mm((pR[blk], pR[blk][:, :]), (gT, gT[:, c, :], ("g", c // GS)), (F_dn, F_dn[:, c, blk * 512:(blk + 1) * 512], c // 11),
                         start=(c == 0), stop=(c == NFC - 1))
                k.tt("dve", (xb, xb[:, blk * 512:(blk + 1) * 512]), (xb, xb[:, blk * 512:(blk + 1) * 512]), (pR[blk], pR[blk][:, :]), ALU.add)
                yield
            if ti == NTP - 1 or smp:
                for b6, (c0, n) in enumerate(BLK6):
                    for kc in range(8):
                        k.mm((pM[2], pM[2][:, 0:n]), (hT3, hT3[:, kc, :]), (F_up, F_up[:, kc, c0:c0 + n]),
                             start=(kc == 0), stop=(kc == 7))
                    if not smp:
                        k.copy("act", (utm, utm[96:128, 0:n]), (pM[2], pM[2][96:128, 0:n]))
                        k.dma("sp", o_pfconv[:, c0:c0 + n], utm[126:128, 0:n], reads=[utm], writes=[o_pfconv])
                    else:
                        k.copy("act", (utm, utm[:, 0:n]), (pM[2], pM[2][:, 0:n]))
                        for b in range(NSB):
                            k.dma("sp", o_sfconv[2 * b:2 * b + 2, c0:c0 + n], utm[ST * b + ST - 2:ST * b + ST, 0:n], reads=[utm], writes=[o_sfconv])
                    yield
            norm_generic(xb, g3bc, hb3b, hT3, ss3, rs3)
            yield
            pd, pap = (pp, pp[ti * 128:(ti + 1) * 128, :]) if not smp else (psm, psm[:, :])
            k.dma("sp", ppt[:], pap, reads=[pd], writes=[ppt])
            k.copy("act", (ppb, ppb[:]), (ppt, ppt[:]))
            for kc in range(2):
                k.tr((pT, pT[:, kc * 128:(kc + 1) * 128]), (ppb, ppb[:, kc * 128:(kc + 1) * 128]), (identb, identb[:]))
            k.copy("act", (peT, peT[:].rearrange("p a b -> p (a b)")), (pT, pT[:, 0:256]))
            yield
            for blk in range(2):
                for kc in range(8):
                    k.mm((pR[blk], pR[blk][:, :]), (hT3, hT3[:, kc, :]), (PGW, PGW[:, kc, blk * 512:(blk + 1) * 512]),
                         start=(kc == 0), stop=(kc == 7))
                yield
                k.act((sg3[blk], sg3[blk][:]), (pR[blk], pR[blk][:, :]), AF.Sigmoid)
                for kc in range(2):
                    k.mm((pM[2], pM[2][:, :]), (peT, peT[:, kc, :]), (PPJ, PPJ[:, kc, blk * 512:(blk + 1) * 512]),
                         start=(kc == 0), stop=(kc == 1))
                k.tt("dve", (sg3[blk], sg3[blk][:]), (sg3[blk], sg3[blk][:]), (pM[2], pM[2][:, :]), ALU.mult)
                k.tt("pool", (xb, xb[:, blk * 512:(blk + 1) * 512]), (xb, xb[:, blk * 512:(blk + 1) * 512]), (sg3[blk], sg3[blk][:]), ALU.add)
                yield
            k.act((hb3b, hb3b[:]), (xb, xb[:]), AF.Square, accum=(ss3, ss3[:]))
            k.ts("dve", (rs3, rs3[:]), (ss3, ss3[:]), 1.0 / D, EPS, op0=ALU.mult, op1=ALU.add)
            k.act((rs3, rs3[:]), (rs3, rs3[:]), AF.Ln)
            k.act((rs3, rs3[:]), (rs3, rs3[:]), AF.Exp, scale=-0.5)
            yield
            k.stt("dve", (xb, xb[:]), (xb, xb[:]), (rs3, rs3[:, 0:1]), (g4bc, g4bc[:]), ALU.mult, ALU.mult)
            if not smp:
                k.dma("sp", y_p[ti * 128:(ti + 1) * 128, :], xb[:], reads=[xb], writes=[y_p])
            else:
                k.dma("sp", y_s[:, :], xb[:], reads=[xb], writes=[y_s])
            if ti in nxt2:
                yield
                load_norm2(nxt2[ti])

        def run_rr(gens):
            gens = list(gens)
            while gens:
                for g_ in list(gens):
                    try:
                        next(g_)
                    except StopIteration:
                        gens.remove(g_)

        tl = list(tiles_all)
        nxt2 = {tl[i]: tl[i + 2] for i in range(len(tl) - 2)}
        load_norm2(tl[0])
        if len(tl) > 1:
            load_norm2(tl[1])
        run_rr([groups_gen(tl[0])])
        for idx, ti in enumerate(tl):
            tg = tail_gen(ti)
            if idx + 1 < len(tl):
                gg = groups_gen(tl[idx + 1])
                next(gg)
                next(gg)
                next(tg)
                next(tg)
                run_rr([gg, tg])
            else:
                run_rr([tg])

    _nt_dbg = int(_os.environ.get("KDBG_NT", "0"))
    for ti in (tiles_all if not _nt_dbg else list(range(_nt_dbg))):
        mixer_tile(ti)
        if stage >= 2:
            rwkv_tile(ti)
            tail_rows(ti)

    if stage >= 3:
        phase_1b(k)
    if stage >= 4:
        phase_2(k)
    k.emit()
    k.stats["sbuf_hiwater"] = k.hiwater
    k.stats["arena_bytes"] = k.arena_bytes
    return nc, k


def _chunk_rows(w, nk):
    return np.ascontiguousarray(w.reshape(nk, 128, w.shape[1]).transpose(1, 0, 2))


def _pcols(v, nc_):
    return v.reshape(nc_, 128).T


_PROG = {}


def _get_prog(stage=99, dbg=False):
    key = (stage, dbg)
    if key not in _PROG:
        _PROG[key] = build_program(stage, dbg)
    return _PROG[key]


def make_in_maps(inp):
    f = lambda a: np.ascontiguousarray(np.asarray(a, dtype=np.float32))
    ptab = np.zeros((128, 128), np.float32)
    mcw = f(inp["m_conv_w"])[0]
    for j in range(4):
        ptab[:, j * 8:(j + 1) * 8] = _pcols(mcw[j], 8)
    ptab[:, 32:40] = _pcols(f(inp["m_conv_b"])[0], 8)
    ptab[:, 40:54] = _pcols(f(inp["r_mix"])[0], 14)
    ptab[:, 54:58] = _pcols(f(inp["r_w0"])[0], 4)
    ptab[:, 58:62] = _pcols(f(inp["r_a0"])[0], 4)
    ptab[:, 62:66] = _pcols(f(inp["r_kk"])[0], 4)
    ptab[:, 66:70] = _pcols(f(inp["r_ka"])[0], 4)
    ptab[:, 70:74] = _pcols(f(inp["r_rk"])[0].reshape(-1), 4)
    ptab[:, 74:78] = _pcols(f(inp["r_ln_g"])[0], 4)
    ptab[:, 78:82] = _pcols(f(inp["r_ln_b"])[0], 4)
    ptab[:, 82:86] = _pcols(f(inp["m_norm_g"])[0], 4)
    fct = np.zeros((128, 4 * NFC), np.float32)
    fcw = f(inp["f_conv_w"])[0]
    for j in range(3):
        fct[:, j * NFC:(j + 1) * NFC] = _pcols(fcw[j], NFC)
    fct[:, 3 * NFC:4 * NFC] = _pcols(f(inp["f_conv_b"])[0], NFC)
    gbias = np.stack([f(inp["m_i_bias"])[0], f(inp["m_f_bias"])[0]], axis=1)
    ra2 = np.zeros((128, RW), np.float32)
    ra2[64:128] = f(inp["r_a2"])[0]
    bc = lambda v: np.ascontiguousarray(np.broadcast_to(f(v).reshape(1, D), (128, D)))
    shared = {
        "w_in": _chunk_rows(f(inp["w_in"])[0], 8),
        "w_bm": _chunk_rows(f(inp["w_branch_m"])[0], 4),
        "w_br": _chunk_rows(f(inp["w_branch_r"])[0], 4),
        "w_out": _chunk_rows(f(inp["w_out"])[0], 8),
        "f_up": _chunk_rows(f(inp["f_up"])[0], 8),
        "f_down": _chunk_rows(f(inp["f_down"])[0], NFC),
        "ple_gate_w": _chunk_rows(f(inp["ple_gate_w"])[0], 8),
        "ple_proj": _chunk_rows(f(inp["ple_proj"])[0], 2),
        "r_w2": f(inp["r_w2"])[0], "r_a2": ra2, "r_g2": f(inp["r_g2"])[0],
        "norm1_g": bc(inp["norm1_g"]), "norm2_g": bc(inp["norm2_g"]),
        "ple_norm_g": bc(inp["ple_norm_g"]), "final_norm_g": bc(inp["final_norm_g"]),
        "ptab": ptab, "fctab": fct, "gate_bias": np.ascontiguousarray(gbias),
    }
    maps = []
    for c in range(NCORES):
        sl = slice(c * NSB, (c + 1) * NSB)
        m = dict(shared)
        m["xp"] = f(inp["x_prompt"][c])
        m["xs"] = f(inp["x_sample"][sl]).reshape(128, D)
        m["pp"] = f(inp["p_prompt"][0, c])
        m["psm"] = f(inp["p_sample"][0, sl]).reshape(128, PLE)
        m["st_mconv"] = f(inp["state_mlstm_conv"][0, sl]).reshape(NSB * 3, 2 * MW)
        m["st_mC"] = f(inp["state_mlstm_C"][0, sl])
        m["st_mn"] = f(inp["state_mlstm_n"][0, sl])
        m["st_mm"] = f(inp["state_mlstm_m"][0, sl])
        m["st_rshift"] = f(inp["state_rwkv_shift"][0, sl])
        m["st_rS"] = f(inp["state_rwkv_S"][0, sl]).reshape(NSB * RH, RN * RN)
        m["st_fconv"] = f(inp["state_ffn_conv"][0, sl]).reshape(NSB * 2, DFF)
        maps.append(m)
    return maps


def assemble(results):
    g = lambda name: [np.asarray(r[name], dtype=np.float32) for r in results]
    y_p = np.stack(g("y_p"), 0)
    y_s = np.concatenate([a.reshape(NSB, ST, D) for a in g("y_s")], 0)
    p_conv = np.stack(g("p_conv"), 0)[None]
    p_C = np.stack(g("p_C"), 0)[None]
    p_n = np.stack(g("p_n"), 0)[None]
    p_m = np.stack([a.reshape(MH) for a in g("p_m")], 0)[None]
    p_shift = np.stack([a.reshape(RCOLS) for a in g("p_shift")], 0)[None]
    p_S = np.stack([a.reshape(RH, RN, RN) for a in g("p_S")], 0)[None]
    p_fconv = np.stack(g("p_fconv"), 0)[None]
    s_conv = np.concatenate([a.reshape(NSB, 3, 2 * MW) for a in g("s_conv")], 0)[None]
    s_C = np.concatenate(g("s_C"), 0)[None]
    s_n = np.concatenate(g("s_n"), 0)[None]
    s_m = np.concatenate(g("s_m"), 0)[None]
    s_shift = np.concatenate(g("s_shift"), 0)[None]
    s_S = np.concatenate([a.reshape(NSB, RH, RN, RN) for a in g("s_S")], 0)[None]
    s_fconv = np.concatenate([a.reshape(NSB, 2, DFF) for a in g("s_fconv")], 0)[None]
    return (y_p, y_s, p_conv, p_C, p_n, p_m, p_shift, p_S, p_fconv,
            s_conv, s_C, s_n, s_m, s_shift, s_S, s_fconv)


def kernel(**inputs):
    nc, _ = _get_prog()
    maps = make_in_maps(inputs)
    res = run_bass_kernel_spmd(nc, maps, core_ids=list(range(NCORES)))
    return assemble(res.results)
```

```python
import math
from contextlib import ExitStack

import numpy as np
import concourse.bass as bass
import concourse.mybir as mybir
from concourse.bass_utils import run_bass_kernel_spmd

F32 = mybir.dt.float32
BF16 = mybir.dt.bfloat16
AF = mybir.ActivationFunctionType
ALU = mybir.AluOpType
AX = mybir.AxisListType

ENGS = ("pe", "act", "dve", "pool", "sp")
N_DMA_SEMS = 8
SAME_ENG_DIST = 2

D = 1024
SEQ = 2048
NCORES = 8
NTP = SEQ // 128
NSB = 16
ST = 8
MW = 512
MH = 4
RW = 512
RH = 8
RN = 64
RCOLS = 1792
DFF = 2816
NFC = DFF // 128
PLE = 256
N_IN = 5896
C_QK, C_V, C_O, C_I, C_F, C_R, C_G = 0, 1024, 1536, 2048, 2052, 2056, 3848
EPS = 1e-6
GN_EPS = 64e-5
KSCALE = 128 ** -0.5
WSCALE = -math.exp(-0.5)


class _Trk:
    __slots__ = ("w", "r")

    def __init__(self):
        self.w = None
        self.r = []


class Buf:
    def __init__(self, t, name):
        self.t = t
        self.name = name
        self.whole = _Trk()
        self.subs = {}

    def __getitem__(self, idx):
        return self.t[idx]

    def view(self, ap, name=None):
        b = Buf(ap, name or self.name + "_v")
        b.whole = self.whole
        b.subs = self.subs
        return b


class _Op:
    __slots__ = ("eng", "fn", "deps", "needs_inc", "is_dma", "sem", "val", "pos", "force")


class K:
    def __init__(self, nc):
        self.nc = nc
        self.es = ExitStack()
        self.streams = {e: [] for e in ENGS}
        self.dma_rr = {e: 0 for e in ENGS}
        self.dma_last = {}
        self.nbuf = 0
        self.ops = []

    def _init_arena(self):
        nbytes = (int(self.nc.sbuf_bytes_remaining) - 512) // 64 * 64
        self.arena_bytes = nbytes
        self.arena = self.es.enter_context(self.nc.sbuf_tensor("arena", [128, nbytes // 2], BF16))
        self.bot = 0
        self.top = nbytes
        self.hiwater = 0

    def _view(self, off, shape, dtype):
        n = 1
        for d in shape[1:]:
            n *= d
        esz = 4 if dtype == F32 else 2
        v = self.arena[:, off // 2:(off + n * esz) // 2]
        if dtype == F32:
            v = v.bitcast(F32)
        if len(shape) > 2:
            names = " ".join(f"d{i}" for i in range(len(shape) - 1))
            v = v.rearrange(f"p ({names}) -> p {names}", **{f"d{i}": shape[i + 1] for i in range(len(shape) - 1)})
        if shape[0] < 128:
            v = v[0:shape[0]]
        return v, n * esz

    def sbuf(self, shape, dtype, name=None, top=False):
        if not hasattr(self, "arena"):
            self._init_arena()
        self.nbuf += 1
        name = name or f"sb{self.nbuf}"
        n = 1
        for d in shape[1:]:
            n *= d
        nb = (n * (4 if dtype == F32 else 2) + 63) // 64 * 64
        if top:
            self.top -= nb
            off = self.top
        else:
            off = self.bot
            self.bot += nb
        assert self.bot <= self.top, f"SBUF arena overflow allocating {name}: bot={self.bot} top={self.top}"
        self.hiwater = max(self.hiwater, self.bot + (self.arena_bytes - self.top))
        v, _ = self._view(off, list(shape), dtype)
        return Buf(v, name)

    def pe_fence(self):
        st = self.streams["pe"]
        if not st:
            return
        last = st[-1]
        o = self.op("pe", lambda h: h.nop(), (), ())
        o.deps.add(last)
        o.force = {last}
        if getattr(self, "fence_mm", None) is not None:
            fb, fi = self.fence_mm
            self.tr((fb, fb[:, 0:128]), (fi, fi[:]), (fi, fi[:]))
            last = self.streams["pe"][-1]
            o = self.op("pe", lambda h: h.nop(), (), ())
            o.deps.add(last)
            o.force = {last}

    def barrier(self):
        lasts = [st[-1] for st in self.streams.values() if st]
        lasts += list(self.dma_last.values())
        for e in ENGS:
            o = self.op(e, lambda h: h.nop(), (), ())
            o.deps.update(x for x in lasts if x is not o)

    def psum(self, shape, dtype, name=None):
        self.nbuf += 1
        name = "ps_" + (name or f"{self.nbuf}")
        t = self.es.enter_context(self.nc.psum_tensor(name, list(shape), dtype))
        return Buf(t, name)

    def dram(self, name, shape, dtype, kind="Internal"):
        t = self.nc.dram_tensor(name, list(shape), dtype, kind=kind)
        return Buf(t.ap(), name)

    def _touch(self, op, item, is_write):
        if isinstance(item, tuple):
            buf, key = item
        else:
            buf, key = item, None
        if key is None:
            trks = [buf.whole] + list(buf.subs.values())
        else:
            if key not in buf.subs:
                buf.subs[key] = _Trk()
            trks = [buf.whole, buf.subs[key]]
        for t in trks:
            if t.w is not None:
                op.deps.add(t.w)
            if is_write:
                op.deps.update(t.r)
        return buf, key

    def _commit(self, op, buf, key, is_write):
        if key is None:
            if is_write:
                buf.whole.w = op
                buf.whole.r = []
                buf.subs.clear()
            else:
                self._add_reader(buf.whole, op)
        else:
            t = buf.subs[key]
            if is_write:
                t.w = op
                t.r = []
            else:
                self._add_reader(t, op)

    @staticmethod
    def _add_reader(t, op):
        if not op.is_dma:
            t.r = [o for o in t.r if o.is_dma or o.eng != op.eng]
        t.r.append(op)

    def op(self, eng, fn, reads=(), writes=(), dma=False):
        o = _Op()
        o.eng = eng
        o.fn = fn
        o.deps = set()
        o.needs_inc = False
        o.is_dma = dma
        o.sem = None
        o.val = None
        o.force = None
        touched = []
        for it in reads:
            touched.append(self._touch(o, it, False) + (False,))
        for it in writes:
            touched.append(self._touch(o, it, True) + (True,))
        o.deps.discard(o)
        for buf, key, w in touched:
            self._commit(o, buf, key, w)
        if dma:
            kk = (eng, self.dma_rr[eng] % N_DMA_SEMS)
            self.dma_rr[eng] += 1
            prev = self.dma_last.get(kk)
            if prev is not None:
                o.deps.add(prev)
            self.dma_last[kk] = o
            o.sem = kk
            o.needs_inc = True
        o.pos = len(self.streams[eng])
        self.streams[eng].append(o)
        self.ops.append(o)
        return o

    def dma(self, eng, out, in_, reads=(), writes=(), **kw):
        return self.op(eng, lambda e: e.dma_start(out=out, in_=in_, **kw), reads, writes, dma=True)

    def emit(self):
        nc = self.nc
        for o in self.ops:
            real = []
            for d in o.deps:
                if (not d.is_dma) and (not o.is_dma) and d.eng == o.eng and o.eng == "pe":
                    if not (o.force and d in o.force):
                        continue
                d.needs_inc = True
                real.append(d)
            o.deps = real
        for e in ENGS:
            cs = [o for o in self.streams[e] if not o.is_dma]
            if cs:
                cs[-1].needs_inc = True
        es = self.es
        esem = {e: es.enter_context(nc.semaphore(f"s_{e}")) for e in ENGS}
        dsem = {}
        for e in ENGS:
            for i in range(min(N_DMA_SEMS, self.dma_rr[e])):
                dsem[(e, i)] = es.enter_context(nc.semaphore(f"d_{e}{i}"))
        dcount = {kk: 0 for kk in dsem}
        for e in ENGS:
            c = 0
            for o in self.streams[e]:
                if o.is_dma:
                    dcount[o.sem] += 16
                    o.val = dcount[o.sem]
                    o.sem = dsem[o.sem]
                elif o.needs_inc:
                    c += 1
                    o.val = c
                    o.sem = esem[e]
        final_waits = [(s, dcount[kk]) for kk, s in dsem.items() if dcount[kk] > 0]
        for e in ENGS:
            if e == "sp":
                continue
            cs = [o for o in self.streams[e] if not o.is_dma and o.needs_inc]
            if cs:
                final_waits.append((esem[e], cs[-1].val))
        streams = self.streams
        nwaits = [0]

        def run(e, handle):
            waited = {}
            for o in streams[e]:
                need = {}
                for d in o.deps:
                    if need.get(d.sem, (None, 0))[1] < d.val:
                        need[d.sem] = (d.sem, d.val)
                for s, v in need.values():
                    if waited.get(s, 0) >= v:
                        continue
                    handle.wait_ge(s, v)
                    nwaits[0] += 1
                    waited[s] = v
                ins = o.fn(handle)
                if o.is_dma:
                    ins.then_inc(o.sem, 16)
                elif o.needs_inc:
                    ins.then_inc(o.sem, 1)
            if e == "sp":
                for s, v in final_waits:
                    handle.wait_ge(s, v)

        with nc.Block() as block:
            @block.tensor
            def _(h):
                run("pe", h)

            @block.scalar
            def _(h):
                run("act", h)

            @block.vector
            def _(h):
                run("dve", h)

            @block.gpsimd
            def _(h):
                run("pool", h)

            @block.sync
            def _(h):
                run("sp", h)
        self.stats = dict(n_ops={e: len(streams[e]) for e in ENGS}, n_waits=nwaits[0])
        self.es.close()

    @staticmethod
    def _it(x):
        return (x[0], x[2]) if len(x) > 2 else x[0]

    def mm(self, out, lhsT, rhs, start=True, stop=True):
        return self.op("pe", lambda e: e.matmul(out[1], lhsT=lhsT[1], rhs=rhs[1], start=start, stop=stop),
                       reads=[self._it(lhsT), self._it(rhs)], writes=[self._it(out)])

    def tr(self, out, in_, ident):
        return self.op("pe", lambda e: e.transpose(out[1], in_[1], ident[1]),
                       reads=[self._it(in_), self._it(ident)], writes=[self._it(out)])

    def act(self, out, in_, func, bias=None, scale=None, accum=None, eng="act"):
        reads = [self._it(in_)]
        kw = {}
        if bias is not None:
            if isinstance(bias, tuple):
                reads.append(self._it(bias))
                kw["bias"] = bias[1]
            else:
                kw["bias"] = bias
        if scale is not None:
            if isinstance(scale, tuple):
                reads.append(self._it(scale))
                kw["scale"] = scale[1]
            else:
                kw["scale"] = scale
        writes = [self._it(out)]
        if accum is not None:
            writes.append(self._it(accum))
            kw["accum_out"] = accum[1]
        return self.op(eng, lambda e: e.activation(out=out[1], in_=in_[1], func=func, **kw), reads, writes)

    def tt(self, eng, out, in0, in1, op):
        return self.op(eng, lambda e: e.tensor_tensor(out=out[1], in0=in0[1], in1=in1[1], op=op),
                       reads=[self._it(in0), self._it(in1)], writes=[self._it(out)])

    def ts(self, eng, out, in0, s1, s2=None, op0=ALU.mult, op1=None, accum=None):
        reads = [self._it(in0)]
        a1 = s1
        a2 = s2
        if isinstance(s1, tuple):
            reads.append(self._it(s1))
            a1 = s1[1]
        if isinstance(s2, tuple):
            reads.append(self._it(s2))
            a2 = s2[1]
        kw = {}
        if op1 is not None:
            kw["op1"] = op1
        writes = [self._it(out)]
        if accum is not None:
            writes.append(self._it(accum))
            kw["accum_out"] = accum[1]
        return self.op(eng, lambda e: e.tensor_scalar(out=out[1], in0=in0[1], scalar1=a1, scalar2=a2, op0=op0, **kw),
                       reads, writes)

    def stt(self, eng, out, in0, scalar, in1, op0, op1):
        reads = [self._it(in0), self._it(in1)]
        a = scalar
        if isinstance(scalar, tuple):
            reads.append(self._it(scalar))
            a = scalar[1]
        return self.op(eng, lambda e: e.scalar_tensor_tensor(out=out[1], in0=in0[1], scalar=a, in1=in1[1], op0=op0, op1=op1),
                       reads, [self._it(out)])

    def copy(self, eng, out, in_):
        if eng == "act":
            return self.op(eng, lambda e: e.activation(out=out[1], in_=in_[1], func=AF.Copy),
                           reads=[self._it(in_)], writes=[self._it(out)])
        return self.op(eng, lambda e: e.tensor_copy(out=out[1], in_=in_[1]),
                       reads=[self._it(in_)], writes=[self._it(out)])

    def red(self, eng, out, in_, op, axis=AX.X):
        return self.op(eng, lambda e: e.tensor_reduce(out=out[1], in_=in_[1], axis=axis, op=op),
                       reads=[self._it(in_)], writes=[self._it(out)])

    def memset(self, eng, out, val):
        return self.op(eng, lambda e: e.memset(out[1], val), reads=[], writes=[self._it(out)])

    def scan(self, eng, out, d0, d1, init, op0, op1):
        return self.op(eng, lambda e: e.tensor_tensor_scan(out=out[1], data0=d0[1], data1=d1[1], initial=init, op0=op0, op1=op1),
                       reads=[self._it(d0), self._it(d1)], writes=[self._it(out)])


def build_program(stage=99, dbg=False):
    import os as _os
    nc = bass.Bass("TRN2", target_bir_lowering=False)
    k = K(nc)
    NT = NTP + 1

    def din(name, shape):
        return k.dram(name, shape, F32, "ExternalInput")

    def dout(name, shape):
        return k.dram(name, shape, F32, "ExternalOutput")

    xp = din("xp", [SEQ, D]); xs = din("xs", [128, D])
    pp = din("pp", [SEQ, PLE]); psm = din("psm", [128, PLE])
    st_mconv = din("st_mconv", [NSB * 3, 2 * MW])
    st_mC = din("st_mC", [NSB, MH, 128, 128])
    st_mn = din("st_mn", [NSB, MH, 128])
    st_mm = din("st_mm", [NSB, MH])
    st_rshift = din("st_rshift", [NSB, RCOLS])
    st_rS = din("st_rS", [NSB * RH, RN * RN])
    st_fconv = din("st_fconv", [NSB * 2, DFF])
    d_w_in = din("w_in", [128, 8, N_IN])
    d_w_bm = din("w_bm", [128, 4, D]); d_w_br = din("w_br", [128, 4, D])
    d_w_out = din("w_out", [128, 8, D])
    d_f_up = din("f_up", [128, 8, 2 * DFF]); d_f_down = din("f_down", [128, NFC, D])
    d_pgw = din("ple_gate_w", [128, 8, D]); d_ppj = din("ple_proj", [128, 2, D])
    d_rw2 = din("r_w2", [64, RW]); d_ra2 = din("r_a2", [128, RW]); d_rg2 = din("r_g2", [128, RW])
    d_g1 = din("norm1_g", [128, D]); d_g2 = din("norm2_g", [128, D])
    d_g3 = din("ple_norm_g", [128, D]); d_g4 = din("final_norm_g", [128, D])
    d_ptab = din("ptab", [128, 128])
    d_fctab = din("fctab", [128, 4 * NFC])
    d_gb = din("gate_bias", [4, 2])
    y_p = dout("y_p", [SEQ, D]); y_s = dout("y_s", [128, D])
    o_pconv = dout("p_conv", [3, 2 * MW]); o_pC = dout("p_C", [MH, 128, 128]); o_pn = dout("p_n", [MH, 128])
    o_pm = dout("p_m", [1, MH]); o_pshift = dout("p_shift", [1, RCOLS]); o_pS = dout("p_S", [RH * RN, RN])
    o_pfconv = dout("p_fconv", [2, DFF])
    o_sconv = dout("s_conv", [NSB * 3, 2 * MW]); o_sC = dout("s_C", [NSB, MH, 128, 128]); o_sn = dout("s_n", [NSB, MH, 128])
    o_sm = dout("s_m", [NSB, MH]); o_sshift = dout("s_shift", [NSB, RCOLS]); o_sS = dout("s_S", [NSB * RH, RN * RN])
    o_sfconv = dout("s_fconv", [NSB * 2, DFF])
    x1s = k.dram("x1_scratch", [NT * 128, D], F32)
    dbgs = {}

    def dbg_out(name, src_buf, src_ap, shape):
        if not dbg:
            return
        t = dout("dbg_" + name, shape)
        dbgs[name] = t
        k.dma("sp", t[:], src_ap, reads=[src_buf], writes=[t])

    identf = k.sbuf([128, 128], F32, "identf")
    identb = k.sbuf([128, 128], BF16, "identb")
    mark_phase = k.bot
    mU_in = [k.sbuf([128, 128], F32, f"mUin{i}") for i in range(2)]
    mU_st = [k.sbuf([128, 128], F32, f"mUst{i}") for i in range(2)]
    mL_st = [k.sbuf([128, 128], F32, f"mLst{i}") for i in range(2)]
    resets = [k.sbuf([128, 512], F32, f"resets{i}") for i in range(2)]
    ones4 = k.sbuf([4, 128], F32, "ones4")

    def aff(out_buf, out_ap, pattern, cm, base, op=ALU.is_ge):
        k.op("pool", lambda e: e.affine_select(out=out_ap, in_=out_ap, pattern=pattern, compare_op=op,
                                               fill=0.0, base=base, channel_multiplier=cm),
             reads=[out_buf], writes=[out_buf])

    k.memset("pool", (identf, identf[:]), 1.0)
    aff(identf, identf[:], [[-1, 128]], 1, 0)
    aff(identf, identf[:], [[1, 128]], -1, 0)
    k.copy("pool", (identb, identb[:]), (identf, identf[:]))
    for i in range(2):
        k.memset("pool", (mU_in[i], mU_in[i][:]), 1.0)
        aff(mU_in[i], mU_in[i][:], [[1, 128]], -1, 0)
        k.memset("pool", (mU_st[i], mU_st[i][:]), 1.0)
        aff(mU_st[i], mU_st[i][:], [[1, 128]], -1, -1)
        k.memset("pool", (mL_st[i], mL_st[i][:]), 1.0)
        aff(mL_st[i], mL_st[i][:], [[-1, 128]], 1, -1)
        k.memset("pool", (resets[i], resets[i][:]), 1.0)
    v3 = lambda b: b[:].rearrange("p (a c) -> p a c", c=ST)
    aff(mU_in[1], v3(mU_in[1]), [[-ST, 16], [0, ST]], 1, 0)
    aff(mU_st[1], v3(mU_st[1]), [[-ST, 16], [0, ST]], 1, 0)
    aff(mL_st[1], v3(mL_st[1]), [[ST, 16], [0, ST]], -1, ST - 1)
    k.memset("pool", (resets[0], resets[0][:].rearrange("p (a c) -> p a c", c=128)[:, :, 0:1]), 0.0)
    k.memset("pool", (resets[1], resets[1][:].rearrange("p (a c) -> p a c", c=ST)[:, :, 0:1]), 0.0)
    k.memset("pool", (ones4, ones4[:]), 1.0)
    mask2 = [k.sbuf([128, 256], F32, f"mask2_{i}") for i in range(2)]
    for i in range(2):
        k.copy("pool", (mask2[i], mask2[i][:, 0:128]), (mU_st[i], mU_st[i][:]))
        k.copy("pool", (mask2[i], mask2[i][:, 128:256]), (mU_in[i], mU_in[i][:]))
    I2 = k.sbuf([128, 64], F32, "I2")
    k.tt("pool", (I2, I2[:]), (identf, identf[:, 0:64]), (identf, identf[:, 64:128]), ALU.add)
    bones = k.sbuf([128, 128], F32, "bones")
    k.memset("pool", (bones, bones[:]), 0.0)
    k.memset("pool", (bones, bones[0:64, 0:64]), 1.0)
    k.memset("pool", (bones, bones[64:128, 64:128]), 1.0)

    ptab = k.sbuf([128, 128], F32, "ptab")
    k.dma("sp", ptab[:], d_ptab[:], reads=[d_ptab], writes=[ptab])
    PT_MCW, PT_MCB, PT_RMIX, PT_RW0, PT_RA0, PT_RKK, PT_RKA, PT_RRK, PT_RLNG, PT_RLNB, PT_MNG = 0, 32, 40, 54, 58, 62, 66, 70, 74, 78, 82
    pcol = lambda c: (ptab, ptab[:, c:c + 1])
    gb = k.sbuf([4, 2], F32, "gb")
    k.dma("sp", gb[:], d_gb[:], reads=[d_gb], writes=[gb])
    nbf = k.sbuf([4, 1], F32, "nbf")
    k.ts("dve", (nbf, nbf[:]), (gb, gb[:, 1:2]), -1.0, None, op0=ALU.mult)
    g1bc = k.sbuf([128, D], F32, "g1bc")
    k.dma("sp", g1bc[:], d_g1[:], reads=[d_g1], writes=[g1bc])

    NA = C_G
    hmT_all = k.sbuf([128, NT, 4, 128], BF16, "hmT_all")
    yrgT_all = k.sbuf([128, NT, 4, 128], BF16, "yrgT_all")
    mark_1a = k.bot
    W_in = k.sbuf([128, 8, NA], BF16, "W_in")
    Wl_w2 = k.sbuf([64, RW], BF16, "Wl_w2")
    Wl_a2 = k.sbuf([128, RW], BF16, "Wl_a2")
    Wl_g2 = k.sbuf([128, RW], BF16, "Wl_g2")
    GRP = {"g0": (0, 1024), "g1": (1024, 2056), "g2": (2056, 3848)}
    for g in ("g0", "g1", "g2"):
        a, b = GRP[g]
        for kh in range(2):
            k.dma("pool", W_in[:, 4 * kh:4 * kh + 4, a:b], d_w_in[:, 4 * kh:4 * kh + 4, a:b], reads=[d_w_in], writes=[(W_in, g)])
        if g == "g1":
            k.dma("pool", Wl_w2[:], d_rw2[:], reads=[d_rw2], writes=[Wl_w2])
            k.dma("pool", Wl_a2[:], d_ra2[:], reads=[d_ra2], writes=[Wl_a2])
            k.dma("pool", Wl_g2[:], d_rg2[:], reads=[d_rg2], writes=[Wl_g2])

    def wgrp(col):
        for g, (a, b) in GRP.items():
            if a <= col < b:
                return g

    pF = [k.psum([128, 512], F32, f"pF{i}") for i in range(2)]
    pR = [k.psum([128, 512], F32, f"pR{i}") for i in range(2)]
    pT = k.psum([128, 1024], BF16, "pT")
    pM = [k.psum([128, 512], F32, f"pM{i}") for i in range(3)]

    xt = [k.sbuf([128, D], F32, "xt0")] * 2
    hb = k.sbuf([128, D], BF16, "hb")
    hT = k.sbuf([128, 8, 128], BF16, "hT")
    ss = k.sbuf([128, 1], F32, "ss")
    rs = k.sbuf([128, 1], F32, "rs")
    ext_q = k.sbuf([128, 8, 131], F32, "ext_q")
    cq = k.sbuf([128, 8, 3], F32, "cq")
    _eqf = ext_q[:].rearrange("p a b -> p (a b)")
    cv = k.sbuf([128, 8, 128], F32, "cv")
    qkT = k.sbuf([128, 8, 128], BF16, "qkT")
    soT = k.sbuf([128, 4, 128], F32, "soT")
    vaug = k.sbuf([128, 4, 130], BF16, "vaug")
    Cst = k.sbuf([128, 4, 129], F32, "Cst")
    Cb = k.sbuf([128, 4, 130], BF16, "Cb")
    gsm = [k.sbuf([4, 128], F32, f"gsm{i}") for i in range(8)]
    gpk = k.sbuf([4, 3, 128], F32, "gpk")
    mst = k.sbuf([4, 16], F32, "mst")
    mnew = k.sbuf([4, 16], F32, "mnew")
    gt = [k.sbuf([4, 16], F32, f"gt{i}") for i in range(4)]
    s0d = k.sbuf([4, 4, 16], F32, "s0d")
    tokS = k.sbuf([128, 12], F32, "tokS")
    s0bc = k.sbuf([128, 64], F32, "s0bc")
    _pk = _eqf[:, 512:1024].bitcast(BF16).rearrange("p (a b c) -> p a b c", a=2, b=4)
    PTm = ext_q.view(_pk[:, 0, :, :], "PTm")
    ktm = ext_q.view(_pk[:, 1, :, :], "ktm")
    dn = k.sbuf([128, 4], F32, "dn")
    hm = ext_q.view(_eqf[:, 0:512].rearrange("p (a b) -> p a b", a=4), "hm")
    hn = hm
    bst = k.sbuf([128, 4, 6], F32, "bst")
    bag = k.sbuf([128, 4, 2], F32, "bag")
    zq_tm = cv.view(cv[:].rearrange("p a b -> p (a b)"), "zq_tm")

    ext_r = k.sbuf([128, 14, 129], F32, "ext_r")
    cr = k.sbuf([128, 14, 1], F32, "cr")
    _erf = ext_r[:].rearrange("p a b -> p (a b)")
    xm = k.sbuf([128, 14, 128], F32, "xm")
    thad = k.sbuf([128, 128], BF16, "thad")
    sgd = k.sbuf([128, 128], BF16, "sgd")
    bst8 = k.sbuf([128, 8, 6], F32, "bst8")
    bag8 = k.sbuf([128, 8, 2], F32, "bag8")
    mark_rw = k.bot
    rt = [k.sbuf([128, 4, 128], F32, f"rt{i}") for i in range(7)]
    rt.append(cv.view(cv[:, 0:4, :], "rt7"))
    rt.append(cv.view(cv[:, 4:8, :], "rt8"))
    gTs = ext_r.view(_erf[:, 0:512].rearrange("p (a b) -> p a b", a=4), "gTs")
    bonT = ext_r.view(_erf[:, 512:1024].rearrange("p (a b) -> p a b", a=4), "bonT")
    ART = k.sbuf([128, 4, 2, 128], BF16, "ART")
    BTb = k.sbuf([128, 4, 128], BF16, "BTb")
    KTb = k.sbuf([128, 4, 128], BF16, "KTb")
    VTb = k.sbuf([128, 4, 128], BF16, "VTb")
    AB_tm = k.sbuf([128, 2, 512], BF16, "AB_tm")
    KV_tm = k.sbuf([128, 2, 512], BF16, "KV_tm")
    GBm = k.sbuf([128, 4, 256], BF16, "GBm")
    GKm = k.sbuf([128, 4, 256], BF16, "GKm")
    Nn = k.sbuf([128, 4, 128], BF16, "Nn")
    GBm_b = k.sbuf([128, 4, 256], BF16, "GBm_b")
    GKm_b = k.sbuf([128, 4, 256], BF16, "GKm_b")
    Nn_b = k.sbuf([128, 4, 128], BF16, "Nn_b")
    PP_b = [k.sbuf([128, 4, 256], BF16, f"PPb{i}") for i in range(2)]
    XX_b = [k.sbuf([128, 4, 128], BF16, f"XXb{i}") for i in range(2)]
    PP = [k.sbuf([128, 4, 256], BF16, f"PP{i}") for i in range(2)]
    XX = [k.sbuf([128, 4, 128], BF16, f"XX{i}") for i in range(2)]
    QT = k.sbuf([128, 2, 128], BF16, "QT")
    IE = k.sbuf([128, 4, 64], F32, "IE")
    STf = k.sbuf([128, 4, 64], F32, "STf")
    STb = k.sbuf([128, 4, 64], BF16, "STb")
    yn = ext_r.view(_erf[:, 1024:1536].rearrange("p (a b) -> p a b", a=8), "yn")
    k.memset("pool", (STf, STf[:]), 0.0)
    k.memset("pool", (STb, STb[:]), 0.0)
    k.memset("pool", (cr, cr[:]), 0.0)
    k.memset("pool", (vaug, vaug[:]), 1.0)
    k.memset("pool", (Cst, Cst[:]), 0.0)
    k.memset("pool", (Cb, Cb[:]), 0.0)
    k.memset("pool", (mst, mst[:]), 0.0)
    k.memset("pool", (cq, cq[:]), 0.0)
    LNK = math.log(KSCALE)

    def x_rows(ti):
        if ti < NTP:
            return xp, xp[ti * 128:(ti + 1) * 128, :]
        return xs, xs[:, :]

    def norm_to_hT(xbuf, gbc):
        k.act((hb, hb[:]), (xbuf, xbuf[:]), AF.Square, accum=(ss, ss[:]))
        k.ts("dve", (rs, rs[:]), (ss, ss[:]), 1.0 / D, EPS, op0=ALU.mult, op1=ALU.add)
        k.act((rs, rs[:]), (rs, rs[:]), AF.Ln)
        k.act((rs, rs[:]), (rs, rs[:]), AF.Exp, scale=-0.5)
        k.stt("dve", (hb, hb[:]), (xbuf, xbuf[:]), (rs, rs[:, 0:1]), (gbc, gbc[:]), ALU.mult, ALU.mult)
        for kc in range(8):
            k.tr((pT, pT[:, kc * 128:(kc + 1) * 128]), (hb, hb[:, kc * 128:(kc + 1) * 128]), (identb, identb[:]))
        k.copy("act", (hT, hT[:].rearrange("p a b -> p (a b)")), (pT, pT[:, :]))

    def proj_fm(ps, ps_ap, col, M=128):
        g = wgrp(col)
        for kc in range(8):
            k.mm((ps, ps_ap), (W_in, W_in[:, kc, col:col + M], g), (hT, hT[:, kc, :]), start=(kc == 0), stop=(kc == 7))

    def proj_tm(ps, ps_ap, col, N):
        g = wgrp(col)
        for kc in range(8):
            k.mm((ps, ps_ap), (hT, hT[:, kc, :]), (W_in, W_in[:, kc, col:col + N], g), start=(kc == 0), stop=(kc == 7))

    def mixer_tile(ti):
        smp = ti == NTP
        mi = 1 if smp else 0
        NB = NSB if smp else 1
        LB = ST if smp else 128
        xb = xt[ti % 2]
        xd, xap = x_rows(ti)
        if smp:
            k.barrier()
            k.bot = mark_rw
            Cs = k.sbuf([128, NSB, 129], F32, "Cs")
            Csb = k.sbuf([128, NSB, 130], BF16, "Csb")
            qTm = k.sbuf([128, NSB, 128], BF16, "qTm")
            ktmb = k.sbuf([128, NSB, 128], BF16, "ktmb")
            blkF = k.sbuf([128, NSB, 128], BF16, "blkF")
            rowm = k.sbuf([128, NSB], F32, "rowm")
            k.memset("pool", (blkF, blkF[:]), 1.0)
            aff(blkF, blkF[:], [[-ST, NSB], [1, 128]], 0, 0)
            aff(blkF, blkF[:], [[ST, NSB], [-1, 128]], 0, ST - 1)
            k.memset("pool", (rowm, rowm[:]), 1.0)
            aff(rowm, rowm[:], [[-ST, NSB]], 1, 0)
            aff(rowm, rowm[:], [[ST, NSB]], -1, ST - 1)
            smc = cv.view(cv[:].rearrange("p a b -> p (a b)")[0:NSB * 3, :], "smc")
            ext_s = xm.view(xm[:].rearrange("p a b -> p (a b)")[:, 0:8 * NSB * 11].rearrange("p (c b t) -> p c b t", c=8, b=NSB), "ext_s")
            k.dma("sp", smc[:], st_mconv[:, :], reads=[st_mconv], writes=[smc])
            for c in range(8):
                k.tr((pM[0], pM[0][:, c * 48:(c + 1) * 48]), (smc, smc[:, c * 128:(c + 1) * 128]), (identf, identf[0:48, 0:48]))
            k.copy("act", (ext_s, ext_s[:, :, :, 0:3]), (pM[0], pM[0][:, 0:384].rearrange("p (c b j) -> p c b j", c=8, b=NSB)))
            k.dma("sp", mst[:, 0:NSB], st_mm[:, :].rearrange("b h -> h b"), reads=[st_mm], writes=[mst], allow_slow_non_contiguous=True)
        k.dma("sp", xb[:], xap, reads=[xd], writes=[xb])
        norm_to_hT(xb, g1bc)

        if smp:
            for g in range(2):
                for c in range(4):
                    proj_fm(pF[g], pF[g][:, c * 128:(c + 1) * 128], C_QK + (4 * g + c) * 128)
                k.copy("act", (ext_s, ext_s[:, 4 * g:4 * g + 4, :, 3:11]), (pF[g], pF[g][:].rearrange("p (c b t) -> p c b t", c=4, b=NSB)))
            for c in range(8):
                cvv = cv[:, c, :].rearrange("p (b t) -> p b t", t=ST)
                k.ts("dve", (cv, cvv), (ext_s, ext_s[:, c, :, 3:11]), pcol(PT_MCW + 3 * 8 + c), pcol(PT_MCB + c),
                     op0=ALU.mult, op1=ALU.add)
                for j in range(3):
                    k.stt("dve", (cv, cvv), (ext_s, ext_s[:, c, :, j:j + ST]), pcol(PT_MCW + j * 8 + c), (cv, cvv),
                          ALU.mult, ALU.add)
        if not smp:
            k.copy("pool", (ext_q, ext_q[:, :, 0:3]), (cq, cq[:]))
            for g in range(2):
                for c in range(4):
                    proj_fm(pF[g], pF[g][:, c * 128:(c + 1) * 128], C_QK + (4 * g + c) * 128)
                k.copy("act", (ext_q, ext_q[:, 4 * g:4 * g + 4, 3:131]), (pF[g], pF[g][:].rearrange("p (c t) -> p c t", c=4)))
            k.copy("pool", (cq, cq[:]), (ext_q, ext_q[:, :, 128:131]))
            for c in range(8):
                k.ts("dve", (cv, cv[:, c, :]), (ext_q, ext_q[:, c, 3:131]), pcol(PT_MCW + 3 * 8 + c), pcol(PT_MCB + c),
                     op0=ALU.mult, op1=ALU.add)
                for j in range(3):
                    k.stt("dve", (cv, cv[:, c, :]), (ext_q, ext_q[:, c, j:j + 128]), pcol(PT_MCW + j * 8 + c), (cv, cv[:, c, :]),
                          ALU.mult, ALU.add)
        k.act((qkT, qkT[:].rearrange("p a b -> p (a b)")), (cv, cv[:].rearrange("p a b -> p (a b)")), AF.Silu)

        proj_tm(pR[0], pR[0][:, :], C_V, 512)
        k.copy("act", (vaug, vaug[:, :, 0:128]), (pR[0], pR[0][:].rearrange("p (h c) -> p h c", h=4)))
        for c in range(4):
            proj_fm(pF[0], pF[0][:, c * 128:(c + 1) * 128], C_O + c * 128)
        k.act((soT, soT[:].rearrange("p a b -> p (a b)")), (pF[0], pF[0][:, :]), AF.Sigmoid)
        proj_fm(pM[0], pM[0][0:4, 0:128], C_I, M=4)
        proj_fm(pM[0], pM[0][0:4, 128:256], C_F, M=4)
        liT, nlf, ncum, gT_, t0, t1 = gsm[0], gsm[1], gsm[2], gsm[3], gsm[4], gsm[5]
        k.ts("dve", (liT, liT[:]), (pM[0], pM[0][0:4, 0:128]), (gb, gb[:, 0:1]), None, op0=ALU.add)
        k.act((t0, t0[:]), (pM[0], pM[0][0:4, 128:256]), AF.Exp, bias=(nbf, nbf[:, 0:1]), scale=-1.0)
        k.act((nlf, nlf[:]), (t0, t0[:]), AF.Ln, bias=1.0)
        k.scan("dve", (ncum, ncum[:]), (resets[mi], resets[mi][0:4, 0:128]), (nlf, nlf[:]), 0.0, ALU.mult, ALU.add)
        k.tt("dve", (gT_, gT_[:]), (liT, liT[:]), (ncum, ncum[:]), ALU.add)
        b3 = lambda buf: buf[:].rearrange("p (b l) -> p b l", l=LB)
        mcb_ = mst[:, 0:NB].unsqueeze(2).to_broadcast([4, NB, LB])
        nlast = ncum[:].rearrange("p (b l) -> p b l", l=LB)[:, :, LB - 1:LB]
        k.stt("dve", (t0, b3(t0)), (gT_, b3(gT_)), LNK, (mst, mcb_), ALU.add, ALU.subtract)
        k.act((gpk, gpk[:, 0, :]), (t0, t0[:]), AF.Exp)
        k.tt("dve", (t1, b3(t1)), (ncum, b3(ncum)), (mst, mcb_), ALU.subtract)
        k.act((gpk, gpk[:, 1, :]), (t1, t1[:]), AF.Exp)
        k.tt("dve", (t1, b3(t1)), (gT_, b3(gT_)), (ncum, nlast.to_broadcast([4, NB, LB])), ALU.subtract)
        k.red("dve", (gt[0], gt[0][:, 0:NB]), (t1, b3(t1)), ALU.max)
        k.tt("dve", (gt[1], gt[1][:, 0:NB]), (mst, mst[:, 0:NB]), (ncum, nlast.rearrange("p b o -> p (b o)")), ALU.subtract)
        k.tt("dve", (mnew, mnew[:, 0:NB]), (gt[1], gt[1][:, 0:NB]), (gt[0], gt[0][:, 0:NB]), ALU.max)
        k.tt("dve", (gt[2], gt[2][:, 0:NB]), (gt[1], gt[1][:, 0:NB]), (mnew, mnew[:, 0:NB]), ALU.subtract)
        k.act((gt[3], gt[3][:, 0:NB]), (gt[2], gt[2][:, 0:NB]), AF.Exp)
        k.tt("dve", (gpk, gpk[:, 2, :].rearrange("p (b l) -> p b l", l=LB)), (gpk, gpk[:, 0, :].rearrange("p (b l) -> p b l", l=LB)),
             (gt[3], gt[3][:, 0:NB].unsqueeze(2).to_broadcast([4, NB, LB])), ALU.mult)
        for j in range(3):
            k.tr((pM[1], pM[1][:, 4 * j:4 * j + 4]), (gpk, gpk[:, j, :]), (identf, identf[0:4, 0:4]))
        k.copy("dve", (tokS, tokS[:]), (pM[1], pM[1][:, 0:12]))
        k.tt("dve", (s0d, s0d[:, :, 0:NB]), (identf, identf[0:4, 0:4].unsqueeze(2).to_broadcast([4, 4, NB])),
             (gt[3], gt[3][:, 0:NB].unsqueeze(1).to_broadcast([4, 4, NB])), ALU.mult)
        k.mm((pM[1], pM[1][:, 16:16 + 4 * NB]), (ones4, ones4[:]), (s0d, s0d[:, :, 0:NB].rearrange("p a b -> p (a b)")))
        k.copy("dve", (s0bc, s0bc[:, 0:4 * NB]), (pM[1], pM[1][:, 16:16 + 4 * NB]))

        for h in range(4):
            k.mm((pM[0], pM[0][:, h * 128:(h + 1) * 128]), (qkT, qkT[:, 4 + h, :]), (qkT, qkT[:, h, :]))
        for h in range(4):
            k.stt("dve", (PTm, PTm[:, h, :]), (pM[0], pM[0][:, h * 128:(h + 1) * 128]), (tokS, tokS[:, h:h + 1]),
                  (mU_in[mi], mU_in[mi][:]), ALU.mult, ALU.mult)
        pO = [pM[1], pM[2]]
        oap = lambda h: pO[h // 2][:, 256 * (h % 2):256 * (h % 2) + 129]
        if not smp:
            for h in range(4):
                k.mm((pO[h // 2], oap(h)), (qkT, qkT[:, h, :]), (Cb, Cb[:, h, 0:129]), start=True, stop=False)
                k.mm((pO[h // 2], oap(h)), (PTm, PTm[:, h, :]), (vaug, vaug[:, h, 0:129]), start=False, stop=True)
        else:
            for h in range(4):
                k.tr((pT, pT[:, h * 128:(h + 1) * 128]), (qkT, qkT[:, 4 + h, :]), (identb, identb[:]))
            for h in range(4):
                k.ts("dve", (ktm, ktm[:, h, :]), (pT, pT[:, h * 128:(h + 1) * 128]), (tokS, tokS[:, 8 + h:9 + h]), None, op0=ALU.mult)
            for h in range(4):
                k.dma("sp", Cs[:, :, 0:128], st_mC[:, h, :, :].rearrange("b d v -> d b v"), reads=[st_mC], writes=[Cs])
                k.dma("sp", Cs[:, :, 128], st_mn[:, h, :].rearrange("b d -> d b"), reads=[st_mn], writes=[Cs], allow_slow_non_contiguous=True)
                k.copy("act", (Csb, Csb[:, :, 0:129]), (Cs, Cs[:]))
                k.tt("dve", (qTm, qTm[:]), (qkT, qkT[:, h, :].unsqueeze(1).to_broadcast([128, NSB, 128])), (blkF, blkF[:]), ALU.mult)
                for b in range(NSB):
                    k.mm((pO[h // 2], oap(h)), (qTm, qTm[:, b, :]), (Csb, Csb[:, b, 0:129]), start=(b == 0), stop=False)
                k.mm((pO[h // 2], oap(h)), (PTm, PTm[:, h, :]), (vaug, vaug[:, h, 0:129]), start=False, stop=True)
                k.tt("dve", (ktmb, ktmb[:]), (ktm, ktm[:, h, :].unsqueeze(1).to_broadcast([128, NSB, 128])),
                     (rowm, rowm[:].unsqueeze(2).to_broadcast([128, NSB, 128])), ALU.mult)
                for grp in range(4):
                    bank = pF[grp % 2]
                    for bi in range(4):
                        b = 4 * grp + bi
                        k.mm((bank, bank[:, bi * 128:(bi + 1) * 128]), (ktmb, ktmb[:, b, :]), (vaug, vaug[:, h, 0:128]))
                    for bi in range(4):
                        b = 4 * grp + bi
                        k.stt("dve", (Cs, Cs[:, b, 0:128]), (Cs, Cs[:, b, 0:128]), (s0bc, s0bc[:, h * NSB + b:h * NSB + b + 1]),
                              (bank, bank[:, bi * 128:(bi + 1) * 128]), ALU.mult, ALU.add)
                for b in range(NSB):
                    k.mm((pR[0], pR[0][:, b:b + 1]), (ktmb, ktmb[:, b, :]), (vaug, vaug[:, h, 128:129]))
                k.tt("dve", (Cs, Cs[:, :, 128]), (Cs, Cs[:, :, 128]), (s0bc, s0bc[:, h * NSB:(h + 1) * NSB]), ALU.mult)
                k.tt("dve", (Cs, Cs[:, :, 128]), (Cs, Cs[:, :, 128]), (pR[0], pR[0][:, 0:NSB]), ALU.add)
                k.dma("sp", o_sC[:, h, :, :].rearrange("b d v -> d b v"), Cs[:, :, 0:128], reads=[Cs], writes=[o_sC])
                k.dma("sp", o_sn[:, h, :].rearrange("b d -> d b"), Cs[:, :, 128], reads=[Cs], writes=[o_sn], allow_slow_non_contiguous=True)
        for h in range(4):
            k.copy("act", (dn, dn[:, h:h + 1]), (pO[h // 2], oap(h)[:, 128:129]))
        k.stt("dve", (dn, dn[:]), (dn, dn[:]), -1.0, (dn, dn[:]), ALU.mult, ALU.max)
        k.tt("dve", (dn, dn[:]), (dn, dn[:]), (tokS, tokS[:, 4:8]), ALU.max)
        k.op("dve", lambda e: e.reciprocal(out=dn[:], in_=dn[:]), reads=[dn], writes=[dn])
        for h in range(4):
            k.act((hm, hm[:, h, :]), (pO[h // 2], oap(h)[:, 0:128]), AF.Copy, scale=(dn, dn[:, h:h + 1]))
        for h in range(4):
            k.op("dve", lambda e, h=h: e.bn_stats(out=bst[:, h, :], in_=hm[:, h, :]), reads=[hm], writes=[(bst, h)])
        for h in range(4):
            k.op("dve", lambda e, h=h: e.bn_aggr(out=bag[:, h, :], in_=bst[:, h, :]), reads=[(bst, h)], writes=[(bag, h)])
        k.act((bag, bag[:, :, 1:2]), (bag, bag[:, :, 1:2]), AF.Ln, bias=EPS)
        k.act((bag, bag[:, :, 1:2]), (bag, bag[:, :, 1:2]), AF.Exp, scale=-0.5)
        for h in range(4):
            k.ts("dve", (hn, hn[:, h, :]), (hm, hm[:, h, :]), (bag, bag[:, h, 0:1]), (bag, bag[:, h, 1:2]),
                 op0=ALU.subtract, op1=ALU.mult)
        for h in range(4):
            k.tr((pM[0], pM[0][:, h * 128:(h + 1) * 128]), (hn, hn[:, h, :]), (identf, identf[:]))
        for h in range(4):
            k.stt("dve", (hmT_all, hmT_all[:, ti, h, :], ti), (pM[0], pM[0][:, h * 128:(h + 1) * 128]), pcol(PT_MNG + h),
                  (soT, soT[:, h, :]), ALU.mult, ALU.mult)
        if not smp:
            for h in range(4):
                k.tr((pT, pT[:, h * 128:(h + 1) * 128]), (qkT, qkT[:, 4 + h, :]), (identb, identb[:]))
            for h in range(4):
                k.ts("dve", (ktm, ktm[:, h, :]), (pT, pT[:, h * 128:(h + 1) * 128]), (tokS, tokS[:, 8 + h:9 + h]), None, op0=ALU.mult)
            for h in range(4):
                k.mm((pO[h // 2], oap(h)), (ktm, ktm[:, h, :]), (vaug, vaug[:, h, 0:129]))
            for h in range(4):
                k.stt("dve", (Cst, Cst[:, h, :]), (Cst, Cst[:, h, :]), (s0bc, s0bc[:, h:h + 1]), (pO[h // 2], oap(h)),
                      ALU.mult, ALU.add)
            k.copy("act", (Cb, Cb[:, :, 0:129]), (Cst, Cst[:]))
            k.copy("dve", (mst, mst[:, 0:1]), (mnew, mnew[:, 0:1]))
        if ti == NTP - 1:
            for h in range(4):
                k.dma("sp", o_pC[h], Cst[:, h, 0:128], reads=[Cst], writes=[o_pC])
            k.dma("sp", o_pn[:].rearrange("h d -> d h"), Cst[:, :, 128], reads=[Cst], writes=[o_pn], allow_slow_non_contiguous=True)
            k.dma("sp", o_pm[:].rearrange("o h -> h o"), mnew[:, 0:1], reads=[mnew], writes=[o_pm], allow_slow_non_contiguous=True)
            for blk in range(2):
                proj_tm(pR[blk], pR[blk][:, :], C_QK + blk * 512, 512)
                k.copy("act", (zq_tm, zq_tm[:, blk * 512:(blk + 1) * 512]), (pR[blk], pR[blk][:, :]))
            k.dma("sp", o_pconv[:], zq_tm[125:128, :], reads=[zq_tm], writes=[o_pconv])
        if smp:
            k.dma("sp", o_sm[:, :].rearrange("b h -> h b"), mnew[:, 0:NSB], reads=[mnew], writes=[o_sm], allow_slow_non_contiguous=True)
            for blk in range(2):
                proj_tm(pR[blk], pR[blk][:, :], C_QK + blk * 512, 512)
                k.copy("act", (zq_tm, zq_tm[:, blk * 512:(blk + 1) * 512]), (pR[blk], pR[blk][:, :]))
            for b in range(NSB):
                k.dma("sp", o_sconv[3 * b:3 * b + 3, :], zq_tm[ST * b + 5:ST * b + 8, :], reads=[zq_tm], writes=[o_sconv])

    k.fence_mm = (pT, identb)
    BK = [pM[0], pM[1], pF[0], pF[1], pR[0], pR[1]]

    _rw_stop = int(_os.environ.get("KDBG_RW", "99"))

    def rwkv_tile(ti):
        smp = ti == NTP
        mi = 0
        NLV = 7
        rtl = rt
        if not smp:
            k.copy("pool", (ext_r, ext_r[:, :, 0:1]), (cr, cr[:]))
            for g in range(4):
                n = min(4, 14 - 4 * g)
                for c in range(n):
                    proj_fm(pF[g % 2], pF[g % 2][:, c * 128:(c + 1) * 128], C_R + (4 * g + c) * 128)
                k.copy("act", (ext_r, ext_r[:, 4 * g:4 * g + n, 1:129]),
                       (pF[g % 2], pF[g % 2][:, 0:n * 128].rearrange("p (c t) -> p c t", c=n)))
            k.copy("pool", (cr, cr[:]), (ext_r, ext_r[:, :, 128:129]))
            k.tt("pool", (xm, xm[:]), (ext_r, ext_r[:, :, 0:128]), (ext_r, ext_r[:, :, 1:129]), ALU.subtract)
            for c in range(14):
                k.stt("dve", (xm, xm[:, c, :]), (xm, xm[:, c, :]), pcol(PT_RMIX + c), (ext_r, ext_r[:, c, 1:129]), ALU.mult, ALU.add)
        else:
            k.barrier()
            k.bot = mark_rw
            ext_rs = k.sbuf([128, 14, NSB, ST + 1], F32, "ext_rs")
            rtl = [k.sbuf([128, 4, 128], F32, f"rts{i}") for i in range(7)] + [rt[7], rt[8]]
            stg = k.sbuf([128, 512], F32, "stg")
            srs = xm.view(xm[:].rearrange("p a b -> p (a b)")[0:NSB, :], "srs")
            k.dma("sp", srs[:], st_rshift[:, :], reads=[st_rshift], writes=[srs])
            for c in range(14):
                k.tr((pM[0], pM[0][:, c * NSB:(c + 1) * NSB]), (srs, srs[:, c * 128:(c + 1) * 128]), (identf, identf[0:NSB, 0:NSB]))
            k.copy("act", (ext_rs, ext_rs[:, :, :, 0]), (pM[0], pM[0][:, 0:14 * NSB].rearrange("p (c b) -> p c b", c=14)))
            for g in range(4):
                n = min(4, 14 - 4 * g)
                for c in range(n):
                    proj_fm(pF[g % 2], pF[g % 2][:, c * 128:(c + 1) * 128], C_R + (4 * g + c) * 128)
                k.copy("act", (ext_rs, ext_rs[:, 4 * g:4 * g + n, :, 1:ST + 1]),
                       (pF[g % 2], pF[g % 2][:, 0:n * 128].rearrange("p (c b t) -> p c b t", c=n, b=NSB)))
            xm4 = xm[:].rearrange("p c (b t) -> p c b t", t=ST)
            k.tt("pool", (xm, xm4), (ext_rs, ext_rs[:, :, :, 0:ST]), (ext_rs, ext_rs[:, :, :, 1:ST + 1]), ALU.subtract)
            for c in range(14):
                k.stt("dve", (xm, xm4[:, c]), (xm, xm4[:, c]), pcol(PT_RMIX + c), (ext_rs, ext_rs[:, c, :, 1:ST + 1]), ALU.mult, ALU.add)
        rT, krT, vrT = xm[:, 0:4, :], xm[:, 4:8, :], xm[:, 8:12, :]
        sig, cums, gam, ginv, gexc, a_, kk, tmp, kr2 = rtl
        if _rw_stop <= 1:
            return
        k.act((thad, thad[0:64, :]), (xm, xm[0:64, 12, :]), AF.Tanh)
        k.copy("act", (thad, thad[64:128, :]), (xm, xm[64:128, 12, :]))
        k.act((sgd, sgd[:]), (xm, xm[:, 13, :]), AF.Sigmoid)
        for c in range(4):
            k.mm((pM[0], pM[0][:, c * 128:(c + 1) * 128]), (Wl_w2, Wl_w2[0:64, c * 128:(c + 1) * 128]), (thad, thad[0:64, :]))
        for c in range(4):
            k.act((sig, sig[:, c, :]), (pM[0], pM[0][:, c * 128:(c + 1) * 128]), AF.Sigmoid, bias=pcol(PT_RW0 + c))
        k.pe_fence()
        for c in range(4):
            k.mm((pM[1], pM[1][:, c * 128:(c + 1) * 128]), (Wl_a2, Wl_a2[64:128, c * 128:(c + 1) * 128]), (thad, thad[64:128, :]))
        k.pe_fence()
        for c in range(4):
            k.act((a_, a_[:, c, :]), (pM[1], pM[1][:, c * 128:(c + 1) * 128]), AF.Sigmoid, bias=pcol(PT_RA0 + c))
        for c in range(4):
            k.mm((pM[2], pM[2][:, c * 128:(c + 1) * 128]), (Wl_g2, Wl_g2[:, c * 128:(c + 1) * 128]), (sgd, sgd[:]))
        k.copy("act", (gTs, gTs[:].rearrange("p a b -> p (a b)")), (pM[2], pM[2][:, :]))
        if _rw_stop <= 2:
            return
        fl = lambda b: b[:].rearrange("p a b -> p (a b)")
        if not smp:
            k.scan("dve", (cums, fl(cums)), (resets[mi], resets[mi][:]), (sig, fl(sig)), 0.0, ALU.mult, ALU.add)
            k.act((gam, fl(gam)), (cums, fl(cums)), AF.Exp, scale=WSCALE)
            k.act((ginv, fl(ginv)), (cums, fl(cums)), AF.Exp, scale=-WSCALE)
            k.tt("pool", (tmp, tmp[:]), (cums, cums[:]), (sig, sig[:]), ALU.subtract)
            k.act((gexc, fl(gexc)), (tmp, fl(tmp)), AF.Exp, scale=WSCALE)
        else:
            k.act((gam, fl(gam)), (sig, fl(sig)), AF.Exp, scale=WSCALE)
        if _rw_stop <= 3:
            return
        for c in range(4):
            k.ts("dve", (kk, kk[:, c, :]), (xm, xm[:, 4 + c, :]), pcol(PT_RKK + c), None, op0=ALU.mult)
        k.tt("pool", (tmp, tmp[:]), (kk, kk[:]), (kk, kk[:]), ALU.mult)
        for c in range(4):
            k.mm((pM[0], pM[0][:, c * 128:(c + 1) * 128]), (bones, bones[:]), (tmp, tmp[:, c, :]))
        k.ts("dve", (tmp, fl(tmp)), (pM[0], pM[0][:, :]), 1e-24, None, op0=ALU.max)
        k.act((tmp, fl(tmp)), (tmp, fl(tmp)), AF.Ln)
        k.act((tmp, fl(tmp)), (tmp, fl(tmp)), AF.Exp, scale=-0.5)
        k.tt("dve", (kk, kk[:]), (kk, kk[:]), (tmp, tmp[:]), ALU.mult)
        for c in range(4):
            k.ts("dve", (tmp, tmp[:, c, :]), (a_, a_[:, c, :]), -1.0, pcol(PT_RKA + c), op0=ALU.add, op1=ALU.mult)
        k.stt("dve", (kr2, kr2[:]), (tmp, tmp[:]), 1.0, (xm, krT), ALU.add, ALU.mult)
        k.tt("pool", (tmp, tmp[:]), (xm, rT), (kr2, kr2[:]), ALU.mult)
        for c in range(4):
            k.ts("dve", (tmp, tmp[:, c, :]), (tmp, tmp[:, c, :]), pcol(PT_RRK + c), None, op0=ALU.mult)
        for c in range(4):
            k.mm((pM[1], pM[1][:, c * 128:(c + 1) * 128]), (bones, bones[:]), (tmp, tmp[:, c, :]))
        k.tt("dve", (bonT, fl(bonT)), (pM[1], pM[1][:, :]), (xm, vrT.rearrange("p a b -> p (a b)") if False else xm[:, 8:12, :].rearrange("p a b -> p (a b)")), ALU.mult)
        if _rw_stop <= 4:
            return
        if smp:
            rwkv_sample_core(xm, gam, kr2, kk, a_, tmp, stg, gTs, bonT)
            return
        k.stt("dve", (ART, ART[:, :, 0, :]), (kk, kk[:]), -1.0, (gexc, gexc[:]), ALU.mult, ALU.mult)
        k.tt("pool", (ART, ART[:, :, 1, :]), (xm, rT), (gam, gam[:]), ALU.mult)
        k.tt("pool", (tmp, tmp[:]), (kk, kk[:]), (a_, a_[:]), ALU.mult)
        k.tt("dve", (BTb, BTb[:]), (tmp, tmp[:]), (ginv, ginv[:]), ALU.mult)
        k.tt("pool", (KTb, KTb[:]), (kr2, kr2[:]), (ginv, ginv[:]), ALU.mult)
        k.copy("act", (VTb, VTb[:]), (xm, vrT))
        if _rw_stop <= 5:
            return
        for c in range(4):
            k.tr((pT, pT[:, c * 128:(c + 1) * 128]), (ART, ART[:, c, 0, :]), (identb, identb[:]))
            k.tr((pT, pT[:, 512 + c * 128:512 + (c + 1) * 128]), (BTb, BTb[:, c, :]), (identb, identb[:]))
        k.copy("act", (AB_tm, AB_tm[:].rearrange("p a b -> p (a b)")), (pT, pT[:, :]))
        for c in range(4):
            k.tr((pT, pT[:, c * 128:(c + 1) * 128]), (KTb, KTb[:, c, :]), (identb, identb[:]))
            k.tr((pT, pT[:, 512 + c * 128:512 + (c + 1) * 128]), (VTb, VTb[:, c, :]), (identb, identb[:]))
        k.copy("dve", (KV_tm, KV_tm[:].rearrange("p a b -> p (a b)")), (pT, pT[:, :]))
        if _rw_stop <= 6:
            return
        A_tm = lambda h: (AB_tm, AB_tm[:, 0, h * 64:(h + 1) * 64])
        B_tm = lambda h: (AB_tm, AB_tm[:, 1, h * 64:(h + 1) * 64])
        K_tm = lambda h: (KV_tm, KV_tm[:, 0, h * 64:(h + 1) * 64])
        V_tm = lambda h: (KV_tm, KV_tm[:, 1, h * 64:(h + 1) * 64])
        m2b = mask2[mi][:].unsqueeze(1).to_broadcast([128, 2, 256])
        GB2, GK2, Nn2, PP2, XX2 = [GBm, GBm_b], [GKm, GKm_b], [Nn, Nn_b], [PP, PP_b], [XX, XX_b]
        LB3 = [[pM[0], pM[1], pR[0]], [pF[0], pF[1], pR[1]]]
        for g in range(2):
            GBm_, GKm_, Nn_, XX_ = GB2[g], GK2[g], Nn2[g], XX2[g]
            heads = [4 * g + i for i in range(4)]
            HO = [(pbs, [(i, h) for i, h in enumerate(heads) if 64 * (h % 2) == pbs]) for pbs in (0, 64)]
            for pbs, hl in HO:
                for i, h in hl:
                    c, pb = h // 2, 64 * (h % 2)
                    off = (i % 2) * 256
                    rAR = (ART, ART[pb:pb + 64, c, :, :].rearrange("p a t -> p (a t)"))
                    k.mm((BK[i // 2], BK[i // 2][:, off:off + 256]), (BTb, BTb[pb:pb + 64, c, :]), rAR)
                    k.mm((BK[2 + i // 2], BK[2 + i // 2][:, off:off + 256]), (KTb, KTb[pb:pb + 64, c, :]), rAR)
                    k.mm((BK[4], BK[4][:, i * 128:(i + 1) * 128]), (ART, ART[pb:pb + 64, c, 0, :]), (BTb, BTb[pb:pb + 64, c, :]))
                k.pe_fence()
            for hf in range(2):
                k.tt("dve", (GBm_, GBm_[:, 2 * hf:2 * hf + 2, :]), (BK[hf], BK[hf][:].rearrange("p (a b) -> p a b", a=2)), (mask2[mi], m2b), ALU.mult)
                k.tt("dve", (GKm_, GKm_[:, 2 * hf:2 * hf + 2, :]), (BK[2 + hf], BK[2 + hf][:].rearrange("p (a b) -> p a b", a=2)), (mask2[mi], m2b), ALU.mult)
            k.tt("dve", (Nn_, Nn_[:]), (BK[4], BK[4][:].rearrange("p (a b) -> p a b", a=4)),
                 (mL_st[mi], mL_st[mi][:].unsqueeze(1).to_broadcast([128, 4, 128])), ALU.mult)
            for i, h in enumerate(heads):
                k.mm((BK[5], BK[5][:, i * 64:(i + 1) * 64]), (GKm_, GKm_[:, i, 0:128]), V_tm(h))
            k.copy("act", (XX_[0], XX_[0][:, :, 64:128]), (BK[5], BK[5][:, 0:256].rearrange("p (a b) -> p a b", a=4)))
            k.copy("pool", (XX_[0], XX_[0][:, :, 0:64]), (AB_tm, AB_tm[:, 0, 256 * g:256 * g + 256].rearrange("p (a b) -> p a b", a=4)))

        xfinal = [None, None]

        def levels_gen(g):
            GBm_, Nn_, PP_, XX_ = GB2[g], Nn2[g], PP2[g], XX2[g]
            bP, bQ, bX = LB3[g]
            Pc = lambda i: (Nn_, Nn_[:, i, :])
            PTc = lambda i: (GBm_, GBm_[:, i, 0:128])
            xi = 0
            for lvl in range(NLV):
                Xc, Xn = XX_[xi], XX_[1 - xi]
                for i in range(4):
                    o = (bX, bX[:, i * 128:(i + 1) * 128])
                    k.mm(o, (identb, identb[:]), (Xc, Xc[:, i, :]), start=True, stop=False)
                    k.mm(o, PTc(i), (Xc, Xc[:, i, :]), start=False, stop=True)
                k.copy("act", (Xn, Xn[:].rearrange("p a b -> p (a b)")), (bX, bX[:, :]))
                xi = 1 - xi
                yield
                if lvl < NLV - 1:
                    bb = [bP, bQ]
                    for i in range(4):
                        off = (i % 2) * 256
                        if lvl < NLV - 2:
                            k.mm((bb[i // 2], bb[i // 2][:, off:off + 128]), PTc(i), Pc(i))
                        k.mm((bb[i // 2], bb[i // 2][:, off + 128:off + 256]), Pc(i), PTc(i))
                    PPn = PP_[lvl % 2]
                    for hf in range(2):
                        if lvl < NLV - 2:
                            k.copy("dve", (PPn, PPn[:, 2 * hf:2 * hf + 2, :]), (bb[hf], bb[hf][:].rearrange("p (a b) -> p a b", a=2)))
                        else:
                            k.copy("dve", (PPn, PPn[:, 2 * hf:2 * hf + 2, 128:256]),
                                   (bb[hf], bb[hf][:].rearrange("p (a b) -> p a b", a=2)[:, :, 128:256]))
                    Pc = lambda i, PPn=PPn: (PPn, PPn[:, i, 0:128])
                    PTc = lambda i, PPn=PPn: (PPn, PPn[:, i, 128:256])
                    yield
            xfinal[g] = XX_[xi]

        gens = [levels_gen(0), levels_gen(1)]
        while gens:
            for g_ in list(gens):
                try:
                    next(g_)
                except StopIteration:
                    gens.remove(g_)

        for g in range(2):
            heads = [4 * g + i for i in range(4)]
            HO = [(pbs, [(i, h) for i, h in enumerate(heads) if 64 * (h % 2) == pbs]) for pbs in (0, 64)]
            GBt, GKt = GB2[g], GK2[g]
            Xf = xfinal[g]
            if _rw_stop <= 8:
                continue
            k.pe_fence()
            for pbs, hl in HO:
                for i, h in hl:
                    c, pb = h // 2, 64 * (h % 2)
                    ci = i // 2
                    o = (BK[0], BK[0][pb:pb + 64, ci * 128:(ci + 1) * 128])
                    k.mm(o, (Xf, Xf[:, i, 0:64]), (GBt, GBt[:, i, 128:256]), start=True, stop=False)
                    k.pe_fence()
                    k.mm(o, (identb, identb[pb:pb + 64, pb:pb + 64]), (ART, ART[pb:pb + 64, c, 1, :]), start=False, stop=True)
                    k.pe_fence()
            k.copy("act", (QT, QT[:].rearrange("p a b -> p (a b)")), (BK[0], BK[0][:, 0:256]))
            for pbs, hl in HO:
                for i, h in hl:
                    c, pb = h // 2, 64 * (h % 2)
                    ci = i // 2
                    o = (pM[2], pM[2][:, h * 64:(h + 1) * 64])
                    k.mm(o, (QT, QT[pb:pb + 64, ci, :]), (STb, STb[pb:pb + 64, c, :]), start=True, stop=False)
                    k.pe_fence()
                    k.mm(o, (GBt, GBt[:, i, 128:256]), (Xf, Xf[:, i, 64:128]), start=False, stop=False)
                    k.mm(o, (GKt, GKt[:, i, 128:256]), V_tm(h), start=False, stop=True)
                    k.pe_fence()
            if _rw_stop <= 9:
                continue
            for pbs, hl in HO:
                for i, h in hl:
                    c, pb = h // 2, 64 * (h % 2)
                    ci = i // 2
                    k.mm((BK[1], BK[1][pb:pb + 64, ci * 64:(ci + 1) * 64]), (Xf, Xf[:, i, 0:64]), B_tm(h))
                k.pe_fence()
            k.tt("dve", (IE, IE[:, 2 * g:2 * g + 2, :]), (BK[1], BK[1][:, 0:128].rearrange("p (a b) -> p a b", a=2)),
                 (I2, I2[:].unsqueeze(1).to_broadcast([128, 2, 64])), ALU.add)
            for pbs, hl in HO:
                for i, h in hl:
                    c, pb = h // 2, 64 * (h % 2)
                    ci = i // 2
                    o = (BK[2], BK[2][pb:pb + 64, ci * 64:(ci + 1) * 64])
                    k.mm(o, (IE, IE[pb:pb + 64, c, :]), (STf, STf[pb:pb + 64, c, :]), start=True, stop=False)
                    k.pe_fence()
                    k.mm(o, B_tm(h), (Xf, Xf[:, i, 64:128]), start=False, stop=False)
                    k.mm(o, K_tm(h), V_tm(h), start=False, stop=True)
                    k.pe_fence()
            for ci in range(2):
                c = 2 * g + ci
                k.ts("dve", (STf, STf[:, c, :]), (BK[2], BK[2][:, ci * 64:(ci + 1) * 64]), (gam, gam[:, c, 127:128]), None, op0=ALU.mult)
            k.copy("act", (STb, STb[:, 2 * g:2 * g + 2, :]), (STf, STf[:, 2 * g:2 * g + 2, :]))
        if _rw_stop <= 10:
            return
        rwkv_epilogue(ti, pM[2], tmp)
        if ti == NTP - 1:
            for c in range(4):
                k.tr((pM[0], pM[0][0:64, c * 128:(c + 1) * 128]), (STf, STf[:, c, :]), (identf, identf[:]))
            k.copy("act", (rt[0], rt[0][0:64, :, :]), (pM[0], pM[0][0:64, :].rearrange("p (a b) -> p a b", a=4)))
            k.dma("sp", o_pS[:].rearrange("(h i) j -> i h j", h=8), rt[0][0:64, :, :].rearrange("p c (f j) -> p (c f) j", f=2),
                  reads=[rt[0]], writes=[o_pS])

    def rwkv_epilogue(ti, Yb, tmp):
        pM2 = [None, None, Yb]
        for h in range(8):
            k.op("dve", lambda e, h=h: e.bn_stats(out=bst8[:, h, :], in_=Yb[:, h * 64:(h + 1) * 64]), reads=[Yb], writes=[(bst8, h)])
        for h in range(8):
            k.op("dve", lambda e, h=h: e.bn_aggr(out=bag8[:, h, :], in_=bst8[:, h, :]), reads=[(bst8, h)], writes=[(bag8, h)])
        k.act((bag8, bag8[:, :, 1:2]), (bag8, bag8[:, :, 1:2]), AF.Ln, bias=GN_EPS)
        k.act((bag8, bag8[:, :, 1:2]), (bag8, bag8[:, :, 1:2]), AF.Exp, scale=-0.5)
        for h in range(8):
            k.ts("dve", (yn, yn[:, h, :]), (Yb, Yb[:, h * 64:(h + 1) * 64]), (bag8, bag8[:, h, 0:1]), (bag8, bag8[:, h, 1:2]),
                 op0=ALU.subtract, op1=ALU.mult)
        for c in range(4):
            k.tr((pM[0], pM[0][:, c * 128:(c + 1) * 128]), (yn, yn[:, 2 * c:2 * c + 2, :].rearrange("p a b -> p (a b)")), (identf, identf[:]))
        for c in range(4):
            k.ts("dve", (tmp, tmp[:, c, :]), (pM[0], pM[0][:, c * 128:(c + 1) * 128]), pcol(PT_RLNG + c), pcol(PT_RLNB + c),
                 op0=ALU.mult, op1=ALU.add)
        k.tt("pool", (tmp, tmp[:]), (tmp, tmp[:]), (bonT, bonT[:]), ALU.add)
        k.tt("dve", (yrgT_all, yrgT_all[:, ti, :, :], ti), (tmp, tmp[:]), (gTs, gTs[:]), ALU.mult)

    rsc = k.dram("rw_scratch", [6, 128, RW], F32)
    ysc = k.dram("ry_scratch", [128, RW], F32)

    def rwkv_sample_core(xm, dec, kr2, kk, a_, tmp, stg, gTs, bonT):
        ti = NTP
        srcs = []
        srcs.append((xm, lambda c: xm[:, c, :]))
        srcs.append((dec, lambda c: dec[:, c, :]))
        srcs.append((kr2, lambda c: kr2[:, c, :]))
        srcs.append((xm, lambda c: xm[:, 8 + c, :]))
        for q in range(6):
            if q == 4:
                k.ts("dve", (tmp, tmp[:]), (kk, kk[:]), -1.0, None, op0=ALU.mult)
                sb_, fn = tmp, (lambda c: tmp[:, c, :])
            elif q == 5:
                k.tt("dve", (tmp, tmp[:]), (kk, kk[:]), (a_, a_[:]), ALU.mult)
                sb_, fn = tmp, (lambda c: tmp[:, c, :])
            else:
                sb_, fn = srcs[q]
            pb_ = pM[q % 2]
            for c in range(4):
                k.tr((pb_, pb_[:, c * 128:(c + 1) * 128]), (sb_, fn(c)), (identf, identf[:]))
            k.copy("act", (stg, stg[:]), (pb_, pb_[:, :]))
            k.dma("sp", rsc[q], stg[:], reads=[stg], writes=[(rsc, q)])
        for blk, (c0, n) in enumerate(((0, 512), (512, 512), (1024, 512), (1536, 256))):
            proj_tm(pR[blk % 2], pR[blk % 2][:, 0:n], C_R + c0, n)
            k.copy("act", (stg, stg[:, 0:n]), (pR[blk % 2], pR[blk % 2][:, 0:n]))
            for b in range(NSB):
                k.dma("sp", o_sshift[b:b + 1, c0:c0 + n], stg[ST * b + ST - 1:ST * b + ST, 0:n], reads=[stg], writes=[o_sshift])
        k.barrier()
        k.bot = mark_rw
        vec6 = k.sbuf([128, 6, ST, RN], F32, "vec6")
        Ssb = k.sbuf([128, RN, RN], F32, "Ssb")
        tmpS = k.sbuf([128, RN, RN], F32, "tmpS")
        sa = k.sbuf([128, RN], F32, "sa")
        ys = k.sbuf([128, ST, RN], F32, "ys")
        Ytm = k.sbuf([128, RW], F32, "Ytm")
        k.dma("sp", Ssb[:].rearrange("p a b -> p (a b)"), st_rS[:, :], reads=[st_rS], writes=[Ssb])
        for q in range(6):
            for b in range(NSB):
                k.dma("sp", vec6[RH * b:RH * b + RH, q, :, :], rsc[q, ST * b:ST * b + ST, :].rearrange("t (h j) -> h t j", h=RH),
                      reads=[(rsc, q)], writes=[(vec6, q)])
        HV = RN // 2

        def rec_gen(hf):
            i0 = hf * HV
            S_ = (Ssb, Ssb[:, i0:i0 + HV, :], hf)
            T_ = (tmpS, tmpS[:, i0:i0 + HV, :], hf)
            bc = lambda q, t: (vec6, vec6[:, q, t, :].unsqueeze(1).to_broadcast([128, HV, RN]), q)
            for t in range(ST):
                k.tt("dve", T_, S_, bc(4, t), ALU.mult)
                yield
                k.red("dve", (sa, sa[:, i0:i0 + HV], hf), T_, ALU.add)
                yield
                k.tt("pool", S_, S_, bc(1, t), ALU.mult)
                yield
                k.tt("dve", T_, (sa, sa[:, i0:i0 + HV].unsqueeze(2).to_broadcast([128, HV, RN]), hf), bc(5, t), ALU.mult)
                yield
                k.tt("dve", S_, S_, T_, ALU.add)
                yield
                k.tt("pool", T_, (vec6, vec6[:, 3, t, i0:i0 + HV].unsqueeze(2).to_broadcast([128, HV, RN]), 3), bc(2, t), ALU.mult)
                yield
                k.tt("dve", S_, S_, T_, ALU.add)
                yield
                k.tt("pool", T_, S_, bc(0, t), ALU.mult)
                yield
                k.red("dve", (ys, ys[:, t, i0:i0 + HV], (hf, t)), T_, ALU.add)
                yield

        gens_ = [rec_gen(0), rec_gen(1)]
        while gens_:
            for g_ in list(gens_):
                try:
                    next(g_)
                except StopIteration:
                    gens_.remove(g_)
        k.dma("sp", o_sS[:, :], Ssb[:].rearrange("p a b -> p (a b)"), reads=[Ssb], writes=[o_sS])
        k.dma("sp", ysc[:, :], ys[:].rearrange("p a b -> p (a b)"), reads=[ys], writes=[ysc])
        for b in range(NSB):
            k.dma("sp", Ytm[ST * b:ST * b + ST, :].rearrange("t (h i) -> t h i", h=RH),
                  ysc[RH * b:RH * b + RH, :].rearrange("h (t i) -> t h i", t=ST), reads=[ysc], writes=[Ytm])
        rwkv_epilogue(ti, Ytm, rt[7])

    def tail_rows(ti):
        if ti != NTP - 1:
            return
        for blk, (c0, n) in enumerate(((0, 512), (512, 512), (1024, 512), (1536, 256))):
            proj_tm(pR[blk % 2], pR[blk % 2][:, 0:n], C_R + c0, n)
            k.copy("act", (rt[1], rt[1][96:128, :, :].rearrange("p a b -> p (a b)")[:, 0:n]), (pR[blk % 2], pR[blk % 2][96:128, 0:n]))
            k.dma("sp", o_pshift[0:1, c0:c0 + n], rt[1][127:128, :, :].rearrange("p a b -> p (a b)")[:, 0:n], reads=[rt[1]], writes=[o_pshift])

    tiles_all = list(range(NT)) if stage >= 5 else list(range(NTP))

    def phase_1b(k):
        k.barrier()
        k.bot = mark_1a
        W_g = k.sbuf([128, 8, 2048], BF16, "W_g")
        W_bm = k.sbuf([128, 4, D], BF16, "W_bm")
        W_br = k.sbuf([128, 4, D], BF16, "W_br")
        W_out = k.sbuf([128, 8, D], BF16, "W_out")
        k.dma("pool", W_bm[:], d_w_bm[:], reads=[d_w_bm], writes=[W_bm])
        for kh in range(2):
            k.dma("pool", W_g[:, 4 * kh:4 * kh + 4, 0:1024], d_w_in[:, 4 * kh:4 * kh + 4, C_G:C_G + 1024], reads=[d_w_in], writes=[(W_g, "a")])
        k.dma("pool", W_br[:], d_w_br[:], reads=[d_w_br], writes=[W_br])
        for kh in range(2):
            k.dma("pool", W_g[:, 4 * kh:4 * kh + 4, 1024:2048], d_w_in[:, 4 * kh:4 * kh + 4, C_G + 1024:C_G + 2048], reads=[d_w_in], writes=[(W_g, "b")])
        k.dma("pool", W_out[:], d_w_out[:], reads=[d_w_out], writes=[W_out])
        xtb = [k.sbuf([128, D], F32, f"xtb{i}") for i in range(2)]
        hb2 = k.sbuf([128, D], BF16, "hb2")
        hT2 = k.sbuf([128, 8, 128], BF16, "hT2")
        ss2 = k.sbuf([128, 1], F32, "ss2")
        rs2 = k.sbuf([128, 1], F32, "rs2")
        sgb = [k.sbuf([128, 512], F32, f"sgb{i}") for i in range(2)]
        yab = k.sbuf([128, D], F32, "yab")
        mg = k.sbuf([128, D], BF16, "mg")
        mT = k.sbuf([128, 8, 128], BF16, "mT")
        for ti in tiles_all:
            xb = xtb[ti % 2]
            xd, xap = x_rows(ti)
            k.dma("sp", xb[:], xap, reads=[xd], writes=[xb])
            norm_generic(xb, g1bc, hb2, hT2, ss2, rs2)
            for half, (Wb, src, key) in enumerate(((W_bm, hmT_all, "a"), (W_br, yrgT_all, "b"))):
                for blk in range(2):
                    for kc in range(4):
                        k.mm((pR[blk], pR[blk][:, :]), (src, src[:, ti, kc, :], ti), (Wb, Wb[:, kc, blk * 512:(blk + 1) * 512]),
                             start=(kc == 0), stop=(kc == 3))
                    col = half * 1024 + blk * 512
                    for kc in range(8):
                        k.mm((pF[blk], pF[blk][:, :]), (hT2, hT2[:, kc, :]), (W_g, W_g[:, kc, col:col + 512], key),
                             start=(kc == 0), stop=(kc == 7))
                    k.act((sgb[blk], sgb[blk][:]), (pF[blk], pF[blk][:, :]), AF.Sigmoid)
                    if half == 0:
                        k.tt("dve", (yab, yab[:, blk * 512:(blk + 1) * 512]), (sgb[blk], sgb[blk][:]), (pR[blk], pR[blk][:, :]), ALU.mult)
                    else:
                        k.tt("dve", (sgb[blk], sgb[blk][:]), (sgb[blk], sgb[blk][:]), (pR[blk], pR[blk][:, :]), ALU.mult)
                        k.tt("pool", (mg, mg[:, blk * 512:(blk + 1) * 512]), (sgb[blk], sgb[blk][:]), (yab, yab[:, blk * 512:(blk + 1) * 512]), ALU.add)
            for kc in range(8):
                k.tr((pT, pT[:, kc * 128:(kc + 1) * 128]), (mg, mg[:, kc * 128:(kc + 1) * 128]), (identb, identb[:]))
            k.copy("act", (mT, mT[:].rearrange("p a b -> p (a b)")), (pT, pT[:, :]))
            for blk in range(2):
                for kc in range(8):
                    k.mm((pM[blk], pM[blk][:, :]), (mT, mT[:, kc, :]), (W_out, W_out[:, kc, blk * 512:(blk + 1) * 512]),
                         start=(kc == 0), stop=(kc == 7))
                k.tt("dve", (xb, xb[:, blk * 512:(blk + 1) * 512]), (xb, xb[:, blk * 512:(blk + 1) * 512]), (pM[blk], pM[blk][:, :]), ALU.add)
            k.dma("sp", x1s[ti * 128:(ti + 1) * 128, :], xb[:], reads=[xb], writes=[(x1s, ti)])

    def norm_generic(xbuf, gbc, hb_, hT_, ss_, rs_):
        k.act((hb_, hb_[:]), (xbuf, xbuf[:]), AF.Square, accum=(ss_, ss_[:]))
        k.ts("dve", (rs_, rs_[:]), (ss_, ss_[:]), 1.0 / D, EPS, op0=ALU.mult, op1=ALU.add)
        k.act((rs_, rs_[:]), (rs_, rs_[:]), AF.Ln)
        k.act((rs_, rs_[:]), (rs_, rs_[:]), AF.Exp, scale=-0.5)
        k.stt("dve", (hb_, hb_[:]), (xbuf, xbuf[:]), (rs_, rs_[:, 0:1]), (gbc, gbc[:]), ALU.mult, ALU.mult)
        for kc in range(8):
            k.tr((pT, pT[:, kc * 128:(kc + 1) * 128]), (hb_, hb_[:, kc * 128:(kc + 1) * 128]), (identb, identb[:]))
        k.copy("act", (hT_, hT_[:].rearrange("p a b -> p (a b)")), (pT, pT[:, :]))

    def phase_2(k):
        k.barrier()
        k.bot = mark_phase
        F_up = k.sbuf([128, 8, 2 * DFF], BF16, "F_up")
        F_dn = k.sbuf([128, NFC, D], BF16, "F_dn")
        PGW = k.sbuf([128, 8, D], BF16, "PGW")
        PPJ = k.sbuf([128, 2, D], BF16, "PPJ")
        NG = 4
        CW = DFF // NG
        for g in range(NG):
            for part in range(2):
                k.dma("pool", F_up[:, :, part * DFF + g * CW:part * DFF + (g + 1) * CW], d_f_up[:, :, part * DFF + g * CW:part * DFF + (g + 1) * CW],
                      reads=[d_f_up], writes=[(F_up, g)])
        for g in range(2):
            k.dma("pool", F_dn[:, 11 * g:11 * g + 11, :], d_f_down[:, 11 * g:11 * g + 11, :], reads=[d_f_down], writes=[(F_dn, g)])
        k.dma("pool", PGW[:], d_pgw[:], reads=[d_pgw], writes=[PGW])
        k.dma("pool", PPJ[:], d_ppj[:], reads=[d_ppj], writes=[PPJ])
        g2bc = k.sbuf([128, D], F32, "g2bc")
        g3bc = k.sbuf([128, D], F32, "g3bc")
        g4bc = k.sbuf([128, D], F32, "g4bc")
        fct = k.sbuf([128, 4 * NFC], F32, "fct")
        k.dma("sp", g2bc[:], d_g2[:], reads=[d_g2], writes=[g2bc])
        k.dma("sp", g3bc[:], d_g3[:], reads=[d_g3], writes=[g3bc])
        k.dma("sp", g4bc[:], d_g4[:], reads=[d_g4], writes=[g4bc])
        k.dma("sp", fct[:], d_fctab[:], reads=[d_fctab], writes=[fct])
        fcol = lambda c: (fct, fct[:, c:c + 1])
        xq = [k.sbuf([128, D], F32, f"xq{i}") for i in range(2)]
        hb3 = k.sbuf([128, D], BF16, "hb3")
        hT3s = [k.sbuf([128, 8, 128], BF16, f"hT3a{i}") for i in range(2)]
        ss3 = k.sbuf([128, 1], F32, "ss3")
        rs3 = k.sbuf([128, 1], F32, "rs3")
        gT = k.sbuf([128, NFC, 128], BF16, "gT")
        cf = k.sbuf([128, NFC, 2], F32, "cf")
        GS = 4
        EXW = NSB * (ST + 2)
        ex4 = [k.sbuf([128, GS, EXW], F32, f"ex4_{i}") for i in range(2)]
        cc4 = [k.sbuf([128, GS, 128], F32, f"cc4_{i}") for i in range(2)]
        t14 = [k.sbuf([128, GS, 128], F32, "t14_0")] * 2
        up4 = [k.sbuf([128, GS, 128], F32, f"up4_{i}") for i in range(2)]
        sg3 = [k.sbuf([128, 512], F32, "sg30")] * 2
        ppt = k.sbuf([128, PLE], F32, "ppt")
        ppb = k.sbuf([128, PLE], BF16, "ppb")
        peT = k.sbuf([128, 2, 128], BF16, "peT")
        utm = sg3[0]
        k.memset("pool", (cf, cf[:]), 0.0)
        cfs = k.sbuf([128, NFC, 2 * NSB], F32, "cfs")
        GC = 1.5957691216057308
        BLK6 = ((0, 512), (512, 512), (1024, 512), (1536, 512), (2048, 512), (2560, 256))
        groups = [list(range(g0, min(g0 + GS, NFC))) for g0 in range(0, NFC, GS)]
        gbank = [pF[0], pF[1]]
        ubank = [pM[0], pM[1]]

        def fup_w(col):
            g = col // CW
            g_hi = (col + 127) // CW
            return g, g_hi

        def stage_A(ti, gi, hT3):
            smp = ti == NTP
            p = gi % 2
            chunks = groups[gi]
            n = len(chunks)
            c0 = chunks[0]
            for part, bank in ((0, gbank[p]), (1, ubank[p])):
                for ci, c in enumerate(chunks):
                    g, g_hi = fup_w(c * 128)
                    for kc in range(8):
                        k.mm((bank, bank[:, ci * 128:(ci + 1) * 128]), (F_up, F_up[:, kc, part * DFF + c * 128:part * DFF + (c + 1) * 128], g),
                             (hT3, hT3[:, kc, :]), start=(kc == 0), stop=(kc == 7))
                        if g_hi != g and g_hi in F_up.subs and F_up.subs[g_hi].w is not None:
                            k.streams["pe"][-1].deps.add(F_up.subs[g_hi].w)
            ex = ex4[p]
            if not smp:
                k.copy("pool", (ex, ex[:, 0:n, 0:2]), (cf, cf[:, c0:c0 + n, :]))
                k.copy("act", (ex, ex[:, 0:n, 2:130]), (gbank[p], gbank[p][:, 0:n * 128].rearrange("p (c t) -> p c t", c=n)))
                k.copy("pool", (cf, cf[:, c0:c0 + n, :]), (ex, ex[:, 0:n, 128:130]))
            else:
                exs = ex[:, 0:n, :].rearrange("p c (b t) -> p c b t", t=ST + 2)
                k.copy("pool", (ex, exs[:, :, :, 0:2]), (cfs, cfs[:, c0:c0 + n, :].rearrange("p c (b j) -> p c b j", j=2)))
                k.copy("act", (ex, exs[:, :, :, 2:ST + 2]), (gbank[p], gbank[p][:, 0:n * 128].rearrange("p (c b t) -> p c b t", c=n, b=NSB)))
            k.copy("act", (up4[p], up4[p][:, 0:n, :]), (ubank[p], ubank[p][:, 0:n * 128].rearrange("p (c t) -> p c t", c=n)))
            for ci, c in enumerate(chunks):
                if not smp:
                    tap = lambda j: ex[:, ci, j:j + 128]
                    ccv = cc4[p][:, ci, :]
                else:
                    e3 = ex[:, ci, :].rearrange("p (b t) -> p b t", t=ST + 2)
                    tap = lambda j, e3=e3: e3[:, :, j:j + ST]
                    ccv = cc4[p][:, ci, :].rearrange("p (b t) -> p b t", t=ST)
                cb = cc4[p]
                k.ts("dve", (cb, ccv), (ex, tap(2)), fcol(2 * NFC + c), fcol(3 * NFC + c), op0=ALU.mult, op1=ALU.add)
                k.stt("dve", (cb, ccv), (ex, tap(1)), fcol(1 * NFC + c), (cb, ccv), ALU.mult, ALU.add)
                k.stt("dve", (cb, ccv), (ex, tap(0)), fcol(0 * NFC + c), (cb, ccv), ALU.mult, ALU.add)

        def stage_B(ti, gi):
            p = gi % 2
            chunks = groups[gi]
            n = len(chunks)
            c0 = chunks[0]
            cb = (cc4[p], cc4[p][:, 0:n, :])
            ta = (t14[p], t14[p][:, 0:n, :])
            k.tt("pool", ta, cb, cb, ALU.mult)
            k.ts("dve", ta, ta, 0.044715, 1.0, op0=ALU.mult, op1=ALU.add)
            k.tt("pool", ta, ta, cb, ALU.mult)
            k.act(ta, ta, AF.Sigmoid, scale=GC)
            k.tt("pool", ta, ta, cb, ALU.mult)
            k.tt("dve", (gT, gT[:, c0:c0 + n, :], ("g", gi)), ta, (up4[p], up4[p][:, 0:n, :]), ALU.mult)

        def load_norm2(ti):
            xb = xq[ti % 2]
            k.dma("sp", xb[:], x1s[ti * 128:(ti + 1) * 128, :], reads=[(x1s, ti)], writes=[xb])
            if ti == NTP:
                for b6, (c0, n) in enumerate(BLK6):
                    k.dma("sp", utm[0:2 * NSB, 0:n], st_fconv[:, c0:c0 + n], reads=[st_fconv], writes=[utm])
                    nch = n // 128
                    for ci in range(nch):
                        k.tr((pR[b6 % 2], pR[b6 % 2][:, ci * 32:(ci + 1) * 32]), (utm, utm[0:2 * NSB, ci * 128:(ci + 1) * 128]),
                             (identf, identf[0:2 * NSB, 0:2 * NSB]))
                    k.copy("act", (cfs, cfs[:, 4 * b6:4 * b6 + nch, :]), (pR[b6 % 2], pR[b6 % 2][:, 0:nch * 32].rearrange("p (c x) -> p c x", c=nch)))
            norm_generic(xb, g2bc, hb3, hT3s[ti % 2], ss3, rs3)

        hb3b = hb3

        def groups_gen(ti):
            hT3 = hT3s[ti % 2]
            for gi in range(len(groups) + 1):
                if gi < len(groups):
                    stage_A(ti, gi, hT3)
                    yield
                if gi >= 1:
                    stage_B(ti, gi - 1)
                    yield

        def tail_gen(ti):
            smp = ti == NTP
            xb = xq[ti % 2]
            hT3 = hT3s[ti % 2]
            for blk in range(2):
                for c in range(NFC):
                    k.mm((pR[blk], pR[blk][:, :]), (gT, gT[:, c, :], ("g", c // GS)), (F_dn, F_dn[:, c, blk * 512:(blk + 1) * 512], c // 11),
                         start=(c == 0), stop=(c == NFC - 1))
                k.tt("dve", (xb, xb[:, blk * 512:(blk + 1) * 512]), (xb, xb[:, blk * 512:(blk + 1) * 512]), (pR[blk], pR[blk][:, :]), ALU.add)
                yield
            if ti == NTP - 1 or smp:
                for b6, (c0, n) in enumerate(BLK6):
                    for kc in range(8):
                        k.mm((pM[2], pM[2][:, 0:n]), (hT3, hT3[:, kc, :]), (F_up, F_up[:, kc, c0:c0 + n]),
                             start=(kc == 0), stop=(kc == 7))
                    if not smp:
                        k.copy("act", (utm, utm[96:128, 0:n]), (pM[2], pM[2][96:128, 0:n]))
                        k.dma("sp", o_pfconv[:, c0:c0 + n], utm[126:128, 0:n], reads=[utm], writes=[o_pfconv])
                    else:
                        k.copy("act", (utm, utm[:, 0:n]), (pM[2], pM[2][:, 0:n]))
                        for b in range(NSB):
                            k.dma("sp", o_sfconv[2 * b:2 * b + 2, c0:c0 + n], utm[ST * b + ST - 2:ST * b + ST, 0:n], reads=[utm], writes=[o_sfconv])
                    yield
            norm_generic(xb, g3bc, hb3b, hT3, ss3, rs3)
            yield
            pd, pap = (pp, pp[ti * 128:(ti + 1) * 128, :]) if not smp else (psm, psm[:, :])
            k.dma("sp", ppt[:], pap, reads=[pd], writes=[ppt])
            k.copy("act", (ppb, ppb[:]), (ppt, ppt[:]))
            for kc in range(2):
                k.tr((pT, pT[:, kc * 128:(kc + 1) * 128]), (ppb, ppb[:, kc * 128:(kc + 1) * 128]), (identb, identb[:]))
            k.copy("act", (peT, peT[:].rearrange("p a b -> p (a b)")), (pT, pT[:, 0:256]))
            yield
            for blk in range(2):
                for kc in range(8):
                    k.mm((pR[blk], pR[blk][:, :]), (hT3, hT3[:, kc, :]), (PGW, PGW[:, kc, blk * 512:(blk + 1) * 512]),
                         start=(kc == 0), stop=(kc == 7))
                yield
                k.act((sg3[blk], sg3[blk][:]), (pR[blk], pR[blk][:, :]), AF.Sigmoid)
                for kc in range(2):
                    k.mm((pM[2], pM[2][:, :]), (peT, peT[:, kc, :]), (PPJ, PPJ[:, kc, blk * 512:(blk + 1) * 512]),
                         start=(kc == 0), stop=(kc == 1))
                k.tt("dve", (sg3[blk], sg3[blk][:]), (sg3[blk], sg3[blk][:]), (pM[2], pM[2][:, :]), ALU.mult)
                k.tt("pool", (xb, xb[:, blk * 512:(blk + 1) * 512]), (xb, xb[:, blk * 512:(blk + 1) * 512]), (sg3[blk], sg3[blk][:]), ALU.add)
                yield
            k.act((hb3b, hb3b[:]), (xb, xb[:]), AF.Square, accum=(ss3, ss3[:]))
            k.ts("dve", (rs3, rs3[:]), (ss3, ss3[:]), 1.0 / D, EPS, op0=ALU.mult, op1=ALU.add)
            k.act((rs3, rs3[:]), (rs3, rs3[:]), AF.Ln)
            k.act((rs3, rs3[:]), (rs3, rs3[:]), AF.Exp, scale=-0.5)
            yield
            k.stt("dve", (xb, xb[:]), (xb, xb[:]), (rs3, rs3[:, 0:1]), (g4bc, g4bc[:]), ALU.mult, ALU.mult)
            if not smp:
                k.dma("sp", y_p[ti * 128:(ti + 1) * 128, :], xb[:], reads=[xb], writes=[y_p])
            else:
                k.dma("sp", y_s[:, :], xb[:], reads=[xb], writes=[y_s])
            if ti in nxt2:
                yield
                load_norm2(nxt2[ti])

        def run_rr(gens):
            gens = list(gens)
            while gens:
                for g_ in list(gens):
                    try:
                        next(g_)
                    except StopIteration:
                        gens.remove(g_)

        tl = list(tiles_all)
        nxt2 = {tl[i]: tl[i + 2] for i in range(len(tl) - 2)}
        load_norm2(tl[0])
        if len(tl) > 1:
            load_norm2(tl[1])
        run_rr([groups_gen(tl[0])])
        for idx, ti in enumerate(tl):
            tg = tail_gen(ti)
            if idx + 1 < len(tl):
                gg = groups_gen(tl[idx + 1])
                next(gg)
                next(gg)
                next(tg)
                next(tg)
                run_rr([gg, tg])
            else:
                run_rr([tg])

    _nt_dbg = int(_os.environ.get("KDBG_NT", "0"))
    for ti in (tiles_all if not _nt_dbg else list(range(_nt_dbg))):
        mixer_tile(ti)
        if stage >= 2:
            rwkv_tile(ti)
            tail_rows(ti)

    if stage >= 3:
        phase_1b(k)
    if stage >= 4:
        phase_2(k)
    k.emit()
    k.stats["sbuf_hiwater"] = k.hiwater
    k.stats["arena_bytes"] = k.arena_bytes
    return nc, k


def _chunk_rows(w, nk):
    return np.ascontiguousarray(w.reshape(nk, 128, w.shape[1]).transpose(1, 0, 2))


def _pcols(v, nc_):
    return v.reshape(nc_, 128).T


_PROG = {}


def _get_prog(stage=99, dbg=False):
    key = (stage, dbg)
    if key not in _PROG:
        _PROG[key] = build_program(stage, dbg)
    return _PROG[key]


def make_in_maps(inp):
    f = lambda a: np.ascontiguousarray(np.asarray(a, dtype=np.float32))
    ptab = np.zeros((128, 128), np.float32)
    mcw = f(inp["m_conv_w"])[0]
    for j in range(4):
        ptab[:, j * 8:(j + 1) * 8] = _pcols(mcw[j], 8)
    ptab[:, 32:40] = _pcols(f(inp["m_conv_b"])[0], 8)
    ptab[:, 40:54] = _pcols(f(inp["r_mix"])[0], 14)
    ptab[:, 54:58] = _pcols(f(inp["r_w0"])[0], 4)
    ptab[:, 58:62] = _pcols(f(inp["r_a0"])[0], 4)
    ptab[:, 62:66] = _pcols(f(inp["r_kk"])[0], 4)
    ptab[:, 66:70] = _pcols(f(inp["r_ka"])[0], 4)
    ptab[:, 70:74] = _pcols(f(inp["r_rk"])[0].reshape(-1), 4)
    ptab[:, 74:78] = _pcols(f(inp["r_ln_g"])[0], 4)
    ptab[:, 78:82] = _pcols(f(inp["r_ln_b"])[0], 4)
    ptab[:, 82:86] = _pcols(f(inp["m_norm_g"])[0], 4)
    fct = np.zeros((128, 4 * NFC), np.float32)
    fcw = f(inp["f_conv_w"])[0]
    for j in range(3):
        fct[:, j * NFC:(j + 1) * NFC] = _pcols(fcw[j], NFC)
    fct[:, 3 * NFC:4 * NFC] = _pcols(f(inp["f_conv_b"])[0], NFC)
    gbias = np.stack([f(inp["m_i_bias"])[0], f(inp["m_f_bias"])[0]], axis=1)
    ra2 = np.zeros((128, RW), np.float32)
    ra2[64:128] = f(inp["r_a2"])[0]
    bc = lambda v: np.ascontiguousarray(np.broadcast_to(f(v).reshape(1, D), (128, D)))
    shared = {
        "w_in": _chunk_rows(f(inp["w_in"])[0], 8),
        "w_bm": _chunk_rows(f(inp["w_branch_m"])[0], 4),
        "w_br": _chunk_rows(f(inp["w_branch_r"])[0], 4),
        "w_out": _chunk_rows(f(inp["w_out"])[0], 8),
        "f_up": _chunk_rows(f(inp["f_up"])[0], 8),
        "f_down": _chunk_rows(f(inp["f_down"])[0], NFC),
        "ple_gate_w": _chunk_rows(f(inp["ple_gate_w"])[0], 8),
        "ple_proj": _chunk_rows(f(inp["ple_proj"])[0], 2),
        "r_w2": f(inp["r_w2"])[0], "r_a2": ra2, "r_g2": f(inp["r_g2"])[0],
        "norm1_g": bc(inp["norm1_g"]), "norm2_g": bc(inp["norm2_g"]),
        "ple_norm_g": bc(inp["ple_norm_g"]), "final_norm_g": bc(inp["final_norm_g"]),
        "ptab": ptab, "fctab": fct, "gate_bias": np.ascontiguousarray(gbias),
    }
    maps = []
    for c in range(NCORES):
        sl = slice(c * NSB, (c + 1) * NSB)
        m = dict(shared)
        m["xp"] = f(inp["x_prompt"][c])
        m["xs"] = f(inp["x_sample"][sl]).reshape(128, D)
        m["pp"] = f(inp["p_prompt"][0, c])
        m["psm"] = f(inp["p_sample"][0, sl]).reshape(128, PLE)
        m["st_mconv"] = f(inp["state_mlstm_conv"][0, sl]).reshape(NSB * 3, 2 * MW)
        m["st_mC"] = f(inp["state_mlstm_C"][0, sl])
        m["st_mn"] = f(inp["state_mlstm_n"][0, sl])
        m["st_mm"] = f(inp["state_mlstm_m"][0, sl])
        m["st_rshift"] = f(inp["state_rwkv_shift"][0, sl])
        m["st_rS"] = f(inp["state_rwkv_S"][0, sl]).reshape(NSB * RH, RN * RN)
        m["st_fconv"] = f(inp["state_ffn_conv"][0, sl]).reshape(NSB * 2, DFF)
        maps.append(m)
    return maps


def assemble(results):
    g = lambda name: [np.asarray(r[name], dtype=np.float32) for r in results]
    y_p = np.stack(g("y_p"), 0)
    y_s = np.concatenate([a.reshape(NSB, ST, D) for a in g("y_s")], 0)
    p_conv = np.stack(g("p_conv"), 0)[None]
    p_C = np.stack(g("p_C"), 0)[None]
    p_n = np.stack(g("p_n"), 0)[None]
    p_m = np.stack([a.reshape(MH) for a in g("p_m")], 0)[None]
    p_shift = np.stack([a.reshape(RCOLS) for a in g("p_shift")], 0)[None]
    p_S = np.stack([a.reshape(RH, RN, RN) for a in g("p_S")], 0)[None]
    p_fconv = np.stack(g("p_fconv"), 0)[None]
    s_conv = np.concatenate([a.reshape(NSB, 3, 2 * MW) for a in g("s_conv")], 0)[None]
    s_C = np.concatenate(g("s_C"), 0)[None]
    s_n = np.concatenate(g("s_n"), 0)[None]
    s_m = np.concatenate(g("s_m"), 0)[None]
    s_shift = np.concatenate(g("s_shift"), 0)[None]
    s_S = np.concatenate([a.reshape(NSB, RH, RN, RN) for a in g("s_S")], 0)[None]
    s_fconv = np.concatenate([a.reshape(NSB, 2, DFF) for a in g("s_fconv")], 0)[None]
    return (y_p, y_s, p_conv, p_C, p_n, p_m, p_shift, p_S, p_fconv,
            s_conv, s_C, s_n, s_m, s_shift, s_S, s_fconv)


def kernel(**inputs):
    nc, _ = _get_prog()
    maps = make_in_maps(inputs)
    res = run_bass_kernel_spmd(nc, maps, core_ids=list(range(NCORES)))
    return assemble(res.results)
```

```python
import math
from contextlib import ExitStack

import numpy as np
import concourse.bass as bass
import concourse.mybir as mybir
from concourse.bass_utils import run_bass_kernel_spmd

F32 = mybir.dt.float32
BF16 = mybir.dt.bfloat16
AF = mybir.ActivationFunctionType
ALU = mybir.AluOpType
AX = mybir.AxisListType

ENGS = ("pe", "act", "dve", "pool", "sp")
N_DMA_SEMS = 8
SAME_ENG_DIST = 2

D = 1024
SEQ = 2048
NCORES = 8
NTP = SEQ // 128
NSB = 16
ST = 8
MW = 512
MH = 4
RW = 512
RH = 8
RN = 64
RCOLS = 1792
DFF = 2816
NFC = DFF // 128
PLE = 256
N_IN = 5896
C_QK, C_V, C_O, C_I, C_F, C_R, C_G = 0, 1024, 1536, 2048, 2052, 2056, 3848
EPS = 1e-6
GN_EPS = 64e-5
KSCALE = 128 ** -0.5
WSCALE = -math.exp(-0.5)


class _Trk:
    __slots__ = ("w", "r")

    def __init__(self):
        self.w = None
        self.r = []


class Buf:
    def __init__(self, t, name):
        self.t = t
        self.name = name
        self.whole = _Trk()
        self.subs = {}

    def __getitem__(self, idx):
        return self.t[idx]

    def view(self, ap, name=None):
        b = Buf(ap, name or self.name + "_v")
        b.whole = self.whole
        b.subs = self.subs
        return b


class _Op:
    __slots__ = ("eng", "fn", "deps", "needs_inc", "is_dma", "sem", "val", "pos", "force")


class K:
    def __init__(self, nc):
        self.nc = nc
        self.es = ExitStack()
        self.streams = {e: [] for e in ENGS}
        self.dma_rr = {e: 0 for e in ENGS}
        self.dma_last = {}
        self.nbuf = 0
        self.ops = []

    def _init_arena(self):
        nbytes = (int(self.nc.sbuf_bytes_remaining) - 512) // 64 * 64
        self.arena_bytes = nbytes
        self.arena = self.es.enter_context(self.nc.sbuf_tensor("arena", [128, nbytes // 2], BF16))
        self.bot = 0
        self.top = nbytes
        self.hiwater = 0

    def _view(self, off, shape, dtype):
        n = 1
        for d in shape[1:]:
            n *= d
        esz = 4 if dtype == F32 else 2
        v = self.arena[:, off // 2:(off + n * esz) // 2]
        if dtype == F32:
            v = v.bitcast(F32)
        if len(shape) > 2:
            names = " ".join(f"d{i}" for i in range(len(shape) - 1))
            v = v.rearrange(f"p ({names}) -> p {names}", **{f"d{i}": shape[i + 1] for i in range(len(shape) - 1)})
        if shape[0] < 128:
            v = v[0:shape[0]]
        return v, n * esz

    def sbuf(self, shape, dtype, name=None, top=False):
        if not hasattr(self, "arena"):
            self._init_arena()
        self.nbuf += 1
        name = name or f"sb{self.nbuf}"
        n = 1
        for d in shape[1:]:
            n *= d
        nb = (n * (4 if dtype == F32 else 2) + 63) // 64 * 64
        if top:
            self.top -= nb
            off = self.top
        else:
            off = self.bot
            self.bot += nb
        assert self.bot <= self.top, f"SBUF arena overflow allocating {name}: bot={self.bot} top={self.top}"
        self.hiwater = max(self.hiwater, self.bot + (self.arena_bytes - self.top))
        v, _ = self._view(off, list(shape), dtype)
        return Buf(v, name)

    def pe_fence(self):
        st = self.streams["pe"]
        if not st:
            return
        last = st[-1]
        o = self.op("pe", lambda h: h.nop(), (), ())
        o.deps.add(last)
        o.force = {last}
        if getattr(self, "fence_mm", None) is not None:
            fb, fi = self.fence_mm
            self.tr((fb, fb[:, 0:128]), (fi, fi[:]), (fi, fi[:]))
            last = self.streams["pe"][-1]
            o = self.op("pe", lambda h: h.nop(), (), ())
            o.deps.add(last)
            o.force = {last}

    def barrier(self):
        lasts = [st[-1] for st in self.streams.values() if st]
        lasts += list(self.dma_last.values())
        for e in ENGS:
            o = self.op(e, lambda h: h.nop(), (), ())
            o.deps.update(x for x in lasts if x is not o)

    def psum(self, shape, dtype, name=None):
        self.nbuf += 1
        name = "ps_" + (name or f"{self.nbuf}")
        t = self.es.enter_context(self.nc.psum_tensor(name, list(shape), dtype))
        return Buf(t, name)

    def dram(self, name, shape, dtype, kind="Internal"):
        t = self.nc.dram_tensor(name, list(shape), dtype, kind=kind)
        return Buf(t.ap(), name)

    def _touch(self, op, item, is_write):
        if isinstance(item, tuple):
            buf, key = item
        else:
            buf, key = item, None
        if key is None:
            trks = [buf.whole] + list(buf.subs.values())
        else:
            if key not in buf.subs:
                buf.subs[key] = _Trk()
            trks = [buf.whole, buf.subs[key]]
        for t in trks:
            if t.w is not None:
                op.deps.add(t.w)
            if is_write:
                op.deps.update(t.r)
        return buf, key

    def _commit(self, op, buf, key, is_write):
        if key is None:
            if is_write:
                buf.whole.w = op
                buf.whole.r = []
                buf.subs.clear()
            else:
                self._add_reader(buf.whole, op)
        else:
            t = buf.subs[key]
            if is_write:
                t.w = op
                t.r = []
            else:
                self._add_reader(t, op)

    @staticmethod
    def _add_reader(t, op):
        if not op.is_dma:
            t.r = [o for o in t.r if o.is_dma or o.eng != op.eng]
        t.r.append(op)

    def op(self, eng, fn, reads=(), writes=(), dma=False):
        o = _Op()
        o.eng = eng
        o.fn = fn
        o.deps = set()
        o.needs_inc = False
        o.is_dma = dma
        o.sem = None
        o.val = None
        o.force = None
        touched = []
        for it in reads:
            touched.append(self._touch(o, it, False) + (False,))
        for it in writes:
            touched.append(self._touch(o, it, True) + (True,))
        o.deps.discard(o)
        for buf, key, w in touched:
            self._commit(o, buf, key, w)
        if dma:
            kk = (eng, self.dma_rr[eng] % N_DMA_SEMS)
            self.dma_rr[eng] += 1
            prev = self.dma_last.get(kk)
            if prev is not None:
                o.deps.add(prev)
            self.dma_last[kk] = o
            o.sem = kk
            o.needs_inc = True
        o.pos = len(self.streams[eng])
        self.streams[eng].append(o)
        self.ops.append(o)
        return o

    def dma(self, eng, out, in_, reads=(), writes=(), **kw):
        return self.op(eng, lambda e: e.dma_start(out=out, in_=in_, **kw), reads, writes, dma=True)

    def emit(self):
        nc = self.nc
        for o in self.ops:
            real = []
            for d in o.deps:
                if (not d.is_dma) and (not o.is_dma) and d.eng == o.eng and o.eng == "pe":
                    if not (o.force and d in o.force):
                        continue
                d.needs_inc = True
                real.append(d)
            o.deps = real
        for e in ENGS:
            cs = [o for o in self.streams[e] if not o.is_dma]
            if cs:
                cs[-1].needs_inc = True
        es = self.es
        esem = {e: es.enter_context(nc.semaphore(f"s_{e}")) for e in ENGS}
        dsem = {}
        for e in ENGS:
            for i in range(min(N_DMA_SEMS, self.dma_rr[e])):
                dsem[(e, i)] = es.enter_context(nc.semaphore(f"d_{e}{i}"))
        dcount = {kk: 0 for kk in dsem}
        for e in ENGS:
            c = 0
            for o in self.streams[e]:
                if o.is_dma:
                    dcount[o.sem] += 16
                    o.val = dcount[o.sem]
                    o.sem = dsem[o.sem]
                elif o.needs_inc:
                    c += 1
                    o.val = c
                    o.sem = esem[e]
        final_waits = [(s, dcount[kk]) for kk, s in dsem.items() if dcount[kk] > 0]
        for e in ENGS:
            if e == "sp":
                continue
            cs = [o for o in self.streams[e] if not o.is_dma and o.needs_inc]
            if cs:
                final_waits.append((esem[e], cs[-1].val))
        streams = self.streams
        nwaits = [0]

        def run(e, handle):
            waited = {}
            for o in streams[e]:
                need = {}
                for d in o.deps:
                    if need.get(d.sem, (None, 0))[1] < d.val:
                        need[d.sem] = (d.sem, d.val)
                for s, v in need.values():
                    if waited.get(s, 0) >= v:
                        continue
                    handle.wait_ge(s, v)
                    nwaits[0] += 1
                    waited[s] = v
                ins = o.fn(handle)
                if o.is_dma:
                    ins.then_inc(o.sem, 16)
                elif o.needs_inc:
                    ins.then_inc(o.sem, 1)
            if e == "sp":
                for s, v in final_waits:
                    handle.wait_ge(s, v)

        with nc.Block() as block:
            @block.tensor
            def _(h):
                run("pe", h)

            @block.scalar
            def _(h):
                run("act", h)

            @block.vector
            def _(h):
                run("dve", h)

            @block.gpsimd
            def _(h):
                run("pool", h)

            @block.sync
            def _(h):
                run("sp", h)
        self.stats = dict(n_ops={e: len(streams[e]) for e in ENGS}, n_waits=nwaits[0])
        self.es.close()

    @staticmethod
    def _it(x):
        return (x[0], x[2]) if len(x) > 2 else x[0]

    def mm(self, out, lhsT, rhs, start=True, stop=True):
        return self.op("pe", lambda e: e.matmul(out[1], lhsT=lhsT[1], rhs=rhs[1], start=start, stop=stop),
                       reads=[self._it(lhsT), self._it(rhs)], writes=[self._it(out)])

    def tr(self, out, in_, ident):
        return self.op("pe", lambda e: e.transpose(out[1], in_[1], ident[1]),
                       reads=[self._it(in_), self._it(ident)], writes=[self._it(out)])

    def act(self, out, in_, func, bias=None, scale=None, accum=None, eng="act"):
        reads = [self._it(in_)]
        kw = {}
        if bias is not None:
            if isinstance(bias, tuple):
                reads.append(self._it(bias))
                kw["bias"] = bias[1]
            else:
                kw["bias"] = bias
        if scale is not None:
            if isinstance(scale, tuple):
                reads.append(self._it(scale))
                kw["scale"] = scale[1]
            else:
                kw["scale"] = scale
        writes = [self._it(out)]
        if accum is not None:
            writes.append(self._it(accum))
            kw["accum_out"] = accum[1]
        return self.op(eng, lambda e: e.activation(out=out[1], in_=in_[1], func=func, **kw), reads, writes)

    def tt(self, eng, out, in0, in1, op):
        return self.op(eng, lambda e: e.tensor_tensor(out=out[1], in0=in0[1], in1=in1[1], op=op),
                       reads=[self._it(in0), self._it(in1)], writes=[self._it(out)])

    def ts(self, eng, out, in0, s1, s2=None, op0=ALU.mult, op1=None, accum=None):
        reads = [self._it(in0)]
        a1 = s1
        a2 = s2
        if isinstance(s1, tuple):
            reads.append(self._it(s1))
            a1 = s1[1]
        if isinstance(s2, tuple):
            reads.append(self._it(s2))
            a2 = s2[1]
        kw = {}
        if op1 is not None:
            kw["op1"] = op1
        writes = [self._it(out)]
        if accum is not None:
            writes.append(self._it(accum))
            kw["accum_out"] = accum[1]
        return self.op(eng, lambda e: e.tensor_scalar(out=out[1], in0=in0[1], scalar1=a1, scalar2=a2, op0=op0, **kw),
                       reads, writes)

    def stt(self, eng, out, in0, scalar, in1, op0, op1):
        reads = [self._it(in0), self._it(in1)]
        a = scalar
        if isinstance(scalar, tuple):
            reads.append(self._it(scalar))
            a = scalar[1]
        return self.op(eng, lambda e: e.scalar_tensor_tensor(out=out[1], in0=in0[1], scalar=a, in1=in1[1], op0=op0, op1=op1),
                       reads, [self._it(out)])

    def copy(self, eng, out, in_):
        if eng == "act":
            return self.op(eng, lambda e: e.activation(out=out[1], in_=in_[1], func=AF.Copy),
                           reads=[self._it(in_)], writes=[self._it(out)])
        return self.op(eng, lambda e: e.tensor_copy(out=out[1], in_=in_[1]),
                       reads=[self._it(in_)], writes=[self._it(out)])

    def red(self, eng, out, in_, op, axis=AX.X):
        return self.op(eng, lambda e: e.tensor_reduce(out=out[1], in_=in_[1], axis=axis, op=op),
                       reads=[self._it(in_)], writes=[self._it(out)])

    def memset(self, eng, out, val):
        return self.op(eng, lambda e: e.memset(out[1], val), reads=[], writes=[self._it(out)])

    def scan(self, eng, out, d0, d1, init, op0, op1):
        return self.op(eng, lambda e: e.tensor_tensor_scan(out=out[1], data0=d0[1], data1=d1[1], initial=init, op0=op0, op1=op1),
                       reads=[self._it(d0), self._it(d1)], writes=[self._it(out)])


def build_program(stage=99, dbg=False):
    import os as _os
    nc = bass.Bass("TRN2", target_bir_lowering=False)
    k = K(nc)
    NT = NTP + 1

    def din(name, shape):
        return k.dram(name, shape, F32, "ExternalInput")

    def dout(name, shape):
        return k.dram(name, shape, F32, "ExternalOutput")

    xp = din("xp", [SEQ, D]); xs = din("xs", [128, D])
    pp = din("pp", [SEQ, PLE]); psm = din("psm", [128, PLE])
    st_mconv = din("st_mconv", [NSB * 3, 2 * MW])
    st_mC = din("st_mC", [NSB, MH, 128, 128])
    st_mn = din("st_mn", [NSB, MH, 128])
    st_mm = din("st_mm", [NSB, MH])
    st_rshift = din("st_rshift", [NSB, RCOLS])
    st_rS = din("st_rS", [NSB * RH, RN * RN])
    st_fconv = din("st_fconv", [NSB * 2, DFF])
    d_w_in = din("w_in", [128, 8, N_IN])
    d_w_bm = din("w_bm", [128, 4, D]); d_w_br = din("w_br", [128, 4, D])
    d_w_out = din("w_out", [128, 8, D])
    d_f_up = din("f_up", [128, 8, 2 * DFF]); d_f_down = din("f_down", [128, NFC, D])
    d_pgw = din("ple_gate_w", [128, 8, D]); d_ppj = din("ple_proj", [128, 2, D])
    d_rw2 = din("r_w2", [64, RW]); d_ra2 = din("r_a2", [128, RW]); d_rg2 = din("r_g2", [128, RW])
    d_g1 = din("norm1_g", [128, D]); d_g2 = din("norm2_g", [128, D])
    d_g3 = din("ple_norm_g", [128, D]); d_g4 = din("final_norm_g", [128, D])
    d_ptab = din("ptab", [128, 128])
    d_fctab = din("fctab", [128, 4 * NFC])
    d_gb = din("gate_bias", [4, 2])
    y_p = dout("y_p", [SEQ, D]); y_s = dout("y_s", [128, D])
    o_pconv = dout("p_conv", [3, 2 * MW]); o_pC = dout("p_C", [MH, 128, 128]); o_pn = dout("p_n", [MH, 128])
    o_pm = dout("p_m", [1, MH]); o_pshift = dout("p_shift", [1, RCOLS]); o_pS = dout("p_S", [RH * RN, RN])
    o_pfconv = dout("p_fconv", [2, DFF])
    o_sconv = dout("s_conv", [NSB * 3, 2 * MW]); o_sC = dout("s_C", [NSB, MH, 128, 128]); o_sn = dout("s_n", [NSB, MH, 128])
    o_sm = dout("s_m", [NSB, MH]); o_sshift = dout("s_shift", [NSB, RCOLS]); o_sS = dout("s_S", [NSB * RH, RN * RN])
    o_sfconv = dout("s_fconv", [NSB * 2, DFF])
    x1s = k.dram("x1_scratch", [NT * 128, D], F32)
    dbgs = {}

    def dbg_out(name, src_buf, src_ap, shape):
        if not dbg:
            return
        t = dout("dbg_" + name, shape)
        dbgs[name] = t
        k.dma("sp", t[:], src_ap, reads=[src_buf], writes=[t])

    identf = k.sbuf([128, 128], F32, "identf")
    identb = k.sbuf([128, 128], BF16, "identb")
    mark_phase = k.bot
    mU_in = [k.sbuf([128, 128], F32, f"mUin{i}") for i in range(2)]
    mU_st = [k.sbuf([128, 128], F32, f"mUst{i}") for i in range(2)]
    mL_st = [k.sbuf([128, 128], F32, f"mLst{i}") for i in range(2)]
    resets = [k.sbuf([128, 512], F32, f"resets{i}") for i in range(2)]
    ones4 = k.sbuf([4, 128], F32, "ones4")

    def aff(out_buf, out_ap, pattern, cm, base, op=ALU.is_ge):
        k.op("pool", lambda e: e.affine_select(out=out_ap, in_=out_ap, pattern=pattern, compare_op=op,
                                               fill=0.0, base=base, channel_multiplier=cm),
             reads=[out_buf], writes=[out_buf])

    k.memset("pool", (identf, identf[:]), 1.0)
    aff(identf, identf[:], [[-1, 128]], 1, 0)
    aff(identf, identf[:], [[1, 128]], -1, 0)
    k.copy("pool", (identb, identb[:]), (identf, identf[:]))
    for i in range(2):
        k.memset("pool", (mU_in[i], mU_in[i][:]), 1.0)
        aff(mU_in[i], mU_in[i][:], [[1, 128]], -1, 0)
        k.memset("pool", (mU_st[i], mU_st[i][:]), 1.0)
        aff(mU_st[i], mU_st[i][:], [[1, 128]], -1, -1)
        k.memset("pool", (mL_st[i], mL_st[i][:]), 1.0)
        aff(mL_st[i], mL_st[i][:], [[-1, 128]], 1, -1)
        k.memset("pool", (resets[i], resets[i][:]), 1.0)
    v3 = lambda b: b[:].rearrange("p (a c) -> p a c", c=ST)
    aff(mU_in[1], v3(mU_in[1]), [[-ST, 16], [0, ST]], 1, 0)
    aff(mU_st[1], v3(mU_st[1]), [[-ST, 16], [0, ST]], 1, 0)
    aff(mL_st[1], v3(mL_st[1]), [[ST, 16], [0, ST]], -1, ST - 1)
    k.memset("pool", (resets[0], resets[0][:].rearrange("p (a c) -> p a c", c=128)[:, :, 0:1]), 0.0)
    k.memset("pool", (resets[1], resets[1][:].rearrange("p (a c) -> p a c", c=ST)[:, :, 0:1]), 0.0)
    k.memset("pool", (ones4, ones4[:]), 1.0)
    mask2 = [k.sbuf([128, 256], F32, f"mask2_{i}") for i in range(2)]
    for i in range(2):
        k.copy("pool", (mask2[i], mask2[i][:, 0:128]), (mU_st[i], mU_st[i][:]))
        k.copy("pool", (mask2[i], mask2[i][:, 128:256]), (mU_in[i], mU_in[i][:]))
    I2 = k.sbuf([128, 64], F32, "I2")
    k.tt("pool", (I2, I2[:]), (identf, identf[:, 0:64]), (identf, identf[:, 64:128]), ALU.add)
    bones = k.sbuf([128, 128], F32, "bones")
    k.memset("pool", (bones, bones[:]), 0.0)
    k.memset("pool", (bones, bones[0:64, 0:64]), 1.0)
    k.memset("pool", (bones, bones[64:128, 64:128]), 1.0)

    ptab = k.sbuf([128, 128], F32, "ptab")
    k.dma("sp", ptab[:], d_ptab[:], reads=[d_ptab], writes=[ptab])
    PT_MCW, PT_MCB, PT_RMIX, PT_RW0, PT_RA0, PT_RKK, PT_RKA, PT_RRK, PT_RLNG, PT_RLNB, PT_MNG = 0, 32, 40, 54, 58, 62, 66, 70, 74, 78, 82
    pcol = lambda c: (ptab, ptab[:, c:c + 1])
    gb = k.sbuf([4, 2], F32, "gb")
    k.dma("sp", gb[:], d_gb[:], reads=[d_gb], writes=[gb])
    nbf = k.sbuf([4, 1], F32, "nbf")
    k.ts("dve", (nbf, nbf[:]), (gb, gb[:, 1:2]), -1.0, None, op0=ALU.mult)
    g1bc = k.sbuf([128, D], F32, "g1bc")
    k.dma("sp", g1bc[:], d_g1[:], reads=[d_g1], writes=[g1bc])

    NA = C_G
    hmT_all = k.sbuf([128, NT, 4, 128], BF16, "hmT_all")
    yrgT_all = k.sbuf([128, NT, 4, 128], BF16, "yrgT_all")
    mark_1a = k.bot
    W_in = k.sbuf([128, 8, NA], BF16, "W_in")
    Wl_w2 = k.sbuf([64, RW], BF16, "Wl_w2")
    Wl_a2 = k.sbuf([128, RW], BF16, "Wl_a2")
    Wl_g2 = k.sbuf([128, RW], BF16, "Wl_g2")
    GRP = {"g0": (0, 1024), "g1": (1024, 2056), "g2": (2056, 3848)}
    for g in ("g0", "g1", "g2"):
        a, b = GRP[g]
        for kh in range(2):
            k.dma("pool", W_in[:, 4 * kh:4 * kh + 4, a:b], d_w_in[:, 4 * kh:4 * kh + 4, a:b], reads=[d_w_in], writes=[(W_in, g)])
        if g == "g1":
            k.dma("pool", Wl_w2[:], d_rw2[:], reads=[d_rw2], writes=[Wl_w2])
            k.dma("pool", Wl_a2[:], d_ra2[:], reads=[d_ra2], writes=[Wl_a2])
            k.dma("pool", Wl_g2[:], d_rg2[:], reads=[d_rg2], writes=[Wl_g2])

    def wgrp(col):
        for g, (a, b) in GRP.items():
            if a <= col < b:
                return g

    pF = [k.psum([128, 512], F32, f"pF{i}") for i in range(2)]
    pR = [k.psum([128, 512], F32, f"pR{i}") for i in range(2)]
    pT = k.psum([128, 1024], BF16, "pT")
    pM = [k.psum([128, 512], F32, f"pM{i}") for i in range(3)]

    xt = [k.sbuf([128, D], F32, "xt0")] * 2
    hb = k.sbuf([128, D], BF16, "hb")
    hT = k.sbuf([128, 8, 128], BF16, "hT")
    ss = k.sbuf([128, 1], F32, "ss")
    rs = k.sbuf([128, 1], F32, "rs")
    ext_q = k.sbuf([128, 8, 131], F32, "ext_q")
    cq = k.sbuf([128, 8, 3], F32, "cq")
    _eqf = ext_q[:].rearrange("p a b -> p (a b)")
    cv = k.sbuf([128, 8, 128], F32, "cv")
    qkT = k.sbuf([128, 8, 128], BF16, "qkT")
    soT = k.sbuf([128, 4, 128], F32, "soT")
    vaug = k.sbuf([128, 4, 130], BF16, "vaug")
    Cst = k.sbuf([128, 4, 129], F32, "Cst")
    Cb = k.sbuf([128, 4, 130], BF16, "Cb")
    gsm = [k.sbuf([4, 128], F32, f"gsm{i}") for i in range(8)]
    gpk = k.sbuf([4, 3, 128], F32, "gpk")
    mst = k.sbuf([4, 16], F32, "mst")
    mnew = k.sbuf([4, 16], F32, "mnew")
    gt = [k.sbuf([4, 16], F32, f"gt{i}") for i in range(4)]
    s0d = k.sbuf([4, 4, 16], F32, "s0d")
    tokS = k.sbuf([128, 12], F32, "tokS")
    s0bc = k.sbuf([128, 64], F32, "s0bc")
    _pk = _eqf[:, 512:1024].bitcast(BF16).rearrange("p (a b c) -> p a b c", a=2, b=4)
    PTm = ext_q.view(_pk[:, 0, :, :], "PTm")
    ktm = ext_q.view(_pk[:, 1, :, :], "ktm")
    dn = k.sbuf([128, 4], F32, "dn")
    hm = ext_q.view(_eqf[:, 0:512].rearrange("p (a b) -> p a b", a=4), "hm")
    hn = hm
    bst = k.sbuf([128, 4, 6], F32, "bst")
    bag = k.sbuf([128, 4, 2], F32, "bag")
    zq_tm = cv.view(cv[:].rearrange("p a b -> p (a b)"), "zq_tm")

    ext_r = k.sbuf([128, 14, 129], F32, "ext_r")
    cr = k.sbuf([128, 14, 1], F32, "cr")
    _erf = ext_r[:].rearrange("p a b -> p (a b)")
    xm = k.sbuf([128, 14, 128], F32, "xm")
    thad = k.sbuf([128, 128], BF16, "thad")
    sgd = k.sbuf([128, 128], BF16, "sgd")
    bst8 = k.sbuf([128, 8, 6], F32, "bst8")
    bag8 = k.sbuf([128, 8, 2], F32, "bag8")
    mark_rw = k.bot
    rt = [k.sbuf([128, 4, 128], F32, f"rt{i}") for i in range(7)]
    rt.append(cv.view(cv[:, 0:4, :], "rt7"))
    rt.append(cv.view(cv[:, 4:8, :], "rt8"))
    gTs = ext_r.view(_erf[:, 0:512].rearrange("p (a b) -> p a b", a=4), "gTs")
    bonT = ext_r.view(_erf[:, 512:1024].rearrange("p (a b) -> p a b", a=4), "bonT")
    ART = k.sbuf([128, 4, 2, 128], BF16, "ART")
    BTb = k.sbuf([128, 4, 128], BF16, "BTb")
    KTb = k.sbuf([128, 4, 128], BF16, "KTb")
    VTb = k.sbuf([128, 4, 128], BF16, "VTb")
    AB_tm = k.sbuf([128, 2, 512], BF16, "AB_tm")
    KV_tm = k.sbuf([128, 2, 512], BF16, "KV_tm")
    GBm = k.sbuf([128, 4, 256], BF16, "GBm")
    GKm = k.sbuf([128, 4, 256], BF16, "GKm")
    Nn = k.sbuf([128, 4, 128], BF16, "Nn")
    GBm_b = k.sbuf([128, 4, 256], BF16, "GBm_b")
    GKm_b = k.sbuf([128, 4, 256], BF16, "GKm_b")
    Nn_b = k.sbuf([128, 4, 128], BF16, "Nn_b")
    PP_b = [k.sbuf([128, 4, 256], BF16, f"PPb{i}") for i in range(2)]
    XX_b = [k.sbuf([128, 4, 128], BF16, f"XXb{i}") for i in range(2)]
    PP = [k.sbuf([128, 4, 256], BF16, f"PP{i}") for i in range(2)]
    XX = [k.sbuf([128, 4, 128], BF16, f"XX{i}") for i in range(2)]
    QT = k.sbuf([128, 2, 128], BF16, "QT")
    IE = k.sbuf([128, 4, 64], F32, "IE")
    STf = k.sbuf([128, 4, 64], F32, "STf")
    STb = k.sbuf([128, 4, 64], BF16, "STb")
    yn = ext_r.view(_erf[:, 1024:1536].rearrange("p (a b) -> p a b", a=8), "yn")
    k.memset("pool", (STf, STf[:]), 0.0)
    k.memset("pool", (STb, STb[:]), 0.0)
    k.memset("pool", (cr, cr[:]), 0.0)
    k.memset("pool", (vaug, vaug[:]), 1.0)
    k.memset("pool", (Cst, Cst[:]), 0.0)
    k.memset("pool", (Cb, Cb[:]), 0.0)
    k.memset("pool", (mst, mst[:]), 0.0)
    k.memset("pool", (cq, cq[:]), 0.0)
    LNK = math.log(KSCALE)

    def x_rows(ti):
        if ti < NTP:
            return xp, xp[ti * 128:(ti + 1) * 128, :]
        return xs, xs[:, :]

    def norm_to_hT(xbuf, gbc):
        k.act((hb, hb[:]), (xbuf, xbuf[:]), AF.Square, accum=(ss, ss[:]))
        k.ts("dve", (rs, rs[:]), (ss, ss[:]), 1.0 / D, EPS, op0=ALU.mult, op1=ALU.add)
        k.act((rs, rs[:]), (rs, rs[:]), AF.Ln)
        k.act((rs, rs[:]), (rs, rs[:]), AF.Exp, scale=-0.5)
        k.stt("dve", (hb, hb[:]), (xbuf, xbuf[:]), (rs, rs[:, 0:1]), (gbc, gbc[:]), ALU.mult, ALU.mult)
        for kc in range(8):
            k.tr((pT, pT[:, kc * 128:(kc + 1) * 128]), (hb, hb[:, kc * 128:(kc + 1) * 128]), (identb, identb[:]))
        k.copy("act", (hT, hT[:].rearrange("p a b -> p (a b)")), (pT, pT[:, :]))

    def proj_fm(ps, ps_ap, col, M=128):
        g = wgrp(col)
        for kc in range(8):
            k.mm((ps, ps_ap), (W_in, W_in[:, kc, col:col + M], g), (hT, hT[:, kc, :]), start=(kc == 0), stop=(kc == 7))

    def proj_tm(ps, ps_ap, col, N):
        g = wgrp(col)
        for kc in range(8):
            k.mm((ps, ps_ap), (hT, hT[:, kc, :]), (W_in, W_in[:, kc, col:col + N], g), start=(kc == 0), stop=(kc == 7))

    def mixer_tile(ti):
        smp = ti == NTP
        mi = 1 if smp else 0
        NB = NSB if smp else 1
        LB = ST if smp else 128
        xb = xt[ti % 2]
        xd, xap = x_rows(ti)
        if smp:
            k.barrier()
            k.bot = mark_rw
            Cs = k.sbuf([128, NSB, 129], F32, "Cs")
            Csb = k.sbuf([128, NSB, 130], BF16, "Csb")
            qTm = k.sbuf([128, NSB, 128], BF16, "qTm")
            ktmb = k.sbuf([128, NSB, 128], BF16, "ktmb")
            blkF = k.sbuf([128, NSB, 128], BF16, "blkF")
            rowm = k.sbuf([128, NSB], F32, "rowm")
            k.memset("pool", (blkF, blkF[:]), 1.0)
            aff(blkF, blkF[:], [[-ST, NSB], [1, 128]], 0, 0)
            aff(blkF, blkF[:], [[ST, NSB], [-1, 128]], 0, ST - 1)
            k.memset("pool", (rowm, rowm[:]), 1.0)
            aff(rowm, rowm[:], [[-ST, NSB]], 1, 0)
            aff(rowm, rowm[:], [[ST, NSB]], -1, ST - 1)
            smc = cv.view(cv[:].rearrange("p a b -> p (a b)")[0:NSB * 3, :], "smc")
            ext_s = xm.view(xm[:].rearrange("p a b -> p (a b)")[:, 0:8 * NSB * 11].rearrange("p (c b t) -> p c b t", c=8, b=NSB), "ext_s")
            k.dma("sp", smc[:], st_mconv[:, :], reads=[st_mconv], writes=[smc])
            for c in range(8):
                k.tr((pM[0], pM[0][:, c * 48:(c + 1) * 48]), (smc, smc[:, c * 128:(c + 1) * 128]), (identf, identf[0:48, 0:48]))
            k.copy("act", (ext_s, ext_s[:, :, :, 0:3]), (pM[0], pM[0][:, 0:384].rearrange("p (c b j) -> p c b j", c=8, b=NSB)))
            k.dma("sp", mst[:, 0:NSB], st_mm[:, :].rearrange("b h -> h b"), reads=[st_mm], writes=[mst], allow_slow_non_contiguous=True)
        k.dma("sp", xb[:], xap, reads=[xd], writes=[xb])
        norm_to_hT(xb, g1bc)

        if smp:
            for g in range(2):
                for c in range(4):
                    proj_fm(pF[g], pF[g][:, c * 128:(c + 1) * 128], C_QK + (4 * g + c) * 128)
                k.copy("act", (ext_s, ext_s[:, 4 * g:4 * g + 4, :, 3:11]), (pF[g], pF[g][:].rearrange("p (c b t) -> p c b t", c=4, b=NSB)))
            for c in range(8):
                cvv = cv[:, c, :].rearrange("p (b t) -> p b t", t=ST)
                k.ts("dve", (cv, cvv), (ext_s, ext_s[:, c, :, 3:11]), pcol(PT_MCW + 3 * 8 + c), pcol(PT_MCB + c),
                     op0=ALU.mult, op1=ALU.add)
                for j in range(3):
                    k.stt("dve", (cv, cvv), (ext_s, ext_s[:, c, :, j:j + ST]), pcol(PT_MCW + j * 8 + c), (cv, cvv),
                          ALU.mult, ALU.add)
        if not smp:
            k.copy("pool", (ext_q, ext_q[:, :, 0:3]), (cq, cq[:]))
            for g in range(2):
                for c in range(4):
                    proj_fm(pF[g], pF[g][:, c * 128:(c + 1) * 128], C_QK + (4 * g + c) * 128)
                k.copy("act", (ext_q, ext_q[:, 4 * g:4 * g + 4, 3:131]), (pF[g], pF[g][:].rearrange("p (c t) -> p c t", c=4)))
            k.copy("pool", (cq, cq[:]), (ext_q, ext_q[:, :, 128:131]))
            for c in range(8):
                k.ts("dve", (cv, cv[:, c, :]), (ext_q, ext_q[:, c, 3:131]), pcol(PT_MCW + 3 * 8 + c), pcol(PT_MCB + c),
                     op0=ALU.mult, op1=ALU.add)
                for j in range(3):
                    k.stt("dve", (cv, cv[:, c, :]), (ext_q, ext_q[:, c, j:j + 128]), pcol(PT_MCW + j * 8 + c), (cv, cv[:, c, :]),
                          ALU.mult, ALU.add)
        k.act((qkT, qkT[:].rearrange("p a b -> p (a b)")), (cv, cv[:].rearrange("p a b -> p (a b)")), AF.Silu)

        proj_tm(pR[0], pR[0][:, :], C_V, 512)
        k.copy("act", (vaug, vaug[:, :, 0:128]), (pR[0], pR[0][:].rearrange("p (h c) -> p h c", h=4)))
        for c in range(4):
            proj_fm(pF[0], pF[0][:, c * 128:(c + 1) * 128], C_O + c * 128)
        k.act((soT, soT[:].rearrange("p a b -> p (a b)")), (pF[0], pF[0][:, :]), AF.Sigmoid)
        proj_fm(pM[0], pM[0][0:4, 0:128], C_I, M=4)
        proj_fm(pM[0], pM[0][0:4, 128:256], C_F, M=4)
        liT, nlf, ncum, gT_, t0, t1 = gsm[0], gsm[1], gsm[2], gsm[3], gsm[4], gsm[5]
        k.ts("dve", (liT, liT[:]), (pM[0], pM[0][0:4, 0:128]), (gb, gb[:, 0:1]), None, op0=ALU.add)
        k.act((t0, t0[:]), (pM[0], pM[0][0:4, 128:256]), AF.Exp, bias=(nbf, nbf[:, 0:1]), scale=-1.0)
        k.act((nlf, nlf[:]), (t0, t0[:]), AF.Ln, bias=1.0)
        k.scan("dve", (ncum, ncum[:]), (resets[mi], resets[mi][0:4, 0:128]), (nlf, nlf[:]), 0.0, ALU.mult, ALU.add)
        k.tt("dve", (gT_, gT_[:]), (liT, liT[:]), (ncum, ncum[:]), ALU.add)
        b3 = lambda buf: buf[:].rearrange("p (b l) -> p b l", l=LB)
        mcb_ = mst[:, 0:NB].unsqueeze(2).to_broadcast([4, NB, LB])
        nlast = ncum[:].rearrange("p (b l) -> p b l", l=LB)[:, :, LB - 1:LB]
        k.stt("dve", (t0, b3(t0)), (gT_, b3(gT_)), LNK, (mst, mcb_), ALU.add, ALU.subtract)
        k.act((gpk, gpk[:, 0, :]), (t0, t0[:]), AF.Exp)
        k.tt("dve", (t1, b3(t1)), (ncum, b3(ncum)), (mst, mcb_), ALU.subtract)
        k.act((gpk, gpk[:, 1, :]), (t1, t1[:]), AF.Exp)
        k.tt("dve", (t1, b3(t1)), (gT_, b3(gT_)), (ncum, nlast.to_broadcast([4, NB, LB])), ALU.subtract)
        k.red("dve", (gt[0], gt[0][:, 0:NB]), (t1, b3(t1)), ALU.max)
        k.tt("dve", (gt[1], gt[1][:, 0:NB]), (mst, mst[:, 0:NB]), (ncum, nlast.rearrange("p b o -> p (b o)")), ALU.subtract)
        k.tt("dve", (mnew, mnew[:, 0:NB]), (gt[1], gt[1][:, 0:NB]), (gt[0], gt[0][:, 0:NB]), ALU.max)
        k.tt("dve", (gt[2], gt[2][:, 0:NB]), (gt[1], gt[1][:, 0:NB]), (mnew, mnew[:, 0:NB]), ALU.subtract)
        k.act((gt[3], gt[3][:, 0:NB]), (gt[2], gt[2][:, 0:NB]), AF.Exp)
        k.tt("dve", (gpk, gpk[:, 2, :].rearrange("p (b l) -> p b l", l=LB)), (gpk, gpk[:, 0, :].rearrange("p (b l) -> p b l", l=LB)),
             (gt[3], gt[3][:, 0:NB].unsqueeze(2).to_broadcast([4, NB, LB])), ALU.mult)
        for j in range(3):
            k.tr((pM[1], pM[1][:, 4 * j:4 * j + 4]), (gpk, gpk[:, j, :]), (identf, identf[0:4, 0:4]))
        k.copy("dve", (tokS, tokS[:]), (pM[1], pM[1][:, 0:12]))
        k.tt("dve", (s0d, s0d[:, :, 0:NB]), (identf, identf[0:4, 0:4].unsqueeze(2).to_broadcast([4, 4, NB])),
             (gt[3], gt[3][:, 0:NB].unsqueeze(1).to_broadcast([4, 4, NB])), ALU.mult)
        k.mm((pM[1], pM[1][:, 16:16 + 4 * NB]), (ones4, ones4[:]), (s0d, s0d[:, :, 0:NB].rearrange("p a b -> p (a b)")))
        k.copy("dve", (s0bc, s0bc[:, 0:4 * NB]), (pM[1], pM[1][:, 16:16 + 4 * NB]))

        for h in range(4):
            k.mm((pM[0], pM[0][:, h * 128:(h + 1) * 128]), (qkT, qkT[:, 4 + h, :]), (qkT, qkT[:, h, :]))
        for h in range(4):
            k.stt("dve", (PTm, PTm[:, h, :]), (pM[0], pM[0][:, h * 128:(h + 1) * 128]), (tokS, tokS[:, h:h + 1]),
                  (mU_in[mi], mU_in[mi][:]), ALU.mult, ALU.mult)
        pO = [pM[1], pM[2]]
        oap = lambda h: pO[h // 2][:, 256 * (h % 2):256 * (h % 2) + 129]
        if not smp:
            for h in range(4):
                k.mm((pO[h // 2], oap(h)), (qkT, qkT[:, h, :]), (Cb, Cb[:, h, 0:129]), start=True, stop=False)
                k.mm((pO[h // 2], oap(h)), (PTm, PTm[:, h, :]), (vaug, vaug[:, h, 0:129]), start=False, stop=True)
        else:
            for h in range(4):
                k.tr((pT, pT[:, h * 128:(h + 1) * 128]), (qkT, qkT[:, 4 + h, :]), (identb, identb[:]))
            for h in range(4):
                k.ts("dve", (ktm, ktm[:, h, :]), (pT, pT[:, h * 128:(h + 1) * 128]), (tokS, tokS[:, 8 + h:9 + h]), None, op0=ALU.mult)
            for h in range(4):
                k.dma("sp", Cs[:, :, 0:128], st_mC[:, h, :, :].rearrange("b d v -> d b v"), reads=[st_mC], writes=[Cs])
                k.dma("sp", Cs[:, :, 128], st_mn[:, h, :].rearrange("b d -> d b"), reads=[st_mn], writes=[Cs], allow_slow_non_contiguous=True)
                k.copy("act", (Csb, Csb[:, :, 0:129]), (Cs, Cs[:]))
                k.tt("dve", (qTm, qTm[:]), (qkT, qkT[:, h, :].unsqueeze(1).to_broadcast([128, NSB, 128])), (blkF, blkF[:]), ALU.mult)
                for b in range(NSB):
                    k.mm((pO[h // 2], oap(h)), (qTm, qTm[:, b, :]), (Csb, Csb[:, b, 0:129]), start=(b == 0), stop=False)
                k.mm((pO[h // 2], oap(h)), (PTm, PTm[:, h, :]), (vaug, vaug[:, h, 0:129]), start=False, stop=True)
                k.tt("dve", (ktmb, ktmb[:]), (ktm, ktm[:, h, :].unsqueeze(1).to_broadcast([128, NSB, 128])),
                     (rowm, rowm[:].unsqueeze(2).to_broadcast([128, NSB, 128])), ALU.mult)
                for grp in range(4):
                    bank = pF[grp % 2]
                    for bi in range(4):
                        b = 4 * grp + bi
                        k.mm((bank, bank[:, bi * 128:(bi + 1) * 128]), (ktmb, ktmb[:, b, :]), (vaug, vaug[:, h, 0:128]))
                    for bi in range(4):
                        b = 4 * grp + bi
                        k.stt("dve", (Cs, Cs[:, b, 0:128]), (Cs, Cs[:, b, 0:128]), (s0bc, s0bc[:, h * NSB + b:h * NSB + b + 1]),
                              (bank, bank[:, bi * 128:(bi + 1) * 128]), ALU.mult, ALU.add)
                for b in range(NSB):
                    k.mm((pR[0], pR[0][:, b:b + 1]), (ktmb, ktmb[:, b, :]), (vaug, vaug[:, h, 128:129]))
                k.tt("dve", (Cs, Cs[:, :, 128]), (Cs, Cs[:, :, 128]), (s0bc, s0bc[:, h * NSB:(h + 1) * NSB]), ALU.mult)
                k.tt("dve", (Cs, Cs[:, :, 128]), (Cs, Cs[:, :, 128]), (pR[0], pR[0][:, 0:NSB]), ALU.add)
                k.dma("sp", o_sC[:, h, :, :].rearrange("b d v -> d b v"), Cs[:, :, 0:128], reads=[Cs], writes=[o_sC])
                k.dma("sp", o_sn[:, h, :].rearrange("b d -> d b"), Cs[:, :, 128], reads=[Cs], writes=[o_sn], allow_slow_non_contiguous=True)
        for h in range(4):
            k.copy("act", (dn, dn[:, h:h + 1]), (pO[h // 2], oap(h)[:, 128:129]))
        k.stt("dve", (dn, dn[:]), (dn, dn[:]), -1.0, (dn, dn[:]), ALU.mult, ALU.max)
        k.tt("dve", (dn, dn[:]), (dn, dn[:]), (tokS, tokS[:, 4:8]), ALU.max)
        k.op("dve", lambda e: e.reciprocal(out=dn[:], in_=dn[:]), reads=[dn], writes=[dn])
        for h in range(4):
            k.act((hm, hm[:, h, :]), (pO[h // 2], oap(h)[:, 0:128]), AF.Copy, scale=(dn, dn[:, h:h + 1]))
        for h in range(4):
            k.op("dve", lambda e, h=h: e.bn_stats(out=bst[:, h, :], in_=hm[:, h, :]), reads=[hm], writes=[(bst, h)])
        for h in range(4):
            k.op("dve", lambda e, h=h: e.bn_aggr(out=bag[:, h, :], in_=bst[:, h, :]), reads=[(bst, h)], writes=[(bag, h)])
        k.act((bag, bag[:, :, 1:2]), (bag, bag[:, :, 1:2]), AF.Ln, bias=EPS)
        k.act((bag, bag[:, :, 1:2]), (bag, bag[:, :, 1:2]), AF.Exp, scale=-0.5)
        for h in range(4):
            k.ts("dve", (hn, hn[:, h, :]), (hm, hm[:, h, :]), (bag, bag[:, h, 0:1]), (bag, bag[:, h, 1:2]),
                 op0=ALU.subtract, op1=ALU.mult)
        for h in range(4):
            k.tr((pM[0], pM[0][:, h * 128:(h + 1) * 128]), (hn, hn[:, h, :]), (identf, identf[:]))
        for h in range(4):
            k.stt("dve", (hmT_all, hmT_all[:, ti, h, :], ti), (pM[0], pM[0][:, h * 128:(h + 1) * 128]), pcol(PT_MNG + h),
                  (soT, soT[:, h, :]), ALU.mult, ALU.mult)
        if not smp:
            for h in range(4):
                k.tr((pT, pT[:, h * 128:(h + 1) * 128]), (qkT, qkT[:, 4 + h, :]), (identb, identb[:]))
            for h in range(4):
                k.ts("dve", (ktm, ktm[:, h, :]), (pT, pT[:, h * 128:(h + 1) * 128]), (tokS, tokS[:, 8 + h:9 + h]), None, op0=ALU.mult)
            for h in range(4):
                k.mm((pO[h // 2], oap(h)), (ktm, ktm[:, h, :]), (vaug, vaug[:, h, 0:129]))
            for h in range(4):
                k.stt("dve", (Cst, Cst[:, h, :]), (Cst, Cst[:, h, :]), (s0bc, s0bc[:, h:h + 1]), (pO[h // 2], oap(h)),
                      ALU.mult, ALU.add)
            k.copy("act", (Cb, Cb[:, :, 0:129]), (Cst, Cst[:]))
            k.copy("dve", (mst, mst[:, 0:1]), (mnew, mnew[:, 0:1]))
        if ti == NTP - 1:
            for h in range(4):
                k.dma("sp", o_pC[h], Cst[:, h, 0:128], reads=[Cst], writes=[o_pC])
            k.dma("sp", o_pn[:].rearrange("h d -> d h"), Cst[:, :, 128], reads=[Cst], writes=[o_pn], allow_slow_non_contiguous=True)
            k.dma("sp", o_pm[:].rearrange("o h -> h o"), mnew[:, 0:1], reads=[mnew], writes=[o_pm], allow_slow_non_contiguous=True)
            for blk in range(2):
                proj_tm(pR[blk], pR[blk][:, :], C_QK + blk * 512, 512)
                k.copy("act", (zq_tm, zq_tm[:, blk * 512:(blk + 1) * 512]), (pR[blk], pR[blk][:, :]))
            k.dma("sp", o_pconv[:], zq_tm[125:128, :], reads=[zq_tm], writes=[o_pconv])
        if smp:
            k.dma("sp", o_sm[:, :].rearrange("b h -> h b"), mnew[:, 0:NSB], reads=[mnew], writes=[o_sm], allow_slow_non_contiguous=True)
            for blk in range(2):
                proj_tm(pR[blk], pR[blk][:, :], C_QK + blk * 512, 512)
                k.copy("act", (zq_tm, zq_tm[:, blk * 512:(blk + 1) * 512]), (pR[blk], pR[blk][:, :]))
            for b in range(NSB):
                k.dma("sp", o_sconv[3 * b:3 * b + 3, :], zq_tm[ST * b + 5:ST * b + 8, :], reads=[zq_tm], writes=[o_sconv])

    k.fence_mm = (pT, identb)
    BK = [pM[0], pM[1], pF[0], pF[1], pR[0], pR[1]]

    _rw_stop = int(_os.environ.get("KDBG_RW", "99"))

    def rwkv_tile(ti):
        smp = ti == NTP
        mi = 0
        NLV = 7
        rtl = rt
        if not smp:
            k.copy("pool", (ext_r, ext_r[:, :, 0:1]), (cr, cr[:]))
            for g in range(4):
                n = min(4, 14 - 4 * g)
                for c in range(n):
                    proj_fm(pF[g % 2], pF[g % 2][:, c * 128:(c + 1) * 128], C_R + (4 * g + c) * 128)
                k.copy("act", (ext_r, ext_r[:, 4 * g:4 * g + n, 1:129]),
                       (pF[g % 2], pF[g % 2][:, 0:n * 128].rearrange("p (c t) -> p c t", c=n)))
            k.copy("pool", (cr, cr[:]), (ext_r, ext_r[:, :, 128:129]))
            k.tt("pool", (xm, xm[:]), (ext_r, ext_r[:, :, 0:128]), (ext_r, ext_r[:, :, 1:129]), ALU.subtract)
            for c in range(14):
                k.stt("dve", (xm, xm[:, c, :]), (xm, xm[:, c, :]), pcol(PT_RMIX + c), (ext_r, ext_r[:, c, 1:129]), ALU.mult, ALU.add)
        else:
            k.barrier()
            k.bot = mark_rw
            ext_rs = k.sbuf([128, 14, NSB, ST + 1], F32, "ext_rs")
            rtl = [k.sbuf([128, 4, 128], F32, f"rts{i}") for i in range(7)] + [rt[7], rt[8]]
            stg = k.sbuf([128, 512], F32, "stg")
            srs = xm.view(xm[:].rearrange("p a b -> p (a b)")[0:NSB, :], "srs")
            k.dma("sp", srs[:], st_rshift[:, :], reads=[st_rshift], writes=[srs])
            for c in range(14):
                k.tr((pM[0], pM[0][:, c * NSB:(c + 1) * NSB]), (srs, srs[:, c * 128:(c + 1) * 128]), (identf, identf[0:NSB, 0:NSB]))
            k.copy("act", (ext_rs, ext_rs[:, :, :, 0]), (pM[0], pM[0][:, 0:14 * NSB].rearrange("p (c b) -> p c b", c=14)))
            for g in range(4):
                n = min(4, 14 - 4 * g)
                for c in range(n):
                    proj_fm(pF[g % 2], pF[g % 2][:, c * 128:(c + 1) * 128], C_R + (4 * g + c) * 128)
                k.copy("act", (ext_rs, ext_rs[:, 4 * g:4 * g + n, :, 1:ST + 1]),
                       (pF[g % 2], pF[g % 2][:, 0:n * 128].rearrange("p (c b t) -> p c b t", c=n, b=NSB)))
            xm4 = xm[:].rearrange("p c (b t) -> p c b t", t=ST)
            k.tt("pool", (xm, xm4), (ext_rs, ext_rs[:, :, :, 0:ST]), (ext_rs, ext_rs[:, :, :, 1:ST + 1]), ALU.subtract)
            for c in range(14):
                k.stt("dve", (xm, xm4[:, c]), (xm, xm4[:, c]), pcol(PT_RMIX + c), (ext_rs, ext_rs[:, c, :, 1:ST + 1]), ALU.mult, ALU.add)
        rT, krT, vrT = xm[:, 0:4, :], xm[:, 4:8, :], xm[:, 8:12, :]
        sig, cums, gam, ginv, gexc, a_, kk, tmp, kr2 = rtl
        if _rw_stop <= 1:
            return
        k.act((thad, thad[0:64, :]), (xm, xm[0:64, 12, :]), AF.Tanh)
        k.copy("act", (thad, thad[64:128, :]), (xm, xm[64:128, 12, :]))
        k.act((sgd, sgd[:]), (xm, xm[:, 13, :]), AF.Sigmoid)
        for c in range(4):
            k.mm((pM[0], pM[0][:, c * 128:(c + 1) * 128]), (Wl_w2, Wl_w2[0:64, c * 128:(c + 1) * 128]), (thad, thad[0:64, :]))
        for c in range(4):
            k.act((sig, sig[:, c, :]), (pM[0], pM[0][:, c * 128:(c + 1) * 128]), AF.Sigmoid, bias=pcol(PT_RW0 + c))
        k.pe_fence()
        for c in range(4):
            k.mm((pM[1], pM[1][:, c * 128:(c + 1) * 128]), (Wl_a2, Wl_a2[64:128, c * 128:(c + 1) * 128]), (thad, thad[64:128, :]))
        k.pe_fence()
        for c in range(4):
            k.act((a_, a_[:, c, :]), (pM[1], pM[1][:, c * 128:(c + 1) * 128]), AF.Sigmoid, bias=pcol(PT_RA0 + c))
        for c in range(4):
            k.mm((pM[2], pM[2][:, c * 128:(c + 1) * 128]), (Wl_g2, Wl_g2[:, c * 128:(c + 1) * 128]), (sgd, sgd[:]))
        k.copy("act", (gTs, gTs[:].rearrange("p a b -> p (a b)")), (pM[2], pM[2][:, :]))
        if _rw_stop <= 2:
            return
        fl = lambda b: b[:].rearrange("p a b -> p (a b)")
        if not smp:
            k.scan("dve", (cums, fl(cums)), (resets[mi], resets[mi][:]), (sig, fl(sig)), 0.0, ALU.mult, ALU.add)
            k.act((gam, fl(gam)), (cums, fl(cums)), AF.Exp, scale=WSCALE)
            k.act((ginv, fl(ginv)), (cums, fl(cums)), AF.Exp, scale=-WSCALE)
            k.tt("pool", (tmp, tmp[:]), (cums, cums[:]), (sig, sig[:]), ALU.subtract)
            k.act((gexc, fl(gexc)), (tmp, fl(tmp)), AF.Exp, scale=WSCALE)
        else:
            k.act((gam, fl(gam)), (sig, fl(sig)), AF.Exp, scale=WSCALE)
        if _rw_stop <= 3:
            return
        for c in range(4):
            k.ts("dve", (kk, kk[:, c, :]), (xm, xm[:, 4 + c, :]), pcol(PT_RKK + c), None, op0=ALU.mult)
        k.tt("pool", (tmp, tmp[:]), (kk, kk[:]), (kk, kk[:]), ALU.mult)
        for c in range(4):
            k.mm((pM[0], pM[0][:, c * 128:(c + 1) * 128]), (bones, bones[:]), (tmp, tmp[:, c, :]))
        k.ts("dve", (tmp, fl(tmp)), (pM[0], pM[0][:, :]), 1e-24, None, op0=ALU.max)
        k.act((tmp, fl(tmp)), (tmp, fl(tmp)), AF.Ln)
        k.act((tmp, fl(tmp)), (tmp, fl(tmp)), AF.Exp, scale=-0.5)
        k.tt("dve", (kk, kk[:]), (kk, kk[:]), (tmp, tmp[:]), ALU.mult)
        for c in range(4):
            k.ts("dve", (tmp, tmp[:, c, :]), (a_, a_[:, c, :]), -1.0, pcol(PT_RKA + c), op0=ALU.add, op1=ALU.mult)
        k.stt("dve", (kr2, kr2[:]), (tmp, tmp[:]), 1.0, (xm, krT), ALU.add, ALU.mult)
        k.tt("pool", (tmp, tmp[:]), (xm, rT), (kr2, kr2[:]), ALU.mult)
        for c in range(4):
            k.ts("dve", (tmp, tmp[:, c, :]), (tmp, tmp[:, c, :]), pcol(PT_RRK + c), None, op0=ALU.mult)
        for c in range(4):
            k.mm((pM[1], pM[1][:, c * 128:(c + 1) * 128]), (bones, bones[:]), (tmp, tmp[:, c, :]))
        k.tt("dve", (bonT, fl(bonT)), (pM[1], pM[1][:, :]), (xm, vrT.rearrange("p a b -> p (a b)") if False else xm[:, 8:12, :].rearrange("p a b -> p (a b)")), ALU.mult)
        if _rw_stop <= 4:
            return
        if smp:
            rwkv_sample_core(xm, gam, kr2, kk, a_, tmp, stg, gTs, bonT)
            return
        k.stt("dve", (ART, ART[:, :, 0, :]), (kk, kk[:]), -1.0, (gexc, gexc[:]), ALU.mult, ALU.mult)
        k.tt("pool", (ART, ART[:, :, 1, :]), (xm, rT), (gam, gam[:]), ALU.mult)
        k.tt("pool", (tmp, tmp[:]), (kk, kk[:]), (a_, a_[:]), ALU.mult)
        k.tt("dve", (BTb, BTb[:]), (tmp, tmp[:]), (ginv, ginv[:]), ALU.mult)
        k.tt("pool", (KTb, KTb[:]), (kr2, kr2[:]), (ginv, ginv[:]), ALU.mult)
        k.copy("act", (VTb, VTb[:]), (xm, vrT))
        if _rw_stop <= 5:
            return
        for c in range(4):
            k.tr((pT, pT[:, c * 128:(c + 1) * 128]), (ART, ART[:, c, 0, :]), (identb, identb[:]))
            k.tr((pT, pT[:, 512 + c * 128:512 + (c + 1) * 128]), (BTb, BTb[:, c, :]), (identb, identb[:]))
        k.copy("act", (AB_tm, AB_tm[:].rearrange("p a b -> p (a b)")), (pT, pT[:, :]))
        for c in range(4):
            k.tr((pT, pT[:, c * 128:(c + 1) * 128]), (KTb, KTb[:, c, :]), (identb, identb[:]))
            k.tr((pT, pT[:, 512 + c * 128:512 + (c + 1) * 128]), (VTb, VTb[:, c, :]), (identb, identb[:]))
        k.copy("dve", (KV_tm, KV_tm[:].rearrange("p a b -> p (a b)")), (pT, pT[:, :]))
        if _rw_stop <= 6:
            return
        A_tm = lambda h: (AB_tm, AB_tm[:, 0, h * 64:(h + 1) * 64])
        B_tm = lambda h: (AB_tm, AB_tm[:, 1, h * 64:(h + 1) * 64])
        K_tm = lambda h: (KV_tm, KV_tm[:, 0, h * 64:(h + 1) * 64])
        V_tm = lambda h: (KV_tm, KV_tm[:, 1, h * 64:(h + 1) * 64])
        m2b = mask2[mi][:].unsqueeze(1).to_broadcast([128, 2, 256])
        GB2, GK2, Nn2, PP2, XX2 = [GBm, GBm_b], [GKm, GKm_b], [Nn, Nn_b], [PP, PP_b], [XX, XX_b]
        LB3 = [[pM[0], pM[1], pR[0]], [pF[0], pF[1], pR[1]]]
        for g in range(2):
            GBm_, GKm_, Nn_, XX_ = GB2[g], GK2[g], Nn2[g], XX2[g]
            heads = [4 * g + i for i in range(4)]
            HO = [(pbs, [(i, h) for i, h in enumerate(heads) if 64 * (h % 2) == pbs]) for pbs in (0, 64)]
            for pbs, hl in HO:
                for i, h in hl:
                    c, pb = h // 2, 64 * (h % 2)
                    off = (i % 2) * 256
                    rAR = (ART, ART[pb:pb + 64, c, :, :].rearrange("p a t -> p (a t)"))
                    k.mm((BK[i // 2], BK[i // 2][:, off:off + 256]), (BTb, BTb[pb:pb + 64, c, :]), rAR)
                    k.mm((BK[2 + i // 2], BK[2 + i // 2][:, off:off + 256]), (KTb, KTb[pb:pb + 64, c, :]), rAR)
                    k.mm((BK[4], BK[4][:, i * 128:(i + 1) * 128]), (ART, ART[pb:pb + 64, c, 0, :]), (BTb, BTb[pb:pb + 64, c, :]))
                k.pe_fence()
            for hf in range(2):
                k.tt("dve", (GBm_, GBm_[:, 2 * hf:2 * hf + 2, :]), (BK[hf], BK[hf][:].rearrange("p (a b) -> p a b", a=2)), (mask2[mi], m2b), ALU.mult)
                k.tt("dve", (GKm_, GKm_[:, 2 * hf:2 * hf + 2, :]), (BK[2 + hf], BK[2 + hf][:].rearrange("p (a b) -> p a b", a=2)), (mask2[mi], m2b), ALU.mult)
            k.tt("dve", (Nn_, Nn_[:]), (BK[4], BK[4][:].rearrange("p (a b) -> p a b", a=4)),
                 (mL_st[mi], mL_st[mi][:].unsqueeze(1).to_broadcast([128, 4, 128])), ALU.mult)
            for i, h in enumerate(heads):
                k.mm((BK[5], BK[5][:, i * 64:(i + 1) * 64]), (GKm_, GKm_[:, i, 0:128]), V_tm(h))
            k.copy("act", (XX_[0], XX_[0][:, :, 64:128]), (BK[5], BK[5][:, 0:256].rearrange("p (a b) -> p a b", a=4)))
            k.copy("pool", (XX_[0], XX_[0][:, :, 0:64]), (AB_tm, AB_tm[:, 0, 256 * g:256 * g + 256].rearrange("p (a b) -> p a b", a=4)))

        xfinal = [None, None]

        def levels_gen(g):
            GBm_, Nn_, PP_, XX_ = GB2[g], Nn2[g], PP2[g], XX2[g]
            bP, bQ, bX = LB3[g]
            Pc = lambda i: (Nn_, Nn_[:, i, :])
            PTc = lambda i: (GBm_, GBm_[:, i, 0:128])
            xi = 0
            for lvl in range(NLV):
                Xc, Xn = XX_[xi], XX_[1 - xi]
                for i in range(4):
                    o = (bX, bX[:, i * 128:(i + 1) * 128])
                    k.mm(o, (identb, identb[:]), (Xc, Xc[:, i, :]), start=True, stop=False)
                    k.mm(o, PTc(i), (Xc, Xc[:, i, :]), start=False, stop=True)
                k.copy("act", (Xn, Xn[:].rearrange("p a b -> p (a b)")), (bX, bX[:, :]))
                xi = 1 - xi
                yield
                if lvl < NLV - 1:
                    bb = [bP, bQ]
                    for i in range(4):
                        off = (i % 2) * 256
                        if lvl < NLV - 2:
                            k.mm((bb[i // 2], bb[i // 2][:, off:off + 128]), PTc(i), Pc(i))
                        k.mm((bb[i // 2], bb[i // 2][:, off + 128:off + 256]), Pc(i), PTc(i))
                    PPn = PP_[lvl % 2]
                    for hf in range(2):
                        if lvl < NLV - 2:
                            k.copy("dve", (PPn, PPn[:, 2 * hf:2 * hf + 2, :]), (bb[hf], bb[hf][:].rearrange("p (a b) -> p a b", a=2)))
                        else:
                            k.copy("dve", (PPn, PPn[:, 2 * hf:2 * hf + 2, 128:256]),
                                   (bb[hf], bb[hf][:].rearrange("p (a b) -> p a b", a=2)[:, :, 128:256]))
                    Pc = lambda i, PPn=PPn: (PPn, PPn[:, i, 0:128])
                    PTc = lambda i, PPn=PPn: (PPn, PPn[:, i, 128:256])
                    yield
            xfinal[g] = XX_[xi]

        gens = [levels_gen(0), levels_gen(1)]
        while gens:
            for g_ in list(gens):
                try:
                    next(g_)
                except StopIteration:
                    gens.remove(g_)

        for g in range(2):
            heads = [4 * g + i for i in range(4)]
            HO = [(pbs, [(i, h) for i, h in enumerate(heads) if 64 * (h % 2) == pbs]) for pbs in (0, 64)]
            GBt, GKt = GB2[g], GK2[g]
            Xf = xfinal[g]
            if _rw_stop <= 8:
                continue
            k.pe_fence()
            for pbs, hl in HO:
                for i, h in hl:
                    c, pb = h // 2, 64 * (h % 2)
                    ci = i // 2
                    o = (BK[0], BK[0][pb:pb + 64, ci * 128:(ci + 1) * 128])
                    k.mm(o, (Xf, Xf[:, i, 0:64]), (GBt, GBt[:, i, 128:256]), start=True, stop=False)
                    k.pe_fence()
                    k.mm(o, (identb, identb[pb:pb + 64, pb:pb + 64]), (ART, ART[pb:pb + 64, c, 1, :]), start=False, stop=True)
                    k.pe_fence()
            k.copy("act", (QT, QT[:].rearrange("p a b -> p (a b)")), (BK[0], BK[0][:, 0:256]))
            for pbs, hl in HO:
                for i, h in hl:
                    c, pb = h // 2, 64 * (h % 2)
                    ci = i // 2
                    o = (pM[2], pM[2][:, h * 64:(h + 1) * 64])
                    k.mm(o, (QT, QT[pb:pb + 64, ci, :]), (STb, STb[pb:pb + 64, c, :]), start=True, stop=False)
                    k.pe_fence()
                    k.mm(o, (GBt, GBt[:, i, 128:256]), (Xf, Xf[:, i, 64:128]), start=False, stop=False)
                    k.mm(o, (GKt, GKt[:, i, 128:256]), V_tm(h), start=False, stop=True)
                    k.pe_fence()
            if _rw_stop <= 9:
                continue
            for pbs, hl in HO:
                for i, h in hl:
                    c, pb = h // 2, 64 * (h % 2)
                    ci = i // 2
                    k.mm((BK[1], BK[1][pb:pb + 64, ci * 64:(ci + 1) * 64]), (Xf, Xf[:, i, 0:64]), B_tm(h))
                k.pe_fence()
            k.tt("dve", (IE, IE[:, 2 * g:2 * g + 2, :]), (BK[1], BK[1][:, 0:128].rearrange("p (a b) -> p a b", a=2)),
                 (I2, I2[:].unsqueeze(1).to_broadcast([128, 2, 64])), ALU.add)
            for pbs, hl in HO:
                for i, h in hl:
                    c, pb = h // 2, 64 * (h % 2)
                    ci = i // 2
                    o = (BK[2], BK[2][pb:pb + 64, ci * 64:(ci + 1) * 64])
                    k.mm(o, (IE, IE[pb:pb + 64, c, :]), (STf, STf[pb:pb + 64, c, :]), start=True, stop=False)
                    k.pe_fence()
                    k.mm(o, B_tm(h), (Xf, Xf[:, i, 64:128]), start=False, stop=False)
                    k.mm(o, K_tm(h), V_tm(h), start=False, stop=True)
                    k.pe_fence()
            for ci in range(2):
                c = 2 * g + ci
                k.ts("dve", (STf, STf[:, c, :]), (BK[2], BK[2][:, ci * 64:(ci + 1) * 64]), (gam, gam[:, c, 127:128]), None, op0=ALU.mult)
            k.copy("act", (STb, STb[:, 2 * g:2 * g + 2, :]), (STf, STf[:, 2 * g:2 * g + 2, :]))
        if _rw_stop <= 10:
            return
        rwkv_epilogue(ti, pM[2], tmp)
        if ti == NTP - 1:
            for c in range(4):
                k.tr((pM[0], pM[0][0:64, c * 128:(c + 1) * 128]), (STf, STf[:, c, :]), (identf, identf[:]))
            k.copy("act", (rt[0], rt[0][0:64, :, :]), (pM[0], pM[0][0:64, :].rearrange("p (a b) -> p a b", a=4)))
            k.dma("sp", o_pS[:].rearrange("(h i) j -> i h j", h=8), rt[0][0:64, :, :].rearrange("p c (f j) -> p (c f) j", f=2),
                  reads=[rt[0]], writes=[o_pS])

    def rwkv_epilogue(ti, Yb, tmp):
        pM2 = [None, None, Yb]
        for h in range(8):
            k.op("dve", lambda e, h=h: e.bn_stats(out=bst8[:, h, :], in_=Yb[:, h * 64:(h + 1) * 64]), reads=[Yb], writes=[(bst8, h)])
        for h in range(8):
            k.op("dve", lambda e, h=h: e.bn_aggr(out=bag8[:, h, :], in_=bst8[:, h, :]), reads=[(bst8, h)], writes=[(bag8, h)])
        k.act((bag8, bag8[:, :, 1:2]), (bag8, bag8[:, :, 1:2]), AF.Ln, bias=GN_EPS)
        k.act((bag8, bag8[:, :, 1:2]), (bag8, bag8[:, :, 1:2]), AF.Exp, scale=-0.5)
        for h in range(8):
            k.ts("dve", (yn, yn[:, h, :]), (Yb, Yb[:, h * 64:(h + 1) * 64]), (bag8, bag8[:, h, 0:1]), (bag8, bag8[:, h, 1:2]),
                 op0=ALU.subtract, op1=ALU.mult)
        for c in range(4):
            k.tr((pM[0], pM[0][:, c * 128:(c + 1) * 128]), (yn, yn[:, 2 * c:2 * c + 2, :].rearrange("p a b -> p (a b)")), (identf, identf[:]))
        for c in range(4):
            k.ts("dve", (tmp, tmp[:, c, :]), (pM[0], pM[0][:, c * 128:(c + 1) * 128]), pcol(PT_RLNG + c), pcol(PT_RLNB + c),
                 op0=ALU.mult, op1=ALU.add)
        k.tt("pool", (tmp, tmp[:]), (tmp, tmp[:]), (bonT, bonT[:]), ALU.add)
        k.tt("dve", (yrgT_all, yrgT_all[:, ti, :, :], ti), (tmp, tmp[:]), (gTs, gTs[:]), ALU.mult)

    rsc = k.dram("rw_scratch", [6, 128, RW], F32)
    ysc = k.dram("ry_scratch", [128, RW], F32)

    def rwkv_sample_core(xm, dec, kr2, kk, a_, tmp, stg, gTs, bonT):
        ti = NTP
        srcs = []
        srcs.append((xm, lambda c: xm[:, c, :]))
        srcs.append((dec, lambda c: dec[:, c, :]))
        srcs.append((kr2, lambda c: kr2[:, c, :]))
        srcs.append((xm, lambda c: xm[:, 8 + c, :]))
        for q in range(6):
            if q == 4:
                k.ts("dve", (tmp, tmp[:]), (kk, kk[:]), -1.0, None, op0=ALU.mult)
                sb_, fn = tmp, (lambda c: tmp[:, c, :])
            elif q == 5:
                k.tt("dve", (tmp, tmp[:]), (kk, kk[:]), (a_, a_[:]), ALU.mult)
                sb_, fn = tmp, (lambda c: tmp[:, c, :])
            else:
                sb_, fn = srcs[q]
            pb_ = pM[q % 2]
            for c in range(4):
                k.tr((pb_, pb_[:, c * 128:(c + 1) * 128]), (sb_, fn(c)), (identf, identf[:]))
            k.copy("act", (stg, stg[:]), (pb_, pb_[:, :]))
            k.dma("sp", rsc[q], stg[:], reads=[stg], writes=[(rsc, q)])
        for blk, (c0, n) in enumerate(((0, 512), (512, 512), (1024, 512), (1536, 256))):
            proj_tm(pR[blk % 2], pR[blk % 2][:, 0:n], C_R + c0, n)
            k.copy("act", (stg, stg[:, 0:n]), (pR[blk % 2], pR[blk % 2][:, 0:n]))
            for b in range(NSB):
                k.dma("sp", o_sshift[b:b + 1, c0:c0 + n], stg[ST * b + ST - 1:ST * b + ST, 0:n], reads=[stg], writes=[o_sshift])
        k.barrier()
        k.bot = mark_rw
        vec6 = k.sbuf([128, 6, ST, RN], F32, "vec6")
        Ssb = k.sbuf([128, RN, RN], F32, "Ssb")
        tmpS = k.sbuf([128, RN, RN], F32, "tmpS")
        sa = k.sbuf([128, RN], F32, "sa")
        ys = k.sbuf([128, ST, RN], F32, "ys")
        Ytm = k.sbuf([128, RW], F32, "Ytm")
        k.dma("sp", Ssb[:].rearrange("p a b -> p (a b)"), st_rS[:, :], reads=[st_rS], writes=[Ssb])
        for q in range(6):
            for b in range(NSB):
                k.dma("sp", vec6[RH * b:RH * b + RH, q, :, :], rsc[q, ST * b:ST * b + ST, :].rearrange("t (h j) -> h t j", h=RH),
                      reads=[(rsc, q)], writes=[(vec6, q)])
        HV = RN // 2

        def rec_gen(hf):
            i0 = hf * HV
            S_ = (Ssb, Ssb[:, i0:i0 + HV, :], hf)
            T_ = (tmpS, tmpS[:, i0:i0 + HV, :], hf)
            bc = lambda q, t: (vec6, vec6[:, q, t, :].unsqueeze(1).to_broadcast([128, HV, RN]), q)
            for t in range(ST):
                k.tt("dve", T_, S_, bc(4, t), ALU.mult)
                yield
                k.red("dve", (sa, sa[:, i0:i0 + HV], hf), T_, ALU.add)
                yield
                k.tt("pool", S_, S_, bc(1, t), ALU.mult)
                yield
                k.tt("dve", T_, (sa, sa[:, i0:i0 + HV].unsqueeze(2).to_broadcast([128, HV, RN]), hf), bc(5, t), ALU.mult)
                yield
                k.tt("dve", S_, S_, T_, ALU.add)
                yield
                k.tt("pool", T_, (vec6, vec6[:, 3, t, i0:i0 + HV].unsqueeze(2).to_broadcast([128, HV, RN]), 3), bc(2, t), ALU.mult)
                yield
                k.tt("dve", S_, S_, T_, ALU.add)
                yield
                k.tt("pool", T_, S_, bc(0, t), ALU.mult)
                yield
                k.red("dve", (ys, ys[:, t, i0:i0 + HV], (hf, t)), T_, ALU.add)
                yield

        gens_ = [rec_gen(0), rec_gen(1)]
        while gens_:
            for g_ in list(gens_):
                try:
                    next(g_)
                except StopIteration:
                    gens_.remove(g_)
        k.dma("sp", o_sS[:, :], Ssb[:].rearrange("p a b -> p (a b)"), reads=[Ssb], writes=[o_sS])
        k.dma("sp", ysc[:, :], ys[:].rearrange("p a b -> p (a b)"), reads=[ys], writes=[ysc])
        for b in range(NSB):
            k.dma("sp", Ytm[ST * b:ST * b + ST, :].rearrange("t (h i) -> t h i", h=RH),
                  ysc[RH * b:RH * b + RH, :].rearrange("h (t i) -> t h i", t=ST), reads=[ysc], writes=[Ytm])
        rwkv_epilogue(ti, Ytm, rt[7])

    def tail_rows(ti):
        if ti != NTP - 1:
            return
        for blk, (c0, n) in enumerate(((0, 512), (512, 512), (1024, 512), (1536, 256))):
            proj_tm(pR[blk % 2], pR[blk % 2][:, 0:n], C_R + c0, n)
            k.copy("act", (rt[1], rt[1][96:128, :, :].rearrange("p a b -> p (a b)")[:, 0:n]), (pR[blk % 2], pR[blk % 2][96:128, 0:n]))
            k.dma("sp", o_pshift[0:1, c0:c0 + n], rt[1][127:128, :, :].rearrange("p a b -> p (a b)")[:, 0:n], reads=[rt[1]], writes=[o_pshift])

    tiles_all = list(range(NT)) if stage >= 5 else list(range(NTP))

    def phase_1b(k):
        k.barrier()
        k.bot = mark_1a
        W_g = k.sbuf([128, 8, 2048], BF16, "W_g")
        W_bm = k.sbuf([128, 4, D], BF16, "W_bm")
        W_br = k.sbuf([128, 4, D], BF16, "W_br")
        W_out = k.sbuf([128, 8, D], BF16, "W_out")
        k.dma("pool", W_bm[:], d_w_bm[:], reads=[d_w_bm], writes=[W_bm])
        for kh in range(2):
            k.dma("pool", W_g[:, 4 * kh:4 * kh + 4, 0:1024], d_w_in[:, 4 * kh:4 * kh + 4, C_G:C_G + 1024], reads=[d_w_in], writes=[(W_g, "a")])
        k.dma("pool", W_br[:], d_w_br[:], reads=[d_w_br], writes=[W_br])
        for kh in range(2):
            k.dma("pool", W_g[:, 4 * kh:4 * kh + 4, 1024:2048], d_w_in[:, 4 * kh:4 * kh + 4, C_G + 1024:C_G + 2048], reads=[d_w_in], writes=[(W_g, "b")])
        k.dma("pool", W_out[:], d_w_out[:], reads=[d_w_out], writes=[W_out])
        xtb = [k.sbuf([128, D], F32, f"xtb{i}") for i in range(2)]
        hb2 = k.sbuf([128, D], BF16, "hb2")
        hT2 = k.sbuf([128, 8, 128], BF16, "hT2")
        ss2 = k.sbuf([128, 1], F32, "ss2")
        rs2 = k.sbuf([128, 1], F32, "rs2")
        sgb = [k.sbuf([128, 512], F32, f"sgb{i}") for i in range(2)]
        yab = k.sbuf([128, D], F32, "yab")
        mg = [k.sbuf([128, D], BF16, f"mg{i}") for i in range(2)]
        mT = k.sbuf([128, 8, 128], BF16, "mT")
        def head_gen(ti):
            xb = xtb[ti % 2]
            xd, xap = x_rows(ti)
            k.dma("sp", xb[:], xap, reads=[xd], writes=[xb])
            norm_generic(xb, g1bc, hb2, hT2, ss2, rs2)
            yield
            for half, (Wb, src, key) in enumerate(((W_bm, hmT_all, "a"), (W_br, yrgT_all, "b"))):
                for blk in range(2):
                    for kc in range(4):
                        k.mm((pR[blk], pR[blk][:, :]), (src, src[:, ti, kc, :], ti), (Wb, Wb[:, kc, blk * 512:(blk + 1) * 512]),
                             start=(kc == 0), stop=(kc == 3))
                    col = half * 1024 + blk * 512
                    for kc in range(8):
                        k.mm((pF[blk], pF[blk][:, :]), (hT2, hT2[:, kc, :]), (W_g, W_g[:, kc, col:col + 512], key),
                             start=(kc == 0), stop=(kc == 7))
                    k.act((sgb[blk], sgb[blk][:]), (pF[blk], pF[blk][:, :]), AF.Sigmoid)
                    if half == 0:
                        k.tt("dve", (yab, yab[:, blk * 512:(blk + 1) * 512]), (sgb[blk], sgb[blk][:]), (pR[blk], pR[blk][:, :]), ALU.mult)
                    else:
                        k.tt("dve", (sgb[blk], sgb[blk][:]), (sgb[blk], sgb[blk][:]), (pR[blk], pR[blk][:, :]), ALU.mult)
                        k.tt("pool", (mg[ti % 2], mg[ti % 2][:, blk * 512:(blk + 1) * 512]), (sgb[blk], sgb[blk][:]), (yab, yab[:, blk * 512:(blk + 1) * 512]), ALU.add)
                    yield

        def tailb_gen(ti):
            xb = xtb[ti % 2]
            mgt = mg[ti % 2]
            for kc in range(8):
                k.tr((pT, pT[:, kc * 128:(kc + 1) * 128]), (mgt, mgt[:, kc * 128:(kc + 1) * 128]), (identb, identb[:]))
            k.copy("act", (mT, mT[:].rearrange("p a b -> p (a b)")), (pT, pT[:, :]))
            yield
            for blk in range(2):
                for kc in range(8):
                    k.mm((pM[blk], pM[blk][:, :]), (mT, mT[:, kc, :]), (W_out, W_out[:, kc, blk * 512:(blk + 1) * 512]),
                         start=(kc == 0), stop=(kc == 7))
                k.tt("dve", (xb, xb[:, blk * 512:(blk + 1) * 512]), (xb, xb[:, blk * 512:(blk + 1) * 512]), (pM[blk], pM[blk][:, :]), ALU.add)
                yield
            k.dma("sp", x1s[ti * 128:(ti + 1) * 128, :], xb[:], reads=[xb], writes=[(x1s, ti)])

        def rr(gens):
            gens = list(gens)
            while gens:
                for g_ in list(gens):
                    try:
                        next(g_)
                    except StopIteration:
                        gens.remove(g_)

        tlb = list(tiles_all)
        rr([head_gen(tlb[0])])
        for idx, ti in enumerate(tlb):
            gl = [tailb_gen(ti)]
            if idx + 1 < len(tlb):
                gl.append(head_gen(tlb[idx + 1]))
            rr(gl)

    def norm_generic(xbuf, gbc, hb_, hT_, ss_, rs_):
        k.act((hb_, hb_[:]), (xbuf, xbuf[:]), AF.Square, accum=(ss_, ss_[:]))
        k.ts("dve", (rs_, rs_[:]), (ss_, ss_[:]), 1.0 / D, EPS, op0=ALU.mult, op1=ALU.add)
        k.act((rs_, rs_[:]), (rs_, rs_[:]), AF.Ln)
        k.act((rs_, rs_[:]), (rs_, rs_[:]), AF.Exp, scale=-0.5)
        k.stt("dve", (hb_, hb_[:]), (xbuf, xbuf[:]), (rs_, rs_[:, 0:1]), (gbc, gbc[:]), ALU.mult, ALU.mult)
        for kc in range(8):
            k.tr((pT, pT[:, kc * 128:(kc + 1) * 128]), (hb_, hb_[:, kc * 128:(kc + 1) * 128]), (identb, identb[:]))
        k.copy("act", (hT_, hT_[:].rearrange("p a b -> p (a b)")), (pT, pT[:, :]))

    def phase_2(k):
        k.barrier()
        k.bot = mark_phase
        F_up = k.sbuf([128, 8, 2 * DFF], BF16, "F_up")
        F_dn = k.sbuf([128, NFC, D], BF16, "F_dn")
        PGW = k.sbuf([128, 8, D], BF16, "PGW")
        PPJ = k.sbuf([128, 2, D], BF16, "PPJ")
        NG = 4
        CW = DFF // NG
        for g in range(NG):
            for part in range(2):
                k.dma("pool", F_up[:, :, part * DFF + g * CW:part * DFF + (g + 1) * CW], d_f_up[:, :, part * DFF + g * CW:part * DFF + (g + 1) * CW],
                      reads=[d_f_up], writes=[(F_up, g)])
        for g in range(2):
            k.dma("pool", F_dn[:, 11 * g:11 * g + 11, :], d_f_down[:, 11 * g:11 * g + 11, :], reads=[d_f_down], writes=[(F_dn, g)])
        k.dma("pool", PGW[:], d_pgw[:], reads=[d_pgw], writes=[PGW])
        k.dma("pool", PPJ[:], d_ppj[:], reads=[d_ppj], writes=[PPJ])
        g2bc = k.sbuf([128, D], F32, "g2bc")
        g3bc = k.sbuf([128, D], F32, "g3bc")
        g4bc = k.sbuf([128, D], F32, "g4bc")
        fct = k.sbuf([128, 4 * NFC], F32, "fct")
        k.dma("sp", g2bc[:], d_g2[:], reads=[d_g2], writes=[g2bc])
        k.dma("sp", g3bc[:], d_g3[:], reads=[d_g3], writes=[g3bc])
        k.dma("sp", g4bc[:], d_g4[:], reads=[d_g4], writes=[g4bc])
        k.dma("sp", fct[:], d_fctab[:], reads=[d_fctab], writes=[fct])
        fcol = lambda c: (fct, fct[:, c:c + 1])
        xq = [k.sbuf([128, D], F32, f"xq{i}") for i in range(2)]
        hb3 = k.sbuf([128, D], BF16, "hb3")
        hT3s = [k.sbuf([128, 8, 128], BF16, f"hT3a{i}") for i in range(2)]
        ss3 = k.sbuf([128, 1], F32, "ss3")
        rs3 = k.sbuf([128, 1], F32, "rs3")
        gT = k.sbuf([128, NFC, 128], BF16, "gT")
        cf = k.sbuf([128, NFC, 2], F32, "cf")
        GS = 4
        EXW = NSB * (ST + 2)
        ex4 = [k.sbuf([128, GS, EXW], F32, f"ex4_{i}") for i in range(2)]
        cc4 = [k.sbuf([128, GS, 128], F32, f"cc4_{i}") for i in range(2)]
        t14 = [k.sbuf([128, GS, 128], F32, "t14_0")] * 2
        up4 = [k.sbuf([128, GS, 128], F32, f"up4_{i}") for i in range(2)]
        sg3 = [k.sbuf([128, 512], F32, "sg30")] * 2
        ppt = k.sbuf([128, PLE], F32, "ppt")
        ppb = k.sbuf([128, PLE], BF16, "ppb")
        peT = k.sbuf([128, 2, 128], BF16, "peT")
        utm = sg3[0]
        k.memset("pool", (cf, cf[:]), 0.0)
        cfs = k.sbuf([128, NFC, 2 * NSB], F32, "cfs")
        GC = 1.5957691216057308
        BLK6 = ((0, 512), (512, 512), (1024, 512), (1536, 512), (2048, 512), (2560, 256))
        groups = [list(range(g0, min(g0 + GS, NFC))) for g0 in range(0, NFC, GS)]
        gbank = [pF[0], pF[1]]
        ubank = [pM[0], pM[1]]

        def fup_w(col):
            g = col // CW
            g_hi = (col + 127) // CW
            return g, g_hi

        def stage_A(ti, gi, hT3):
            smp = ti == NTP
            p = gi % 2
            chunks = groups[gi]
            n = len(chunks)
            c0 = chunks[0]
            for part, bank in ((0, gbank[p]), (1, ubank[p])):
                for ci, c in enumerate(chunks):
                    g, g_hi = fup_w(c * 128)
                    for kc in range(8):
                        k.mm((bank, bank[:, ci * 128:(ci + 1) * 128]), (F_up, F_up[:, kc, part * DFF + c * 128:part * DFF + (c + 1) * 128], g),
                             (hT3, hT3[:, kc, :]), start=(kc == 0), stop=(kc == 7))
                        if g_hi != g and g_hi in F_up.subs and F_up.subs[g_hi].w is not None:
                            k.streams["pe"][-1].deps.add(F_up.subs[g_hi].w)
            ex = ex4[p]
            if not smp:
                k.copy("pool", (ex, ex[:, 0:n, 0:2]), (cf, cf[:, c0:c0 + n, :]))
                k.copy("act", (ex, ex[:, 0:n, 2:130]), (gbank[p], gbank[p][:, 0:n * 128].rearrange("p (c t) -> p c t", c=n)))
                k.copy("pool", (cf, cf[:, c0:c0 + n, :]), (ex, ex[:, 0:n, 128:130]))
            else:
                exs = ex[:, 0:n, :].rearrange("p c (b t) -> p c b t", t=ST + 2)
                k.copy("pool", (ex, exs[:, :, :, 0:2]), (cfs, cfs[:, c0:c0 + n, :].rearrange("p c (b j) -> p c b j", j=2)))
                k.copy("act", (ex, exs[:, :, :, 2:ST + 2]), (gbank[p], gbank[p][:, 0:n * 128].rearrange("p (c b t) -> p c b t", c=n, b=NSB)))
            k.copy("act", (up4[p], up4[p][:, 0:n, :]), (ubank[p], ubank[p][:, 0:n * 128].rearrange("p (c t) -> p c t", c=n)))
            for ci, c in enumerate(chunks):
                if not smp:
                    tap = lambda j: ex[:, ci, j:j + 128]
                    ccv = cc4[p][:, ci, :]
                else:
                    e3 = ex[:, ci, :].rearrange("p (b t) -> p b t", t=ST + 2)
                    tap = lambda j, e3=e3: e3[:, :, j:j + ST]
                    ccv = cc4[p][:, ci, :].rearrange("p (b t) -> p b t", t=ST)
                cb = cc4[p]
                k.ts("dve", (cb, ccv), (ex, tap(2)), fcol(2 * NFC + c), fcol(3 * NFC + c), op0=ALU.mult, op1=ALU.add)
                k.stt("dve", (cb, ccv), (ex, tap(1)), fcol(1 * NFC + c), (cb, ccv), ALU.mult, ALU.add)
                k.stt("dve", (cb, ccv), (ex, tap(0)), fcol(0 * NFC + c), (cb, ccv), ALU.mult, ALU.add)

        def stage_B(ti, gi):
            p = gi % 2
            chunks = groups[gi]
            n = len(chunks)
            c0 = chunks[0]
            cb = (cc4[p], cc4[p][:, 0:n, :])
            ta = (t14[p], t14[p][:, 0:n, :])
            k.tt("pool", ta, cb, cb, ALU.mult)
            k.ts("dve", ta, ta, 0.044715, 1.0, op0=ALU.mult, op1=ALU.add)
            k.tt("pool", ta, ta, cb, ALU.mult)
            k.act(ta, ta, AF.Sigmoid, scale=GC)
            k.tt("pool", ta, ta, cb, ALU.mult)
            k.tt("dve", (gT, gT[:, c0:c0 + n, :], ("g", gi)), ta, (up4[p], up4[p][:, 0:n, :]), ALU.mult)

        def load_norm2(ti):
            xb = xq[ti % 2]
            k.dma("sp", xb[:], x1s[ti * 128:(ti + 1) * 128, :], reads=[(x1s, ti)], writes=[xb])
            if ti == NTP:
                for b6, (c0, n) in enumerate(BLK6):
                    k.dma("sp", utm[0:2 * NSB, 0:n], st_fconv[:, c0:c0 + n], reads=[st_fconv], writes=[utm])
                    nch = n // 128
                    for ci in range(nch):
                        k.tr((pR[b6 % 2], pR[b6 % 2][:, ci * 32:(ci + 1) * 32]), (utm, utm[0:2 * NSB, ci * 128:(ci + 1) * 128]),
                             (identf, identf[0:2 * NSB, 0:2 * NSB]))
                    k.copy("act", (cfs, cfs[:, 4 * b6:4 * b6 + nch, :]), (pR[b6 % 2], pR[b6 % 2][:, 0:nch * 32].rearrange("p (c x) -> p c x", c=nch)))
            norm_generic(xb, g2bc, hb3, hT3s[ti % 2], ss3, rs3)

        hb3b = hb3

        def groups_gen(ti):
            hT3 = hT3s[ti % 2]
            for gi in range(len(groups) + 1):
                if gi < len(groups):
                    stage_A(ti, gi, hT3)
                    yield
                if gi >= 1:
                    stage_B(ti, gi - 1)
                    yield

        def tail_gen(ti):
            smp = ti == NTP
            xb = xq[ti % 2]
            hT3 = hT3s[ti % 2]
            for blk in range(2):
                for c in range(NFC):
                    k.mm((pR[blk], pR[blk][:, :]), (gT, gT[:, c, :], ("g", c // GS)), (F_dn, F_dn[:, c, blk * 512:(blk + 1) * 512], c // 11),
                         start=(c == 0), stop=(c == NFC - 1))
                k.tt("dve", (xb, xb[:, blk * 512:(blk + 1) * 512]), (xb, xb[:, blk * 512:(blk + 1) * 512]), (pR[blk], pR[blk][:, :]), ALU.add)
                yield
            if ti == NTP - 1 or smp:
                for b6, (c0, n) in enumerate(BLK6):
                    for kc in range(8):
                        k.mm((pM[2], pM[2][:, 0:n]), (hT3, hT3[:, kc, :]), (F_up, F_up[:, kc, c0:c0 + n]),
                             start=(kc == 0), stop=(kc == 7))
                    if not smp:
                        k.copy("act", (utm, utm[96:128, 0:n]), (pM[2], pM[2][96:128, 0:n]))
                        k.dma("sp", o_pfconv[:, c0:c0 + n], utm[126:128, 0:n], reads=[utm], writes=[o_pfconv])
                    else:
                        k.copy("act", (utm, utm[:, 0:n]), (pM[2], pM[2][:, 0:n]))
                        for b in range(NSB):
                            k.dma("sp", o_sfconv[2 * b:2 * b + 2, c0:c0 + n], utm[ST * b + ST - 2:ST * b + ST, 0:n], reads=[utm], writes=[o_sfconv])
                    yield
            norm_generic(xb, g3bc, hb3b, hT3, ss3, rs3)
            yield
            pd, pap = (pp, pp[ti * 128:(ti + 1) * 128, :]) if not smp else (psm, psm[:, :])
            k.dma("sp", ppt[:], pap, reads=[pd], writes=[ppt])
            k.copy("act", (ppb, ppb[:]), (ppt, ppt[:]))
            for kc in range(2):
                k.tr((pT, pT[:, kc * 128:(kc + 1) * 128]), (ppb, ppb[:, kc * 128:(kc + 1) * 128]), (identb, identb[:]))
            k.copy("act", (peT, peT[:].rearrange("p a b -> p (a b)")), (pT, pT[:, 0:256]))
            yield
            for blk in range(2):
                for kc in range(8):
                    k.mm((pR[blk], pR[blk][:, :]), (hT3, hT3[:, kc, :]), (PGW, PGW[:, kc, blk * 512:(blk + 1) * 512]),
                         start=(kc == 0), stop=(kc == 7))
                yield
                k.act((sg3[blk], sg3[blk][:]), (pR[blk], pR[blk][:, :]), AF.Sigmoid)
                for kc in range(2):
                    k.mm((pM[2], pM[2][:, :]), (peT, peT[:, kc, :]), (PPJ, PPJ[:, kc, blk * 512:(blk + 1) * 512]),
                         start=(kc == 0), stop=(kc == 1))
                k.tt("dve", (sg3[blk], sg3[blk][:]), (sg3[blk], sg3[blk][:]), (pM[2], pM[2][:, :]), ALU.mult)
                k.tt("pool", (xb, xb[:, blk * 512:(blk + 1) * 512]), (xb, xb[:, blk * 512:(blk + 1) * 512]), (sg3[blk], sg3[blk][:]), ALU.add)
                yield
            k.act((hb3b, hb3b[:]), (xb, xb[:]), AF.Square, accum=(ss3, ss3[:]))
            k.ts("dve", (rs3, rs3[:]), (ss3, ss3[:]), 1.0 / D, EPS, op0=ALU.mult, op1=ALU.add)
            k.act((rs3, rs3[:]), (rs3, rs3[:]), AF.Ln)
            k.act((rs3, rs3[:]), (rs3, rs3[:]), AF.Exp, scale=-0.5)
            yield
            k.stt("dve", (xb, xb[:]), (xb, xb[:]), (rs3, rs3[:, 0:1]), (g4bc, g4bc[:]), ALU.mult, ALU.mult)
            if not smp:
                k.dma("sp", y_p[ti * 128:(ti + 1) * 128, :], xb[:], reads=[xb], writes=[y_p])
            else:
                k.dma("sp", y_s[:, :], xb[:], reads=[xb], writes=[y_s])
            if ti in nxt2:
                yield
                load_norm2(nxt2[ti])

        def run_rr(gens):
            gens = list(gens)
            while gens:
                for g_ in list(gens):
                    try:
                        next(g_)
                    except StopIteration:
                        gens.remove(g_)

        tl = list(tiles_all)
        nxt2 = {tl[i]: tl[i + 2] for i in range(len(tl) - 2)}
        load_norm2(tl[0])
        if len(tl) > 1:
            load_norm2(tl[1])
        run_rr([groups_gen(tl[0])])
        for idx, ti in enumerate(tl):
            tg = tail_gen(ti)
            if idx + 1 < len(tl):
                gg = groups_gen(tl[idx + 1])
                next(gg)
                next(gg)
                next(tg)
                next(tg)
                run_rr([gg, tg])
            else:
                run_rr([tg])

    _nt_dbg = int(_os.environ.get("KDBG_NT", "0"))
    for ti in (tiles_all if not _nt_dbg else list(range(_nt_dbg))):
        mixer_tile(ti)
        if stage >= 2:
            rwkv_tile(ti)
            tail_rows(ti)

    if stage >= 3:
        phase_1b(k)
    if stage >= 4:
        phase_2(k)
    k.emit()
    k.stats["sbuf_hiwater"] = k.hiwater
    k.stats["arena_bytes"] = k.arena_bytes
    return nc, k


def _chunk_rows(w, nk):
    return np.ascontiguousarray(w.reshape(nk, 128, w.shape[1]).transpose(1, 0, 2))


def _pcols(v, nc_):
    return v.reshape(nc_, 128).T


_PROG = {}


def _get_prog(stage=99, dbg=False):
    key = (stage, dbg)
    if key not in _PROG:
        _PROG[key] = build_program(stage, dbg)
    return _PROG[key]


def make_in_maps(inp):
    f = lambda a: np.ascontiguousarray(np.asarray(a, dtype=np.float32))
    ptab = np.zeros((128, 128), np.float32)
    mcw = f(inp["m_conv_w"])[0]
    for j in range(4):
        ptab[:, j * 8:(j + 1) * 8] = _pcols(mcw[j], 8)
    ptab[:, 32:40] = _pcols(f(inp["m_conv_b"])[0], 8)
    ptab[:, 40:54] = _pcols(f(inp["r_mix"])[0], 14)
    ptab[:, 54:58] = _pcols(f(inp["r_w0"])[0], 4)
    ptab[:, 58:62] = _pcols(f(inp["r_a0"])[0], 4)
    ptab[:, 62:66] = _pcols(f(inp["r_kk"])[0], 4)
    ptab[:, 66:70] = _pcols(f(inp["r_ka"])[0], 4)
    ptab[:, 70:74] = _pcols(f(inp["r_rk"])[0].reshape(-1), 4)
    ptab[:, 74:78] = _pcols(f(inp["r_ln_g"])[0], 4)
    ptab[:, 78:82] = _pcols(f(inp["r_ln_b"])[0], 4)
    ptab[:, 82:86] = _pcols(f(inp["m_norm_g"])[0], 4)
    fct = np.zeros((128, 4 * NFC), np.float32)
    fcw = f(inp["f_conv_w"])[0]
    for j in range(3):
        fct[:, j * NFC:(j + 1) * NFC] = _pcols(fcw[j], NFC)
    fct[:, 3 * NFC:4 * NFC] = _pcols(f(inp["f_conv_b"])[0], NFC)
    gbias = np.stack([f(inp["m_i_bias"])[0], f(inp["m_f_bias"])[0]], axis=1)
    ra2 = np.zeros((128, RW), np.float32)
    ra2[64:128] = f(inp["r_a2"])[0]
    bc = lambda v: np.ascontiguousarray(np.broadcast_to(f(v).reshape(1, D), (128, D)))
    shared = {
        "w_in": _chunk_rows(f(inp["w_in"])[0], 8),
        "w_bm": _chunk_rows(f(inp["w_branch_m"])[0], 4),
        "w_br": _chunk_rows(f(inp["w_branch_r"])[0], 4),
        "w_out": _chunk_rows(f(inp["w_out"])[0], 8),
        "f_up": _chunk_rows(f(inp["f_up"])[0], 8),
        "f_down": _chunk_rows(f(inp["f_down"])[0], NFC),
        "ple_gate_w": _chunk_rows(f(inp["ple_gate_w"])[0], 8),
        "ple_proj": _chunk_rows(f(inp["ple_proj"])[0], 2),
        "r_w2": f(inp["r_w2"])[0], "r_a2": ra2, "r_g2": f(inp["r_g2"])[0],
        "norm1_g": bc(inp["norm1_g"]), "norm2_g": bc(inp["norm2_g"]),
        "ple_norm_g": bc(inp["ple_norm_g"]), "final_norm_g": bc(inp["final_norm_g"]),
        "ptab": ptab, "fctab": fct, "gate_bias": np.ascontiguousarray(gbias),
    }
    maps = []
    for c in range(NCORES):
        sl = slice(c * NSB, (c + 1) * NSB)
        m = dict(shared)
        m["xp"] = f(inp["x_prompt"][c])
        m["xs"] = f(inp["x_sample"][sl]).reshape(128, D)
        m["pp"] = f(inp["p_prompt"][0, c])
        m["psm"] = f(inp["p_sample"][0, sl]).reshape(128, PLE)
        m["st_mconv"] = f(inp["state_mlstm_conv"][0, sl]).reshape(NSB * 3, 2 * MW)
        m["st_mC"] = f(inp["state_mlstm_C"][0, sl])
        m["st_mn"] = f(inp["state_mlstm_n"][0, sl])
        m["st_mm"] = f(inp["state_mlstm_m"][0, sl])
        m["st_rshift"] = f(inp["state_rwkv_shift"][0, sl])
        m["st_rS"] = f(inp["state_rwkv_S"][0, sl]).reshape(NSB * RH, RN * RN)
        m["st_fconv"] = f(inp["state_ffn_conv"][0, sl]).reshape(NSB * 2, DFF)
        maps.append(m)
    return maps


def assemble(results):
    g = lambda name: [np.asarray(r[name], dtype=np.float32) for r in results]
    y_p = np.stack(g("y_p"), 0)
    y_s = np.concatenate([a.reshape(NSB, ST, D) for a in g("y_s")], 0)
    p_conv = np.stack(g("p_conv"), 0)[None]
    p_C = np.stack(g("p_C"), 0)[None]
    p_n = np.stack(g("p_n"), 0)[None]
    p_m = np.stack([a.reshape(MH) for a in g("p_m")], 0)[None]
    p_shift = np.stack([a.reshape(RCOLS) for a in g("p_shift")], 0)[None]
    p_S = np.stack([a.reshape(RH, RN, RN) for a in g("p_S")], 0)[None]
    p_fconv = np.stack(g("p_fconv"), 0)[None]
    s_conv = np.concatenate([a.reshape(NSB, 3, 2 * MW) for a in g("s_conv")], 0)[None]
    s_C = np.concatenate(g("s_C"), 0)[None]
    s_n = np.concatenate(g("s_n"), 0)[None]
    s_m = np.concatenate(g("s_m"), 0)[None]
    s_shift = np.concatenate(g("s_shift"), 0)[None]
    s_S = np.concatenate([a.reshape(NSB, RH, RN, RN) for a in g("s_S")], 0)[None]
    s_fconv = np.concatenate([a.reshape(NSB, 2, DFF) for a in g("s_fconv")], 0)[None]
    return (y_p, y_s, p_conv, p_C, p_n, p_m, p_shift, p_S, p_fconv,
            s_conv, s_C, s_n, s_m, s_shift, s_S, s_fconv)


def kernel(**inputs):
    nc, _ = _get_prog()
    maps = make_in_maps(inputs)
    res = run_bass_kernel_spmd(nc, maps, core_ids=list(range(NCORES)))
    return assemble(res.results)
```

```python
import math
from contextlib import ExitStack

import numpy as np
import concourse.bass as bass
import concourse.mybir as mybir
from concourse.bass_utils import run_bass_kernel_spmd

F32 = mybir.dt.float32
BF16 = mybir.dt.bfloat16
AF = mybir.ActivationFunctionType
ALU = mybir.AluOpType
AX = mybir.AxisListType

ENGS = ("pe", "act", "dve", "pool", "sp")
N_DMA_SEMS = 8
SAME_ENG_DIST = 2

D = 1024
SEQ = 2048
NCORES = 8
NTP = SEQ // 128
NSB = 16
ST = 8
MW = 512
MH = 4
RW = 512
RH = 8
RN = 64
RCOLS = 1792
DFF = 2816
NFC = DFF // 128
PLE = 256
N_IN = 5896
C_QK, C_V, C_O, C_I, C_F, C_R, C_G = 0, 1024, 1536, 2048, 2052, 2056, 3848
EPS = 1e-6
GN_EPS = 64e-5
KSCALE = 128 ** -0.5
WSCALE = -math.exp(-0.5)


class _Trk:
    __slots__ = ("w", "r")

    def __init__(self):
        self.w = None
        self.r = []


class Buf:
    def __init__(self, t, name):
        self.t = t
        self.name = name
        self.whole = _Trk()
        self.subs = {}

    def __getitem__(self, idx):
        return self.t[idx]

    def view(self, ap, name=None):
        b = Buf(ap, name or self.name + "_v")
        b.whole = self.whole
        b.subs = self.subs
        return b


class _Op:
    __slots__ = ("eng", "fn", "deps", "needs_inc", "is_dma", "sem", "val", "pos", "force")


class K:
    def __init__(self, nc):
        self.nc = nc
        self.es = ExitStack()
        self.streams = {e: [] for e in ENGS}
        self.dma_rr = {e: 0 for e in ENGS}
        self.dma_last = {}
        self.nbuf = 0
        self.ops = []

    def _init_arena(self):
        nbytes = (int(self.nc.sbuf_bytes_remaining) - 512) // 64 * 64
        self.arena_bytes = nbytes
        self.arena = self.es.enter_context(self.nc.sbuf_tensor("arena", [128, nbytes // 2], BF16))
        self.bot = 0
        self.top = nbytes
        self.hiwater = 0

    def _view(self, off, shape, dtype):
        n = 1
        for d in shape[1:]:
            n *= d
        esz = 4 if dtype == F32 else 2
        v = self.arena[:, off // 2:(off + n * esz) // 2]
        if dtype == F32:
            v = v.bitcast(F32)
        if len(shape) > 2:
            names = " ".join(f"d{i}" for i in range(len(shape) - 1))
            v = v.rearrange(f"p ({names}) -> p {names}", **{f"d{i}": shape[i + 1] for i in range(len(shape) - 1)})
        if shape[0] < 128:
            v = v[0:shape[0]]
        return v, n * esz

    def sbuf(self, shape, dtype, name=None, top=False):
        if not hasattr(self, "arena"):
            self._init_arena()
        self.nbuf += 1
        name = name or f"sb{self.nbuf}"
        n = 1
        for d in shape[1:]:
            n *= d
        nb = (n * (4 if dtype == F32 else 2) + 63) // 64 * 64
        if top:
            self.top -= nb
            off = self.top
        else:
            off = self.bot
            self.bot += nb
        assert self.bot <= self.top, f"SBUF arena overflow allocating {name}: bot={self.bot} top={self.top}"
        self.hiwater = max(self.hiwater, self.bot + (self.arena_bytes - self.top))
        v, _ = self._view(off, list(shape), dtype)
        return Buf(v, name)

    def pe_fence(self):
        st = self.streams["pe"]
        if not st:
            return
        last = st[-1]
        o = self.op("pe", lambda h: h.nop(), (), ())
        o.deps.add(last)
        o.force = {last}
        if getattr(self, "fence_mm", None) is not None:
            fb, fi = self.fence_mm
            self.tr((fb, fb[:, 0:128]), (fi, fi[:]), (fi, fi[:]))
            last = self.streams["pe"][-1]
            o = self.op("pe", lambda h: h.nop(), (), ())
            o.deps.add(last)
            o.force = {last}

    def barrier(self):
        lasts = [st[-1] for st in self.streams.values() if st]
        lasts += list(self.dma_last.values())
        for e in ENGS:
            o = self.op(e, lambda h: h.nop(), (), ())
            o.deps.update(x for x in lasts if x is not o)

    def psum(self, shape, dtype, name=None):
        self.nbuf += 1
        name = "ps_" + (name or f"{self.nbuf}")
        t = self.es.enter_context(self.nc.psum_tensor(name, list(shape), dtype))
        return Buf(t, name)

    def dram(self, name, shape, dtype, kind="Internal"):
        t = self.nc.dram_tensor(name, list(shape), dtype, kind=kind)
        return Buf(t.ap(), name)

    def _touch(self, op, item, is_write):
        if isinstance(item, tuple):
            buf, key = item
        else:
            buf, key = item, None
        if key is None:
            trks = [buf.whole] + list(buf.subs.values())
        else:
            if key not in buf.subs:
                buf.subs[key] = _Trk()
            trks = [buf.whole, buf.subs[key]]
        for t in trks:
            if t.w is not None:
                op.deps.add(t.w)
            if is_write:
                op.deps.update(t.r)
        return buf, key

    def _commit(self, op, buf, key, is_write):
        if key is None:
            if is_write:
                buf.whole.w = op
                buf.whole.r = []
                buf.subs.clear()
            else:
                self._add_reader(buf.whole, op)
        else:
            t = buf.subs[key]
            if is_write:
                t.w = op
                t.r = []
            else:
                self._add_reader(t, op)

    @staticmethod
    def _add_reader(t, op):
        if not op.is_dma:
            t.r = [o for o in t.r if o.is_dma or o.eng != op.eng]
        t.r.append(op)

    def op(self, eng, fn, reads=(), writes=(), dma=False):
        o = _Op()
        o.eng = eng
        o.fn = fn
        o.deps = set()
        o.needs_inc = False
        o.is_dma = dma
        o.sem = None
        o.val = None
        o.force = None
        touched = []
        for it in reads:
            touched.append(self._touch(o, it, False) + (False,))
        for it in writes:
            touched.append(self._touch(o, it, True) + (True,))
        o.deps.discard(o)
        for buf, key, w in touched:
            self._commit(o, buf, key, w)
        if dma:
            kk = (eng, self.dma_rr[eng] % N_DMA_SEMS)
            self.dma_rr[eng] += 1
            prev = self.dma_last.get(kk)
            if prev is not None:
                o.deps.add(prev)
            self.dma_last[kk] = o
            o.sem = kk
            o.needs_inc = True
        o.pos = len(self.streams[eng])
        self.streams[eng].append(o)
        self.ops.append(o)
        return o

    def dma(self, eng, out, in_, reads=(), writes=(), **kw):
        return self.op(eng, lambda e: e.dma_start(out=out, in_=in_, **kw), reads, writes, dma=True)

    def emit(self):
        nc = self.nc
        for o in self.ops:
            real = []
            for d in o.deps:
                if (not d.is_dma) and (not o.is_dma) and d.eng == o.eng and o.eng == "pe":
                    if not (o.force and d in o.force):
                        continue
                d.needs_inc = True
                real.append(d)
            o.deps = real
        for e in ENGS:
            cs = [o for o in self.streams[e] if not o.is_dma]
            if cs:
                cs[-1].needs_inc = True
        es = self.es
        esem = {e: es.enter_context(nc.semaphore(f"s_{e}")) for e in ENGS}
        dsem = {}
        for e in ENGS:
            for i in range(min(N_DMA_SEMS, self.dma_rr[e])):
                dsem[(e, i)] = es.enter_context(nc.semaphore(f"d_{e}{i}"))
        dcount = {kk: 0 for kk in dsem}
        for e in ENGS:
            c = 0
            for o in self.streams[e]:
                if o.is_dma:
                    dcount[o.sem] += 16
                    o.val = dcount[o.sem]
                    o.sem = dsem[o.sem]
                elif o.needs_inc:
                    c += 1
                    o.val = c
                    o.sem = esem[e]
        final_waits = [(s, dcount[kk]) for kk, s in dsem.items() if dcount[kk] > 0]
        for e in ENGS:
            if e == "sp":
                continue
            cs = [o for o in self.streams[e] if not o.is_dma and o.needs_inc]
            if cs:
                final_waits.append((esem[e], cs[-1].val))
        streams = self.streams
        nwaits = [0]

        def run(e, handle):
            waited = {}
            for o in streams[e]:
                need = {}
                for d in o.deps:
                    if need.get(d.sem, (None, 0))[1] < d.val:
                        need[d.sem] = (d.sem, d.val)
                for s, v in need.values():
                    if waited.get(s, 0) >= v:
                        continue
                    handle.wait_ge(s, v)
                    nwaits[0] += 1
                    waited[s] = v
                ins = o.fn(handle)
                if o.is_dma:
                    ins.then_inc(o.sem, 16)
                elif o.needs_inc:
                    ins.then_inc(o.sem, 1)
            if e == "sp":
                for s, v in final_waits:
                    handle.wait_ge(s, v)

        with nc.Block() as block:
            @block.tensor
            def _(h):
                run("pe", h)

            @block.scalar
            def _(h):
                run("act", h)

            @block.vector
            def _(h):
                run("dve", h)

            @block.gpsimd
            def _(h):
                run("pool", h)

            @block.sync
            def _(h):
                run("sp", h)
        self.stats = dict(n_ops={e: len(streams[e]) for e in ENGS}, n_waits=nwaits[0])
        self.es.close()

    @staticmethod
    def _it(x):
        return (x[0], x[2]) if len(x) > 2 else x[0]

    def mm(self, out, lhsT, rhs, start=True, stop=True):
        return self.op("pe", lambda e: e.matmul(out[1], lhsT=lhsT[1], rhs=rhs[1], start=start, stop=stop),
                       reads=[self._it(lhsT), self._it(rhs)], writes=[self._it(out)])

    def tr(self, out, in_, ident):
        return self.op("pe", lambda e: e.transpose(out[1], in_[1], ident[1]),
                       reads=[self._it(in_), self._it(ident)], writes=[self._it(out)])

    def act(self, out, in_, func, bias=None, scale=None, accum=None, eng="act"):
        reads = [self._it(in_)]
        kw = {}
        if bias is not None:
            if isinstance(bias, tuple):
                reads.append(self._it(bias))
                kw["bias"] = bias[1]
            else:
                kw["bias"] = bias
        if scale is not None:
            if isinstance(scale, tuple):
                reads.append(self._it(scale))
                kw["scale"] = scale[1]
            else:
                kw["scale"] = scale
        writes = [self._it(out)]
        if accum is not None:
            writes.append(self._it(accum))
            kw["accum_out"] = accum[1]
        return self.op(eng, lambda e: e.activation(out=out[1], in_=in_[1], func=func, **kw), reads, writes)

    def tt(self, eng, out, in0, in1, op):
        return self.op(eng, lambda e: e.tensor_tensor(out=out[1], in0=in0[1], in1=in1[1], op=op),
                       reads=[self._it(in0), self._it(in1)], writes=[self._it(out)])

    def ts(self, eng, out, in0, s1, s2=None, op0=ALU.mult, op1=None, accum=None):
        reads = [self._it(in0)]
        a1 = s1
        a2 = s2
        if isinstance(s1, tuple):
            reads.append(self._it(s1))
            a1 = s1[1]
        if isinstance(s2, tuple):
            reads.append(self._it(s2))
            a2 = s2[1]
        kw = {}
        if op1 is not None:
            kw["op1"] = op1
        writes = [self._it(out)]
        if accum is not None:
            writes.append(self._it(accum))
            kw["accum_out"] = accum[1]
        return self.op(eng, lambda e: e.tensor_scalar(out=out[1], in0=in0[1], scalar1=a1, scalar2=a2, op0=op0, **kw),
                       reads, writes)

    def stt(self, eng, out, in0, scalar, in1, op0, op1):
        reads = [self._it(in0), self._it(in1)]
        a = scalar
        if isinstance(scalar, tuple):
            reads.append(self._it(scalar))
            a = scalar[1]
        return self.op(eng, lambda e: e.scalar_tensor_tensor(out=out[1], in0=in0[1], scalar=a, in1=in1[1], op0=op0, op1=op1),
                       reads, [self._it(out)])

    def copy(self, eng, out, in_):
        if eng == "act":
            return self.op(eng, lambda e: e.activation(out=out[1], in_=in_[1], func=AF.Copy),
                           reads=[self._it(in_)], writes=[self._it(out)])
        return self.op(eng, lambda e: e.tensor_copy(out=out[1], in_=in_[1]),
                       reads=[self._it(in_)], writes=[self._it(out)])

    def red(self, eng, out, in_, op, axis=AX.X):
        return self.op(eng, lambda e: e.tensor_reduce(out=out[1], in_=in_[1], axis=axis, op=op),
                       reads=[self._it(in_)], writes=[self._it(out)])

    def memset(self, eng, out, val):
        return self.op(eng, lambda e: e.memset(out[1], val), reads=[], writes=[self._it(out)])

    def scan(self, eng, out, d0, d1, init, op0, op1):
        return self.op(eng, lambda e: e.tensor_tensor_scan(out=out[1], data0=d0[1], data1=d1[1], initial=init, op0=op0, op1=op1),
                       reads=[self._it(d0), self._it(d1)], writes=[self._it(out)])


def build_program(stage=99, dbg=False):
    import os as _os
    nc = bass.Bass("TRN2", target_bir_lowering=False)
    k = K(nc)
    NT = NTP + 1

    def din(name, shape):
        return k.dram(name, shape, F32, "ExternalInput")

    def dout(name, shape):
        return k.dram(name, shape, F32, "ExternalOutput")

    xp = din("xp", [SEQ, D]); xs = din("xs", [128, D])
    pp = din("pp", [SEQ, PLE]); psm = din("psm", [128, PLE])
    st_mconv = din("st_mconv", [NSB * 3, 2 * MW])
    st_mC = din("st_mC", [NSB, MH, 128, 128])
    st_mn = din("st_mn", [NSB, MH, 128])
    st_mm = din("st_mm", [NSB, MH])
    st_rshift = din("st_rshift", [NSB, RCOLS])
    st_rS = din("st_rS", [NSB * RH, RN * RN])
    st_fconv = din("st_fconv", [NSB * 2, DFF])
    d_w_in = din("w_in", [128, 8, N_IN])
    d_w_bm = din("w_bm", [128, 4, D]); d_w_br = din("w_br", [128, 4, D])
    d_w_out = din("w_out", [128, 8, D])
    d_f_up = din("f_up", [128, 8, 2 * DFF]); d_f_down = din("f_down", [128, NFC, D])
    d_pgw = din("ple_gate_w", [128, 8, D]); d_ppj = din("ple_proj", [128, 2, D])
    d_rw2 = din("r_w2", [64, RW]); d_ra2 = din("r_a2", [128, RW]); d_rg2 = din("r_g2", [128, RW])
    d_g1 = din("norm1_g", [128, D]); d_g2 = din("norm2_g", [128, D])
    d_g3 = din("ple_norm_g", [128, D]); d_g4 = din("final_norm_g", [128, D])
    d_ptab = din("ptab", [128, 128])
    d_fctab = din("fctab", [128, 4 * NFC])
    d_gb = din("gate_bias", [4, 2])
    y_p = dout("y_p", [SEQ, D]); y_s = dout("y_s", [128, D])
    o_pconv = dout("p_conv", [3, 2 * MW]); o_pC = dout("p_C", [MH, 128, 128]); o_pn = dout("p_n", [MH, 128])
    o_pm = dout("p_m", [1, MH]); o_pshift = dout("p_shift", [1, RCOLS]); o_pS = dout("p_S", [RH * RN, RN])
    o_pfconv = dout("p_fconv", [2, DFF])
    o_sconv = dout("s_conv", [NSB * 3, 2 * MW]); o_sC = dout("s_C", [NSB, MH, 128, 128]); o_sn = dout("s_n", [NSB, MH, 128])
    o_sm = dout("s_m", [NSB, MH]); o_sshift = dout("s_shift", [NSB, RCOLS]); o_sS = dout("s_S", [NSB * RH, RN * RN])
    o_sfconv = dout("s_fconv", [NSB * 2, DFF])
    x1s = k.dram("x1_scratch", [NT * 128, D], F32)
    dbgs = {}

    def dbg_out(name, src_buf, src_ap, shape):
        if not dbg:
            return
        t = dout("dbg_" + name, shape)
        dbgs[name] = t
        k.dma("sp", t[:], src_ap, reads=[src_buf], writes=[t])

    identf = k.sbuf([128, 128], F32, "identf")
    identb = k.sbuf([128, 128], BF16, "identb")
    mark_phase = k.bot
    mU_in = [k.sbuf([128, 128], F32, f"mUin{i}") for i in range(2)]
    mU_st = [k.sbuf([128, 128], F32, f"mUst{i}") for i in range(2)]
    mL_st = [k.sbuf([128, 128], F32, f"mLst{i}") for i in range(2)]
    resets = [k.sbuf([128, 512], F32, f"resets{i}") for i in range(2)]
    ones4 = k.sbuf([4, 128], F32, "ones4")

    def aff(out_buf, out_ap, pattern, cm, base, op=ALU.is_ge):
        k.op("pool", lambda e: e.affine_select(out=out_ap, in_=out_ap, pattern=pattern, compare_op=op,
                                               fill=0.0, base=base, channel_multiplier=cm),
             reads=[out_buf], writes=[out_buf])

    k.memset("pool", (identf, identf[:]), 1.0)
    aff(identf, identf[:], [[-1, 128]], 1, 0)
    aff(identf, identf[:], [[1, 128]], -1, 0)
    k.copy("pool", (identb, identb[:]), (identf, identf[:]))
    for i in range(2):
        k.memset("pool", (mU_in[i], mU_in[i][:]), 1.0)
        aff(mU_in[i], mU_in[i][:], [[1, 128]], -1, 0)
        k.memset("pool", (mU_st[i], mU_st[i][:]), 1.0)
        aff(mU_st[i], mU_st[i][:], [[1, 128]], -1, -1)
        k.memset("pool", (mL_st[i], mL_st[i][:]), 1.0)
        aff(mL_st[i], mL_st[i][:], [[-1, 128]], 1, -1)
        k.memset("pool", (resets[i], resets[i][:]), 1.0)
    v3 = lambda b: b[:].rearrange("p (a c) -> p a c", c=ST)
    aff(mU_in[1], v3(mU_in[1]), [[-ST, 16], [0, ST]], 1, 0)
    aff(mU_st[1], v3(mU_st[1]), [[-ST, 16], [0, ST]], 1, 0)
    aff(mL_st[1], v3(mL_st[1]), [[ST, 16], [0, ST]], -1, ST - 1)
    k.memset("pool", (resets[0], resets[0][:].rearrange("p (a c) -> p a c", c=128)[:, :, 0:1]), 0.0)
    k.memset("pool", (resets[1], resets[1][:].rearrange("p (a c) -> p a c", c=ST)[:, :, 0:1]), 0.0)
    k.memset("pool", (ones4, ones4[:]), 1.0)
    mask2 = [k.sbuf([128, 256], F32, f"mask2_{i}") for i in range(2)]
    for i in range(2):
        k.copy("pool", (mask2[i], mask2[i][:, 0:128]), (mU_st[i], mU_st[i][:]))
        k.copy("pool", (mask2[i], mask2[i][:, 128:256]), (mU_in[i], mU_in[i][:]))
    I2 = k.sbuf([128, 64], F32, "I2")
    k.tt("pool", (I2, I2[:]), (identf, identf[:, 0:64]), (identf, identf[:, 64:128]), ALU.add)
    bones = k.sbuf([128, 128], F32, "bones")
    k.memset("pool", (bones, bones[:]), 0.0)
    k.memset("pool", (bones, bones[0:64, 0:64]), 1.0)
    k.memset("pool", (bones, bones[64:128, 64:128]), 1.0)

    ptab = k.sbuf([128, 128], F32, "ptab")
    k.dma("sp", ptab[:], d_ptab[:], reads=[d_ptab], writes=[ptab])
    PT_MCW, PT_MCB, PT_RMIX, PT_RW0, PT_RA0, PT_RKK, PT_RKA, PT_RRK, PT_RLNG, PT_RLNB, PT_MNG = 0, 32, 40, 54, 58, 62, 66, 70, 74, 78, 82
    pcol = lambda c: (ptab, ptab[:, c:c + 1])
    gb = k.sbuf([4, 2], F32, "gb")
    k.dma("sp", gb[:], d_gb[:], reads=[d_gb], writes=[gb])
    nbf = k.sbuf([4, 1], F32, "nbf")
    k.ts("dve", (nbf, nbf[:]), (gb, gb[:, 1:2]), -1.0, None, op0=ALU.mult)
    g1bc = k.sbuf([128, D], F32, "g1bc")
    k.dma("sp", g1bc[:], d_g1[:], reads=[d_g1], writes=[g1bc])

    NA = C_G
    hmT_all = k.sbuf([128, NT, 4, 128], BF16, "hmT_all")
    yrgT_all = k.sbuf([128, NT, 4, 128], BF16, "yrgT_all")
    mark_1a = k.bot
    W_in = k.sbuf([128, 8, NA], BF16, "W_in")
    Wl_w2 = k.sbuf([64, RW], BF16, "Wl_w2")
    Wl_a2 = k.sbuf([128, RW], BF16, "Wl_a2")
    Wl_g2 = k.sbuf([128, RW], BF16, "Wl_g2")
    GRP = {"g0": (0, 1024), "g1": (1024, 2056), "g2": (2056, 3848)}
    for g in ("g0", "g1", "g2"):
        a, b = GRP[g]
        for kh in range(2):
            k.dma("pool", W_in[:, 4 * kh:4 * kh + 4, a:b], d_w_in[:, 4 * kh:4 * kh + 4, a:b], reads=[d_w_in], writes=[(W_in, g)])
        if g == "g1":
            k.dma("pool", Wl_w2[:], d_rw2[:], reads=[d_rw2], writes=[Wl_w2])
            k.dma("pool", Wl_a2[:], d_ra2[:], reads=[d_ra2], writes=[Wl_a2])
            k.dma("pool", Wl_g2[:], d_rg2[:], reads=[d_rg2], writes=[Wl_g2])

    def wgrp(col):
        for g, (a, b) in GRP.items():
            if a <= col < b:
                return g

    pF = [k.psum([128, 512], F32, f"pF{i}") for i in range(2)]
    pR = [k.psum([128, 512], F32, f"pR{i}") for i in range(2)]
    pT = k.psum([128, 1024], BF16, "pT")
    pM = [k.psum([128, 512], F32, f"pM{i}") for i in range(3)]

    xt = [k.sbuf([128, D], F32, "xt0")] * 2
    hb = k.sbuf([128, D], BF16, "hb")
    hT = k.sbuf([128, 8, 128], BF16, "hT")
    ss = k.sbuf([128, 1], F32, "ss")
    rs = k.sbuf([128, 1], F32, "rs")
    ext_q = k.sbuf([128, 8, 131], F32, "ext_q")
    cq = k.sbuf([128, 8, 3], F32, "cq")
    _eqf = ext_q[:].rearrange("p a b -> p (a b)")
    cv = k.sbuf([128, 8, 128], F32, "cv")
    qkT = k.sbuf([128, 8, 128], BF16, "qkT")
    soT = k.sbuf([128, 4, 128], F32, "soT")
    vaug = k.sbuf([128, 4, 130], BF16, "vaug")
    Cst = k.sbuf([128, 4, 129], F32, "Cst")
    Cb = k.sbuf([128, 4, 130], BF16, "Cb")
    gsm = [k.sbuf([4, 128], F32, f"gsm{i}") for i in range(8)]
    gpk = k.sbuf([4, 3, 128], F32, "gpk")
    mst = k.sbuf([4, 16], F32, "mst")
    mnew = k.sbuf([4, 16], F32, "mnew")
    gt = [k.sbuf([4, 16], F32, f"gt{i}") for i in range(4)]
    s0d = k.sbuf([4, 4, 16], F32, "s0d")
    tokS = k.sbuf([128, 12], F32, "tokS")
    s0bc = k.sbuf([128, 64], F32, "s0bc")
    _pk = _eqf[:, 512:1024].bitcast(BF16).rearrange("p (a b c) -> p a b c", a=2, b=4)
    PTm = ext_q.view(_pk[:, 0, :, :], "PTm")
    ktm = ext_q.view(_pk[:, 1, :, :], "ktm")
    dn = k.sbuf([128, 4], F32, "dn")
    hm = ext_q.view(_eqf[:, 0:512].rearrange("p (a b) -> p a b", a=4), "hm")
    hn = hm
    bst = k.sbuf([128, 4, 6], F32, "bst")
    bag = k.sbuf([128, 4, 2], F32, "bag")
    zq_tm = cv.view(cv[:].rearrange("p a b -> p (a b)"), "zq_tm")

    ext_r = k.sbuf([128, 14, 129], F32, "ext_r")
    cr = k.sbuf([128, 14, 1], F32, "cr")
    _erf = ext_r[:].rearrange("p a b -> p (a b)")
    xm = k.sbuf([128, 14, 128], F32, "xm")
    thad = k.sbuf([128, 128], BF16, "thad")
    sgd = k.sbuf([128, 128], BF16, "sgd")
    bst8 = k.sbuf([128, 8, 6], F32, "bst8")
    bag8 = k.sbuf([128, 8, 2], F32, "bag8")
    mark_rw = k.bot
    rt = [k.sbuf([128, 4, 128], F32, f"rt{i}") for i in range(7)]
    rt.append(cv.view(cv[:, 0:4, :], "rt7"))
    rt.append(cv.view(cv[:, 4:8, :], "rt8"))
    gTs = ext_r.view(_erf[:, 0:512].rearrange("p (a b) -> p a b", a=4), "gTs")
    bonT = ext_r.view(_erf[:, 512:1024].rearrange("p (a b) -> p a b", a=4), "bonT")
    ART = k.sbuf([128, 4, 2, 128], BF16, "ART")
    BTb = k.sbuf([128, 4, 128], BF16, "BTb")
    KTb = k.sbuf([128, 4, 128], BF16, "KTb")
    VTb = k.sbuf([128, 4, 128], BF16, "VTb")
    AB_tm = k.sbuf([128, 2, 512], BF16, "AB_tm")
    KV_tm = k.sbuf([128, 2, 512], BF16, "KV_tm")
    GBm = k.sbuf([128, 4, 256], BF16, "GBm")
    GKm = k.sbuf([128, 4, 256], BF16, "GKm")
    Nn = k.sbuf([128, 4, 128], BF16, "Nn")
    GBm_b = k.sbuf([128, 4, 256], BF16, "GBm_b")
    GKm_b = k.sbuf([128, 4, 256], BF16, "GKm_b")
    Nn_b = k.sbuf([128, 4, 128], BF16, "Nn_b")
    PP_b = [k.sbuf([128, 4, 256], BF16, f"PPb{i}") for i in range(2)]
    XX_b = [k.sbuf([128, 4, 128], BF16, f"XXb{i}") for i in range(2)]
    PP = [k.sbuf([128, 4, 256], BF16, f"PP{i}") for i in range(2)]
    XX = [k.sbuf([128, 4, 128], BF16, f"XX{i}") for i in range(2)]
    QT = k.sbuf([128, 2, 128], BF16, "QT")
    IE = k.sbuf([128, 4, 64], F32, "IE")
    STf = k.sbuf([128, 4, 64], F32, "STf")
    STb = k.sbuf([128, 4, 64], BF16, "STb")
    yn = ext_r.view(_erf[:, 1024:1536].rearrange("p (a b) -> p a b", a=8), "yn")
    k.memset("pool", (STf, STf[:]), 0.0)
    k.memset("pool", (STb, STb[:]), 0.0)
    k.memset("pool", (cr, cr[:]), 0.0)
    k.memset("pool", (vaug, vaug[:]), 1.0)
    k.memset("pool", (Cst, Cst[:]), 0.0)
    k.memset("pool", (Cb, Cb[:]), 0.0)
    k.memset("pool", (mst, mst[:]), 0.0)
    k.memset("pool", (cq, cq[:]), 0.0)
    LNK = math.log(KSCALE)

    def x_rows(ti):
        if ti < NTP:
            return xp, xp[ti * 128:(ti + 1) * 128, :]
        return xs, xs[:, :]

    def norm_to_hT(xbuf, gbc):
        k.act((hb, hb[:]), (xbuf, xbuf[:]), AF.Square, accum=(ss, ss[:]))
        k.ts("dve", (rs, rs[:]), (ss, ss[:]), 1.0 / D, EPS, op0=ALU.mult, op1=ALU.add)
        k.act((rs, rs[:]), (rs, rs[:]), AF.Ln)
        k.act((rs, rs[:]), (rs, rs[:]), AF.Exp, scale=-0.5)
        k.stt("dve", (hb, hb[:]), (xbuf, xbuf[:]), (rs, rs[:, 0:1]), (gbc, gbc[:]), ALU.mult, ALU.mult)
        for kc in range(8):
            k.tr((pT, pT[:, kc * 128:(kc + 1) * 128]), (hb, hb[:, kc * 128:(kc + 1) * 128]), (identb, identb[:]))
        k.copy("act", (hT, hT[:].rearrange("p a b -> p (a b)")), (pT, pT[:, :]))

    def proj_fm(ps, ps_ap, col, M=128):
        g = wgrp(col)
        for kc in range(8):
            k.mm((ps, ps_ap), (W_in, W_in[:, kc, col:col + M], g), (hT, hT[:, kc, :]), start=(kc == 0), stop=(kc == 7))

    def proj_tm(ps, ps_ap, col, N):
        g = wgrp(col)
        for kc in range(8):
            k.mm((ps, ps_ap), (hT, hT[:, kc, :]), (W_in, W_in[:, kc, col:col + N], g), start=(kc == 0), stop=(kc == 7))

    def mixer_tile(ti):
        smp = ti == NTP
        mi = 1 if smp else 0
        NB = NSB if smp else 1
        LB = ST if smp else 128
        xb = xt[ti % 2]
        xd, xap = x_rows(ti)
        if smp:
            k.barrier()
            k.bot = mark_rw
            Cs = k.sbuf([128, NSB, 129], F32, "Cs")
            Csb = k.sbuf([128, NSB, 130], BF16, "Csb")
            qTm = k.sbuf([128, NSB, 128], BF16, "qTm")
            ktmb = k.sbuf([128, NSB, 128], BF16, "ktmb")
            blkF = k.sbuf([128, NSB, 128], BF16, "blkF")
            rowm = k.sbuf([128, NSB], F32, "rowm")
            k.memset("pool", (blkF, blkF[:]), 1.0)
            aff(blkF, blkF[:], [[-ST, NSB], [1, 128]], 0, 0)
            aff(blkF, blkF[:], [[ST, NSB], [-1, 128]], 0, ST - 1)
            k.memset("pool", (rowm, rowm[:]), 1.0)
            aff(rowm, rowm[:], [[-ST, NSB]], 1, 0)
            aff(rowm, rowm[:], [[ST, NSB]], -1, ST - 1)
            smc = cv.view(cv[:].rearrange("p a b -> p (a b)")[0:NSB * 3, :], "smc")
            ext_s = xm.view(xm[:].rearrange("p a b -> p (a b)")[:, 0:8 * NSB * 11].rearrange("p (c b t) -> p c b t", c=8, b=NSB), "ext_s")
            k.dma("sp", smc[:], st_mconv[:, :], reads=[st_mconv], writes=[smc])
            for c in range(8):
                k.tr((pM[0], pM[0][:, c * 48:(c + 1) * 48]), (smc, smc[:, c * 128:(c + 1) * 128]), (identf, identf[0:48, 0:48]))
            k.copy("act", (ext_s, ext_s[:, :, :, 0:3]), (pM[0], pM[0][:, 0:384].rearrange("p (c b j) -> p c b j", c=8, b=NSB)))
            k.dma("sp", mst[:, 0:NSB], st_mm[:, :].rearrange("b h -> h b"), reads=[st_mm], writes=[mst], allow_slow_non_contiguous=True)
        k.dma("sp", xb[:], xap, reads=[xd], writes=[xb])
        norm_to_hT(xb, g1bc)

        if smp:
            for g in range(2):
                for c in range(4):
                    proj_fm(pF[g], pF[g][:, c * 128:(c + 1) * 128], C_QK + (4 * g + c) * 128)
                k.copy("act", (ext_s, ext_s[:, 4 * g:4 * g + 4, :, 3:11]), (pF[g], pF[g][:].rearrange("p (c b t) -> p c b t", c=4, b=NSB)))
            for c in range(8):
                cvv = cv[:, c, :].rearrange("p (b t) -> p b t", t=ST)
                k.ts("dve", (cv, cvv), (ext_s, ext_s[:, c, :, 3:11]), pcol(PT_MCW + 3 * 8 + c), pcol(PT_MCB + c),
                     op0=ALU.mult, op1=ALU.add)
                for j in range(3):
                    k.stt("dve", (cv, cvv), (ext_s, ext_s[:, c, :, j:j + ST]), pcol(PT_MCW + j * 8 + c), (cv, cvv),
                          ALU.mult, ALU.add)
        if not smp:
            k.copy("pool", (ext_q, ext_q[:, :, 0:3]), (cq, cq[:]))
            for g in range(2):
                for c in range(4):
                    proj_fm(pF[g], pF[g][:, c * 128:(c + 1) * 128], C_QK + (4 * g + c) * 128)
                k.copy("act", (ext_q, ext_q[:, 4 * g:4 * g + 4, 3:131]), (pF[g], pF[g][:].rearrange("p (c t) -> p c t", c=4)))
            k.copy("pool", (cq, cq[:]), (ext_q, ext_q[:, :, 128:131]))
            for c in range(8):
                k.ts("dve", (cv, cv[:, c, :]), (ext_q, ext_q[:, c, 3:131]), pcol(PT_MCW + 3 * 8 + c), pcol(PT_MCB + c),
                     op0=ALU.mult, op1=ALU.add)
                for j in range(3):
                    k.stt("dve", (cv, cv[:, c, :]), (ext_q, ext_q[:, c, j:j + 128]), pcol(PT_MCW + j * 8 + c), (cv, cv[:, c, :]),
                          ALU.mult, ALU.add)
        k.act((qkT, qkT[:].rearrange("p a b -> p (a b)")), (cv, cv[:].rearrange("p a b -> p (a b)")), AF.Silu)

        proj_tm(pR[0], pR[0][:, :], C_V, 512)
        k.copy("act", (vaug, vaug[:, :, 0:128]), (pR[0], pR[0][:].rearrange("p (h c) -> p h c", h=4)))
        for c in range(4):
            proj_fm(pF[0], pF[0][:, c * 128:(c + 1) * 128], C_O + c * 128)
        k.act((soT, soT[:].rearrange("p a b -> p (a b)")), (pF[0], pF[0][:, :]), AF.Sigmoid)
        if not smp and stage >= 2:
            rwkv_front_proj(ti)
        proj_fm(pM[0], pM[0][0:4, 0:128], C_I, M=4)
        proj_fm(pM[0], pM[0][0:4, 128:256], C_F, M=4)
        liT, nlf, ncum, gT_, t0, t1 = gsm[0], gsm[1], gsm[2], gsm[3], gsm[4], gsm[5]
        k.ts("dve", (liT, liT[:]), (pM[0], pM[0][0:4, 0:128]), (gb, gb[:, 0:1]), None, op0=ALU.add)
        k.act((t0, t0[:]), (pM[0], pM[0][0:4, 128:256]), AF.Exp, bias=(nbf, nbf[:, 0:1]), scale=-1.0)
        k.act((nlf, nlf[:]), (t0, t0[:]), AF.Ln, bias=1.0)
        k.scan("dve", (ncum, ncum[:]), (resets[mi], resets[mi][0:4, 0:128]), (nlf, nlf[:]), 0.0, ALU.mult, ALU.add)
        k.tt("dve", (gT_, gT_[:]), (liT, liT[:]), (ncum, ncum[:]), ALU.add)
        b3 = lambda buf: buf[:].rearrange("p (b l) -> p b l", l=LB)
        mcb_ = mst[:, 0:NB].unsqueeze(2).to_broadcast([4, NB, LB])
        nlast = ncum[:].rearrange("p (b l) -> p b l", l=LB)[:, :, LB - 1:LB]
        k.stt("dve", (t0, b3(t0)), (gT_, b3(gT_)), LNK, (mst, mcb_), ALU.add, ALU.subtract)
        k.act((gpk, gpk[:, 0, :]), (t0, t0[:]), AF.Exp)
        k.tt("dve", (t1, b3(t1)), (ncum, b3(ncum)), (mst, mcb_), ALU.subtract)
        k.act((gpk, gpk[:, 1, :]), (t1, t1[:]), AF.Exp)
        k.tt("dve", (t1, b3(t1)), (gT_, b3(gT_)), (ncum, nlast.to_broadcast([4, NB, LB])), ALU.subtract)
        k.red("dve", (gt[0], gt[0][:, 0:NB]), (t1, b3(t1)), ALU.max)
        k.tt("dve", (gt[1], gt[1][:, 0:NB]), (mst, mst[:, 0:NB]), (ncum, nlast.rearrange("p b o -> p (b o)")), ALU.subtract)
        k.tt("dve", (mnew, mnew[:, 0:NB]), (gt[1], gt[1][:, 0:NB]), (gt[0], gt[0][:, 0:NB]), ALU.max)
        k.tt("dve", (gt[2], gt[2][:, 0:NB]), (gt[1], gt[1][:, 0:NB]), (mnew, mnew[:, 0:NB]), ALU.subtract)
        k.act((gt[3], gt[3][:, 0:NB]), (gt[2], gt[2][:, 0:NB]), AF.Exp)
        k.tt("dve", (gpk, gpk[:, 2, :].rearrange("p (b l) -> p b l", l=LB)), (gpk, gpk[:, 0, :].rearrange("p (b l) -> p b l", l=LB)),
             (gt[3], gt[3][:, 0:NB].unsqueeze(2).to_broadcast([4, NB, LB])), ALU.mult)
        for j in range(3):
            k.tr((pM[1], pM[1][:, 4 * j:4 * j + 4]), (gpk, gpk[:, j, :]), (identf, identf[0:4, 0:4]))
        k.copy("dve", (tokS, tokS[:]), (pM[1], pM[1][:, 0:12]))
        k.tt("dve", (s0d, s0d[:, :, 0:NB]), (identf, identf[0:4, 0:4].unsqueeze(2).to_broadcast([4, 4, NB])),
             (gt[3], gt[3][:, 0:NB].unsqueeze(1).to_broadcast([4, 4, NB])), ALU.mult)
        k.mm((pM[1], pM[1][:, 16:16 + 4 * NB]), (ones4, ones4[:]), (s0d, s0d[:, :, 0:NB].rearrange("p a b -> p (a b)")))
        k.copy("dve", (s0bc, s0bc[:, 0:4 * NB]), (pM[1], pM[1][:, 16:16 + 4 * NB]))

        for h in range(4):
            k.mm((pM[0], pM[0][:, h * 128:(h + 1) * 128]), (qkT, qkT[:, 4 + h, :]), (qkT, qkT[:, h, :]))
        for h in range(4):
            k.stt("dve", (PTm, PTm[:, h, :]), (pM[0], pM[0][:, h * 128:(h + 1) * 128]), (tokS, tokS[:, h:h + 1]),
                  (mU_in[mi], mU_in[mi][:]), ALU.mult, ALU.mult)
        pO = [pM[1], pM[2]]
        oap = lambda h: pO[h // 2][:, 256 * (h % 2):256 * (h % 2) + 129]
        if not smp:
            for h in range(4):
                k.mm((pO[h // 2], oap(h)), (qkT, qkT[:, h, :]), (Cb, Cb[:, h, 0:129]), start=True, stop=False)
                k.mm((pO[h // 2], oap(h)), (PTm, PTm[:, h, :]), (vaug, vaug[:, h, 0:129]), start=False, stop=True)
        else:
            for h in range(4):
                k.tr((pT, pT[:, h * 128:(h + 1) * 128]), (qkT, qkT[:, 4 + h, :]), (identb, identb[:]))
            for h in range(4):
                k.ts("dve", (ktm, ktm[:, h, :]), (pT, pT[:, h * 128:(h + 1) * 128]), (tokS, tokS[:, 8 + h:9 + h]), None, op0=ALU.mult)
            for h in range(4):
                k.dma("sp", Cs[:, :, 0:128], st_mC[:, h, :, :].rearrange("b d v -> d b v"), reads=[st_mC], writes=[Cs])
                k.dma("sp", Cs[:, :, 128], st_mn[:, h, :].rearrange("b d -> d b"), reads=[st_mn], writes=[Cs], allow_slow_non_contiguous=True)
                k.copy("act", (Csb, Csb[:, :, 0:129]), (Cs, Cs[:]))
                k.tt("dve", (qTm, qTm[:]), (qkT, qkT[:, h, :].unsqueeze(1).to_broadcast([128, NSB, 128])), (blkF, blkF[:]), ALU.mult)
                for b in range(NSB):
                    k.mm((pO[h // 2], oap(h)), (qTm, qTm[:, b, :]), (Csb, Csb[:, b, 0:129]), start=(b == 0), stop=False)
                k.mm((pO[h // 2], oap(h)), (PTm, PTm[:, h, :]), (vaug, vaug[:, h, 0:129]), start=False, stop=True)
                k.tt("dve", (ktmb, ktmb[:]), (ktm, ktm[:, h, :].unsqueeze(1).to_broadcast([128, NSB, 128])),
                     (rowm, rowm[:].unsqueeze(2).to_broadcast([128, NSB, 128])), ALU.mult)
                for grp in range(4):
                    bank = pF[grp % 2]
                    for bi in range(4):
                        b = 4 * grp + bi
                        k.mm((bank, bank[:, bi * 128:(bi + 1) * 128]), (ktmb, ktmb[:, b, :]), (vaug, vaug[:, h, 0:128]))
                    for bi in range(4):
                        b = 4 * grp + bi
                        k.stt("dve", (Cs, Cs[:, b, 0:128]), (Cs, Cs[:, b, 0:128]), (s0bc, s0bc[:, h * NSB + b:h * NSB + b + 1]),
                              (bank, bank[:, bi * 128:(bi + 1) * 128]), ALU.mult, ALU.add)
                for b in range(NSB):
                    k.mm((pR[0], pR[0][:, b:b + 1]), (ktmb, ktmb[:, b, :]), (vaug, vaug[:, h, 128:129]))
                k.tt("dve", (Cs, Cs[:, :, 128]), (Cs, Cs[:, :, 128]), (s0bc, s0bc[:, h * NSB:(h + 1) * NSB]), ALU.mult)
                k.tt("dve", (Cs, Cs[:, :, 128]), (Cs, Cs[:, :, 128]), (pR[0], pR[0][:, 0:NSB]), ALU.add)
                k.dma("sp", o_sC[:, h, :, :].rearrange("b d v -> d b v"), Cs[:, :, 0:128], reads=[Cs], writes=[o_sC])
                k.dma("sp", o_sn[:, h, :].rearrange("b d -> d b"), Cs[:, :, 128], reads=[Cs], writes=[o_sn], allow_slow_non_contiguous=True)
        for h in range(4):
            k.copy("act", (dn, dn[:, h:h + 1]), (pO[h // 2], oap(h)[:, 128:129]))
        k.stt("dve", (dn, dn[:]), (dn, dn[:]), -1.0, (dn, dn[:]), ALU.mult, ALU.max)
        k.tt("dve", (dn, dn[:]), (dn, dn[:]), (tokS, tokS[:, 4:8]), ALU.max)
        k.op("dve", lambda e: e.reciprocal(out=dn[:], in_=dn[:]), reads=[dn], writes=[dn])
        for h in range(4):
            k.act((hm, hm[:, h, :]), (pO[h // 2], oap(h)[:, 0:128]), AF.Copy, scale=(dn, dn[:, h:h + 1]))
        for h in range(4):
            k.op("dve", lambda e, h=h: e.bn_stats(out=bst[:, h, :], in_=hm[:, h, :]), reads=[hm], writes=[(bst, h)])
        for h in range(4):
            k.op("dve", lambda e, h=h: e.bn_aggr(out=bag[:, h, :], in_=bst[:, h, :]), reads=[(bst, h)], writes=[(bag, h)])
        k.act((bag, bag[:, :, 1:2]), (bag, bag[:, :, 1:2]), AF.Ln, bias=EPS)
        k.act((bag, bag[:, :, 1:2]), (bag, bag[:, :, 1:2]), AF.Exp, scale=-0.5)
        for h in range(4):
            k.ts("dve", (hn, hn[:, h, :]), (hm, hm[:, h, :]), (bag, bag[:, h, 0:1]), (bag, bag[:, h, 1:2]),
                 op0=ALU.subtract, op1=ALU.mult)
        for h in range(4):
            k.tr((pM[0], pM[0][:, h * 128:(h + 1) * 128]), (hn, hn[:, h, :]), (identf, identf[:]))
        for h in range(4):
            k.stt("dve", (hmT_all, hmT_all[:, ti, h, :], ti), (pM[0], pM[0][:, h * 128:(h + 1) * 128]), pcol(PT_MNG + h),
                  (soT, soT[:, h, :]), ALU.mult, ALU.mult)
        if not smp:
            for h in range(4):
                k.tr((pT, pT[:, h * 128:(h + 1) * 128]), (qkT, qkT[:, 4 + h, :]), (identb, identb[:]))
            for h in range(4):
                k.ts("dve", (ktm, ktm[:, h, :]), (pT, pT[:, h * 128:(h + 1) * 128]), (tokS, tokS[:, 8 + h:9 + h]), None, op0=ALU.mult)
            for h in range(4):
                k.mm((pO[h // 2], oap(h)), (ktm, ktm[:, h, :]), (vaug, vaug[:, h, 0:129]))
            for h in range(4):
                k.stt("dve", (Cst, Cst[:, h, :]), (Cst, Cst[:, h, :]), (s0bc, s0bc[:, h:h + 1]), (pO[h // 2], oap(h)),
                      ALU.mult, ALU.add)
            k.copy("act", (Cb, Cb[:, :, 0:129]), (Cst, Cst[:]))
            k.copy("dve", (mst, mst[:, 0:1]), (mnew, mnew[:, 0:1]))
        if ti == NTP - 1:
            for h in range(4):
                k.dma("sp", o_pC[h], Cst[:, h, 0:128], reads=[Cst], writes=[o_pC])
            k.dma("sp", o_pn[:].rearrange("h d -> d h"), Cst[:, :, 128], reads=[Cst], writes=[o_pn], allow_slow_non_contiguous=True)
            k.dma("sp", o_pm[:].rearrange("o h -> h o"), mnew[:, 0:1], reads=[mnew], writes=[o_pm], allow_slow_non_contiguous=True)
            for blk in range(2):
                proj_tm(pR[blk], pR[blk][:, :], C_QK + blk * 512, 512)
                k.copy("act", (zq_tm, zq_tm[:, blk * 512:(blk + 1) * 512]), (pR[blk], pR[blk][:, :]))
            k.dma("sp", o_pconv[:], zq_tm[125:128, :], reads=[zq_tm], writes=[o_pconv])
        if smp:
            k.dma("sp", o_sm[:, :].rearrange("b h -> h b"), mnew[:, 0:NSB], reads=[mnew], writes=[o_sm], allow_slow_non_contiguous=True)
            for blk in range(2):
                proj_tm(pR[blk], pR[blk][:, :], C_QK + blk * 512, 512)
                k.copy("act", (zq_tm, zq_tm[:, blk * 512:(blk + 1) * 512]), (pR[blk], pR[blk][:, :]))
            for b in range(NSB):
                k.dma("sp", o_sconv[3 * b:3 * b + 3, :], zq_tm[ST * b + 5:ST * b + 8, :], reads=[zq_tm], writes=[o_sconv])

    k.fence_mm = (pT, identb)
    BK = [pM[0], pM[1], pF[0], pF[1], pR[0], pR[1]]

    _rw_stop = int(_os.environ.get("KDBG_RW", "99"))

    def rwkv_front_proj(ti):
        k.copy("pool", (ext_r, ext_r[:, :, 0:1]), (cr, cr[:]))
        for g in range(4):
            n = min(4, 14 - 4 * g)
            for c in range(n):
                proj_fm(pF[g % 2], pF[g % 2][:, c * 128:(c + 1) * 128], C_R + (4 * g + c) * 128)
            k.copy("act", (ext_r, ext_r[:, 4 * g:4 * g + n, 1:129]),
                   (pF[g % 2], pF[g % 2][:, 0:n * 128].rearrange("p (c t) -> p c t", c=n)))
        k.copy("pool", (cr, cr[:]), (ext_r, ext_r[:, :, 128:129]))
        k.tt("pool", (xm, xm[:]), (ext_r, ext_r[:, :, 0:128]), (ext_r, ext_r[:, :, 1:129]), ALU.subtract)
        for c in range(14):
            k.stt("dve", (xm, xm[:, c, :]), (xm, xm[:, c, :]), pcol(PT_RMIX + c), (ext_r, ext_r[:, c, 1:129]), ALU.mult, ALU.add)

    def rwkv_tile(ti):
        smp = ti == NTP
        mi = 0
        NLV = 7
        rtl = rt
        if not smp:
            pass
        else:
            k.barrier()
            k.bot = mark_rw
            ext_rs = k.sbuf([128, 14, NSB, ST + 1], F32, "ext_rs")
            rtl = [k.sbuf([128, 4, 128], F32, f"rts{i}") for i in range(7)] + [rt[7], rt[8]]
            stg = k.sbuf([128, 512], F32, "stg")
            srs = xm.view(xm[:].rearrange("p a b -> p (a b)")[0:NSB, :], "srs")
            k.dma("sp", srs[:], st_rshift[:, :], reads=[st_rshift], writes=[srs])
            for c in range(14):
                k.tr((pM[0], pM[0][:, c * NSB:(c + 1) * NSB]), (srs, srs[:, c * 128:(c + 1) * 128]), (identf, identf[0:NSB, 0:NSB]))
            k.copy("act", (ext_rs, ext_rs[:, :, :, 0]), (pM[0], pM[0][:, 0:14 * NSB].rearrange("p (c b) -> p c b", c=14)))
            for g in range(4):
                n = min(4, 14 - 4 * g)
                for c in range(n):
                    proj_fm(pF[g % 2], pF[g % 2][:, c * 128:(c + 1) * 128], C_R + (4 * g + c) * 128)
                k.copy("act", (ext_rs, ext_rs[:, 4 * g:4 * g + n, :, 1:ST + 1]),
                       (pF[g % 2], pF[g % 2][:, 0:n * 128].rearrange("p (c b t) -> p c b t", c=n, b=NSB)))
            xm4 = xm[:].rearrange("p c (b t) -> p c b t", t=ST)
            k.tt("pool", (xm, xm4), (ext_rs, ext_rs[:, :, :, 0:ST]), (ext_rs, ext_rs[:, :, :, 1:ST + 1]), ALU.subtract)
            for c in range(14):
                k.stt("dve", (xm, xm4[:, c]), (xm, xm4[:, c]), pcol(PT_RMIX + c), (ext_rs, ext_rs[:, c, :, 1:ST + 1]), ALU.mult, ALU.add)
        rT, krT, vrT = xm[:, 0:4, :], xm[:, 4:8, :], xm[:, 8:12, :]
        sig, cums, gam, ginv, gexc, a_, kk, tmp, kr2 = rtl
        if _rw_stop <= 1:
            return
        k.act((thad, thad[0:64, :]), (xm, xm[0:64, 12, :]), AF.Tanh)
        k.copy("act", (thad, thad[64:128, :]), (xm, xm[64:128, 12, :]))
        k.act((sgd, sgd[:]), (xm, xm[:, 13, :]), AF.Sigmoid)
        for c in range(4):
            k.mm((pM[0], pM[0][:, c * 128:(c + 1) * 128]), (Wl_w2, Wl_w2[0:64, c * 128:(c + 1) * 128]), (thad, thad[0:64, :]))
        for c in range(4):
            k.act((sig, sig[:, c, :]), (pM[0], pM[0][:, c * 128:(c + 1) * 128]), AF.Sigmoid, bias=pcol(PT_RW0 + c))
        k.pe_fence()
        for c in range(4):
            k.mm((pM[1], pM[1][:, c * 128:(c + 1) * 128]), (Wl_a2, Wl_a2[64:128, c * 128:(c + 1) * 128]), (thad, thad[64:128, :]))
        k.pe_fence()
        for c in range(4):
            k.act((a_, a_[:, c, :]), (pM[1], pM[1][:, c * 128:(c + 1) * 128]), AF.Sigmoid, bias=pcol(PT_RA0 + c))
        for c in range(4):
            k.mm((pM[2], pM[2][:, c * 128:(c + 1) * 128]), (Wl_g2, Wl_g2[:, c * 128:(c + 1) * 128]), (sgd, sgd[:]))
        k.copy("act", (gTs, gTs[:].rearrange("p a b -> p (a b)")), (pM[2], pM[2][:, :]))
        if _rw_stop <= 2:
            return
        fl = lambda b: b[:].rearrange("p a b -> p (a b)")
        if not smp:
            k.scan("dve", (cums, fl(cums)), (resets[mi], resets[mi][:]), (sig, fl(sig)), 0.0, ALU.mult, ALU.add)
            k.act((gam, fl(gam)), (cums, fl(cums)), AF.Exp, scale=WSCALE)
            k.act((ginv, fl(ginv)), (cums, fl(cums)), AF.Exp, scale=-WSCALE)
            k.tt("pool", (tmp, tmp[:]), (cums, cums[:]), (sig, sig[:]), ALU.subtract)
            k.act((gexc, fl(gexc)), (tmp, fl(tmp)), AF.Exp, scale=WSCALE)
        else:
            k.act((gam, fl(gam)), (sig, fl(sig)), AF.Exp, scale=WSCALE)
        if _rw_stop <= 3:
            return
        for c in range(4):
            k.ts("dve", (kk, kk[:, c, :]), (xm, xm[:, 4 + c, :]), pcol(PT_RKK + c), None, op0=ALU.mult)
        k.tt("pool", (tmp, tmp[:]), (kk, kk[:]), (kk, kk[:]), ALU.mult)
        for c in range(4):
            k.mm((pM[0], pM[0][:, c * 128:(c + 1) * 128]), (bones, bones[:]), (tmp, tmp[:, c, :]))
        k.ts("dve", (tmp, fl(tmp)), (pM[0], pM[0][:, :]), 1e-24, None, op0=ALU.max)
        k.act((tmp, fl(tmp)), (tmp, fl(tmp)), AF.Ln)
        k.act((tmp, fl(tmp)), (tmp, fl(tmp)), AF.Exp, scale=-0.5)
        k.tt("dve", (kk, kk[:]), (kk, kk[:]), (tmp, tmp[:]), ALU.mult)
        for c in range(4):
            k.ts("dve", (tmp, tmp[:, c, :]), (a_, a_[:, c, :]), -1.0, pcol(PT_RKA + c), op0=ALU.add, op1=ALU.mult)
        k.stt("dve", (kr2, kr2[:]), (tmp, tmp[:]), 1.0, (xm, krT), ALU.add, ALU.mult)
        k.tt("pool", (tmp, tmp[:]), (xm, rT), (kr2, kr2[:]), ALU.mult)
        for c in range(4):
            k.ts("dve", (tmp, tmp[:, c, :]), (tmp, tmp[:, c, :]), pcol(PT_RRK + c), None, op0=ALU.mult)
        for c in range(4):
            k.mm((pM[1], pM[1][:, c * 128:(c + 1) * 128]), (bones, bones[:]), (tmp, tmp[:, c, :]))
        k.tt("dve", (bonT, fl(bonT)), (pM[1], pM[1][:, :]), (xm, vrT.rearrange("p a b -> p (a b)") if False else xm[:, 8:12, :].rearrange("p a b -> p (a b)")), ALU.mult)
        if _rw_stop <= 4:
            return
        if smp:
            rwkv_sample_core(xm, gam, kr2, kk, a_, tmp, stg, gTs, bonT)
            return
        k.stt("dve", (ART, ART[:, :, 0, :]), (kk, kk[:]), -1.0, (gexc, gexc[:]), ALU.mult, ALU.mult)
        k.tt("pool", (ART, ART[:, :, 1, :]), (xm, rT), (gam, gam[:]), ALU.mult)
        k.tt("pool", (tmp, tmp[:]), (kk, kk[:]), (a_, a_[:]), ALU.mult)
        k.tt("dve", (BTb, BTb[:]), (tmp, tmp[:]), (ginv, ginv[:]), ALU.mult)
        k.tt("pool", (KTb, KTb[:]), (kr2, kr2[:]), (ginv, ginv[:]), ALU.mult)
        k.copy("act", (VTb, VTb[:]), (xm, vrT))
        if _rw_stop <= 5:
            return
        for c in range(4):
            k.tr((pT, pT[:, c * 128:(c + 1) * 128]), (ART, ART[:, c, 0, :]), (identb, identb[:]))
            k.tr((pT, pT[:, 512 + c * 128:512 + (c + 1) * 128]), (BTb, BTb[:, c, :]), (identb, identb[:]))
        k.copy("act", (AB_tm, AB_tm[:].rearrange("p a b -> p (a b)")), (pT, pT[:, :]))
        for c in range(4):
            k.tr((pT, pT[:, c * 128:(c + 1) * 128]), (KTb, KTb[:, c, :]), (identb, identb[:]))
            k.tr((pT, pT[:, 512 + c * 128:512 + (c + 1) * 128]), (VTb, VTb[:, c, :]), (identb, identb[:]))
        k.copy("dve", (KV_tm, KV_tm[:].rearrange("p a b -> p (a b)")), (pT, pT[:, :]))
        if _rw_stop <= 6:
            return
        A_tm = lambda h: (AB_tm, AB_tm[:, 0, h * 64:(h + 1) * 64])
        B_tm = lambda h: (AB_tm, AB_tm[:, 1, h * 64:(h + 1) * 64])
        K_tm = lambda h: (KV_tm, KV_tm[:, 0, h * 64:(h + 1) * 64])
        V_tm = lambda h: (KV_tm, KV_tm[:, 1, h * 64:(h + 1) * 64])
        m2b = mask2[mi][:].unsqueeze(1).to_broadcast([128, 2, 256])
        GB2, GK2, Nn2, PP2, XX2 = [GBm, GBm_b], [GKm, GKm_b], [Nn, Nn_b], [PP, PP_b], [XX, XX_b]
        LB3 = [[pM[0], pM[1], pR[0]], [pF[0], pF[1], pR[1]]]
        for g in range(2):
            GBm_, GKm_, Nn_, XX_ = GB2[g], GK2[g], Nn2[g], XX2[g]
            heads = [4 * g + i for i in range(4)]
            HO = [(pbs, [(i, h) for i, h in enumerate(heads) if 64 * (h % 2) == pbs]) for pbs in (0, 64)]
            for pbs, hl in HO:
                for i, h in hl:
                    c, pb = h // 2, 64 * (h % 2)
                    off = (i % 2) * 256
                    rAR = (ART, ART[pb:pb + 64, c, :, :].rearrange("p a t -> p (a t)"))
                    k.mm((BK[i // 2], BK[i // 2][:, off:off + 256]), (BTb, BTb[pb:pb + 64, c, :]), rAR)
                    k.mm((BK[2 + i // 2], BK[2 + i // 2][:, off:off + 256]), (KTb, KTb[pb:pb + 64, c, :]), rAR)
                    k.mm((BK[4], BK[4][:, i * 128:(i + 1) * 128]), (ART, ART[pb:pb + 64, c, 0, :]), (BTb, BTb[pb:pb + 64, c, :]))
                k.pe_fence()
            for hf in range(2):
                k.tt("dve", (GBm_, GBm_[:, 2 * hf:2 * hf + 2, :]), (BK[hf], BK[hf][:].rearrange("p (a b) -> p a b", a=2)), (mask2[mi], m2b), ALU.mult)
                k.tt("dve", (GKm_, GKm_[:, 2 * hf:2 * hf + 2, :]), (BK[2 + hf], BK[2 + hf][:].rearrange("p (a b) -> p a b", a=2)), (mask2[mi], m2b), ALU.mult)
            k.tt("dve", (Nn_, Nn_[:]), (BK[4], BK[4][:].rearrange("p (a b) -> p a b", a=4)),
                 (mL_st[mi], mL_st[mi][:].unsqueeze(1).to_broadcast([128, 4, 128])), ALU.mult)
            for i, h in enumerate(heads):
                k.mm((BK[5], BK[5][:, i * 64:(i + 1) * 64]), (GKm_, GKm_[:, i, 0:128]), V_tm(h))
            k.copy("act", (XX_[0], XX_[0][:, :, 64:128]), (BK[5], BK[5][:, 0:256].rearrange("p (a b) -> p a b", a=4)))
            k.copy("pool", (XX_[0], XX_[0][:, :, 0:64]), (AB_tm, AB_tm[:, 0, 256 * g:256 * g + 256].rearrange("p (a b) -> p a b", a=4)))

        xfinal = [None, None]

        def levels_gen(g):
            GBm_, Nn_, PP_, XX_ = GB2[g], Nn2[g], PP2[g], XX2[g]
            bP, bQ, bX = LB3[g]
            Pc = lambda i: (Nn_, Nn_[:, i, :])
            PTc = lambda i: (GBm_, GBm_[:, i, 0:128])
            xi = 0
            for lvl in range(NLV):
                Xc, Xn = XX_[xi], XX_[1 - xi]
                for i in range(4):
                    o = (bX, bX[:, i * 128:(i + 1) * 128])
                    k.mm(o, (identb, identb[:]), (Xc, Xc[:, i, :]), start=True, stop=False)
                    k.mm(o, PTc(i), (Xc, Xc[:, i, :]), start=False, stop=True)
                k.copy("act", (Xn, Xn[:].rearrange("p a b -> p (a b)")), (bX, bX[:, :]))
                xi = 1 - xi
                yield
                if lvl < NLV - 1:
                    bb = [bP, bQ]
                    for i in range(4):
                        off = (i % 2) * 256
                        if lvl < NLV - 2:
                            k.mm((bb[i // 2], bb[i // 2][:, off:off + 128]), PTc(i), Pc(i))
                        k.mm((bb[i // 2], bb[i // 2][:, off + 128:off + 256]), Pc(i), PTc(i))
                    PPn = PP_[lvl % 2]
                    for hf in range(2):
                        if lvl < NLV - 2:
                            k.copy("dve", (PPn, PPn[:, 2 * hf:2 * hf + 2, :]), (bb[hf], bb[hf][:].rearrange("p (a b) -> p a b", a=2)))
                        else:
                            k.copy("dve", (PPn, PPn[:, 2 * hf:2 * hf + 2, 128:256]),
                                   (bb[hf], bb[hf][:].rearrange("p (a b) -> p a b", a=2)[:, :, 128:256]))
                    Pc = lambda i, PPn=PPn: (PPn, PPn[:, i, 0:128])
                    PTc = lambda i, PPn=PPn: (PPn, PPn[:, i, 128:256])
                    yield
            xfinal[g] = XX_[xi]

        gens = [levels_gen(0), levels_gen(1)]
        while gens:
            for g_ in list(gens):
                try:
                    next(g_)
                except StopIteration:
                    gens.remove(g_)

        for g in range(2):
            heads = [4 * g + i for i in range(4)]
            HO = [(pbs, [(i, h) for i, h in enumerate(heads) if 64 * (h % 2) == pbs]) for pbs in (0, 64)]
            GBt, GKt = GB2[g], GK2[g]
            Xf = xfinal[g]
            if _rw_stop <= 8:
                continue
            k.pe_fence()
            for pbs, hl in HO:
                for i, h in hl:
                    c, pb = h // 2, 64 * (h % 2)
                    ci = i // 2
                    o = (BK[0], BK[0][pb:pb + 64, ci * 128:(ci + 1) * 128])
                    k.mm(o, (Xf, Xf[:, i, 0:64]), (GBt, GBt[:, i, 128:256]), start=True, stop=False)
                    k.pe_fence()
                    k.mm(o, (identb, identb[pb:pb + 64, pb:pb + 64]), (ART, ART[pb:pb + 64, c, 1, :]), start=False, stop=True)
                    k.pe_fence()
            k.copy("act", (QT, QT[:].rearrange("p a b -> p (a b)")), (BK[0], BK[0][:, 0:256]))
            for pbs, hl in HO:
                for i, h in hl:
                    c, pb = h // 2, 64 * (h % 2)
                    ci = i // 2
                    o = (pM[2], pM[2][:, h * 64:(h + 1) * 64])
                    k.mm(o, (QT, QT[pb:pb + 64, ci, :]), (STb, STb[pb:pb + 64, c, :]), start=True, stop=False)
                    k.pe_fence()
                    k.mm(o, (GBt, GBt[:, i, 128:256]), (Xf, Xf[:, i, 64:128]), start=False, stop=False)
                    k.mm(o, (GKt, GKt[:, i, 128:256]), V_tm(h), start=False, stop=True)
                    k.pe_fence()
            if _rw_stop <= 9:
                continue
            for pbs, hl in HO:
                for i, h in hl:
                    c, pb = h // 2, 64 * (h % 2)
                    ci = i // 2
                    k.mm((BK[1], BK[1][pb:pb + 64, ci * 64:(ci + 1) * 64]), (Xf, Xf[:, i, 0:64]), B_tm(h))
                k.pe_fence()
            k.tt("dve", (IE, IE[:, 2 * g:2 * g + 2, :]), (BK[1], BK[1][:, 0:128].rearrange("p (a b) -> p a b", a=2)),
                 (I2, I2[:].unsqueeze(1).to_broadcast([128, 2, 64])), ALU.add)
            for pbs, hl in HO:
                for i, h in hl:
                    c, pb = h // 2, 64 * (h % 2)
                    ci = i // 2
                    o = (BK[2], BK[2][pb:pb + 64, ci * 64:(ci + 1) * 64])
                    k.mm(o, (IE, IE[pb:pb + 64, c, :]), (STf, STf[pb:pb + 64, c, :]), start=True, stop=False)
                    k.pe_fence()
                    k.mm(o, B_tm(h), (Xf, Xf[:, i, 64:128]), start=False, stop=False)
                    k.mm(o, K_tm(h), V_tm(h), start=False, stop=True)
                    k.pe_fence()
            for ci in range(2):
                c = 2 * g + ci
                k.ts("dve", (STf, STf[:, c, :]), (BK[2], BK[2][:, ci * 64:(ci + 1) * 64]), (gam, gam[:, c, 127:128]), None, op0=ALU.mult)
            k.copy("act", (STb, STb[:, 2 * g:2 * g + 2, :]), (STf, STf[:, 2 * g:2 * g + 2, :]))
        if _rw_stop <= 10:
            return
        rwkv_epilogue(ti, pM[2], tmp)
        if ti == NTP - 1:
            for c in range(4):
                k.tr((pM[0], pM[0][0:64, c * 128:(c + 1) * 128]), (STf, STf[:, c, :]), (identf, identf[:]))
            k.copy("act", (rt[0], rt[0][0:64, :, :]), (pM[0], pM[0][0:64, :].rearrange("p (a b) -> p a b", a=4)))
            k.dma("sp", o_pS[:].rearrange("(h i) j -> i h j", h=8), rt[0][0:64, :, :].rearrange("p c (f j) -> p (c f) j", f=2),
                  reads=[rt[0]], writes=[o_pS])

    def rwkv_epilogue(ti, Yb, tmp):
        pM2 = [None, None, Yb]
        for h in range(8):
            k.op("dve", lambda e, h=h: e.bn_stats(out=bst8[:, h, :], in_=Yb[:, h * 64:(h + 1) * 64]), reads=[Yb], writes=[(bst8, h)])
        for h in range(8):
            k.op("dve", lambda e, h=h: e.bn_aggr(out=bag8[:, h, :], in_=bst8[:, h, :]), reads=[(bst8, h)], writes=[(bag8, h)])
        k.act((bag8, bag8[:, :, 1:2]), (bag8, bag8[:, :, 1:2]), AF.Ln, bias=GN_EPS)
        k.act((bag8, bag8[:, :, 1:2]), (bag8, bag8[:, :, 1:2]), AF.Exp, scale=-0.5)
        for h in range(8):
            k.ts("dve", (yn, yn[:, h, :]), (Yb, Yb[:, h * 64:(h + 1) * 64]), (bag8, bag8[:, h, 0:1]), (bag8, bag8[:, h, 1:2]),
                 op0=ALU.subtract, op1=ALU.mult)
        for c in range(4):
            k.tr((pM[0], pM[0][:, c * 128:(c + 1) * 128]), (yn, yn[:, 2 * c:2 * c + 2, :].rearrange("p a b -> p (a b)")), (identf, identf[:]))
        for c in range(4):
            k.ts("dve", (tmp, tmp[:, c, :]), (pM[0], pM[0][:, c * 128:(c + 1) * 128]), pcol(PT_RLNG + c), pcol(PT_RLNB + c),
                 op0=ALU.mult, op1=ALU.add)
        k.tt("pool", (tmp, tmp[:]), (tmp, tmp[:]), (bonT, bonT[:]), ALU.add)
        k.tt("dve", (yrgT_all, yrgT_all[:, ti, :, :], ti), (tmp, tmp[:]), (gTs, gTs[:]), ALU.mult)

    rsc = k.dram("rw_scratch", [6, 128, RW], F32)
    ysc = k.dram("ry_scratch", [128, RW], F32)

    def rwkv_sample_core(xm, dec, kr2, kk, a_, tmp, stg, gTs, bonT):
        ti = NTP
        srcs = []
        srcs.append((xm, lambda c: xm[:, c, :]))
        srcs.append((dec, lambda c: dec[:, c, :]))
        srcs.append((kr2, lambda c: kr2[:, c, :]))
        srcs.append((xm, lambda c: xm[:, 8 + c, :]))
        for q in range(6):
            if q == 4:
                k.ts("dve", (tmp, tmp[:]), (kk, kk[:]), -1.0, None, op0=ALU.mult)
                sb_, fn = tmp, (lambda c: tmp[:, c, :])
            elif q == 5:
                k.tt("dve", (tmp, tmp[:]), (kk, kk[:]), (a_, a_[:]), ALU.mult)
                sb_, fn = tmp, (lambda c: tmp[:, c, :])
            else:
                sb_, fn = srcs[q]
            pb_ = pM[q % 2]
            for c in range(4):
                k.tr((pb_, pb_[:, c * 128:(c + 1) * 128]), (sb_, fn(c)), (identf, identf[:]))
            k.copy("act", (stg, stg[:]), (pb_, pb_[:, :]))
            k.dma("sp", rsc[q], stg[:], reads=[stg], writes=[(rsc, q)])
        for blk, (c0, n) in enumerate(((0, 512), (512, 512), (1024, 512), (1536, 256))):
            proj_tm(pR[blk % 2], pR[blk % 2][:, 0:n], C_R + c0, n)
            k.copy("act", (stg, stg[:, 0:n]), (pR[blk % 2], pR[blk % 2][:, 0:n]))
            for b in range(NSB):
                k.dma("sp", o_sshift[b:b + 1, c0:c0 + n], stg[ST * b + ST - 1:ST * b + ST, 0:n], reads=[stg], writes=[o_sshift])
        k.barrier()
        k.bot = mark_rw
        vec6 = k.sbuf([128, 6, ST, RN], F32, "vec6")
        Ssb = k.sbuf([128, RN, RN], F32, "Ssb")
        tmpS = k.sbuf([128, RN, RN], F32, "tmpS")
        sa = k.sbuf([128, RN], F32, "sa")
        ys = k.sbuf([128, ST, RN], F32, "ys")
        Ytm = k.sbuf([128, RW], F32, "Ytm")
        k.dma("sp", Ssb[:].rearrange("p a b -> p (a b)"), st_rS[:, :], reads=[st_rS], writes=[Ssb])
        for q in range(6):
            for b in range(NSB):
                k.dma("sp", vec6[RH * b:RH * b + RH, q, :, :], rsc[q, ST * b:ST * b + ST, :].rearrange("t (h j) -> h t j", h=RH),
                      reads=[(rsc, q)], writes=[(vec6, q)])
        HV = RN // 2

        def rec_gen(hf):
            i0 = hf * HV
            S_ = (Ssb, Ssb[:, i0:i0 + HV, :], hf)
            T_ = (tmpS, tmpS[:, i0:i0 + HV, :], hf)
            bc = lambda q, t: (vec6, vec6[:, q, t, :].unsqueeze(1).to_broadcast([128, HV, RN]), q)
            for t in range(ST):
                k.tt("dve", T_, S_, bc(4, t), ALU.mult)
                yield
                k.red("dve", (sa, sa[:, i0:i0 + HV], hf), T_, ALU.add)
                yield
                k.tt("pool", S_, S_, bc(1, t), ALU.mult)
                yield
                k.tt("dve", T_, (sa, sa[:, i0:i0 + HV].unsqueeze(2).to_broadcast([128, HV, RN]), hf), bc(5, t), ALU.mult)
                yield
                k.tt("dve", S_, S_, T_, ALU.add)
                yield
                k.tt("pool", T_, (vec6, vec6[:, 3, t, i0:i0 + HV].unsqueeze(2).to_broadcast([128, HV, RN]), 3), bc(2, t), ALU.mult)
                yield
                k.tt("dve", S_, S_, T_, ALU.add)
                yield
                k.tt("pool", T_, S_, bc(0, t), ALU.mult)
                yield
                k.red("dve", (ys, ys[:, t, i0:i0 + HV], (hf, t)), T_, ALU.add)
                yield

        gens_ = [rec_gen(0), rec_gen(1)]
        while gens_:
            for g_ in list(gens_):
                try:
                    next(g_)
                except StopIteration:
                    gens_.remove(g_)
        k.dma("sp", o_sS[:, :], Ssb[:].rearrange("p a b -> p (a b)"), reads=[Ssb], writes=[o_sS])
        k.dma("sp", ysc[:, :], ys[:].rearrange("p a b -> p (a b)"), reads=[ys], writes=[ysc])
        for b in range(NSB):
            k.dma("sp", Ytm[ST * b:ST * b + ST, :].rearrange("t (h i) -> t h i", h=RH),
                  ysc[RH * b:RH * b + RH, :].rearrange("h (t i) -> t h i", t=ST), reads=[ysc], writes=[Ytm])
        rwkv_epilogue(ti, Ytm, rt[7])

    def tail_rows(ti):
        if ti != NTP - 1:
            return
        for blk, (c0, n) in enumerate(((0, 512), (512, 512), (1024, 512), (1536, 256))):
            proj_tm(pR[blk % 2], pR[blk % 2][:, 0:n], C_R + c0, n)
            k.copy("act", (rt[1], rt[1][96:128, :, :].rearrange("p a b -> p (a b)")[:, 0:n]), (pR[blk % 2], pR[blk % 2][96:128, 0:n]))
            k.dma("sp", o_pshift[0:1, c0:c0 + n], rt[1][127:128, :, :].rearrange("p a b -> p (a b)")[:, 0:n], reads=[rt[1]], writes=[o_pshift])

    tiles_all = list(range(NT)) if stage >= 5 else list(range(NTP))

    def phase_1b(k):
        k.barrier()
        k.bot = mark_1a
        W_g = k.sbuf([128, 8, 2048], BF16, "W_g")
        W_bm = k.sbuf([128, 4, D], BF16, "W_bm")
        W_br = k.sbuf([128, 4, D], BF16, "W_br")
        W_out = k.sbuf([128, 8, D], BF16, "W_out")
        k.dma("pool", W_bm[:], d_w_bm[:], reads=[d_w_bm], writes=[W_bm])
        for kh in range(2):
            k.dma("pool", W_g[:, 4 * kh:4 * kh + 4, 0:1024], d_w_in[:, 4 * kh:4 * kh + 4, C_G:C_G + 1024], reads=[d_w_in], writes=[(W_g, "a")])
        k.dma("pool", W_br[:], d_w_br[:], reads=[d_w_br], writes=[W_br])
        for kh in range(2):
            k.dma("pool", W_g[:, 4 * kh:4 * kh + 4, 1024:2048], d_w_in[:, 4 * kh:4 * kh + 4, C_G + 1024:C_G + 2048], reads=[d_w_in], writes=[(W_g, "b")])
        k.dma("pool", W_out[:], d_w_out[:], reads=[d_w_out], writes=[W_out])
        xtb = [k.sbuf([128, D], F32, f"xtb{i}") for i in range(2)]
        hb2 = k.sbuf([128, D], BF16, "hb2")
        hT2 = k.sbuf([128, 8, 128], BF16, "hT2")
        ss2 = k.sbuf([128, 1], F32, "ss2")
        rs2 = k.sbuf([128, 1], F32, "rs2")
        sgb = [k.sbuf([128, 512], F32, f"sgb{i}") for i in range(2)]
        yab = k.sbuf([128, D], F32, "yab")
        mg = [k.sbuf([128, D], BF16, f"mg{i}") for i in range(2)]
        mT = k.sbuf([128, 8, 128], BF16, "mT")
        def head_gen(ti):
            xb = xtb[ti % 2]
            xd, xap = x_rows(ti)
            k.dma("sp", xb[:], xap, reads=[xd], writes=[xb])
            norm_generic(xb, g1bc, hb2, hT2, ss2, rs2)
            yield
            for half, (Wb, src, key) in enumerate(((W_bm, hmT_all, "a"), (W_br, yrgT_all, "b"))):
                for blk in range(2):
                    for kc in range(4):
                        k.mm((pR[blk], pR[blk][:, :]), (src, src[:, ti, kc, :], ti), (Wb, Wb[:, kc, blk * 512:(blk + 1) * 512]),
                             start=(kc == 0), stop=(kc == 3))
                    col = half * 1024 + blk * 512
                    for kc in range(8):
                        k.mm((pF[blk], pF[blk][:, :]), (hT2, hT2[:, kc, :]), (W_g, W_g[:, kc, col:col + 512], key),
                             start=(kc == 0), stop=(kc == 7))
                    k.act((sgb[blk], sgb[blk][:]), (pF[blk], pF[blk][:, :]), AF.Sigmoid)
                    if half == 0:
                        k.tt("dve", (yab, yab[:, blk * 512:(blk + 1) * 512]), (sgb[blk], sgb[blk][:]), (pR[blk], pR[blk][:, :]), ALU.mult)
                    else:
                        k.tt("dve", (sgb[blk], sgb[blk][:]), (sgb[blk], sgb[blk][:]), (pR[blk], pR[blk][:, :]), ALU.mult)
                        k.tt("pool", (mg[ti % 2], mg[ti % 2][:, blk * 512:(blk + 1) * 512]), (sgb[blk], sgb[blk][:]), (yab, yab[:, blk * 512:(blk + 1) * 512]), ALU.add)
                    yield

        def tailb_gen(ti):
            xb = xtb[ti % 2]
            mgt = mg[ti % 2]
            for kc in range(8):
                k.tr((pT, pT[:, kc * 128:(kc + 1) * 128]), (mgt, mgt[:, kc * 128:(kc + 1) * 128]), (identb, identb[:]))
            k.copy("act", (mT, mT[:].rearrange("p a b -> p (a b)")), (pT, pT[:, :]))
            yield
            for blk in range(2):
                for kc in range(8):
                    k.mm((pM[blk], pM[blk][:, :]), (mT, mT[:, kc, :]), (W_out, W_out[:, kc, blk * 512:(blk + 1) * 512]),
                         start=(kc == 0), stop=(kc == 7))
                k.tt("dve", (xb, xb[:, blk * 512:(blk + 1) * 512]), (xb, xb[:, blk * 512:(blk + 1) * 512]), (pM[blk], pM[blk][:, :]), ALU.add)
                yield
            k.dma("sp", x1s[ti * 128:(ti + 1) * 128, :], xb[:], reads=[xb], writes=[(x1s, ti)])

        def rr(gens):
            gens = list(gens)
            while gens:
                for g_ in list(gens):
                    try:
                        next(g_)
                    except StopIteration:
                        gens.remove(g_)

        tlb = list(tiles_all)
        rr([head_gen(tlb[0])])
        for idx, ti in enumerate(tlb):
            gl = [tailb_gen(ti)]
            if idx + 1 < len(tlb):
                gl.append(head_gen(tlb[idx + 1]))
            rr(gl)

    def norm_generic(xbuf, gbc, hb_, hT_, ss_, rs_):
        k.act((hb_, hb_[:]), (xbuf, xbuf[:]), AF.Square, accum=(ss_, ss_[:]))
        k.ts("dve", (rs_, rs_[:]), (ss_, ss_[:]), 1.0 / D, EPS, op0=ALU.mult, op1=ALU.add)
        k.act((rs_, rs_[:]), (rs_, rs_[:]), AF.Ln)
        k.act((rs_, rs_[:]), (rs_, rs_[:]), AF.Exp, scale=-0.5)
        k.stt("dve", (hb_, hb_[:]), (xbuf, xbuf[:]), (rs_, rs_[:, 0:1]), (gbc, gbc[:]), ALU.mult, ALU.mult)
        for kc in range(8):
            k.tr((pT, pT[:, kc * 128:(kc + 1) * 128]), (hb_, hb_[:, kc * 128:(kc + 1) * 128]), (identb, identb[:]))
        k.copy("act", (hT_, hT_[:].rearrange("p a b -> p (a b)")), (pT, pT[:, :]))

    def phase_2(k):
        k.barrier()
        k.bot = mark_phase
        F_up = k.sbuf([128, 8, 2 * DFF], BF16, "F_up")
        F_dn = k.sbuf([128, NFC, D], BF16, "F_dn")
        PGW = k.sbuf([128, 8, D], BF16, "PGW")
        PPJ = k.sbuf([128, 2, D], BF16, "PPJ")
        NG = 4
        CW = DFF // NG
        for g in range(NG):
            for part in range(2):
                k.dma("pool", F_up[:, :, part * DFF + g * CW:part * DFF + (g + 1) * CW], d_f_up[:, :, part * DFF + g * CW:part * DFF + (g + 1) * CW],
                      reads=[d_f_up], writes=[(F_up, g)])
        for g in range(2):
            k.dma("pool", F_dn[:, 11 * g:11 * g + 11, :], d_f_down[:, 11 * g:11 * g + 11, :], reads=[d_f_down], writes=[(F_dn, g)])
        k.dma("pool", PGW[:], d_pgw[:], reads=[d_pgw], writes=[PGW])
        k.dma("pool", PPJ[:], d_ppj[:], reads=[d_ppj], writes=[PPJ])
        g2bc = k.sbuf([128, D], F32, "g2bc")
        g3bc = k.sbuf([128, D], F32, "g3bc")
        g4bc = k.sbuf([128, D], F32, "g4bc")
        fct = k.sbuf([128, 4 * NFC], F32, "fct")
        k.dma("sp", g2bc[:], d_g2[:], reads=[d_g2], writes=[g2bc])
        k.dma("sp", g3bc[:], d_g3[:], reads=[d_g3], writes=[g3bc])
        k.dma("sp", g4bc[:], d_g4[:], reads=[d_g4], writes=[g4bc])
        k.dma("sp", fct[:], d_fctab[:], reads=[d_fctab], writes=[fct])
        fcol = lambda c: (fct, fct[:, c:c + 1])
        xq = [k.sbuf([128, D], F32, f"xq{i}") for i in range(2)]
        hb3 = k.sbuf([128, D], BF16, "hb3")
        hT3s = [k.sbuf([128, 8, 128], BF16, f"hT3a{i}") for i in range(2)]
        ss3 = k.sbuf([128, 1], F32, "ss3")
        rs3 = k.sbuf([128, 1], F32, "rs3")
        gT = k.sbuf([128, NFC, 128], BF16, "gT")
        cf = k.sbuf([128, NFC, 2], F32, "cf")
        GS = 4
        EXW = NSB * (ST + 2)
        ex4 = [k.sbuf([128, GS, EXW], F32, f"ex4_{i}") for i in range(2)]
        cc4 = [k.sbuf([128, GS, 128], F32, f"cc4_{i}") for i in range(2)]
        t14 = [k.sbuf([128, GS, 128], F32, "t14_0")] * 2
        up4 = [k.sbuf([128, GS, 128], F32, f"up4_{i}") for i in range(2)]
        sg3 = [k.sbuf([128, 512], F32, "sg30")] * 2
        ppt = k.sbuf([128, PLE], F32, "ppt")
        ppb = k.sbuf([128, PLE], BF16, "ppb")
        peT = k.sbuf([128, 2, 128], BF16, "peT")
        utm = sg3[0]
        k.memset("pool", (cf, cf[:]), 0.0)
        cfs = k.sbuf([128, NFC, 2 * NSB], F32, "cfs")
        GC = 1.5957691216057308
        BLK6 = ((0, 512), (512, 512), (1024, 512), (1536, 512), (2048, 512), (2560, 256))
        groups = [list(range(g0, min(g0 + GS, NFC))) for g0 in range(0, NFC, GS)]
        gbank = [pF[0], pF[1]]
        ubank = [pM[0], pM[1]]

        def fup_w(col):
            g = col // CW
            g_hi = (col + 127) // CW
            return g, g_hi

        def stage_A(ti, gi, hT3):
            smp = ti == NTP
            p = gi % 2
            chunks = groups[gi]
            n = len(chunks)
            c0 = chunks[0]
            for part, bank in ((0, gbank[p]), (1, ubank[p])):
                for ci, c in enumerate(chunks):
                    g, g_hi = fup_w(c * 128)
                    for kc in range(8):
                        k.mm((bank, bank[:, ci * 128:(ci + 1) * 128]), (F_up, F_up[:, kc, part * DFF + c * 128:part * DFF + (c + 1) * 128], g),
                             (hT3, hT3[:, kc, :]), start=(kc == 0), stop=(kc == 7))
                        if g_hi != g and g_hi in F_up.subs and F_up.subs[g_hi].w is not None:
                            k.streams["pe"][-1].deps.add(F_up.subs[g_hi].w)
            ex = ex4[p]
            if not smp:
                k.copy("pool", (ex, ex[:, 0:n, 0:2]), (cf, cf[:, c0:c0 + n, :]))
                k.copy("act", (ex, ex[:, 0:n, 2:130]), (gbank[p], gbank[p][:, 0:n * 128].rearrange("p (c t) -> p c t", c=n)))
                k.copy("pool", (cf, cf[:, c0:c0 + n, :]), (ex, ex[:, 0:n, 128:130]))
            else:
                exs = ex[:, 0:n, :].rearrange("p c (b t) -> p c b t", t=ST + 2)
                k.copy("pool", (ex, exs[:, :, :, 0:2]), (cfs, cfs[:, c0:c0 + n, :].rearrange("p c (b j) -> p c b j", j=2)))
                k.copy("act", (ex, exs[:, :, :, 2:ST + 2]), (gbank[p], gbank[p][:, 0:n * 128].rearrange("p (c b t) -> p c b t", c=n, b=NSB)))
            k.copy("act", (up4[p], up4[p][:, 0:n, :]), (ubank[p], ubank[p][:, 0:n * 128].rearrange("p (c t) -> p c t", c=n)))
            for ci, c in enumerate(chunks):
                if not smp:
                    tap = lambda j: ex[:, ci, j:j + 128]
                    ccv = cc4[p][:, ci, :]
                else:
                    e3 = ex[:, ci, :].rearrange("p (b t) -> p b t", t=ST + 2)
                    tap = lambda j, e3=e3: e3[:, :, j:j + ST]
                    ccv = cc4[p][:, ci, :].rearrange("p (b t) -> p b t", t=ST)
                cb = cc4[p]
                k.ts("dve", (cb, ccv), (ex, tap(2)), fcol(2 * NFC + c), fcol(3 * NFC + c), op0=ALU.mult, op1=ALU.add)
                k.stt("dve", (cb, ccv), (ex, tap(1)), fcol(1 * NFC + c), (cb, ccv), ALU.mult, ALU.add)
                k.stt("dve", (cb, ccv), (ex, tap(0)), fcol(0 * NFC + c), (cb, ccv), ALU.mult, ALU.add)

        def stage_B(ti, gi):
            p = gi % 2
            chunks = groups[gi]
            n = len(chunks)
            c0 = chunks[0]
            cb = (cc4[p], cc4[p][:, 0:n, :])
            ta = (t14[p], t14[p][:, 0:n, :])
            k.tt("pool", ta, cb, cb, ALU.mult)
            k.ts("dve", ta, ta, 0.044715, 1.0, op0=ALU.mult, op1=ALU.add)
            k.tt("pool", ta, ta, cb, ALU.mult)
            k.act(ta, ta, AF.Sigmoid, scale=GC)
            k.tt("pool", ta, ta, cb, ALU.mult)
            k.tt("dve", (gT, gT[:, c0:c0 + n, :], ("g", gi)), ta, (up4[p], up4[p][:, 0:n, :]), ALU.mult)

        def load_norm2(ti):
            xb = xq[ti % 2]
            k.dma("sp", xb[:], x1s[ti * 128:(ti + 1) * 128, :], reads=[(x1s, ti)], writes=[xb])
            if ti == NTP:
                for b6, (c0, n) in enumerate(BLK6):
                    k.dma("sp", utm[0:2 * NSB, 0:n], st_fconv[:, c0:c0 + n], reads=[st_fconv], writes=[utm])
                    nch = n // 128
                    for ci in range(nch):
                        k.tr((pR[b6 % 2], pR[b6 % 2][:, ci * 32:(ci + 1) * 32]), (utm, utm[0:2 * NSB, ci * 128:(ci + 1) * 128]),
                             (identf, identf[0:2 * NSB, 0:2 * NSB]))
                    k.copy("act", (cfs, cfs[:, 4 * b6:4 * b6 + nch, :]), (pR[b6 % 2], pR[b6 % 2][:, 0:nch * 32].rearrange("p (c x) -> p c x", c=nch)))
            norm_generic(xb, g2bc, hb3, hT3s[ti % 2], ss3, rs3)

        hb3b = hb3

        def groups_gen(ti):
            hT3 = hT3s[ti % 2]
            for gi in range(len(groups) + 1):
                if gi < len(groups):
                    stage_A(ti, gi, hT3)
                    yield
                if gi >= 1:
                    stage_B(ti, gi - 1)
                    yield

        def tail_gen(ti):
            smp = ti == NTP
            xb = xq[ti % 2]
            hT3 = hT3s[ti % 2]
            for blk in range(2):
                for c in range(NFC):
                    k.mm((pR[blk], pR[blk][:, :]), (gT, gT[:, c, :], ("g", c // GS)), (F_dn, F_dn[:, c, blk * 512:(blk + 1) * 512], c // 11),
                         start=(c == 0), stop=(c == NFC - 1))
                k.tt("dve", (xb, xb[:, blk * 512:(blk + 1) * 512]), (xb, xb[:, blk * 512:(blk + 1) * 512]), (pR[blk], pR[blk][:, :]), ALU.add)
                yield
            if ti == NTP - 1 or smp:
                for b6, (c0, n) in enumerate(BLK6):
                    for kc in range(8):
                        k.mm((pM[2], pM[2][:, 0:n]), (hT3, hT3[:, kc, :]), (F_up, F_up[:, kc, c0:c0 + n]),
                             start=(kc == 0), stop=(kc == 7))
                    if not smp:
                        k.copy("act", (utm, utm[96:128, 0:n]), (pM[2], pM[2][96:128, 0:n]))
                        k.dma("sp", o_pfconv[:, c0:c0 + n], utm[126:128, 0:n], reads=[utm], writes=[o_pfconv])
                    else:
                        k.copy("act", (utm, utm[:, 0:n]), (pM[2], pM[2][:, 0:n]))
                        for b in range(NSB):
                            k.dma("sp", o_sfconv[2 * b:2 * b + 2, c0:c0 + n], utm[ST * b + ST - 2:ST * b + ST, 0:n], reads=[utm], writes=[o_sfconv])
                    yield
            norm_generic(xb, g3bc, hb3b, hT3, ss3, rs3)
            yield
            pd, pap = (pp, pp[ti * 128:(ti + 1) * 128, :]) if not smp else (psm, psm[:, :])
            k.dma("sp", ppt[:], pap, reads=[pd], writes=[ppt])
            k.copy("act", (ppb, ppb[:]), (ppt, ppt[:]))
            for kc in range(2):
                k.tr((pT, pT[:, kc * 128:(kc + 1) * 128]), (ppb, ppb[:, kc * 128:(kc + 1) * 128]), (identb, identb[:]))
            k.copy("act", (peT, peT[:].rearrange("p a b -> p (a b)")), (pT, pT[:, 0:256]))
            yield
            for blk in range(2):
                for kc in range(8):
                    k.mm((pR[blk], pR[blk][:, :]), (hT3, hT3[:, kc, :]), (PGW, PGW[:, kc, blk * 512:(blk + 1) * 512]),
                         start=(kc == 0), stop=(kc == 7))
                yield
                k.act((sg3[blk], sg3[blk][:]), (pR[blk], pR[blk][:, :]), AF.Sigmoid)
                for kc in range(2):
                    k.mm((pM[2], pM[2][:, :]), (peT, peT[:, kc, :]), (PPJ, PPJ[:, kc, blk * 512:(blk + 1) * 512]),
                         start=(kc == 0), stop=(kc == 1))
                k.tt("dve", (sg3[blk], sg3[blk][:]), (sg3[blk], sg3[blk][:]), (pM[2], pM[2][:, :]), ALU.mult)
                k.tt("pool", (xb, xb[:, blk * 512:(blk + 1) * 512]), (xb, xb[:, blk * 512:(blk + 1) * 512]), (sg3[blk], sg3[blk][:]), ALU.add)
                yield
            k.act((hb3b, hb3b[:]), (xb, xb[:]), AF.Square, accum=(ss3, ss3[:]))
            k.ts("dve", (rs3, rs3[:]), (ss3, ss3[:]), 1.0 / D, EPS, op0=ALU.mult, op1=ALU.add)
            k.act((rs3, rs3[:]), (rs3, rs3[:]), AF.Ln)
            k.act((rs3, rs3[:]), (rs3, rs3[:]), AF.Exp, scale=-0.5)
            yield
            k.stt("dve", (xb, xb[:]), (xb, xb[:]), (rs3, rs3[:, 0:1]), (g4bc, g4bc[:]), ALU.mult, ALU.mult)
            if not smp:
                k.dma("sp", y_p[ti * 128:(ti + 1) * 128, :], xb[:], reads=[xb], writes=[y_p])
            else:
                k.dma("sp", y_s[:, :], xb[:], reads=[xb], writes=[y_s])
            if ti in nxt2:
                yield
                load_norm2(nxt2[ti])

        def run_rr(gens):
            gens = list(gens)
            while gens:
                for g_ in list(gens):
                    try:
                        next(g_)
                    except StopIteration:
                        gens.remove(g_)

        tl = list(tiles_all)
        nxt2 = {tl[i]: tl[i + 2] for i in range(len(tl) - 2)}
        load_norm2(tl[0])
        if len(tl) > 1:
            load_norm2(tl[1])
        run_rr([groups_gen(tl[0])])
        for idx, ti in enumerate(tl):
            tg = tail_gen(ti)
            if idx + 1 < len(tl):
                gg = groups_gen(tl[idx + 1])
                next(gg)
                next(gg)
                next(tg)
                next(tg)
                run_rr([gg, tg])
            else:
                run_rr([tg])

    _nt_dbg = int(_os.environ.get("KDBG_NT", "0"))
    for ti in (tiles_all if not _nt_dbg else list(range(_nt_dbg))):
        mixer_tile(ti)
        if stage >= 2:
            rwkv_tile(ti)
            tail_rows(ti)

    if stage >= 3:
        phase_1b(k)
    if stage >= 4:
        phase_2(k)
    k.emit()
    k.stats["sbuf_hiwater"] = k.hiwater
    k.stats["arena_bytes"] = k.arena_bytes
    return nc, k


def _chunk_rows(w, nk):
    return np.ascontiguousarray(w.reshape(nk, 128, w.shape[1]).transpose(1, 0, 2))


def _pcols(v, nc_):
    return v.reshape(nc_, 128).T


_PROG = {}


def _get_prog(stage=99, dbg=False):
    key = (stage, dbg)
    if key not in _PROG:
        _PROG[key] = build_program(stage, dbg)
    return _PROG[key]


def make_in_maps(inp):
    f = lambda a: np.ascontiguousarray(np.asarray(a, dtype=np.float32))
    ptab = np.zeros((128, 128), np.float32)
    mcw = f(inp["m_conv_w"])[0]
    for j in range(4):
        ptab[:, j * 8:(j + 1) * 8] = _pcols(mcw[j], 8)
    ptab[:, 32:40] = _pcols(f(inp["m_conv_b"])[0], 8)
    ptab[:, 40:54] = _pcols(f(inp["r_mix"])[0], 14)
    ptab[:, 54:58] = _pcols(f(inp["r_w0"])[0], 4)
    ptab[:, 58:62] = _pcols(f(inp["r_a0"])[0], 4)
    ptab[:, 62:66] = _pcols(f(inp["r_kk"])[0], 4)
    ptab[:, 66:70] = _pcols(f(inp["r_ka"])[0], 4)
    ptab[:, 70:74] = _pcols(f(inp["r_rk"])[0].reshape(-1), 4)
    ptab[:, 74:78] = _pcols(f(inp["r_ln_g"])[0], 4)
    ptab[:, 78:82] = _pcols(f(inp["r_ln_b"])[0], 4)
    ptab[:, 82:86] = _pcols(f(inp["m_norm_g"])[0], 4)
    fct = np.zeros((128, 4 * NFC), np.float32)
    fcw = f(inp["f_conv_w"])[0]
    for j in range(3):
        fct[:, j * NFC:(j + 1) * NFC] = _pcols(fcw[j], NFC)
    fct[:, 3 * NFC:4 * NFC] = _pcols(f(inp["f_conv_b"])[0], NFC)
    gbias = np.stack([f(inp["m_i_bias"])[0], f(inp["m_f_bias"])[0]], axis=1)
    ra2 = np.zeros((128, RW), np.float32)
    ra2[64:128] = f(inp["r_a2"])[0]
    bc = lambda v: np.ascontiguousarray(np.broadcast_to(f(v).reshape(1, D), (128, D)))
    shared = {
        "w_in": _chunk_rows(f(inp["w_in"])[0], 8),
        "w_bm": _chunk_rows(f(inp["w_branch_m"])[0], 4),
        "w_br": _chunk_rows(f(inp["w_branch_r"])[0], 4),
        "w_out": _chunk_rows(f(inp["w_out"])[0], 8),
        "f_up": _chunk_rows(f(inp["f_up"])[0], 8),
        "f_down": _chunk_rows(f(inp["f_down"])[0], NFC),
        "ple_gate_w": _chunk_rows(f(inp["ple_gate_w"])[0], 8),
        "ple_proj": _chunk_rows(f(inp["ple_proj"])[0], 2),
        "r_w2": f(inp["r_w2"])[0], "r_a2": ra2, "r_g2": f(inp["r_g2"])[0],
        "norm1_g": bc(inp["norm1_g"]), "norm2_g": bc(inp["norm2_g"]),
        "ple_norm_g": bc(inp["ple_norm_g"]), "final_norm_g": bc(inp["final_norm_g"]),
        "ptab": ptab, "fctab": fct, "gate_bias": np.ascontiguousarray(gbias),
    }
    maps = []
    for c in range(NCORES):
        sl = slice(c * NSB, (c + 1) * NSB)
        m = dict(shared)
        m["xp"] = f(inp["x_prompt"][c])
        m["xs"] = f(inp["x_sample"][sl]).reshape(128, D)
        m["pp"] = f(inp["p_prompt"][0, c])
        m["psm"] = f(inp["p_sample"][0, sl]).reshape(128, PLE)
        m["st_mconv"] = f(inp["state_mlstm_conv"][0, sl]).reshape(NSB * 3, 2 * MW)
        m["st_mC"] = f(inp["state_mlstm_C"][0, sl])
        m["st_mn"] = f(inp["state_mlstm_n"][0, sl])
        m["st_mm"] = f(inp["state_mlstm_m"][0, sl])
        m["st_rshift"] = f(inp["state_rwkv_shift"][0, sl])
        m["st_rS"] = f(inp["state_rwkv_S"][0, sl]).reshape(NSB * RH, RN * RN)
        m["st_fconv"] = f(inp["state_ffn_conv"][0, sl]).reshape(NSB * 2, DFF)
        maps.append(m)
    return maps


def assemble(results):
    g = lambda name: [np.asarray(r[name], dtype=np.float32) for r in results]
    y_p = np.stack(g("y_p"), 0)
    y_s = np.concatenate([a.reshape(NSB, ST, D) for a in g("y_s")], 0)
    p_conv = np.stack(g("p_conv"), 0)[None]
    p_C = np.stack(g("p_C"), 0)[None]
    p_n = np.stack(g("p_n"), 0)[None]
    p_m = np.stack([a.reshape(MH) for a in g("p_m")], 0)[None]
    p_shift = np.stack([a.reshape(RCOLS) for a in g("p_shift")], 0)[None]
    p_S = np.stack([a.reshape(RH, RN, RN) for a in g("p_S")], 0)[None]
    p_fconv = np.stack(g("p_fconv"), 0)[None]
    s_conv = np.concatenate([a.reshape(NSB, 3, 2 * MW) for a in g("s_conv")], 0)[None]
    s_C = np.concatenate(g("s_C"), 0)[None]
    s_n = np.concatenate(g("s_n"), 0)[None]
    s_m = np.concatenate(g("s_m"), 0)[None]
    s_shift = np.concatenate(g("s_shift"), 0)[None]
    s_S = np.concatenate([a.reshape(NSB, RH, RN, RN) for a in g("s_S")], 0)[None]
    s_fconv = np.concatenate([a.reshape(NSB, 2, DFF) for a in g("s_fconv")], 0)[None]
    return (y_p, y_s, p_conv, p_C, p_n, p_m, p_shift, p_S, p_fconv,
            s_conv, s_C, s_n, s_m, s_shift, s_S, s_fconv)


def kernel(**inputs):
    nc, _ = _get_prog()
    maps = make_in_maps(inputs)
    res = run_bass_kernel_spmd(nc, maps, core_ids=list(range(NCORES)))
    return assemble(res.results)
```

```python
import math
from contextlib import ExitStack

import numpy as np
import concourse.bass as bass
import concourse.mybir as mybir
from concourse.bass_utils import run_bass_kernel_spmd

F32 = mybir.dt.float32
BF16 = mybir.dt.bfloat16
AF = mybir.ActivationFunctionType
ALU = mybir.AluOpType
AX = mybir.AxisListType

ENGS = ("pe", "act", "dve", "pool", "sp")
N_DMA_SEMS = 8
SAME_ENG_DIST = 2

D = 1024
SEQ = 2048
NCORES = 8
NTP = SEQ // 128
NSB = 16
ST = 8
MW = 512
MH = 4
RW = 512
RH = 8
RN = 64
RCOLS = 1792
DFF = 2816
NFC = DFF // 128
PLE = 256
N_IN = 5896
C_QK, C_V, C_O, C_I, C_F, C_R, C_G = 0, 1024, 1536, 2048, 2052, 2056, 3848
EPS = 1e-6
GN_EPS = 64e-5
KSCALE = 128 ** -0.5
WSCALE = -math.exp(-0.5)


class _Trk:
    __slots__ = ("w", "r")

    def __init__(self):
        self.w = None
        self.r = []


class Buf:
    def __init__(self, t, name):
        self.t = t
        self.name = name
        self.whole = _Trk()
        self.subs = {}

    def __getitem__(self, idx):
        return self.t[idx]

    def view(self, ap, name=None):
        b = Buf(ap, name or self.name + "_v")
        b.whole = self.whole
        b.subs = self.subs
        return b


class _Op:
    __slots__ = ("eng", "fn", "deps", "needs_inc", "is_dma", "sem", "val", "pos", "force")


class K:
    def __init__(self, nc):
        self.nc = nc
        self.es = ExitStack()
        self.streams = {e: [] for e in ENGS}
        self.dma_rr = {e: 0 for e in ENGS}
        self.dma_last = {}
        self.nbuf = 0
        self.ops = []

    def _init_arena(self):
        nbytes = (int(self.nc.sbuf_bytes_remaining) - 512) // 64 * 64
        self.arena_bytes = nbytes
        self.arena = self.es.enter_context(self.nc.sbuf_tensor("arena", [128, nbytes // 2], BF16))
        self.bot = 0
        self.top = nbytes
        self.hiwater = 0

    def _view(self, off, shape, dtype):
        n = 1
        for d in shape[1:]:
            n *= d
        esz = 4 if dtype == F32 else 2
        v = self.arena[:, off // 2:(off + n * esz) // 2]
        if dtype == F32:
            v = v.bitcast(F32)
        if len(shape) > 2:
            names = " ".join(f"d{i}" for i in range(len(shape) - 1))
            v = v.rearrange(f"p ({names}) -> p {names}", **{f"d{i}": shape[i + 1] for i in range(len(shape) - 1)})
        if shape[0] < 128:
            v = v[0:shape[0]]
        return v, n * esz

    def sbuf(self, shape, dtype, name=None, top=False):
        if not hasattr(self, "arena"):
            self._init_arena()
        self.nbuf += 1
        name = name or f"sb{self.nbuf}"
        n = 1
        for d in shape[1:]:
            n *= d
        nb = (n * (4 if dtype == F32 else 2) + 63) // 64 * 64
        if top:
            self.top -= nb
            off = self.top
        else:
            off = self.bot
            self.bot += nb
        assert self.bot <= self.top, f"SBUF arena overflow allocating {name}: bot={self.bot} top={self.top}"
        self.hiwater = max(self.hiwater, self.bot + (self.arena_bytes - self.top))
        v, _ = self._view(off, list(shape), dtype)
        return Buf(v, name)

    def pe_fence(self):
        st = self.streams["pe"]
        if not st:
            return
        last = st[-1]
        o = self.op("pe", lambda h: h.nop(), (), ())
        o.deps.add(last)
        o.force = {last}
        if getattr(self, "fence_mm", None) is not None:
            fb, fi = self.fence_mm
            self.tr((fb, fb[:, 0:128]), (fi, fi[:]), (fi, fi[:]))
            last = self.streams["pe"][-1]
            o = self.op("pe", lambda h: h.nop(), (), ())
            o.deps.add(last)
            o.force = {last}

    def barrier(self):
        lasts = [st[-1] for st in self.streams.values() if st]
        lasts += list(self.dma_last.values())
        for e in ENGS:
            o = self.op(e, lambda h: h.nop(), (), ())
            o.deps.update(x for x in lasts if x is not o)

    def psum(self, shape, dtype, name=None):
        self.nbuf += 1
        name = "ps_" + (name or f"{self.nbuf}")
        t = self.es.enter_context(self.nc.psum_tensor(name, list(shape), dtype))
        return Buf(t, name)

    def dram(self, name, shape, dtype, kind="Internal"):
        t = self.nc.dram_tensor(name, list(shape), dtype, kind=kind)
        return Buf(t.ap(), name)

    def _touch(self, op, item, is_write):
        if isinstance(item, tuple):
            buf, key = item
        else:
            buf, key = item, None
        if key is None:
            trks = [buf.whole] + list(buf.subs.values())
        else:
            if key not in buf.subs:
                buf.subs[key] = _Trk()
            trks = [buf.whole, buf.subs[key]]
        for t in trks:
            if t.w is not None:
                op.deps.add(t.w)
            if is_write:
                op.deps.update(t.r)
        return buf, key

    def _commit(self, op, buf, key, is_write):
        if key is None:
            if is_write:
                buf.whole.w = op
                buf.whole.r = []
                buf.subs.clear()
            else:
                self._add_reader(buf.whole, op)
        else:
            t = buf.subs[key]
            if is_write:
                t.w = op
                t.r = []
            else:
                self._add_reader(t, op)

    @staticmethod
    def _add_reader(t, op):
        if not op.is_dma:
            t.r = [o for o in t.r if o.is_dma or o.eng != op.eng]
        t.r.append(op)

    def op(self, eng, fn, reads=(), writes=(), dma=False):
        o = _Op()
        o.eng = eng
        o.fn = fn
        o.deps = set()
        o.needs_inc = False
        o.is_dma = dma
        o.sem = None
        o.val = None
        o.force = None
        touched = []
        for it in reads:
            touched.append(self._touch(o, it, False) + (False,))
        for it in writes:
            touched.append(self._touch(o, it, True) + (True,))
        o.deps.discard(o)
        for buf, key, w in touched:
            self._commit(o, buf, key, w)
        if dma:
            kk = (eng, self.dma_rr[eng] % N_DMA_SEMS)
            self.dma_rr[eng] += 1
            prev = self.dma_last.get(kk)
            if prev is not None:
                o.deps.add(prev)
            self.dma_last[kk] = o
            o.sem = kk
            o.needs_inc = True
        o.pos = len(self.streams[eng])
        self.streams[eng].append(o)
        self.ops.append(o)
        return o

    def dma(self, eng, out, in_, reads=(), writes=(), **kw):
        return self.op(eng, lambda e: e.dma_start(out=out, in_=in_, **kw), reads, writes, dma=True)

    def emit(self):
        nc = self.nc
        for o in self.ops:
            real = []
            for d in o.deps:
                if (not d.is_dma) and (not o.is_dma) and d.eng == o.eng and o.eng == "pe":
                    if not (o.force and d in o.force):
                        continue
                d.needs_inc = True
                real.append(d)
            o.deps = real
        for e in ENGS:
            cs = [o for o in self.streams[e] if not o.is_dma]
            if cs:
                cs[-1].needs_inc = True
        es = self.es
        esem = {e: es.enter_context(nc.semaphore(f"s_{e}")) for e in ENGS}
        dsem = {}
        for e in ENGS:
            for i in range(min(N_DMA_SEMS, self.dma_rr[e])):
                dsem[(e, i)] = es.enter_context(nc.semaphore(f"d_{e}{i}"))
        dcount = {kk: 0 for kk in dsem}
        for e in ENGS:
            c = 0
            for o in self.streams[e]:
                if o.is_dma:
                    dcount[o.sem] += 16
                    o.val = dcount[o.sem]
                    o.sem = dsem[o.sem]
                elif o.needs_inc:
                    c += 1
                    o.val = c
                    o.sem = esem[e]
        final_waits = [(s, dcount[kk]) for kk, s in dsem.items() if dcount[kk] > 0]
        for e in ENGS:
            if e == "sp":
                continue
            cs = [o for o in self.streams[e] if not o.is_dma and o.needs_inc]
            if cs:
                final_waits.append((esem[e], cs[-1].val))
        streams = self.streams
        nwaits = [0]

        def run(e, handle):
            waited = {}
            for o in streams[e]:
                need = {}
                for d in o.deps:
                    if need.get(d.sem, (None, 0))[1] < d.val:
                        need[d.sem] = (d.sem, d.val)
                for s, v in need.values():
                    if waited.get(s, 0) >= v:
                        continue
                    handle.wait_ge(s, v)
                    nwaits[0] += 1
                    waited[s] = v
                ins = o.fn(handle)
                if o.is_dma:
                    ins.then_inc(o.sem, 16)
                elif o.needs_inc:
                    ins.then_inc(o.sem, 1)
            if e == "sp":
                for s, v in final_waits:
                    handle.wait_ge(s, v)

        with nc.Block() as block:
            @block.tensor
            def _(h):
                run("pe", h)

            @block.scalar
            def _(h):
                run("act", h)

            @block.vector
            def _(h):
                run("dve", h)

            @block.gpsimd
            def _(h):
                run("pool", h)

            @block.sync
            def _(h):
                run("sp", h)
        self.stats = dict(n_ops={e: len(streams[e]) for e in ENGS}, n_waits=nwaits[0])
        self.es.close()

    @staticmethod
    def _it(x):
        return (x[0], x[2]) if len(x) > 2 else x[0]

    def mm(self, out, lhsT, rhs, start=True, stop=True):
        return self.op("pe", lambda e: e.matmul(out[1], lhsT=lhsT[1], rhs=rhs[1], start=start, stop=stop),
                       reads=[self._it(lhsT), self._it(rhs)], writes=[self._it(out)])

    def tr(self, out, in_, ident):
        return self.op("pe", lambda e: e.transpose(out[1], in_[1], ident[1]),
                       reads=[self._it(in_), self._it(ident)], writes=[self._it(out)])

    def act(self, out, in_, func, bias=None, scale=None, accum=None, eng="act"):
        reads = [self._it(in_)]
        kw = {}
        if bias is not None:
            if isinstance(bias, tuple):
                reads.append(self._it(bias))
                kw["bias"] = bias[1]
            else:
                kw["bias"] = bias
        if scale is not None:
            if isinstance(scale, tuple):
                reads.append(self._it(scale))
                kw["scale"] = scale[1]
            else:
                kw["scale"] = scale
        writes = [self._it(out)]
        if accum is not None:
            writes.append(self._it(accum))
            kw["accum_out"] = accum[1]
        return self.op(eng, lambda e: e.activation(out=out[1], in_=in_[1], func=func, **kw), reads, writes)

    def tt(self, eng, out, in0, in1, op):
        return self.op(eng, lambda e: e.tensor_tensor(out=out[1], in0=in0[1], in1=in1[1], op=op),
                       reads=[self._it(in0), self._it(in1)], writes=[self._it(out)])

    def ts(self, eng, out, in0, s1, s2=None, op0=ALU.mult, op1=None, accum=None):
        reads = [self._it(in0)]
        a1 = s1
        a2 = s2
        if isinstance(s1, tuple):
            reads.append(self._it(s1))
            a1 = s1[1]
        if isinstance(s2, tuple):
            reads.append(self._it(s2))
            a2 = s2[1]
        kw = {}
        if op1 is not None:
            kw["op1"] = op1
        writes = [self._it(out)]
        if accum is not None:
            writes.append(self._it(accum))
            kw["accum_out"] = accum[1]
        return self.op(eng, lambda e: e.tensor_scalar(out=out[1], in0=in0[1], scalar1=a1, scalar2=a2, op0=op0, **kw),
                       reads, writes)

    def stt(self, eng, out, in0, scalar, in1, op0, op1):
        reads = [self._it(in0), self._it(in1)]
        a = scalar
        if isinstance(scalar, tuple):
            reads.append(self._it(scalar))
            a = scalar[1]
        return self.op(eng, lambda e: e.scalar_tensor_tensor(out=out[1], in0=in0[1], scalar=a, in1=in1[1], op0=op0, op1=op1),
                       reads, [self._it(out)])

    def copy(self, eng, out, in_):
        if eng == "act":
            return self.op(eng, lambda e: e.activation(out=out[1], in_=in_[1], func=AF.Copy),
                           reads=[self._it(in_)], writes=[self._it(out)])
        return self.op(eng, lambda e: e.tensor_copy(out=out[1], in_=in_[1]),
                       reads=[self._it(in_)], writes=[self._it(out)])

    def red(self, eng, out, in_, op, axis=AX.X):
        return self.op(eng, lambda e: e.tensor_reduce(out=out[1], in_=in_[1], axis=axis, op=op),
                       reads=[self._it(in_)], writes=[self._it(out)])

    def memset(self, eng, out, val):
        return self.op(eng, lambda e: e.memset(out[1], val), reads=[], writes=[self._it(out)])

    def scan(self, eng, out, d0, d1, init, op0, op1):
        return self.op(eng, lambda e: e.tensor_tensor_scan(out=out[1], data0=d0[1], data1=d1[1], initial=init, op0=op0, op1=op1),
                       reads=[self._it(d0), self._it(d1)], writes=[self._it(out)])


def build_program(stage=99, dbg=False):
    import os as _os
    nc = bass.Bass("TRN2", target_bir_lowering=False)
    k = K(nc)
    NT = NTP + 1

    def din(name, shape):
        return k.dram(name, shape, F32, "ExternalInput")

    def dout(name, shape):
        return k.dram(name, shape, F32, "ExternalOutput")

    xp = din("xp", [SEQ, D]); xs = din("xs", [128, D])
    pp = din("pp", [SEQ, PLE]); psm = din("psm", [128, PLE])
    st_mconv = din("st_mconv", [NSB * 3, 2 * MW])
    st_mC = din("st_mC", [NSB, MH, 128, 128])
    st_mn = din("st_mn", [NSB, MH, 128])
    st_mm = din("st_mm", [NSB, MH])
    st_rshift = din("st_rshift", [NSB, RCOLS])
    st_rS = din("st_rS", [NSB * RH, RN * RN])
    st_fconv = din("st_fconv", [NSB * 2, DFF])
    d_w_in = din("w_in", [128, 8, N_IN])
    d_w_bm = din("w_bm", [128, 4, D]); d_w_br = din("w_br", [128, 4, D])
    d_w_out = din("w_out", [128, 8, D])
    d_f_up = din("f_up", [128, 8, 2 * DFF]); d_f_down = din("f_down", [128, NFC, D])
    d_pgw = din("ple_gate_w", [128, 8, D]); d_ppj = din("ple_proj", [128, 2, D])
    d_rw2 = din("r_w2", [64, RW]); d_ra2 = din("r_a2", [128, RW]); d_rg2 = din("r_g2", [128, RW])
    d_g1 = din("norm1_g", [128, D]); d_g2 = din("norm2_g", [128, D])
    d_g3 = din("ple_norm_g", [128, D]); d_g4 = din("final_norm_g", [128, D])
    d_ptab = din("ptab", [128, 128])
    d_fctab = din("fctab", [128, 4 * NFC])
    d_gb = din("gate_bias", [4, 2])
    y_p = dout("y_p", [SEQ, D]); y_s = dout("y_s", [128, D])
    o_pconv = dout("p_conv", [3, 2 * MW]); o_pC = dout("p_C", [MH, 128, 128]); o_pn = dout("p_n", [MH, 128])
    o_pm = dout("p_m", [1, MH]); o_pshift = dout("p_shift", [1, RCOLS]); o_pS = dout("p_S", [RH * RN, RN])
    o_pfconv = dout("p_fconv", [2, DFF])
    o_sconv = dout("s_conv", [NSB * 3, 2 * MW]); o_sC = dout("s_C", [NSB, MH, 128, 128]); o_sn = dout("s_n", [NSB, MH, 128])
    o_sm = dout("s_m", [NSB, MH]); o_sshift = dout("s_shift", [NSB, RCOLS]); o_sS = dout("s_S", [NSB * RH, RN * RN])
    o_sfconv = dout("s_fconv", [NSB * 2, DFF])
    x1s = k.dram("x1_scratch", [NT * 128, D], F32)
    dbgs = {}

    def dbg_out(name, src_buf, src_ap, shape):
        if not dbg:
            return
        t = dout("dbg_" + name, shape)
        dbgs[name] = t
        k.dma("sp", t[:], src_ap, reads=[src_buf], writes=[t])

    identf = k.sbuf([128, 128], F32, "identf")
    identb = k.sbuf([128, 128], BF16, "identb")
    mark_phase = k.bot
    mU_in = [k.sbuf([128, 128], F32, f"mUin{i}") for i in range(2)]
    mU_st = [k.sbuf([128, 128], F32, f"mUst{i}") for i in range(2)]
    mL_st = [k.sbuf([128, 128], F32, f"mLst{i}") for i in range(2)]
    resets = [k.sbuf([128, 512], F32, f"resets{i}") for i in range(2)]
    ones4 = k.sbuf([4, 128], F32, "ones4")

    def aff(out_buf, out_ap, pattern, cm, base, op=ALU.is_ge):
        k.op("pool", lambda e: e.affine_select(out=out_ap, in_=out_ap, pattern=pattern, compare_op=op,
                                               fill=0.0, base=base, channel_multiplier=cm),
             reads=[out_buf], writes=[out_buf])

    k.memset("pool", (identf, identf[:]), 1.0)
    aff(identf, identf[:], [[-1, 128]], 1, 0)
    aff(identf, identf[:], [[1, 128]], -1, 0)
    k.copy("pool", (identb, identb[:]), (identf, identf[:]))
    for i in range(2):
        k.memset("pool", (mU_in[i], mU_in[i][:]), 1.0)
        aff(mU_in[i], mU_in[i][:], [[1, 128]], -1, 0)
        k.memset("pool", (mU_st[i], mU_st[i][:]), 1.0)
        aff(mU_st[i], mU_st[i][:], [[1, 128]], -1, -1)
        k.memset("pool", (mL_st[i], mL_st[i][:]), 1.0)
        aff(mL_st[i], mL_st[i][:], [[-1, 128]], 1, -1)
        k.memset("pool", (resets[i], resets[i][:]), 1.0)
    v3 = lambda b: b[:].rearrange("p (a c) -> p a c", c=ST)
    aff(mU_in[1], v3(mU_in[1]), [[-ST, 16], [0, ST]], 1, 0)
    aff(mU_st[1], v3(mU_st[1]), [[-ST, 16], [0, ST]], 1, 0)
    aff(mL_st[1], v3(mL_st[1]), [[ST, 16], [0, ST]], -1, ST - 1)
    k.memset("pool", (resets[0], resets[0][:].rearrange("p (a c) -> p a c", c=128)[:, :, 0:1]), 0.0)
    k.memset("pool", (resets[1], resets[1][:].rearrange("p (a c) -> p a c", c=ST)[:, :, 0:1]), 0.0)
    k.memset("pool", (ones4, ones4[:]), 1.0)
    mask2 = [k.sbuf([128, 256], F32, f"mask2_{i}") for i in range(2)]
    for i in range(2):
        k.copy("pool", (mask2[i], mask2[i][:, 0:128]), (mU_st[i], mU_st[i][:]))
        k.copy("pool", (mask2[i], mask2[i][:, 128:256]), (mU_in[i], mU_in[i][:]))
    I2 = k.sbuf([128, 64], F32, "I2")
    k.tt("pool", (I2, I2[:]), (identf, identf[:, 0:64]), (identf, identf[:, 64:128]), ALU.add)
    bones = k.sbuf([128, 128], F32, "bones")
    k.memset("pool", (bones, bones[:]), 0.0)
    k.memset("pool", (bones, bones[0:64, 0:64]), 1.0)
    k.memset("pool", (bones, bones[64:128, 64:128]), 1.0)

    ptab = k.sbuf([128, 128], F32, "ptab")
    k.dma("sp", ptab[:], d_ptab[:], reads=[d_ptab], writes=[ptab])
    PT_MCW, PT_MCB, PT_RMIX, PT_RW0, PT_RA0, PT_RKK, PT_RKA, PT_RRK, PT_RLNG, PT_RLNB, PT_MNG = 0, 32, 40, 54, 58, 62, 66, 70, 74, 78, 82
    pcol = lambda c: (ptab, ptab[:, c:c + 1])
    gb = k.sbuf([4, 2], F32, "gb")
    k.dma("sp", gb[:], d_gb[:], reads=[d_gb], writes=[gb])
    nbf = k.sbuf([4, 1], F32, "nbf")
    k.ts("dve", (nbf, nbf[:]), (gb, gb[:, 1:2]), -1.0, None, op0=ALU.mult)
    g1bc = k.sbuf([128, D], F32, "g1bc")
    k.dma("sp", g1bc[:], d_g1[:], reads=[d_g1], writes=[g1bc])

    NA = C_G
    hmT_all = k.sbuf([128, NT, 4, 128], BF16, "hmT_all")
    yrgT_all = k.sbuf([128, NT, 4, 128], BF16, "yrgT_all")
    mark_1a = k.bot
    W_in = k.sbuf([128, 8, NA], BF16, "W_in")
    Wl_w2 = k.sbuf([64, RW], BF16, "Wl_w2")
    Wl_a2 = k.sbuf([128, RW], BF16, "Wl_a2")
    Wl_g2 = k.sbuf([128, RW], BF16, "Wl_g2")
    GRP = {"g0": (0, 1024), "g1": (1024, 2056), "g2": (2056, 3848)}
    for g in ("g0", "g1", "g2"):
        a, b = GRP[g]
        for kh in range(2):
            k.dma("pool", W_in[:, 4 * kh:4 * kh + 4, a:b], d_w_in[:, 4 * kh:4 * kh + 4, a:b], reads=[d_w_in], writes=[(W_in, g)])
        if g == "g1":
            k.dma("pool", Wl_w2[:], d_rw2[:], reads=[d_rw2], writes=[Wl_w2])
            k.dma("pool", Wl_a2[:], d_ra2[:], reads=[d_ra2], writes=[Wl_a2])
            k.dma("pool", Wl_g2[:], d_rg2[:], reads=[d_rg2], writes=[Wl_g2])

    def wgrp(col):
        for g, (a, b) in GRP.items():
            if a <= col < b:
                return g

    pF = [k.psum([128, 512], F32, f"pF{i}") for i in range(2)]
    pR = [k.psum([128, 512], F32, f"pR{i}") for i in range(2)]
    pT = k.psum([128, 1024], BF16, "pT")
    pM = [k.psum([128, 512], F32, f"pM{i}") for i in range(3)]

    xt = [k.sbuf([128, D], F32, "xt0")] * 2
    hb = k.sbuf([128, D], BF16, "hb")
    hT = k.sbuf([128, 8, 128], BF16, "hT")
    ss = k.sbuf([128, 1], F32, "ss")
    rs = k.sbuf([128, 1], F32, "rs")
    ext_q = k.sbuf([128, 8, 131], F32, "ext_q")
    cq = k.sbuf([128, 8, 3], F32, "cq")
    _eqf = ext_q[:].rearrange("p a b -> p (a b)")
    cv = k.sbuf([128, 8, 128], F32, "cv")
    qkT = k.sbuf([128, 8, 128], BF16, "qkT")
    soT = k.sbuf([128, 4, 128], F32, "soT")
    vaug = k.sbuf([128, 4, 130], BF16, "vaug")
    Cst = k.sbuf([128, 4, 129], F32, "Cst")
    Cb = k.sbuf([128, 4, 130], BF16, "Cb")
    gsm = [k.sbuf([4, 128], F32, f"gsm{i}") for i in range(8)]
    gpk = k.sbuf([4, 3, 128], F32, "gpk")
    mst = k.sbuf([4, 16], F32, "mst")
    mnew = k.sbuf([4, 16], F32, "mnew")
    gt = [k.sbuf([4, 16], F32, f"gt{i}") for i in range(4)]
    s0d = k.sbuf([4, 4, 16], F32, "s0d")
    tokS = k.sbuf([128, 12], F32, "tokS")
    s0bc = k.sbuf([128, 64], F32, "s0bc")
    _pk = _eqf[:, 512:1024].bitcast(BF16).rearrange("p (a b c) -> p a b c", a=2, b=4)
    PTm = ext_q.view(_pk[:, 0, :, :], "PTm")
    ktm = ext_q.view(_pk[:, 1, :, :], "ktm")
    dn = k.sbuf([128, 4], F32, "dn")
    hm = ext_q.view(_eqf[:, 0:512].rearrange("p (a b) -> p a b", a=4), "hm")
    hn = hm
    bst = k.sbuf([128, 4, 6], F32, "bst")
    bag = k.sbuf([128, 4, 2], F32, "bag")
    zq_tm = cv.view(cv[:].rearrange("p a b -> p (a b)"), "zq_tm")

    ext_r = k.sbuf([128, 14, 129], F32, "ext_r")
    cr = k.sbuf([128, 14, 1], F32, "cr")
    _erf = ext_r[:].rearrange("p a b -> p (a b)")
    xm = k.sbuf([128, 14, 128], F32, "xm")
    thad = k.sbuf([128, 128], BF16, "thad")
    sgd = k.sbuf([128, 128], BF16, "sgd")
    bst8 = k.sbuf([128, 8, 6], F32, "bst8")
    bag8 = k.sbuf([128, 8, 2], F32, "bag8")
    mark_rw = k.bot
    rt = [k.sbuf([128, 4, 128], F32, f"rt{i}") for i in range(7)]
    rt.append(cv.view(cv[:, 0:4, :], "rt7"))
    rt.append(cv.view(cv[:, 4:8, :], "rt8"))
    gTs = ext_r.view(_erf[:, 0:512].rearrange("p (a b) -> p a b", a=4), "gTs")
    bonT = ext_r.view(_erf[:, 512:1024].rearrange("p (a b) -> p a b", a=4), "bonT")
    ART = k.sbuf([128, 4, 2, 128], BF16, "ART")
    BTb = k.sbuf([128, 4, 128], BF16, "BTb")
    KTb = k.sbuf([128, 4, 128], BF16, "KTb")
    VTb = k.sbuf([128, 4, 128], BF16, "VTb")
    AB_tm = k.sbuf([128, 2, 512], BF16, "AB_tm")
    KV_tm = k.sbuf([128, 2, 512], BF16, "KV_tm")
    GBm = k.sbuf([128, 4, 256], BF16, "GBm")
    GKm = k.sbuf([128, 4, 256], BF16, "GKm")
    Nn = k.sbuf([128, 4, 128], BF16, "Nn")
    GBm_b = k.sbuf([128, 4, 256], BF16, "GBm_b")
    GKm_b = k.sbuf([128, 4, 256], BF16, "GKm_b")
    Nn_b = k.sbuf([128, 4, 128], BF16, "Nn_b")
    PP_b = [k.sbuf([128, 4, 256], BF16, f"PPb{i}") for i in range(2)]
    XX_b = [k.sbuf([128, 4, 128], BF16, f"XXb{i}") for i in range(2)]
    PP = [k.sbuf([128, 4, 256], BF16, f"PP{i}") for i in range(2)]
    XX = [k.sbuf([128, 4, 128], BF16, f"XX{i}") for i in range(2)]
    QT = k.sbuf([128, 2, 128], BF16, "QT")
    IE = k.sbuf([128, 4, 64], F32, "IE")
    STf = k.sbuf([128, 4, 64], F32, "STf")
    STb = k.sbuf([128, 4, 64], BF16, "STb")
    yn = ext_r.view(_erf[:, 1024:1536].rearrange("p (a b) -> p a b", a=8), "yn")
    k.memset("pool", (STf, STf[:]), 0.0)
    k.memset("pool", (STb, STb[:]), 0.0)
    k.memset("pool", (cr, cr[:]), 0.0)
    k.memset("pool", (vaug, vaug[:]), 1.0)
    k.memset("pool", (Cst, Cst[:]), 0.0)
    k.memset("pool", (Cb, Cb[:]), 0.0)
    k.memset("pool", (mst, mst[:]), 0.0)
    k.memset("pool", (cq, cq[:]), 0.0)
    LNK = math.log(KSCALE)

    def x_rows(ti):
        if ti < NTP:
            return xp, xp[ti * 128:(ti + 1) * 128, :]
        return xs, xs[:, :]

    def norm_to_hT(xbuf, gbc):
        k.act((hb, hb[:]), (xbuf, xbuf[:]), AF.Square, accum=(ss, ss[:]))
        k.ts("dve", (rs, rs[:]), (ss, ss[:]), 1.0 / D, EPS, op0=ALU.mult, op1=ALU.add)
        k.act((rs, rs[:]), (rs, rs[:]), AF.Ln)
        k.act((rs, rs[:]), (rs, rs[:]), AF.Exp, scale=-0.5)
        k.stt("dve", (hb, hb[:]), (xbuf, xbuf[:]), (rs, rs[:, 0:1]), (gbc, gbc[:]), ALU.mult, ALU.mult)
        for kc in range(8):
            k.tr((pT, pT[:, kc * 128:(kc + 1) * 128]), (hb, hb[:, kc * 128:(kc + 1) * 128]), (identb, identb[:]))
        k.copy("act", (hT, hT[:].rearrange("p a b -> p (a b)")), (pT, pT[:, :]))

    def proj_fm(ps, ps_ap, col, M=128):
        g = wgrp(col)
        for kc in range(8):
            k.mm((ps, ps_ap), (W_in, W_in[:, kc, col:col + M], g), (hT, hT[:, kc, :]), start=(kc == 0), stop=(kc == 7))

    def proj_tm(ps, ps_ap, col, N):
        g = wgrp(col)
        for kc in range(8):
            k.mm((ps, ps_ap), (hT, hT[:, kc, :]), (W_in, W_in[:, kc, col:col + N], g), start=(kc == 0), stop=(kc == 7))

    prefetched = set()

    def prefetch_gen(tn):
        xb = xt[tn % 2]
        xd, xap = x_rows(tn)
        k.dma("sp", xb[:], xap, reads=[xd], writes=[xb])
        norm_to_hT(xb, g1bc)
        yield
        k.copy("pool", (ext_q, ext_q[:, :, 0:3]), (cq, cq[:]))
        for g in range(2):
            for c in range(4):
                proj_fm(pM[2], pM[2][:, c * 128:(c + 1) * 128], C_QK + (4 * g + c) * 128)
                if c == 1:
                    yield
            k.copy("act", (ext_q, ext_q[:, 4 * g:4 * g + 4, 3:131]), (pM[2], pM[2][:].rearrange("p (c t) -> p c t", c=4)))
            yield
        k.copy("pool", (cq, cq[:]), (ext_q, ext_q[:, :, 128:131]))
        yield
        proj_tm(pM[2], pM[2][:, :], C_V, 512)
        k.copy("act", (vaug, vaug[:, :, 0:128]), (pM[2], pM[2][:].rearrange("p (h c) -> p h c", h=4)))
        yield
        for c in range(4):
            proj_fm(pM[2], pM[2][:, c * 128:(c + 1) * 128], C_O + c * 128)
            if c == 1:
                yield
        k.act((soT, soT[:].rearrange("p a b -> p (a b)")), (pM[2], pM[2][:, :]), AF.Sigmoid)

    def mixer_tile(ti):
        smp = ti == NTP
        mi = 1 if smp else 0
        NB = NSB if smp else 1
        LB = ST if smp else 128
        xb = xt[ti % 2]
        xd, xap = x_rows(ti)
        if smp:
            k.barrier()
            k.bot = mark_rw
            Cs = k.sbuf([128, NSB, 129], F32, "Cs")
            Csb = k.sbuf([128, NSB, 130], BF16, "Csb")
            qTm = k.sbuf([128, NSB, 128], BF16, "qTm")
            ktmb = k.sbuf([128, NSB, 128], BF16, "ktmb")
            blkF = k.sbuf([128, NSB, 128], BF16, "blkF")
            rowm = k.sbuf([128, NSB], F32, "rowm")
            k.memset("pool", (blkF, blkF[:]), 1.0)
            aff(blkF, blkF[:], [[-ST, NSB], [1, 128]], 0, 0)
            aff(blkF, blkF[:], [[ST, NSB], [-1, 128]], 0, ST - 1)
            k.memset("pool", (rowm, rowm[:]), 1.0)
            aff(rowm, rowm[:], [[-ST, NSB]], 1, 0)
            aff(rowm, rowm[:], [[ST, NSB]], -1, ST - 1)
            smc = cv.view(cv[:].rearrange("p a b -> p (a b)")[0:NSB * 3, :], "smc")
            ext_s = xm.view(xm[:].rearrange("p a b -> p (a b)")[:, 0:8 * NSB * 11].rearrange("p (c b t) -> p c b t", c=8, b=NSB), "ext_s")
            k.dma("sp", smc[:], st_mconv[:, :], reads=[st_mconv], writes=[smc])
            for c in range(8):
                k.tr((pM[0], pM[0][:, c * 48:(c + 1) * 48]), (smc, smc[:, c * 128:(c + 1) * 128]), (identf, identf[0:48, 0:48]))
            k.copy("act", (ext_s, ext_s[:, :, :, 0:3]), (pM[0], pM[0][:, 0:384].rearrange("p (c b j) -> p c b j", c=8, b=NSB)))
            k.dma("sp", mst[:, 0:NSB], st_mm[:, :].rearrange("b h -> h b"), reads=[st_mm], writes=[mst], allow_slow_non_contiguous=True)
        if ti not in prefetched:
            k.dma("sp", xb[:], xap, reads=[xd], writes=[xb])
            norm_to_hT(xb, g1bc)

        if smp:
            for g in range(2):
                for c in range(4):
                    proj_fm(pF[g], pF[g][:, c * 128:(c + 1) * 128], C_QK + (4 * g + c) * 128)
                k.copy("act", (ext_s, ext_s[:, 4 * g:4 * g + 4, :, 3:11]), (pF[g], pF[g][:].rearrange("p (c b t) -> p c b t", c=4, b=NSB)))
            for c in range(8):
                cvv = cv[:, c, :].rearrange("p (b t) -> p b t", t=ST)
                k.ts("dve", (cv, cvv), (ext_s, ext_s[:, c, :, 3:11]), pcol(PT_MCW + 3 * 8 + c), pcol(PT_MCB + c),
                     op0=ALU.mult, op1=ALU.add)
                for j in range(3):
                    k.stt("dve", (cv, cvv), (ext_s, ext_s[:, c, :, j:j + ST]), pcol(PT_MCW + j * 8 + c), (cv, cvv),
                          ALU.mult, ALU.add)
        if not smp:
            if ti not in prefetched:
                k.copy("pool", (ext_q, ext_q[:, :, 0:3]), (cq, cq[:]))
                for g in range(2):
                    for c in range(4):
                        proj_fm(pF[g], pF[g][:, c * 128:(c + 1) * 128], C_QK + (4 * g + c) * 128)
                    k.copy("act", (ext_q, ext_q[:, 4 * g:4 * g + 4, 3:131]), (pF[g], pF[g][:].rearrange("p (c t) -> p c t", c=4)))
                k.copy("pool", (cq, cq[:]), (ext_q, ext_q[:, :, 128:131]))
            for c in range(8):
                k.ts("dve", (cv, cv[:, c, :]), (ext_q, ext_q[:, c, 3:131]), pcol(PT_MCW + 3 * 8 + c), pcol(PT_MCB + c),
                     op0=ALU.mult, op1=ALU.add)
                for j in range(3):
                    k.stt("dve", (cv, cv[:, c, :]), (ext_q, ext_q[:, c, j:j + 128]), pcol(PT_MCW + j * 8 + c), (cv, cv[:, c, :]),
                          ALU.mult, ALU.add)
        k.act((qkT, qkT[:].rearrange("p a b -> p (a b)")), (cv, cv[:].rearrange("p a b -> p (a b)")), AF.Silu)

        if ti not in prefetched:
            proj_tm(pR[0], pR[0][:, :], C_V, 512)
            k.copy("act", (vaug, vaug[:, :, 0:128]), (pR[0], pR[0][:].rearrange("p (h c) -> p h c", h=4)))
            for c in range(4):
                proj_fm(pF[0], pF[0][:, c * 128:(c + 1) * 128], C_O + c * 128)
            k.act((soT, soT[:].rearrange("p a b -> p (a b)")), (pF[0], pF[0][:, :]), AF.Sigmoid)
        if not smp and stage >= 2:
            rwkv_front_proj(ti)
        proj_fm(pM[0], pM[0][0:4, 0:128], C_I, M=4)
        proj_fm(pM[0], pM[0][0:4, 128:256], C_F, M=4)
        liT, nlf, ncum, gT_, t0, t1 = gsm[0], gsm[1], gsm[2], gsm[3], gsm[4], gsm[5]
        k.ts("dve", (liT, liT[:]), (pM[0], pM[0][0:4, 0:128]), (gb, gb[:, 0:1]), None, op0=ALU.add)
        k.act((t0, t0[:]), (pM[0], pM[0][0:4, 128:256]), AF.Exp, bias=(nbf, nbf[:, 0:1]), scale=-1.0)
        k.act((nlf, nlf[:]), (t0, t0[:]), AF.Ln, bias=1.0)
        k.scan("dve", (ncum, ncum[:]), (resets[mi], resets[mi][0:4, 0:128]), (nlf, nlf[:]), 0.0, ALU.mult, ALU.add)
        k.tt("dve", (gT_, gT_[:]), (liT, liT[:]), (ncum, ncum[:]), ALU.add)
        b3 = lambda buf: buf[:].rearrange("p (b l) -> p b l", l=LB)
        mcb_ = mst[:, 0:NB].unsqueeze(2).to_broadcast([4, NB, LB])
        nlast = ncum[:].rearrange("p (b l) -> p b l", l=LB)[:, :, LB - 1:LB]
        k.stt("dve", (t0, b3(t0)), (gT_, b3(gT_)), LNK, (mst, mcb_), ALU.add, ALU.subtract)
        k.act((gpk, gpk[:, 0, :]), (t0, t0[:]), AF.Exp)
        k.tt("dve", (t1, b3(t1)), (ncum, b3(ncum)), (mst, mcb_), ALU.subtract)
        k.act((gpk, gpk[:, 1, :]), (t1, t1[:]), AF.Exp)
        k.tt("dve", (t1, b3(t1)), (gT_, b3(gT_)), (ncum, nlast.to_broadcast([4, NB, LB])), ALU.subtract)
        k.red("dve", (gt[0], gt[0][:, 0:NB]), (t1, b3(t1)), ALU.max)
        k.tt("dve", (gt[1], gt[1][:, 0:NB]), (mst, mst[:, 0:NB]), (ncum, nlast.rearrange("p b o -> p (b o)")), ALU.subtract)
        k.tt("dve", (mnew, mnew[:, 0:NB]), (gt[1], gt[1][:, 0:NB]), (gt[0], gt[0][:, 0:NB]), ALU.max)
        k.tt("dve", (gt[2], gt[2][:, 0:NB]), (gt[1], gt[1][:, 0:NB]), (mnew, mnew[:, 0:NB]), ALU.subtract)
        k.act((gt[3], gt[3][:, 0:NB]), (gt[2], gt[2][:, 0:NB]), AF.Exp)
        k.tt("dve", (gpk, gpk[:, 2, :].rearrange("p (b l) -> p b l", l=LB)), (gpk, gpk[:, 0, :].rearrange("p (b l) -> p b l", l=LB)),
             (gt[3], gt[3][:, 0:NB].unsqueeze(2).to_broadcast([4, NB, LB])), ALU.mult)
        for j in range(3):
            k.tr((pM[1], pM[1][:, 4 * j:4 * j + 4]), (gpk, gpk[:, j, :]), (identf, identf[0:4, 0:4]))
        k.copy("dve", (tokS, tokS[:]), (pM[1], pM[1][:, 0:12]))
        k.tt("dve", (s0d, s0d[:, :, 0:NB]), (identf, identf[0:4, 0:4].unsqueeze(2).to_broadcast([4, 4, NB])),
             (gt[3], gt[3][:, 0:NB].unsqueeze(1).to_broadcast([4, 4, NB])), ALU.mult)
        k.mm((pM[1], pM[1][:, 16:16 + 4 * NB]), (ones4, ones4[:]), (s0d, s0d[:, :, 0:NB].rearrange("p a b -> p (a b)")))
        k.copy("dve", (s0bc, s0bc[:, 0:4 * NB]), (pM[1], pM[1][:, 16:16 + 4 * NB]))

        for h in range(4):
            k.mm((pM[0], pM[0][:, h * 128:(h + 1) * 128]), (qkT, qkT[:, 4 + h, :]), (qkT, qkT[:, h, :]))
        for h in range(4):
            k.stt("dve", (PTm, PTm[:, h, :]), (pM[0], pM[0][:, h * 128:(h + 1) * 128]), (tokS, tokS[:, h:h + 1]),
                  (mU_in[mi], mU_in[mi][:]), ALU.mult, ALU.mult)
        pO = [pM[1], pM[2]]
        oap = lambda h: pO[h // 2][:, 256 * (h % 2):256 * (h % 2) + 129]
        if not smp:
            for h in range(4):
                k.mm((pO[h // 2], oap(h)), (qkT, qkT[:, h, :]), (Cb, Cb[:, h, 0:129]), start=True, stop=False)
                k.mm((pO[h // 2], oap(h)), (PTm, PTm[:, h, :]), (vaug, vaug[:, h, 0:129]), start=False, stop=True)
        else:
            for h in range(4):
                k.tr((pT, pT[:, h * 128:(h + 1) * 128]), (qkT, qkT[:, 4 + h, :]), (identb, identb[:]))
            for h in range(4):
                k.ts("dve", (ktm, ktm[:, h, :]), (pT, pT[:, h * 128:(h + 1) * 128]), (tokS, tokS[:, 8 + h:9 + h]), None, op0=ALU.mult)
            for h in range(4):
                k.dma("sp", Cs[:, :, 0:128], st_mC[:, h, :, :].rearrange("b d v -> d b v"), reads=[st_mC], writes=[Cs])
                k.dma("sp", Cs[:, :, 128], st_mn[:, h, :].rearrange("b d -> d b"), reads=[st_mn], writes=[Cs], allow_slow_non_contiguous=True)
                k.copy("act", (Csb, Csb[:, :, 0:129]), (Cs, Cs[:]))
                k.tt("dve", (qTm, qTm[:]), (qkT, qkT[:, h, :].unsqueeze(1).to_broadcast([128, NSB, 128])), (blkF, blkF[:]), ALU.mult)
                for b in range(NSB):
                    k.mm((pO[h // 2], oap(h)), (qTm, qTm[:, b, :]), (Csb, Csb[:, b, 0:129]), start=(b == 0), stop=False)
                k.mm((pO[h // 2], oap(h)), (PTm, PTm[:, h, :]), (vaug, vaug[:, h, 0:129]), start=False, stop=True)
                k.tt("dve", (ktmb, ktmb[:]), (ktm, ktm[:, h, :].unsqueeze(1).to_broadcast([128, NSB, 128])),
                     (rowm, rowm[:].unsqueeze(2).to_broadcast([128, NSB, 128])), ALU.mult)
                for grp in range(4):
                    bank = pF[grp % 2]
                    for bi in range(4):
                        b = 4 * grp + bi
                        k.mm((bank, bank[:, bi * 128:(bi + 1) * 128]), (ktmb, ktmb[:, b, :]), (vaug, vaug[:, h, 0:128]))
                    for bi in range(4):
                        b = 4 * grp + bi
                        k.stt("dve", (Cs, Cs[:, b, 0:128]), (Cs, Cs[:, b, 0:128]), (s0bc, s0bc[:, h * NSB + b:h * NSB + b + 1]),
                              (bank, bank[:, bi * 128:(bi + 1) * 128]), ALU.mult, ALU.add)
                for b in range(NSB):
                    k.mm((pR[0], pR[0][:, b:b + 1]), (ktmb, ktmb[:, b, :]), (vaug, vaug[:, h, 128:129]))
                k.tt("dve", (Cs, Cs[:, :, 128]), (Cs, Cs[:, :, 128]), (s0bc, s0bc[:, h * NSB:(h + 1) * NSB]), ALU.mult)
                k.tt("dve", (Cs, Cs[:, :, 128]), (Cs, Cs[:, :, 128]), (pR[0], pR[0][:, 0:NSB]), ALU.add)
                k.dma("sp", o_sC[:, h, :, :].rearrange("b d v -> d b v"), Cs[:, :, 0:128], reads=[Cs], writes=[o_sC])
                k.dma("sp", o_sn[:, h, :].rearrange("b d -> d b"), Cs[:, :, 128], reads=[Cs], writes=[o_sn], allow_slow_non_contiguous=True)
        for h in range(4):
            k.copy("act", (dn, dn[:, h:h + 1]), (pO[h // 2], oap(h)[:, 128:129]))
        k.stt("dve", (dn, dn[:]), (dn, dn[:]), -1.0, (dn, dn[:]), ALU.mult, ALU.max)
        k.tt("dve", (dn, dn[:]), (dn, dn[:]), (tokS, tokS[:, 4:8]), ALU.max)
        k.op("dve", lambda e: e.reciprocal(out=dn[:], in_=dn[:]), reads=[dn], writes=[dn])
        for h in range(4):
            k.act((hm, hm[:, h, :]), (pO[h // 2], oap(h)[:, 0:128]), AF.Copy, scale=(dn, dn[:, h:h + 1]))
        for h in range(4):
            k.op("dve", lambda e, h=h: e.bn_stats(out=bst[:, h, :], in_=hm[:, h, :]), reads=[hm], writes=[(bst, h)])
        for h in range(4):
            k.op("dve", lambda e, h=h: e.bn_aggr(out=bag[:, h, :], in_=bst[:, h, :]), reads=[(bst, h)], writes=[(bag, h)])
        k.act((bag, bag[:, :, 1:2]), (bag, bag[:, :, 1:2]), AF.Ln, bias=EPS)
        k.act((bag, bag[:, :, 1:2]), (bag, bag[:, :, 1:2]), AF.Exp, scale=-0.5)
        for h in range(4):
            k.ts("dve", (hn, hn[:, h, :]), (hm, hm[:, h, :]), (bag, bag[:, h, 0:1]), (bag, bag[:, h, 1:2]),
                 op0=ALU.subtract, op1=ALU.mult)
        for h in range(4):
            k.tr((pM[0], pM[0][:, h * 128:(h + 1) * 128]), (hn, hn[:, h, :]), (identf, identf[:]))
        for h in range(4):
            k.stt("dve", (hmT_all, hmT_all[:, ti, h, :], ti), (pM[0], pM[0][:, h * 128:(h + 1) * 128]), pcol(PT_MNG + h),
                  (soT, soT[:, h, :]), ALU.mult, ALU.mult)
        if not smp:
            for h in range(4):
                k.tr((pT, pT[:, h * 128:(h + 1) * 128]), (qkT, qkT[:, 4 + h, :]), (identb, identb[:]))
            for h in range(4):
                k.ts("dve", (ktm, ktm[:, h, :]), (pT, pT[:, h * 128:(h + 1) * 128]), (tokS, tokS[:, 8 + h:9 + h]), None, op0=ALU.mult)
            for h in range(4):
                k.mm((pO[h // 2], oap(h)), (ktm, ktm[:, h, :]), (vaug, vaug[:, h, 0:129]))
            for h in range(4):
                k.stt("dve", (Cst, Cst[:, h, :]), (Cst, Cst[:, h, :]), (s0bc, s0bc[:, h:h + 1]), (pO[h // 2], oap(h)),
                      ALU.mult, ALU.add)
            k.copy("act", (Cb, Cb[:, :, 0:129]), (Cst, Cst[:]))
            k.copy("dve", (mst, mst[:, 0:1]), (mnew, mnew[:, 0:1]))
        if ti == NTP - 1:
            for h in range(4):
                k.dma("sp", o_pC[h], Cst[:, h, 0:128], reads=[Cst], writes=[o_pC])
            k.dma("sp", o_pn[:].rearrange("h d -> d h"), Cst[:, :, 128], reads=[Cst], writes=[o_pn], allow_slow_non_contiguous=True)
            k.dma("sp", o_pm[:].rearrange("o h -> h o"), mnew[:, 0:1], reads=[mnew], writes=[o_pm], allow_slow_non_contiguous=True)
            for blk in range(2):
                proj_tm(pR[blk], pR[blk][:, :], C_QK + blk * 512, 512)
                k.copy("act", (zq_tm, zq_tm[:, blk * 512:(blk + 1) * 512]), (pR[blk], pR[blk][:, :]))
            k.dma("sp", o_pconv[:], zq_tm[125:128, :], reads=[zq_tm], writes=[o_pconv])
        if smp:
            k.dma("sp", o_sm[:, :].rearrange("b h -> h b"), mnew[:, 0:NSB], reads=[mnew], writes=[o_sm], allow_slow_non_contiguous=True)
            for blk in range(2):
                proj_tm(pR[blk], pR[blk][:, :], C_QK + blk * 512, 512)
                k.copy("act", (zq_tm, zq_tm[:, blk * 512:(blk + 1) * 512]), (pR[blk], pR[blk][:, :]))
            for b in range(NSB):
                k.dma("sp", o_sconv[3 * b:3 * b + 3, :], zq_tm[ST * b + 5:ST * b + 8, :], reads=[zq_tm], writes=[o_sconv])

    k.fence_mm = (pT, identb)
    BK = [pM[0], pM[1], pF[0], pF[1], pR[0], pR[1]]

    _rw_stop = int(_os.environ.get("KDBG_RW", "99"))

    def rwkv_front_proj(ti):
        k.copy("pool", (ext_r, ext_r[:, :, 0:1]), (cr, cr[:]))
        for g in range(4):
            n = min(4, 14 - 4 * g)
            for c in range(n):
                proj_fm(pF[g % 2], pF[g % 2][:, c * 128:(c + 1) * 128], C_R + (4 * g + c) * 128)
            k.copy("act", (ext_r, ext_r[:, 4 * g:4 * g + n, 1:129]),
                   (pF[g % 2], pF[g % 2][:, 0:n * 128].rearrange("p (c t) -> p c t", c=n)))
        k.copy("pool", (cr, cr[:]), (ext_r, ext_r[:, :, 128:129]))
        k.tt("pool", (xm, xm[:]), (ext_r, ext_r[:, :, 0:128]), (ext_r, ext_r[:, :, 1:129]), ALU.subtract)
        for c in range(14):
            k.stt("dve", (xm, xm[:, c, :]), (xm, xm[:, c, :]), pcol(PT_RMIX + c), (ext_r, ext_r[:, c, 1:129]), ALU.mult, ALU.add)

    def rwkv_tile(ti):
        smp = ti == NTP
        mi = 0
        NLV = 7
        rtl = rt
        if not smp:
            pass
        else:
            k.barrier()
            k.bot = mark_rw
            ext_rs = k.sbuf([128, 14, NSB, ST + 1], F32, "ext_rs")
            rtl = [k.sbuf([128, 4, 128], F32, f"rts{i}") for i in range(7)] + [rt[7], rt[8]]
            stg = k.sbuf([128, 512], F32, "stg")
            srs = xm.view(xm[:].rearrange("p a b -> p (a b)")[0:NSB, :], "srs")
            k.dma("sp", srs[:], st_rshift[:, :], reads=[st_rshift], writes=[srs])
            for c in range(14):
                k.tr((pM[0], pM[0][:, c * NSB:(c + 1) * NSB]), (srs, srs[:, c * 128:(c + 1) * 128]), (identf, identf[0:NSB, 0:NSB]))
            k.copy("act", (ext_rs, ext_rs[:, :, :, 0]), (pM[0], pM[0][:, 0:14 * NSB].rearrange("p (c b) -> p c b", c=14)))
            for g in range(4):
                n = min(4, 14 - 4 * g)
                for c in range(n):
                    proj_fm(pF[g % 2], pF[g % 2][:, c * 128:(c + 1) * 128], C_R + (4 * g + c) * 128)
                k.copy("act", (ext_rs, ext_rs[:, 4 * g:4 * g + n, :, 1:ST + 1]),
                       (pF[g % 2], pF[g % 2][:, 0:n * 128].rearrange("p (c b t) -> p c b t", c=n, b=NSB)))
            xm4 = xm[:].rearrange("p c (b t) -> p c b t", t=ST)
            k.tt("pool", (xm, xm4), (ext_rs, ext_rs[:, :, :, 0:ST]), (ext_rs, ext_rs[:, :, :, 1:ST + 1]), ALU.subtract)
            for c in range(14):
                k.stt("dve", (xm, xm4[:, c]), (xm, xm4[:, c]), pcol(PT_RMIX + c), (ext_rs, ext_rs[:, c, :, 1:ST + 1]), ALU.mult, ALU.add)
        rT, krT, vrT = xm[:, 0:4, :], xm[:, 4:8, :], xm[:, 8:12, :]
        sig, cums, gam, ginv, gexc, a_, kk, tmp, kr2 = rtl
        if _rw_stop <= 1:
            return
        k.act((thad, thad[0:64, :]), (xm, xm[0:64, 12, :]), AF.Tanh)
        k.copy("act", (thad, thad[64:128, :]), (xm, xm[64:128, 12, :]))
        k.act((sgd, sgd[:]), (xm, xm[:, 13, :]), AF.Sigmoid)
        for c in range(4):
            k.mm((pM[0], pM[0][:, c * 128:(c + 1) * 128]), (Wl_w2, Wl_w2[0:64, c * 128:(c + 1) * 128]), (thad, thad[0:64, :]))
        for c in range(4):
            k.act((sig, sig[:, c, :]), (pM[0], pM[0][:, c * 128:(c + 1) * 128]), AF.Sigmoid, bias=pcol(PT_RW0 + c))
        k.pe_fence()
        for c in range(4):
            k.mm((pM[1], pM[1][:, c * 128:(c + 1) * 128]), (Wl_a2, Wl_a2[64:128, c * 128:(c + 1) * 128]), (thad, thad[64:128, :]))
        k.pe_fence()
        for c in range(4):
            k.act((a_, a_[:, c, :]), (pM[1], pM[1][:, c * 128:(c + 1) * 128]), AF.Sigmoid, bias=pcol(PT_RA0 + c))
        for c in range(4):
            k.mm((pM[2], pM[2][:, c * 128:(c + 1) * 128]), (Wl_g2, Wl_g2[:, c * 128:(c + 1) * 128]), (sgd, sgd[:]))
        k.copy("act", (gTs, gTs[:].rearrange("p a b -> p (a b)")), (pM[2], pM[2][:, :]))
        if _rw_stop <= 2:
            return
        fl = lambda b: b[:].rearrange("p a b -> p (a b)")
        if not smp:
            k.scan("dve", (cums, fl(cums)), (resets[mi], resets[mi][:]), (sig, fl(sig)), 0.0, ALU.mult, ALU.add)
            k.act((gam, fl(gam)), (cums, fl(cums)), AF.Exp, scale=WSCALE)
            k.act((ginv, fl(ginv)), (cums, fl(cums)), AF.Exp, scale=-WSCALE)
            k.tt("pool", (tmp, tmp[:]), (cums, cums[:]), (sig, sig[:]), ALU.subtract)
            k.act((gexc, fl(gexc)), (tmp, fl(tmp)), AF.Exp, scale=WSCALE)
        else:
            k.act((gam, fl(gam)), (sig, fl(sig)), AF.Exp, scale=WSCALE)
        if _rw_stop <= 3:
            return
        for c in range(4):
            k.ts("dve", (kk, kk[:, c, :]), (xm, xm[:, 4 + c, :]), pcol(PT_RKK + c), None, op0=ALU.mult)
        k.tt("pool", (tmp, tmp[:]), (kk, kk[:]), (kk, kk[:]), ALU.mult)
        for c in range(4):
            k.mm((pM[0], pM[0][:, c * 128:(c + 1) * 128]), (bones, bones[:]), (tmp, tmp[:, c, :]))
        k.ts("dve", (tmp, fl(tmp)), (pM[0], pM[0][:, :]), 1e-24, None, op0=ALU.max)
        k.act((tmp, fl(tmp)), (tmp, fl(tmp)), AF.Ln)
        k.act((tmp, fl(tmp)), (tmp, fl(tmp)), AF.Exp, scale=-0.5)
        k.tt("dve", (kk, kk[:]), (kk, kk[:]), (tmp, tmp[:]), ALU.mult)
        for c in range(4):
            k.ts("dve", (tmp, tmp[:, c, :]), (a_, a_[:, c, :]), -1.0, pcol(PT_RKA + c), op0=ALU.add, op1=ALU.mult)
        k.stt("dve", (kr2, kr2[:]), (tmp, tmp[:]), 1.0, (xm, krT), ALU.add, ALU.mult)
        k.tt("pool", (tmp, tmp[:]), (xm, rT), (kr2, kr2[:]), ALU.mult)
        for c in range(4):
            k.ts("dve", (tmp, tmp[:, c, :]), (tmp, tmp[:, c, :]), pcol(PT_RRK + c), None, op0=ALU.mult)
        for c in range(4):
            k.mm((pM[1], pM[1][:, c * 128:(c + 1) * 128]), (bones, bones[:]), (tmp, tmp[:, c, :]))
        k.tt("dve", (bonT, fl(bonT)), (pM[1], pM[1][:, :]), (xm, vrT.rearrange("p a b -> p (a b)") if False else xm[:, 8:12, :].rearrange("p a b -> p (a b)")), ALU.mult)
        if _rw_stop <= 4:
            return
        if smp:
            rwkv_sample_core(xm, gam, kr2, kk, a_, tmp, stg, gTs, bonT)
            return
        k.stt("dve", (ART, ART[:, :, 0, :]), (kk, kk[:]), -1.0, (gexc, gexc[:]), ALU.mult, ALU.mult)
        k.tt("pool", (ART, ART[:, :, 1, :]), (xm, rT), (gam, gam[:]), ALU.mult)
        k.tt("pool", (tmp, tmp[:]), (kk, kk[:]), (a_, a_[:]), ALU.mult)
        k.tt("dve", (BTb, BTb[:]), (tmp, tmp[:]), (ginv, ginv[:]), ALU.mult)
        k.tt("pool", (KTb, KTb[:]), (kr2, kr2[:]), (ginv, ginv[:]), ALU.mult)
        k.copy("act", (VTb, VTb[:]), (xm, vrT))
        if _rw_stop <= 5:
            return
        for c in range(4):
            k.tr((pT, pT[:, c * 128:(c + 1) * 128]), (ART, ART[:, c, 0, :]), (identb, identb[:]))
            k.tr((pT, pT[:, 512 + c * 128:512 + (c + 1) * 128]), (BTb, BTb[:, c, :]), (identb, identb[:]))
        k.copy("act", (AB_tm, AB_tm[:].rearrange("p a b -> p (a b)")), (pT, pT[:, :]))
        for c in range(4):
            k.tr((pT, pT[:, c * 128:(c + 1) * 128]), (KTb, KTb[:, c, :]), (identb, identb[:]))
            k.tr((pT, pT[:, 512 + c * 128:512 + (c + 1) * 128]), (VTb, VTb[:, c, :]), (identb, identb[:]))
        k.copy("dve", (KV_tm, KV_tm[:].rearrange("p a b -> p (a b)")), (pT, pT[:, :]))
        if _rw_stop <= 6:
            return
        A_tm = lambda h: (AB_tm, AB_tm[:, 0, h * 64:(h + 1) * 64])
        B_tm = lambda h: (AB_tm, AB_tm[:, 1, h * 64:(h + 1) * 64])
        K_tm = lambda h: (KV_tm, KV_tm[:, 0, h * 64:(h + 1) * 64])
        V_tm = lambda h: (KV_tm, KV_tm[:, 1, h * 64:(h + 1) * 64])
        m2b = mask2[mi][:].unsqueeze(1).to_broadcast([128, 2, 256])
        GB2, GK2, Nn2, PP2, XX2 = [GBm, GBm_b], [GKm, GKm_b], [Nn, Nn_b], [PP, PP_b], [XX, XX_b]
        LB3 = [[pM[0], pM[1], pR[0]], [pF[0], pF[1], pR[1]]]
        for g in range(2):
            GBm_, GKm_, Nn_, XX_ = GB2[g], GK2[g], Nn2[g], XX2[g]
            heads = [4 * g + i for i in range(4)]
            HO = [(pbs, [(i, h) for i, h in enumerate(heads) if 64 * (h % 2) == pbs]) for pbs in (0, 64)]
            for pbs, hl in HO:
                for i, h in hl:
                    c, pb = h // 2, 64 * (h % 2)
                    off = (i % 2) * 256
                    rAR = (ART, ART[pb:pb + 64, c, :, :].rearrange("p a t -> p (a t)"))
                    k.mm((BK[i // 2], BK[i // 2][:, off:off + 256]), (BTb, BTb[pb:pb + 64, c, :]), rAR)
                    k.mm((BK[2 + i // 2], BK[2 + i // 2][:, off:off + 256]), (KTb, KTb[pb:pb + 64, c, :]), rAR)
                    k.mm((BK[4], BK[4][:, i * 128:(i + 1) * 128]), (ART, ART[pb:pb + 64, c, 0, :]), (BTb, BTb[pb:pb + 64, c, :]))
                k.pe_fence()
            for hf in range(2):
                k.tt("dve", (GBm_, GBm_[:, 2 * hf:2 * hf + 2, :]), (BK[hf], BK[hf][:].rearrange("p (a b) -> p a b", a=2)), (mask2[mi], m2b), ALU.mult)
                k.tt("dve", (GKm_, GKm_[:, 2 * hf:2 * hf + 2, :]), (BK[2 + hf], BK[2 + hf][:].rearrange("p (a b) -> p a b", a=2)), (mask2[mi], m2b), ALU.mult)
            k.tt("dve", (Nn_, Nn_[:]), (BK[4], BK[4][:].rearrange("p (a b) -> p a b", a=4)),
                 (mL_st[mi], mL_st[mi][:].unsqueeze(1).to_broadcast([128, 4, 128])), ALU.mult)
            for i, h in enumerate(heads):
                k.mm((BK[5], BK[5][:, i * 64:(i + 1) * 64]), (GKm_, GKm_[:, i, 0:128]), V_tm(h))
            k.copy("act", (XX_[0], XX_[0][:, :, 64:128]), (BK[5], BK[5][:, 0:256].rearrange("p (a b) -> p a b", a=4)))
            k.copy("pool", (XX_[0], XX_[0][:, :, 0:64]), (AB_tm, AB_tm[:, 0, 256 * g:256 * g + 256].rearrange("p (a b) -> p a b", a=4)))

        xfinal = [None, None]

        def levels_gen(g):
            GBm_, Nn_, PP_, XX_ = GB2[g], Nn2[g], PP2[g], XX2[g]
            bP, bQ, bX = LB3[g]
            Pc = lambda i: (Nn_, Nn_[:, i, :])
            PTc = lambda i: (GBm_, GBm_[:, i, 0:128])
            xi = 0
            for lvl in range(NLV):
                Xc, Xn = XX_[xi], XX_[1 - xi]
                for i in range(4):
                    o = (bX, bX[:, i * 128:(i + 1) * 128])
                    k.mm(o, (identb, identb[:]), (Xc, Xc[:, i, :]), start=True, stop=False)
                    k.mm(o, PTc(i), (Xc, Xc[:, i, :]), start=False, stop=True)
                k.copy("act", (Xn, Xn[:].rearrange("p a b -> p (a b)")), (bX, bX[:, :]))
                xi = 1 - xi
                yield
                if lvl < NLV - 1:
                    bb = [bP, bQ]
                    for i in range(4):
                        off = (i % 2) * 256
                        if lvl < NLV - 2:
                            k.mm((bb[i // 2], bb[i // 2][:, off:off + 128]), PTc(i), Pc(i))
                        k.mm((bb[i // 2], bb[i // 2][:, off + 128:off + 256]), Pc(i), PTc(i))
                    PPn = PP_[lvl % 2]
                    for hf in range(2):
                        if lvl < NLV - 2:
                            k.copy("dve", (PPn, PPn[:, 2 * hf:2 * hf + 2, :]), (bb[hf], bb[hf][:].rearrange("p (a b) -> p a b", a=2)))
                        else:
                            k.copy("dve", (PPn, PPn[:, 2 * hf:2 * hf + 2, 128:256]),
                                   (bb[hf], bb[hf][:].rearrange("p (a b) -> p a b", a=2)[:, :, 128:256]))
                    Pc = lambda i, PPn=PPn: (PPn, PPn[:, i, 0:128])
                    PTc = lambda i, PPn=PPn: (PPn, PPn[:, i, 128:256])
                    yield
            xfinal[g] = XX_[xi]

        gens = [levels_gen(0), levels_gen(1)]
        if ti + 1 < NTP and (ti + 1) in tiles_run:
            gens.append(prefetch_gen(ti + 1))
            prefetched.add(ti + 1)
        while gens:
            for g_ in list(gens):
                try:
                    next(g_)
                except StopIteration:
                    gens.remove(g_)

        for g in range(2):
            heads = [4 * g + i for i in range(4)]
            HO = [(pbs, [(i, h) for i, h in enumerate(heads) if 64 * (h % 2) == pbs]) for pbs in (0, 64)]
            GBt, GKt = GB2[g], GK2[g]
            Xf = xfinal[g]
            if _rw_stop <= 8:
                continue
            k.pe_fence()
            for pbs, hl in HO:
                for i, h in hl:
                    c, pb = h // 2, 64 * (h % 2)
                    ci = i // 2
                    o = (BK[0], BK[0][pb:pb + 64, ci * 128:(ci + 1) * 128])
                    k.mm(o, (Xf, Xf[:, i, 0:64]), (GBt, GBt[:, i, 128:256]), start=True, stop=False)
                    k.pe_fence()
                    k.mm(o, (identb, identb[pb:pb + 64, pb:pb + 64]), (ART, ART[pb:pb + 64, c, 1, :]), start=False, stop=True)
                    k.pe_fence()
            k.copy("act", (QT, QT[:].rearrange("p a b -> p (a b)")), (BK[0], BK[0][:, 0:256]))
            for pbs, hl in HO:
                for i, h in hl:
                    c, pb = h // 2, 64 * (h % 2)
                    ci = i // 2
                    o = (pM[2], pM[2][:, h * 64:(h + 1) * 64])
                    k.mm(o, (QT, QT[pb:pb + 64, ci, :]), (STb, STb[pb:pb + 64, c, :]), start=True, stop=False)
                    k.pe_fence()
                    k.mm(o, (GBt, GBt[:, i, 128:256]), (Xf, Xf[:, i, 64:128]), start=False, stop=False)
                    k.mm(o, (GKt, GKt[:, i, 128:256]), V_tm(h), start=False, stop=True)
                    k.pe_fence()
            if _rw_stop <= 9:
                continue
            for pbs, hl in HO:
                for i, h in hl:
                    c, pb = h // 2, 64 * (h % 2)
                    ci = i // 2
                    k.mm((BK[1], BK[1][pb:pb + 64, ci * 64:(ci + 1) * 64]), (Xf, Xf[:, i, 0:64]), B_tm(h))
                k.pe_fence()
            k.tt("dve", (IE, IE[:, 2 * g:2 * g + 2, :]), (BK[1], BK[1][:, 0:128].rearrange("p (a b) -> p a b", a=2)),
                 (I2, I2[:].unsqueeze(1).to_broadcast([128, 2, 64])), ALU.add)
            for pbs, hl in HO:
                for i, h in hl:
                    c, pb = h // 2, 64 * (h % 2)
                    ci = i // 2
                    o = (BK[2], BK[2][pb:pb + 64, ci * 64:(ci + 1) * 64])
                    k.mm(o, (IE, IE[pb:pb + 64, c, :]), (STf, STf[pb:pb + 64, c, :]), start=True, stop=False)
                    k.pe_fence()
                    k.mm(o, B_tm(h), (Xf, Xf[:, i, 64:128]), start=False, stop=False)
                    k.mm(o, K_tm(h), V_tm(h), start=False, stop=True)
                    k.pe_fence()
            for ci in range(2):
                c = 2 * g + ci
                k.ts("dve", (STf, STf[:, c, :]), (BK[2], BK[2][:, ci * 64:(ci + 1) * 64]), (gam, gam[:, c, 127:128]), None, op0=ALU.mult)
            k.copy("act", (STb, STb[:, 2 * g:2 * g + 2, :]), (STf, STf[:, 2 * g:2 * g + 2, :]))
        if _rw_stop <= 10:
            return
        rwkv_epilogue(ti, pM[2], tmp)
        if ti == NTP - 1:
            for c in range(4):
                k.tr((pM[0], pM[0][0:64, c * 128:(c + 1) * 128]), (STf, STf[:, c, :]), (identf, identf[:]))
            k.copy("act", (rt[0], rt[0][0:64, :, :]), (pM[0], pM[0][0:64, :].rearrange("p (a b) -> p a b", a=4)))
            k.dma("sp", o_pS[:].rearrange("(h i) j -> i h j", h=8), rt[0][0:64, :, :].rearrange("p c (f j) -> p (c f) j", f=2),
                  reads=[rt[0]], writes=[o_pS])

    def rwkv_epilogue(ti, Yb, tmp):
        pM2 = [None, None, Yb]
        for h in range(8):
            k.op("dve", lambda e, h=h: e.bn_stats(out=bst8[:, h, :], in_=Yb[:, h * 64:(h + 1) * 64]), reads=[Yb], writes=[(bst8, h)])
        for h in range(8):
            k.op("dve", lambda e, h=h: e.bn_aggr(out=bag8[:, h, :], in_=bst8[:, h, :]), reads=[(bst8, h)], writes=[(bag8, h)])
        k.act((bag8, bag8[:, :, 1:2]), (bag8, bag8[:, :, 1:2]), AF.Ln, bias=GN_EPS)
        k.act((bag8, bag8[:, :, 1:2]), (bag8, bag8[:, :, 1:2]), AF.Exp, scale=-0.5)
        for h in range(8):
            k.ts("dve", (yn, yn[:, h, :]), (Yb, Yb[:, h * 64:(h + 1) * 64]), (bag8, bag8[:, h, 0:1]), (bag8, bag8[:, h, 1:2]),
                 op0=ALU.subtract, op1=ALU.mult)
        for c in range(4):
            k.tr((pM[0], pM[0][:, c * 128:(c + 1) * 128]), (yn, yn[:, 2 * c:2 * c + 2, :].rearrange("p a b -> p (a b)")), (identf, identf[:]))
        for c in range(4):
            k.ts("dve", (tmp, tmp[:, c, :]), (pM[0], pM[0][:, c * 128:(c + 1) * 128]), pcol(PT_RLNG + c), pcol(PT_RLNB + c),
                 op0=ALU.mult, op1=ALU.add)
        k.tt("pool", (tmp, tmp[:]), (tmp, tmp[:]), (bonT, bonT[:]), ALU.add)
        k.tt("dve", (yrgT_all, yrgT_all[:, ti, :, :], ti), (tmp, tmp[:]), (gTs, gTs[:]), ALU.mult)

    rsc = k.dram("rw_scratch", [6, 128, RW], F32)
    ysc = k.dram("ry_scratch", [128, RW], F32)

    def rwkv_sample_core(xm, dec, kr2, kk, a_, tmp, stg, gTs, bonT):
        ti = NTP
        srcs = []
        srcs.append((xm, lambda c: xm[:, c, :]))
        srcs.append((dec, lambda c: dec[:, c, :]))
        srcs.append((kr2, lambda c: kr2[:, c, :]))
        srcs.append((xm, lambda c: xm[:, 8 + c, :]))
        for q in range(6):
            if q == 4:
                k.ts("dve", (tmp, tmp[:]), (kk, kk[:]), -1.0, None, op0=ALU.mult)
                sb_, fn = tmp, (lambda c: tmp[:, c, :])
            elif q == 5:
                k.tt("dve", (tmp, tmp[:]), (kk, kk[:]), (a_, a_[:]), ALU.mult)
                sb_, fn = tmp, (lambda c: tmp[:, c, :])
            else:
                sb_, fn = srcs[q]
            pb_ = pM[q % 2]
            for c in range(4):
                k.tr((pb_, pb_[:, c * 128:(c + 1) * 128]), (sb_, fn(c)), (identf, identf[:]))
            k.copy("act", (stg, stg[:]), (pb_, pb_[:, :]))
            k.dma("sp", rsc[q], stg[:], reads=[stg], writes=[(rsc, q)])
        for blk, (c0, n) in enumerate(((0, 512), (512, 512), (1024, 512), (1536, 256))):
            proj_tm(pR[blk % 2], pR[blk % 2][:, 0:n], C_R + c0, n)
            k.copy("act", (stg, stg[:, 0:n]), (pR[blk % 2], pR[blk % 2][:, 0:n]))
            for b in range(NSB):
                k.dma("sp", o_sshift[b:b + 1, c0:c0 + n], stg[ST * b + ST - 1:ST * b + ST, 0:n], reads=[stg], writes=[o_sshift])
        k.barrier()
        k.bot = mark_rw
        vec6 = k.sbuf([128, 6, ST, RN], F32, "vec6")
        Ssb = k.sbuf([128, RN, RN], F32, "Ssb")
        tmpS = k.sbuf([128, RN, RN], F32, "tmpS")
        sa = k.sbuf([128, RN], F32, "sa")
        ys = k.sbuf([128, ST, RN], F32, "ys")
        Ytm = k.sbuf([128, RW], F32, "Ytm")
        k.dma("sp", Ssb[:].rearrange("p a b -> p (a b)"), st_rS[:, :], reads=[st_rS], writes=[Ssb])
        for q in range(6):
            for b in range(NSB):
                k.dma("sp", vec6[RH * b:RH * b + RH, q, :, :], rsc[q, ST * b:ST * b + ST, :].rearrange("t (h j) -> h t j", h=RH),
                      reads=[(rsc, q)], writes=[(vec6, q)])
        HV = RN // 2

        def rec_gen(hf):
            i0 = hf * HV
            S_ = (Ssb, Ssb[:, i0:i0 + HV, :], hf)
            T_ = (tmpS, tmpS[:, i0:i0 + HV, :], hf)
            bc = lambda q, t: (vec6, vec6[:, q, t, :].unsqueeze(1).to_broadcast([128, HV, RN]), q)
            for t in range(ST):
                k.tt("dve", T_, S_, bc(4, t), ALU.mult)
                yield
                k.red("dve", (sa, sa[:, i0:i0 + HV], hf), T_, ALU.add)
                yield
                k.tt("pool", S_, S_, bc(1, t), ALU.mult)
                yield
                k.tt("dve", T_, (sa, sa[:, i0:i0 + HV].unsqueeze(2).to_broadcast([128, HV, RN]), hf), bc(5, t), ALU.mult)
                yield
                k.tt("dve", S_, S_, T_, ALU.add)
                yield
                k.tt("pool", T_, (vec6, vec6[:, 3, t, i0:i0 + HV].unsqueeze(2).to_broadcast([128, HV, RN]), 3), bc(2, t), ALU.mult)
                yield
                k.tt("dve", S_, S_, T_, ALU.add)
                yield
                k.tt("pool", T_, S_, bc(0, t), ALU.mult)
                yield
                k.red("dve", (ys, ys[:, t, i0:i0 + HV], (hf, t)), T_, ALU.add)
                yield

        gens_ = [rec_gen(0), rec_gen(1)]
        while gens_:
            for g_ in list(gens_):
                try:
                    next(g_)
                except StopIteration:
                    gens_.remove(g_)
        k.dma("sp", o_sS[:, :], Ssb[:].rearrange("p a b -> p (a b)"), reads=[Ssb], writes=[o_sS])
        k.dma("sp", ysc[:, :], ys[:].rearrange("p a b -> p (a b)"), reads=[ys], writes=[ysc])
        for b in range(NSB):
            k.dma("sp", Ytm[ST * b:ST * b + ST, :].rearrange("t (h i) -> t h i", h=RH),
                  ysc[RH * b:RH * b + RH, :].rearrange("h (t i) -> t h i", t=ST), reads=[ysc], writes=[Ytm])
        rwkv_epilogue(ti, Ytm, rt[7])

    def tail_rows(ti):
        if ti != NTP - 1:
            return
        for blk, (c0, n) in enumerate(((0, 512), (512, 512), (1024, 512), (1536, 256))):
            proj_tm(pR[blk % 2], pR[blk % 2][:, 0:n], C_R + c0, n)
            k.copy("act", (rt[1], rt[1][96:128, :, :].rearrange("p a b -> p (a b)")[:, 0:n]), (pR[blk % 2], pR[blk % 2][96:128, 0:n]))
            k.dma("sp", o_pshift[0:1, c0:c0 + n], rt[1][127:128, :, :].rearrange("p a b -> p (a b)")[:, 0:n], reads=[rt[1]], writes=[o_pshift])

    tiles_all = list(range(NT)) if stage >= 5 else list(range(NTP))

    def phase_1b(k):
        k.barrier()
        k.bot = mark_1a
        W_g = k.sbuf([128, 8, 2048], BF16, "W_g")
        W_bm = k.sbuf([128, 4, D], BF16, "W_bm")
        W_br = k.sbuf([128, 4, D], BF16, "W_br")
        W_out = k.sbuf([128, 8, D], BF16, "W_out")
        k.dma("pool", W_bm[:], d_w_bm[:], reads=[d_w_bm], writes=[W_bm])
        for kh in range(2):
            k.dma("pool", W_g[:, 4 * kh:4 * kh + 4, 0:1024], d_w_in[:, 4 * kh:4 * kh + 4, C_G:C_G + 1024], reads=[d_w_in], writes=[(W_g, "a")])
        k.dma("pool", W_br[:], d_w_br[:], reads=[d_w_br], writes=[W_br])
        for kh in range(2):
            k.dma("pool", W_g[:, 4 * kh:4 * kh + 4, 1024:2048], d_w_in[:, 4 * kh:4 * kh + 4, C_G + 1024:C_G + 2048], reads=[d_w_in], writes=[(W_g, "b")])
        k.dma("pool", W_out[:], d_w_out[:], reads=[d_w_out], writes=[W_out])
        xtb = [k.sbuf([128, D], F32, f"xtb{i}") for i in range(2)]
        hb2 = k.sbuf([128, D], BF16, "hb2")
        hT2 = k.sbuf([128, 8, 128], BF16, "hT2")
        ss2 = k.sbuf([128, 1], F32, "ss2")
        rs2 = k.sbuf([128, 1], F32, "rs2")
        sgb = [k.sbuf([128, 512], F32, f"sgb{i}") for i in range(2)]
        yab = k.sbuf([128, D], F32, "yab")
        mg = [k.sbuf([128, D], BF16, f"mg{i}") for i in range(2)]
        mT = k.sbuf([128, 8, 128], BF16, "mT")
        def head_gen(ti):
            xb = xtb[ti % 2]
            xd, xap = x_rows(ti)
            k.dma("sp", xb[:], xap, reads=[xd], writes=[xb])
            norm_generic(xb, g1bc, hb2, hT2, ss2, rs2)
            yield
            for half, (Wb, src, key) in enumerate(((W_bm, hmT_all, "a"), (W_br, yrgT_all, "b"))):
                for blk in range(2):
                    for kc in range(4):
                        k.mm((pR[blk], pR[blk][:, :]), (src, src[:, ti, kc, :], ti), (Wb, Wb[:, kc, blk * 512:(blk + 1) * 512]),
                             start=(kc == 0), stop=(kc == 3))
                    col = half * 1024 + blk * 512
                    for kc in range(8):
                        k.mm((pF[blk], pF[blk][:, :]), (hT2, hT2[:, kc, :]), (W_g, W_g[:, kc, col:col + 512], key),
                             start=(kc == 0), stop=(kc == 7))
                    k.act((sgb[blk], sgb[blk][:]), (pF[blk], pF[blk][:, :]), AF.Sigmoid)
                    if half == 0:
                        k.tt("dve", (yab, yab[:, blk * 512:(blk + 1) * 512]), (sgb[blk], sgb[blk][:]), (pR[blk], pR[blk][:, :]), ALU.mult)
                    else:
                        k.tt("dve", (sgb[blk], sgb[blk][:]), (sgb[blk], sgb[blk][:]), (pR[blk], pR[blk][:, :]), ALU.mult)
                        k.tt("pool", (mg[ti % 2], mg[ti % 2][:, blk * 512:(blk + 1) * 512]), (sgb[blk], sgb[blk][:]), (yab, yab[:, blk * 512:(blk + 1) * 512]), ALU.add)
                    yield

        def tailb_gen(ti):
            xb = xtb[ti % 2]
            mgt = mg[ti % 2]
            for kc in range(8):
                k.tr((pT, pT[:, kc * 128:(kc + 1) * 128]), (mgt, mgt[:, kc * 128:(kc + 1) * 128]), (identb, identb[:]))
            k.copy("act", (mT, mT[:].rearrange("p a b -> p (a b)")), (pT, pT[:, :]))
            yield
            for blk in range(2):
                for kc in range(8):
                    k.mm((pM[blk], pM[blk][:, :]), (mT, mT[:, kc, :]), (W_out, W_out[:, kc, blk * 512:(blk + 1) * 512]),
                         start=(kc == 0), stop=(kc == 7))
                k.tt("dve", (xb, xb[:, blk * 512:(blk + 1) * 512]), (xb, xb[:, blk * 512:(blk + 1) * 512]), (pM[blk], pM[blk][:, :]), ALU.add)
                yield
            k.dma("sp", x1s[ti * 128:(ti + 1) * 128, :], xb[:], reads=[xb], writes=[(x1s, ti)])

        def rr(gens):
            gens = list(gens)
            while gens:
                for g_ in list(gens):
                    try:
                        next(g_)
                    except StopIteration:
                        gens.remove(g_)

        tlb = list(tiles_all)
        rr([head_gen(tlb[0])])
        for idx, ti in enumerate(tlb):
            gl = [tailb_gen(ti)]
            if idx + 1 < len(tlb):
                gl.append(head_gen(tlb[idx + 1]))
            rr(gl)

    def norm_generic(xbuf, gbc, hb_, hT_, ss_, rs_):
        k.act((hb_, hb_[:]), (xbuf, xbuf[:]), AF.Square, accum=(ss_, ss_[:]))
        k.ts("dve", (rs_, rs_[:]), (ss_, ss_[:]), 1.0 / D, EPS, op0=ALU.mult, op1=ALU.add)
        k.act((rs_, rs_[:]), (rs_, rs_[:]), AF.Ln)
        k.act((rs_, rs_[:]), (rs_, rs_[:]), AF.Exp, scale=-0.5)
        k.stt("dve", (hb_, hb_[:]), (xbuf, xbuf[:]), (rs_, rs_[:, 0:1]), (gbc, gbc[:]), ALU.mult, ALU.mult)
        for kc in range(8):
            k.tr((pT, pT[:, kc * 128:(kc + 1) * 128]), (hb_, hb_[:, kc * 128:(kc + 1) * 128]), (identb, identb[:]))
        k.copy("act", (hT_, hT_[:].rearrange("p a b -> p (a b)")), (pT, pT[:, :]))

    def phase_2(k):
        k.barrier()
        k.bot = mark_phase
        F_up = k.sbuf([128, 8, 2 * DFF], BF16, "F_up")
        F_dn = k.sbuf([128, NFC, D], BF16, "F_dn")
        PGW = k.sbuf([128, 8, D], BF16, "PGW")
        PPJ = k.sbuf([128, 2, D], BF16, "PPJ")
        NG = 4
        CW = DFF // NG
        for g in range(NG):
            for part in range(2):
                k.dma("pool", F_up[:, :, part * DFF + g * CW:part * DFF + (g + 1) * CW], d_f_up[:, :, part * DFF + g * CW:part * DFF + (g + 1) * CW],
                      reads=[d_f_up], writes=[(F_up, g)])
        for g in range(2):
            k.dma("pool", F_dn[:, 11 * g:11 * g + 11, :], d_f_down[:, 11 * g:11 * g + 11, :], reads=[d_f_down], writes=[(F_dn, g)])
        k.dma("pool", PGW[:], d_pgw[:], reads=[d_pgw], writes=[PGW])
        k.dma("pool", PPJ[:], d_ppj[:], reads=[d_ppj], writes=[PPJ])
        g2bc = k.sbuf([128, D], F32, "g2bc")
        g3bc = k.sbuf([128, D], F32, "g3bc")
        g4bc = k.sbuf([128, D], F32, "g4bc")
        fct = k.sbuf([128, 4 * NFC], F32, "fct")
        k.dma("sp", g2bc[:], d_g2[:], reads=[d_g2], writes=[g2bc])
        k.dma("sp", g3bc[:], d_g3[:], reads=[d_g3], writes=[g3bc])
        k.dma("sp", g4bc[:], d_g4[:], reads=[d_g4], writes=[g4bc])
        k.dma("sp", fct[:], d_fctab[:], reads=[d_fctab], writes=[fct])
        fcol = lambda c: (fct, fct[:, c:c + 1])
        xq = [k.sbuf([128, D], F32, f"xq{i}") for i in range(2)]
        hb3 = k.sbuf([128, D], BF16, "hb3")
        hT3s = [k.sbuf([128, 8, 128], BF16, f"hT3a{i}") for i in range(2)]
        ss3 = k.sbuf([128, 1], F32, "ss3")
        rs3 = k.sbuf([128, 1], F32, "rs3")
        gT = k.sbuf([128, NFC, 128], BF16, "gT")
        cf = k.sbuf([128, NFC, 2], F32, "cf")
        GS = 4
        EXW = NSB * (ST + 2)
        ex4 = [k.sbuf([128, GS, EXW], F32, f"ex4_{i}") for i in range(2)]
        cc4 = [k.sbuf([128, GS, 128], F32, f"cc4_{i}") for i in range(2)]
        t14 = [k.sbuf([128, GS, 128], F32, "t14_0")] * 2
        up4 = [k.sbuf([128, GS, 128], F32, f"up4_{i}") for i in range(2)]
        sg3 = [k.sbuf([128, 512], F32, "sg30")] * 2
        ppt = k.sbuf([128, PLE], F32, "ppt")
        ppb = k.sbuf([128, PLE], BF16, "ppb")
        peT = k.sbuf([128, 2, 128], BF16, "peT")
        utm = sg3[0]
        k.memset("pool", (cf, cf[:]), 0.0)
        cfs = k.sbuf([128, NFC, 2 * NSB], F32, "cfs")
        GC = 1.5957691216057308
        BLK6 = ((0, 512), (512, 512), (1024, 512), (1536, 512), (2048, 512), (2560, 256))
        groups = [list(range(g0, min(g0 + GS, NFC))) for g0 in range(0, NFC, GS)]
        gbank = [pF[0], pF[1]]
        ubank = [pM[0], pM[1]]

        def fup_w(col):
            g = col // CW
            g_hi = (col + 127) // CW
            return g, g_hi

        def stage_A(ti, gi, hT3):
            smp = ti == NTP
            p = gi % 2
            chunks = groups[gi]
            n = len(chunks)
            c0 = chunks[0]
            for part, bank in ((0, gbank[p]), (1, ubank[p])):
                for ci, c in enumerate(chunks):
                    g, g_hi = fup_w(c * 128)
                    for kc in range(8):
                        k.mm((bank, bank[:, ci * 128:(ci + 1) * 128]), (F_up, F_up[:, kc, part * DFF + c * 128:part * DFF + (c + 1) * 128], g),
                             (hT3, hT3[:, kc, :]), start=(kc == 0), stop=(kc == 7))
                        if g_hi != g and g_hi in F_up.subs and F_up.subs[g_hi].w is not None:
                            k.streams["pe"][-1].deps.add(F_up.subs[g_hi].w)
            ex = ex4[p]
            if not smp:
                k.copy("pool", (ex, ex[:, 0:n, 0:2]), (cf, cf[:, c0:c0 + n, :]))
                k.copy("act", (ex, ex[:, 0:n, 2:130]), (gbank[p], gbank[p][:, 0:n * 128].rearrange("p (c t) -> p c t", c=n)))
                k.copy("pool", (cf, cf[:, c0:c0 + n, :]), (ex, ex[:, 0:n, 128:130]))
            else:
                exs = ex[:, 0:n, :].rearrange("p c (b t) -> p c b t", t=ST + 2)
                k.copy("pool", (ex, exs[:, :, :, 0:2]), (cfs, cfs[:, c0:c0 + n, :].rearrange("p c (b j) -> p c b j", j=2)))
                k.copy("act", (ex, exs[:, :, :, 2:ST + 2]), (gbank[p], gbank[p][:, 0:n * 128].rearrange("p (c b t) -> p c b t", c=n, b=NSB)))
            k.copy("act", (up4[p], up4[p][:, 0:n, :]), (ubank[p], ubank[p][:, 0:n * 128].rearrange("p (c t) -> p c t", c=n)))
            for ci, c in enumerate(chunks):
                if not smp:
                    tap = lambda j: ex[:, ci, j:j + 128]
                    ccv = cc4[p][:, ci, :]
                else:
                    e3 = ex[:, ci, :].rearrange("p (b t) -> p b t", t=ST + 2)
                    tap = lambda j, e3=e3: e3[:, :, j:j + ST]
                    ccv = cc4[p][:, ci, :].rearrange("p (b t) -> p b t", t=ST)
                cb = cc4[p]
                k.ts("dve", (cb, ccv), (ex, tap(2)), fcol(2 * NFC + c), fcol(3 * NFC + c), op0=ALU.mult, op1=ALU.add)
                k.stt("dve", (cb, ccv), (ex, tap(1)), fcol(1 * NFC + c), (cb, ccv), ALU.mult, ALU.add)
                k.stt("dve", (cb, ccv), (ex, tap(0)), fcol(0 * NFC + c), (cb, ccv), ALU.mult, ALU.add)

        def stage_B(ti, gi):
            p = gi % 2
            chunks = groups[gi]
            n = len(chunks)
            c0 = chunks[0]
            cb = (cc4[p], cc4[p][:, 0:n, :])
            ta = (t14[p], t14[p][:, 0:n, :])
            k.tt("pool", ta, cb, cb, ALU.mult)
            k.ts("dve", ta, ta, 0.044715, 1.0, op0=ALU.mult, op1=ALU.add)
            k.tt("pool", ta, ta, cb, ALU.mult)
            k.act(ta, ta, AF.Sigmoid, scale=GC)
            k.tt("pool", ta, ta, cb, ALU.mult)
            k.tt("dve", (gT, gT[:, c0:c0 + n, :], ("g", gi)), ta, (up4[p], up4[p][:, 0:n, :]), ALU.mult)

        def load_norm2(ti):
            xb = xq[ti % 2]
            k.dma("sp", xb[:], x1s[ti * 128:(ti + 1) * 128, :], reads=[(x1s, ti)], writes=[xb])
            if ti == NTP:
                for b6, (c0, n) in enumerate(BLK6):
                    k.dma("sp", utm[0:2 * NSB, 0:n], st_fconv[:, c0:c0 + n], reads=[st_fconv], writes=[utm])
                    nch = n // 128
                    for ci in range(nch):
                        k.tr((pR[b6 % 2], pR[b6 % 2][:, ci * 32:(ci + 1) * 32]), (utm, utm[0:2 * NSB, ci * 128:(ci + 1) * 128]),
                             (identf, identf[0:2 * NSB, 0:2 * NSB]))
                    k.copy("act", (cfs, cfs[:, 4 * b6:4 * b6 + nch, :]), (pR[b6 % 2], pR[b6 % 2][:, 0:nch * 32].rearrange("p (c x) -> p c x", c=nch)))
            norm_generic(xb, g2bc, hb3, hT3s[ti % 2], ss3, rs3)

        hb3b = hb3

        def groups_gen(ti):
            hT3 = hT3s[ti % 2]
            for gi in range(len(groups) + 1):
                if gi < len(groups):
                    stage_A(ti, gi, hT3)
                    yield
                if gi >= 1:
                    stage_B(ti, gi - 1)
                    yield

        def tail_gen(ti):
            smp = ti == NTP
            xb = xq[ti % 2]
            hT3 = hT3s[ti % 2]
            for blk in range(2):
                for c in range(NFC):
                    k.mm((pR[blk], pR[blk][:, :]), (gT, gT[:, c, :], ("g", c // GS)), (F_dn, F_dn[:, c, blk * 512:(blk + 1) * 512], c // 11),
                         start=(c == 0), stop=(c == NFC - 1))
                k.tt("dve", (xb, xb[:, blk * 512:(blk + 1) * 512]), (xb, xb[:, blk * 512:(blk + 1) * 512]), (pR[blk], pR[blk][:, :]), ALU.add)
                yield
            if ti == NTP - 1 or smp:
                for b6, (c0, n) in enumerate(BLK6):
                    for kc in range(8):
                        k.mm((pM[2], pM[2][:, 0:n]), (hT3, hT3[:, kc, :]), (F_up, F_up[:, kc, c0:c0 + n]),
                             start=(kc == 0), stop=(kc == 7))
                    if not smp:
                        k.copy("act", (utm, utm[96:128, 0:n]), (pM[2], pM[2][96:128, 0:n]))
                        k.dma("sp", o_pfconv[:, c0:c0 + n], utm[126:128, 0:n], reads=[utm], writes=[o_pfconv])
                    else:
                        k.copy("act", (utm, utm[:, 0:n]), (pM[2], pM[2][:, 0:n]))
                        for b in range(NSB):
                            k.dma("sp", o_sfconv[2 * b:2 * b + 2, c0:c0 + n], utm[ST * b + ST - 2:ST * b + ST, 0:n], reads=[utm], writes=[o_sfconv])
                    yield
            norm_generic(xb, g3bc, hb3b, hT3, ss3, rs3)
            yield
            pd, pap = (pp, pp[ti * 128:(ti + 1) * 128, :]) if not smp else (psm, psm[:, :])
            k.dma("sp", ppt[:], pap, reads=[pd], writes=[ppt])
            k.copy("act", (ppb, ppb[:]), (ppt, ppt[:]))
            for kc in range(2):
                k.tr((pT, pT[:, kc * 128:(kc + 1) * 128]), (ppb, ppb[:, kc * 128:(kc + 1) * 128]), (identb, identb[:]))
            k.copy("act", (peT, peT[:].rearrange("p a b -> p (a b)")), (pT, pT[:, 0:256]))
            yield
            for blk in range(2):
                for kc in range(8):
                    k.mm((pR[blk], pR[blk][:, :]), (hT3, hT3[:, kc, :]), (PGW, PGW[:, kc, blk * 512:(blk + 1) * 512]),
                         start=(kc == 0), stop=(kc == 7))
                yield
                k.act((sg3[blk], sg3[blk][:]), (pR[blk], pR[blk][:, :]), AF.Sigmoid)
                for kc in range(2):
                    k.mm((pM[2], pM[2][:, :]), (peT, peT[:, kc, :]), (PPJ, PPJ[:, kc, blk * 512:(blk + 1) * 512]),
                         start=(kc == 0), stop=(kc == 1))
                k.tt("dve", (sg3[blk], sg3[blk][:]), (sg3[blk], sg3[blk][:]), (pM[2], pM[2][:, :]), ALU.mult)
                k.tt("pool", (xb, xb[:, blk * 512:(blk + 1) * 512]), (xb, xb[:, blk * 512:(blk + 1) * 512]), (sg3[blk], sg3[blk][:]), ALU.add)
                yield
            k.act((hb3b, hb3b[:]), (xb, xb[:]), AF.Square, accum=(ss3, ss3[:]))
            k.ts("dve", (rs3, rs3[:]), (ss3, ss3[:]), 1.0 / D, EPS, op0=ALU.mult, op1=ALU.add)
            k.act((rs3, rs3[:]), (rs3, rs3[:]), AF.Ln)
            k.act((rs3, rs3[:]), (rs3, rs3[:]), AF.Exp, scale=-0.5)
            yield
            k.stt("dve", (xb, xb[:]), (xb, xb[:]), (rs3, rs3[:, 0:1]), (g4bc, g4bc[:]), ALU.mult, ALU.mult)
            if not smp:
                k.dma("sp", y_p[ti * 128:(ti + 1) * 128, :], xb[:], reads=[xb], writes=[y_p])
            else:
                k.dma("sp", y_s[:, :], xb[:], reads=[xb], writes=[y_s])
            if ti in nxt2:
                yield
                load_norm2(nxt2[ti])

        def run_rr(gens):
            gens = list(gens)
            while gens:
                for g_ in list(gens):
                    try:
                        next(g_)
                    except StopIteration:
                        gens.remove(g_)

        tl = list(tiles_all)
        nxt2 = {tl[i]: tl[i + 2] for i in range(len(tl) - 2)}
        load_norm2(tl[0])
        if len(tl) > 1:
            load_norm2(tl[1])
        run_rr([groups_gen(tl[0])])
        for idx, ti in enumerate(tl):
            tg = tail_gen(ti)
            if idx + 1 < len(tl):
                gg = groups_gen(tl[idx + 1])
                next(gg)
                next(gg)
                next(tg)
                next(tg)
                run_rr([gg, tg])
            else:
                run_rr([tg])

    _nt_dbg = int(_os.environ.get("KDBG_NT", "0"))
    tiles_run = (tiles_all if not _nt_dbg else list(range(_nt_dbg)))
    for ti in tiles_run:
        mixer_tile(ti)
        if stage >= 2:
            rwkv_tile(ti)
            tail_rows(ti)

    if stage >= 3:
        phase_1b(k)
    if stage >= 4:
        phase_2(k)
    k.emit()
    k.stats["sbuf_hiwater"] = k.hiwater
    k.stats["arena_bytes"] = k.arena_bytes
    return nc, k


def _chunk_rows(w, nk):
    return np.ascontiguousarray(w.reshape(nk, 128, w.shape[1]).transpose(1, 0, 2))


def _pcols(v, nc_):
    return v.reshape(nc_, 128).T


_PROG = {}


def _get_prog(stage=99, dbg=False):
    key = (stage, dbg)
    if key not in _PROG:
        _PROG[key] = build_program(stage, dbg)
    return _PROG[key]


def make_in_maps(inp):
    f = lambda a: np.ascontiguousarray(np.asarray(a, dtype=np.float32))
    ptab = np.zeros((128, 128), np.float32)
    mcw = f(inp["m_conv_w"])[0]
    for j in range(4):
        ptab[:, j * 8:(j + 1) * 8] = _pcols(mcw[j], 8)
    ptab[:, 32:40] = _pcols(f(inp["m_conv_b"])[0], 8)
    ptab[:, 40:54] = _pcols(f(inp["r_mix"])[0], 14)
    ptab[:, 54:58] = _pcols(f(inp["r_w0"])[0], 4)
    ptab[:, 58:62] = _pcols(f(inp["r_a0"])[0], 4)
    ptab[:, 62:66] = _pcols(f(inp["r_kk"])[0], 4)
    ptab[:, 66:70] = _pcols(f(inp["r_ka"])[0], 4)
    ptab[:, 70:74] = _pcols(f(inp["r_rk"])[0].reshape(-1), 4)
    ptab[:, 74:78] = _pcols(f(inp["r_ln_g"])[0], 4)
    ptab[:, 78:82] = _pcols(f(inp["r_ln_b"])[0], 4)
    ptab[:, 82:86] = _pcols(f(inp["m_norm_g"])[0], 4)
    fct = np.zeros((128, 4 * NFC), np.float32)
    fcw = f(inp["f_conv_w"])[0]
    for j in range(3):
        fct[:, j * NFC:(j + 1) * NFC] = _pcols(fcw[j], NFC)
    fct[:, 3 * NFC:4 * NFC] = _pcols(f(inp["f_conv_b"])[0], NFC)
    gbias = np.stack([f(inp["m_i_bias"])[0], f(inp["m_f_bias"])[0]], axis=1)
    ra2 = np.zeros((128, RW), np.float32)
    ra2[64:128] = f(inp["r_a2"])[0]
    bc = lambda v: np.ascontiguousarray(np.broadcast_to(f(v).reshape(1, D), (128, D)))
    shared = {
        "w_in": _chunk_rows(f(inp["w_in"])[0], 8),
        "w_bm": _chunk_rows(f(inp["w_branch_m"])[0], 4),
        "w_br": _chunk_rows(f(inp["w_branch_r"])[0], 4),
        "w_out": _chunk_rows(f(inp["w_out"])[0], 8),
        "f_up": _chunk_rows(f(inp["f_up"])[0], 8),
        "f_down": _chunk_rows(f(inp["f_down"])[0], NFC),
        "ple_gate_w": _chunk_rows(f(inp["ple_gate_w"])[0], 8),
        "ple_proj": _chunk_rows(f(inp["ple_proj"])[0], 2),
        "r_w2": f(inp["r_w2"])[0], "r_a2": ra2, "r_g2": f(inp["r_g2"])[0],
        "norm1_g": bc(inp["norm1_g"]), "norm2_g": bc(inp["norm2_g"]),
        "ple_norm_g": bc(inp["ple_norm_g"]), "final_norm_g": bc(inp["final_norm_g"]),
        "ptab": ptab, "fctab": fct, "gate_bias": np.ascontiguousarray(gbias),
    }
    maps = []
    for c in range(NCORES):
        sl = slice(c * NSB, (c + 1) * NSB)
        m = dict(shared)
        m["xp"] = f(inp["x_prompt"][c])
        m["xs"] = f(inp["x_sample"][sl]).reshape(128, D)
        m["pp"] = f(inp["p_prompt"][0, c])
        m["psm"] = f(inp["p_sample"][0, sl]).reshape(128, PLE)
        m["st_mconv"] = f(inp["state_mlstm_conv"][0, sl]).reshape(NSB * 3, 2 * MW)
        m["st_mC"] = f(inp["state_mlstm_C"][0, sl])
        m["st_mn"] = f(inp["state_mlstm_n"][0, sl])
        m["st_mm"] = f(inp["state_mlstm_m"][0, sl])
        m["st_rshift"] = f(inp["state_rwkv_shift"][0, sl])
        m["st_rS"] = f(inp["state_rwkv_S"][0, sl]).reshape(NSB * RH, RN * RN)
        m["st_fconv"] = f(inp["state_ffn_conv"][0, sl]).reshape(NSB * 2, DFF)
        maps.append(m)
    return maps


def assemble(results):
    g = lambda name: [np.asarray(r[name], dtype=np.float32) for r in results]
    y_p = np.stack(g("y_p"), 0)
    y_s = np.concatenate([a.reshape(NSB, ST, D) for a in g("y_s")], 0)
    p_conv = np.stack(g("p_conv"), 0)[None]
    p_C = np.stack(g("p_C"), 0)[None]
    p_n = np.stack(g("p_n"), 0)[None]
    p_m = np.stack([a.reshape(MH) for a in g("p_m")], 0)[None]
    p_shift = np.stack([a.reshape(RCOLS) for a in g("p_shift")], 0)[None]
    p_S = np.stack([a.reshape(RH, RN, RN) for a in g("p_S")], 0)[None]
    p_fconv = np.stack(g("p_fconv"), 0)[None]
    s_conv = np.concatenate([a.reshape(NSB, 3, 2 * MW) for a in g("s_conv")], 0)[None]
    s_C = np.concatenate(g("s_C"), 0)[None]
    s_n = np.concatenate(g("s_n"), 0)[None]
    s_m = np.concatenate(g("s_m"), 0)[None]
    s_shift = np.concatenate(g("s_shift"), 0)[None]
    s_S = np.concatenate([a.reshape(NSB, RH, RN, RN) for a in g("s_S")], 0)[None]
    s_fconv = np.concatenate([a.reshape(NSB, 2, DFF) for a in g("s_fconv")], 0)[None]
    return (y_p, y_s, p_conv, p_C, p_n, p_m, p_shift, p_S, p_fconv,
            s_conv, s_C, s_n, s_m, s_shift, s_S, s_fconv)


def kernel(**inputs):
    nc, _ = _get_prog()
    maps = make_in_maps(inputs)
    res = run_bass_kernel_spmd(nc, maps, core_ids=list(range(NCORES)))
    return assemble(res.results)
```

```python
import math
from contextlib import ExitStack

import numpy as np
import concourse.bass as bass
import concourse.mybir as mybir
from concourse.bass_utils import run_bass_kernel_spmd

F32 = mybir.dt.float32
BF16 = mybir.dt.bfloat16
AF = mybir.ActivationFunctionType
ALU = mybir.AluOpType
AX = mybir.AxisListType

ENGS = ("pe", "act", "dve", "pool", "sp")
N_DMA_SEMS = 8
SAME_ENG_DIST = 2

D = 1024
SEQ = 2048
NCORES = 8
NTP = SEQ // 128
NSB = 16
ST = 8
MW = 512
MH = 4
RW = 512
RH = 8
RN = 64
RCOLS = 1792
DFF = 2816
NFC = DFF // 128
PLE = 256
N_IN = 5896
C_QK, C_V, C_O, C_I, C_F, C_R, C_G = 0, 1024, 1536, 2048, 2052, 2056, 3848
EPS = 1e-6
GN_EPS = 64e-5
KSCALE = 128 ** -0.5
WSCALE = -math.exp(-0.5)


class _Trk:
    __slots__ = ("w", "r")

    def __init__(self):
        self.w = None
        self.r = []


class Buf:
    def __init__(self, t, name):
        self.t = t
        self.name = name
        self.whole = _Trk()
        self.subs = {}

    def __getitem__(self, idx):
        return self.t[idx]

    def view(self, ap, name=None):
        b = Buf(ap, name or self.name + "_v")
        b.whole = self.whole
        b.subs = self.subs
        return b


class _Op:
    __slots__ = ("eng", "fn", "deps", "needs_inc", "is_dma", "sem", "val", "pos", "force")


class K:
    def __init__(self, nc):
        self.nc = nc
        self.es = ExitStack()
        self.streams = {e: [] for e in ENGS}
        self.dma_rr = {e: 0 for e in ENGS}
        self.dma_last = {}
        self.nbuf = 0
        self.ops = []

    def _init_arena(self):
        nbytes = (int(self.nc.sbuf_bytes_remaining) - 512) // 64 * 64
        self.arena_bytes = nbytes
        self.arena = self.es.enter_context(self.nc.sbuf_tensor("arena", [128, nbytes // 2], BF16))
        self.bot = 0
        self.top = nbytes
        self.hiwater = 0

    def _view(self, off, shape, dtype):
        n = 1
        for d in shape[1:]:
            n *= d
        esz = 4 if dtype == F32 else 2
        v = self.arena[:, off // 2:(off + n * esz) // 2]
        if dtype == F32:
            v = v.bitcast(F32)
        if len(shape) > 2:
            names = " ".join(f"d{i}" for i in range(len(shape) - 1))
            v = v.rearrange(f"p ({names}) -> p {names}", **{f"d{i}": shape[i + 1] for i in range(len(shape) - 1)})
        if shape[0] < 128:
            v = v[0:shape[0]]
        return v, n * esz

    def sbuf(self, shape, dtype, name=None, top=False):
        if not hasattr(self, "arena"):
            self._init_arena()
        self.nbuf += 1
        name = name or f"sb{self.nbuf}"
        n = 1
        for d in shape[1:]:
            n *= d
        nb = (n * (4 if dtype == F32 else 2) + 63) // 64 * 64
        if top:
            self.top -= nb
            off = self.top
        else:
            off = self.bot
            self.bot += nb
        assert self.bot <= self.top, f"SBUF arena overflow allocating {name}: bot={self.bot} top={self.top}"
        self.hiwater = max(self.hiwater, self.bot + (self.arena_bytes - self.top))
        v, _ = self._view(off, list(shape), dtype)
        return Buf(v, name)

    def pe_fence(self):
        st = self.streams["pe"]
        if not st:
            return
        last = st[-1]
        o = self.op("pe", lambda h: h.nop(), (), ())
        o.deps.add(last)
        o.force = {last}
        if getattr(self, "fence_mm", None) is not None:
            fb, fi = self.fence_mm
            self.tr((fb, fb[:, 0:128]), (fi, fi[:]), (fi, fi[:]))
            last = self.streams["pe"][-1]
            o = self.op("pe", lambda h: h.nop(), (), ())
            o.deps.add(last)
            o.force = {last}

    def barrier(self):
        lasts = [st[-1] for st in self.streams.values() if st]
        lasts += list(self.dma_last.values())
        for e in ENGS:
            o = self.op(e, lambda h: h.nop(), (), ())
            o.deps.update(x for x in lasts if x is not o)

    def psum(self, shape, dtype, name=None):
        self.nbuf += 1
        name = "ps_" + (name or f"{self.nbuf}")
        t = self.es.enter_context(self.nc.psum_tensor(name, list(shape), dtype))
        return Buf(t, name)

    def dram(self, name, shape, dtype, kind="Internal"):
        t = self.nc.dram_tensor(name, list(shape), dtype, kind=kind)
        return Buf(t.ap(), name)

    def _touch(self, op, item, is_write):
        if isinstance(item, tuple):
            buf, key = item
        else:
            buf, key = item, None
        if key is None:
            trks = [buf.whole] + list(buf.subs.values())
        else:
            if key not in buf.subs:
                buf.subs[key] = _Trk()
            trks = [buf.whole, buf.subs[key]]
        for t in trks:
            if t.w is not None:
                op.deps.add(t.w)
            if is_write:
                op.deps.update(t.r)
        return buf, key

    def _commit(self, op, buf, key, is_write):
        if key is None:
            if is_write:
                buf.whole.w = op
                buf.whole.r = []
                buf.subs.clear()
            else:
                self._add_reader(buf.whole, op)
        else:
            t = buf.subs[key]
            if is_write:
                t.w = op
                t.r = []
            else:
                self._add_reader(t, op)

    @staticmethod
    def _add_reader(t, op):
        if not op.is_dma:
            t.r = [o for o in t.r if o.is_dma or o.eng != op.eng]
        t.r.append(op)

    def op(self, eng, fn, reads=(), writes=(), dma=False):
        o = _Op()
        o.eng = eng
        o.fn = fn
        o.deps = set()
        o.needs_inc = False
        o.is_dma = dma
        o.sem = None
        o.val = None
        o.force = None
        touched = []
        for it in reads:
            touched.append(self._touch(o, it, False) + (False,))
        for it in writes:
            touched.append(self._touch(o, it, True) + (True,))
        o.deps.discard(o)
        for buf, key, w in touched:
            self._commit(o, buf, key, w)
        if dma:
            kk = (eng, self.dma_rr[eng] % N_DMA_SEMS)
            self.dma_rr[eng] += 1
            prev = self.dma_last.get(kk)
            if prev is not None:
                o.deps.add(prev)
            self.dma_last[kk] = o
            o.sem = kk
            o.needs_inc = True
        o.pos = len(self.streams[eng])
        self.streams[eng].append(o)
        self.ops.append(o)
        return o

    def dma(self, eng, out, in_, reads=(), writes=(), **kw):
        return self.op(eng, lambda e: e.dma_start(out=out, in_=in_, **kw), reads, writes, dma=True)

    def emit(self):
        nc = self.nc
        for o in self.ops:
            real = []
            for d in o.deps:
                if (not d.is_dma) and (not o.is_dma) and d.eng == o.eng and o.eng == "pe":
                    if not (o.force and d in o.force):
                        continue
                d.needs_inc = True
                real.append(d)
            o.deps = real
        for e in ENGS:
            cs = [o for o in self.streams[e] if not o.is_dma]
            if cs:
                cs[-1].needs_inc = True
        es = self.es
        esem = {e: es.enter_context(nc.semaphore(f"s_{e}")) for e in ENGS}
        dsem = {}
        for e in ENGS:
            for i in range(min(N_DMA_SEMS, self.dma_rr[e])):
                dsem[(e, i)] = es.enter_context(nc.semaphore(f"d_{e}{i}"))
        dcount = {kk: 0 for kk in dsem}
        for e in ENGS:
            c = 0
            for o in self.streams[e]:
                if o.is_dma:
                    dcount[o.sem] += 16
                    o.val = dcount[o.sem]
                    o.sem = dsem[o.sem]
                elif o.needs_inc:
                    c += 1
                    o.val = c
                    o.sem = esem[e]
        final_waits = [(s, dcount[kk]) for kk, s in dsem.items() if dcount[kk] > 0]
        for e in ENGS:
            if e == "sp":
                continue
            cs = [o for o in self.streams[e] if not o.is_dma and o.needs_inc]
            if cs:
                final_waits.append((esem[e], cs[-1].val))
        streams = self.streams
        nwaits = [0]

        def run(e, handle):
            waited = {}
            for o in streams[e]:
                need = {}
                for d in o.deps:
                    if need.get(d.sem, (None, 0))[1] < d.val:
                        need[d.sem] = (d.sem, d.val)
                for s, v in need.values():
                    if waited.get(s, 0) >= v:
                        continue
                    handle.wait_ge(s, v)
                    nwaits[0] += 1
                    waited[s] = v
                ins = o.fn(handle)
                if o.is_dma:
                    ins.then_inc(o.sem, 16)
                elif o.needs_inc:
                    ins.then_inc(o.sem, 1)
            if e == "sp":
                for s, v in final_waits:
                    handle.wait_ge(s, v)

        with nc.Block() as block:
            @block.tensor
            def _(h):
                run("pe", h)

            @block.scalar
            def _(h):
                run("act", h)

            @block.vector
            def _(h):
                run("dve", h)

            @block.gpsimd
            def _(h):
                run("pool", h)

            @block.sync
            def _(h):
                run("sp", h)
        self.stats = dict(n_ops={e: len(streams[e]) for e in ENGS}, n_waits=nwaits[0])
        self.es.close()

    @staticmethod
    def _it(x):
        return (x[0], x[2]) if len(x) > 2 else x[0]

    def mm(self, out, lhsT, rhs, start=True, stop=True):
        return self.op("pe", lambda e: e.matmul(out[1], lhsT=lhsT[1], rhs=rhs[1], start=start, stop=stop),
                       reads=[self._it(lhsT), self._it(rhs)], writes=[self._it(out)])

    def tr(self, out, in_, ident):
        return self.op("pe", lambda e: e.transpose(out[1], in_[1], ident[1]),
                       reads=[self._it(in_), self._it(ident)], writes=[self._it(out)])

    def act(self, out, in_, func, bias=None, scale=None, accum=None, eng="act"):
        reads = [self._it(in_)]
        kw = {}
        if bias is not None:
            if isinstance(bias, tuple):
                reads.append(self._it(bias))
                kw["bias"] = bias[1]
            else:
                kw["bias"] = bias
        if scale is not None:
            if isinstance(scale, tuple):
                reads.append(self._it(scale))
                kw["scale"] = scale[1]
            else:
                kw["scale"] = scale
        writes = [self._it(out)]
        if accum is not None:
            writes.append(self._it(accum))
            kw["accum_out"] = accum[1]
        return self.op(eng, lambda e: e.activation(out=out[1], in_=in_[1], func=func, **kw), reads, writes)

    def tt(self, eng, out, in0, in1, op):
        return self.op(eng, lambda e: e.tensor_tensor(out=out[1], in0=in0[1], in1=in1[1], op=op),
                       reads=[self._it(in0), self._it(in1)], writes=[self._it(out)])

    def ts(self, eng, out, in0, s1, s2=None, op0=ALU.mult, op1=None, accum=None):
        reads = [self._it(in0)]
        a1 = s1
        a2 = s2
        if isinstance(s1, tuple):
            reads.append(self._it(s1))
            a1 = s1[1]
        if isinstance(s2, tuple):
            reads.append(self._it(s2))
            a2 = s2[1]
        kw = {}
        if op1 is not None:
            kw["op1"] = op1
        writes = [self._it(out)]
        if accum is not None:
            writes.append(self._it(accum))
            kw["accum_out"] = accum[1]
        return self.op(eng, lambda e: e.tensor_scalar(out=out[1], in0=in0[1], scalar1=a1, scalar2=a2, op0=op0, **kw),
                       reads, writes)

    def stt(self, eng, out, in0, scalar, in1, op0, op1):
        reads = [self._it(in0), self._it(in1)]
        a = scalar
        if isinstance(scalar, tuple):
            reads.append(self._it(scalar))
            a = scalar[1]
        return self.op(eng, lambda e: e.scalar_tensor_tensor(out=out[1], in0=in0[1], scalar=a, in1=in1[1], op0=op0, op1=op1),
                       reads, [self._it(out)])

    def copy(self, eng, out, in_):
        if eng == "act":
            return self.op(eng, lambda e: e.activation(out=out[1], in_=in_[1], func=AF.Copy),
                           reads=[self._it(in_)], writes=[self._it(out)])
        return self.op(eng, lambda e: e.tensor_copy(out=out[1], in_=in_[1]),
                       reads=[self._it(in_)], writes=[self._it(out)])

    def red(self, eng, out, in_, op, axis=AX.X):
        return self.op(eng, lambda e: e.tensor_reduce(out=out[1], in_=in_[1], axis=axis, op=op),
                       reads=[self._it(in_)], writes=[self._it(out)])

    def memset(self, eng, out, val):
        return self.op(eng, lambda e: e.memset(out[1], val), reads=[], writes=[self._it(out)])

    def scan(self, eng, out, d0, d1, init, op0, op1):
        return self.op(eng, lambda e: e.tensor_tensor_scan(out=out[1], data0=d0[1], data1=d1[1], initial=init, op0=op0, op1=op1),
                       reads=[self._it(d0), self._it(d1)], writes=[self._it(out)])


def build_program(stage=99, dbg=False):
    import os as _os
    nc = bass.Bass("TRN2", target_bir_lowering=False)
    k = K(nc)
    NT = NTP + 1

    def din(name, shape):
        return k.dram(name, shape, F32, "ExternalInput")

    def dout(name, shape):
        return k.dram(name, shape, F32, "ExternalOutput")

    xp = din("xp", [SEQ, D]); xs = din("xs", [128, D])
    pp = din("pp", [SEQ, PLE]); psm = din("psm", [128, PLE])
    st_mconv = din("st_mconv", [NSB * 3, 2 * MW])
    st_mC = din("st_mC", [NSB, MH, 128, 128])
    st_mn = din("st_mn", [NSB, MH, 128])
    st_mm = din("st_mm", [NSB, MH])
    st_rshift = din("st_rshift", [NSB, RCOLS])
    st_rS = din("st_rS", [NSB * RH, RN * RN])
    st_fconv = din("st_fconv", [NSB * 2, DFF])
    d_w_in = din("w_in", [128, 8, N_IN])
    d_w_bm = din("w_bm", [128, 4, D]); d_w_br = din("w_br", [128, 4, D])
    d_w_out = din("w_out", [128, 8, D])
    d_f_up = din("f_up", [128, 8, 2 * DFF]); d_f_down = din("f_down", [128, NFC, D])
    d_pgw = din("ple_gate_w", [128, 8, D]); d_ppj = din("ple_proj", [128, 2, D])
    d_rw2 = din("r_w2", [64, RW]); d_ra2 = din("r_a2", [128, RW]); d_rg2 = din("r_g2", [128, RW])
    d_g1 = din("norm1_g", [128, D]); d_g2 = din("norm2_g", [128, D])
    d_g3 = din("ple_norm_g", [128, D]); d_g4 = din("final_norm_g", [128, D])
    d_ptab = din("ptab", [128, 128])
    d_fctab = din("fctab", [128, 4 * NFC])
    d_gb = din("gate_bias", [4, 2])
    y_p = dout("y_p", [SEQ, D]); y_s = dout("y_s", [128, D])
    o_pconv = dout("p_conv", [3, 2 * MW]); o_pC = dout("p_C", [MH, 128, 128]); o_pn = dout("p_n", [MH, 128])
    o_pm = dout("p_m", [1, MH]); o_pshift = dout("p_shift", [1, RCOLS]); o_pS = dout("p_S", [RH * RN, RN])
    o_pfconv = dout("p_fconv", [2, DFF])
    o_sconv = dout("s_conv", [NSB * 3, 2 * MW]); o_sC = dout("s_C", [NSB, MH, 128, 128]); o_sn = dout("s_n", [NSB, MH, 128])
    o_sm = dout("s_m", [NSB, MH]); o_sshift = dout("s_shift", [NSB, RCOLS]); o_sS = dout("s_S", [NSB * RH, RN * RN])
    o_sfconv = dout("s_fconv", [NSB * 2, DFF])
    x1s = k.dram("x1_scratch", [NT * 128, D], F32)
    dbgs = {}

    def dbg_out(name, src_buf, src_ap, shape):
        if not dbg:
            return
        t = dout("dbg_" + name, shape)
        dbgs[name] = t
        k.dma("sp", t[:], src_ap, reads=[src_buf], writes=[t])

    identf = k.sbuf([128, 128], F32, "identf")
    identb = k.sbuf([128, 128], BF16, "identb")
    mark_phase = k.bot
    mU_in = [k.sbuf([128, 128], F32, f"mUin{i}") for i in range(2)]
    mU_st = [k.sbuf([128, 128], F32, f"mUst{i}") for i in range(2)]
    mL_st = [k.sbuf([128, 128], F32, f"mLst{i}") for i in range(2)]
    resets = [k.sbuf([128, 512], F32, f"resets{i}") for i in range(2)]
    ones4 = k.sbuf([4, 128], F32, "ones4")

    def aff(out_buf, out_ap, pattern, cm, base, op=ALU.is_ge):
        k.op("pool", lambda e: e.affine_select(out=out_ap, in_=out_ap, pattern=pattern, compare_op=op,
                                               fill=0.0, base=base, channel_multiplier=cm),
             reads=[out_buf], writes=[out_buf])

    k.memset("pool", (identf, identf[:]), 1.0)
    aff(identf, identf[:], [[-1, 128]], 1, 0)
    aff(identf, identf[:], [[1, 128]], -1, 0)
    k.copy("pool", (identb, identb[:]), (identf, identf[:]))
    for i in range(2):
        k.memset("pool", (mU_in[i], mU_in[i][:]), 1.0)
        aff(mU_in[i], mU_in[i][:], [[1, 128]], -1, 0)
        k.memset("pool", (mU_st[i], mU_st[i][:]), 1.0)
        aff(mU_st[i], mU_st[i][:], [[1, 128]], -1, -1)
        k.memset("pool", (mL_st[i], mL_st[i][:]), 1.0)
        aff(mL_st[i], mL_st[i][:], [[-1, 128]], 1, -1)
        k.memset("pool", (resets[i], resets[i][:]), 1.0)
    v3 = lambda b: b[:].rearrange("p (a c) -> p a c", c=ST)
    aff(mU_in[1], v3(mU_in[1]), [[-ST, 16], [0, ST]], 1, 0)
    aff(mU_st[1], v3(mU_st[1]), [[-ST, 16], [0, ST]], 1, 0)
    aff(mL_st[1], v3(mL_st[1]), [[ST, 16], [0, ST]], -1, ST - 1)
    k.memset("pool", (resets[0], resets[0][:].rearrange("p (a c) -> p a c", c=128)[:, :, 0:1]), 0.0)
    k.memset("pool", (resets[1], resets[1][:].rearrange("p (a c) -> p a c", c=ST)[:, :, 0:1]), 0.0)
    k.memset("pool", (ones4, ones4[:]), 1.0)
    mask2 = [k.sbuf([128, 256], F32, f"mask2_{i}") for i in range(2)]
    for i in range(2):
        k.copy("pool", (mask2[i], mask2[i][:, 0:128]), (mU_st[i], mU_st[i][:]))
        k.copy("pool", (mask2[i], mask2[i][:, 128:256]), (mU_in[i], mU_in[i][:]))
    I2 = k.sbuf([128, 64], F32, "I2")
    k.tt("pool", (I2, I2[:]), (identf, identf[:, 0:64]), (identf, identf[:, 64:128]), ALU.add)
    bones = k.sbuf([128, 128], F32, "bones")
    k.memset("pool", (bones, bones[:]), 0.0)
    k.memset("pool", (bones, bones[0:64, 0:64]), 1.0)
    k.memset("pool", (bones, bones[64:128, 64:128]), 1.0)

    ptab = k.sbuf([128, 128], F32, "ptab")
    k.dma("sp", ptab[:], d_ptab[:], reads=[d_ptab], writes=[ptab])
    PT_MCW, PT_MCB, PT_RMIX, PT_RW0, PT_RA0, PT_RKK, PT_RKA, PT_RRK, PT_RLNG, PT_RLNB, PT_MNG = 0, 32, 40, 54, 58, 62, 66, 70, 74, 78, 82
    pcol = lambda c: (ptab, ptab[:, c:c + 1])
    gb = k.sbuf([4, 2], F32, "gb")
    k.dma("sp", gb[:], d_gb[:], reads=[d_gb], writes=[gb])
    nbf = k.sbuf([4, 1], F32, "nbf")
    k.ts("dve", (nbf, nbf[:]), (gb, gb[:, 1:2]), -1.0, None, op0=ALU.mult)
    g1bc = k.sbuf([128, D], F32, "g1bc")
    k.dma("sp", g1bc[:], d_g1[:], reads=[d_g1], writes=[g1bc])

    NA = C_G
    hmT_all = k.sbuf([128, NT, 4, 128], BF16, "hmT_all")
    yrgT_all = k.sbuf([128, NT, 4, 128], BF16, "yrgT_all")
    mark_1a = k.bot
    W_in = k.sbuf([128, 8, NA], BF16, "W_in")
    Wl_w2 = k.sbuf([64, RW], BF16, "Wl_w2")
    Wl_a2 = k.sbuf([128, RW], BF16, "Wl_a2")
    Wl_g2 = k.sbuf([128, RW], BF16, "Wl_g2")
    GRP = {"g0": (0, 1024), "g1": (1024, 2056), "g2": (2056, 3848)}
    for g in ("g0", "g1", "g2"):
        a, b = GRP[g]
        for kh in range(2):
            k.dma("pool", W_in[:, 4 * kh:4 * kh + 4, a:b], d_w_in[:, 4 * kh:4 * kh + 4, a:b], reads=[d_w_in], writes=[(W_in, g)])
        if g == "g1":
            k.dma("pool", Wl_w2[:], d_rw2[:], reads=[d_rw2], writes=[Wl_w2])
            k.dma("pool", Wl_a2[:], d_ra2[:], reads=[d_ra2], writes=[Wl_a2])
            k.dma("pool", Wl_g2[:], d_rg2[:], reads=[d_rg2], writes=[Wl_g2])

    def wgrp(col):
        for g, (a, b) in GRP.items():
            if a <= col < b:
                return g

    pF = [k.psum([128, 512], F32, f"pF{i}") for i in range(2)]
    pR = [k.psum([128, 512], F32, f"pR{i}") for i in range(2)]
    pT = k.psum([128, 1024], BF16, "pT")
    pM = [k.psum([128, 512], F32, f"pM{i}") for i in range(3)]

    xt = [k.sbuf([128, D], F32, "xt0")] * 2
    hb = k.sbuf([128, D], BF16, "hb")
    hT = k.sbuf([128, 8, 128], BF16, "hT")
    ss = k.sbuf([128, 1], F32, "ss")
    rs = k.sbuf([128, 1], F32, "rs")
    ext_q = k.sbuf([128, 8, 131], F32, "ext_q")
    cq = k.sbuf([128, 8, 3], F32, "cq")
    _eqf = ext_q[:].rearrange("p a b -> p (a b)")
    cv = k.sbuf([128, 8, 128], F32, "cv")
    qkT = k.sbuf([128, 8, 128], BF16, "qkT")
    soT = k.sbuf([128, 4, 128], F32, "soT")
    vaug = k.sbuf([128, 4, 130], BF16, "vaug")
    Cst = k.sbuf([128, 4, 129], F32, "Cst")
    Cb = k.sbuf([128, 4, 130], BF16, "Cb")
    gsm = [k.sbuf([4, 128], F32, f"gsm{i}") for i in range(8)]
    gpk = k.sbuf([4, 3, 128], F32, "gpk")
    mst = k.sbuf([4, 16], F32, "mst")
    mnew = k.sbuf([4, 16], F32, "mnew")
    gt = [k.sbuf([4, 16], F32, f"gt{i}") for i in range(4)]
    s0d = k.sbuf([4, 4, 16], F32, "s0d")
    tokS = k.sbuf([128, 12], F32, "tokS")
    s0bc = k.sbuf([128, 64], F32, "s0bc")
    _pk = _eqf[:, 512:1024].bitcast(BF16).rearrange("p (a b c) -> p a b c", a=2, b=4)
    PTm = ext_q.view(_pk[:, 0, :, :], "PTm")
    ktm = ext_q.view(_pk[:, 1, :, :], "ktm")
    dn = k.sbuf([128, 4], F32, "dn")
    hm = ext_q.view(_eqf[:, 0:512].rearrange("p (a b) -> p a b", a=4), "hm")
    hn = hm
    bst = k.sbuf([128, 4, 6], F32, "bst")
    bag = k.sbuf([128, 4, 2], F32, "bag")
    zq_tm = cv.view(cv[:].rearrange("p a b -> p (a b)"), "zq_tm")

    ext_r = k.sbuf([128, 14, 129], F32, "ext_r")
    cr = k.sbuf([128, 14, 1], F32, "cr")
    _erf = ext_r[:].rearrange("p a b -> p (a b)")
    xm = k.sbuf([128, 14, 128], F32, "xm")
    thad = k.sbuf([128, 128], BF16, "thad")
    sgd = k.sbuf([128, 128], BF16, "sgd")
    bst8 = k.sbuf([128, 8, 6], F32, "bst8")
    bag8 = k.sbuf([128, 8, 2], F32, "bag8")
    mark_rw = k.bot
    rt = [k.sbuf([128, 4, 128], F32, f"rt{i}") for i in range(7)]
    rt.append(cv.view(cv[:, 0:4, :], "rt7"))
    rt.append(cv.view(cv[:, 4:8, :], "rt8"))
    gTs = ext_r.view(_erf[:, 0:512].rearrange("p (a b) -> p a b", a=4), "gTs")
    bonT = ext_r.view(_erf[:, 512:1024].rearrange("p (a b) -> p a b", a=4), "bonT")
    ART = k.sbuf([128, 4, 2, 128], BF16, "ART")
    BTb = k.sbuf([128, 4, 128], BF16, "BTb")
    KTb = k.sbuf([128, 4, 128], BF16, "KTb")
    VTb = k.sbuf([128, 4, 128], BF16, "VTb")
    AB_tm = k.sbuf([128, 2, 512], BF16, "AB_tm")
    KV_tm = k.sbuf([128, 2, 512], BF16, "KV_tm")
    GBm = k.sbuf([128, 4, 256], BF16, "GBm")
    GKm = k.sbuf([128, 4, 256], BF16, "GKm")
    Nn = k.sbuf([128, 4, 128], BF16, "Nn")
    GBm_b = k.sbuf([128, 4, 256], BF16, "GBm_b")
    GKm_b = k.sbuf([128, 4, 256], BF16, "GKm_b")
    Nn_b = k.sbuf([128, 4, 128], BF16, "Nn_b")
    PP_b = [k.sbuf([128, 4, 256], BF16, f"PPb{i}") for i in range(2)]
    XX_b = [k.sbuf([128, 4, 128], BF16, f"XXb{i}") for i in range(2)]
    PP = [k.sbuf([128, 4, 256], BF16, f"PP{i}") for i in range(2)]
    XX = [k.sbuf([128, 4, 128], BF16, f"XX{i}") for i in range(2)]
    QT = k.sbuf([128, 2, 128], BF16, "QT")
    IE = k.sbuf([128, 4, 64], F32, "IE")
    STf = k.sbuf([128, 4, 64], F32, "STf")
    STb = k.sbuf([128, 4, 64], BF16, "STb")
    yn = ext_r.view(_erf[:, 1024:1536].rearrange("p (a b) -> p a b", a=8), "yn")
    k.memset("pool", (STf, STf[:]), 0.0)
    k.memset("pool", (STb, STb[:]), 0.0)
    k.memset("pool", (cr, cr[:]), 0.0)
    k.memset("pool", (vaug, vaug[:]), 1.0)
    k.memset("pool", (Cst, Cst[:]), 0.0)
    k.memset("pool", (Cb, Cb[:]), 0.0)
    k.memset("pool", (mst, mst[:]), 0.0)
    k.memset("pool", (cq, cq[:]), 0.0)
    LNK = math.log(KSCALE)

    def x_rows(ti):
        if ti < NTP:
            return xp, xp[ti * 128:(ti + 1) * 128, :]
        return xs, xs[:, :]

    def norm_to_hT(xbuf, gbc):
        k.act((hb, hb[:]), (xbuf, xbuf[:]), AF.Square, accum=(ss, ss[:]))
        k.ts("dve", (rs, rs[:]), (ss, ss[:]), 1.0 / D, EPS, op0=ALU.mult, op1=ALU.add)
        k.act((rs, rs[:]), (rs, rs[:]), AF.Ln)
        k.act((rs, rs[:]), (rs, rs[:]), AF.Exp, scale=-0.5)
        k.stt("dve", (hb, hb[:]), (xbuf, xbuf[:]), (rs, rs[:, 0:1]), (gbc, gbc[:]), ALU.mult, ALU.mult)
        for kc in range(8):
            k.tr((pT, pT[:, kc * 128:(kc + 1) * 128]), (hb, hb[:, kc * 128:(kc + 1) * 128]), (identb, identb[:]))
        k.copy("act", (hT, hT[:].rearrange("p a b -> p (a b)")), (pT, pT[:, :]))

    def proj_fm(ps, ps_ap, col, M=128):
        g = wgrp(col)
        for kc in range(8):
            k.mm((ps, ps_ap), (W_in, W_in[:, kc, col:col + M], g), (hT, hT[:, kc, :]), start=(kc == 0), stop=(kc == 7))

    def proj_tm(ps, ps_ap, col, N):
        g = wgrp(col)
        for kc in range(8):
            k.mm((ps, ps_ap), (hT, hT[:, kc, :]), (W_in, W_in[:, kc, col:col + N], g), start=(kc == 0), stop=(kc == 7))

    prefetched = set()

    def conv_silu():
        for c in range(8):
            k.ts("dve", (cv, cv[:, c, :]), (ext_q, ext_q[:, c, 3:131]), pcol(PT_MCW + 3 * 8 + c), pcol(PT_MCB + c),
                 op0=ALU.mult, op1=ALU.add)
            for j in range(3):
                k.stt("dve", (cv, cv[:, c, :]), (ext_q, ext_q[:, c, j:j + 128]), pcol(PT_MCW + j * 8 + c), (cv, cv[:, c, :]),
                      ALU.mult, ALU.add)
        k.act((qkT, qkT[:].rearrange("p a b -> p (a b)")), (cv, cv[:].rearrange("p a b -> p (a b)")), AF.Silu)

    def prefetch_gen(tn):
        xb = xt[tn % 2]
        xd, xap = x_rows(tn)
        k.dma("sp", xb[:], xap, reads=[xd], writes=[xb])
        norm_to_hT(xb, g1bc)
        yield
        k.copy("pool", (ext_q, ext_q[:, :, 0:3]), (cq, cq[:]))
        for g in range(2):
            for c in range(4):
                proj_fm(pM[2], pM[2][:, c * 128:(c + 1) * 128], C_QK + (4 * g + c) * 128)
                if c == 1:
                    yield
            k.copy("act", (ext_q, ext_q[:, 4 * g:4 * g + 4, 3:131]), (pM[2], pM[2][:].rearrange("p (c t) -> p c t", c=4)))
            yield
        k.copy("pool", (cq, cq[:]), (ext_q, ext_q[:, :, 128:131]))
        yield
        proj_tm(pM[2], pM[2][:, :], C_V, 512)
        k.copy("act", (vaug, vaug[:, :, 0:128]), (pM[2], pM[2][:].rearrange("p (h c) -> p h c", h=4)))
        yield
        for c in range(4):
            proj_fm(pM[2], pM[2][:, c * 128:(c + 1) * 128], C_O + c * 128)
            if c == 1:
                yield
        k.act((soT, soT[:].rearrange("p a b -> p (a b)")), (pM[2], pM[2][:, :]), AF.Sigmoid)
        yield
        conv_silu()

    def mixer_tile(ti):
        smp = ti == NTP
        mi = 1 if smp else 0
        NB = NSB if smp else 1
        LB = ST if smp else 128
        xb = xt[ti % 2]
        xd, xap = x_rows(ti)
        if smp:
            k.barrier()
            k.bot = mark_rw
            Cs = k.sbuf([128, NSB, 129], F32, "Cs")
            Csb = k.sbuf([128, NSB, 130], BF16, "Csb")
            qTm = k.sbuf([128, NSB, 128], BF16, "qTm")
            ktmb = k.sbuf([128, NSB, 128], BF16, "ktmb")
            blkF = k.sbuf([128, NSB, 128], BF16, "blkF")
            rowm = k.sbuf([128, NSB], F32, "rowm")
            k.memset("pool", (blkF, blkF[:]), 1.0)
            aff(blkF, blkF[:], [[-ST, NSB], [1, 128]], 0, 0)
            aff(blkF, blkF[:], [[ST, NSB], [-1, 128]], 0, ST - 1)
            k.memset("pool", (rowm, rowm[:]), 1.0)
            aff(rowm, rowm[:], [[-ST, NSB]], 1, 0)
            aff(rowm, rowm[:], [[ST, NSB]], -1, ST - 1)
            smc = cv.view(cv[:].rearrange("p a b -> p (a b)")[0:NSB * 3, :], "smc")
            ext_s = xm.view(xm[:].rearrange("p a b -> p (a b)")[:, 0:8 * NSB * 11].rearrange("p (c b t) -> p c b t", c=8, b=NSB), "ext_s")
            k.dma("sp", smc[:], st_mconv[:, :], reads=[st_mconv], writes=[smc])
            for c in range(8):
                k.tr((pM[0], pM[0][:, c * 48:(c + 1) * 48]), (smc, smc[:, c * 128:(c + 1) * 128]), (identf, identf[0:48, 0:48]))
            k.copy("act", (ext_s, ext_s[:, :, :, 0:3]), (pM[0], pM[0][:, 0:384].rearrange("p (c b j) -> p c b j", c=8, b=NSB)))
            k.dma("sp", mst[:, 0:NSB], st_mm[:, :].rearrange("b h -> h b"), reads=[st_mm], writes=[mst], allow_slow_non_contiguous=True)
        if ti not in prefetched:
            k.dma("sp", xb[:], xap, reads=[xd], writes=[xb])
            norm_to_hT(xb, g1bc)

        if smp:
            for g in range(2):
                for c in range(4):
                    proj_fm(pF[g], pF[g][:, c * 128:(c + 1) * 128], C_QK + (4 * g + c) * 128)
                k.copy("act", (ext_s, ext_s[:, 4 * g:4 * g + 4, :, 3:11]), (pF[g], pF[g][:].rearrange("p (c b t) -> p c b t", c=4, b=NSB)))
            for c in range(8):
                cvv = cv[:, c, :].rearrange("p (b t) -> p b t", t=ST)
                k.ts("dve", (cv, cvv), (ext_s, ext_s[:, c, :, 3:11]), pcol(PT_MCW + 3 * 8 + c), pcol(PT_MCB + c),
                     op0=ALU.mult, op1=ALU.add)
                for j in range(3):
                    k.stt("dve", (cv, cvv), (ext_s, ext_s[:, c, :, j:j + ST]), pcol(PT_MCW + j * 8 + c), (cv, cvv),
                          ALU.mult, ALU.add)
        if not smp:
            if ti not in prefetched:
                k.copy("pool", (ext_q, ext_q[:, :, 0:3]), (cq, cq[:]))
                for g in range(2):
                    for c in range(4):
                        proj_fm(pF[g], pF[g][:, c * 128:(c + 1) * 128], C_QK + (4 * g + c) * 128)
                    k.copy("act", (ext_q, ext_q[:, 4 * g:4 * g + 4, 3:131]), (pF[g], pF[g][:].rearrange("p (c t) -> p c t", c=4)))
                k.copy("pool", (cq, cq[:]), (ext_q, ext_q[:, :, 128:131]))
            if ti not in prefetched:
                conv_silu()
        if smp:
            k.act((qkT, qkT[:].rearrange("p a b -> p (a b)")), (cv, cv[:].rearrange("p a b -> p (a b)")), AF.Silu)

        if ti not in prefetched:
            proj_tm(pR[0], pR[0][:, :], C_V, 512)
            k.copy("act", (vaug, vaug[:, :, 0:128]), (pR[0], pR[0][:].rearrange("p (h c) -> p h c", h=4)))
            for c in range(4):
                proj_fm(pF[0], pF[0][:, c * 128:(c + 1) * 128], C_O + c * 128)
            k.act((soT, soT[:].rearrange("p a b -> p (a b)")), (pF[0], pF[0][:, :]), AF.Sigmoid)
        if not smp and stage >= 2:
            rwkv_front_proj(ti)
        proj_fm(pM[0], pM[0][0:4, 0:128], C_I, M=4)
        proj_fm(pM[0], pM[0][0:4, 128:256], C_F, M=4)
        liT, nlf, ncum, gT_, t0, t1 = gsm[0], gsm[1], gsm[2], gsm[3], gsm[4], gsm[5]
        k.ts("dve", (liT, liT[:]), (pM[0], pM[0][0:4, 0:128]), (gb, gb[:, 0:1]), None, op0=ALU.add)
        k.act((t0, t0[:]), (pM[0], pM[0][0:4, 128:256]), AF.Exp, bias=(nbf, nbf[:, 0:1]), scale=-1.0)
        k.act((nlf, nlf[:]), (t0, t0[:]), AF.Ln, bias=1.0)
        k.scan("dve", (ncum, ncum[:]), (resets[mi], resets[mi][0:4, 0:128]), (nlf, nlf[:]), 0.0, ALU.mult, ALU.add)
        k.tt("dve", (gT_, gT_[:]), (liT, liT[:]), (ncum, ncum[:]), ALU.add)
        b3 = lambda buf: buf[:].rearrange("p (b l) -> p b l", l=LB)
        mcb_ = mst[:, 0:NB].unsqueeze(2).to_broadcast([4, NB, LB])
        nlast = ncum[:].rearrange("p (b l) -> p b l", l=LB)[:, :, LB - 1:LB]
        k.stt("dve", (t0, b3(t0)), (gT_, b3(gT_)), LNK, (mst, mcb_), ALU.add, ALU.subtract)
        k.act((gpk, gpk[:, 0, :]), (t0, t0[:]), AF.Exp)
        k.tt("dve", (t1, b3(t1)), (ncum, b3(ncum)), (mst, mcb_), ALU.subtract)
        k.act((gpk, gpk[:, 1, :]), (t1, t1[:]), AF.Exp)
        k.tt("dve", (t1, b3(t1)), (gT_, b3(gT_)), (ncum, nlast.to_broadcast([4, NB, LB])), ALU.subtract)
        k.red("dve", (gt[0], gt[0][:, 0:NB]), (t1, b3(t1)), ALU.max)
        k.tt("dve", (gt[1], gt[1][:, 0:NB]), (mst, mst[:, 0:NB]), (ncum, nlast.rearrange("p b o -> p (b o)")), ALU.subtract)
        k.tt("dve", (mnew, mnew[:, 0:NB]), (gt[1], gt[1][:, 0:NB]), (gt[0], gt[0][:, 0:NB]), ALU.max)
        k.tt("dve", (gt[2], gt[2][:, 0:NB]), (gt[1], gt[1][:, 0:NB]), (mnew, mnew[:, 0:NB]), ALU.subtract)
        k.act((gt[3], gt[3][:, 0:NB]), (gt[2], gt[2][:, 0:NB]), AF.Exp)
        k.tt("dve", (gpk, gpk[:, 2, :].rearrange("p (b l) -> p b l", l=LB)), (gpk, gpk[:, 0, :].rearrange("p (b l) -> p b l", l=LB)),
             (gt[3], gt[3][:, 0:NB].unsqueeze(2).to_broadcast([4, NB, LB])), ALU.mult)
        for j in range(3):
            k.tr((pM[1], pM[1][:, 4 * j:4 * j + 4]), (gpk, gpk[:, j, :]), (identf, identf[0:4, 0:4]))
        k.copy("dve", (tokS, tokS[:]), (pM[1], pM[1][:, 0:12]))
        k.tt("dve", (s0d, s0d[:, :, 0:NB]), (identf, identf[0:4, 0:4].unsqueeze(2).to_broadcast([4, 4, NB])),
             (gt[3], gt[3][:, 0:NB].unsqueeze(1).to_broadcast([4, 4, NB])), ALU.mult)
        k.mm((pM[1], pM[1][:, 16:16 + 4 * NB]), (ones4, ones4[:]), (s0d, s0d[:, :, 0:NB].rearrange("p a b -> p (a b)")))
        k.copy("dve", (s0bc, s0bc[:, 0:4 * NB]), (pM[1], pM[1][:, 16:16 + 4 * NB]))

        for h in range(4):
            k.mm((pM[0], pM[0][:, h * 128:(h + 1) * 128]), (qkT, qkT[:, 4 + h, :]), (qkT, qkT[:, h, :]))
        for h in range(4):
            k.stt("dve", (PTm, PTm[:, h, :]), (pM[0], pM[0][:, h * 128:(h + 1) * 128]), (tokS, tokS[:, h:h + 1]),
                  (mU_in[mi], mU_in[mi][:]), ALU.mult, ALU.mult)
        pO = [pM[1], pM[2]]
        oap = lambda h: pO[h // 2][:, 256 * (h % 2):256 * (h % 2) + 129]
        if not smp:
            for h in range(4):
                k.mm((pO[h // 2], oap(h)), (qkT, qkT[:, h, :]), (Cb, Cb[:, h, 0:129]), start=True, stop=False)
                k.mm((pO[h // 2], oap(h)), (PTm, PTm[:, h, :]), (vaug, vaug[:, h, 0:129]), start=False, stop=True)
        else:
            for h in range(4):
                k.tr((pT, pT[:, h * 128:(h + 1) * 128]), (qkT, qkT[:, 4 + h, :]), (identb, identb[:]))
            for h in range(4):
                k.ts("dve", (ktm, ktm[:, h, :]), (pT, pT[:, h * 128:(h + 1) * 128]), (tokS, tokS[:, 8 + h:9 + h]), None, op0=ALU.mult)
            for h in range(4):
                k.dma("sp", Cs[:, :, 0:128], st_mC[:, h, :, :].rearrange("b d v -> d b v"), reads=[st_mC], writes=[Cs])
                k.dma("sp", Cs[:, :, 128], st_mn[:, h, :].rearrange("b d -> d b"), reads=[st_mn], writes=[Cs], allow_slow_non_contiguous=True)
                k.copy("act", (Csb, Csb[:, :, 0:129]), (Cs, Cs[:]))
                k.tt("dve", (qTm, qTm[:]), (qkT, qkT[:, h, :].unsqueeze(1).to_broadcast([128, NSB, 128])), (blkF, blkF[:]), ALU.mult)
                for b in range(NSB):
                    k.mm((pO[h // 2], oap(h)), (qTm, qTm[:, b, :]), (Csb, Csb[:, b, 0:129]), start=(b == 0), stop=False)
                k.mm((pO[h // 2], oap(h)), (PTm, PTm[:, h, :]), (vaug, vaug[:, h, 0:129]), start=False, stop=True)
                k.tt("dve", (ktmb, ktmb[:]), (ktm, ktm[:, h, :].unsqueeze(1).to_broadcast([128, NSB, 128])),
                     (rowm, rowm[:].unsqueeze(2).to_broadcast([128, NSB, 128])), ALU.mult)
                for grp in range(4):
                    bank = pF[grp % 2]
                    for bi in range(4):
                        b = 4 * grp + bi
                        k.mm((bank, bank[:, bi * 128:(bi + 1) * 128]), (ktmb, ktmb[:, b, :]), (vaug, vaug[:, h, 0:128]))
                    for bi in range(4):
                        b = 4 * grp + bi
                        k.stt("dve", (Cs, Cs[:, b, 0:128]), (Cs, Cs[:, b, 0:128]), (s0bc, s0bc[:, h * NSB + b:h * NSB + b + 1]),
                              (bank, bank[:, bi * 128:(bi + 1) * 128]), ALU.mult, ALU.add)
                for b in range(NSB):
                    k.mm((pR[0], pR[0][:, b:b + 1]), (ktmb, ktmb[:, b, :]), (vaug, vaug[:, h, 128:129]))
                k.tt("dve", (Cs, Cs[:, :, 128]), (Cs, Cs[:, :, 128]), (s0bc, s0bc[:, h * NSB:(h + 1) * NSB]), ALU.mult)
                k.tt("dve", (Cs, Cs[:, :, 128]), (Cs, Cs[:, :, 128]), (pR[0], pR[0][:, 0:NSB]), ALU.add)
                k.dma("sp", o_sC[:, h, :, :].rearrange("b d v -> d b v"), Cs[:, :, 0:128], reads=[Cs], writes=[o_sC])
                k.dma("sp", o_sn[:, h, :].rearrange("b d -> d b"), Cs[:, :, 128], reads=[Cs], writes=[o_sn], allow_slow_non_contiguous=True)
        for h in range(4):
            k.copy("act", (dn, dn[:, h:h + 1]), (pO[h // 2], oap(h)[:, 128:129]))
        k.stt("dve", (dn, dn[:]), (dn, dn[:]), -1.0, (dn, dn[:]), ALU.mult, ALU.max)
        k.tt("dve", (dn, dn[:]), (dn, dn[:]), (tokS, tokS[:, 4:8]), ALU.max)
        k.op("dve", lambda e: e.reciprocal(out=dn[:], in_=dn[:]), reads=[dn], writes=[dn])
        for h in range(4):
            k.act((hm, hm[:, h, :]), (pO[h // 2], oap(h)[:, 0:128]), AF.Copy, scale=(dn, dn[:, h:h + 1]))
        for h in range(4):
            k.op("dve", lambda e, h=h: e.bn_stats(out=bst[:, h, :], in_=hm[:, h, :]), reads=[hm], writes=[(bst, h)])
        for h in range(4):
            k.op("dve", lambda e, h=h: e.bn_aggr(out=bag[:, h, :], in_=bst[:, h, :]), reads=[(bst, h)], writes=[(bag, h)])
        k.act((bag, bag[:, :, 1:2]), (bag, bag[:, :, 1:2]), AF.Ln, bias=EPS)
        k.act((bag, bag[:, :, 1:2]), (bag, bag[:, :, 1:2]), AF.Exp, scale=-0.5)
        for h in range(4):
            k.ts("dve", (hn, hn[:, h, :]), (hm, hm[:, h, :]), (bag, bag[:, h, 0:1]), (bag, bag[:, h, 1:2]),
                 op0=ALU.subtract, op1=ALU.mult)
        for h in range(4):
            k.tr((pM[0], pM[0][:, h * 128:(h + 1) * 128]), (hn, hn[:, h, :]), (identf, identf[:]))
        for h in range(4):
            k.stt("dve", (hmT_all, hmT_all[:, ti, h, :], ti), (pM[0], pM[0][:, h * 128:(h + 1) * 128]), pcol(PT_MNG + h),
                  (soT, soT[:, h, :]), ALU.mult, ALU.mult)
        if not smp:
            for h in range(4):
                k.tr((pT, pT[:, h * 128:(h + 1) * 128]), (qkT, qkT[:, 4 + h, :]), (identb, identb[:]))
            for h in range(4):
                k.ts("dve", (ktm, ktm[:, h, :]), (pT, pT[:, h * 128:(h + 1) * 128]), (tokS, tokS[:, 8 + h:9 + h]), None, op0=ALU.mult)
            for h in range(4):
                k.mm((pO[h // 2], oap(h)), (ktm, ktm[:, h, :]), (vaug, vaug[:, h, 0:129]))
            for h in range(4):
                k.stt("dve", (Cst, Cst[:, h, :]), (Cst, Cst[:, h, :]), (s0bc, s0bc[:, h:h + 1]), (pO[h // 2], oap(h)),
                      ALU.mult, ALU.add)
            k.copy("act", (Cb, Cb[:, :, 0:129]), (Cst, Cst[:]))
            k.copy("dve", (mst, mst[:, 0:1]), (mnew, mnew[:, 0:1]))
        if ti == NTP - 1:
            for h in range(4):
                k.dma("sp", o_pC[h], Cst[:, h, 0:128], reads=[Cst], writes=[o_pC])
            k.dma("sp", o_pn[:].rearrange("h d -> d h"), Cst[:, :, 128], reads=[Cst], writes=[o_pn], allow_slow_non_contiguous=True)
            k.dma("sp", o_pm[:].rearrange("o h -> h o"), mnew[:, 0:1], reads=[mnew], writes=[o_pm], allow_slow_non_contiguous=True)
            for blk in range(2):
                proj_tm(pR[blk], pR[blk][:, :], C_QK + blk * 512, 512)
                k.copy("act", (zq_tm, zq_tm[:, blk * 512:(blk + 1) * 512]), (pR[blk], pR[blk][:, :]))
            k.dma("sp", o_pconv[:], zq_tm[125:128, :], reads=[zq_tm], writes=[o_pconv])
        if smp:
            k.dma("sp", o_sm[:, :].rearrange("b h -> h b"), mnew[:, 0:NSB], reads=[mnew], writes=[o_sm], allow_slow_non_contiguous=True)
            for blk in range(2):
                proj_tm(pR[blk], pR[blk][:, :], C_QK + blk * 512, 512)
                k.copy("act", (zq_tm, zq_tm[:, blk * 512:(blk + 1) * 512]), (pR[blk], pR[blk][:, :]))
            for b in range(NSB):
                k.dma("sp", o_sconv[3 * b:3 * b + 3, :], zq_tm[ST * b + 5:ST * b + 8, :], reads=[zq_tm], writes=[o_sconv])

    k.fence_mm = (pT, identb)
    BK = [pM[0], pM[1], pF[0], pF[1], pR[0], pR[1]]

    _rw_stop = int(_os.environ.get("KDBG_RW", "99"))

    def rwkv_front_proj(ti):
        k.copy("pool", (ext_r, ext_r[:, :, 0:1]), (cr, cr[:]))
        for g in range(4):
            n = min(4, 14 - 4 * g)
            for c in range(n):
                proj_fm(pF[g % 2], pF[g % 2][:, c * 128:(c + 1) * 128], C_R + (4 * g + c) * 128)
            k.copy("act", (ext_r, ext_r[:, 4 * g:4 * g + n, 1:129]),
                   (pF[g % 2], pF[g % 2][:, 0:n * 128].rearrange("p (c t) -> p c t", c=n)))
        k.copy("pool", (cr, cr[:]), (ext_r, ext_r[:, :, 128:129]))
        k.tt("pool", (xm, xm[:]), (ext_r, ext_r[:, :, 0:128]), (ext_r, ext_r[:, :, 1:129]), ALU.subtract)
        for c in range(14):
            k.stt("dve", (xm, xm[:, c, :]), (xm, xm[:, c, :]), pcol(PT_RMIX + c), (ext_r, ext_r[:, c, 1:129]), ALU.mult, ALU.add)

    def rwkv_tile(ti):
        smp = ti == NTP
        mi = 0
        NLV = 7
        rtl = rt
        if not smp:
            pass
        else:
            k.barrier()
            k.bot = mark_rw
            ext_rs = k.sbuf([128, 14, NSB, ST + 1], F32, "ext_rs")
            rtl = [k.sbuf([128, 4, 128], F32, f"rts{i}") for i in range(7)] + [rt[7], rt[8]]
            stg = k.sbuf([128, 512], F32, "stg")
            srs = xm.view(xm[:].rearrange("p a b -> p (a b)")[0:NSB, :], "srs")
            k.dma("sp", srs[:], st_rshift[:, :], reads=[st_rshift], writes=[srs])
            for c in range(14):
                k.tr((pM[0], pM[0][:, c * NSB:(c + 1) * NSB]), (srs, srs[:, c * 128:(c + 1) * 128]), (identf, identf[0:NSB, 0:NSB]))
            k.copy("act", (ext_rs, ext_rs[:, :, :, 0]), (pM[0], pM[0][:, 0:14 * NSB].rearrange("p (c b) -> p c b", c=14)))
            for g in range(4):
                n = min(4, 14 - 4 * g)
                for c in range(n):
                    proj_fm(pF[g % 2], pF[g % 2][:, c * 128:(c + 1) * 128], C_R + (4 * g + c) * 128)
                k.copy("act", (ext_rs, ext_rs[:, 4 * g:4 * g + n, :, 1:ST + 1]),
                       (pF[g % 2], pF[g % 2][:, 0:n * 128].rearrange("p (c b t) -> p c b t", c=n, b=NSB)))
            xm4 = xm[:].rearrange("p c (b t) -> p c b t", t=ST)
            k.tt("pool", (xm, xm4), (ext_rs, ext_rs[:, :, :, 0:ST]), (ext_rs, ext_rs[:, :, :, 1:ST + 1]), ALU.subtract)
            for c in range(14):
                k.stt("dve", (xm, xm4[:, c]), (xm, xm4[:, c]), pcol(PT_RMIX + c), (ext_rs, ext_rs[:, c, :, 1:ST + 1]), ALU.mult, ALU.add)
        rT, krT, vrT = xm[:, 0:4, :], xm[:, 4:8, :], xm[:, 8:12, :]
        sig, cums, gam, ginv, gexc, a_, kk, tmp, kr2 = rtl
        if _rw_stop <= 1:
            return
        k.act((thad, thad[0:64, :]), (xm, xm[0:64, 12, :]), AF.Tanh)
        k.copy("act", (thad, thad[64:128, :]), (xm, xm[64:128, 12, :]))
        k.act((sgd, sgd[:]), (xm, xm[:, 13, :]), AF.Sigmoid)
        for c in range(4):
            k.mm((pM[0], pM[0][:, c * 128:(c + 1) * 128]), (Wl_w2, Wl_w2[0:64, c * 128:(c + 1) * 128]), (thad, thad[0:64, :]))
        for c in range(4):
            k.act((sig, sig[:, c, :]), (pM[0], pM[0][:, c * 128:(c + 1) * 128]), AF.Sigmoid, bias=pcol(PT_RW0 + c))
        k.pe_fence()
        for c in range(4):
            k.mm((pM[1], pM[1][:, c * 128:(c + 1) * 128]), (Wl_a2, Wl_a2[64:128, c * 128:(c + 1) * 128]), (thad, thad[64:128, :]))
        k.pe_fence()
        for c in range(4):
            k.act((a_, a_[:, c, :]), (pM[1], pM[1][:, c * 128:(c + 1) * 128]), AF.Sigmoid, bias=pcol(PT_RA0 + c))
        for c in range(4):
            k.mm((pM[2], pM[2][:, c * 128:(c + 1) * 128]), (Wl_g2, Wl_g2[:, c * 128:(c + 1) * 128]), (sgd, sgd[:]))
        k.copy("act", (gTs, gTs[:].rearrange("p a b -> p (a b)")), (pM[2], pM[2][:, :]))
        if _rw_stop <= 2:
            return
        fl = lambda b: b[:].rearrange("p a b -> p (a b)")
        if not smp:
            k.scan("dve", (cums, fl(cums)), (resets[mi], resets[mi][:]), (sig, fl(sig)), 0.0, ALU.mult, ALU.add)
            k.act((gam, fl(gam)), (cums, fl(cums)), AF.Exp, scale=WSCALE)
            k.act((ginv, fl(ginv)), (cums, fl(cums)), AF.Exp, scale=-WSCALE)
            k.tt("pool", (tmp, tmp[:]), (cums, cums[:]), (sig, sig[:]), ALU.subtract)
            k.act((gexc, fl(gexc)), (tmp, fl(tmp)), AF.Exp, scale=WSCALE)
        else:
            k.act((gam, fl(gam)), (sig, fl(sig)), AF.Exp, scale=WSCALE)
        if _rw_stop <= 3:
            return
        for c in range(4):
            k.ts("dve", (kk, kk[:, c, :]), (xm, xm[:, 4 + c, :]), pcol(PT_RKK + c), None, op0=ALU.mult)
        k.tt("pool", (tmp, tmp[:]), (kk, kk[:]), (kk, kk[:]), ALU.mult)
        for c in range(4):
            k.mm((pM[0], pM[0][:, c * 128:(c + 1) * 128]), (bones, bones[:]), (tmp, tmp[:, c, :]))
        k.ts("dve", (tmp, fl(tmp)), (pM[0], pM[0][:, :]), 1e-24, None, op0=ALU.max)
        k.act((tmp, fl(tmp)), (tmp, fl(tmp)), AF.Ln)
        k.act((tmp, fl(tmp)), (tmp, fl(tmp)), AF.Exp, scale=-0.5)
        k.tt("dve", (kk, kk[:]), (kk, kk[:]), (tmp, tmp[:]), ALU.mult)
        for c in range(4):
            k.ts("dve", (tmp, tmp[:, c, :]), (a_, a_[:, c, :]), -1.0, pcol(PT_RKA + c), op0=ALU.add, op1=ALU.mult)
        k.stt("dve", (kr2, kr2[:]), (tmp, tmp[:]), 1.0, (xm, krT), ALU.add, ALU.mult)
        k.tt("pool", (tmp, tmp[:]), (xm, rT), (kr2, kr2[:]), ALU.mult)
        for c in range(4):
            k.ts("dve", (tmp, tmp[:, c, :]), (tmp, tmp[:, c, :]), pcol(PT_RRK + c), None, op0=ALU.mult)
        for c in range(4):
            k.mm((pM[1], pM[1][:, c * 128:(c + 1) * 128]), (bones, bones[:]), (tmp, tmp[:, c, :]))
        k.tt("dve", (bonT, fl(bonT)), (pM[1], pM[1][:, :]), (xm, vrT.rearrange("p a b -> p (a b)") if False else xm[:, 8:12, :].rearrange("p a b -> p (a b)")), ALU.mult)
        if _rw_stop <= 4:
            return
        if smp:
            rwkv_sample_core(xm, gam, kr2, kk, a_, tmp, stg, gTs, bonT)
            return
        k.stt("dve", (ART, ART[:, :, 0, :]), (kk, kk[:]), -1.0, (gexc, gexc[:]), ALU.mult, ALU.mult)
        k.tt("pool", (ART, ART[:, :, 1, :]), (xm, rT), (gam, gam[:]), ALU.mult)
        k.tt("pool", (tmp, tmp[:]), (kk, kk[:]), (a_, a_[:]), ALU.mult)
        k.tt("dve", (BTb, BTb[:]), (tmp, tmp[:]), (ginv, ginv[:]), ALU.mult)
        k.tt("pool", (KTb, KTb[:]), (kr2, kr2[:]), (ginv, ginv[:]), ALU.mult)
        k.copy("act", (VTb, VTb[:]), (xm, vrT))
        if _rw_stop <= 5:
            return
        for c in range(4):
            k.tr((pT, pT[:, c * 128:(c + 1) * 128]), (ART, ART[:, c, 0, :]), (identb, identb[:]))
            k.tr((pT, pT[:, 512 + c * 128:512 + (c + 1) * 128]), (BTb, BTb[:, c, :]), (identb, identb[:]))
        k.copy("act", (AB_tm, AB_tm[:].rearrange("p a b -> p (a b)")), (pT, pT[:, :]))
        for c in range(4):
            k.tr((pT, pT[:, c * 128:(c + 1) * 128]), (KTb, KTb[:, c, :]), (identb, identb[:]))
            k.tr((pT, pT[:, 512 + c * 128:512 + (c + 1) * 128]), (VTb, VTb[:, c, :]), (identb, identb[:]))
        k.copy("dve", (KV_tm, KV_tm[:].rearrange("p a b -> p (a b)")), (pT, pT[:, :]))
        if _rw_stop <= 6:
            return
        A_tm = lambda h: (AB_tm, AB_tm[:, 0, h * 64:(h + 1) * 64])
        B_tm = lambda h: (AB_tm, AB_tm[:, 1, h * 64:(h + 1) * 64])
        K_tm = lambda h: (KV_tm, KV_tm[:, 0, h * 64:(h + 1) * 64])
        V_tm = lambda h: (KV_tm, KV_tm[:, 1, h * 64:(h + 1) * 64])
        m2b = mask2[mi][:].unsqueeze(1).to_broadcast([128, 2, 256])
        GB2, GK2, Nn2, PP2, XX2 = [GBm, GBm_b], [GKm, GKm_b], [Nn, Nn_b], [PP, PP_b], [XX, XX_b]
        LB3 = [[pM[0], pM[1], pR[0]], [pF[0], pF[1], pR[1]]]
        for g in range(2):
            GBm_, GKm_, Nn_, XX_ = GB2[g], GK2[g], Nn2[g], XX2[g]
            heads = [4 * g + i for i in range(4)]
            HO = [(pbs, [(i, h) for i, h in enumerate(heads) if 64 * (h % 2) == pbs]) for pbs in (0, 64)]
            for pbs, hl in HO:
                for i, h in hl:
                    c, pb = h // 2, 64 * (h % 2)
                    off = (i % 2) * 256
                    rAR = (ART, ART[pb:pb + 64, c, :, :].rearrange("p a t -> p (a t)"))
                    k.mm((BK[i // 2], BK[i // 2][:, off:off + 256]), (BTb, BTb[pb:pb + 64, c, :]), rAR)
                    k.mm((BK[2 + i // 2], BK[2 + i // 2][:, off:off + 256]), (KTb, KTb[pb:pb + 64, c, :]), rAR)
                    k.mm((BK[4], BK[4][:, i * 128:(i + 1) * 128]), (ART, ART[pb:pb + 64, c, 0, :]), (BTb, BTb[pb:pb + 64, c, :]))
                k.pe_fence()
            for hf in range(2):
                k.tt("dve", (GBm_, GBm_[:, 2 * hf:2 * hf + 2, :]), (BK[hf], BK[hf][:].rearrange("p (a b) -> p a b", a=2)), (mask2[mi], m2b), ALU.mult)
                k.tt("dve", (GKm_, GKm_[:, 2 * hf:2 * hf + 2, :]), (BK[2 + hf], BK[2 + hf][:].rearrange("p (a b) -> p a b", a=2)), (mask2[mi], m2b), ALU.mult)
            k.tt("dve", (Nn_, Nn_[:]), (BK[4], BK[4][:].rearrange("p (a b) -> p a b", a=4)),
                 (mL_st[mi], mL_st[mi][:].unsqueeze(1).to_broadcast([128, 4, 128])), ALU.mult)
            for i, h in enumerate(heads):
                k.mm((BK[5], BK[5][:, i * 64:(i + 1) * 64]), (GKm_, GKm_[:, i, 0:128]), V_tm(h))
            k.copy("act", (XX_[0], XX_[0][:, :, 64:128]), (BK[5], BK[5][:, 0:256].rearrange("p (a b) -> p a b", a=4)))
            k.copy("pool", (XX_[0], XX_[0][:, :, 0:64]), (AB_tm, AB_tm[:, 0, 256 * g:256 * g + 256].rearrange("p (a b) -> p a b", a=4)))

        xfinal = [None, None]

        def levels_gen(g):
            GBm_, Nn_, PP_, XX_ = GB2[g], Nn2[g], PP2[g], XX2[g]
            bP, bQ, bX = LB3[g]
            Pc = lambda i: (Nn_, Nn_[:, i, :])
            PTc = lambda i: (GBm_, GBm_[:, i, 0:128])
            xi = 0
            for lvl in range(NLV):
                Xc, Xn = XX_[xi], XX_[1 - xi]
                for i in range(4):
                    o = (bX, bX[:, i * 128:(i + 1) * 128])
                    k.mm(o, (identb, identb[:]), (Xc, Xc[:, i, :]), start=True, stop=False)
                    k.mm(o, PTc(i), (Xc, Xc[:, i, :]), start=False, stop=True)
                k.copy("act", (Xn, Xn[:].rearrange("p a b -> p (a b)")), (bX, bX[:, :]))
                xi = 1 - xi
                yield
                if lvl < NLV - 1:
                    bb = [bP, bQ]
                    for i in range(4):
                        off = (i % 2) * 256
                        if lvl < NLV - 2:
                            k.mm((bb[i // 2], bb[i // 2][:, off:off + 128]), PTc(i), Pc(i))
                        k.mm((bb[i // 2], bb[i // 2][:, off + 128:off + 256]), Pc(i), PTc(i))
                    PPn = PP_[lvl % 2]
                    for hf in range(2):
                        if lvl < NLV - 2:
                            k.copy("dve", (PPn, PPn[:, 2 * hf:2 * hf + 2, :]), (bb[hf], bb[hf][:].rearrange("p (a b) -> p a b", a=2)))
                        else:
                            k.copy("dve", (PPn, PPn[:, 2 * hf:2 * hf + 2, 128:256]),
                                   (bb[hf], bb[hf][:].rearrange("p (a b) -> p a b", a=2)[:, :, 128:256]))
                    Pc = lambda i, PPn=PPn: (PPn, PPn[:, i, 0:128])
                    PTc = lambda i, PPn=PPn: (PPn, PPn[:, i, 128:256])
                    yield
            xfinal[g] = XX_[xi]

        gens = [levels_gen(0), levels_gen(1)]
        if ti + 1 < NTP and (ti + 1) in tiles_run:
            gens.append(prefetch_gen(ti + 1))
            prefetched.add(ti + 1)
        while gens:
            for g_ in list(gens):
                try:
                    next(g_)
                except StopIteration:
                    gens.remove(g_)

        for g in range(2):
            heads = [4 * g + i for i in range(4)]
            HO = [(pbs, [(i, h) for i, h in enumerate(heads) if 64 * (h % 2) == pbs]) for pbs in (0, 64)]
            GBt, GKt = GB2[g], GK2[g]
            Xf = xfinal[g]
            if _rw_stop <= 8:
                continue
            k.pe_fence()
            for pbs, hl in HO:
                for i, h in hl:
                    c, pb = h // 2, 64 * (h % 2)
                    ci = i // 2
                    o = (BK[0], BK[0][pb:pb + 64, ci * 128:(ci + 1) * 128])
                    k.mm(o, (Xf, Xf[:, i, 0:64]), (GBt, GBt[:, i, 128:256]), start=True, stop=False)
                    k.pe_fence()
                    k.mm(o, (identb, identb[pb:pb + 64, pb:pb + 64]), (ART, ART[pb:pb + 64, c, 1, :]), start=False, stop=True)
                    k.pe_fence()
            k.copy("act", (QT, QT[:].rearrange("p a b -> p (a b)")), (BK[0], BK[0][:, 0:256]))
            for pbs, hl in HO:
                for i, h in hl:
                    c, pb = h // 2, 64 * (h % 2)
                    ci = i // 2
                    o = (pM[2], pM[2][:, h * 64:(h + 1) * 64])
                    k.mm(o, (QT, QT[pb:pb + 64, ci, :]), (STb, STb[pb:pb + 64, c, :]), start=True, stop=False)
                    k.pe_fence()
                    k.mm(o, (GBt, GBt[:, i, 128:256]), (Xf, Xf[:, i, 64:128]), start=False, stop=False)
                    k.mm(o, (GKt, GKt[:, i, 128:256]), V_tm(h), start=False, stop=True)
                    k.pe_fence()
            if _rw_stop <= 9:
                continue
            for pbs, hl in HO:
                for i, h in hl:
                    c, pb = h // 2, 64 * (h % 2)
                    ci = i // 2
                    k.mm((BK[1], BK[1][pb:pb + 64, ci * 64:(ci + 1) * 64]), (Xf, Xf[:, i, 0:64]), B_tm(h))
                k.pe_fence()
            k.tt("dve", (IE, IE[:, 2 * g:2 * g + 2, :]), (BK[1], BK[1][:, 0:128].rearrange("p (a b) -> p a b", a=2)),
                 (I2, I2[:].unsqueeze(1).to_broadcast([128, 2, 64])), ALU.add)
            for pbs, hl in HO:
                for i, h in hl:
                    c, pb = h // 2, 64 * (h % 2)
                    ci = i // 2
                    o = (BK[2], BK[2][pb:pb + 64, ci * 64:(ci + 1) * 64])
                    k.mm(o, (IE, IE[pb:pb + 64, c, :]), (STf, STf[pb:pb + 64, c, :]), start=True, stop=False)
                    k.pe_fence()
                    k.mm(o, B_tm(h), (Xf, Xf[:, i, 64:128]), start=False, stop=False)
                    k.mm(o, K_tm(h), V_tm(h), start=False, stop=True)
                    k.pe_fence()
            for ci in range(2):
                c = 2 * g + ci
                k.ts("dve", (STf, STf[:, c, :]), (BK[2], BK[2][:, ci * 64:(ci + 1) * 64]), (gam, gam[:, c, 127:128]), None, op0=ALU.mult)
            k.copy("act", (STb, STb[:, 2 * g:2 * g + 2, :]), (STf, STf[:, 2 * g:2 * g + 2, :]))
        if _rw_stop <= 10:
            return
        rwkv_epilogue(ti, pM[2], rt[0])
        if ti == NTP - 1:
            for c in range(4):
                k.tr((pM[0], pM[0][0:64, c * 128:(c + 1) * 128]), (STf, STf[:, c, :]), (identf, identf[:]))
            k.copy("act", (rt[0], rt[0][0:64, :, :]), (pM[0], pM[0][0:64, :].rearrange("p (a b) -> p a b", a=4)))
            k.dma("sp", o_pS[:].rearrange("(h i) j -> i h j", h=8), rt[0][0:64, :, :].rearrange("p c (f j) -> p (c f) j", f=2),
                  reads=[rt[0]], writes=[o_pS])

    def rwkv_epilogue(ti, Yb, tmp):
        pM2 = [None, None, Yb]
        for h in range(8):
            k.op("dve", lambda e, h=h: e.bn_stats(out=bst8[:, h, :], in_=Yb[:, h * 64:(h + 1) * 64]), reads=[Yb], writes=[(bst8, h)])
        for h in range(8):
            k.op("dve", lambda e, h=h: e.bn_aggr(out=bag8[:, h, :], in_=bst8[:, h, :]), reads=[(bst8, h)], writes=[(bag8, h)])
        k.act((bag8, bag8[:, :, 1:2]), (bag8, bag8[:, :, 1:2]), AF.Ln, bias=GN_EPS)
        k.act((bag8, bag8[:, :, 1:2]), (bag8, bag8[:, :, 1:2]), AF.Exp, scale=-0.5)
        for h in range(8):
            k.ts("dve", (yn, yn[:, h, :]), (Yb, Yb[:, h * 64:(h + 1) * 64]), (bag8, bag8[:, h, 0:1]), (bag8, bag8[:, h, 1:2]),
                 op0=ALU.subtract, op1=ALU.mult)
        for c in range(4):
            k.tr((pM[0], pM[0][:, c * 128:(c + 1) * 128]), (yn, yn[:, 2 * c:2 * c + 2, :].rearrange("p a b -> p (a b)")), (identf, identf[:]))
        for c in range(4):
            k.ts("dve", (tmp, tmp[:, c, :]), (pM[0], pM[0][:, c * 128:(c + 1) * 128]), pcol(PT_RLNG + c), pcol(PT_RLNB + c),
                 op0=ALU.mult, op1=ALU.add)
        k.tt("pool", (tmp, tmp[:]), (tmp, tmp[:]), (bonT, bonT[:]), ALU.add)
        k.tt("dve", (yrgT_all, yrgT_all[:, ti, :, :], ti), (tmp, tmp[:]), (gTs, gTs[:]), ALU.mult)

    rsc = k.dram("rw_scratch", [6, 128, RW], F32)
    ysc = k.dram("ry_scratch", [128, RW], F32)

    def rwkv_sample_core(xm, dec, kr2, kk, a_, tmp, stg, gTs, bonT):
        ti = NTP
        srcs = []
        srcs.append((xm, lambda c: xm[:, c, :]))
        srcs.append((dec, lambda c: dec[:, c, :]))
        srcs.append((kr2, lambda c: kr2[:, c, :]))
        srcs.append((xm, lambda c: xm[:, 8 + c, :]))
        for q in range(6):
            if q == 4:
                k.ts("dve", (tmp, tmp[:]), (kk, kk[:]), -1.0, None, op0=ALU.mult)
                sb_, fn = tmp, (lambda c: tmp[:, c, :])
            elif q == 5:
                k.tt("dve", (tmp, tmp[:]), (kk, kk[:]), (a_, a_[:]), ALU.mult)
                sb_, fn = tmp, (lambda c: tmp[:, c, :])
            else:
                sb_, fn = srcs[q]
            pb_ = pM[q % 2]
            for c in range(4):
                k.tr((pb_, pb_[:, c * 128:(c + 1) * 128]), (sb_, fn(c)), (identf, identf[:]))
            k.copy("act", (stg, stg[:]), (pb_, pb_[:, :]))
            k.dma("sp", rsc[q], stg[:], reads=[stg], writes=[(rsc, q)])
        for blk, (c0, n) in enumerate(((0, 512), (512, 512), (1024, 512), (1536, 256))):
            proj_tm(pR[blk % 2], pR[blk % 2][:, 0:n], C_R + c0, n)
            k.copy("act", (stg, stg[:, 0:n]), (pR[blk % 2], pR[blk % 2][:, 0:n]))
            for b in range(NSB):
                k.dma("sp", o_sshift[b:b + 1, c0:c0 + n], stg[ST * b + ST - 1:ST * b + ST, 0:n], reads=[stg], writes=[o_sshift])
        k.barrier()
        k.bot = mark_rw
        vec6 = k.sbuf([128, 6, ST, RN], F32, "vec6")
        Ssb = k.sbuf([128, RN, RN], F32, "Ssb")
        tmpS = k.sbuf([128, RN, RN], F32, "tmpS")
        sa = k.sbuf([128, RN], F32, "sa")
        ys = k.sbuf([128, ST, RN], F32, "ys")
        Ytm = k.sbuf([128, RW], F32, "Ytm")
        k.dma("sp", Ssb[:].rearrange("p a b -> p (a b)"), st_rS[:, :], reads=[st_rS], writes=[Ssb])
        for q in range(6):
            for b in range(NSB):
                k.dma("sp", vec6[RH * b:RH * b + RH, q, :, :], rsc[q, ST * b:ST * b + ST, :].rearrange("t (h j) -> h t j", h=RH),
                      reads=[(rsc, q)], writes=[(vec6, q)])
        HV = RN // 2

        def rec_gen(hf):
            i0 = hf * HV
            S_ = (Ssb, Ssb[:, i0:i0 + HV, :], hf)
            T_ = (tmpS, tmpS[:, i0:i0 + HV, :], hf)
            bc = lambda q, t: (vec6, vec6[:, q, t, :].unsqueeze(1).to_broadcast([128, HV, RN]), q)
            for t in range(ST):
                k.tt("dve", T_, S_, bc(4, t), ALU.mult)
                yield
                k.red("dve", (sa, sa[:, i0:i0 + HV], hf), T_, ALU.add)
                yield
                k.tt("pool", S_, S_, bc(1, t), ALU.mult)
                yield
                k.tt("dve", T_, (sa, sa[:, i0:i0 + HV].unsqueeze(2).to_broadcast([128, HV, RN]), hf), bc(5, t), ALU.mult)
                yield
                k.tt("dve", S_, S_, T_, ALU.add)
                yield
                k.tt("pool", T_, (vec6, vec6[:, 3, t, i0:i0 + HV].unsqueeze(2).to_broadcast([128, HV, RN]), 3), bc(2, t), ALU.mult)
                yield
                k.tt("dve", S_, S_, T_, ALU.add)
                yield
                k.tt("pool", T_, S_, bc(0, t), ALU.mult)
                yield
                k.red("dve", (ys, ys[:, t, i0:i0 + HV], (hf, t)), T_, ALU.add)
                yield

        gens_ = [rec_gen(0), rec_gen(1)]
        while gens_:
            for g_ in list(gens_):
                try:
                    next(g_)
                except StopIteration:
                    gens_.remove(g_)
        k.dma("sp", o_sS[:, :], Ssb[:].rearrange("p a b -> p (a b)"), reads=[Ssb], writes=[o_sS])
        k.dma("sp", ysc[:, :], ys[:].rearrange("p a b -> p (a b)"), reads=[ys], writes=[ysc])
        for b in range(NSB):
            k.dma("sp", Ytm[ST * b:ST * b + ST, :].rearrange("t (h i) -> t h i", h=RH),
                  ysc[RH * b:RH * b + RH, :].rearrange("h (t i) -> t h i", t=ST), reads=[ysc], writes=[Ytm])
        rwkv_epilogue(ti, Ytm, rt[7])

    def tail_rows(ti):
        if ti != NTP - 1:
            return
        for blk, (c0, n) in enumerate(((0, 512), (512, 512), (1024, 512), (1536, 256))):
            proj_tm(pR[blk % 2], pR[blk % 2][:, 0:n], C_R + c0, n)
            k.copy("act", (rt[1], rt[1][96:128, :, :].rearrange("p a b -> p (a b)")[:, 0:n]), (pR[blk % 2], pR[blk % 2][96:128, 0:n]))
            k.dma("sp", o_pshift[0:1, c0:c0 + n], rt[1][127:128, :, :].rearrange("p a b -> p (a b)")[:, 0:n], reads=[rt[1]], writes=[o_pshift])

    tiles_all = list(range(NT)) if stage >= 5 else list(range(NTP))

    def phase_1b(k):
        k.barrier()
        k.bot = mark_1a
        W_g = k.sbuf([128, 8, 2048], BF16, "W_g")
        W_bm = k.sbuf([128, 4, D], BF16, "W_bm")
        W_br = k.sbuf([128, 4, D], BF16, "W_br")
        W_out = k.sbuf([128, 8, D], BF16, "W_out")
        k.dma("pool", W_bm[:], d_w_bm[:], reads=[d_w_bm], writes=[W_bm])
        for kh in range(2):
            k.dma("pool", W_g[:, 4 * kh:4 * kh + 4, 0:1024], d_w_in[:, 4 * kh:4 * kh + 4, C_G:C_G + 1024], reads=[d_w_in], writes=[(W_g, "a")])
        k.dma("pool", W_br[:], d_w_br[:], reads=[d_w_br], writes=[W_br])
        for kh in range(2):
            k.dma("pool", W_g[:, 4 * kh:4 * kh + 4, 1024:2048], d_w_in[:, 4 * kh:4 * kh + 4, C_G + 1024:C_G + 2048], reads=[d_w_in], writes=[(W_g, "b")])
        k.dma("pool", W_out[:], d_w_out[:], reads=[d_w_out], writes=[W_out])
        xtb = [k.sbuf([128, D], F32, f"xtb{i}") for i in range(2)]
        hb2 = k.sbuf([128, D], BF16, "hb2")
        hT2 = k.sbuf([128, 8, 128], BF16, "hT2")
        ss2 = k.sbuf([128, 1], F32, "ss2")
        rs2 = k.sbuf([128, 1], F32, "rs2")
        sgb = [k.sbuf([128, 512], F32, f"sgb{i}") for i in range(2)]
        yab = k.sbuf([128, D], F32, "yab")
        mg = [k.sbuf([128, D], BF16, f"mg{i}") for i in range(2)]
        mT = k.sbuf([128, 8, 128], BF16, "mT")
        def head_gen(ti):
            xb = xtb[ti % 2]
            xd, xap = x_rows(ti)
            k.dma("sp", xb[:], xap, reads=[xd], writes=[xb])
            norm_generic(xb, g1bc, hb2, hT2, ss2, rs2)
            yield
            for half, (Wb, src, key) in enumerate(((W_bm, hmT_all, "a"), (W_br, yrgT_all, "b"))):
                for blk in range(2):
                    for kc in range(4):
                        k.mm((pR[blk], pR[blk][:, :]), (src, src[:, ti, kc, :], ti), (Wb, Wb[:, kc, blk * 512:(blk + 1) * 512]),
                             start=(kc == 0), stop=(kc == 3))
                    col = half * 1024 + blk * 512
                    for kc in range(8):
                        k.mm((pF[blk], pF[blk][:, :]), (hT2, hT2[:, kc, :]), (W_g, W_g[:, kc, col:col + 512], key),
                             start=(kc == 0), stop=(kc == 7))
                    k.act((sgb[blk], sgb[blk][:]), (pF[blk], pF[blk][:, :]), AF.Sigmoid)
                    if half == 0:
                        k.tt("dve", (yab, yab[:, blk * 512:(blk + 1) * 512]), (sgb[blk], sgb[blk][:]), (pR[blk], pR[blk][:, :]), ALU.mult)
                    else:
                        k.tt("dve", (sgb[blk], sgb[blk][:]), (sgb[blk], sgb[blk][:]), (pR[blk], pR[blk][:, :]), ALU.mult)
                        k.tt("pool", (mg[ti % 2], mg[ti % 2][:, blk * 512:(blk + 1) * 512]), (sgb[blk], sgb[blk][:]), (yab, yab[:, blk * 512:(blk + 1) * 512]), ALU.add)
                    yield

        def tailb_gen(ti):
            xb = xtb[ti % 2]
            mgt = mg[ti % 2]
            for kc in range(8):
                k.tr((pT, pT[:, kc * 128:(kc + 1) * 128]), (mgt, mgt[:, kc * 128:(kc + 1) * 128]), (identb, identb[:]))
            k.copy("act", (mT, mT[:].rearrange("p a b -> p (a b)")), (pT, pT[:, :]))
            yield
            for blk in range(2):
                for kc in range(8):
                    k.mm((pM[blk], pM[blk][:, :]), (mT, mT[:, kc, :]), (W_out, W_out[:, kc, blk * 512:(blk + 1) * 512]),
                         start=(kc == 0), stop=(kc == 7))
                k.tt("dve", (xb, xb[:, blk * 512:(blk + 1) * 512]), (xb, xb[:, blk * 512:(blk + 1) * 512]), (pM[blk], pM[blk][:, :]), ALU.add)
                yield
            k.dma("sp", x1s[ti * 128:(ti + 1) * 128, :], xb[:], reads=[xb], writes=[(x1s, ti)])

        def rr(gens):
            gens = list(gens)
            while gens:
                for g_ in list(gens):
                    try:
                        next(g_)
                    except StopIteration:
                        gens.remove(g_)

        tlb = list(tiles_all)
        rr([head_gen(tlb[0])])
        for idx, ti in enumerate(tlb):
            gl = [tailb_gen(ti)]
            if idx + 1 < len(tlb):
                gl.append(head_gen(tlb[idx + 1]))
            rr(gl)

    def norm_generic(xbuf, gbc, hb_, hT_, ss_, rs_):
        k.act((hb_, hb_[:]), (xbuf, xbuf[:]), AF.Square, accum=(ss_, ss_[:]))
        k.ts("dve", (rs_, rs_[:]), (ss_, ss_[:]), 1.0 / D, EPS, op0=ALU.mult, op1=ALU.add)
        k.act((rs_, rs_[:]), (rs_, rs_[:]), AF.Ln)
        k.act((rs_, rs_[:]), (rs_, rs_[:]), AF.Exp, scale=-0.5)
        k.stt("dve", (hb_, hb_[:]), (xbuf, xbuf[:]), (rs_, rs_[:, 0:1]), (gbc, gbc[:]), ALU.mult, ALU.mult)
        for kc in range(8):
            k.tr((pT, pT[:, kc * 128:(kc + 1) * 128]), (hb_, hb_[:, kc * 128:(kc + 1) * 128]), (identb, identb[:]))
        k.copy("act", (hT_, hT_[:].rearrange("p a b -> p (a b)")), (pT, pT[:, :]))

    def phase_2(k):
        k.barrier()
        k.bot = mark_phase
        F_up = k.sbuf([128, 8, 2 * DFF], BF16, "F_up")
        F_dn = k.sbuf([128, NFC, D], BF16, "F_dn")
        PGW = k.sbuf([128, 8, D], BF16, "PGW")
        PPJ = k.sbuf([128, 2, D], BF16, "PPJ")
        NG = 4
        CW = DFF // NG
        for g in range(NG):
            for part in range(2):
                k.dma("pool", F_up[:, :, part * DFF + g * CW:part * DFF + (g + 1) * CW], d_f_up[:, :, part * DFF + g * CW:part * DFF + (g + 1) * CW],
                      reads=[d_f_up], writes=[(F_up, g)])
        for g in range(2):
            k.dma("pool", F_dn[:, 11 * g:11 * g + 11, :], d_f_down[:, 11 * g:11 * g + 11, :], reads=[d_f_down], writes=[(F_dn, g)])
        k.dma("pool", PGW[:], d_pgw[:], reads=[d_pgw], writes=[PGW])
        k.dma("pool", PPJ[:], d_ppj[:], reads=[d_ppj], writes=[PPJ])
        g2bc = k.sbuf([128, D], F32, "g2bc")
        g3bc = k.sbuf([128, D], F32, "g3bc")
        g4bc = k.sbuf([128, D], F32, "g4bc")
        fct = k.sbuf([128, 4 * NFC], F32, "fct")
        k.dma("sp", g2bc[:], d_g2[:], reads=[d_g2], writes=[g2bc])
        k.dma("sp", g3bc[:], d_g3[:], reads=[d_g3], writes=[g3bc])
        k.dma("sp", g4bc[:], d_g4[:], reads=[d_g4], writes=[g4bc])
        k.dma("sp", fct[:], d_fctab[:], reads=[d_fctab], writes=[fct])
        fcol = lambda c: (fct, fct[:, c:c + 1])
        xq = [k.sbuf([128, D], F32, f"xq{i}") for i in range(2)]
        hb3 = k.sbuf([128, D], BF16, "hb3")
        hT3s = [k.sbuf([128, 8, 128], BF16, f"hT3a{i}") for i in range(2)]
        ss3 = k.sbuf([128, 1], F32, "ss3")
        rs3 = k.sbuf([128, 1], F32, "rs3")
        gT = k.sbuf([128, NFC, 128], BF16, "gT")
        cf = k.sbuf([128, NFC, 2], F32, "cf")
        GS = 4
        EXW = NSB * (ST + 2)
        ex4 = [k.sbuf([128, GS, EXW], F32, f"ex4_{i}") for i in range(2)]
        cc4 = [k.sbuf([128, GS, 128], F32, f"cc4_{i}") for i in range(2)]
        t14 = [k.sbuf([128, GS, 128], F32, "t14_0")] * 2
        up4 = [k.sbuf([128, GS, 128], F32, f"up4_{i}") for i in range(2)]
        sg3 = [k.sbuf([128, 512], F32, "sg30")] * 2
        ppt = k.sbuf([128, PLE], F32, "ppt")
        ppb = k.sbuf([128, PLE], BF16, "ppb")
        peT = k.sbuf([128, 2, 128], BF16, "peT")
        utm = sg3[0]
        k.memset("pool", (cf, cf[:]), 0.0)
        cfs = k.sbuf([128, NFC, 2 * NSB], F32, "cfs")
        GC = 1.5957691216057308
        BLK6 = ((0, 512), (512, 512), (1024, 512), (1536, 512), (2048, 512), (2560, 256))
        groups = [list(range(g0, min(g0 + GS, NFC))) for g0 in range(0, NFC, GS)]
        gbank = [pF[0], pF[1]]
        ubank = [pM[0], pM[1]]

        def fup_w(col):
            g = col // CW
            g_hi = (col + 127) // CW
            return g, g_hi

        def stage_A(ti, gi, hT3):
            smp = ti == NTP
            p = gi % 2
            chunks = groups[gi]
            n = len(chunks)
            c0 = chunks[0]
            for part, bank in ((0, gbank[p]), (1, ubank[p])):
                for ci, c in enumerate(chunks):
                    g, g_hi = fup_w(c * 128)
                    for kc in range(8):
                        k.mm((bank, bank[:, ci * 128:(ci + 1) * 128]), (F_up, F_up[:, kc, part * DFF + c * 128:part * DFF + (c + 1) * 128], g),
                             (hT3, hT3[:, kc, :]), start=(kc == 0), stop=(kc == 7))
                        if g_hi != g and g_hi in F_up.subs and F_up.subs[g_hi].w is not None:
                            k.streams["pe"][-1].deps.add(F_up.subs[g_hi].w)
            ex = ex4[p]
            if not smp:
                k.copy("pool", (ex, ex[:, 0:n, 0:2]), (cf, cf[:, c0:c0 + n, :]))
                k.copy("act", (ex, ex[:, 0:n, 2:130]), (gbank[p], gbank[p][:, 0:n * 128].rearrange("p (c t) -> p c t", c=n)))
                k.copy("pool", (cf, cf[:, c0:c0 + n, :]), (ex, ex[:, 0:n, 128:130]))
            else:
                exs = ex[:, 0:n, :].rearrange("p c (b t) -> p c b t", t=ST + 2)
                k.copy("pool", (ex, exs[:, :, :, 0:2]), (cfs, cfs[:, c0:c0 + n, :].rearrange("p c (b j) -> p c b j", j=2)))
                k.copy("act", (ex, exs[:, :, :, 2:ST + 2]), (gbank[p], gbank[p][:, 0:n * 128].rearrange("p (c b t) -> p c b t", c=n, b=NSB)))
            k.copy("act", (up4[p], up4[p][:, 0:n, :]), (ubank[p], ubank[p][:, 0:n * 128].rearrange("p (c t) -> p c t", c=n)))
            for ci, c in enumerate(chunks):
                if not smp:
                    tap = lambda j: ex[:, ci, j:j + 128]
                    ccv = cc4[p][:, ci, :]
                else:
                    e3 = ex[:, ci, :].rearrange("p (b t) -> p b t", t=ST + 2)
                    tap = lambda j, e3=e3: e3[:, :, j:j + ST]
                    ccv = cc4[p][:, ci, :].rearrange("p (b t) -> p b t", t=ST)
                cb = cc4[p]
                k.ts("dve", (cb, ccv), (ex, tap(2)), fcol(2 * NFC + c), fcol(3 * NFC + c), op0=ALU.mult, op1=ALU.add)
                k.stt("dve", (cb, ccv), (ex, tap(1)), fcol(1 * NFC + c), (cb, ccv), ALU.mult, ALU.add)
                k.stt("dve", (cb, ccv), (ex, tap(0)), fcol(0 * NFC + c), (cb, ccv), ALU.mult, ALU.add)

        def stage_B(ti, gi):
            p = gi % 2
            chunks = groups[gi]
            n = len(chunks)
            c0 = chunks[0]
            cb = (cc4[p], cc4[p][:, 0:n, :])
            ta = (t14[p], t14[p][:, 0:n, :])
            k.tt("pool", ta, cb, cb, ALU.mult)
            k.ts("dve", ta, ta, 0.044715, 1.0, op0=ALU.mult, op1=ALU.add)
            k.tt("pool", ta, ta, cb, ALU.mult)
            k.act(ta, ta, AF.Sigmoid, scale=GC)
            k.tt("pool", ta, ta, cb, ALU.mult)
            k.tt("dve", (gT, gT[:, c0:c0 + n, :], ("g", gi)), ta, (up4[p], up4[p][:, 0:n, :]), ALU.mult)

        def load_norm2(ti):
            xb = xq[ti % 2]
            k.dma("sp", xb[:], x1s[ti * 128:(ti + 1) * 128, :], reads=[(x1s, ti)], writes=[xb])
            if ti == NTP:
                for b6, (c0, n) in enumerate(BLK6):
                    k.dma("sp", utm[0:2 * NSB, 0:n], st_fconv[:, c0:c0 + n], reads=[st_fconv], writes=[utm])
                    nch = n // 128
                    for ci in range(nch):
                        k.tr((pR[b6 % 2], pR[b6 % 2][:, ci * 32:(ci + 1) * 32]), (utm, utm[0:2 * NSB, ci * 128:(ci + 1) * 128]),
                             (identf, identf[0:2 * NSB, 0:2 * NSB]))
                    k.copy("act", (cfs, cfs[:, 4 * b6:4 * b6 + nch, :]), (pR[b6 % 2], pR[b6 % 2][:, 0:nch * 32].rearrange("p (c x) -> p c x", c=nch)))
            norm_generic(xb, g2bc, hb3, hT3s[ti % 2], ss3, rs3)

        hb3b = hb3

        def groups_gen(ti):
            hT3 = hT3s[ti % 2]
            for gi in range(len(groups) + 1):
                if gi < len(groups):
                    stage_A(ti, gi, hT3)
                    yield
                if gi >= 1:
                    stage_B(ti, gi - 1)
                    yield

        def tail_gen(ti):
            smp = ti == NTP
            xb = xq[ti % 2]
            hT3 = hT3s[ti % 2]
            for blk in range(2):
                for c in range(NFC):
                    k.mm((pR[blk], pR[blk][:, :]), (gT, gT[:, c, :], ("g", c // GS)), (F_dn, F_dn[:, c, blk * 512:(blk + 1) * 512], c // 11),
                         start=(c == 0), stop=(c == NFC - 1))
                k.tt("dve", (xb, xb[:, blk * 512:(blk + 1) * 512]), (xb, xb[:, blk * 512:(blk + 1) * 512]), (pR[blk], pR[blk][:, :]), ALU.add)
                yield
            if ti == NTP - 1 or smp:
                for b6, (c0, n) in enumerate(BLK6):
                    for kc in range(8):
                        k.mm((pM[2], pM[2][:, 0:n]), (hT3, hT3[:, kc, :]), (F_up, F_up[:, kc, c0:c0 + n]),
                             start=(kc == 0), stop=(kc == 7))
                    if not smp:
                        k.copy("act", (utm, utm[96:128, 0:n]), (pM[2], pM[2][96:128, 0:n]))
                        k.dma("sp", o_pfconv[:, c0:c0 + n], utm[126:128, 0:n], reads=[utm], writes=[o_pfconv])
                    else:
                        k.copy("act", (utm, utm[:, 0:n]), (pM[2], pM[2][:, 0:n]))
                        for b in range(NSB):
                            k.dma("sp", o_sfconv[2 * b:2 * b + 2, c0:c0 + n], utm[ST * b + ST - 2:ST * b + ST, 0:n], reads=[utm], writes=[o_sfconv])
                    yield
            norm_generic(xb, g3bc, hb3b, hT3, ss3, rs3)
            yield
            pd, pap = (pp, pp[ti * 128:(ti + 1) * 128, :]) if not smp else (psm, psm[:, :])
            k.dma("sp", ppt[:], pap, reads=[pd], writes=[ppt])
            k.copy("act", (ppb, ppb[:]), (ppt, ppt[:]))
            for kc in range(2):
                k.tr((pT, pT[:, kc * 128:(kc + 1) * 128]), (ppb, ppb[:, kc * 128:(kc + 1) * 128]), (identb, identb[:]))
            k.copy("act", (peT, peT[:].rearrange("p a b -> p (a b)")), (pT, pT[:, 0:256]))
            yield
            for blk in range(2):
                for kc in range(8):
                    k.mm((pR[blk], pR[blk][:, :]), (hT3, hT3[:, kc, :]), (PGW, PGW[:, kc, blk * 512:(blk + 1) * 512]),
                         start=(kc == 0), stop=(kc == 7))
                yield
                k.act((sg3[blk], sg3[blk][:]), (pR[blk], pR[blk][:, :]), AF.Sigmoid)
                for kc in range(2):
                    k.mm((pM[2], pM[2][:, :]), (peT, peT[:, kc, :]), (PPJ, PPJ[:, kc, blk * 512:(blk + 1) * 512]),
                         start=(kc == 0), stop=(kc == 1))
                k.tt("dve", (sg3[blk], sg3[blk][:]), (sg3[blk], sg3[blk][:]), (pM[2], pM[2][:, :]), ALU.mult)
                k.tt("pool", (xb, xb[:, blk * 512:(blk + 1) * 512]), (xb, xb[:, blk * 512:(blk + 1) * 512]), (sg3[blk], sg3[blk][:]), ALU.add)
                yield
            k.act((hb3b, hb3b[:]), (xb, xb[:]), AF.Square, accum=(ss3, ss3[:]))
            k.ts("dve", (rs3, rs3[:]), (ss3, ss3[:]), 1.0 / D, EPS, op0=ALU.mult, op1=ALU.add)
            k.act((rs3, rs3[:]), (rs3, rs3[:]), AF.Ln)
            k.act((rs3, rs3[:]), (rs3, rs3[:]), AF.Exp, scale=-0.5)
            yield
            k.stt("dve", (xb, xb[:]), (xb, xb[:]), (rs3, rs3[:, 0:1]), (g4bc, g4bc[:]), ALU.mult, ALU.mult)
            if not smp:
                k.dma("sp", y_p[ti * 128:(ti + 1) * 128, :], xb[:], reads=[xb], writes=[y_p])
            else:
                k.dma("sp", y_s[:, :], xb[:], reads=[xb], writes=[y_s])
            if ti in nxt2:
                yield
                load_norm2(nxt2[ti])

        def run_rr(gens):
            gens = list(gens)
            while gens:
                for g_ in list(gens):
                    try:
                        next(g_)
                    except StopIteration:
                        gens.remove(g_)

        tl = list(tiles_all)
        nxt2 = {tl[i]: tl[i + 2] for i in range(len(tl) - 2)}
        load_norm2(tl[0])
        if len(tl) > 1:
            load_norm2(tl[1])
        run_rr([groups_gen(tl[0])])
        for idx, ti in enumerate(tl):
            tg = tail_gen(ti)
            if idx + 1 < len(tl):
                gg = groups_gen(tl[idx + 1])
                next(gg)
                next(gg)
                next(tg)
                next(tg)
                run_rr([gg, tg])
            else:
                run_rr([tg])

    _nt_dbg = int(_os.environ.get("KDBG_NT", "0"))
    tiles_run = (tiles_all if not _nt_dbg else list(range(_nt_dbg)))
    for ti in tiles_run:
        mixer_tile(ti)
        if stage >= 2:
            rwkv_tile(ti)
            tail_rows(ti)

    if stage >= 3:
        phase_1b(k)
    if stage >= 4:
        phase_2(k)
    k.emit()
    k.stats["sbuf_hiwater"] = k.hiwater
    k.stats["arena_bytes"] = k.arena_bytes
    return nc, k


def _chunk_rows(w, nk):
    return np.ascontiguousarray(w.reshape(nk, 128, w.shape[1]).transpose(1, 0, 2))


def _pcols(v, nc_):
    return v.reshape(nc_, 128).T


_PROG = {}


def _get_prog(stage=99, dbg=False):
    key = (stage, dbg)
    if key not in _PROG:
        _PROG[key] = build_program(stage, dbg)
    return _PROG[key]


def make_in_maps(inp):
    f = lambda a: np.ascontiguousarray(np.asarray(a, dtype=np.float32))
    ptab = np.zeros((128, 128), np.float32)
    mcw = f(inp["m_conv_w"])[0]
    for j in range(4):
        ptab[:, j * 8:(j + 1) * 8] = _pcols(mcw[j], 8)
    ptab[:, 32:40] = _pcols(f(inp["m_conv_b"])[0], 8)
    ptab[:, 40:54] = _pcols(f(inp["r_mix"])[0], 14)
    ptab[:, 54:58] = _pcols(f(inp["r_w0"])[0], 4)
    ptab[:, 58:62] = _pcols(f(inp["r_a0"])[0], 4)
    ptab[:, 62:66] = _pcols(f(inp["r_kk"])[0], 4)
    ptab[:, 66:70] = _pcols(f(inp["r_ka"])[0], 4)
    ptab[:, 70:74] = _pcols(f(inp["r_rk"])[0].reshape(-1), 4)
    ptab[:, 74:78] = _pcols(f(inp["r_ln_g"])[0], 4)
    ptab[:, 78:82] = _pcols(f(inp["r_ln_b"])[0], 4)
    ptab[:, 82:86] = _pcols(f(inp["m_norm_g"])[0], 4)
    fct = np.zeros((128, 4 * NFC), np.float32)
    fcw = f(inp["f_conv_w"])[0]
    for j in range(3):
        fct[:, j * NFC:(j + 1) * NFC] = _pcols(fcw[j], NFC)
    fct[:, 3 * NFC:4 * NFC] = _pcols(f(inp["f_conv_b"])[0], NFC)
    gbias = np.stack([f(inp["m_i_bias"])[0], f(inp["m_f_bias"])[0]], axis=1)
    ra2 = np.zeros((128, RW), np.float32)
    ra2[64:128] = f(inp["r_a2"])[0]
    bc = lambda v: np.ascontiguousarray(np.broadcast_to(f(v).reshape(1, D), (128, D)))
    shared = {
        "w_in": _chunk_rows(f(inp["w_in"])[0], 8),
        "w_bm": _chunk_rows(f(inp["w_branch_m"])[0], 4),
        "w_br": _chunk_rows(f(inp["w_branch_r"])[0], 4),
        "w_out": _chunk_rows(f(inp["w_out"])[0], 8),
        "f_up": _chunk_rows(f(inp["f_up"])[0], 8),
        "f_down": _chunk_rows(f(inp["f_down"])[0], NFC),
        "ple_gate_w": _chunk_rows(f(inp["ple_gate_w"])[0], 8),
        "ple_proj": _chunk_rows(f(inp["ple_proj"])[0], 2),
        "r_w2": f(inp["r_w2"])[0], "r_a2": ra2, "r_g2": f(inp["r_g2"])[0],
        "norm1_g": bc(inp["norm1_g"]), "norm2_g": bc(inp["norm2_g"]),
        "ple_norm_g": bc(inp["ple_norm_g"]), "final_norm_g": bc(inp["final_norm_g"]),
        "ptab": ptab, "fctab": fct, "gate_bias": np.ascontiguousarray(gbias),
    }
    maps = []
    for c in range(NCORES):
        sl = slice(c * NSB, (c + 1) * NSB)
        m = dict(shared)
        m["xp"] = f(inp["x_prompt"][c])
        m["xs"] = f(inp["x_sample"][sl]).reshape(128, D)
        m["pp"] = f(inp["p_prompt"][0, c])
        m["psm"] = f(inp["p_sample"][0, sl]).reshape(128, PLE)
        m["st_mconv"] = f(inp["state_mlstm_conv"][0, sl]).reshape(NSB * 3, 2 * MW)
        m["st_mC"] = f(inp["state_mlstm_C"][0, sl])
        m["st_mn"] = f(inp["state_mlstm_n"][0, sl])
        m["st_mm"] = f(inp["state_mlstm_m"][0, sl])
        m["st_rshift"] = f(inp["state_rwkv_shift"][0, sl])
        m["st_rS"] = f(inp["state_rwkv_S"][0, sl]).reshape(NSB * RH, RN * RN)
        m["st_fconv"] = f(inp["state_ffn_conv"][0, sl]).reshape(NSB * 2, DFF)
        maps.append(m)
    return maps


def assemble(results):
    g = lambda name: [np.asarray(r[name], dtype=np.float32) for r in results]
    y_p = np.stack(g("y_p"), 0)
    y_s = np.concatenate([a.reshape(NSB, ST, D) for a in g("y_s")], 0)
    p_conv = np.stack(g("p_conv"), 0)[None]
    p_C = np.stack(g("p_C"), 0)[None]
    p_n = np.stack(g("p_n"), 0)[None]
    p_m = np.stack([a.reshape(MH) for a in g("p_m")], 0)[None]
    p_shift = np.stack([a.reshape(RCOLS) for a in g("p_shift")], 0)[None]
    p_S = np.stack([a.reshape(RH, RN, RN) for a in g("p_S")], 0)[None]
    p_fconv = np.stack(g("p_fconv"), 0)[None]
    s_conv = np.concatenate([a.reshape(NSB, 3, 2 * MW) for a in g("s_conv")], 0)[None]
    s_C = np.concatenate(g("s_C"), 0)[None]
    s_n = np.concatenate(g("s_n"), 0)[None]
    s_m = np.concatenate(g("s_m"), 0)[None]
    s_shift = np.concatenate(g("s_shift"), 0)[None]
    s_S = np.concatenate([a.reshape(NSB, RH, RN, RN) for a in g("s_S")], 0)[None]
    s_fconv = np.concatenate([a.reshape(NSB, 2, DFF) for a in g("s_fconv")], 0)[None]
    return (y_p, y_s, p_conv, p_C, p_n, p_m, p_shift, p_S, p_fconv,
            s_conv, s_C, s_n, s_m, s_shift, s_S, s_fconv)


def kernel(**inputs):
    nc, _ = _get_prog()
    maps = make_in_maps(inputs)
    res = run_bass_kernel_spmd(nc, maps, core_ids=list(range(NCORES)))
    return assemble(res.results)
```

```python
import math
from contextlib import ExitStack

import numpy as np
import concourse.bass as bass
import concourse.mybir as mybir
from concourse.bass_utils import run_bass_kernel_spmd

F32 = mybir.dt.float32
BF16 = mybir.dt.bfloat16
AF = mybir.ActivationFunctionType
ALU = mybir.AluOpType
AX = mybir.AxisListType

ENGS = ("pe", "act", "dve", "pool", "sp")
N_DMA_SEMS = 8
SAME_ENG_DIST = 2

D = 1024
SEQ = 2048
NCORES = 8
NTP = SEQ // 128
NSB = 16
ST = 8
MW = 512
MH = 4
RW = 512
RH = 8
RN = 64
RCOLS = 1792
DFF = 2816
NFC = DFF // 128
PLE = 256
N_IN = 5896
C_QK, C_V, C_O, C_I, C_F, C_R, C_G = 0, 1024, 1536, 2048, 2052, 2056, 3848
EPS = 1e-6
GN_EPS = 64e-5
KSCALE = 128 ** -0.5
WSCALE = -math.exp(-0.5)


class _Trk:
    __slots__ = ("w", "r")

    def __init__(self):
        self.w = None
        self.r = []


class Buf:
    def __init__(self, t, name):
        self.t = t
        self.name = name
        self.whole = _Trk()
        self.subs = {}

    def __getitem__(self, idx):
        return self.t[idx]

    def view(self, ap, name=None):
        b = Buf(ap, name or self.name + "_v")
        b.whole = self.whole
        b.subs = self.subs
        return b


class _Op:
    __slots__ = ("eng", "fn", "deps", "needs_inc", "is_dma", "sem", "val", "pos", "force")


class K:
    def __init__(self, nc):
        self.nc = nc
        self.es = ExitStack()
        self.streams = {e: [] for e in ENGS}
        self.dma_rr = {e: 0 for e in ENGS}
        self.dma_last = {}
        self.nbuf = 0
        self.ops = []

    def _init_arena(self):
        nbytes = (int(self.nc.sbuf_bytes_remaining) - 512) // 64 * 64
        self.arena_bytes = nbytes
        self.arena = self.es.enter_context(self.nc.sbuf_tensor("arena", [128, nbytes // 2], BF16))
        self.bot = 0
        self.top = nbytes
        self.hiwater = 0

    def _view(self, off, shape, dtype):
        n = 1
        for d in shape[1:]:
            n *= d
        esz = 4 if dtype == F32 else 2
        v = self.arena[:, off // 2:(off + n * esz) // 2]
        if dtype == F32:
            v = v.bitcast(F32)
        if len(shape) > 2:
            names = " ".join(f"d{i}" for i in range(len(shape) - 1))
            v = v.rearrange(f"p ({names}) -> p {names}", **{f"d{i}": shape[i + 1] for i in range(len(shape) - 1)})
        if shape[0] < 128:
            v = v[0:shape[0]]
        return v, n * esz

    def sbuf(self, shape, dtype, name=None, top=False):
        if not hasattr(self, "arena"):
            self._init_arena()
        self.nbuf += 1
        name = name or f"sb{self.nbuf}"
        n = 1
        for d in shape[1:]:
            n *= d
        nb = (n * (4 if dtype == F32 else 2) + 63) // 64 * 64
        if top:
            self.top -= nb
            off = self.top
        else:
            off = self.bot
            self.bot += nb
        assert self.bot <= self.top, f"SBUF arena overflow allocating {name}: bot={self.bot} top={self.top}"
        self.hiwater = max(self.hiwater, self.bot + (self.arena_bytes - self.top))
        v, _ = self._view(off, list(shape), dtype)
        return Buf(v, name)

    def pe_fence(self):
        st = self.streams["pe"]
        if not st:
            return
        last = st[-1]
        o = self.op("pe", lambda h: h.nop(), (), ())
        o.deps.add(last)
        o.force = {last}
        if getattr(self, "fence_mm", None) is not None:
            fb, fi = self.fence_mm
            self.tr((fb, fb[:, 0:128]), (fi, fi[:]), (fi, fi[:]))
            last = self.streams["pe"][-1]
            o = self.op("pe", lambda h: h.nop(), (), ())
            o.deps.add(last)
            o.force = {last}

    def barrier(self):
        lasts = [st[-1] for st in self.streams.values() if st]
        lasts += list(self.dma_last.values())
        for e in ENGS:
            o = self.op(e, lambda h: h.nop(), (), ())
            o.deps.update(x for x in lasts if x is not o)

    def psum(self, shape, dtype, name=None):
        self.nbuf += 1
        name = "ps_" + (name or f"{self.nbuf}")
        t = self.es.enter_context(self.nc.psum_tensor(name, list(shape), dtype))
        return Buf(t, name)

    def dram(self, name, shape, dtype, kind="Internal"):
        t = self.nc.dram_tensor(name, list(shape), dtype, kind=kind)
        return Buf(t.ap(), name)

    def _touch(self, op, item, is_write):
        if isinstance(item, tuple):
            buf, key = item
        else:
            buf, key = item, None
        if key is None:
            trks = [buf.whole] + list(buf.subs.values())
        else:
            if key not in buf.subs:
                buf.subs[key] = _Trk()
            trks = [buf.whole, buf.subs[key]]
        for t in trks:
            if t.w is not None:
                op.deps.add(t.w)
            if is_write:
                op.deps.update(t.r)
        return buf, key

    def _commit(self, op, buf, key, is_write):
        if key is None:
            if is_write:
                buf.whole.w = op
                buf.whole.r = []
                buf.subs.clear()
            else:
                self._add_reader(buf.whole, op)
        else:
            t = buf.subs[key]
            if is_write:
                t.w = op
                t.r = []
            else:
                self._add_reader(t, op)

    @staticmethod
    def _add_reader(t, op):
        if not op.is_dma:
            t.r = [o for o in t.r if o.is_dma or o.eng != op.eng]
        t.r.append(op)

    def op(self, eng, fn, reads=(), writes=(), dma=False):
        o = _Op()
        o.eng = eng
        o.fn = fn
        o.deps = set()
        o.needs_inc = False
        o.is_dma = dma
        o.sem = None
        o.val = None
        o.force = None
        touched = []
        for it in reads:
            touched.append(self._touch(o, it, False) + (False,))
        for it in writes:
            touched.append(self._touch(o, it, True) + (True,))
        o.deps.discard(o)
        for buf, key, w in touched:
            self._commit(o, buf, key, w)
        if dma:
            kk = (eng, self.dma_rr[eng] % N_DMA_SEMS)
            self.dma_rr[eng] += 1
            prev = self.dma_last.get(kk)
            if prev is not None:
                o.deps.add(prev)
            self.dma_last[kk] = o
            o.sem = kk
            o.needs_inc = True
        o.pos = len(self.streams[eng])
        self.streams[eng].append(o)
        self.ops.append(o)
        return o

    def dma(self, eng, out, in_, reads=(), writes=(), **kw):
        return self.op(eng, lambda e: e.dma_start(out=out, in_=in_, **kw), reads, writes, dma=True)

    def emit(self):
        nc = self.nc
        for o in self.ops:
            real = []
            for d in o.deps:
                if (not d.is_dma) and (not o.is_dma) and d.eng == o.eng and o.eng == "pe":
                    if not (o.force and d in o.force):
                        continue
                d.needs_inc = True
                real.append(d)
            o.deps = real
        for e in ENGS:
            cs = [o for o in self.streams[e] if not o.is_dma]
            if cs:
                cs[-1].needs_inc = True
        es = self.es
        esem = {e: es.enter_context(nc.semaphore(f"s_{e}")) for e in ENGS}
        dsem = {}
        for e in ENGS:
            for i in range(min(N_DMA_SEMS, self.dma_rr[e])):
                dsem[(e, i)] = es.enter_context(nc.semaphore(f"d_{e}{i}"))
        dcount = {kk: 0 for kk in dsem}
        for e in ENGS:
            c = 0
            for o in self.streams[e]:
                if o.is_dma:
                    dcount[o.sem] += 16
                    o.val = dcount[o.sem]
                    o.sem = dsem[o.sem]
                elif o.needs_inc:
                    c += 1
                    o.val = c
                    o.sem = esem[e]
        final_waits = [(s, dcount[kk]) for kk, s in dsem.items() if dcount[kk] > 0]
        for e in ENGS:
            if e == "sp":
                continue
            cs = [o for o in self.streams[e] if not o.is_dma and o.needs_inc]
            if cs:
                final_waits.append((esem[e], cs[-1].val))
        streams = self.streams
        nwaits = [0]

        def run(e, handle):
            waited = {}
            for o in streams[e]:
                need = {}
                for d in o.deps:
                    if need.get(d.sem, (None, 0))[1] < d.val:
                        need[d.sem] = (d.sem, d.val)
                for s, v in need.values():
                    if waited.get(s, 0) >= v:
                        continue
                    handle.wait_ge(s, v)
                    nwaits[0] += 1
                    waited[s] = v
                ins = o.fn(handle)
                if o.is_dma:
                    ins.then_inc(o.sem, 16)
                elif o.needs_inc:
                    ins.then_inc(o.sem, 1)
            if e == "sp":
                for s, v in final_waits:
                    handle.wait_ge(s, v)

        with nc.Block() as block:
            @block.tensor
            def _(h):
                run("pe", h)

            @block.scalar
            def _(h):
                run("act", h)

            @block.vector
            def _(h):
                run("dve", h)

            @block.gpsimd
            def _(h):
                run("pool", h)

            @block.sync
            def _(h):
                run("sp", h)
        self.stats = dict(n_ops={e: len(streams[e]) for e in ENGS}, n_waits=nwaits[0])
        self.es.close()

    @staticmethod
    def _it(x):
        return (x[0], x[2]) if len(x) > 2 else x[0]

    def mm(self, out, lhsT, rhs, start=True, stop=True):
        return self.op("pe", lambda e: e.matmul(out[1], lhsT=lhsT[1], rhs=rhs[1], start=start, stop=stop),
                       reads=[self._it(lhsT), self._it(rhs)], writes=[self._it(out)])

    def tr(self, out, in_, ident):
        return self.op("pe", lambda e: e.transpose(out[1], in_[1], ident[1]),
                       reads=[self._it(in_), self._it(ident)], writes=[self._it(out)])

    def act(self, out, in_, func, bias=None, scale=None, accum=None, eng="act"):
        reads = [self._it(in_)]
        kw = {}
        if bias is not None:
            if isinstance(bias, tuple):
                reads.append(self._it(bias))
                kw["bias"] = bias[1]
            else:
                kw["bias"] = bias
        if scale is not None:
            if isinstance(scale, tuple):
                reads.append(self._it(scale))
                kw["scale"] = scale[1]
            else:
                kw["scale"] = scale
        writes = [self._it(out)]
        if accum is not None:
            writes.append(self._it(accum))
            kw["accum_out"] = accum[1]
        return self.op(eng, lambda e: e.activation(out=out[1], in_=in_[1], func=func, **kw), reads, writes)

    def tt(self, eng, out, in0, in1, op):
        return self.op(eng, lambda e: e.tensor_tensor(out=out[1], in0=in0[1], in1=in1[1], op=op),
                       reads=[self._it(in0), self._it(in1)], writes=[self._it(out)])

    def ts(self, eng, out, in0, s1, s2=None, op0=ALU.mult, op1=None, accum=None):
        reads = [self._it(in0)]
        a1 = s1
        a2 = s2
        if isinstance(s1, tuple):
            reads.append(self._it(s1))
            a1 = s1[1]
        if isinstance(s2, tuple):
            reads.append(self._it(s2))
            a2 = s2[1]
        kw = {}
        if op1 is not None:
            kw["op1"] = op1
        writes = [self._it(out)]
        if accum is not None:
            writes.append(self._it(accum))
            kw["accum_out"] = accum[1]
        return self.op(eng, lambda e: e.tensor_scalar(out=out[1], in0=in0[1], scalar1=a1, scalar2=a2, op0=op0, **kw),
                       reads, writes)

    def stt(self, eng, out, in0, scalar, in1, op0, op1):
        reads = [self._it(in0), self._it(in1)]
        a = scalar
        if isinstance(scalar, tuple):
            reads.append(self._it(scalar))
            a = scalar[1]
        return self.op(eng, lambda e: e.scalar_tensor_tensor(out=out[1], in0=in0[1], scalar=a, in1=in1[1], op0=op0, op1=op1),
                       reads, [self._it(out)])

    def copy(self, eng, out, in_):
        if eng == "act":
            return self.op(eng, lambda e: e.activation(out=out[1], in_=in_[1], func=AF.Copy),
                           reads=[self._it(in_)], writes=[self._it(out)])
        return self.op(eng, lambda e: e.tensor_copy(out=out[1], in_=in_[1]),
                       reads=[self._it(in_)], writes=[self._it(out)])

    def red(self, eng, out, in_, op, axis=AX.X):
        return self.op(eng, lambda e: e.tensor_reduce(out=out[1], in_=in_[1], axis=axis, op=op),
                       reads=[self._it(in_)], writes=[self._it(out)])

    def memset(self, eng, out, val):
        return self.op(eng, lambda e: e.memset(out[1], val), reads=[], writes=[self._it(out)])

    def scan(self, eng, out, d0, d1, init, op0, op1):
        return self.op(eng, lambda e: e.tensor_tensor_scan(out=out[1], data0=d0[1], data1=d1[1], initial=init, op0=op0, op1=op1),
                       reads=[self._it(d0), self._it(d1)], writes=[self._it(out)])


def build_program(stage=99, dbg=False):
    import os as _os
    nc = bass.Bass("TRN2", target_bir_lowering=False)
    k = K(nc)
    NT = NTP + 1

    def din(name, shape):
        return k.dram(name, shape, F32, "ExternalInput")

    def dout(name, shape):
        return k.dram(name, shape, F32, "ExternalOutput")

    xp = din("xp", [SEQ, D]); xs = din("xs", [128, D])
    pp = din("pp", [SEQ, PLE]); psm = din("psm", [128, PLE])
    st_mconv = din("st_mconv", [NSB * 3, 2 * MW])
    st_mC = din("st_mC", [NSB, MH, 128, 128])
    st_mn = din("st_mn", [NSB, MH, 128])
    st_mm = din("st_mm", [NSB, MH])
    st_rshift = din("st_rshift", [NSB, RCOLS])
    st_rS = din("st_rS", [NSB * RH, RN * RN])
    st_fconv = din("st_fconv", [NSB * 2, DFF])
    d_w_in = din("w_in", [128, 8, N_IN])
    d_w_bm = din("w_bm", [128, 4, D]); d_w_br = din("w_br", [128, 4, D])
    d_w_out = din("w_out", [128, 8, D])
    d_f_up = din("f_up", [128, 8, 2 * DFF]); d_f_down = din("f_down", [128, NFC, D])
    d_pgw = din("ple_gate_w", [128, 8, D]); d_ppj = din("ple_proj", [128, 2, D])
    d_rw2 = din("r_w2", [64, RW]); d_ra2 = din("r_a2", [128, RW]); d_rg2 = din("r_g2", [128, RW])
    d_g1 = din("norm1_g", [128, D]); d_g2 = din("norm2_g", [128, D])
    d_g3 = din("ple_norm_g", [128, D]); d_g4 = din("final_norm_g", [128, D])
    d_ptab = din("ptab", [128, 128])
    d_fctab = din("fctab", [128, 4 * NFC])
    d_gb = din("gate_bias", [4, 2])
    y_p = dout("y_p", [SEQ, D]); y_s = dout("y_s", [128, D])
    o_pconv = dout("p_conv", [3, 2 * MW]); o_pC = dout("p_C", [MH, 128, 128]); o_pn = dout("p_n", [MH, 128])
    o_pm = dout("p_m", [1, MH]); o_pshift = dout("p_shift", [1, RCOLS]); o_pS = dout("p_S", [RH * RN, RN])
    o_pfconv = dout("p_fconv", [2, DFF])
    o_sconv = dout("s_conv", [NSB * 3, 2 * MW]); o_sC = dout("s_C", [NSB, MH, 128, 128]); o_sn = dout("s_n", [NSB, MH, 128])
    o_sm = dout("s_m", [NSB, MH]); o_sshift = dout("s_shift", [NSB, RCOLS]); o_sS = dout("s_S", [NSB * RH, RN * RN])
    o_sfconv = dout("s_fconv", [NSB * 2, DFF])
    x1s = k.dram("x1_scratch", [NT * 128, D], F32)
    dbgs = {}

    def dbg_out(name, src_buf, src_ap, shape):
        if not dbg:
            return
        t = dout("dbg_" + name, shape)
        dbgs[name] = t
        k.dma("sp", t[:], src_ap, reads=[src_buf], writes=[t])

    identf = k.sbuf([128, 128], F32, "identf")
    identb = k.sbuf([128, 128], BF16, "identb")
    mark_phase = k.bot
    mU_in = [k.sbuf([128, 128], F32, f"mUin{i}") for i in range(2)]
    mU_st = [k.sbuf([128, 128], F32, f"mUst{i}") for i in range(2)]
    mL_st = [k.sbuf([128, 128], F32, f"mLst{i}") for i in range(2)]
    resets = [k.sbuf([128, 512], F32, f"resets{i}") for i in range(2)]
    ones4 = k.sbuf([4, 128], F32, "ones4")

    def aff(out_buf, out_ap, pattern, cm, base, op=ALU.is_ge):
        k.op("pool", lambda e: e.affine_select(out=out_ap, in_=out_ap, pattern=pattern, compare_op=op,
                                               fill=0.0, base=base, channel_multiplier=cm),
             reads=[out_buf], writes=[out_buf])

    k.memset("pool", (identf, identf[:]), 1.0)
    aff(identf, identf[:], [[-1, 128]], 1, 0)
    aff(identf, identf[:], [[1, 128]], -1, 0)
    k.copy("pool", (identb, identb[:]), (identf, identf[:]))
    for i in range(2):
        k.memset("pool", (mU_in[i], mU_in[i][:]), 1.0)
        aff(mU_in[i], mU_in[i][:], [[1, 128]], -1, 0)
        k.memset("pool", (mU_st[i], mU_st[i][:]), 1.0)
        aff(mU_st[i], mU_st[i][:], [[1, 128]], -1, -1)
        k.memset("pool", (mL_st[i], mL_st[i][:]), 1.0)
        aff(mL_st[i], mL_st[i][:], [[-1, 128]], 1, -1)
        k.memset("pool", (resets[i], resets[i][:]), 1.0)
    v3 = lambda b: b[:].rearrange("p (a c) -> p a c", c=ST)
    aff(mU_in[1], v3(mU_in[1]), [[-ST, 16], [0, ST]], 1, 0)
    aff(mU_st[1], v3(mU_st[1]), [[-ST, 16], [0, ST]], 1, 0)
    aff(mL_st[1], v3(mL_st[1]), [[ST, 16], [0, ST]], -1, ST - 1)
    k.memset("pool", (resets[0], resets[0][:].rearrange("p (a c) -> p a c", c=128)[:, :, 0:1]), 0.0)
    k.memset("pool", (resets[1], resets[1][:].rearrange("p (a c) -> p a c", c=ST)[:, :, 0:1]), 0.0)
    k.memset("pool", (ones4, ones4[:]), 1.0)
    mask2 = [k.sbuf([128, 256], F32, f"mask2_{i}") for i in range(2)]
    for i in range(2):
        k.copy("pool", (mask2[i], mask2[i][:, 0:128]), (mU_st[i], mU_st[i][:]))
        k.copy("pool", (mask2[i], mask2[i][:, 128:256]), (mU_in[i], mU_in[i][:]))
    I2 = k.sbuf([128, 64], F32, "I2")
    k.tt("pool", (I2, I2[:]), (identf, identf[:, 0:64]), (identf, identf[:, 64:128]), ALU.add)
    bones = k.sbuf([128, 128], F32, "bones")
    k.memset("pool", (bones, bones[:]), 0.0)
    k.memset("pool", (bones, bones[0:64, 0:64]), 1.0)
    k.memset("pool", (bones, bones[64:128, 64:128]), 1.0)

    ptab = k.sbuf([128, 128], F32, "ptab")
    k.dma("sp", ptab[:], d_ptab[:], reads=[d_ptab], writes=[ptab])
    PT_MCW, PT_MCB, PT_RMIX, PT_RW0, PT_RA0, PT_RKK, PT_RKA, PT_RRK, PT_RLNG, PT_RLNB, PT_MNG = 0, 32, 40, 54, 58, 62, 66, 70, 74, 78, 82
    pcol = lambda c: (ptab, ptab[:, c:c + 1])
    gb = k.sbuf([4, 2], F32, "gb")
    k.dma("sp", gb[:], d_gb[:], reads=[d_gb], writes=[gb])
    nbf = k.sbuf([4, 1], F32, "nbf")
    k.ts("dve", (nbf, nbf[:]), (gb, gb[:, 1:2]), -1.0, None, op0=ALU.mult)
    g1bc = k.sbuf([128, D], F32, "g1bc")
    k.dma("sp", g1bc[:], d_g1[:], reads=[d_g1], writes=[g1bc])

    NA = C_G
    hmT_all = k.sbuf([128, NT, 4, 128], BF16, "hmT_all")
    yrgT_all = k.sbuf([128, NT, 4, 128], BF16, "yrgT_all")
    mark_1a = k.bot
    W_in = k.sbuf([128, 8, NA], BF16, "W_in")
    Wl_w2 = k.sbuf([64, RW], BF16, "Wl_w2")
    Wl_a2 = k.sbuf([128, RW], BF16, "Wl_a2")
    Wl_g2 = k.sbuf([128, RW], BF16, "Wl_g2")
    GRP = {"g0": (0, 1024), "g1": (1024, 2056), "g2": (2056, 3848)}
    for g in ("g0", "g1", "g2"):
        a, b = GRP[g]
        for kh in range(2):
            k.dma("pool", W_in[:, 4 * kh:4 * kh + 4, a:b], d_w_in[:, 4 * kh:4 * kh + 4, a:b], reads=[d_w_in], writes=[(W_in, g)])
        if g == "g1":
            k.dma("pool", Wl_w2[:], d_rw2[:], reads=[d_rw2], writes=[Wl_w2])
            k.dma("pool", Wl_a2[:], d_ra2[:], reads=[d_ra2], writes=[Wl_a2])
            k.dma("pool", Wl_g2[:], d_rg2[:], reads=[d_rg2], writes=[Wl_g2])

    def wgrp(col):
        for g, (a, b) in GRP.items():
            if a <= col < b:
                return g

    pF = [k.psum([128, 512], F32, f"pF{i}") for i in range(2)]
    pR = [k.psum([128, 512], F32, f"pR{i}") for i in range(2)]
    pT = k.psum([128, 1024], BF16, "pT")
    pM = [k.psum([128, 512], F32, f"pM{i}") for i in range(3)]

    xt = [k.sbuf([128, D], F32, "xt0")] * 2
    hb = k.sbuf([128, D], BF16, "hb")
    hT = k.sbuf([128, 8, 128], BF16, "hT")
    ss = k.sbuf([128, 1], F32, "ss")
    rs = k.sbuf([128, 1], F32, "rs")
    ext_q = k.sbuf([128, 8, 131], F32, "ext_q")
    cq = k.sbuf([128, 8, 3], F32, "cq")
    _eqf = ext_q[:].rearrange("p a b -> p (a b)")
    cv = k.sbuf([128, 8, 128], F32, "cv")
    qkT = k.sbuf([128, 8, 128], BF16, "qkT")
    soT = k.sbuf([128, 4, 128], F32, "soT")
    vaug = k.sbuf([128, 4, 130], BF16, "vaug")
    Cst = k.sbuf([128, 4, 129], F32, "Cst")
    Cb = k.sbuf([128, 4, 130], BF16, "Cb")
    gsm = [k.sbuf([4, 128], F32, f"gsm{i}") for i in range(8)]
    gpk = k.sbuf([4, 3, 128], F32, "gpk")
    mst = k.sbuf([4, 16], F32, "mst")
    mnew = k.sbuf([4, 16], F32, "mnew")
    gt = [k.sbuf([4, 16], F32, f"gt{i}") for i in range(4)]
    s0d = k.sbuf([4, 4, 16], F32, "s0d")
    tokS = k.sbuf([128, 12], F32, "tokS")
    s0bc = k.sbuf([128, 64], F32, "s0bc")
    _pk = _eqf[:, 512:1024].bitcast(BF16).rearrange("p (a b c) -> p a b c", a=2, b=4)
    PTm = ext_q.view(_pk[:, 0, :, :], "PTm")
    ktm = ext_q.view(_pk[:, 1, :, :], "ktm")
    dn = k.sbuf([128, 4], F32, "dn")
    hm = ext_q.view(_eqf[:, 0:512].rearrange("p (a b) -> p a b", a=4), "hm")
    hn = hm
    bst = k.sbuf([128, 4, 6], F32, "bst")
    bag = k.sbuf([128, 4, 2], F32, "bag")
    zq_tm = cv.view(cv[:].rearrange("p a b -> p (a b)"), "zq_tm")

    ext_r = k.sbuf([128, 14, 129], F32, "ext_r")
    cr = k.sbuf([128, 14, 1], F32, "cr")
    _erf = ext_r[:].rearrange("p a b -> p (a b)")
    xm = k.sbuf([128, 14, 128], F32, "xm")
    thad = k.sbuf([128, 128], BF16, "thad")
    sgd = k.sbuf([128, 128], BF16, "sgd")
    bst8 = k.sbuf([128, 8, 6], F32, "bst8")
    bag8 = k.sbuf([128, 8, 2], F32, "bag8")
    mark_rw = k.bot
    rt = [k.sbuf([128, 4, 128], F32, f"rt{i}") for i in range(7)]
    rt.append(cv.view(cv[:, 0:4, :], "rt7"))
    rt.append(cv.view(cv[:, 4:8, :], "rt8"))
    gTs = ext_r.view(_erf[:, 0:512].rearrange("p (a b) -> p a b", a=4), "gTs")
    bonT = ext_r.view(_erf[:, 512:1024].rearrange("p (a b) -> p a b", a=4), "bonT")
    ART = k.sbuf([128, 4, 2, 128], BF16, "ART")
    BTb = k.sbuf([128, 4, 128], BF16, "BTb")
    KTb = k.sbuf([128, 4, 128], BF16, "KTb")
    VTb = k.sbuf([128, 4, 128], BF16, "VTb")
    AB_tm = k.sbuf([128, 2, 512], BF16, "AB_tm")
    KV_tm = k.sbuf([128, 2, 512], BF16, "KV_tm")
    GBm = k.sbuf([128, 4, 256], BF16, "GBm")
    GKm = k.sbuf([128, 4, 256], BF16, "GKm")
    Nn = k.sbuf([128, 4, 128], BF16, "Nn")
    GBm_b = k.sbuf([128, 4, 256], BF16, "GBm_b")
    GKm_b = k.sbuf([128, 4, 256], BF16, "GKm_b")
    Nn_b = k.sbuf([128, 4, 128], BF16, "Nn_b")
    PP_b = [k.sbuf([128, 4, 256], BF16, f"PPb{i}") for i in range(2)]
    XX_b = [k.sbuf([128, 4, 128], BF16, f"XXb{i}") for i in range(2)]
    PP = [k.sbuf([128, 4, 256], BF16, f"PP{i}") for i in range(2)]
    XX = [k.sbuf([128, 4, 128], BF16, f"XX{i}") for i in range(2)]
    QT = k.sbuf([128, 2, 128], BF16, "QT")
    IE = k.sbuf([128, 4, 64], F32, "IE")
    STf = k.sbuf([128, 4, 64], F32, "STf")
    STb = k.sbuf([128, 4, 64], BF16, "STb")
    yn = ext_r.view(_erf[:, 1024:1536].rearrange("p (a b) -> p a b", a=8), "yn")
    k.memset("pool", (STf, STf[:]), 0.0)
    k.memset("pool", (STb, STb[:]), 0.0)
    k.memset("pool", (cr, cr[:]), 0.0)
    k.memset("pool", (vaug, vaug[:]), 1.0)
    k.memset("pool", (Cst, Cst[:]), 0.0)
    k.memset("pool", (Cb, Cb[:]), 0.0)
    k.memset("pool", (mst, mst[:]), 0.0)
    k.memset("pool", (cq, cq[:]), 0.0)
    LNK = math.log(KSCALE)

    def x_rows(ti):
        if ti < NTP:
            return xp, xp[ti * 128:(ti + 1) * 128, :]
        return xs, xs[:, :]

    def norm_to_hT(xbuf, gbc):
        k.act((hb, hb[:]), (xbuf, xbuf[:]), AF.Square, accum=(ss, ss[:]))
        k.ts("dve", (rs, rs[:]), (ss, ss[:]), 1.0 / D, EPS, op0=ALU.mult, op1=ALU.add)
        k.act((rs, rs[:]), (rs, rs[:]), AF.Ln)
        k.act((rs, rs[:]), (rs, rs[:]), AF.Exp, scale=-0.5)
        k.stt("dve", (hb, hb[:]), (xbuf, xbuf[:]), (rs, rs[:, 0:1]), (gbc, gbc[:]), ALU.mult, ALU.mult)
        for kc in range(8):
            k.tr((pT, pT[:, kc * 128:(kc + 1) * 128]), (hb, hb[:, kc * 128:(kc + 1) * 128]), (identb, identb[:]))
        k.copy("act", (hT, hT[:].rearrange("p a b -> p (a b)")), (pT, pT[:, :]))

    def proj_fm(ps, ps_ap, col, M=128):
        g = wgrp(col)
        for kc in range(8):
            k.mm((ps, ps_ap), (W_in, W_in[:, kc, col:col + M], g), (hT, hT[:, kc, :]), start=(kc == 0), stop=(kc == 7))

    def proj_tm(ps, ps_ap, col, N):
        g = wgrp(col)
        for kc in range(8):
            k.mm((ps, ps_ap), (hT, hT[:, kc, :]), (W_in, W_in[:, kc, col:col + N], g), start=(kc == 0), stop=(kc == 7))

    prefetched = set()

    def prefetch_gen(tn):
        xb = xt[tn % 2]
        xd, xap = x_rows(tn)
        k.dma("sp", xb[:], xap, reads=[xd], writes=[xb])
        norm_to_hT(xb, g1bc)
        yield
        k.copy("pool", (ext_q, ext_q[:, :, 0:3]), (cq, cq[:]))
        for g in range(2):
            for c in range(4):
                proj_fm(pM[2], pM[2][:, c * 128:(c + 1) * 128], C_QK + (4 * g + c) * 128)
                if c == 1:
                    yield
            k.copy("act", (ext_q, ext_q[:, 4 * g:4 * g + 4, 3:131]), (pM[2], pM[2][:].rearrange("p (c t) -> p c t", c=4)))
            yield
        k.copy("pool", (cq, cq[:]), (ext_q, ext_q[:, :, 128:131]))
        yield
        proj_tm(pM[2], pM[2][:, :], C_V, 512)
        k.copy("act", (vaug, vaug[:, :, 0:128]), (pM[2], pM[2][:].rearrange("p (h c) -> p h c", h=4)))
        yield
        for c in range(4):
            proj_fm(pM[2], pM[2][:, c * 128:(c + 1) * 128], C_O + c * 128)
            if c == 1:
                yield
        k.act((soT, soT[:].rearrange("p a b -> p (a b)")), (pM[2], pM[2][:, :]), AF.Sigmoid)

    def mixer_tile(ti):
        smp = ti == NTP
        mi = 1 if smp else 0
        NB = NSB if smp else 1
        LB = ST if smp else 128
        xb = xt[ti % 2]
        xd, xap = x_rows(ti)
        if smp:
            k.barrier()
            k.bot = mark_rw
            Cs = k.sbuf([128, NSB, 129], F32, "Cs")
            Csb = k.sbuf([128, NSB, 130], BF16, "Csb")
            qTm = k.sbuf([128, NSB, 128], BF16, "qTm")
            ktmb = k.sbuf([128, NSB, 128], BF16, "ktmb")
            blkF = k.sbuf([128, NSB, 128], BF16, "blkF")
            rowm = k.sbuf([128, NSB], F32, "rowm")
            k.memset("pool", (blkF, blkF[:]), 1.0)
            aff(blkF, blkF[:], [[-ST, NSB], [1, 128]], 0, 0)
            aff(blkF, blkF[:], [[ST, NSB], [-1, 128]], 0, ST - 1)
            k.memset("pool", (rowm, rowm[:]), 1.0)
            aff(rowm, rowm[:], [[-ST, NSB]], 1, 0)
            aff(rowm, rowm[:], [[ST, NSB]], -1, ST - 1)
            smc = cv.view(cv[:].rearrange("p a b -> p (a b)")[0:NSB * 3, :], "smc")
            ext_s = xm.view(xm[:].rearrange("p a b -> p (a b)")[:, 0:8 * NSB * 11].rearrange("p (c b t) -> p c b t", c=8, b=NSB), "ext_s")
            k.dma("sp", smc[:], st_mconv[:, :], reads=[st_mconv], writes=[smc])
            for c in range(8):
                k.tr((pM[0], pM[0][:, c * 48:(c + 1) * 48]), (smc, smc[:, c * 128:(c + 1) * 128]), (identf, identf[0:48, 0:48]))
            k.copy("act", (ext_s, ext_s[:, :, :, 0:3]), (pM[0], pM[0][:, 0:384].rearrange("p (c b j) -> p c b j", c=8, b=NSB)))
            k.dma("sp", mst[:, 0:NSB], st_mm[:, :].rearrange("b h -> h b"), reads=[st_mm], writes=[mst], allow_slow_non_contiguous=True)
        if ti not in prefetched:
            k.dma("sp", xb[:], xap, reads=[xd], writes=[xb])
            norm_to_hT(xb, g1bc)

        if smp:
            for g in range(2):
                for c in range(4):
                    proj_fm(pF[g], pF[g][:, c * 128:(c + 1) * 128], C_QK + (4 * g + c) * 128)
                k.copy("act", (ext_s, ext_s[:, 4 * g:4 * g + 4, :, 3:11]), (pF[g], pF[g][:].rearrange("p (c b t) -> p c b t", c=4, b=NSB)))
            for c in range(8):
                cvv = cv[:, c, :].rearrange("p (b t) -> p b t", t=ST)
                k.ts("dve", (cv, cvv), (ext_s, ext_s[:, c, :, 3:11]), pcol(PT_MCW + 3 * 8 + c), pcol(PT_MCB + c),
                     op0=ALU.mult, op1=ALU.add)
                for j in range(3):
                    k.stt("dve", (cv, cvv), (ext_s, ext_s[:, c, :, j:j + ST]), pcol(PT_MCW + j * 8 + c), (cv, cvv),
                          ALU.mult, ALU.add)
        if not smp:
            if ti not in prefetched:
                k.copy("pool", (ext_q, ext_q[:, :, 0:3]), (cq, cq[:]))
                for g in range(2):
                    for c in range(4):
                        proj_fm(pF[g], pF[g][:, c * 128:(c + 1) * 128], C_QK + (4 * g + c) * 128)
                    k.copy("act", (ext_q, ext_q[:, 4 * g:4 * g + 4, 3:131]), (pF[g], pF[g][:].rearrange("p (c t) -> p c t", c=4)))
                k.copy("pool", (cq, cq[:]), (ext_q, ext_q[:, :, 128:131]))
            for c in range(8):
                k.ts("dve", (cv, cv[:, c, :]), (ext_q, ext_q[:, c, 3:131]), pcol(PT_MCW + 3 * 8 + c), pcol(PT_MCB + c),
                     op0=ALU.mult, op1=ALU.add)
                for j in range(3):
                    k.stt("dve", (cv, cv[:, c, :]), (ext_q, ext_q[:, c, j:j + 128]), pcol(PT_MCW + j * 8 + c), (cv, cv[:, c, :]),
                          ALU.mult, ALU.add)
        k.act((qkT, qkT[:].rearrange("p a b -> p (a b)")), (cv, cv[:].rearrange("p a b -> p (a b)")), AF.Silu)

        if ti not in prefetched:
            proj_tm(pR[0], pR[0][:, :], C_V, 512)
            k.copy("act", (vaug, vaug[:, :, 0:128]), (pR[0], pR[0][:].rearrange("p (h c) -> p h c", h=4)))
            for c in range(4):
                proj_fm(pF[0], pF[0][:, c * 128:(c + 1) * 128], C_O + c * 128)
            k.act((soT, soT[:].rearrange("p a b -> p (a b)")), (pF[0], pF[0][:, :]), AF.Sigmoid)
        if not smp and stage >= 2:
            rwkv_front_proj(ti)
        proj_fm(pM[0], pM[0][0:4, 0:128], C_I, M=4)
        proj_fm(pM[0], pM[0][0:4, 128:256], C_F, M=4)
        liT, nlf, ncum, gT_, t0, t1 = gsm[0], gsm[1], gsm[2], gsm[3], gsm[4], gsm[5]
        k.ts("dve", (liT, liT[:]), (pM[0], pM[0][0:4, 0:128]), (gb, gb[:, 0:1]), None, op0=ALU.add)
        k.act((t0, t0[:]), (pM[0], pM[0][0:4, 128:256]), AF.Exp, bias=(nbf, nbf[:, 0:1]), scale=-1.0)
        k.act((nlf, nlf[:]), (t0, t0[:]), AF.Ln, bias=1.0)
        k.scan("dve", (ncum, ncum[:]), (resets[mi], resets[mi][0:4, 0:128]), (nlf, nlf[:]), 0.0, ALU.mult, ALU.add)
        k.tt("dve", (gT_, gT_[:]), (liT, liT[:]), (ncum, ncum[:]), ALU.add)
        b3 = lambda buf: buf[:].rearrange("p (b l) -> p b l", l=LB)
        mcb_ = mst[:, 0:NB].unsqueeze(2).to_broadcast([4, NB, LB])
        nlast = ncum[:].rearrange("p (b l) -> p b l", l=LB)[:, :, LB - 1:LB]
        k.stt("dve", (t0, b3(t0)), (gT_, b3(gT_)), LNK, (mst, mcb_), ALU.add, ALU.subtract)
        k.act((gpk, gpk[:, 0, :]), (t0, t0[:]), AF.Exp)
        k.tt("dve", (t1, b3(t1)), (ncum, b3(ncum)), (mst, mcb_), ALU.subtract)
        k.act((gpk, gpk[:, 1, :]), (t1, t1[:]), AF.Exp)
        k.tt("dve", (t1, b3(t1)), (gT_, b3(gT_)), (ncum, nlast.to_broadcast([4, NB, LB])), ALU.subtract)
        k.red("dve", (gt[0], gt[0][:, 0:NB]), (t1, b3(t1)), ALU.max)
        k.tt("dve", (gt[1], gt[1][:, 0:NB]), (mst, mst[:, 0:NB]), (ncum, nlast.rearrange("p b o -> p (b o)")), ALU.subtract)
        k.tt("dve", (mnew, mnew[:, 0:NB]), (gt[1], gt[1][:, 0:NB]), (gt[0], gt[0][:, 0:NB]), ALU.max)
        k.tt("dve", (gt[2], gt[2][:, 0:NB]), (gt[1], gt[1][:, 0:NB]), (mnew, mnew[:, 0:NB]), ALU.subtract)
        k.act((gt[3], gt[3][:, 0:NB]), (gt[2], gt[2][:, 0:NB]), AF.Exp)
        k.tt("dve", (gpk, gpk[:, 2, :].rearrange("p (b l) -> p b l", l=LB)), (gpk, gpk[:, 0, :].rearrange("p (b l) -> p b l", l=LB)),
             (gt[3], gt[3][:, 0:NB].unsqueeze(2).to_broadcast([4, NB, LB])), ALU.mult)
        for j in range(3):
            k.tr((pM[1], pM[1][:, 4 * j:4 * j + 4]), (gpk, gpk[:, j, :]), (identf, identf[0:4, 0:4]))
        k.copy("dve", (tokS, tokS[:]), (pM[1], pM[1][:, 0:12]))
        k.tt("dve", (s0d, s0d[:, :, 0:NB]), (identf, identf[0:4, 0:4].unsqueeze(2).to_broadcast([4, 4, NB])),
             (gt[3], gt[3][:, 0:NB].unsqueeze(1).to_broadcast([4, 4, NB])), ALU.mult)
        k.mm((pM[1], pM[1][:, 16:16 + 4 * NB]), (ones4, ones4[:]), (s0d, s0d[:, :, 0:NB].rearrange("p a b -> p (a b)")))
        k.copy("dve", (s0bc, s0bc[:, 0:4 * NB]), (pM[1], pM[1][:, 16:16 + 4 * NB]))

        for h in range(4):
            k.mm((pM[0], pM[0][:, h * 128:(h + 1) * 128]), (qkT, qkT[:, 4 + h, :]), (qkT, qkT[:, h, :]))
        for h in range(4):
            k.stt("dve", (PTm, PTm[:, h, :]), (pM[0], pM[0][:, h * 128:(h + 1) * 128]), (tokS, tokS[:, h:h + 1]),
                  (mU_in[mi], mU_in[mi][:]), ALU.mult, ALU.mult)
        pO = [pM[1], pM[2]]
        oap = lambda h: pO[h // 2][:, 256 * (h % 2):256 * (h % 2) + 129]
        if not smp:
            for h in range(4):
                k.mm((pO[h // 2], oap(h)), (qkT, qkT[:, h, :]), (Cb, Cb[:, h, 0:129]), start=True, stop=False)
                k.mm((pO[h // 2], oap(h)), (PTm, PTm[:, h, :]), (vaug, vaug[:, h, 0:129]), start=False, stop=True)
        else:
            for h in range(4):
                k.tr((pT, pT[:, h * 128:(h + 1) * 128]), (qkT, qkT[:, 4 + h, :]), (identb, identb[:]))
            for h in range(4):
                k.ts("dve", (ktm, ktm[:, h, :]), (pT, pT[:, h * 128:(h + 1) * 128]), (tokS, tokS[:, 8 + h:9 + h]), None, op0=ALU.mult)
            for h in range(4):
                k.dma("sp", Cs[:, :, 0:128], st_mC[:, h, :, :].rearrange("b d v -> d b v"), reads=[st_mC], writes=[Cs])
                k.dma("sp", Cs[:, :, 128], st_mn[:, h, :].rearrange("b d -> d b"), reads=[st_mn], writes=[Cs], allow_slow_non_contiguous=True)
                k.copy("act", (Csb, Csb[:, :, 0:129]), (Cs, Cs[:]))
                k.tt("dve", (qTm, qTm[:]), (qkT, qkT[:, h, :].unsqueeze(1).to_broadcast([128, NSB, 128])), (blkF, blkF[:]), ALU.mult)
                for b in range(NSB):
                    k.mm((pO[h // 2], oap(h)), (qTm, qTm[:, b, :]), (Csb, Csb[:, b, 0:129]), start=(b == 0), stop=False)
                k.mm((pO[h // 2], oap(h)), (PTm, PTm[:, h, :]), (vaug, vaug[:, h, 0:129]), start=False, stop=True)
                k.tt("dve", (ktmb, ktmb[:]), (ktm, ktm[:, h, :].unsqueeze(1).to_broadcast([128, NSB, 128])),
                     (rowm, rowm[:].unsqueeze(2).to_broadcast([128, NSB, 128])), ALU.mult)
                for grp in range(4):
                    bank = pF[grp % 2]
                    for bi in range(4):
                        b = 4 * grp + bi
                        k.mm((bank, bank[:, bi * 128:(bi + 1) * 128]), (ktmb, ktmb[:, b, :]), (vaug, vaug[:, h, 0:128]))
                    for bi in range(4):
                        b = 4 * grp + bi
                        k.stt("dve", (Cs, Cs[:, b, 0:128]), (Cs, Cs[:, b, 0:128]), (s0bc, s0bc[:, h * NSB + b:h * NSB + b + 1]),
                              (bank, bank[:, bi * 128:(bi + 1) * 128]), ALU.mult, ALU.add)
                for b in range(NSB):
                    k.mm((pR[0], pR[0][:, b:b + 1]), (ktmb, ktmb[:, b, :]), (vaug, vaug[:, h, 128:129]))
                k.tt("dve", (Cs, Cs[:, :, 128]), (Cs, Cs[:, :, 128]), (s0bc, s0bc[:, h * NSB:(h + 1) * NSB]), ALU.mult)
                k.tt("dve", (Cs, Cs[:, :, 128]), (Cs, Cs[:, :, 128]), (pR[0], pR[0][:, 0:NSB]), ALU.add)
                k.dma("sp", o_sC[:, h, :, :].rearrange("b d v -> d b v"), Cs[:, :, 0:128], reads=[Cs], writes=[o_sC])
                k.dma("sp", o_sn[:, h, :].rearrange("b d -> d b"), Cs[:, :, 128], reads=[Cs], writes=[o_sn], allow_slow_non_contiguous=True)
        for h in range(4):
            k.copy("act", (dn, dn[:, h:h + 1]), (pO[h // 2], oap(h)[:, 128:129]))
        k.stt("dve", (dn, dn[:]), (dn, dn[:]), -1.0, (dn, dn[:]), ALU.mult, ALU.max)
        k.tt("dve", (dn, dn[:]), (dn, dn[:]), (tokS, tokS[:, 4:8]), ALU.max)
        k.op("dve", lambda e: e.reciprocal(out=dn[:], in_=dn[:]), reads=[dn], writes=[dn])
        for h in range(4):
            k.act((hm, hm[:, h, :]), (pO[h // 2], oap(h)[:, 0:128]), AF.Copy, scale=(dn, dn[:, h:h + 1]))
        for h in range(4):
            k.op("dve", lambda e, h=h: e.bn_stats(out=bst[:, h, :], in_=hm[:, h, :]), reads=[hm], writes=[(bst, h)])
        for h in range(4):
            k.op("dve", lambda e, h=h: e.bn_aggr(out=bag[:, h, :], in_=bst[:, h, :]), reads=[(bst, h)], writes=[(bag, h)])
        k.act((bag, bag[:, :, 1:2]), (bag, bag[:, :, 1:2]), AF.Ln, bias=EPS)
        k.act((bag, bag[:, :, 1:2]), (bag, bag[:, :, 1:2]), AF.Exp, scale=-0.5)
        for h in range(4):
            k.ts("dve", (hn, hn[:, h, :]), (hm, hm[:, h, :]), (bag, bag[:, h, 0:1]), (bag, bag[:, h, 1:2]),
                 op0=ALU.subtract, op1=ALU.mult)
        for h in range(4):
            k.tr((pM[0], pM[0][:, h * 128:(h + 1) * 128]), (hn, hn[:, h, :]), (identf, identf[:]))
        for h in range(4):
            k.stt("dve", (hmT_all, hmT_all[:, ti, h, :], ti), (pM[0], pM[0][:, h * 128:(h + 1) * 128]), pcol(PT_MNG + h),
                  (soT, soT[:, h, :]), ALU.mult, ALU.mult)
        if not smp:
            for h in range(4):
                k.tr((pT, pT[:, h * 128:(h + 1) * 128]), (qkT, qkT[:, 4 + h, :]), (identb, identb[:]))
            for h in range(4):
                k.ts("dve", (ktm, ktm[:, h, :]), (pT, pT[:, h * 128:(h + 1) * 128]), (tokS, tokS[:, 8 + h:9 + h]), None, op0=ALU.mult)
            for h in range(4):
                k.mm((pO[h // 2], oap(h)), (ktm, ktm[:, h, :]), (vaug, vaug[:, h, 0:129]))
            for h in range(4):
                k.stt("dve", (Cst, Cst[:, h, :]), (Cst, Cst[:, h, :]), (s0bc, s0bc[:, h:h + 1]), (pO[h // 2], oap(h)),
                      ALU.mult, ALU.add)
            k.copy("act", (Cb, Cb[:, :, 0:129]), (Cst, Cst[:]))
            k.copy("dve", (mst, mst[:, 0:1]), (mnew, mnew[:, 0:1]))
        if ti == NTP - 1:
            for h in range(4):
                k.dma("sp", o_pC[h], Cst[:, h, 0:128], reads=[Cst], writes=[o_pC])
            k.dma("sp", o_pn[:].rearrange("h d -> d h"), Cst[:, :, 128], reads=[Cst], writes=[o_pn], allow_slow_non_contiguous=True)
            k.dma("sp", o_pm[:].rearrange("o h -> h o"), mnew[:, 0:1], reads=[mnew], writes=[o_pm], allow_slow_non_contiguous=True)
            for blk in range(2):
                proj_tm(pR[blk], pR[blk][:, :], C_QK + blk * 512, 512)
                k.copy("act", (zq_tm, zq_tm[:, blk * 512:(blk + 1) * 512]), (pR[blk], pR[blk][:, :]))
            k.dma("sp", o_pconv[:], zq_tm[125:128, :], reads=[zq_tm], writes=[o_pconv])
        if smp:
            k.dma("sp", o_sm[:, :].rearrange("b h -> h b"), mnew[:, 0:NSB], reads=[mnew], writes=[o_sm], allow_slow_non_contiguous=True)
            for blk in range(2):
                proj_tm(pR[blk], pR[blk][:, :], C_QK + blk * 512, 512)
                k.copy("act", (zq_tm, zq_tm[:, blk * 512:(blk + 1) * 512]), (pR[blk], pR[blk][:, :]))
            for b in range(NSB):
                k.dma("sp", o_sconv[3 * b:3 * b + 3, :], zq_tm[ST * b + 5:ST * b + 8, :], reads=[zq_tm], writes=[o_sconv])

    k.fence_mm = (pT, identb)
    BK = [pM[0], pM[1], pF[0], pF[1], pR[0], pR[1]]

    _rw_stop = int(_os.environ.get("KDBG_RW", "99"))

    def rwkv_front_proj(ti):
        k.copy("pool", (ext_r, ext_r[:, :, 0:1]), (cr, cr[:]))
        for g in range(4):
            n = min(4, 14 - 4 * g)
            for c in range(n):
                proj_fm(pF[g % 2], pF[g % 2][:, c * 128:(c + 1) * 128], C_R + (4 * g + c) * 128)
            k.copy("act", (ext_r, ext_r[:, 4 * g:4 * g + n, 1:129]),
                   (pF[g % 2], pF[g % 2][:, 0:n * 128].rearrange("p (c t) -> p c t", c=n)))
        k.copy("pool", (cr, cr[:]), (ext_r, ext_r[:, :, 128:129]))
        k.tt("pool", (xm, xm[:]), (ext_r, ext_r[:, :, 0:128]), (ext_r, ext_r[:, :, 1:129]), ALU.subtract)
        for c in range(14):
            k.stt("dve", (xm, xm[:, c, :]), (xm, xm[:, c, :]), pcol(PT_RMIX + c), (ext_r, ext_r[:, c, 1:129]), ALU.mult, ALU.add)

    def rwkv_tile(ti):
        smp = ti == NTP
        mi = 0
        NLV = 7
        rtl = rt
        if not smp:
            pass
        else:
            k.barrier()
            k.bot = mark_rw
            ext_rs = k.sbuf([128, 14, NSB, ST + 1], F32, "ext_rs")
            rtl = [k.sbuf([128, 4, 128], F32, f"rts{i}") for i in range(7)] + [rt[7], rt[8]]
            stg = k.sbuf([128, 512], F32, "stg")
            srs = xm.view(xm[:].rearrange("p a b -> p (a b)")[0:NSB, :], "srs")
            k.dma("sp", srs[:], st_rshift[:, :], reads=[st_rshift], writes=[srs])
            for c in range(14):
                k.tr((pM[0], pM[0][:, c * NSB:(c + 1) * NSB]), (srs, srs[:, c * 128:(c + 1) * 128]), (identf, identf[0:NSB, 0:NSB]))
            k.copy("act", (ext_rs, ext_rs[:, :, :, 0]), (pM[0], pM[0][:, 0:14 * NSB].rearrange("p (c b) -> p c b", c=14)))
            for g in range(4):
                n = min(4, 14 - 4 * g)
                for c in range(n):
                    proj_fm(pF[g % 2], pF[g % 2][:, c * 128:(c + 1) * 128], C_R + (4 * g + c) * 128)
                k.copy("act", (ext_rs, ext_rs[:, 4 * g:4 * g + n, :, 1:ST + 1]),
                       (pF[g % 2], pF[g % 2][:, 0:n * 128].rearrange("p (c b t) -> p c b t", c=n, b=NSB)))
            xm4 = xm[:].rearrange("p c (b t) -> p c b t", t=ST)
            k.tt("pool", (xm, xm4), (ext_rs, ext_rs[:, :, :, 0:ST]), (ext_rs, ext_rs[:, :, :, 1:ST + 1]), ALU.subtract)
            for c in range(14):
                k.stt("dve", (xm, xm4[:, c]), (xm, xm4[:, c]), pcol(PT_RMIX + c), (ext_rs, ext_rs[:, c, :, 1:ST + 1]), ALU.mult, ALU.add)
        rT, krT, vrT = xm[:, 0:4, :], xm[:, 4:8, :], xm[:, 8:12, :]
        sig, cums, gam, ginv, gexc, a_, kk, tmp, kr2 = rtl
        if _rw_stop <= 1:
            return
        k.act((thad, thad[0:64, :]), (xm, xm[0:64, 12, :]), AF.Tanh)
        k.copy("act", (thad, thad[64:128, :]), (xm, xm[64:128, 12, :]))
        k.act((sgd, sgd[:]), (xm, xm[:, 13, :]), AF.Sigmoid)
        for c in range(4):
            k.mm((pM[0], pM[0][:, c * 128:(c + 1) * 128]), (Wl_w2, Wl_w2[0:64, c * 128:(c + 1) * 128]), (thad, thad[0:64, :]))
        for c in range(4):
            k.act((sig, sig[:, c, :]), (pM[0], pM[0][:, c * 128:(c + 1) * 128]), AF.Sigmoid, bias=pcol(PT_RW0 + c))
        k.pe_fence()
        for c in range(4):
            k.mm((pM[1], pM[1][:, c * 128:(c + 1) * 128]), (Wl_a2, Wl_a2[64:128, c * 128:(c + 1) * 128]), (thad, thad[64:128, :]))
        k.pe_fence()
        for c in range(4):
            k.act((a_, a_[:, c, :]), (pM[1], pM[1][:, c * 128:(c + 1) * 128]), AF.Sigmoid, bias=pcol(PT_RA0 + c))
        for c in range(4):
            k.mm((pM[2], pM[2][:, c * 128:(c + 1) * 128]), (Wl_g2, Wl_g2[:, c * 128:(c + 1) * 128]), (sgd, sgd[:]))
        k.copy("act", (gTs, gTs[:].rearrange("p a b -> p (a b)")), (pM[2], pM[2][:, :]))
        if _rw_stop <= 2:
            return
        fl = lambda b: b[:].rearrange("p a b -> p (a b)")
        if not smp:
            k.scan("dve", (cums, fl(cums)), (resets[mi], resets[mi][:]), (sig, fl(sig)), 0.0, ALU.mult, ALU.add)
            k.act((gam, fl(gam)), (cums, fl(cums)), AF.Exp, scale=WSCALE)
            k.act((ginv, fl(ginv)), (cums, fl(cums)), AF.Exp, scale=-WSCALE)
            k.tt("pool", (tmp, tmp[:]), (cums, cums[:]), (sig, sig[:]), ALU.subtract)
            k.act((gexc, fl(gexc)), (tmp, fl(tmp)), AF.Exp, scale=WSCALE)
        else:
            k.act((gam, fl(gam)), (sig, fl(sig)), AF.Exp, scale=WSCALE)
        if _rw_stop <= 3:
            return
        for c in range(4):
            k.ts("dve", (kk, kk[:, c, :]), (xm, xm[:, 4 + c, :]), pcol(PT_RKK + c), None, op0=ALU.mult)
        k.tt("pool", (tmp, tmp[:]), (kk, kk[:]), (kk, kk[:]), ALU.mult)
        for c in range(4):
            k.mm((pM[0], pM[0][:, c * 128:(c + 1) * 128]), (bones, bones[:]), (tmp, tmp[:, c, :]))
        k.ts("dve", (tmp, fl(tmp)), (pM[0], pM[0][:, :]), 1e-24, None, op0=ALU.max)
        k.act((tmp, fl(tmp)), (tmp, fl(tmp)), AF.Ln)
        k.act((tmp, fl(tmp)), (tmp, fl(tmp)), AF.Exp, scale=-0.5)
        k.tt("dve", (kk, kk[:]), (kk, kk[:]), (tmp, tmp[:]), ALU.mult)
        for c in range(4):
            k.ts("dve", (tmp, tmp[:, c, :]), (a_, a_[:, c, :]), -1.0, pcol(PT_RKA + c), op0=ALU.add, op1=ALU.mult)
        k.stt("dve", (kr2, kr2[:]), (tmp, tmp[:]), 1.0, (xm, krT), ALU.add, ALU.mult)
        k.tt("pool", (tmp, tmp[:]), (xm, rT), (kr2, kr2[:]), ALU.mult)
        for c in range(4):
            k.ts("dve", (tmp, tmp[:, c, :]), (tmp, tmp[:, c, :]), pcol(PT_RRK + c), None, op0=ALU.mult)
        for c in range(4):
            k.mm((pM[1], pM[1][:, c * 128:(c + 1) * 128]), (bones, bones[:]), (tmp, tmp[:, c, :]))
        k.tt("dve", (bonT, fl(bonT)), (pM[1], pM[1][:, :]), (xm, vrT.rearrange("p a b -> p (a b)") if False else xm[:, 8:12, :].rearrange("p a b -> p (a b)")), ALU.mult)
        if _rw_stop <= 4:
            return
        if smp:
            rwkv_sample_core(xm, gam, kr2, kk, a_, tmp, stg, gTs, bonT)
            return
        k.stt("dve", (ART, ART[:, :, 0, :]), (kk, kk[:]), -1.0, (gexc, gexc[:]), ALU.mult, ALU.mult)
        k.tt("pool", (ART, ART[:, :, 1, :]), (xm, rT), (gam, gam[:]), ALU.mult)
        k.tt("pool", (tmp, tmp[:]), (kk, kk[:]), (a_, a_[:]), ALU.mult)
        k.tt("dve", (BTb, BTb[:]), (tmp, tmp[:]), (ginv, ginv[:]), ALU.mult)
        k.tt("pool", (KTb, KTb[:]), (kr2, kr2[:]), (ginv, ginv[:]), ALU.mult)
        k.copy("act", (VTb, VTb[:]), (xm, vrT))
        if _rw_stop <= 5:
            return
        for c in range(4):
            k.tr((pT, pT[:, c * 128:(c + 1) * 128]), (ART, ART[:, c, 0, :]), (identb, identb[:]))
            k.tr((pT, pT[:, 512 + c * 128:512 + (c + 1) * 128]), (BTb, BTb[:, c, :]), (identb, identb[:]))
        k.copy("act", (AB_tm, AB_tm[:].rearrange("p a b -> p (a b)")), (pT, pT[:, :]))
        for c in range(4):
            k.tr((pT, pT[:, c * 128:(c + 1) * 128]), (KTb, KTb[:, c, :]), (identb, identb[:]))
            k.tr((pT, pT[:, 512 + c * 128:512 + (c + 1) * 128]), (VTb, VTb[:, c, :]), (identb, identb[:]))
        k.copy("dve", (KV_tm, KV_tm[:].rearrange("p a b -> p (a b)")), (pT, pT[:, :]))
        if _rw_stop <= 6:
            return
        A_tm = lambda h: (AB_tm, AB_tm[:, 0, h * 64:(h + 1) * 64])
        B_tm = lambda h: (AB_tm, AB_tm[:, 1, h * 64:(h + 1) * 64])
        K_tm = lambda h: (KV_tm, KV_tm[:, 0, h * 64:(h + 1) * 64])
        V_tm = lambda h: (KV_tm, KV_tm[:, 1, h * 64:(h + 1) * 64])
        m2b = mask2[mi][:].unsqueeze(1).to_broadcast([128, 2, 256])
        GB2, GK2, Nn2, PP2, XX2 = [GBm, GBm_b], [GKm, GKm_b], [Nn, Nn_b], [PP, PP_b], [XX, XX_b]
        LB3 = [[pM[0], pM[1], pR[0]], [pF[0], pF[1], pR[1]]]
        for g in range(2):
            GBm_, GKm_, Nn_, XX_ = GB2[g], GK2[g], Nn2[g], XX2[g]
            heads = [4 * g + i for i in range(4)]
            HO = [(pbs, [(i, h) for i, h in enumerate(heads) if 64 * (h % 2) == pbs]) for pbs in (0, 64)]
            for pbs, hl in HO:
                for i, h in hl:
                    c, pb = h // 2, 64 * (h % 2)
                    off = (i % 2) * 256
                    rAR = (ART, ART[pb:pb + 64, c, :, :].rearrange("p a t -> p (a t)"))
                    k.mm((BK[i // 2], BK[i // 2][:, off:off + 256]), (BTb, BTb[pb:pb + 64, c, :]), rAR)
                    k.mm((BK[2 + i // 2], BK[2 + i // 2][:, off:off + 256]), (KTb, KTb[pb:pb + 64, c, :]), rAR)
                    k.mm((BK[4], BK[4][:, i * 128:(i + 1) * 128]), (ART, ART[pb:pb + 64, c, 0, :]), (BTb, BTb[pb:pb + 64, c, :]))
                k.pe_fence()
            for hf in range(2):
                k.tt("dve", (GBm_, GBm_[:, 2 * hf:2 * hf + 2, :]), (BK[hf], BK[hf][:].rearrange("p (a b) -> p a b", a=2)), (mask2[mi], m2b), ALU.mult)
                k.tt("dve", (GKm_, GKm_[:, 2 * hf:2 * hf + 2, :]), (BK[2 + hf], BK[2 + hf][:].rearrange("p (a b) -> p a b", a=2)), (mask2[mi], m2b), ALU.mult)
            k.tt("dve", (Nn_, Nn_[:]), (BK[4], BK[4][:].rearrange("p (a b) -> p a b", a=4)),
                 (mL_st[mi], mL_st[mi][:].unsqueeze(1).to_broadcast([128, 4, 128])), ALU.mult)
            for i, h in enumerate(heads):
                k.mm((BK[5], BK[5][:, i * 64:(i + 1) * 64]), (GKm_, GKm_[:, i, 0:128]), V_tm(h))
            k.copy("act", (XX_[0], XX_[0][:, :, 64:128]), (BK[5], BK[5][:, 0:256].rearrange("p (a b) -> p a b", a=4)))
            k.copy("pool", (XX_[0], XX_[0][:, :, 0:64]), (AB_tm, AB_tm[:, 0, 256 * g:256 * g + 256].rearrange("p (a b) -> p a b", a=4)))

        xfinal = [None, None]

        def levels_gen(g):
            GBm_, Nn_, PP_, XX_ = GB2[g], Nn2[g], PP2[g], XX2[g]
            bP, bQ, bX = LB3[g]
            Pc = lambda i: (Nn_, Nn_[:, i, :])
            PTc = lambda i: (GBm_, GBm_[:, i, 0:128])
            xi = 0
            for lvl in range(NLV):
                Xc, Xn = XX_[xi], XX_[1 - xi]
                for i in range(4):
                    o = (bX, bX[:, i * 128:(i + 1) * 128])
                    k.mm(o, (identb, identb[:]), (Xc, Xc[:, i, :]), start=True, stop=False)
                    k.mm(o, PTc(i), (Xc, Xc[:, i, :]), start=False, stop=True)
                k.copy("act", (Xn, Xn[:].rearrange("p a b -> p (a b)")), (bX, bX[:, :]))
                xi = 1 - xi
                yield
                if lvl < NLV - 1:
                    bb = [bP, bQ]
                    for i in range(4):
                        off = (i % 2) * 256
                        if lvl < NLV - 2:
                            k.mm((bb[i // 2], bb[i // 2][:, off:off + 128]), PTc(i), Pc(i))
                        k.mm((bb[i // 2], bb[i // 2][:, off + 128:off + 256]), Pc(i), PTc(i))
                    PPn = PP_[lvl % 2]
                    for hf in range(2):
                        if lvl < NLV - 2:
                            k.copy("dve", (PPn, PPn[:, 2 * hf:2 * hf + 2, :]), (bb[hf], bb[hf][:].rearrange("p (a b) -> p a b", a=2)))
                        else:
                            k.copy("dve", (PPn, PPn[:, 2 * hf:2 * hf + 2, 128:256]),
                                   (bb[hf], bb[hf][:].rearrange("p (a b) -> p a b", a=2)[:, :, 128:256]))
                    Pc = lambda i, PPn=PPn: (PPn, PPn[:, i, 0:128])
                    PTc = lambda i, PPn=PPn: (PPn, PPn[:, i, 128:256])
                    yield
            xfinal[g] = XX_[xi]

        gens = [levels_gen(0), levels_gen(1)]
        if ti + 1 < NTP and (ti + 1) in tiles_run:
            gens.append(prefetch_gen(ti + 1))
            prefetched.add(ti + 1)
        while gens:
            for g_ in list(gens):
                try:
                    next(g_)
                except StopIteration:
                    gens.remove(g_)

        for g in range(2):
            heads = [4 * g + i for i in range(4)]
            HO = [(pbs, [(i, h) for i, h in enumerate(heads) if 64 * (h % 2) == pbs]) for pbs in (0, 64)]
            GBt, GKt = GB2[g], GK2[g]
            Xf = xfinal[g]
            if _rw_stop <= 8:
                continue
            k.pe_fence()
            for pbs, hl in HO:
                for i, h in hl:
                    c, pb = h // 2, 64 * (h % 2)
                    ci = i // 2
                    o = (BK[0], BK[0][pb:pb + 64, ci * 128:(ci + 1) * 128])
                    k.mm(o, (Xf, Xf[:, i, 0:64]), (GBt, GBt[:, i, 128:256]), start=True, stop=False)
                    k.pe_fence()
                    k.mm(o, (identb, identb[pb:pb + 64, pb:pb + 64]), (ART, ART[pb:pb + 64, c, 1, :]), start=False, stop=True)
                    k.pe_fence()
            k.copy("act", (QT, QT[:].rearrange("p a b -> p (a b)")), (BK[0], BK[0][:, 0:256]))
            for pbs, hl in HO:
                for i, h in hl:
                    c, pb = h // 2, 64 * (h % 2)
                    ci = i // 2
                    o = (pM[2], pM[2][:, h * 64:(h + 1) * 64])
                    k.mm(o, (QT, QT[pb:pb + 64, ci, :]), (STb, STb[pb:pb + 64, c, :]), start=True, stop=False)
                    k.pe_fence()
                    k.mm(o, (GBt, GBt[:, i, 128:256]), (Xf, Xf[:, i, 64:128]), start=False, stop=False)
                    k.mm(o, (GKt, GKt[:, i, 128:256]), V_tm(h), start=False, stop=True)
                    k.pe_fence()
            if _rw_stop <= 9:
                continue
            for pbs, hl in HO:
                for i, h in hl:
                    c, pb = h // 2, 64 * (h % 2)
                    ci = i // 2
                    k.mm((BK[1], BK[1][pb:pb + 64, ci * 64:(ci + 1) * 64]), (Xf, Xf[:, i, 0:64]), B_tm(h))
                k.pe_fence()
            k.tt("dve", (IE, IE[:, 2 * g:2 * g + 2, :]), (BK[1], BK[1][:, 0:128].rearrange("p (a b) -> p a b", a=2)),
                 (I2, I2[:].unsqueeze(1).to_broadcast([128, 2, 64])), ALU.add)
            for pbs, hl in HO:
                for i, h in hl:
                    c, pb = h // 2, 64 * (h % 2)
                    ci = i // 2
                    o = (BK[2], BK[2][pb:pb + 64, ci * 64:(ci + 1) * 64])
                    k.mm(o, (IE, IE[pb:pb + 64, c, :]), (STf, STf[pb:pb + 64, c, :]), start=True, stop=False)
                    k.pe_fence()
                    k.mm(o, B_tm(h), (Xf, Xf[:, i, 64:128]), start=False, stop=False)
                    k.mm(o, K_tm(h), V_tm(h), start=False, stop=True)
                    k.pe_fence()
            for ci in range(2):
                c = 2 * g + ci
                k.ts("dve", (STf, STf[:, c, :]), (BK[2], BK[2][:, ci * 64:(ci + 1) * 64]), (gam, gam[:, c, 127:128]), None, op0=ALU.mult)
            k.copy("act", (STb, STb[:, 2 * g:2 * g + 2, :]), (STf, STf[:, 2 * g:2 * g + 2, :]))
        if _rw_stop <= 10:
            return
        rwkv_epilogue(ti, pM[2], tmp)
        if ti == NTP - 1:
            for c in range(4):
                k.tr((pM[0], pM[0][0:64, c * 128:(c + 1) * 128]), (STf, STf[:, c, :]), (identf, identf[:]))
            k.copy("act", (rt[0], rt[0][0:64, :, :]), (pM[0], pM[0][0:64, :].rearrange("p (a b) -> p a b", a=4)))
            k.dma("sp", o_pS[:].rearrange("(h i) j -> i h j", h=8), rt[0][0:64, :, :].rearrange("p c (f j) -> p (c f) j", f=2),
                  reads=[rt[0]], writes=[o_pS])

    def rwkv_epilogue(ti, Yb, tmp):
        pM2 = [None, None, Yb]
        for h in range(8):
            k.op("dve", lambda e, h=h: e.bn_stats(out=bst8[:, h, :], in_=Yb[:, h * 64:(h + 1) * 64]), reads=[Yb], writes=[(bst8, h)])
        for h in range(8):
            k.op("dve", lambda e, h=h: e.bn_aggr(out=bag8[:, h, :], in_=bst8[:, h, :]), reads=[(bst8, h)], writes=[(bag8, h)])
        k.act((bag8, bag8[:, :, 1:2]), (bag8, bag8[:, :, 1:2]), AF.Ln, bias=GN_EPS)
        k.act((bag8, bag8[:, :, 1:2]), (bag8, bag8[:, :, 1:2]), AF.Exp, scale=-0.5)
        for h in range(8):
            k.ts("dve", (yn, yn[:, h, :]), (Yb, Yb[:, h * 64:(h + 1) * 64]), (bag8, bag8[:, h, 0:1]), (bag8, bag8[:, h, 1:2]),
                 op0=ALU.subtract, op1=ALU.mult)
        for c in range(4):
            k.tr((pM[0], pM[0][:, c * 128:(c + 1) * 128]), (yn, yn[:, 2 * c:2 * c + 2, :].rearrange("p a b -> p (a b)")), (identf, identf[:]))
        for c in range(4):
            k.ts("dve", (tmp, tmp[:, c, :]), (pM[0], pM[0][:, c * 128:(c + 1) * 128]), pcol(PT_RLNG + c), pcol(PT_RLNB + c),
                 op0=ALU.mult, op1=ALU.add)
        k.tt("pool", (tmp, tmp[:]), (tmp, tmp[:]), (bonT, bonT[:]), ALU.add)
        k.tt("dve", (yrgT_all, yrgT_all[:, ti, :, :], ti), (tmp, tmp[:]), (gTs, gTs[:]), ALU.mult)

    rsc = k.dram("rw_scratch", [6, 128, RW], F32)
    ysc = k.dram("ry_scratch", [128, RW], F32)

    def rwkv_sample_core(xm, dec, kr2, kk, a_, tmp, stg, gTs, bonT):
        ti = NTP
        srcs = []
        srcs.append((xm, lambda c: xm[:, c, :]))
        srcs.append((dec, lambda c: dec[:, c, :]))
        srcs.append((kr2, lambda c: kr2[:, c, :]))
        srcs.append((xm, lambda c: xm[:, 8 + c, :]))
        for q in range(6):
            if q == 4:
                k.ts("dve", (tmp, tmp[:]), (kk, kk[:]), -1.0, None, op0=ALU.mult)
                sb_, fn = tmp, (lambda c: tmp[:, c, :])
            elif q == 5:
                k.tt("dve", (tmp, tmp[:]), (kk, kk[:]), (a_, a_[:]), ALU.mult)
                sb_, fn = tmp, (lambda c: tmp[:, c, :])
            else:
                sb_, fn = srcs[q]
            pb_ = pM[q % 2]
            for c in range(4):
                k.tr((pb_, pb_[:, c * 128:(c + 1) * 128]), (sb_, fn(c)), (identf, identf[:]))
            k.copy("act", (stg, stg[:]), (pb_, pb_[:, :]))
            k.dma("sp", rsc[q], stg[:], reads=[stg], writes=[(rsc, q)])
        for blk, (c0, n) in enumerate(((0, 512), (512, 512), (1024, 512), (1536, 256))):
            proj_tm(pR[blk % 2], pR[blk % 2][:, 0:n], C_R + c0, n)
            k.copy("act", (stg, stg[:, 0:n]), (pR[blk % 2], pR[blk % 2][:, 0:n]))
            for b in range(NSB):
                k.dma("sp", o_sshift[b:b + 1, c0:c0 + n], stg[ST * b + ST - 1:ST * b + ST, 0:n], reads=[stg], writes=[o_sshift])
        k.barrier()
        k.bot = mark_rw
        vec6 = k.sbuf([128, 6, ST, RN], F32, "vec6")
        Ssb = k.sbuf([128, RN, RN], F32, "Ssb")
        tmpS = k.sbuf([128, RN, RN], F32, "tmpS")
        sa = k.sbuf([128, RN], F32, "sa")
        ys = k.sbuf([128, ST, RN], F32, "ys")
        Ytm = k.sbuf([128, RW], F32, "Ytm")
        k.dma("sp", Ssb[:].rearrange("p a b -> p (a b)"), st_rS[:, :], reads=[st_rS], writes=[Ssb])
        for q in range(6):
            for b in range(NSB):
                k.dma("sp", vec6[RH * b:RH * b + RH, q, :, :], rsc[q, ST * b:ST * b + ST, :].rearrange("t (h j) -> h t j", h=RH),
                      reads=[(rsc, q)], writes=[(vec6, q)])
        HV = RN // 2

        def rec_gen(hf):
            i0 = hf * HV
            S_ = (Ssb, Ssb[:, i0:i0 + HV, :], hf)
            T_ = (tmpS, tmpS[:, i0:i0 + HV, :], hf)
            bc = lambda q, t: (vec6, vec6[:, q, t, :].unsqueeze(1).to_broadcast([128, HV, RN]), q)
            for t in range(ST):
                k.tt("dve", T_, S_, bc(4, t), ALU.mult)
                yield
                k.red("dve", (sa, sa[:, i0:i0 + HV], hf), T_, ALU.add)
                yield
                k.tt("pool", S_, S_, bc(1, t), ALU.mult)
                yield
                k.tt("dve", T_, (sa, sa[:, i0:i0 + HV].unsqueeze(2).to_broadcast([128, HV, RN]), hf), bc(5, t), ALU.mult)
                yield
                k.tt("dve", S_, S_, T_, ALU.add)
                yield
                k.tt("pool", T_, (vec6, vec6[:, 3, t, i0:i0 + HV].unsqueeze(2).to_broadcast([128, HV, RN]), 3), bc(2, t), ALU.mult)
                yield
                k.tt("dve", S_, S_, T_, ALU.add)
                yield
                k.tt("pool", T_, S_, bc(0, t), ALU.mult)
                yield
                k.red("dve", (ys, ys[:, t, i0:i0 + HV], (hf, t)), T_, ALU.add)
                yield

        gens_ = [rec_gen(0), rec_gen(1)]
        while gens_:
            for g_ in list(gens_):
                try:
                    next(g_)
                except StopIteration:
                    gens_.remove(g_)
        k.dma("sp", o_sS[:, :], Ssb[:].rearrange("p a b -> p (a b)"), reads=[Ssb], writes=[o_sS])
        k.dma("sp", ysc[:, :], ys[:].rearrange("p a b -> p (a b)"), reads=[ys], writes=[ysc])
        for b in range(NSB):
            k.dma("sp", Ytm[ST * b:ST * b + ST, :].rearrange("t (h i) -> t h i", h=RH),
                  ysc[RH * b:RH * b + RH, :].rearrange("h (t i) -> t h i", t=ST), reads=[ysc], writes=[Ytm])
        rwkv_epilogue(ti, Ytm, rt[7])

    def tail_rows(ti):
        if ti != NTP - 1:
            return
        for blk, (c0, n) in enumerate(((0, 512), (512, 512), (1024, 512), (1536, 256))):
            proj_tm(pR[blk % 2], pR[blk % 2][:, 0:n], C_R + c0, n)
            k.copy("act", (rt[1], rt[1][96:128, :, :].rearrange("p a b -> p (a b)")[:, 0:n]), (pR[blk % 2], pR[blk % 2][96:128, 0:n]))
            k.dma("sp", o_pshift[0:1, c0:c0 + n], rt[1][127:128, :, :].rearrange("p a b -> p (a b)")[:, 0:n], reads=[rt[1]], writes=[o_pshift])

    tiles_all = list(range(NT)) if stage >= 5 else list(range(NTP))

    def phase_1b(k):
        k.barrier()
        k.bot = mark_1a
        W_g = k.sbuf([128, 8, 2048], BF16, "W_g")
        W_bm = k.sbuf([128, 4, D], BF16, "W_bm")
        W_br = k.sbuf([128, 4, D], BF16, "W_br")
        W_out = k.sbuf([128, 8, D], BF16, "W_out")
        k.dma("pool", W_bm[:], d_w_bm[:], reads=[d_w_bm], writes=[W_bm])
        for kh in range(2):
            k.dma("pool", W_g[:, 4 * kh:4 * kh + 4, 0:1024], d_w_in[:, 4 * kh:4 * kh + 4, C_G:C_G + 1024], reads=[d_w_in], writes=[(W_g, "a")])
        k.dma("pool", W_br[:], d_w_br[:], reads=[d_w_br], writes=[W_br])
        for kh in range(2):
            k.dma("pool", W_g[:, 4 * kh:4 * kh + 4, 1024:2048], d_w_in[:, 4 * kh:4 * kh + 4, C_G + 1024:C_G + 2048], reads=[d_w_in], writes=[(W_g, "b")])
        k.dma("pool", W_out[:], d_w_out[:], reads=[d_w_out], writes=[W_out])
        xtb = [k.sbuf([128, D], F32, f"xtb{i}") for i in range(2)]
        hb2 = k.sbuf([128, D], BF16, "hb2")
        hT2 = k.sbuf([128, 8, 128], BF16, "hT2")
        ss2 = k.sbuf([128, 1], F32, "ss2")
        rs2 = k.sbuf([128, 1], F32, "rs2")
        sgb = [k.sbuf([128, 512], F32, f"sgb{i}") for i in range(2)]
        yab = k.sbuf([128, D], F32, "yab")
        mg = [k.sbuf([128, D], BF16, f"mg{i}") for i in range(2)]
        mT = k.sbuf([128, 8, 128], BF16, "mT")
        def head_gen(ti):
            xb = xtb[ti % 2]
            xd, xap = x_rows(ti)
            k.dma("sp", xb[:], xap, reads=[xd], writes=[xb])
            norm_generic(xb, g1bc, hb2, hT2, ss2, rs2)
            yield
            for half, (Wb, src, key) in enumerate(((W_bm, hmT_all, "a"), (W_br, yrgT_all, "b"))):
                for blk in range(2):
                    for kc in range(4):
                        k.mm((pR[blk], pR[blk][:, :]), (src, src[:, ti, kc, :], ti), (Wb, Wb[:, kc, blk * 512:(blk + 1) * 512]),
                             start=(kc == 0), stop=(kc == 3))
                    col = half * 1024 + blk * 512
                    for kc in range(8):
                        k.mm((pF[blk], pF[blk][:, :]), (hT2, hT2[:, kc, :]), (W_g, W_g[:, kc, col:col + 512], key),
                             start=(kc == 0), stop=(kc == 7))
                    k.act((sgb[blk], sgb[blk][:]), (pF[blk], pF[blk][:, :]), AF.Sigmoid)
                    if half == 0:
                        k.tt("dve", (yab, yab[:, blk * 512:(blk + 1) * 512]), (sgb[blk], sgb[blk][:]), (pR[blk], pR[blk][:, :]), ALU.mult)
                    else:
                        k.tt("dve", (sgb[blk], sgb[blk][:]), (sgb[blk], sgb[blk][:]), (pR[blk], pR[blk][:, :]), ALU.mult)
                        k.tt("pool", (mg[ti % 2], mg[ti % 2][:, blk * 512:(blk + 1) * 512]), (sgb[blk], sgb[blk][:]), (yab, yab[:, blk * 512:(blk + 1) * 512]), ALU.add)
                    yield

        def tailb_gen(ti):
            xb = xtb[ti % 2]
            mgt = mg[ti % 2]
            for kc in range(8):
                k.tr((pT, pT[:, kc * 128:(kc + 1) * 128]), (mgt, mgt[:, kc * 128:(kc + 1) * 128]), (identb, identb[:]))
            k.copy("act", (mT, mT[:].rearrange("p a b -> p (a b)")), (pT, pT[:, :]))
            yield
            for blk in range(2):
                for kc in range(8):
                    k.mm((pM[blk], pM[blk][:, :]), (mT, mT[:, kc, :]), (W_out, W_out[:, kc, blk * 512:(blk + 1) * 512]),
                         start=(kc == 0), stop=(kc == 7))
                k.tt("dve", (xb, xb[:, blk * 512:(blk + 1) * 512]), (xb, xb[:, blk * 512:(blk + 1) * 512]), (pM[blk], pM[blk][:, :]), ALU.add)
                yield
            k.dma("sp", x1s[ti * 128:(ti + 1) * 128, :], xb[:], reads=[xb], writes=[(x1s, ti)])

        def rr(gens):
            gens = list(gens)
            while gens:
                for g_ in list(gens):
                    try:
                        next(g_)
                    except StopIteration:
                        gens.remove(g_)

        tlb = list(tiles_all)
        rr([head_gen(tlb[0])])
        for idx, ti in enumerate(tlb):
            gl = [tailb_gen(ti)]
            if idx + 1 < len(tlb):
                gl.append(head_gen(tlb[idx + 1]))
            rr(gl)

    def norm_generic(xbuf, gbc, hb_, hT_, ss_, rs_):
        k.act((hb_, hb_[:]), (xbuf, xbuf[:]), AF.Square, accum=(ss_, ss_[:]))
        k.ts("dve", (rs_, rs_[:]), (ss_, ss_[:]), 1.0 / D, EPS, op0=ALU.mult, op1=ALU.add)
        k.act((rs_, rs_[:]), (rs_, rs_[:]), AF.Ln)
        k.act((rs_, rs_[:]), (rs_, rs_[:]), AF.Exp, scale=-0.5)
        k.stt("dve", (hb_, hb_[:]), (xbuf, xbuf[:]), (rs_, rs_[:, 0:1]), (gbc, gbc[:]), ALU.mult, ALU.mult)
        for kc in range(8):
            k.tr((pT, pT[:, kc * 128:(kc + 1) * 128]), (hb_, hb_[:, kc * 128:(kc + 1) * 128]), (identb, identb[:]))
        k.copy("act", (hT_, hT_[:].rearrange("p a b -> p (a b)")), (pT, pT[:, :]))

    def phase_2(k):
        k.barrier()
        k.bot = mark_phase
        F_up = k.sbuf([128, 8, 2 * DFF], BF16, "F_up")
        F_dn = k.sbuf([128, NFC, D], BF16, "F_dn")
        PGW = k.sbuf([128, 8, D], BF16, "PGW")
        PPJ = k.sbuf([128, 2, D], BF16, "PPJ")
        NG = 4
        CW = DFF // NG
        for g in range(NG):
            for part in range(2):
                k.dma("pool", F_up[:, :, part * DFF + g * CW:part * DFF + (g + 1) * CW], d_f_up[:, :, part * DFF + g * CW:part * DFF + (g + 1) * CW],
                      reads=[d_f_up], writes=[(F_up, g)])
        for g in range(2):
            k.dma("pool", F_dn[:, 11 * g:11 * g + 11, :], d_f_down[:, 11 * g:11 * g + 11, :], reads=[d_f_down], writes=[(F_dn, g)])
        k.dma("pool", PGW[:], d_pgw[:], reads=[d_pgw], writes=[PGW])
        k.dma("pool", PPJ[:], d_ppj[:], reads=[d_ppj], writes=[PPJ])
        g2bc = k.sbuf([128, D], F32, "g2bc")
        g3bc = k.sbuf([128, D], F32, "g3bc")
        g4bc = k.sbuf([128, D], F32, "g4bc")
        fct = k.sbuf([128, 4 * NFC], F32, "fct")
        k.dma("sp", g2bc[:], d_g2[:], reads=[d_g2], writes=[g2bc])
        k.dma("sp", g3bc[:], d_g3[:], reads=[d_g3], writes=[g3bc])
        k.dma("sp", g4bc[:], d_g4[:], reads=[d_g4], writes=[g4bc])
        k.dma("sp", fct[:], d_fctab[:], reads=[d_fctab], writes=[fct])
        fcol = lambda c: (fct, fct[:, c:c + 1])
        xq = [k.sbuf([128, D], F32, f"xq{i}") for i in range(2)]
        hb3 = k.sbuf([128, D], BF16, "hb3")
        hT3s = [k.sbuf([128, 8, 128], BF16, f"hT3a{i}") for i in range(2)]
        ss3 = k.sbuf([128, 1], F32, "ss3")
        rs3 = k.sbuf([128, 1], F32, "rs3")
        gT = k.sbuf([128, NFC, 128], BF16, "gT")
        cf = k.sbuf([128, NFC, 2], F32, "cf")
        GS = 4
        EXW = NSB * (ST + 2)
        ex4 = [k.sbuf([128, GS, EXW], F32, f"ex4_{i}") for i in range(2)]
        cc4 = [k.sbuf([128, GS, 128], F32, f"cc4_{i}") for i in range(2)]
        t14 = [k.sbuf([128, GS, 128], F32, "t14_0")] * 2
        up4 = [k.sbuf([128, GS, 128], F32, f"up4_{i}") for i in range(2)]
        sg3 = [k.sbuf([128, 512], F32, "sg30")] * 2
        ppt = k.sbuf([128, PLE], F32, "ppt")
        ppb = k.sbuf([128, PLE], BF16, "ppb")
        peT = k.sbuf([128, 2, 128], BF16, "peT")
        utm = sg3[0]
        k.memset("pool", (cf, cf[:]), 0.0)
        cfs = k.sbuf([128, NFC, 2 * NSB], F32, "cfs")
        GC = 1.5957691216057308
        BLK6 = ((0, 512), (512, 512), (1024, 512), (1536, 512), (2048, 512), (2560, 256))
        groups = [list(range(g0, min(g0 + GS, NFC))) for g0 in range(0, NFC, GS)]
        gbank = [pF[0], pF[1]]
        ubank = [pM[0], pM[1]]

        def fup_w(col):
            g = col // CW
            g_hi = (col + 127) // CW
            return g, g_hi

        def stage_A(ti, gi, hT3):
            smp = ti == NTP
            p = gi % 2
            chunks = groups[gi]
            n = len(chunks)
            c0 = chunks[0]
            for part, bank in ((0, gbank[p]), (1, ubank[p])):
                for ci, c in enumerate(chunks):
                    g, g_hi = fup_w(c * 128)
                    for kc in range(8):
                        k.mm((bank, bank[:, ci * 128:(ci + 1) * 128]), (F_up, F_up[:, kc, part * DFF + c * 128:part * DFF + (c + 1) * 128], g),
                             (hT3, hT3[:, kc, :]), start=(kc == 0), stop=(kc == 7))
                        if g_hi != g and g_hi in F_up.subs and F_up.subs[g_hi].w is not None:
                            k.streams["pe"][-1].deps.add(F_up.subs[g_hi].w)
            ex = ex4[p]
            if not smp:
                k.copy("pool", (ex, ex[:, 0:n, 0:2]), (cf, cf[:, c0:c0 + n, :]))
                k.copy("act", (ex, ex[:, 0:n, 2:130]), (gbank[p], gbank[p][:, 0:n * 128].rearrange("p (c t) -> p c t", c=n)))
                k.copy("pool", (cf, cf[:, c0:c0 + n, :]), (ex, ex[:, 0:n, 128:130]))
            else:
                exs = ex[:, 0:n, :].rearrange("p c (b t) -> p c b t", t=ST + 2)
                k.copy("pool", (ex, exs[:, :, :, 0:2]), (cfs, cfs[:, c0:c0 + n, :].rearrange("p c (b j) -> p c b j", j=2)))
                k.copy("act", (ex, exs[:, :, :, 2:ST + 2]), (gbank[p], gbank[p][:, 0:n * 128].rearrange("p (c b t) -> p c b t", c=n, b=NSB)))
            k.copy("act", (up4[p], up4[p][:, 0:n, :]), (ubank[p], ubank[p][:, 0:n * 128].rearrange("p (c t) -> p c t", c=n)))
            for ci, c in enumerate(chunks):
                if not smp:
                    tap = lambda j: ex[:, ci, j:j + 128]
                    ccv = cc4[p][:, ci, :]
                else:
                    e3 = ex[:, ci, :].rearrange("p (b t) -> p b t", t=ST + 2)
                    tap = lambda j, e3=e3: e3[:, :, j:j + ST]
                    ccv = cc4[p][:, ci, :].rearrange("p (b t) -> p b t", t=ST)
                cb = cc4[p]
                k.ts("dve", (cb, ccv), (ex, tap(2)), fcol(2 * NFC + c), fcol(3 * NFC + c), op0=ALU.mult, op1=ALU.add)
                k.stt("dve", (cb, ccv), (ex, tap(1)), fcol(1 * NFC + c), (cb, ccv), ALU.mult, ALU.add)
                k.stt("dve", (cb, ccv), (ex, tap(0)), fcol(0 * NFC + c), (cb, ccv), ALU.mult, ALU.add)

        def stage_B(ti, gi):
            p = gi % 2
            chunks = groups[gi]
            n = len(chunks)
            c0 = chunks[0]
            cb = (cc4[p], cc4[p][:, 0:n, :])
            ta = (t14[p], t14[p][:, 0:n, :])
            k.tt("dve", ta, cb, cb, ALU.mult)
            k.ts("dve", ta, ta, 0.044715, 1.0, op0=ALU.mult, op1=ALU.add)
            k.tt("dve", ta, ta, cb, ALU.mult)
            k.act(ta, ta, AF.Sigmoid, scale=GC)
            k.tt("dve", ta, ta, cb, ALU.mult)
            k.tt("dve", (gT, gT[:, c0:c0 + n, :], ("g", gi)), ta, (up4[p], up4[p][:, 0:n, :]), ALU.mult)

        def load_norm2(ti):
            xb = xq[ti % 2]
            k.dma("sp", xb[:], x1s[ti * 128:(ti + 1) * 128, :], reads=[(x1s, ti)], writes=[xb])
            if ti == NTP:
                for b6, (c0, n) in enumerate(BLK6):
                    k.dma("sp", utm[0:2 * NSB, 0:n], st_fconv[:, c0:c0 + n], reads=[st_fconv], writes=[utm])
                    nch = n // 128
                    for ci in range(nch):
                        k.tr((pR[b6 % 2], pR[b6 % 2][:, ci * 32:(ci + 1) * 32]), (utm, utm[0:2 * NSB, ci * 128:(ci + 1) * 128]),
                             (identf, identf[0:2 * NSB, 0:2 * NSB]))
                    k.copy("act", (cfs, cfs[:, 4 * b6:4 * b6 + nch, :]), (pR[b6 % 2], pR[b6 % 2][:, 0:nch * 32].rearrange("p (c x) -> p c x", c=nch)))
            norm_generic(xb, g2bc, hb3, hT3s[ti % 2], ss3, rs3)

        hb3b = hb3

        def groups_gen(ti):
            hT3 = hT3s[ti % 2]
            for gi in range(len(groups) + 1):
                if gi < len(groups):
                    stage_A(ti, gi, hT3)
                    yield
                if gi >= 1:
                    stage_B(ti, gi - 1)
                    yield

        def tail_gen(ti):
            smp = ti == NTP
            xb = xq[ti % 2]
            hT3 = hT3s[ti % 2]
            for blk in range(2):
                for c in range(NFC):
                    k.mm((pR[blk], pR[blk][:, :]), (gT, gT[:, c, :], ("g", c // GS)), (F_dn, F_dn[:, c, blk * 512:(blk + 1) * 512], c // 11),
                         start=(c == 0), stop=(c == NFC - 1))
                k.tt("dve", (xb, xb[:, blk * 512:(blk + 1) * 512]), (xb, xb[:, blk * 512:(blk + 1) * 512]), (pR[blk], pR[blk][:, :]), ALU.add)
                yield
            if ti == NTP - 1 or smp:
                for b6, (c0, n) in enumerate(BLK6):
                    for kc in range(8):
                        k.mm((pM[2], pM[2][:, 0:n]), (hT3, hT3[:, kc, :]), (F_up, F_up[:, kc, c0:c0 + n]),
                             start=(kc == 0), stop=(kc == 7))
                    if not smp:
                        k.copy("act", (utm, utm[96:128, 0:n]), (pM[2], pM[2][96:128, 0:n]))
                        k.dma("sp", o_pfconv[:, c0:c0 + n], utm[126:128, 0:n], reads=[utm], writes=[o_pfconv])
                    else:
                        k.copy("act", (utm, utm[:, 0:n]), (pM[2], pM[2][:, 0:n]))
                        for b in range(NSB):
                            k.dma("sp", o_sfconv[2 * b:2 * b + 2, c0:c0 + n], utm[ST * b + ST - 2:ST * b + ST, 0:n], reads=[utm], writes=[o_sfconv])
                    yield
            norm_generic(xb, g3bc, hb3b, hT3, ss3, rs3)
            yield
            pd, pap = (pp, pp[ti * 128:(ti + 1) * 128, :]) if not smp else (psm, psm[:, :])
            k.dma("sp", ppt[:], pap, reads=[pd], writes=[ppt])
            k.copy("act", (ppb, ppb[:]), (ppt, ppt[:]))
            for kc in range(2):
                k.tr((pT, pT[:, kc * 128:(kc + 1) * 128]), (ppb, ppb[:, kc * 128:(kc + 1) * 128]), (identb, identb[:]))
            k.copy("act", (peT, peT[:].rearrange("p a b -> p (a b)")), (pT, pT[:, 0:256]))
            yield
            for blk in range(2):
                for kc in range(8):
                    k.mm((pR[blk], pR[blk][:, :]), (hT3, hT3[:, kc, :]), (PGW, PGW[:, kc, blk * 512:(blk + 1) * 512]),
                         start=(kc == 0), stop=(kc == 7))
                yield
                k.act((sg3[blk], sg3[blk][:]), (pR[blk], pR[blk][:, :]), AF.Sigmoid)
                for kc in range(2):
                    k.mm((pM[2], pM[2][:, :]), (peT, peT[:, kc, :]), (PPJ, PPJ[:, kc, blk * 512:(blk + 1) * 512]),
                         start=(kc == 0), stop=(kc == 1))
                k.tt("dve", (sg3[blk], sg3[blk][:]), (sg3[blk], sg3[blk][:]), (pM[2], pM[2][:, :]), ALU.mult)
                k.tt("pool", (xb, xb[:, blk * 512:(blk + 1) * 512]), (xb, xb[:, blk * 512:(blk + 1) * 512]), (sg3[blk], sg3[blk][:]), ALU.add)
                yield
            k.act((hb3b, hb3b[:]), (xb, xb[:]), AF.Square, accum=(ss3, ss3[:]))
            k.ts("dve", (rs3, rs3[:]), (ss3, ss3[:]), 1.0 / D, EPS, op0=ALU.mult, op1=ALU.add)
            k.act((rs3, rs3[:]), (rs3, rs3[:]), AF.Ln)
            k.act((rs3, rs3[:]), (rs3, rs3[:]), AF.Exp, scale=-0.5)
            yield
            k.stt("dve", (xb, xb[:]), (xb, xb[:]), (rs3, rs3[:, 0:1]), (g4bc, g4bc[:]), ALU.mult, ALU.mult)
            if not smp:
                k.dma("sp", y_p[ti * 128:(ti + 1) * 128, :], xb[:], reads=[xb], writes=[y_p])
            else:
                k.dma("sp", y_s[:, :], xb[:], reads=[xb], writes=[y_s])
            if ti in nxt2:
                yield
                load_norm2(nxt2[ti])

        def run_rr(gens):
            gens = list(gens)
            while gens:
                for g_ in list(gens):
                    try:
                        next(g_)
                    except StopIteration:
                        gens.remove(g_)

        tl = list(tiles_all)
        nxt2 = {tl[i]: tl[i + 2] for i in range(len(tl) - 2)}
        load_norm2(tl[0])
        if len(tl) > 1:
            load_norm2(tl[1])
        run_rr([groups_gen(tl[0])])
        for idx, ti in enumerate(tl):
            tg = tail_gen(ti)
            if idx + 1 < len(tl):
                gg = groups_gen(tl[idx + 1])
                next(gg)
                next(gg)
                next(tg)
                next(tg)
                run_rr([gg, tg])
            else:
                run_rr([tg])

    _nt_dbg = int(_os.environ.get("KDBG_NT", "0"))
    tiles_run = (tiles_all if not _nt_dbg else list(range(_nt_dbg)))
    for ti in tiles_run:
        mixer_tile(ti)
        if stage >= 2:
            rwkv_tile(ti)
            tail_rows(ti)

    if stage >= 3:
        phase_1b(k)
    if stage >= 4:
        phase_2(k)
    k.emit()
    k.stats["sbuf_hiwater"] = k.hiwater
    k.stats["arena_bytes"] = k.arena_bytes
    return nc, k


def _chunk_rows(w, nk):
    return np.ascontiguousarray(w.reshape(nk, 128, w.shape[1]).transpose(1, 0, 2))


def _pcols(v, nc_):
    return v.reshape(nc_, 128).T


_PROG = {}


def _get_prog(stage=99, dbg=False):
    key = (stage, dbg)
    if key not in _PROG:
        _PROG[key] = build_program(stage, dbg)
    return _PROG[key]


def make_in_maps(inp):
    f = lambda a: np.ascontiguousarray(np.asarray(a, dtype=np.float32))
    ptab = np.zeros((128, 128), np.float32)
    mcw = f(inp["m_conv_w"])[0]
    for j in range(4):
        ptab[:, j * 8:(j + 1) * 8] = _pcols(mcw[j], 8)
    ptab[:, 32:40] = _pcols(f(inp["m_conv_b"])[0], 8)
    ptab[:, 40:54] = _pcols(f(inp["r_mix"])[0], 14)
    ptab[:, 54:58] = _pcols(f(inp["r_w0"])[0], 4)
    ptab[:, 58:62] = _pcols(f(inp["r_a0"])[0], 4)
    ptab[:, 62:66] = _pcols(f(inp["r_kk"])[0], 4)
    ptab[:, 66:70] = _pcols(f(inp["r_ka"])[0], 4)
    ptab[:, 70:74] = _pcols(f(inp["r_rk"])[0].reshape(-1), 4)
    ptab[:, 74:78] = _pcols(f(inp["r_ln_g"])[0], 4)
    ptab[:, 78:82] = _pcols(f(inp["r_ln_b"])[0], 4)
    ptab[:, 82:86] = _pcols(f(inp["m_norm_g"])[0], 4)
    fct = np.zeros((128, 4 * NFC), np.float32)
    fcw = f(inp["f_conv_w"])[0]
    for j in range(3):
        fct[:, j * NFC:(j + 1) * NFC] = _pcols(fcw[j], NFC)
    fct[:, 3 * NFC:4 * NFC] = _pcols(f(inp["f_conv_b"])[0], NFC)
    gbias = np.stack([f(inp["m_i_bias"])[0], f(inp["m_f_bias"])[0]], axis=1)
    ra2 = np.zeros((128, RW), np.float32)
    ra2[64:128] = f(inp["r_a2"])[0]
    bc = lambda v: np.ascontiguousarray(np.broadcast_to(f(v).reshape(1, D), (128, D)))
    shared = {
        "w_in": _chunk_rows(f(inp["w_in"])[0], 8),
        "w_bm": _chunk_rows(f(inp["w_branch_m"])[0], 4),
        "w_br": _chunk_rows(f(inp["w_branch_r"])[0], 4),
        "w_out": _chunk_rows(f(inp["w_out"])[0], 8),
        "f_up": _chunk_rows(f(inp["f_up"])[0], 8),
        "f_down": _chunk_rows(f(inp["f_down"])[0], NFC),
        "ple_gate_w": _chunk_rows(f(inp["ple_gate_w"])[0], 8),
        "ple_proj": _chunk_rows(f(inp["ple_proj"])[0], 2),
        "r_w2": f(inp["r_w2"])[0], "r_a2": ra2, "r_g2": f(inp["r_g2"])[0],
        "norm1_g": bc(inp["norm1_g"]), "norm2_g": bc(inp["norm2_g"]),
        "ple_norm_g": bc(inp["ple_norm_g"]), "final_norm_g": bc(inp["final_norm_g"]),
        "ptab": ptab, "fctab": fct, "gate_bias": np.ascontiguousarray(gbias),
    }
    maps = []
    for c in range(NCORES):
        sl = slice(c * NSB, (c + 1) * NSB)
        m = dict(shared)
        m["xp"] = f(inp["x_prompt"][c])
        m["xs"] = f(inp["x_sample"][sl]).reshape(128, D)
        m["pp"] = f(inp["p_prompt"][0, c])
        m["psm"] = f(inp["p_sample"][0, sl]).reshape(128, PLE)
        m["st_mconv"] = f(inp["state_mlstm_conv"][0, sl]).reshape(NSB * 3, 2 * MW)
        m["st_mC"] = f(inp["state_mlstm_C"][0, sl])
        m["st_mn"] = f(inp["state_mlstm_n"][0, sl])
        m["st_mm"] = f(inp["state_mlstm_m"][0, sl])
        m["st_rshift"] = f(inp["state_rwkv_shift"][0, sl])
        m["st_rS"] = f(inp["state_rwkv_S"][0, sl]).reshape(NSB * RH, RN * RN)
        m["st_fconv"] = f(inp["state_ffn_conv"][0, sl]).reshape(NSB * 2, DFF)
        maps.append(m)
    return maps


def assemble(results):
    g = lambda name: [np.asarray(r[name], dtype=np.float32) for r in results]
    y_p = np.stack(g("y_p"), 0)
    y_s = np.concatenate([a.reshape(NSB, ST, D) for a in g("y_s")], 0)
    p_conv = np.stack(g("p_conv"), 0)[None]
    p_C = np.stack(g("p_C"), 0)[None]
    p_n = np.stack(g("p_n"), 0)[None]
    p_m = np.stack([a.reshape(MH) for a in g("p_m")], 0)[None]
    p_shift = np.stack([a.reshape(RCOLS) for a in g("p_shift")], 0)[None]
    p_S = np.stack([a.reshape(RH, RN, RN) for a in g("p_S")], 0)[None]
    p_fconv = np.stack(g("p_fconv"), 0)[None]
    s_conv = np.concatenate([a.reshape(NSB, 3, 2 * MW) for a in g("s_conv")], 0)[None]
    s_C = np.concatenate(g("s_C"), 0)[None]
    s_n = np.concatenate(g("s_n"), 0)[None]
    s_m = np.concatenate(g("s_m"), 0)[None]
    s_shift = np.concatenate(g("s_shift"), 0)[None]
    s_S = np.concatenate([a.reshape(NSB, RH, RN, RN) for a in g("s_S")], 0)[None]
    s_fconv = np.concatenate([a.reshape(NSB, 2, DFF) for a in g("s_fconv")], 0)[None]
    return (y_p, y_s, p_conv, p_C, p_n, p_m, p_shift, p_S, p_fconv,
            s_conv, s_C, s_n, s_m, s_shift, s_S, s_fconv)


def kernel(**inputs):
    nc, _ = _get_prog()
    maps = make_in_maps(inputs)
    res = run_bass_kernel_spmd(nc, maps, core_ids=list(range(NCORES)))
    return assemble(res.results)
```

```python
import math
from contextlib import ExitStack

import numpy as np
import concourse.bass as bass
import concourse.mybir as mybir
from concourse.bass_utils import run_bass_kernel_spmd

F32 = mybir.dt.float32
BF16 = mybir.dt.bfloat16
AF = mybir.ActivationFunctionType
ALU = mybir.AluOpType
AX = mybir.AxisListType

ENGS = ("pe", "act", "dve", "pool", "sp")
N_DMA_SEMS = 8
SAME_ENG_DIST = 2

D = 1024
SEQ = 2048
NCORES = 8
NTP = SEQ // 128
NSB = 16
ST = 8
MW = 512
MH = 4
RW = 512
RH = 8
RN = 64
RCOLS = 1792
DFF = 2816
NFC = DFF // 128
PLE = 256
N_IN = 5896
C_QK, C_V, C_O, C_I, C_F, C_R, C_G = 0, 1024, 1536, 2048, 2052, 2056, 3848
EPS = 1e-6
GN_EPS = 64e-5
KSCALE = 128 ** -0.5
WSCALE = -math.exp(-0.5)


class _Trk:
    __slots__ = ("w", "r")

    def __init__(self):
        self.w = None
        self.r = []


class Buf:
    def __init__(self, t, name):
        self.t = t
        self.name = name
        self.whole = _Trk()
        self.subs = {}

    def __getitem__(self, idx):
        return self.t[idx]

    def view(self, ap, name=None):
        b = Buf(ap, name or self.name + "_v")
        b.whole = self.whole
        b.subs = self.subs
        return b


class _Op:
    __slots__ = ("eng", "fn", "deps", "needs_inc", "is_dma", "sem", "val", "pos", "force")


class K:
    def __init__(self, nc):
        self.nc = nc
        self.es = ExitStack()
        self.streams = {e: [] for e in ENGS}
        self.dma_rr = {e: 0 for e in ENGS}
        self.dma_last = {}
        self.nbuf = 0
        self.ops = []

    def _init_arena(self):
        nbytes = (int(self.nc.sbuf_bytes_remaining) - 512) // 64 * 64
        self.arena_bytes = nbytes
        self.arena = self.es.enter_context(self.nc.sbuf_tensor("arena", [128, nbytes // 2], BF16))
        self.bot = 0
        self.top = nbytes
        self.hiwater = 0

    def _view(self, off, shape, dtype):
        n = 1
        for d in shape[1:]:
            n *= d
        esz = 4 if dtype == F32 else 2
        v = self.arena[:, off // 2:(off + n * esz) // 2]
        if dtype == F32:
            v = v.bitcast(F32)
        if len(shape) > 2:
            names = " ".join(f"d{i}" for i in range(len(shape) - 1))
            v = v.rearrange(f"p ({names}) -> p {names}", **{f"d{i}": shape[i + 1] for i in range(len(shape) - 1)})
        if shape[0] < 128:
            v = v[0:shape[0]]
        return v, n * esz

    def sbuf(self, shape, dtype, name=None, top=False):
        if not hasattr(self, "arena"):
            self._init_arena()
        self.nbuf += 1
        name = name or f"sb{self.nbuf}"
        n = 1
        for d in shape[1:]:
            n *= d
        nb = (n * (4 if dtype == F32 else 2) + 63) // 64 * 64
        if top:
            self.top -= nb
            off = self.top
        else:
            off = self.bot
            self.bot += nb
        assert self.bot <= self.top, f"SBUF arena overflow allocating {name}: bot={self.bot} top={self.top}"
        self.hiwater = max(self.hiwater, self.bot + (self.arena_bytes - self.top))
        v, _ = self._view(off, list(shape), dtype)
        return Buf(v, name)

    def pe_fence(self):
        st = self.streams["pe"]
        if not st:
            return
        last = st[-1]
        o = self.op("pe", lambda h: h.nop(), (), ())
        o.deps.add(last)
        o.force = {last}
        if getattr(self, "fence_mm", None) is not None:
            fb, fi = self.fence_mm
            self.tr((fb, fb[:, 0:128]), (fi, fi[:]), (fi, fi[:]))
            last = self.streams["pe"][-1]
            o = self.op("pe", lambda h: h.nop(), (), ())
            o.deps.add(last)
            o.force = {last}

    def barrier(self):
        lasts = [st[-1] for st in self.streams.values() if st]
        lasts += list(self.dma_last.values())
        for e in ENGS:
            o = self.op(e, lambda h: h.nop(), (), ())
            o.deps.update(x for x in lasts if x is not o)

    def psum(self, shape, dtype, name=None):
        self.nbuf += 1
        name = "ps_" + (name or f"{self.nbuf}")
        t = self.es.enter_context(self.nc.psum_tensor(name, list(shape), dtype))
        return Buf(t, name)

    def dram(self, name, shape, dtype, kind="Internal"):
        t = self.nc.dram_tensor(name, list(shape), dtype, kind=kind)
        return Buf(t.ap(), name)

    def _touch(self, op, item, is_write):
        if isinstance(item, tuple):
            buf, key = item
        else:
            buf, key = item, None
        if key is None:
            trks = [buf.whole] + list(buf.subs.values())
        else:
            if key not in buf.subs:
                buf.subs[key] = _Trk()
            trks = [buf.whole, buf.subs[key]]
        for t in trks:
            if t.w is not None:
                op.deps.add(t.w)
            if is_write:
                op.deps.update(t.r)
        return buf, key

    def _commit(self, op, buf, key, is_write):
        if key is None:
            if is_write:
                buf.whole.w = op
                buf.whole.r = []
                buf.subs.clear()
            else:
                self._add_reader(buf.whole, op)
        else:
            t = buf.subs[key]
            if is_write:
                t.w = op
                t.r = []
            else:
                self._add_reader(t, op)

    @staticmethod
    def _add_reader(t, op):
        if not op.is_dma:
            t.r = [o for o in t.r if o.is_dma or o.eng != op.eng]
        t.r.append(op)

    def op(self, eng, fn, reads=(), writes=(), dma=False):
        o = _Op()
        o.eng = eng
        o.fn = fn
        o.deps = set()
        o.needs_inc = False
        o.is_dma = dma
        o.sem = None
        o.val = None
        o.force = None
        touched = []
        for it in reads:
            touched.append(self._touch(o, it, False) + (False,))
        for it in writes:
            touched.append(self._touch(o, it, True) + (True,))
        o.deps.discard(o)
        for buf, key, w in touched:
            self._commit(o, buf, key, w)
        if dma:
            kk = (eng, self.dma_rr[eng] % N_DMA_SEMS)
            self.dma_rr[eng] += 1
            prev = self.dma_last.get(kk)
            if prev is not None:
                o.deps.add(prev)
            self.dma_last[kk] = o
            o.sem = kk
            o.needs_inc = True
        o.pos = len(self.streams[eng])
        self.streams[eng].append(o)
        self.ops.append(o)
        return o

    def dma(self, eng, out, in_, reads=(), writes=(), **kw):
        return self.op(eng, lambda e: e.dma_start(out=out, in_=in_, **kw), reads, writes, dma=True)

    def emit(self):
        nc = self.nc
        for o in self.ops:
            real = []
            for d in o.deps:
                if (not d.is_dma) and (not o.is_dma) and d.eng == o.eng and o.eng == "pe":
                    if not (o.force and d in o.force):
                        continue
                d.needs_inc = True
                real.append(d)
            o.deps = real
        for e in ENGS:
            cs = [o for o in self.streams[e] if not o.is_dma]
            if cs:
                cs[-1].needs_inc = True
        es = self.es
        esem = {e: es.enter_context(nc.semaphore(f"s_{e}")) for e in ENGS}
        dsem = {}
        for e in ENGS:
            for i in range(min(N_DMA_SEMS, self.dma_rr[e])):
                dsem[(e, i)] = es.enter_context(nc.semaphore(f"d_{e}{i}"))
        dcount = {kk: 0 for kk in dsem}
        for e in ENGS:
            c = 0
            for o in self.streams[e]:
                if o.is_dma:
                    dcount[o.sem] += 16
                    o.val = dcount[o.sem]
                    o.sem = dsem[o.sem]
                elif o.needs_inc:
                    c += 1
                    o.val = c
                    o.sem = esem[e]
        final_waits = [(s, dcount[kk]) for kk, s in dsem.items() if dcount[kk] > 0]
        for e in ENGS:
            if e == "sp":
                continue
            cs = [o for o in self.streams[e] if not o.is_dma and o.needs_inc]
            if cs:
                final_waits.append((esem[e], cs[-1].val))
        streams = self.streams
        nwaits = [0]

        def run(e, handle):
            waited = {}
            for o in streams[e]:
                need = {}
                for d in o.deps:
                    if need.get(d.sem, (None, 0))[1] < d.val:
                        need[d.sem] = (d.sem, d.val)
                for s, v in need.values():
                    if waited.get(s, 0) >= v:
                        continue
                    handle.wait_ge(s, v)
                    nwaits[0] += 1
                    waited[s] = v
                ins = o.fn(handle)
                if o.is_dma:
                    ins.then_inc(o.sem, 16)
                elif o.needs_inc:
                    ins.then_inc(o.sem, 1)
            if e == "sp":
                for s, v in final_waits:
                    handle.wait_ge(s, v)

        with nc.Block() as block:
            @block.tensor
            def _(h):
                run("pe", h)

            @block.scalar
            def _(h):
                run("act", h)

            @block.vector
            def _(h):
                run("dve", h)

            @block.gpsimd
            def _(h):
                run("pool", h)

            @block.sync
            def _(h):
                run("sp", h)
        self.stats = dict(n_ops={e: len(streams[e]) for e in ENGS}, n_waits=nwaits[0])
        self.es.close()

    @staticmethod
    def _it(x):
        return (x[0], x[2]) if len(x) > 2 else x[0]

    def mm(self, out, lhsT, rhs, start=True, stop=True):
        return self.op("pe", lambda e: e.matmul(out[1], lhsT=lhsT[1], rhs=rhs[1], start=start, stop=stop),
                       reads=[self._it(lhsT), self._it(rhs)], writes=[self._it(out)])

    def tr(self, out, in_, ident):
        return self.op("pe", lambda e: e.transpose(out[1], in_[1], ident[1]),
                       reads=[self._it(in_), self._it(ident)], writes=[self._it(out)])

    def act(self, out, in_, func, bias=None, scale=None, accum=None, eng="act"):
        reads = [self._it(in_)]
        kw = {}
        if bias is not None:
            if isinstance(bias, tuple):
                reads.append(self._it(bias))
                kw["bias"] = bias[1]
            else:
                kw["bias"] = bias
        if scale is not None:
            if isinstance(scale, tuple):
                reads.append(self._it(scale))
                kw["scale"] = scale[1]
            else:
                kw["scale"] = scale
        writes = [self._it(out)]
        if accum is not None:
            writes.append(self._it(accum))
            kw["accum_out"] = accum[1]
        return self.op(eng, lambda e: e.activation(out=out[1], in_=in_[1], func=func, **kw), reads, writes)

    def tt(self, eng, out, in0, in1, op):
        return self.op(eng, lambda e: e.tensor_tensor(out=out[1], in0=in0[1], in1=in1[1], op=op),
                       reads=[self._it(in0), self._it(in1)], writes=[self._it(out)])

    def ts(self, eng, out, in0, s1, s2=None, op0=ALU.mult, op1=None, accum=None):
        reads = [self._it(in0)]
        a1 = s1
        a2 = s2
        if isinstance(s1, tuple):
            reads.append(self._it(s1))
            a1 = s1[1]
        if isinstance(s2, tuple):
            reads.append(self._it(s2))
            a2 = s2[1]
        kw = {}
        if op1 is not None:
            kw["op1"] = op1
        writes = [self._it(out)]
        if accum is not None:
            writes.append(self._it(accum))
            kw["accum_out"] = accum[1]
        return self.op(eng, lambda e: e.tensor_scalar(out=out[1], in0=in0[1], scalar1=a1, scalar2=a2, op0=op0, **kw),
                       reads, writes)

    def stt(self, eng, out, in0, scalar, in1, op0, op1):
        reads = [self._it(in0), self._it(in1)]
        a = scalar
        if isinstance(scalar, tuple):
            reads.append(self._it(scalar))
            a = scalar[1]
        return self.op(eng, lambda e: e.scalar_tensor_tensor(out=out[1], in0=in0[1], scalar=a, in1=in1[1], op0=op0, op1=op1),
                       reads, [self._it(out)])

    def copy(self, eng, out, in_):
        if eng == "act":
            return self.op(eng, lambda e: e.activation(out=out[1], in_=in_[1], func=AF.Copy),
                           reads=[self._it(in_)], writes=[self._it(out)])
        return self.op(eng, lambda e: e.tensor_copy(out=out[1], in_=in_[1]),
                       reads=[self._it(in_)], writes=[self._it(out)])

    def red(self, eng, out, in_, op, axis=AX.X):
        return self.op(eng, lambda e: e.tensor_reduce(out=out[1], in_=in_[1], axis=axis, op=op),
                       reads=[self._it(in_)], writes=[self._it(out)])

    def memset(self, eng, out, val):
        return self.op(eng, lambda e: e.memset(out[1], val), reads=[], writes=[self._it(out)])

    def scan(self, eng, out, d0, d1, init, op0, op1):
        return self.op(eng, lambda e: e.tensor_tensor_scan(out=out[1], data0=d0[1], data1=d1[1], initial=init, op0=op0, op1=op1),
                       reads=[self._it(d0), self._it(d1)], writes=[self._it(out)])


def build_program(stage=99, dbg=False):
    import os as _os
    nc = bass.Bass("TRN2", target_bir_lowering=False)
    k = K(nc)
    NT = NTP + 1

    def din(name, shape):
        return k.dram(name, shape, F32, "ExternalInput")

    def dout(name, shape):
        return k.dram(name, shape, F32, "ExternalOutput")

    xp = din("xp", [SEQ, D]); xs = din("xs", [128, D])
    pp = din("pp", [SEQ, PLE]); psm = din("psm", [128, PLE])
    st_mconv = din("st_mconv", [NSB * 3, 2 * MW])
    st_mC = din("st_mC", [NSB, MH, 128, 128])
    st_mn = din("st_mn", [NSB, MH, 128])
    st_mm = din("st_mm", [NSB, MH])
    st_rshift = din("st_rshift", [NSB, RCOLS])
    st_rS = din("st_rS", [NSB * RH, RN * RN])
    st_fconv = din("st_fconv", [NSB * 2, DFF])
    d_w_in = din("w_in", [128, 8, N_IN])
    d_w_bm = din("w_bm", [128, 4, D]); d_w_br = din("w_br", [128, 4, D])
    d_w_out = din("w_out", [128, 8, D])
    d_f_up = din("f_up", [128, 8, 2 * DFF]); d_f_down = din("f_down", [128, NFC, D])
    d_pgw = din("ple_gate_w", [128, 8, D]); d_ppj = din("ple_proj", [128, 2, D])
    d_rw2 = din("r_w2", [64, RW]); d_ra2 = din("r_a2", [128, RW]); d_rg2 = din("r_g2", [128, RW])
    d_g1 = din("norm1_g", [128, D]); d_g2 = din("norm2_g", [128, D])
    d_g3 = din("ple_norm_g", [128, D]); d_g4 = din("final_norm_g", [128, D])
    d_ptab = din("ptab", [128, 128])
    d_fctab = din("fctab", [128, 4 * NFC])
    d_gb = din("gate_bias", [4, 2])
    y_p = dout("y_p", [SEQ, D]); y_s = dout("y_s", [128, D])
    o_pconv = dout("p_conv", [3, 2 * MW]); o_pC = dout("p_C", [MH, 128, 128]); o_pn = dout("p_n", [MH, 128])
    o_pm = dout("p_m", [1, MH]); o_pshift = dout("p_shift", [1, RCOLS]); o_pS = dout("p_S", [RH * RN, RN])
    o_pfconv = dout("p_fconv", [2, DFF])
    o_sconv = dout("s_conv", [NSB * 3, 2 * MW]); o_sC = dout("s_C", [NSB, MH, 128, 128]); o_sn = dout("s_n", [NSB, MH, 128])
    o_sm = dout("s_m", [NSB, MH]); o_sshift = dout("s_shift", [NSB, RCOLS]); o_sS = dout("s_S", [NSB * RH, RN * RN])
    o_sfconv = dout("s_fconv", [NSB * 2, DFF])
    x1s = k.dram("x1_scratch", [NT * 128, D], F32)
    dbgs = {}

    def dbg_out(name, src_buf, src_ap, shape):
        if not dbg:
            return
        t = dout("dbg_" + name, shape)
        dbgs[name] = t
        k.dma("sp", t[:], src_ap, reads=[src_buf], writes=[t])

    identf = k.sbuf([128, 128], F32, "identf")
    identb = k.sbuf([128, 128], BF16, "identb")
    mark_phase = k.bot
    mU_in = [k.sbuf([128, 128], F32, f"mUin{i}") for i in range(2)]
    mU_st = [k.sbuf([128, 128], F32, f"mUst{i}") for i in range(2)]
    mL_st = [k.sbuf([128, 128], F32, f"mLst{i}") for i in range(2)]
    resets = [k.sbuf([128, 512], F32, f"resets{i}") for i in range(2)]
    ones4 = k.sbuf([4, 128], F32, "ones4")

    def aff(out_buf, out_ap, pattern, cm, base, op=ALU.is_ge):
        k.op("pool", lambda e: e.affine_select(out=out_ap, in_=out_ap, pattern=pattern, compare_op=op,
                                               fill=0.0, base=base, channel_multiplier=cm),
             reads=[out_buf], writes=[out_buf])

    k.memset("pool", (identf, identf[:]), 1.0)
    aff(identf, identf[:], [[-1, 128]], 1, 0)
    aff(identf, identf[:], [[1, 128]], -1, 0)
    k.copy("pool", (identb, identb[:]), (identf, identf[:]))
    for i in range(2):
        k.memset("pool", (mU_in[i], mU_in[i][:]), 1.0)
        aff(mU_in[i], mU_in[i][:], [[1, 128]], -1, 0)
        k.memset("pool", (mU_st[i], mU_st[i][:]), 1.0)
        aff(mU_st[i], mU_st[i][:], [[1, 128]], -1, -1)
        k.memset("pool", (mL_st[i], mL_st[i][:]), 1.0)
        aff(mL_st[i], mL_st[i][:], [[-1, 128]], 1, -1)
        k.memset("pool", (resets[i], resets[i][:]), 1.0)
    v3 = lambda b: b[:].rearrange("p (a c) -> p a c", c=ST)
    aff(mU_in[1], v3(mU_in[1]), [[-ST, 16], [0, ST]], 1, 0)
    aff(mU_st[1], v3(mU_st[1]), [[-ST, 16], [0, ST]], 1, 0)
    aff(mL_st[1], v3(mL_st[1]), [[ST, 16], [0, ST]], -1, ST - 1)
    k.memset("pool", (resets[0], resets[0][:].rearrange("p (a c) -> p a c", c=128)[:, :, 0:1]), 0.0)
    k.memset("pool", (resets[1], resets[1][:].rearrange("p (a c) -> p a c", c=ST)[:, :, 0:1]), 0.0)
    k.memset("pool", (ones4, ones4[:]), 1.0)
    mask2 = [k.sbuf([128, 256], F32, f"mask2_{i}") for i in range(2)]
    for i in range(2):
        k.copy("pool", (mask2[i], mask2[i][:, 0:128]), (mU_st[i], mU_st[i][:]))
        k.copy("pool", (mask2[i], mask2[i][:, 128:256]), (mU_in[i], mU_in[i][:]))
    I2 = k.sbuf([128, 64], F32, "I2")
    k.tt("pool", (I2, I2[:]), (identf, identf[:, 0:64]), (identf, identf[:, 64:128]), ALU.add)
    bones = k.sbuf([128, 128], F32, "bones")
    k.memset("pool", (bones, bones[:]), 0.0)
    k.memset("pool", (bones, bones[0:64, 0:64]), 1.0)
    k.memset("pool", (bones, bones[64:128, 64:128]), 1.0)

    ptab = k.sbuf([128, 128], F32, "ptab")
    k.dma("sp", ptab[:], d_ptab[:], reads=[d_ptab], writes=[ptab])
    PT_MCW, PT_MCB, PT_RMIX, PT_RW0, PT_RA0, PT_RKK, PT_RKA, PT_RRK, PT_RLNG, PT_RLNB, PT_MNG = 0, 32, 40, 54, 58, 62, 66, 70, 74, 78, 82
    pcol = lambda c: (ptab, ptab[:, c:c + 1])
    gb = k.sbuf([4, 2], F32, "gb")
    k.dma("sp", gb[:], d_gb[:], reads=[d_gb], writes=[gb])
    nbf = k.sbuf([4, 1], F32, "nbf")
    k.ts("dve", (nbf, nbf[:]), (gb, gb[:, 1:2]), -1.0, None, op0=ALU.mult)
    g1bc = k.sbuf([128, D], F32, "g1bc")
    k.dma("sp", g1bc[:], d_g1[:], reads=[d_g1], writes=[g1bc])

    NA = C_G
    hmT_all = k.sbuf([128, NT, 4, 128], BF16, "hmT_all")
    yrgT_all = k.sbuf([128, NT, 4, 128], BF16, "yrgT_all")
    mark_1a = k.bot
    W_in = k.sbuf([128, 8, NA], BF16, "W_in")
    Wl_w2 = k.sbuf([64, RW], BF16, "Wl_w2")
    Wl_a2 = k.sbuf([128, RW], BF16, "Wl_a2")
    Wl_g2 = k.sbuf([128, RW], BF16, "Wl_g2")
    GRP = {"g0": (0, 1024), "g1": (1024, 2056), "g2": (2056, 3848)}
    for g in ("g0", "g1", "g2"):
        a, b = GRP[g]
        for kh in range(2):
            k.dma("pool", W_in[:, 4 * kh:4 * kh + 4, a:b], d_w_in[:, 4 * kh:4 * kh + 4, a:b], reads=[d_w_in], writes=[(W_in, g)])
        if g == "g1":
            k.dma("pool", Wl_w2[:], d_rw2[:], reads=[d_rw2], writes=[Wl_w2])
            k.dma("pool", Wl_a2[:], d_ra2[:], reads=[d_ra2], writes=[Wl_a2])
            k.dma("pool", Wl_g2[:], d_rg2[:], reads=[d_rg2], writes=[Wl_g2])

    def wgrp(col):
        for g, (a, b) in GRP.items():
            if a <= col < b:
                return g

    pF = [k.psum([128, 512], F32, f"pF{i}") for i in range(2)]
    pR = [k.psum([128, 512], F32, f"pR{i}") for i in range(2)]
    pT = k.psum([128, 1024], BF16, "pT")
    pM = [k.psum([128, 512], F32, f"pM{i}") for i in range(3)]

    xt = [k.sbuf([128, D], F32, "xt0")] * 2
    hb = k.sbuf([128, D], BF16, "hb")
    hT = k.sbuf([128, 8, 128], BF16, "hT")
    ss = k.sbuf([128, 1], F32, "ss")
    rs = k.sbuf([128, 1], F32, "rs")
    ext_q = k.sbuf([128, 8, 131], F32, "ext_q")
    cq = k.sbuf([128, 8, 3], F32, "cq")
    _eqf = ext_q[:].rearrange("p a b -> p (a b)")
    cv = k.sbuf([128, 8, 128], F32, "cv")
    qkT = k.sbuf([128, 8, 128], BF16, "qkT")
    soT = k.sbuf([128, 4, 128], F32, "soT")
    vaug = k.sbuf([128, 4, 130], BF16, "vaug")
    Cst = k.sbuf([128, 4, 129], F32, "Cst")
    Cb = k.sbuf([128, 4, 130], BF16, "Cb")
    gsm = [k.sbuf([4, 128], F32, f"gsm{i}") for i in range(8)]
    gpk = k.sbuf([4, 3, 128], F32, "gpk")
    mst = k.sbuf([4, 16], F32, "mst")
    mnew = k.sbuf([4, 16], F32, "mnew")
    gt = [k.sbuf([4, 16], F32, f"gt{i}") for i in range(4)]
    s0d = k.sbuf([4, 4, 16], F32, "s0d")
    tokS = k.sbuf([128, 12], F32, "tokS")
    s0bc = k.sbuf([128, 64], F32, "s0bc")
    _pk = _eqf[:, 512:1024].bitcast(BF16).rearrange("p (a b c) -> p a b c", a=2, b=4)
    PTm = ext_q.view(_pk[:, 0, :, :], "PTm")
    ktm = ext_q.view(_pk[:, 1, :, :], "ktm")
    dn = k.sbuf([128, 4], F32, "dn")
    hm = ext_q.view(_eqf[:, 0:512].rearrange("p (a b) -> p a b", a=4), "hm")
    hn = hm
    bst = k.sbuf([128, 4, 6], F32, "bst")
    bag = k.sbuf([128, 4, 2], F32, "bag")
    zq_tm = cv.view(cv[:].rearrange("p a b -> p (a b)"), "zq_tm")

    ext_r = k.sbuf([128, 14, 129], F32, "ext_r")
    cr = k.sbuf([128, 14, 1], F32, "cr")
    _erf = ext_r[:].rearrange("p a b -> p (a b)")
    xm = k.sbuf([128, 14, 128], F32, "xm")
    thad = k.sbuf([128, 128], BF16, "thad")
    sgd = k.sbuf([128, 128], BF16, "sgd")
    bst8 = k.sbuf([128, 8, 6], F32, "bst8")
    bag8 = k.sbuf([128, 8, 2], F32, "bag8")
    mark_rw = k.bot
    rt = [k.sbuf([128, 4, 128], F32, f"rt{i}") for i in range(7)]
    rt.append(cv.view(cv[:, 0:4, :], "rt7"))
    rt.append(cv.view(cv[:, 4:8, :], "rt8"))
    gTs = ext_r.view(_erf[:, 0:512].rearrange("p (a b) -> p a b", a=4), "gTs")
    bonT = ext_r.view(_erf[:, 512:1024].rearrange("p (a b) -> p a b", a=4), "bonT")
    ART = k.sbuf([128, 4, 2, 128], BF16, "ART")
    BTb = k.sbuf([128, 4, 128], BF16, "BTb")
    KTb = k.sbuf([128, 4, 128], BF16, "KTb")
    VTb = k.sbuf([128, 4, 128], BF16, "VTb")
    AB_tm = k.sbuf([128, 2, 512], BF16, "AB_tm")
    KV_tm = k.sbuf([128, 2, 512], BF16, "KV_tm")
    GBm = k.sbuf([128, 4, 256], BF16, "GBm")
    GKm = k.sbuf([128, 4, 256], BF16, "GKm")
    Nn = k.sbuf([128, 4, 128], BF16, "Nn")
    GBm_b = k.sbuf([128, 4, 256], BF16, "GBm_b")
    GKm_b = k.sbuf([128, 4, 256], BF16, "GKm_b")
    Nn_b = k.sbuf([128, 4, 128], BF16, "Nn_b")
    PP_b = [k.sbuf([128, 4, 256], BF16, f"PPb{i}") for i in range(2)]
    XX_b = [k.sbuf([128, 4, 128], BF16, f"XXb{i}") for i in range(2)]
    PP = [k.sbuf([128, 4, 256], BF16, f"PP{i}") for i in range(2)]
    XX = [k.sbuf([128, 4, 128], BF16, f"XX{i}") for i in range(2)]
    QT = k.sbuf([128, 2, 128], BF16, "QT")
    IE = k.sbuf([128, 4, 64], F32, "IE")
    STf = k.sbuf([128, 4, 64], F32, "STf")
    STb = k.sbuf([128, 4, 64], BF16, "STb")
    yn = ext_r.view(_erf[:, 1024:1536].rearrange("p (a b) -> p a b", a=8), "yn")
    k.memset("pool", (STf, STf[:]), 0.0)
    k.memset("pool", (STb, STb[:]), 0.0)
    k.memset("pool", (cr, cr[:]), 0.0)
    k.memset("pool", (vaug, vaug[:]), 1.0)
    k.memset("pool", (Cst, Cst[:]), 0.0)
    k.memset("pool", (Cb, Cb[:]), 0.0)
    k.memset("pool", (mst, mst[:]), 0.0)
    k.memset("pool", (cq, cq[:]), 0.0)
    LNK = math.log(KSCALE)

    def x_rows(ti):
        if ti < NTP:
            return xp, xp[ti * 128:(ti + 1) * 128, :]
        return xs, xs[:, :]

    def norm_to_hT(xbuf, gbc):
        k.act((hb, hb[:]), (xbuf, xbuf[:]), AF.Square, accum=(ss, ss[:]))
        k.ts("dve", (rs, rs[:]), (ss, ss[:]), 1.0 / D, EPS, op0=ALU.mult, op1=ALU.add)
        k.act((rs, rs[:]), (rs, rs[:]), AF.Ln)
        k.act((rs, rs[:]), (rs, rs[:]), AF.Exp, scale=-0.5)
        k.stt("dve", (hb, hb[:]), (xbuf, xbuf[:]), (rs, rs[:, 0:1]), (gbc, gbc[:]), ALU.mult, ALU.mult)
        for kc in range(8):
            k.tr((pT, pT[:, kc * 128:(kc + 1) * 128]), (hb, hb[:, kc * 128:(kc + 1) * 128]), (identb, identb[:]))
        k.copy("act", (hT, hT[:].rearrange("p a b -> p (a b)")), (pT, pT[:, :]))

    def proj_fm(ps, ps_ap, col, M=128):
        g = wgrp(col)
        for kc in range(8):
            k.mm((ps, ps_ap), (W_in, W_in[:, kc, col:col + M], g), (hT, hT[:, kc, :]), start=(kc == 0), stop=(kc == 7))

    def proj_tm(ps, ps_ap, col, N):
        g = wgrp(col)
        for kc in range(8):
            k.mm((ps, ps_ap), (hT, hT[:, kc, :]), (W_in, W_in[:, kc, col:col + N], g), start=(kc == 0), stop=(kc == 7))

    prefetched = set()

    def prefetch_gen(tn):
        xb = xt[tn % 2]
        xd, xap = x_rows(tn)
        k.dma("sp", xb[:], xap, reads=[xd], writes=[xb])
        norm_to_hT(xb, g1bc)
        yield
        k.copy("pool", (ext_q, ext_q[:, :, 0:3]), (cq, cq[:]))
        for g in range(2):
            for c in range(4):
                proj_fm(pM[2], pM[2][:, c * 128:(c + 1) * 128], C_QK + (4 * g + c) * 128)
                if c == 1:
                    yield
            k.copy("act", (ext_q, ext_q[:, 4 * g:4 * g + 4, 3:131]), (pM[2], pM[2][:].rearrange("p (c t) -> p c t", c=4)))
            yield
        k.copy("pool", (cq, cq[:]), (ext_q, ext_q[:, :, 128:131]))
        yield
        proj_tm(pM[2], pM[2][:, :], C_V, 512)
        k.copy("act", (vaug, vaug[:, :, 0:128]), (pM[2], pM[2][:].rearrange("p (h c) -> p h c", h=4)))
        yield
        for c in range(4):
            proj_fm(pM[2], pM[2][:, c * 128:(c + 1) * 128], C_O + c * 128)
            if c == 1:
                yield
        k.act((soT, soT[:].rearrange("p a b -> p (a b)")), (pM[2], pM[2][:, :]), AF.Sigmoid)

    def mixer_tile(ti):
        smp = ti == NTP
        mi = 1 if smp else 0
        NB = NSB if smp else 1
        LB = ST if smp else 128
        xb = xt[ti % 2]
        xd, xap = x_rows(ti)
        if smp:
            k.barrier()
            k.bot = mark_rw
            Cs = k.sbuf([128, NSB, 129], F32, "Cs")
            Csb = k.sbuf([128, NSB, 130], BF16, "Csb")
            qTm = k.sbuf([128, NSB, 128], BF16, "qTm")
            ktmb = k.sbuf([128, NSB, 128], BF16, "ktmb")
            blkF = k.sbuf([128, NSB, 128], BF16, "blkF")
            rowm = k.sbuf([128, NSB], F32, "rowm")
            k.memset("pool", (blkF, blkF[:]), 1.0)
            aff(blkF, blkF[:], [[-ST, NSB], [1, 128]], 0, 0)
            aff(blkF, blkF[:], [[ST, NSB], [-1, 128]], 0, ST - 1)
            k.memset("pool", (rowm, rowm[:]), 1.0)
            aff(rowm, rowm[:], [[-ST, NSB]], 1, 0)
            aff(rowm, rowm[:], [[ST, NSB]], -1, ST - 1)
            smc = cv.view(cv[:].rearrange("p a b -> p (a b)")[0:NSB * 3, :], "smc")
            ext_s = xm.view(xm[:].rearrange("p a b -> p (a b)")[:, 0:8 * NSB * 11].rearrange("p (c b t) -> p c b t", c=8, b=NSB), "ext_s")
            k.dma("sp", smc[:], st_mconv[:, :], reads=[st_mconv], writes=[smc])
            for c in range(8):
                k.tr((pM[0], pM[0][:, c * 48:(c + 1) * 48]), (smc, smc[:, c * 128:(c + 1) * 128]), (identf, identf[0:48, 0:48]))
            k.copy("act", (ext_s, ext_s[:, :, :, 0:3]), (pM[0], pM[0][:, 0:384].rearrange("p (c b j) -> p c b j", c=8, b=NSB)))
            k.dma("sp", mst[:, 0:NSB], st_mm[:, :].rearrange("b h -> h b"), reads=[st_mm], writes=[mst], allow_slow_non_contiguous=True)
        if ti not in prefetched:
            k.dma("sp", xb[:], xap, reads=[xd], writes=[xb])
            norm_to_hT(xb, g1bc)

        if smp:
            for g in range(2):
                for c in range(4):
                    proj_fm(pF[g], pF[g][:, c * 128:(c + 1) * 128], C_QK + (4 * g + c) * 128)
                k.copy("act", (ext_s, ext_s[:, 4 * g:4 * g + 4, :, 3:11]), (pF[g], pF[g][:].rearrange("p (c b t) -> p c b t", c=4, b=NSB)))
            for c in range(8):
                cvv = cv[:, c, :].rearrange("p (b t) -> p b t", t=ST)
                k.ts("dve", (cv, cvv), (ext_s, ext_s[:, c, :, 3:11]), pcol(PT_MCW + 3 * 8 + c), pcol(PT_MCB + c),
                     op0=ALU.mult, op1=ALU.add)
                for j in range(3):
                    k.stt("dve", (cv, cvv), (ext_s, ext_s[:, c, :, j:j + ST]), pcol(PT_MCW + j * 8 + c), (cv, cvv),
                          ALU.mult, ALU.add)
        if not smp:
            if ti not in prefetched:
                k.copy("pool", (ext_q, ext_q[:, :, 0:3]), (cq, cq[:]))
                for g in range(2):
                    for c in range(4):
                        proj_fm(pF[g], pF[g][:, c * 128:(c + 1) * 128], C_QK + (4 * g + c) * 128)
                    k.copy("act", (ext_q, ext_q[:, 4 * g:4 * g + 4, 3:131]), (pF[g], pF[g][:].rearrange("p (c t) -> p c t", c=4)))
                k.copy("pool", (cq, cq[:]), (ext_q, ext_q[:, :, 128:131]))
            for c in range(8):
                k.ts("dve", (cv, cv[:, c, :]), (ext_q, ext_q[:, c, 3:131]), pcol(PT_MCW + 3 * 8 + c), pcol(PT_MCB + c),
                     op0=ALU.mult, op1=ALU.add)
                for j in range(3):
                    k.stt("dve", (cv, cv[:, c, :]), (ext_q, ext_q[:, c, j:j + 128]), pcol(PT_MCW + j * 8 + c), (cv, cv[:, c, :]),
                          ALU.mult, ALU.add)
        k.act((qkT, qkT[:].rearrange("p a b -> p (a b)")), (cv, cv[:].rearrange("p a b -> p (a b)")), AF.Silu)

        if ti not in prefetched:
            proj_tm(pR[0], pR[0][:, :], C_V, 512)
            k.copy("act", (vaug, vaug[:, :, 0:128]), (pR[0], pR[0][:].rearrange("p (h c) -> p h c", h=4)))
            for c in range(4):
                proj_fm(pF[0], pF[0][:, c * 128:(c + 1) * 128], C_O + c * 128)
            k.act((soT, soT[:].rearrange("p a b -> p (a b)")), (pF[0], pF[0][:, :]), AF.Sigmoid)
        if not smp and stage >= 2:
            rwkv_front_proj(ti)
        proj_fm(pM[0], pM[0][0:4, 0:128], C_I, M=4)
        proj_fm(pM[0], pM[0][0:4, 128:256], C_F, M=4)
        liT, nlf, ncum, gT_, t0, t1 = gsm[0], gsm[1], gsm[2], gsm[3], gsm[4], gsm[5]
        k.ts("dve", (liT, liT[:]), (pM[0], pM[0][0:4, 0:128]), (gb, gb[:, 0:1]), None, op0=ALU.add)
        k.act((t0, t0[:]), (pM[0], pM[0][0:4, 128:256]), AF.Exp, bias=(nbf, nbf[:, 0:1]), scale=-1.0)
        k.act((nlf, nlf[:]), (t0, t0[:]), AF.Ln, bias=1.0)
        k.scan("dve", (ncum, ncum[:]), (resets[mi], resets[mi][0:4, 0:128]), (nlf, nlf[:]), 0.0, ALU.mult, ALU.add)
        k.tt("dve", (gT_, gT_[:]), (liT, liT[:]), (ncum, ncum[:]), ALU.add)
        b3 = lambda buf: buf[:].rearrange("p (b l) -> p b l", l=LB)
        mcb_ = mst[:, 0:NB].unsqueeze(2).to_broadcast([4, NB, LB])
        nlast = ncum[:].rearrange("p (b l) -> p b l", l=LB)[:, :, LB - 1:LB]
        k.stt("dve", (t0, b3(t0)), (gT_, b3(gT_)), LNK, (mst, mcb_), ALU.add, ALU.subtract)
        k.act((gpk, gpk[:, 0, :]), (t0, t0[:]), AF.Exp)
        k.tt("dve", (t1, b3(t1)), (ncum, b3(ncum)), (mst, mcb_), ALU.subtract)
        k.act((gpk, gpk[:, 1, :]), (t1, t1[:]), AF.Exp)
        k.tt("dve", (t1, b3(t1)), (gT_, b3(gT_)), (ncum, nlast.to_broadcast([4, NB, LB])), ALU.subtract)
        k.red("dve", (gt[0], gt[0][:, 0:NB]), (t1, b3(t1)), ALU.max)
        k.tt("dve", (gt[1], gt[1][:, 0:NB]), (mst, mst[:, 0:NB]), (ncum, nlast.rearrange("p b o -> p (b o)")), ALU.subtract)
        k.tt("dve", (mnew, mnew[:, 0:NB]), (gt[1], gt[1][:, 0:NB]), (gt[0], gt[0][:, 0:NB]), ALU.max)
        k.tt("dve", (gt[2], gt[2][:, 0:NB]), (gt[1], gt[1][:, 0:NB]), (mnew, mnew[:, 0:NB]), ALU.subtract)
        k.act((gt[3], gt[3][:, 0:NB]), (gt[2], gt[2][:, 0:NB]), AF.Exp)
        k.tt("dve", (gpk, gpk[:, 2, :].rearrange("p (b l) -> p b l", l=LB)), (gpk, gpk[:, 0, :].rearrange("p (b l) -> p b l", l=LB)),
             (gt[3], gt[3][:, 0:NB].unsqueeze(2).to_broadcast([4, NB, LB])), ALU.mult)
        for j in range(3):
            k.tr((pM[1], pM[1][:, 4 * j:4 * j + 4]), (gpk, gpk[:, j, :]), (identf, identf[0:4, 0:4]))
        k.copy("dve", (tokS, tokS[:]), (pM[1], pM[1][:, 0:12]))
        k.tt("dve", (s0d, s0d[:, :, 0:NB]), (identf, identf[0:4, 0:4].unsqueeze(2).to_broadcast([4, 4, NB])),
             (gt[3], gt[3][:, 0:NB].unsqueeze(1).to_broadcast([4, 4, NB])), ALU.mult)
        k.mm((pM[1], pM[1][:, 16:16 + 4 * NB]), (ones4, ones4[:]), (s0d, s0d[:, :, 0:NB].rearrange("p a b -> p (a b)")))
        k.copy("dve", (s0bc, s0bc[:, 0:4 * NB]), (pM[1], pM[1][:, 16:16 + 4 * NB]))

        for h in range(4):
            k.mm((pM[0], pM[0][:, h * 128:(h + 1) * 128]), (qkT, qkT[:, 4 + h, :]), (qkT, qkT[:, h, :]))
        for h in range(4):
            k.stt("dve", (PTm, PTm[:, h, :]), (pM[0], pM[0][:, h * 128:(h + 1) * 128]), (tokS, tokS[:, h:h + 1]),
                  (mU_in[mi], mU_in[mi][:]), ALU.mult, ALU.mult)
        pO = [pM[1], pM[2]]
        oap = lambda h: pO[h // 2][:, 256 * (h % 2):256 * (h % 2) + 129]
        if not smp:
            for h in range(4):
                k.mm((pO[h // 2], oap(h)), (qkT, qkT[:, h, :]), (Cb, Cb[:, h, 0:129]), start=True, stop=False)
                k.mm((pO[h // 2], oap(h)), (PTm, PTm[:, h, :]), (vaug, vaug[:, h, 0:129]), start=False, stop=True)
        else:
            for h in range(4):
                k.tr((pT, pT[:, h * 128:(h + 1) * 128]), (qkT, qkT[:, 4 + h, :]), (identb, identb[:]))
            for h in range(4):
                k.ts("dve", (ktm, ktm[:, h, :]), (pT, pT[:, h * 128:(h + 1) * 128]), (tokS, tokS[:, 8 + h:9 + h]), None, op0=ALU.mult)
            for h in range(4):
                k.dma("sp", Cs[:, :, 0:128], st_mC[:, h, :, :].rearrange("b d v -> d b v"), reads=[st_mC], writes=[Cs])
                k.dma("sp", Cs[:, :, 128], st_mn[:, h, :].rearrange("b d -> d b"), reads=[st_mn], writes=[Cs], allow_slow_non_contiguous=True)
                k.copy("act", (Csb, Csb[:, :, 0:129]), (Cs, Cs[:]))
                k.tt("dve", (qTm, qTm[:]), (qkT, qkT[:, h, :].unsqueeze(1).to_broadcast([128, NSB, 128])), (blkF, blkF[:]), ALU.mult)
                for b in range(NSB):
                    k.mm((pO[h // 2], oap(h)), (qTm, qTm[:, b, :]), (Csb, Csb[:, b, 0:129]), start=(b == 0), stop=False)
                k.mm((pO[h // 2], oap(h)), (PTm, PTm[:, h, :]), (vaug, vaug[:, h, 0:129]), start=False, stop=True)
                k.tt("dve", (ktmb, ktmb[:]), (ktm, ktm[:, h, :].unsqueeze(1).to_broadcast([128, NSB, 128])),
                     (rowm, rowm[:].unsqueeze(2).to_broadcast([128, NSB, 128])), ALU.mult)
                for grp in range(4):
                    bank = pF[grp % 2]
                    for bi in range(4):
                        b = 4 * grp + bi
                        k.mm((bank, bank[:, bi * 128:(bi + 1) * 128]), (ktmb, ktmb[:, b, :]), (vaug, vaug[:, h, 0:128]))
                    for bi in range(4):
                        b = 4 * grp + bi
                        k.stt("dve", (Cs, Cs[:, b, 0:128]), (Cs, Cs[:, b, 0:128]), (s0bc, s0bc[:, h * NSB + b:h * NSB + b + 1]),
                              (bank, bank[:, bi * 128:(bi + 1) * 128]), ALU.mult, ALU.add)
                for b in range(NSB):
                    k.mm((pR[0], pR[0][:, b:b + 1]), (ktmb, ktmb[:, b, :]), (vaug, vaug[:, h, 128:129]))
                k.tt("dve", (Cs, Cs[:, :, 128]), (Cs, Cs[:, :, 128]), (s0bc, s0bc[:, h * NSB:(h + 1) * NSB]), ALU.mult)
                k.tt("dve", (Cs, Cs[:, :, 128]), (Cs, Cs[:, :, 128]), (pR[0], pR[0][:, 0:NSB]), ALU.add)
                k.dma("sp", o_sC[:, h, :, :].rearrange("b d v -> d b v"), Cs[:, :, 0:128], reads=[Cs], writes=[o_sC])
                k.dma("sp", o_sn[:, h, :].rearrange("b d -> d b"), Cs[:, :, 128], reads=[Cs], writes=[o_sn], allow_slow_non_contiguous=True)
        for h in range(4):
            k.copy("act", (dn, dn[:, h:h + 1]), (pO[h // 2], oap(h)[:, 128:129]))
        k.stt("dve", (dn, dn[:]), (dn, dn[:]), -1.0, (dn, dn[:]), ALU.mult, ALU.max)
        k.tt("dve", (dn, dn[:]), (dn, dn[:]), (tokS, tokS[:, 4:8]), ALU.max)
        k.op("dve", lambda e: e.reciprocal(out=dn[:], in_=dn[:]), reads=[dn], writes=[dn])
        for h in range(4):
            k.act((hm, hm[:, h, :]), (pO[h // 2], oap(h)[:, 0:128]), AF.Copy, scale=(dn, dn[:, h:h + 1]))
        for h in range(4):
            k.op("dve", lambda e, h=h: e.bn_stats(out=bst[:, h, :], in_=hm[:, h, :]), reads=[hm], writes=[(bst, h)])
        for h in range(4):
            k.op("dve", lambda e, h=h: e.bn_aggr(out=bag[:, h, :], in_=bst[:, h, :]), reads=[(bst, h)], writes=[(bag, h)])
        k.act((bag, bag[:, :, 1:2]), (bag, bag[:, :, 1:2]), AF.Ln, bias=EPS)
        k.act((bag, bag[:, :, 1:2]), (bag, bag[:, :, 1:2]), AF.Exp, scale=-0.5)
        for h in range(4):
            k.ts("dve", (hn, hn[:, h, :]), (hm, hm[:, h, :]), (bag, bag[:, h, 0:1]), (bag, bag[:, h, 1:2]),
                 op0=ALU.subtract, op1=ALU.mult)
        for h in range(4):
            k.tr((pM[0], pM[0][:, h * 128:(h + 1) * 128]), (hn, hn[:, h, :]), (identf, identf[:]))
        for h in range(4):
            k.stt("dve", (hmT_all, hmT_all[:, ti, h, :], ti), (pM[0], pM[0][:, h * 128:(h + 1) * 128]), pcol(PT_MNG + h),
                  (soT, soT[:, h, :]), ALU.mult, ALU.mult)
        if not smp:
            for h in range(4):
                k.tr((pT, pT[:, h * 128:(h + 1) * 128]), (qkT, qkT[:, 4 + h, :]), (identb, identb[:]))
            for h in range(4):
                k.ts("dve", (ktm, ktm[:, h, :]), (pT, pT[:, h * 128:(h + 1) * 128]), (tokS, tokS[:, 8 + h:9 + h]), None, op0=ALU.mult)
            for h in range(4):
                k.mm((pO[h // 2], oap(h)), (ktm, ktm[:, h, :]), (vaug, vaug[:, h, 0:129]))
            for h in range(4):
                k.stt("dve", (Cst, Cst[:, h, :]), (Cst, Cst[:, h, :]), (s0bc, s0bc[:, h:h + 1]), (pO[h // 2], oap(h)),
                      ALU.mult, ALU.add)
            k.copy("act", (Cb, Cb[:, :, 0:129]), (Cst, Cst[:]))
            k.copy("dve", (mst, mst[:, 0:1]), (mnew, mnew[:, 0:1]))
        if ti == NTP - 1:
            for h in range(4):
                k.dma("sp", o_pC[h], Cst[:, h, 0:128], reads=[Cst], writes=[o_pC])
            k.dma("sp", o_pn[:].rearrange("h d -> d h"), Cst[:, :, 128], reads=[Cst], writes=[o_pn], allow_slow_non_contiguous=True)
            k.dma("sp", o_pm[:].rearrange("o h -> h o"), mnew[:, 0:1], reads=[mnew], writes=[o_pm], allow_slow_non_contiguous=True)
            for blk in range(2):
                proj_tm(pR[blk], pR[blk][:, :], C_QK + blk * 512, 512)
                k.copy("act", (zq_tm, zq_tm[:, blk * 512:(blk + 1) * 512]), (pR[blk], pR[blk][:, :]))
            k.dma("sp", o_pconv[:], zq_tm[125:128, :], reads=[zq_tm], writes=[o_pconv])
        if smp:
            k.dma("sp", o_sm[:, :].rearrange("b h -> h b"), mnew[:, 0:NSB], reads=[mnew], writes=[o_sm], allow_slow_non_contiguous=True)
            for blk in range(2):
                proj_tm(pR[blk], pR[blk][:, :], C_QK + blk * 512, 512)
                k.copy("act", (zq_tm, zq_tm[:, blk * 512:(blk + 1) * 512]), (pR[blk], pR[blk][:, :]))
            for b in range(NSB):
                k.dma("sp", o_sconv[3 * b:3 * b + 3, :], zq_tm[ST * b + 5:ST * b + 8, :], reads=[zq_tm], writes=[o_sconv])

    k.fence_mm = (pT, identb)
    BK = [pM[0], pM[1], pF[0], pF[1], pR[0], pR[1]]

    _rw_stop = int(_os.environ.get("KDBG_RW", "99"))

    def rwkv_front_proj(ti):
        k.copy("pool", (ext_r, ext_r[:, :, 0:1]), (cr, cr[:]))
        for g in range(4):
            n = min(4, 14 - 4 * g)
            for c in range(n):
                proj_fm(pF[g % 2], pF[g % 2][:, c * 128:(c + 1) * 128], C_R + (4 * g + c) * 128)
            k.copy("act", (ext_r, ext_r[:, 4 * g:4 * g + n, 1:129]),
                   (pF[g % 2], pF[g % 2][:, 0:n * 128].rearrange("p (c t) -> p c t", c=n)))
        k.copy("pool", (cr, cr[:]), (ext_r, ext_r[:, :, 128:129]))
        k.tt("pool", (xm, xm[:]), (ext_r, ext_r[:, :, 0:128]), (ext_r, ext_r[:, :, 1:129]), ALU.subtract)
        for c in range(14):
            k.stt("dve", (xm, xm[:, c, :]), (xm, xm[:, c, :]), pcol(PT_RMIX + c), (ext_r, ext_r[:, c, 1:129]), ALU.mult, ALU.add)

    def rwkv_tile(ti):
        smp = ti == NTP
        mi = 0
        NLV = 7
        rtl = rt
        if not smp:
            pass
        else:
            k.barrier()
            k.bot = mark_rw
            ext_rs = k.sbuf([128, 14, NSB, ST + 1], F32, "ext_rs")
            rtl = [k.sbuf([128, 4, 128], F32, f"rts{i}") for i in range(7)] + [rt[7], rt[8]]
            stg = k.sbuf([128, 512], F32, "stg")
            srs = xm.view(xm[:].rearrange("p a b -> p (a b)")[0:NSB, :], "srs")
            k.dma("sp", srs[:], st_rshift[:, :], reads=[st_rshift], writes=[srs])
            for c in range(14):
                k.tr((pM[0], pM[0][:, c * NSB:(c + 1) * NSB]), (srs, srs[:, c * 128:(c + 1) * 128]), (identf, identf[0:NSB, 0:NSB]))
            k.copy("act", (ext_rs, ext_rs[:, :, :, 0]), (pM[0], pM[0][:, 0:14 * NSB].rearrange("p (c b) -> p c b", c=14)))
            for g in range(4):
                n = min(4, 14 - 4 * g)
                for c in range(n):
                    proj_fm(pF[g % 2], pF[g % 2][:, c * 128:(c + 1) * 128], C_R + (4 * g + c) * 128)
                k.copy("act", (ext_rs, ext_rs[:, 4 * g:4 * g + n, :, 1:ST + 1]),
                       (pF[g % 2], pF[g % 2][:, 0:n * 128].rearrange("p (c b t) -> p c b t", c=n, b=NSB)))
            xm4 = xm[:].rearrange("p c (b t) -> p c b t", t=ST)
            k.tt("pool", (xm, xm4), (ext_rs, ext_rs[:, :, :, 0:ST]), (ext_rs, ext_rs[:, :, :, 1:ST + 1]), ALU.subtract)
            for c in range(14):
                k.stt("dve", (xm, xm4[:, c]), (xm, xm4[:, c]), pcol(PT_RMIX + c), (ext_rs, ext_rs[:, c, :, 1:ST + 1]), ALU.mult, ALU.add)
        rT, krT, vrT = xm[:, 0:4, :], xm[:, 4:8, :], xm[:, 8:12, :]
        sig, cums, gam, ginv, gexc, a_, kk, tmp, kr2 = rtl
        if _rw_stop <= 1:
            return
        k.act((thad, thad[0:64, :]), (xm, xm[0:64, 12, :]), AF.Tanh)
        k.copy("act", (thad, thad[64:128, :]), (xm, xm[64:128, 12, :]))
        k.act((sgd, sgd[:]), (xm, xm[:, 13, :]), AF.Sigmoid)
        for c in range(4):
            k.mm((pM[0], pM[0][:, c * 128:(c + 1) * 128]), (Wl_w2, Wl_w2[0:64, c * 128:(c + 1) * 128]), (thad, thad[0:64, :]))
        for c in range(4):
            k.act((sig, sig[:, c, :]), (pM[0], pM[0][:, c * 128:(c + 1) * 128]), AF.Sigmoid, bias=pcol(PT_RW0 + c))
        k.pe_fence()
        for c in range(4):
            k.mm((pM[1], pM[1][:, c * 128:(c + 1) * 128]), (Wl_a2, Wl_a2[64:128, c * 128:(c + 1) * 128]), (thad, thad[64:128, :]))
        k.pe_fence()
        for c in range(4):
            k.act((a_, a_[:, c, :]), (pM[1], pM[1][:, c * 128:(c + 1) * 128]), AF.Sigmoid, bias=pcol(PT_RA0 + c))
        for c in range(4):
            k.mm((pM[2], pM[2][:, c * 128:(c + 1) * 128]), (Wl_g2, Wl_g2[:, c * 128:(c + 1) * 128]), (sgd, sgd[:]))
        k.copy("act", (gTs, gTs[:].rearrange("p a b -> p (a b)")), (pM[2], pM[2][:, :]))
        if _rw_stop <= 2:
            return
        fl = lambda b: b[:].rearrange("p a b -> p (a b)")
        if not smp:
            k.scan("dve", (cums, fl(cums)), (resets[mi], resets[mi][:]), (sig, fl(sig)), 0.0, ALU.mult, ALU.add)
            k.act((gam, fl(gam)), (cums, fl(cums)), AF.Exp, scale=WSCALE)
            k.act((ginv, fl(ginv)), (cums, fl(cums)), AF.Exp, scale=-WSCALE)
            k.tt("dve", (tmp, tmp[:]), (cums, cums[:]), (sig, sig[:]), ALU.subtract)
            k.act((gexc, fl(gexc)), (tmp, fl(tmp)), AF.Exp, scale=WSCALE)
        else:
            k.act((gam, fl(gam)), (sig, fl(sig)), AF.Exp, scale=WSCALE)
        if _rw_stop <= 3:
            return
        for c in range(4):
            k.ts("dve", (kk, kk[:, c, :]), (xm, xm[:, 4 + c, :]), pcol(PT_RKK + c), None, op0=ALU.mult)
        k.tt("dve", (tmp, tmp[:]), (kk, kk[:]), (kk, kk[:]), ALU.mult)
        for c in range(4):
            k.mm((pM[0], pM[0][:, c * 128:(c + 1) * 128]), (bones, bones[:]), (tmp, tmp[:, c, :]))
        k.ts("dve", (tmp, fl(tmp)), (pM[0], pM[0][:, :]), 1e-24, None, op0=ALU.max)
        k.act((tmp, fl(tmp)), (tmp, fl(tmp)), AF.Ln)
        k.act((tmp, fl(tmp)), (tmp, fl(tmp)), AF.Exp, scale=-0.5)
        k.tt("dve", (kk, kk[:]), (kk, kk[:]), (tmp, tmp[:]), ALU.mult)
        for c in range(4):
            k.ts("dve", (tmp, tmp[:, c, :]), (a_, a_[:, c, :]), -1.0, pcol(PT_RKA + c), op0=ALU.add, op1=ALU.mult)
        k.stt("dve", (kr2, kr2[:]), (tmp, tmp[:]), 1.0, (xm, krT), ALU.add, ALU.mult)
        k.tt("dve", (tmp, tmp[:]), (xm, rT), (kr2, kr2[:]), ALU.mult)
        for c in range(4):
            k.ts("dve", (tmp, tmp[:, c, :]), (tmp, tmp[:, c, :]), pcol(PT_RRK + c), None, op0=ALU.mult)
        for c in range(4):
            k.mm((pM[1], pM[1][:, c * 128:(c + 1) * 128]), (bones, bones[:]), (tmp, tmp[:, c, :]))
        k.tt("dve", (bonT, fl(bonT)), (pM[1], pM[1][:, :]), (xm, vrT.rearrange("p a b -> p (a b)") if False else xm[:, 8:12, :].rearrange("p a b -> p (a b)")), ALU.mult)
        if _rw_stop <= 4:
            return
        if smp:
            rwkv_sample_core(xm, gam, kr2, kk, a_, tmp, stg, gTs, bonT)
            return
        k.stt("dve", (ART, ART[:, :, 0, :]), (kk, kk[:]), -1.0, (gexc, gexc[:]), ALU.mult, ALU.mult)
        k.tt("dve", (ART, ART[:, :, 1, :]), (xm, rT), (gam, gam[:]), ALU.mult)
        k.tt("dve", (tmp, tmp[:]), (kk, kk[:]), (a_, a_[:]), ALU.mult)
        k.tt("dve", (BTb, BTb[:]), (tmp, tmp[:]), (ginv, ginv[:]), ALU.mult)
        k.tt("dve", (KTb, KTb[:]), (kr2, kr2[:]), (ginv, ginv[:]), ALU.mult)
        k.copy("act", (VTb, VTb[:]), (xm, vrT))
        if _rw_stop <= 5:
            return
        for c in range(4):
            k.tr((pT, pT[:, c * 128:(c + 1) * 128]), (ART, ART[:, c, 0, :]), (identb, identb[:]))
            k.tr((pT, pT[:, 512 + c * 128:512 + (c + 1) * 128]), (BTb, BTb[:, c, :]), (identb, identb[:]))
        k.copy("act", (AB_tm, AB_tm[:].rearrange("p a b -> p (a b)")), (pT, pT[:, :]))
        for c in range(4):
            k.tr((pT, pT[:, c * 128:(c + 1) * 128]), (KTb, KTb[:, c, :]), (identb, identb[:]))
            k.tr((pT, pT[:, 512 + c * 128:512 + (c + 1) * 128]), (VTb, VTb[:, c, :]), (identb, identb[:]))
        k.copy("dve", (KV_tm, KV_tm[:].rearrange("p a b -> p (a b)")), (pT, pT[:, :]))
        if _rw_stop <= 6:
            return
        A_tm = lambda h: (AB_tm, AB_tm[:, 0, h * 64:(h + 1) * 64])
        B_tm = lambda h: (AB_tm, AB_tm[:, 1, h * 64:(h + 1) * 64])
        K_tm = lambda h: (KV_tm, KV_tm[:, 0, h * 64:(h + 1) * 64])
        V_tm = lambda h: (KV_tm, KV_tm[:, 1, h * 64:(h + 1) * 64])
        m2b = mask2[mi][:].unsqueeze(1).to_broadcast([128, 2, 256])
        GB2, GK2, Nn2, PP2, XX2 = [GBm, GBm_b], [GKm, GKm_b], [Nn, Nn_b], [PP, PP_b], [XX, XX_b]
        LB3 = [[pM[0], pM[1], pR[0]], [pF[0], pF[1], pR[1]]]
        for g in range(2):
            GBm_, GKm_, Nn_, XX_ = GB2[g], GK2[g], Nn2[g], XX2[g]
            heads = [4 * g + i for i in range(4)]
            HO = [(pbs, [(i, h) for i, h in enumerate(heads) if 64 * (h % 2) == pbs]) for pbs in (0, 64)]
            for pbs, hl in HO:
                for i, h in hl:
                    c, pb = h // 2, 64 * (h % 2)
                    off = (i % 2) * 256
                    rAR = (ART, ART[pb:pb + 64, c, :, :].rearrange("p a t -> p (a t)"))
                    k.mm((BK[i // 2], BK[i // 2][:, off:off + 256]), (BTb, BTb[pb:pb + 64, c, :]), rAR)
                    k.mm((BK[2 + i // 2], BK[2 + i // 2][:, off:off + 256]), (KTb, KTb[pb:pb + 64, c, :]), rAR)
                    k.mm((BK[4], BK[4][:, i * 128:(i + 1) * 128]), (ART, ART[pb:pb + 64, c, 0, :]), (BTb, BTb[pb:pb + 64, c, :]))
                k.pe_fence()
            for hf in range(2):
                k.tt("dve", (GBm_, GBm_[:, 2 * hf:2 * hf + 2, :]), (BK[hf], BK[hf][:].rearrange("p (a b) -> p a b", a=2)), (mask2[mi], m2b), ALU.mult)
                k.tt("dve", (GKm_, GKm_[:, 2 * hf:2 * hf + 2, :]), (BK[2 + hf], BK[2 + hf][:].rearrange("p (a b) -> p a b", a=2)), (mask2[mi], m2b), ALU.mult)
            k.tt("dve", (Nn_, Nn_[:]), (BK[4], BK[4][:].rearrange("p (a b) -> p a b", a=4)),
                 (mL_st[mi], mL_st[mi][:].unsqueeze(1).to_broadcast([128, 4, 128])), ALU.mult)
            for i, h in enumerate(heads):
                k.mm((BK[5], BK[5][:, i * 64:(i + 1) * 64]), (GKm_, GKm_[:, i, 0:128]), V_tm(h))
            k.copy("act", (XX_[0], XX_[0][:, :, 64:128]), (BK[5], BK[5][:, 0:256].rearrange("p (a b) -> p a b", a=4)))
            k.copy("pool", (XX_[0], XX_[0][:, :, 0:64]), (AB_tm, AB_tm[:, 0, 256 * g:256 * g + 256].rearrange("p (a b) -> p a b", a=4)))

        xfinal = [None, None]

        def levels_gen(g):
            GBm_, Nn_, PP_, XX_ = GB2[g], Nn2[g], PP2[g], XX2[g]
            bP, bQ, bX = LB3[g]
            Pc = lambda i: (Nn_, Nn_[:, i, :])
            PTc = lambda i: (GBm_, GBm_[:, i, 0:128])
            xi = 0
            for lvl in range(NLV):
                Xc, Xn = XX_[xi], XX_[1 - xi]
                for i in range(4):
                    o = (bX, bX[:, i * 128:(i + 1) * 128])
                    k.mm(o, (identb, identb[:]), (Xc, Xc[:, i, :]), start=True, stop=False)
                    k.mm(o, PTc(i), (Xc, Xc[:, i, :]), start=False, stop=True)
                k.copy("act", (Xn, Xn[:].rearrange("p a b -> p (a b)")), (bX, bX[:, :]))
                xi = 1 - xi
                yield
                if lvl < NLV - 1:
                    bb = [bP, bQ]
                    for i in range(4):
                        off = (i % 2) * 256
                        if lvl < NLV - 2:
                            k.mm((bb[i // 2], bb[i // 2][:, off:off + 128]), PTc(i), Pc(i))
                        k.mm((bb[i // 2], bb[i // 2][:, off + 128:off + 256]), Pc(i), PTc(i))
                    PPn = PP_[lvl % 2]
                    for hf in range(2):
                        if lvl < NLV - 2:
                            k.copy("dve", (PPn, PPn[:, 2 * hf:2 * hf + 2, :]), (bb[hf], bb[hf][:].rearrange("p (a b) -> p a b", a=2)))
                        else:
                            k.copy("dve", (PPn, PPn[:, 2 * hf:2 * hf + 2, 128:256]),
                                   (bb[hf], bb[hf][:].rearrange("p (a b) -> p a b", a=2)[:, :, 128:256]))
                    Pc = lambda i, PPn=PPn: (PPn, PPn[:, i, 0:128])
                    PTc = lambda i, PPn=PPn: (PPn, PPn[:, i, 128:256])
                    yield
            xfinal[g] = XX_[xi]

        gens = [levels_gen(0), levels_gen(1)]
        if ti + 1 < NTP and (ti + 1) in tiles_run:
            gens.append(prefetch_gen(ti + 1))
            prefetched.add(ti + 1)
        while gens:
            for g_ in list(gens):
                try:
                    next(g_)
                except StopIteration:
                    gens.remove(g_)

        for g in range(2):
            heads = [4 * g + i for i in range(4)]
            HO = [(pbs, [(i, h) for i, h in enumerate(heads) if 64 * (h % 2) == pbs]) for pbs in (0, 64)]
            GBt, GKt = GB2[g], GK2[g]
            Xf = xfinal[g]
            if _rw_stop <= 8:
                continue
            k.pe_fence()
            for pbs, hl in HO:
                for i, h in hl:
                    c, pb = h // 2, 64 * (h % 2)
                    ci = i // 2
                    o = (BK[0], BK[0][pb:pb + 64, ci * 128:(ci + 1) * 128])
                    k.mm(o, (Xf, Xf[:, i, 0:64]), (GBt, GBt[:, i, 128:256]), start=True, stop=False)
                    k.pe_fence()
                    k.mm(o, (identb, identb[pb:pb + 64, pb:pb + 64]), (ART, ART[pb:pb + 64, c, 1, :]), start=False, stop=True)
                    k.pe_fence()
            k.copy("act", (QT, QT[:].rearrange("p a b -> p (a b)")), (BK[0], BK[0][:, 0:256]))
            for pbs, hl in HO:
                for i, h in hl:
                    c, pb = h // 2, 64 * (h % 2)
                    ci = i // 2
                    o = (pM[2], pM[2][:, h * 64:(h + 1) * 64])
                    k.mm(o, (QT, QT[pb:pb + 64, ci, :]), (STb, STb[pb:pb + 64, c, :]), start=True, stop=False)
                    k.pe_fence()
                    k.mm(o, (GBt, GBt[:, i, 128:256]), (Xf, Xf[:, i, 64:128]), start=False, stop=False)
                    k.mm(o, (GKt, GKt[:, i, 128:256]), V_tm(h), start=False, stop=True)
                    k.pe_fence()
            if _rw_stop <= 9:
                continue
            for pbs, hl in HO:
                for i, h in hl:
                    c, pb = h // 2, 64 * (h % 2)
                    ci = i // 2
                    k.mm((BK[1], BK[1][pb:pb + 64, ci * 64:(ci + 1) * 64]), (Xf, Xf[:, i, 0:64]), B_tm(h))
                k.pe_fence()
            k.tt("dve", (IE, IE[:, 2 * g:2 * g + 2, :]), (BK[1], BK[1][:, 0:128].rearrange("p (a b) -> p a b", a=2)),
                 (I2, I2[:].unsqueeze(1).to_broadcast([128, 2, 64])), ALU.add)
            for pbs, hl in HO:
                for i, h in hl:
                    c, pb = h // 2, 64 * (h % 2)
                    ci = i // 2
                    o = (BK[2], BK[2][pb:pb + 64, ci * 64:(ci + 1) * 64])
                    k.mm(o, (IE, IE[pb:pb + 64, c, :]), (STf, STf[pb:pb + 64, c, :]), start=True, stop=False)
                    k.pe_fence()
                    k.mm(o, B_tm(h), (Xf, Xf[:, i, 64:128]), start=False, stop=False)
                    k.mm(o, K_tm(h), V_tm(h), start=False, stop=True)
                    k.pe_fence()
            for ci in range(2):
                c = 2 * g + ci
                k.ts("dve", (STf, STf[:, c, :]), (BK[2], BK[2][:, ci * 64:(ci + 1) * 64]), (gam, gam[:, c, 127:128]), None, op0=ALU.mult)
            k.copy("act", (STb, STb[:, 2 * g:2 * g + 2, :]), (STf, STf[:, 2 * g:2 * g + 2, :]))
        if _rw_stop <= 10:
            return
        rwkv_epilogue(ti, pM[2], tmp)
        if ti == NTP - 1:
            for c in range(4):
                k.tr((pM[0], pM[0][0:64, c * 128:(c + 1) * 128]), (STf, STf[:, c, :]), (identf, identf[:]))
            k.copy("act", (rt[0], rt[0][0:64, :, :]), (pM[0], pM[0][0:64, :].rearrange("p (a b) -> p a b", a=4)))
            k.dma("sp", o_pS[:].rearrange("(h i) j -> i h j", h=8), rt[0][0:64, :, :].rearrange("p c (f j) -> p (c f) j", f=2),
                  reads=[rt[0]], writes=[o_pS])

    def rwkv_epilogue(ti, Yb, tmp):
        pM2 = [None, None, Yb]
        for h in range(8):
            k.op("dve", lambda e, h=h: e.bn_stats(out=bst8[:, h, :], in_=Yb[:, h * 64:(h + 1) * 64]), reads=[Yb], writes=[(bst8, h)])
        for h in range(8):
            k.op("dve", lambda e, h=h: e.bn_aggr(out=bag8[:, h, :], in_=bst8[:, h, :]), reads=[(bst8, h)], writes=[(bag8, h)])
        k.act((bag8, bag8[:, :, 1:2]), (bag8, bag8[:, :, 1:2]), AF.Ln, bias=GN_EPS)
        k.act((bag8, bag8[:, :, 1:2]), (bag8, bag8[:, :, 1:2]), AF.Exp, scale=-0.5)
        for h in range(8):
            k.ts("dve", (yn, yn[:, h, :]), (Yb, Yb[:, h * 64:(h + 1) * 64]), (bag8, bag8[:, h, 0:1]), (bag8, bag8[:, h, 1:2]),
                 op0=ALU.subtract, op1=ALU.mult)
        for c in range(4):
            k.tr((pM[0], pM[0][:, c * 128:(c + 1) * 128]), (yn, yn[:, 2 * c:2 * c + 2, :].rearrange("p a b -> p (a b)")), (identf, identf[:]))
        for c in range(4):
            k.ts("dve", (tmp, tmp[:, c, :]), (pM[0], pM[0][:, c * 128:(c + 1) * 128]), pcol(PT_RLNG + c), pcol(PT_RLNB + c),
                 op0=ALU.mult, op1=ALU.add)
        k.tt("pool", (tmp, tmp[:]), (tmp, tmp[:]), (bonT, bonT[:]), ALU.add)
        k.tt("dve", (yrgT_all, yrgT_all[:, ti, :, :], ti), (tmp, tmp[:]), (gTs, gTs[:]), ALU.mult)

    rsc = k.dram("rw_scratch", [6, 128, RW], F32)
    ysc = k.dram("ry_scratch", [128, RW], F32)

    def rwkv_sample_core(xm, dec, kr2, kk, a_, tmp, stg, gTs, bonT):
        ti = NTP
        srcs = []
        srcs.append((xm, lambda c: xm[:, c, :]))
        srcs.append((dec, lambda c: dec[:, c, :]))
        srcs.append((kr2, lambda c: kr2[:, c, :]))
        srcs.append((xm, lambda c: xm[:, 8 + c, :]))
        for q in range(6):
            if q == 4:
                k.ts("dve", (tmp, tmp[:]), (kk, kk[:]), -1.0, None, op0=ALU.mult)
                sb_, fn = tmp, (lambda c: tmp[:, c, :])
            elif q == 5:
                k.tt("dve", (tmp, tmp[:]), (kk, kk[:]), (a_, a_[:]), ALU.mult)
                sb_, fn = tmp, (lambda c: tmp[:, c, :])
            else:
                sb_, fn = srcs[q]
            pb_ = pM[q % 2]
            for c in range(4):
                k.tr((pb_, pb_[:, c * 128:(c + 1) * 128]), (sb_, fn(c)), (identf, identf[:]))
            k.copy("act", (stg, stg[:]), (pb_, pb_[:, :]))
            k.dma("sp", rsc[q], stg[:], reads=[stg], writes=[(rsc, q)])
        for blk, (c0, n) in enumerate(((0, 512), (512, 512), (1024, 512), (1536, 256))):
            proj_tm(pR[blk % 2], pR[blk % 2][:, 0:n], C_R + c0, n)
            k.copy("act", (stg, stg[:, 0:n]), (pR[blk % 2], pR[blk % 2][:, 0:n]))
            for b in range(NSB):
                k.dma("sp", o_sshift[b:b + 1, c0:c0 + n], stg[ST * b + ST - 1:ST * b + ST, 0:n], reads=[stg], writes=[o_sshift])
        k.barrier()
        k.bot = mark_rw
        vec6 = k.sbuf([128, 6, ST, RN], F32, "vec6")
        Ssb = k.sbuf([128, RN, RN], F32, "Ssb")
        tmpS = k.sbuf([128, RN, RN], F32, "tmpS")
        sa = k.sbuf([128, RN], F32, "sa")
        ys = k.sbuf([128, ST, RN], F32, "ys")
        Ytm = k.sbuf([128, RW], F32, "Ytm")
        k.dma("sp", Ssb[:].rearrange("p a b -> p (a b)"), st_rS[:, :], reads=[st_rS], writes=[Ssb])
        for q in range(6):
            for b in range(NSB):
                k.dma("sp", vec6[RH * b:RH * b + RH, q, :, :], rsc[q, ST * b:ST * b + ST, :].rearrange("t (h j) -> h t j", h=RH),
                      reads=[(rsc, q)], writes=[(vec6, q)])
        HV = RN // 2

        def rec_gen(hf):
            i0 = hf * HV
            S_ = (Ssb, Ssb[:, i0:i0 + HV, :], hf)
            T_ = (tmpS, tmpS[:, i0:i0 + HV, :], hf)
            bc = lambda q, t: (vec6, vec6[:, q, t, :].unsqueeze(1).to_broadcast([128, HV, RN]), q)
            for t in range(ST):
                k.tt("dve", T_, S_, bc(4, t), ALU.mult)
                yield
                k.red("dve", (sa, sa[:, i0:i0 + HV], hf), T_, ALU.add)
                yield
                k.tt("pool", S_, S_, bc(1, t), ALU.mult)
                yield
                k.tt("dve", T_, (sa, sa[:, i0:i0 + HV].unsqueeze(2).to_broadcast([128, HV, RN]), hf), bc(5, t), ALU.mult)
                yield
                k.tt("dve", S_, S_, T_, ALU.add)
                yield
                k.tt("pool", T_, (vec6, vec6[:, 3, t, i0:i0 + HV].unsqueeze(2).to_broadcast([128, HV, RN]), 3), bc(2, t), ALU.mult)
                yield
                k.tt("dve", S_, S_, T_, ALU.add)
                yield
                k.tt("pool", T_, S_, bc(0, t), ALU.mult)
                yield
                k.red("dve", (ys, ys[:, t, i0:i0 + HV], (hf, t)), T_, ALU.add)
                yield

        gens_ = [rec_gen(0), rec_gen(1)]
        while gens_:
            for g_ in list(gens_):
                try:
                    next(g_)
                except StopIteration:
                    gens_.remove(g_)
        k.dma("sp", o_sS[:, :], Ssb[:].rearrange("p a b -> p (a b)"), reads=[Ssb], writes=[o_sS])
        k.dma("sp", ysc[:, :], ys[:].rearrange("p a b -> p (a b)"), reads=[ys], writes=[ysc])
        for b in range(NSB):
            k.dma("sp", Ytm[ST * b:ST * b + ST, :].rearrange("t (h i) -> t h i", h=RH),
                  ysc[RH * b:RH * b + RH, :].rearrange("h (t i) -> t h i", t=ST), reads=[ysc], writes=[Ytm])
        rwkv_epilogue(ti, Ytm, rt[7])

    def tail_rows(ti):
        if ti != NTP - 1:
            return
        for blk, (c0, n) in enumerate(((0, 512), (512, 512), (1024, 512), (1536, 256))):
            proj_tm(pR[blk % 2], pR[blk % 2][:, 0:n], C_R + c0, n)
            k.copy("act", (rt[1], rt[1][96:128, :, :].rearrange("p a b -> p (a b)")[:, 0:n]), (pR[blk % 2], pR[blk % 2][96:128, 0:n]))
            k.dma("sp", o_pshift[0:1, c0:c0 + n], rt[1][127:128, :, :].rearrange("p a b -> p (a b)")[:, 0:n], reads=[rt[1]], writes=[o_pshift])

    tiles_all = list(range(NT)) if stage >= 5 else list(range(NTP))

    def phase_1b(k):
        k.barrier()
        k.bot = mark_1a
        W_g = k.sbuf([128, 8, 2048], BF16, "W_g")
        W_bm = k.sbuf([128, 4, D], BF16, "W_bm")
        W_br = k.sbuf([128, 4, D], BF16, "W_br")
        W_out = k.sbuf([128, 8, D], BF16, "W_out")
        k.dma("pool", W_bm[:], d_w_bm[:], reads=[d_w_bm], writes=[W_bm])
        for kh in range(2):
            k.dma("pool", W_g[:, 4 * kh:4 * kh + 4, 0:1024], d_w_in[:, 4 * kh:4 * kh + 4, C_G:C_G + 1024], reads=[d_w_in], writes=[(W_g, "a")])
        k.dma("pool", W_br[:], d_w_br[:], reads=[d_w_br], writes=[W_br])
        for kh in range(2):
            k.dma("pool", W_g[:, 4 * kh:4 * kh + 4, 1024:2048], d_w_in[:, 4 * kh:4 * kh + 4, C_G + 1024:C_G + 2048], reads=[d_w_in], writes=[(W_g, "b")])
        k.dma("pool", W_out[:], d_w_out[:], reads=[d_w_out], writes=[W_out])
        xtb = [k.sbuf([128, D], F32, f"xtb{i}") for i in range(2)]
        hb2 = k.sbuf([128, D], BF16, "hb2")
        hT2 = k.sbuf([128, 8, 128], BF16, "hT2")
        ss2 = k.sbuf([128, 1], F32, "ss2")
        rs2 = k.sbuf([128, 1], F32, "rs2")
        sgb = [k.sbuf([128, 512], F32, f"sgb{i}") for i in range(2)]
        yab = k.sbuf([128, D], F32, "yab")
        mg = [k.sbuf([128, D], BF16, f"mg{i}") for i in range(2)]
        mT = k.sbuf([128, 8, 128], BF16, "mT")
        def head_gen(ti):
            xb = xtb[ti % 2]
            xd, xap = x_rows(ti)
            k.dma("sp", xb[:], xap, reads=[xd], writes=[xb])
            norm_generic(xb, g1bc, hb2, hT2, ss2, rs2)
            yield
            for half, (Wb, src, key) in enumerate(((W_bm, hmT_all, "a"), (W_br, yrgT_all, "b"))):
                for blk in range(2):
                    for kc in range(4):
                        k.mm((pR[blk], pR[blk][:, :]), (src, src[:, ti, kc, :], ti), (Wb, Wb[:, kc, blk * 512:(blk + 1) * 512]),
                             start=(kc == 0), stop=(kc == 3))
                    col = half * 1024 + blk * 512
                    for kc in range(8):
                        k.mm((pF[blk], pF[blk][:, :]), (hT2, hT2[:, kc, :]), (W_g, W_g[:, kc, col:col + 512], key),
                             start=(kc == 0), stop=(kc == 7))
                    k.act((sgb[blk], sgb[blk][:]), (pF[blk], pF[blk][:, :]), AF.Sigmoid)
                    if half == 0:
                        k.tt("dve", (yab, yab[:, blk * 512:(blk + 1) * 512]), (sgb[blk], sgb[blk][:]), (pR[blk], pR[blk][:, :]), ALU.mult)
                    else:
                        k.tt("dve", (sgb[blk], sgb[blk][:]), (sgb[blk], sgb[blk][:]), (pR[blk], pR[blk][:, :]), ALU.mult)
                        k.tt("dve", (mg[ti % 2], mg[ti % 2][:, blk * 512:(blk + 1) * 512]), (sgb[blk], sgb[blk][:]), (yab, yab[:, blk * 512:(blk + 1) * 512]), ALU.add)
                    yield

        def tailb_gen(ti):
            xb = xtb[ti % 2]
            mgt = mg[ti % 2]
            for kc in range(8):
                k.tr((pT, pT[:, kc * 128:(kc + 1) * 128]), (mgt, mgt[:, kc * 128:(kc + 1) * 128]), (identb, identb[:]))
            k.copy("act", (mT, mT[:].rearrange("p a b -> p (a b)")), (pT, pT[:, :]))
            yield
            for blk in range(2):
                for kc in range(8):
                    k.mm((pM[blk], pM[blk][:, :]), (mT, mT[:, kc, :]), (W_out, W_out[:, kc, blk * 512:(blk + 1) * 512]),
                         start=(kc == 0), stop=(kc == 7))
                k.tt("dve", (xb, xb[:, blk * 512:(blk + 1) * 512]), (xb, xb[:, blk * 512:(blk + 1) * 512]), (pM[blk], pM[blk][:, :]), ALU.add)
                yield
            k.dma("sp", x1s[ti * 128:(ti + 1) * 128, :], xb[:], reads=[xb], writes=[(x1s, ti)])

        def rr(gens):
            gens = list(gens)
            while gens:
                for g_ in list(gens):
                    try:
                        next(g_)
                    except StopIteration:
                        gens.remove(g_)

        tlb = list(tiles_all)
        rr([head_gen(tlb[0])])
        for idx, ti in enumerate(tlb):
            gl = [tailb_gen(ti)]
            if idx + 1 < len(tlb):
                gl.append(head_gen(tlb[idx + 1]))
            rr(gl)

    def norm_generic(xbuf, gbc, hb_, hT_, ss_, rs_):
        k.act((hb_, hb_[:]), (xbuf, xbuf[:]), AF.Square, accum=(ss_, ss_[:]))
        k.ts("dve", (rs_, rs_[:]), (ss_, ss_[:]), 1.0 / D, EPS, op0=ALU.mult, op1=ALU.add)
        k.act((rs_, rs_[:]), (rs_, rs_[:]), AF.Ln)
        k.act((rs_, rs_[:]), (rs_, rs_[:]), AF.Exp, scale=-0.5)
        k.stt("dve", (hb_, hb_[:]), (xbuf, xbuf[:]), (rs_, rs_[:, 0:1]), (gbc, gbc[:]), ALU.mult, ALU.mult)
        for kc in range(8):
            k.tr((pT, pT[:, kc * 128:(kc + 1) * 128]), (hb_, hb_[:, kc * 128:(kc + 1) * 128]), (identb, identb[:]))
        k.copy("act", (hT_, hT_[:].rearrange("p a b -> p (a b)")), (pT, pT[:, :]))

    def phase_2(k):
        k.barrier()
        k.bot = mark_phase
        F_up = k.sbuf([128, 8, 2 * DFF], BF16, "F_up")
        F_dn = k.sbuf([128, NFC, D], BF16, "F_dn")
        PGW = k.sbuf([128, 8, D], BF16, "PGW")
        PPJ = k.sbuf([128, 2, D], BF16, "PPJ")
        NG = 4
        CW = DFF // NG
        for g in range(NG):
            for part in range(2):
                k.dma("pool", F_up[:, :, part * DFF + g * CW:part * DFF + (g + 1) * CW], d_f_up[:, :, part * DFF + g * CW:part * DFF + (g + 1) * CW],
                      reads=[d_f_up], writes=[(F_up, g)])
        for g in range(2):
            k.dma("pool", F_dn[:, 11 * g:11 * g + 11, :], d_f_down[:, 11 * g:11 * g + 11, :], reads=[d_f_down], writes=[(F_dn, g)])
        k.dma("pool", PGW[:], d_pgw[:], reads=[d_pgw], writes=[PGW])
        k.dma("pool", PPJ[:], d_ppj[:], reads=[d_ppj], writes=[PPJ])
        g2bc = k.sbuf([128, D], F32, "g2bc")
        g3bc = k.sbuf([128, D], F32, "g3bc")
        g4bc = k.sbuf([128, D], F32, "g4bc")
        fct = k.sbuf([128, 4 * NFC], F32, "fct")
        k.dma("sp", g2bc[:], d_g2[:], reads=[d_g2], writes=[g2bc])
        k.dma("sp", g3bc[:], d_g3[:], reads=[d_g3], writes=[g3bc])
        k.dma("sp", g4bc[:], d_g4[:], reads=[d_g4], writes=[g4bc])
        k.dma("sp", fct[:], d_fctab[:], reads=[d_fctab], writes=[fct])
        fcol = lambda c: (fct, fct[:, c:c + 1])
        xq = [k.sbuf([128, D], F32, f"xq{i}") for i in range(2)]
        hb3 = k.sbuf([128, D], BF16, "hb3")
        hT3s = [k.sbuf([128, 8, 128], BF16, f"hT3a{i}") for i in range(2)]
        ss3 = k.sbuf([128, 1], F32, "ss3")
        rs3 = k.sbuf([128, 1], F32, "rs3")
        gT = k.sbuf([128, NFC, 128], BF16, "gT")
        cf = k.sbuf([128, NFC, 2], F32, "cf")
        GS = 4
        EXW = NSB * (ST + 2)
        ex4 = [k.sbuf([128, GS, EXW], F32, f"ex4_{i}") for i in range(2)]
        cc4 = [k.sbuf([128, GS, 128], F32, f"cc4_{i}") for i in range(2)]
        t14 = [k.sbuf([128, GS, 128], F32, "t14_0")] * 2
        up4 = [k.sbuf([128, GS, 128], F32, f"up4_{i}") for i in range(2)]
        sg3 = [k.sbuf([128, 512], F32, "sg30")] * 2
        ppt = k.sbuf([128, PLE], F32, "ppt")
        ppb = k.sbuf([128, PLE], BF16, "ppb")
        peT = k.sbuf([128, 2, 128], BF16, "peT")
        utm = sg3[0]
        k.memset("pool", (cf, cf[:]), 0.0)
        cfs = k.sbuf([128, NFC, 2 * NSB], F32, "cfs")
        GC = 1.5957691216057308
        BLK6 = ((0, 512), (512, 512), (1024, 512), (1536, 512), (2048, 512), (2560, 256))
        groups = [list(range(g0, min(g0 + GS, NFC))) for g0 in range(0, NFC, GS)]
        gbank = [pF[0], pF[1]]
        ubank = [pM[0], pM[1]]

        def fup_w(col):
            g = col // CW
            g_hi = (col + 127) // CW
            return g, g_hi

        def stage_A(ti, gi, hT3):
            smp = ti == NTP
            p = gi % 2
            chunks = groups[gi]
            n = len(chunks)
            c0 = chunks[0]
            for part, bank in ((0, gbank[p]), (1, ubank[p])):
                for ci, c in enumerate(chunks):
                    g, g_hi = fup_w(c * 128)
                    for kc in range(8):
                        k.mm((bank, bank[:, ci * 128:(ci + 1) * 128]), (F_up, F_up[:, kc, part * DFF + c * 128:part * DFF + (c + 1) * 128], g),
                             (hT3, hT3[:, kc, :]), start=(kc == 0), stop=(kc == 7))
                        if g_hi != g and g_hi in F_up.subs and F_up.subs[g_hi].w is not None:
                            k.streams["pe"][-1].deps.add(F_up.subs[g_hi].w)
            ex = ex4[p]
            if not smp:
                k.copy("pool", (ex, ex[:, 0:n, 0:2]), (cf, cf[:, c0:c0 + n, :]))
                k.copy("act", (ex, ex[:, 0:n, 2:130]), (gbank[p], gbank[p][:, 0:n * 128].rearrange("p (c t) -> p c t", c=n)))
                k.copy("pool", (cf, cf[:, c0:c0 + n, :]), (ex, ex[:, 0:n, 128:130]))
            else:
                exs = ex[:, 0:n, :].rearrange("p c (b t) -> p c b t", t=ST + 2)
                k.copy("pool", (ex, exs[:, :, :, 0:2]), (cfs, cfs[:, c0:c0 + n, :].rearrange("p c (b j) -> p c b j", j=2)))
                k.copy("act", (ex, exs[:, :, :, 2:ST + 2]), (gbank[p], gbank[p][:, 0:n * 128].rearrange("p (c b t) -> p c b t", c=n, b=NSB)))
            k.copy("act", (up4[p], up4[p][:, 0:n, :]), (ubank[p], ubank[p][:, 0:n * 128].rearrange("p (c t) -> p c t", c=n)))
            for ci, c in enumerate(chunks):
                if not smp:
                    tap = lambda j: ex[:, ci, j:j + 128]
                    ccv = cc4[p][:, ci, :]
                else:
                    e3 = ex[:, ci, :].rearrange("p (b t) -> p b t", t=ST + 2)
                    tap = lambda j, e3=e3: e3[:, :, j:j + ST]
                    ccv = cc4[p][:, ci, :].rearrange("p (b t) -> p b t", t=ST)
                cb = cc4[p]
                k.ts("dve", (cb, ccv), (ex, tap(2)), fcol(2 * NFC + c), fcol(3 * NFC + c), op0=ALU.mult, op1=ALU.add)
                k.stt("dve", (cb, ccv), (ex, tap(1)), fcol(1 * NFC + c), (cb, ccv), ALU.mult, ALU.add)
                k.stt("dve", (cb, ccv), (ex, tap(0)), fcol(0 * NFC + c), (cb, ccv), ALU.mult, ALU.add)

        def stage_B(ti, gi):
            p = gi % 2
            chunks = groups[gi]
            n = len(chunks)
            c0 = chunks[0]
            cb = (cc4[p], cc4[p][:, 0:n, :])
            ta = (t14[p], t14[p][:, 0:n, :])
            k.tt("dve", ta, cb, cb, ALU.mult)
            k.ts("dve", ta, ta, 0.044715, 1.0, op0=ALU.mult, op1=ALU.add)
            k.tt("dve", ta, ta, cb, ALU.mult)
            k.act(ta, ta, AF.Sigmoid, scale=GC)
            k.tt("dve", ta, ta, cb, ALU.mult)
            k.tt("dve", (gT, gT[:, c0:c0 + n, :], ("g", gi)), ta, (up4[p], up4[p][:, 0:n, :]), ALU.mult)

        def load_norm2(ti):
            xb = xq[ti % 2]
            k.dma("sp", xb[:], x1s[ti * 128:(ti + 1) * 128, :], reads=[(x1s, ti)], writes=[xb])
            if ti == NTP:
                for b6, (c0, n) in enumerate(BLK6):
                    k.dma("sp", utm[0:2 * NSB, 0:n], st_fconv[:, c0:c0 + n], reads=[st_fconv], writes=[utm])
                    nch = n // 128
                    for ci in range(nch):
                        k.tr((pR[b6 % 2], pR[b6 % 2][:, ci * 32:(ci + 1) * 32]), (utm, utm[0:2 * NSB, ci * 128:(ci + 1) * 128]),
                             (identf, identf[0:2 * NSB, 0:2 * NSB]))
                    k.copy("act", (cfs, cfs[:, 4 * b6:4 * b6 + nch, :]), (pR[b6 % 2], pR[b6 % 2][:, 0:nch * 32].rearrange("p (c x) -> p c x", c=nch)))
            norm_generic(xb, g2bc, hb3, hT3s[ti % 2], ss3, rs3)

        hb3b = hb3

        def groups_gen(ti):
            hT3 = hT3s[ti % 2]
            for gi in range(len(groups) + 1):
                if gi < len(groups):
                    stage_A(ti, gi, hT3)
                    yield
                if gi >= 1:
                    stage_B(ti, gi - 1)
                    yield

        def tail_gen(ti):
            smp = ti == NTP
            xb = xq[ti % 2]
            hT3 = hT3s[ti % 2]
            for blk in range(2):
                for c in range(NFC):
                    k.mm((pR[blk], pR[blk][:, :]), (gT, gT[:, c, :], ("g", c // GS)), (F_dn, F_dn[:, c, blk * 512:(blk + 1) * 512], c // 11),
                         start=(c == 0), stop=(c == NFC - 1))
                k.tt("dve", (xb, xb[:, blk * 512:(blk + 1) * 512]), (xb, xb[:, blk * 512:(blk + 1) * 512]), (pR[blk], pR[blk][:, :]), ALU.add)
                yield
            if ti == NTP - 1 or smp:
                for b6, (c0, n) in enumerate(BLK6):
                    for kc in range(8):
                        k.mm((pM[2], pM[2][:, 0:n]), (hT3, hT3[:, kc, :]), (F_up, F_up[:, kc, c0:c0 + n]),
                             start=(kc == 0), stop=(kc == 7))
                    if not smp:
                        k.copy("act", (utm, utm[96:128, 0:n]), (pM[2], pM[2][96:128, 0:n]))
                        k.dma("sp", o_pfconv[:, c0:c0 + n], utm[126:128, 0:n], reads=[utm], writes=[o_pfconv])
                    else:
                        k.copy("act", (utm, utm[:, 0:n]), (pM[2], pM[2][:, 0:n]))
                        for b in range(NSB):
                            k.dma("sp", o_sfconv[2 * b:2 * b + 2, c0:c0 + n], utm[ST * b + ST - 2:ST * b + ST, 0:n], reads=[utm], writes=[o_sfconv])
                    yield
            norm_generic(xb, g3bc, hb3b, hT3, ss3, rs3)
            yield
            pd, pap = (pp, pp[ti * 128:(ti + 1) * 128, :]) if not smp else (psm, psm[:, :])
            k.dma("sp", ppt[:], pap, reads=[pd], writes=[ppt])
            k.copy("act", (ppb, ppb[:]), (ppt, ppt[:]))
            for kc in range(2):
                k.tr((pT, pT[:, kc * 128:(kc + 1) * 128]), (ppb, ppb[:, kc * 128:(kc + 1) * 128]), (identb, identb[:]))
            k.copy("act", (peT, peT[:].rearrange("p a b -> p (a b)")), (pT, pT[:, 0:256]))
            yield
            for blk in range(2):
                for kc in range(8):
                    k.mm((pR[blk], pR[blk][:, :]), (hT3, hT3[:, kc, :]), (PGW, PGW[:, kc, blk * 512:(blk + 1) * 512]),
                         start=(kc == 0), stop=(kc == 7))
                yield
                k.act((sg3[blk], sg3[blk][:]), (pR[blk], pR[blk][:, :]), AF.Sigmoid)
                for kc in range(2):
                    k.mm((pM[2], pM[2][:, :]), (peT, peT[:, kc, :]), (PPJ, PPJ[:, kc, blk * 512:(blk + 1) * 512]),
                         start=(kc == 0), stop=(kc == 1))
                k.tt("dve", (sg3[blk], sg3[blk][:]), (sg3[blk], sg3[blk][:]), (pM[2], pM[2][:, :]), ALU.mult)
                k.tt("dve", (xb, xb[:, blk * 512:(blk + 1) * 512]), (xb, xb[:, blk * 512:(blk + 1) * 512]), (sg3[blk], sg3[blk][:]), ALU.add)
                yield
            k.act((hb3b, hb3b[:]), (xb, xb[:]), AF.Square, accum=(ss3, ss3[:]))
            k.ts("dve", (rs3, rs3[:]), (ss3, ss3[:]), 1.0 / D, EPS, op0=ALU.mult, op1=ALU.add)
            k.act((rs3, rs3[:]), (rs3, rs3[:]), AF.Ln)
            k.act((rs3, rs3[:]), (rs3, rs3[:]), AF.Exp, scale=-0.5)
            yield
            k.stt("dve", (xb, xb[:]), (xb, xb[:]), (rs3, rs3[:, 0:1]), (g4bc, g4bc[:]), ALU.mult, ALU.mult)
            if not smp:
                k.dma("sp", y_p[ti * 128:(ti + 1) * 128, :], xb[:], reads=[xb], writes=[y_p])
            else:
                k.dma("sp", y_s[:, :], xb[:], reads=[xb], writes=[y_s])
            if ti in nxt2:
                yield
                load_norm2(nxt2[ti])

        def run_rr(gens):
            gens = list(gens)
            while gens:
                for g_ in list(gens):
                    try:
                        next(g_)
                    except StopIteration:
                        gens.remove(g_)

        tl = list(tiles_all)
        nxt2 = {tl[i]: tl[i + 2] for i in range(len(tl) - 2)}
        load_norm2(tl[0])
        if len(tl) > 1:
            load_norm2(tl[1])
        run_rr([groups_gen(tl[0])])
        for idx, ti in enumerate(tl):
            tg = tail_gen(ti)
            if idx + 1 < len(tl):
                gg = groups_gen(tl[idx + 1])
                next(gg)
                next(gg)
                next(tg)
                next(tg)
                run_rr([gg, tg])
            else:
                run_rr([tg])

    _nt_dbg = int(_os.environ.get("KDBG_NT", "0"))
    tiles_run = (tiles_all if not _nt_dbg else list(range(_nt_dbg)))
    for ti in tiles_run:
        mixer_tile(ti)
        if stage >= 2:
            rwkv_tile(ti)
            tail_rows(ti)

    if stage >= 3:
        phase_1b(k)
    if stage >= 4:
        phase_2(k)
    k.emit()
    k.stats["sbuf_hiwater"] = k.hiwater
    k.stats["arena_bytes"] = k.arena_bytes
    return nc, k


def _chunk_rows(w, nk):
    return np.ascontiguousarray(w.reshape(nk, 128, w.shape[1]).transpose(1, 0, 2))


def _pcols(v, nc_):
    return v.reshape(nc_, 128).T


_PROG = {}


def _get_prog(stage=99, dbg=False):
    key = (stage, dbg)
    if key not in _PROG:
        _PROG[key] = build_program(stage, dbg)
    return _PROG[key]


def make_in_maps(inp):
    f = lambda a: np.ascontiguousarray(np.asarray(a, dtype=np.float32))
    ptab = np.zeros((128, 128), np.float32)
    mcw = f(inp["m_conv_w"])[0]
    for j in range(4):
        ptab[:, j * 8:(j + 1) * 8] = _pcols(mcw[j], 8)
    ptab[:, 32:40] = _pcols(f(inp["m_conv_b"])[0], 8)
    ptab[:, 40:54] = _pcols(f(inp["r_mix"])[0], 14)
    ptab[:, 54:58] = _pcols(f(inp["r_w0"])[0], 4)
    ptab[:, 58:62] = _pcols(f(inp["r_a0"])[0], 4)
    ptab[:, 62:66] = _pcols(f(inp["r_kk"])[0], 4)
    ptab[:, 66:70] = _pcols(f(inp["r_ka"])[0], 4)
    ptab[:, 70:74] = _pcols(f(inp["r_rk"])[0].reshape(-1), 4)
    ptab[:, 74:78] = _pcols(f(inp["r_ln_g"])[0], 4)
    ptab[:, 78:82] = _pcols(f(inp["r_ln_b"])[0], 4)
    ptab[:, 82:86] = _pcols(f(inp["m_norm_g"])[0], 4)
    fct = np.zeros((128, 4 * NFC), np.float32)
    fcw = f(inp["f_conv_w"])[0]
    for j in range(3):
        fct[:, j * NFC:(j + 1) * NFC] = _pcols(fcw[j], NFC)
    fct[:, 3 * NFC:4 * NFC] = _pcols(f(inp["f_conv_b"])[0], NFC)
    gbias = np.stack([f(inp["m_i_bias"])[0], f(inp["m_f_bias"])[0]], axis=1)
    ra2 = np.zeros((128, RW), np.float32)
    ra2[64:128] = f(inp["r_a2"])[0]
    bc = lambda v: np.ascontiguousarray(np.broadcast_to(f(v).reshape(1, D), (128, D)))
    shared = {
        "w_in": _chunk_rows(f(inp["w_in"])[0], 8),
        "w_bm": _chunk_rows(f(inp["w_branch_m"])[0], 4),
        "w_br": _chunk_rows(f(inp["w_branch_r"])[0], 4),
        "w_out": _chunk_rows(f(inp["w_out"])[0], 8),
        "f_up": _chunk_rows(f(inp["f_up"])[0], 8),
        "f_down": _chunk_rows(f(inp["f_down"])[0], NFC),
        "ple_gate_w": _chunk_rows(f(inp["ple_gate_w"])[0], 8),
        "ple_proj": _chunk_rows(f(inp["ple_proj"])[0], 2),
        "r_w2": f(inp["r_w2"])[0], "r_a2": ra2, "r_g2": f(inp["r_g2"])[0],
        "norm1_g": bc(inp["norm1_g"]), "norm2_g": bc(inp["norm2_g"]),
        "ple_norm_g": bc(inp["ple_norm_g"]), "final_norm_g": bc(inp["final_norm_g"]),
        "ptab": ptab, "fctab": fct, "gate_bias": np.ascontiguousarray(gbias),
    }
    maps = []
    for c in range(NCORES):
        sl = slice(c * NSB, (c + 1) * NSB)
        m = dict(shared)
        m["xp"] = f(inp["x_prompt"][c])
        m["xs"] = f(inp["x_sample"][sl]).reshape(128, D)
        m["pp"] = f(inp["p_prompt"][0, c])
        m["psm"] = f(inp["p_sample"][0, sl]).reshape(128, PLE)
        m["st_mconv"] = f(inp["state_mlstm_conv"][0, sl]).reshape(NSB * 3, 2 * MW)
        m["st_mC"] = f(inp["state_mlstm_C"][0, sl])
        m["st_mn"] = f(inp["state_mlstm_n"][0, sl])
        m["st_mm"] = f(inp["state_mlstm_m"][0, sl])
        m["st_rshift"] = f(inp["state_rwkv_shift"][0, sl])
        m["st_rS"] = f(inp["state_rwkv_S"][0, sl]).reshape(NSB * RH, RN * RN)
        m["st_fconv"] = f(inp["state_ffn_conv"][0, sl]).reshape(NSB * 2, DFF)
        maps.append(m)
    return maps


def assemble(results):
    g = lambda name: [np.asarray(r[name], dtype=np.float32) for r in results]
    y_p = np.stack(g("y_p"), 0)
    y_s = np.concatenate([a.reshape(NSB, ST, D) for a in g("y_s")], 0)
    p_conv = np.stack(g("p_conv"), 0)[None]
    p_C = np.stack(g("p_C"), 0)[None]
    p_n = np.stack(g("p_n"), 0)[None]
    p_m = np.stack([a.reshape(MH) for a in g("p_m")], 0)[None]
    p_shift = np.stack([a.reshape(RCOLS) for a in g("p_shift")], 0)[None]
    p_S = np.stack([a.reshape(RH, RN, RN) for a in g("p_S")], 0)[None]
    p_fconv = np.stack(g("p_fconv"), 0)[None]
    s_conv = np.concatenate([a.reshape(NSB, 3, 2 * MW) for a in g("s_conv")], 0)[None]
    s_C = np.concatenate(g("s_C"), 0)[None]
    s_n = np.concatenate(g("s_n"), 0)[None]
    s_m = np.concatenate(g("s_m"), 0)[None]
    s_shift = np.concatenate(g("s_shift"), 0)[None]
    s_S = np.concatenate([a.reshape(NSB, RH, RN, RN) for a in g("s_S")], 0)[None]
    s_fconv = np.concatenate([a.reshape(NSB, 2, DFF) for a in g("s_fconv")], 0)[None]
    return (y_p, y_s, p_conv, p_C, p_n, p_m, p_shift, p_S, p_fconv,
            s_conv, s_C, s_n, s_m, s_shift, s_S, s_fconv)


def kernel(**inputs):
    nc, _ = _get_prog()
    maps = make_in_maps(inputs)
    res = run_bass_kernel_spmd(nc, maps, core_ids=list(range(NCORES)))
    return assemble(res.results)
```

```python
import math
from contextlib import ExitStack

import numpy as np
import concourse.bass as bass
import concourse.mybir as mybir
from concourse.bass_utils import run_bass_kernel_spmd

F32 = mybir.dt.float32
BF16 = mybir.dt.bfloat16
AF = mybir.ActivationFunctionType
ALU = mybir.AluOpType
AX = mybir.AxisListType

ENGS = ("pe", "act", "dve", "pool", "sp")
N_DMA_SEMS = 8
SAME_ENG_DIST = 2

D = 1024
SEQ = 2048
NCORES = 8
NTP = SEQ // 128
NSB = 16
ST = 8
MW = 512
MH = 4
RW = 512
RH = 8
RN = 64
RCOLS = 1792
DFF = 2816
NFC = DFF // 128
PLE = 256
N_IN = 5896
C_QK, C_V, C_O, C_I, C_F, C_R, C_G = 0, 1024, 1536, 2048, 2052, 2056, 3848
EPS = 1e-6
GN_EPS = 64e-5
KSCALE = 128 ** -0.5
WSCALE = -math.exp(-0.5)


class _Trk:
    __slots__ = ("w", "r")

    def __init__(self):
        self.w = None
        self.r = []


class Buf:
    def __init__(self, t, name):
        self.t = t
        self.name = name
        self.whole = _Trk()
        self.subs = {}

    def __getitem__(self, idx):
        return self.t[idx]

    def view(self, ap, name=None):
        b = Buf(ap, name or self.name + "_v")
        b.whole = self.whole
        b.subs = self.subs
        return b


class _Op:
    __slots__ = ("eng", "fn", "deps", "needs_inc", "is_dma", "sem", "val", "pos", "force")


class K:
    def __init__(self, nc):
        self.nc = nc
        self.es = ExitStack()
        self.streams = {e: [] for e in ENGS}
        self.dma_rr = {e: 0 for e in ENGS}
        self.dma_last = {}
        self.nbuf = 0
        self.ops = []

    def _init_arena(self):
        nbytes = (int(self.nc.sbuf_bytes_remaining) - 512) // 64 * 64
        self.arena_bytes = nbytes
        self.arena = self.es.enter_context(self.nc.sbuf_tensor("arena", [128, nbytes // 2], BF16))
        self.bot = 0
        self.top = nbytes
        self.hiwater = 0

    def _view(self, off, shape, dtype):
        n = 1
        for d in shape[1:]:
            n *= d
        esz = 4 if dtype == F32 else 2
        v = self.arena[:, off // 2:(off + n * esz) // 2]
        if dtype == F32:
            v = v.bitcast(F32)
        if len(shape) > 2:
            names = " ".join(f"d{i}" for i in range(len(shape) - 1))
            v = v.rearrange(f"p ({names}) -> p {names}", **{f"d{i}": shape[i + 1] for i in range(len(shape) - 1)})
        if shape[0] < 128:
            v = v[0:shape[0]]
        return v, n * esz

    def sbuf(self, shape, dtype, name=None, top=False):
        if not hasattr(self, "arena"):
            self._init_arena()
        self.nbuf += 1
        name = name or f"sb{self.nbuf}"
        n = 1
        for d in shape[1:]:
            n *= d
        nb = (n * (4 if dtype == F32 else 2) + 63) // 64 * 64
        if top:
            self.top -= nb
            off = self.top
        else:
            off = self.bot
            self.bot += nb
        assert self.bot <= self.top, f"SBUF arena overflow allocating {name}: bot={self.bot} top={self.top}"
        self.hiwater = max(self.hiwater, self.bot + (self.arena_bytes - self.top))
        v, _ = self._view(off, list(shape), dtype)
        return Buf(v, name)

    def pe_fence(self):
        st = self.streams["pe"]
        if not st:
            return
        last = st[-1]
        o = self.op("pe", lambda h: h.nop(), (), ())
        o.deps.add(last)
        o.force = {last}
        if getattr(self, "fence_mm", None) is not None:
            fb, fi = self.fence_mm
            self.tr((fb, fb[:, 0:128]), (fi, fi[:]), (fi, fi[:]))
            last = self.streams["pe"][-1]
            o = self.op("pe", lambda h: h.nop(), (), ())
            o.deps.add(last)
            o.force = {last}

    def barrier(self):
        lasts = [st[-1] for st in self.streams.values() if st]
        lasts += list(self.dma_last.values())
        for e in ENGS:
            o = self.op(e, lambda h: h.nop(), (), ())
            o.deps.update(x for x in lasts if x is not o)

    def psum(self, shape, dtype, name=None):
        self.nbuf += 1
        name = "ps_" + (name or f"{self.nbuf}")
        t = self.es.enter_context(self.nc.psum_tensor(name, list(shape), dtype))
        return Buf(t, name)

    def dram(self, name, shape, dtype, kind="Internal"):
        t = self.nc.dram_tensor(name, list(shape), dtype, kind=kind)
        return Buf(t.ap(), name)

    def _touch(self, op, item, is_write):
        if isinstance(item, tuple):
            buf, key = item
        else:
            buf, key = item, None
        if key is None:
            trks = [buf.whole] + list(buf.subs.values())
        else:
            if key not in buf.subs:
                buf.subs[key] = _Trk()
            trks = [buf.whole, buf.subs[key]]
        for t in trks:
            if t.w is not None:
                op.deps.add(t.w)
            if is_write:
                op.deps.update(t.r)
        return buf, key

    def _commit(self, op, buf, key, is_write):
        if key is None:
            if is_write:
                buf.whole.w = op
                buf.whole.r = []
                buf.subs.clear()
            else:
                self._add_reader(buf.whole, op)
        else:
            t = buf.subs[key]
            if is_write:
                t.w = op
                t.r = []
            else:
                self._add_reader(t, op)

    @staticmethod
    def _add_reader(t, op):
        if not op.is_dma:
            t.r = [o for o in t.r if o.is_dma or o.eng != op.eng]
        t.r.append(op)

    def op(self, eng, fn, reads=(), writes=(), dma=False):
        o = _Op()
        o.eng = eng
        o.fn = fn
        o.deps = set()
        o.needs_inc = False
        o.is_dma = dma
        o.sem = None
        o.val = None
        o.force = None
        touched = []
        for it in reads:
            touched.append(self._touch(o, it, False) + (False,))
        for it in writes:
            touched.append(self._touch(o, it, True) + (True,))
        o.deps.discard(o)
        for buf, key, w in touched:
            self._commit(o, buf, key, w)
        if dma:
            kk = (eng, self.dma_rr[eng] % N_DMA_SEMS)
            self.dma_rr[eng] += 1
            prev = self.dma_last.get(kk)
            if prev is not None:
                o.deps.add(prev)
            self.dma_last[kk] = o
            o.sem = kk
            o.needs_inc = True
        o.pos = len(self.streams[eng])
        self.streams[eng].append(o)
        self.ops.append(o)
        return o

    def dma(self, eng, out, in_, reads=(), writes=(), **kw):
        return self.op(eng, lambda e: e.dma_start(out=out, in_=in_, **kw), reads, writes, dma=True)

    def emit(self):
        nc = self.nc
        for o in self.ops:
            real = []
            for d in o.deps:
                if (not d.is_dma) and (not o.is_dma) and d.eng == o.eng and o.eng == "pe":
                    if not (o.force and d in o.force):
                        continue
                d.needs_inc = True
                real.append(d)
            o.deps = real
        for e in ENGS:
            cs = [o for o in self.streams[e] if not o.is_dma]
            if cs:
                cs[-1].needs_inc = True
        es = self.es
        esem = {e: es.enter_context(nc.semaphore(f"s_{e}")) for e in ENGS}
        dsem = {}
        for e in ENGS:
            for i in range(min(N_DMA_SEMS, self.dma_rr[e])):
                dsem[(e, i)] = es.enter_context(nc.semaphore(f"d_{e}{i}"))
        dcount = {kk: 0 for kk in dsem}
        for e in ENGS:
            c = 0
            for o in self.streams[e]:
                if o.is_dma:
                    dcount[o.sem] += 16
                    o.val = dcount[o.sem]
                    o.sem = dsem[o.sem]
                elif o.needs_inc:
                    c += 1
                    o.val = c
                    o.sem = esem[e]
        final_waits = [(s, dcount[kk]) for kk, s in dsem.items() if dcount[kk] > 0]
        for e in ENGS:
            if e == "sp":
                continue
            cs = [o for o in self.streams[e] if not o.is_dma and o.needs_inc]
            if cs:
                final_waits.append((esem[e], cs[-1].val))
        streams = self.streams
        nwaits = [0]

        def run(e, handle):
            waited = {}
            for o in streams[e]:
                need = {}
                for d in o.deps:
                    if need.get(d.sem, (None, 0))[1] < d.val:
                        need[d.sem] = (d.sem, d.val)
                for s, v in need.values():
                    if waited.get(s, 0) >= v:
                        continue
                    handle.wait_ge(s, v)
                    nwaits[0] += 1
                    waited[s] = v
                ins = o.fn(handle)
                if o.is_dma:
                    ins.then_inc(o.sem, 16)
                elif o.needs_inc:
                    ins.then_inc(o.sem, 1)
            if e == "sp":
                for s, v in final_waits:
                    handle.wait_ge(s, v)

        with nc.Block() as block:
            @block.tensor
            def _(h):
                run("pe", h)

            @block.scalar
            def _(h):
                run("act", h)

            @block.vector
            def _(h):
                run("dve", h)

            @block.gpsimd
            def _(h):
                run("pool", h)

            @block.sync
            def _(h):
                run("sp", h)
        self.stats = dict(n_ops={e: len(streams[e]) for e in ENGS}, n_waits=nwaits[0])
        self.es.close()

    @staticmethod
    def _it(x):
        return (x[0], x[2]) if len(x) > 2 else x[0]

    def mm(self, out, lhsT, rhs, start=True, stop=True):
        return self.op("pe", lambda e: e.matmul(out[1], lhsT=lhsT[1], rhs=rhs[1], start=start, stop=stop),
                       reads=[self._it(lhsT), self._it(rhs)], writes=[self._it(out)])

    def tr(self, out, in_, ident):
        return self.op("pe", lambda e: e.transpose(out[1], in_[1], ident[1]),
                       reads=[self._it(in_), self._it(ident)], writes=[self._it(out)])

    def act(self, out, in_, func, bias=None, scale=None, accum=None, eng="act"):
        reads = [self._it(in_)]
        kw = {}
        if bias is not None:
            if isinstance(bias, tuple):
                reads.append(self._it(bias))
                kw["bias"] = bias[1]
            else:
                kw["bias"] = bias
        if scale is not None:
            if isinstance(scale, tuple):
                reads.append(self._it(scale))
                kw["scale"] = scale[1]
            else:
                kw["scale"] = scale
        writes = [self._it(out)]
        if accum is not None:
            writes.append(self._it(accum))
            kw["accum_out"] = accum[1]
        return self.op(eng, lambda e: e.activation(out=out[1], in_=in_[1], func=func, **kw), reads, writes)

    def tt(self, eng, out, in0, in1, op):
        return self.op(eng, lambda e: e.tensor_tensor(out=out[1], in0=in0[1], in1=in1[1], op=op),
                       reads=[self._it(in0), self._it(in1)], writes=[self._it(out)])

    def ts(self, eng, out, in0, s1, s2=None, op0=ALU.mult, op1=None, accum=None):
        reads = [self._it(in0)]
        a1 = s1
        a2 = s2
        if isinstance(s1, tuple):
            reads.append(self._it(s1))
            a1 = s1[1]
        if isinstance(s2, tuple):
            reads.append(self._it(s2))
            a2 = s2[1]
        kw = {}
        if op1 is not None:
            kw["op1"] = op1
        writes = [self._it(out)]
        if accum is not None:
            writes.append(self._it(accum))
            kw["accum_out"] = accum[1]
        return self.op(eng, lambda e: e.tensor_scalar(out=out[1], in0=in0[1], scalar1=a1, scalar2=a2, op0=op0, **kw),
                       reads, writes)

    def stt(self, eng, out, in0, scalar, in1, op0, op1):
        reads = [self._it(in0), self._it(in1)]
        a = scalar
        if isinstance(scalar, tuple):
            reads.append(self._it(scalar))
            a = scalar[1]
        return self.op(eng, lambda e: e.scalar_tensor_tensor(out=out[1], in0=in0[1], scalar=a, in1=in1[1], op0=op0, op1=op1),
                       reads, [self._it(out)])

    def copy(self, eng, out, in_):
        if eng == "act":
            return self.op(eng, lambda e: e.activation(out=out[1], in_=in_[1], func=AF.Copy),
                           reads=[self._it(in_)], writes=[self._it(out)])
        return self.op(eng, lambda e: e.tensor_copy(out=out[1], in_=in_[1]),
                       reads=[self._it(in_)], writes=[self._it(out)])

    def red(self, eng, out, in_, op, axis=AX.X):
        return self.op(eng, lambda e: e.tensor_reduce(out=out[1], in_=in_[1], axis=axis, op=op),
                       reads=[self._it(in_)], writes=[self._it(out)])

    def memset(self, eng, out, val):
        return self.op(eng, lambda e: e.memset(out[1], val), reads=[], writes=[self._it(out)])

    def scan(self, eng, out, d0, d1, init, op0, op1):
        return self.op(eng, lambda e: e.tensor_tensor_scan(out=out[1], data0=d0[1], data1=d1[1], initial=init, op0=op0, op1=op1),
                       reads=[self._it(d0), self._it(d1)], writes=[self._it(out)])


def build_program(stage=99, dbg=False):
    import os as _os
    nc = bass.Bass("TRN2", target_bir_lowering=False)
    k = K(nc)
    NT = NTP + 1

    def din(name, shape):
        return k.dram(name, shape, F32, "ExternalInput")

    def dout(name, shape):
        return k.dram(name, shape, F32, "ExternalOutput")

    xp = din("xp", [SEQ, D]); xs = din("xs", [128, D])
    pp = din("pp", [SEQ, PLE]); psm = din("psm", [128, PLE])
    st_mconv = din("st_mconv", [NSB * 3, 2 * MW])
    st_mC = din("st_mC", [NSB, MH, 128, 128])
    st_mn = din("st_mn", [NSB, MH, 128])
    st_mm = din("st_mm", [NSB, MH])
    st_rshift = din("st_rshift", [NSB, RCOLS])
    st_rS = din("st_rS", [NSB * RH, RN * RN])
    st_fconv = din("st_fconv", [NSB * 2, DFF])
    d_w_in = din("w_in", [128, 8, N_IN])
    d_w_bm = din("w_bm", [128, 4, D]); d_w_br = din("w_br", [128, 4, D])
    d_w_out = din("w_out", [128, 8, D])
    d_f_up = din("f_up", [128, 8, 2 * DFF]); d_f_down = din("f_down", [128, NFC, D])
    d_pgw = din("ple_gate_w", [128, 8, D]); d_ppj = din("ple_proj", [128, 2, D])
    d_rw2 = din("r_w2", [64, RW]); d_ra2 = din("r_a2", [128, RW]); d_rg2 = din("r_g2", [128, RW])
    d_g1 = din("norm1_g", [128, D]); d_g2 = din("norm2_g", [128, D])
    d_g3 = din("ple_norm_g", [128, D]); d_g4 = din("final_norm_g", [128, D])
    d_ptab = din("ptab", [128, 128])
    d_fctab = din("fctab", [128, 4 * NFC])
    d_gb = din("gate_bias", [4, 2])
    y_p = dout("y_p", [SEQ, D]); y_s = dout("y_s", [128, D])
    o_pconv = dout("p_conv", [3, 2 * MW]); o_pC = dout("p_C", [MH, 128, 128]); o_pn = dout("p_n", [MH, 128])
    o_pm = dout("p_m", [1, MH]); o_pshift = dout("p_shift", [1, RCOLS]); o_pS = dout("p_S", [RH * RN, RN])
    o_pfconv = dout("p_fconv", [2, DFF])
    o_sconv = dout("s_conv", [NSB * 3, 2 * MW]); o_sC = dout("s_C", [NSB, MH, 128, 128]); o_sn = dout("s_n", [NSB, MH, 128])
    o_sm = dout("s_m", [NSB, MH]); o_sshift = dout("s_shift", [NSB, RCOLS]); o_sS = dout("s_S", [NSB * RH, RN * RN])
    o_sfconv = dout("s_fconv", [NSB * 2, DFF])
    x1s = k.dram("x1_scratch", [NT * 128, D], F32)
    dbgs = {}

    def dbg_out(name, src_buf, src_ap, shape):
        if not dbg:
            return
        t = dout("dbg_" + name, shape)
        dbgs[name] = t
        k.dma("sp", t[:], src_ap, reads=[src_buf], writes=[t])

    identf = k.sbuf([128, 128], F32, "identf")
    identb = k.sbuf([128, 128], BF16, "identb")
    mark_phase = k.bot
    mU_in = [k.sbuf([128, 128], F32, f"mUin{i}") for i in range(2)]
    mU_st = [k.sbuf([128, 128], F32, f"mUst{i}") for i in range(2)]
    mL_st = [k.sbuf([128, 128], F32, f"mLst{i}") for i in range(2)]
    resets = [k.sbuf([128, 512], F32, f"resets{i}") for i in range(2)]
    ones4 = k.sbuf([4, 128], F32, "ones4")

    def aff(out_buf, out_ap, pattern, cm, base, op=ALU.is_ge):
        k.op("pool", lambda e: e.affine_select(out=out_ap, in_=out_ap, pattern=pattern, compare_op=op,
                                               fill=0.0, base=base, channel_multiplier=cm),
             reads=[out_buf], writes=[out_buf])

    k.memset("pool", (identf, identf[:]), 1.0)
    aff(identf, identf[:], [[-1, 128]], 1, 0)
    aff(identf, identf[:], [[1, 128]], -1, 0)
    k.copy("pool", (identb, identb[:]), (identf, identf[:]))
    for i in range(2):
        k.memset("pool", (mU_in[i], mU_in[i][:]), 1.0)
        aff(mU_in[i], mU_in[i][:], [[1, 128]], -1, 0)
        k.memset("pool", (mU_st[i], mU_st[i][:]), 1.0)
        aff(mU_st[i], mU_st[i][:], [[1, 128]], -1, -1)
        k.memset("pool", (mL_st[i], mL_st[i][:]), 1.0)
        aff(mL_st[i], mL_st[i][:], [[-1, 128]], 1, -1)
        k.memset("pool", (resets[i], resets[i][:]), 1.0)
    v3 = lambda b: b[:].rearrange("p (a c) -> p a c", c=ST)
    aff(mU_in[1], v3(mU_in[1]), [[-ST, 16], [0, ST]], 1, 0)
    aff(mU_st[1], v3(mU_st[1]), [[-ST, 16], [0, ST]], 1, 0)
    aff(mL_st[1], v3(mL_st[1]), [[ST, 16], [0, ST]], -1, ST - 1)
    k.memset("pool", (resets[0], resets[0][:].rearrange("p (a c) -> p a c", c=128)[:, :, 0:1]), 0.0)
    k.memset("pool", (resets[1], resets[1][:].rearrange("p (a c) -> p a c", c=ST)[:, :, 0:1]), 0.0)
    k.memset("pool", (ones4, ones4[:]), 1.0)
    mask2 = [k.sbuf([128, 256], F32, f"mask2_{i}") for i in range(2)]
    for i in range(2):
        k.copy("pool", (mask2[i], mask2[i][:, 0:128]), (mU_st[i], mU_st[i][:]))
        k.copy("pool", (mask2[i], mask2[i][:, 128:256]), (mU_in[i], mU_in[i][:]))
    I2 = k.sbuf([128, 64], F32, "I2")
    k.tt("pool", (I2, I2[:]), (identf, identf[:, 0:64]), (identf, identf[:, 64:128]), ALU.add)
    bones = k.sbuf([128, 128], F32, "bones")
    k.memset("pool", (bones, bones[:]), 0.0)
    k.memset("pool", (bones, bones[0:64, 0:64]), 1.0)
    k.memset("pool", (bones, bones[64:128, 64:128]), 1.0)

    ptab = k.sbuf([128, 128], F32, "ptab")
    k.dma("sp", ptab[:], d_ptab[:], reads=[d_ptab], writes=[ptab])
    PT_MCW, PT_MCB, PT_RMIX, PT_RW0, PT_RA0, PT_RKK, PT_RKA, PT_RRK, PT_RLNG, PT_RLNB, PT_MNG = 0, 32, 40, 54, 58, 62, 66, 70, 74, 78, 82
    pcol = lambda c: (ptab, ptab[:, c:c + 1])
    gb = k.sbuf([4, 2], F32, "gb")
    k.dma("sp", gb[:], d_gb[:], reads=[d_gb], writes=[gb])
    nbf = k.sbuf([4, 1], F32, "nbf")
    k.ts("dve", (nbf, nbf[:]), (gb, gb[:, 1:2]), -1.0, None, op0=ALU.mult)
    g1bc = k.sbuf([128, D], F32, "g1bc")
    k.dma("sp", g1bc[:], d_g1[:], reads=[d_g1], writes=[g1bc])

    NA = C_G
    hmT_all = k.sbuf([128, NT, 4, 128], BF16, "hmT_all")
    yrgT_all = k.sbuf([128, NT, 4, 128], BF16, "yrgT_all")
    mark_1a = k.bot
    W_in = k.sbuf([128, 8, NA], BF16, "W_in")
    Wl_w2 = k.sbuf([64, RW], BF16, "Wl_w2")
    Wl_a2 = k.sbuf([128, RW], BF16, "Wl_a2")
    Wl_g2 = k.sbuf([128, RW], BF16, "Wl_g2")
    GRP = {"g0": (0, 1024), "g1": (1024, 2056), "g2": (2056, 3848)}
    for g in ("g0", "g1", "g2"):
        a, b = GRP[g]
        for kh in range(2):
            k.dma("pool", W_in[:, 4 * kh:4 * kh + 4, a:b], d_w_in[:, 4 * kh:4 * kh + 4, a:b], reads=[d_w_in], writes=[(W_in, g)])
        if g == "g1":
            k.dma("pool", Wl_w2[:], d_rw2[:], reads=[d_rw2], writes=[Wl_w2])
            k.dma("pool", Wl_a2[:], d_ra2[:], reads=[d_ra2], writes=[Wl_a2])
            k.dma("pool", Wl_g2[:], d_rg2[:], reads=[d_rg2], writes=[Wl_g2])

    def wgrp(col):
        for g, (a, b) in GRP.items():
            if a <= col < b:
                return g

    pF = [k.psum([128, 512], F32, f"pF{i}") for i in range(2)]
    pR = [k.psum([128, 512], F32, f"pR{i}") for i in range(2)]
    pT = k.psum([128, 1024], BF16, "pT")
    pM = [k.psum([128, 512], F32, f"pM{i}") for i in range(3)]

    xt = [k.sbuf([128, D], F32, "xt0")] * 2
    hb = k.sbuf([128, D], BF16, "hb")
    hT = k.sbuf([128, 8, 128], BF16, "hT")
    ss = k.sbuf([128, 1], F32, "ss")
    rs = k.sbuf([128, 1], F32, "rs")
    ext_q = k.sbuf([128, 8, 131], F32, "ext_q")
    cq = k.sbuf([128, 8, 3], F32, "cq")
    _eqf = ext_q[:].rearrange("p a b -> p (a b)")
    cv = k.sbuf([128, 8, 128], F32, "cv")
    qkT = k.sbuf([128, 8, 128], BF16, "qkT")
    soT = k.sbuf([128, 4, 128], F32, "soT")
    vaug = k.sbuf([128, 4, 130], BF16, "vaug")
    Cst = k.sbuf([128, 4, 129], F32, "Cst")
    Cb = k.sbuf([128, 4, 130], BF16, "Cb")
    gsm = [k.sbuf([4, 128], F32, f"gsm{i}") for i in range(8)]
    gpk = k.sbuf([4, 3, 128], F32, "gpk")
    mst = k.sbuf([4, 16], F32, "mst")
    mnew = k.sbuf([4, 16], F32, "mnew")
    gt = [k.sbuf([4, 16], F32, f"gt{i}") for i in range(4)]
    s0d = k.sbuf([4, 4, 16], F32, "s0d")
    tokS = k.sbuf([128, 12], F32, "tokS")
    s0bc = k.sbuf([128, 64], F32, "s0bc")
    _pk = _eqf[:, 512:1024].bitcast(BF16).rearrange("p (a b c) -> p a b c", a=2, b=4)
    PTm = ext_q.view(_pk[:, 0, :, :], "PTm")
    ktm = ext_q.view(_pk[:, 1, :, :], "ktm")
    dn = k.sbuf([128, 4], F32, "dn")
    hm = ext_q.view(_eqf[:, 0:512].rearrange("p (a b) -> p a b", a=4), "hm")
    hn = hm
    bst = k.sbuf([128, 4, 6], F32, "bst")
    bag = k.sbuf([128, 4, 2], F32, "bag")
    zq_tm = cv.view(cv[:].rearrange("p a b -> p (a b)"), "zq_tm")

    ext_r = k.sbuf([128, 14, 129], F32, "ext_r")
    cr = k.sbuf([128, 14, 1], F32, "cr")
    _erf = ext_r[:].rearrange("p a b -> p (a b)")
    xm = k.sbuf([128, 14, 128], F32, "xm")
    thad = k.sbuf([128, 128], BF16, "thad")
    sgd = k.sbuf([128, 128], BF16, "sgd")
    bst8 = k.sbuf([128, 8, 6], F32, "bst8")
    bag8 = k.sbuf([128, 8, 2], F32, "bag8")
    mark_rw = k.bot
    rt = [k.sbuf([128, 4, 128], F32, f"rt{i}") for i in range(7)]
    rt.append(cv.view(cv[:, 0:4, :], "rt7"))
    rt.append(cv.view(cv[:, 4:8, :], "rt8"))
    gTs = ext_r.view(_erf[:, 0:512].rearrange("p (a b) -> p a b", a=4), "gTs")
    bonT = ext_r.view(_erf[:, 512:1024].rearrange("p (a b) -> p a b", a=4), "bonT")
    ART = k.sbuf([128, 4, 2, 128], BF16, "ART")
    BTb = k.sbuf([128, 4, 128], BF16, "BTb")
    KTb = k.sbuf([128, 4, 128], BF16, "KTb")
    VTb = k.sbuf([128, 4, 128], BF16, "VTb")
    AB_tm = k.sbuf([128, 2, 512], BF16, "AB_tm")
    KV_tm = k.sbuf([128, 2, 512], BF16, "KV_tm")
    GBm = k.sbuf([128, 4, 256], BF16, "GBm")
    GKm = k.sbuf([128, 4, 256], BF16, "GKm")
    Nn = k.sbuf([128, 4, 128], BF16, "Nn")
    GBm_b = k.sbuf([128, 4, 256], BF16, "GBm_b")
    GKm_b = k.sbuf([128, 4, 256], BF16, "GKm_b")
    Nn_b = k.sbuf([128, 4, 128], BF16, "Nn_b")
    PP_b = [k.sbuf([128, 4, 256], BF16, f"PPb{i}") for i in range(2)]
    XX_b = [k.sbuf([128, 4, 128], BF16, f"XXb{i}") for i in range(2)]
    PP = [k.sbuf([128, 4, 256], BF16, f"PP{i}") for i in range(2)]
    XX = [k.sbuf([128, 4, 128], BF16, f"XX{i}") for i in range(2)]
    QT = k.sbuf([128, 2, 128], BF16, "QT")
    IE = k.sbuf([128, 4, 64], F32, "IE")
    STf = k.sbuf([128, 4, 64], F32, "STf")
    STb = k.sbuf([128, 4, 64], BF16, "STb")
    yn = ext_r.view(_erf[:, 1024:1536].rearrange("p (a b) -> p a b", a=8), "yn")
    k.memset("pool", (STf, STf[:]), 0.0)
    k.memset("pool", (STb, STb[:]), 0.0)
    k.memset("pool", (cr, cr[:]), 0.0)
    k.memset("pool", (vaug, vaug[:]), 1.0)
    k.memset("pool", (Cst, Cst[:]), 0.0)
    k.memset("pool", (Cb, Cb[:]), 0.0)
    k.memset("pool", (mst, mst[:]), 0.0)
    k.memset("pool", (cq, cq[:]), 0.0)
    LNK = math.log(KSCALE)

    def x_rows(ti):
        if ti < NTP:
            return xp, xp[ti * 128:(ti + 1) * 128, :]
        return xs, xs[:, :]

    def norm_to_hT(xbuf, gbc):
        k.act((hb, hb[:]), (xbuf, xbuf[:]), AF.Square, accum=(ss, ss[:]))
        k.ts("dve", (rs, rs[:]), (ss, ss[:]), 1.0 / D, EPS, op0=ALU.mult, op1=ALU.add)
        k.act((rs, rs[:]), (rs, rs[:]), AF.Ln)
        k.act((rs, rs[:]), (rs, rs[:]), AF.Exp, scale=-0.5)
        k.stt("dve", (hb, hb[:]), (xbuf, xbuf[:]), (rs, rs[:, 0:1]), (gbc, gbc[:]), ALU.mult, ALU.mult)
        for kc in range(8):
            k.tr((pT, pT[:, kc * 128:(kc + 1) * 128]), (hb, hb[:, kc * 128:(kc + 1) * 128]), (identb, identb[:]))
        k.copy("act", (hT, hT[:].rearrange("p a b -> p (a b)")), (pT, pT[:, :]))

    def proj_fm(ps, ps_ap, col, M=128):
        g = wgrp(col)
        for kc in range(8):
            k.mm((ps, ps_ap), (W_in, W_in[:, kc, col:col + M], g), (hT, hT[:, kc, :]), start=(kc == 0), stop=(kc == 7))

    def proj_tm(ps, ps_ap, col, N):
        g = wgrp(col)
        for kc in range(8):
            k.mm((ps, ps_ap), (hT, hT[:, kc, :]), (W_in, W_in[:, kc, col:col + N], g), start=(kc == 0), stop=(kc == 7))

    prefetched = set()

    def prefetch_gen(tn):
        xb = xt[tn % 2]
        xd, xap = x_rows(tn)
        k.dma("sp", xb[:], xap, reads=[xd], writes=[xb])
        norm_to_hT(xb, g1bc)
        yield
        k.copy("dve", (ext_q, ext_q[:, :, 0:3]), (cq, cq[:]))
        for g in range(2):
            for c in range(4):
                proj_fm(pM[2], pM[2][:, c * 128:(c + 1) * 128], C_QK + (4 * g + c) * 128)
                if c == 1:
                    yield
            k.copy("act", (ext_q, ext_q[:, 4 * g:4 * g + 4, 3:131]), (pM[2], pM[2][:].rearrange("p (c t) -> p c t", c=4)))
            yield
        k.copy("dve", (cq, cq[:]), (ext_q, ext_q[:, :, 128:131]))
        yield
        proj_tm(pM[2], pM[2][:, :], C_V, 512)
        k.copy("act", (vaug, vaug[:, :, 0:128]), (pM[2], pM[2][:].rearrange("p (h c) -> p h c", h=4)))
        yield
        for c in range(4):
            proj_fm(pM[2], pM[2][:, c * 128:(c + 1) * 128], C_O + c * 128)
            if c == 1:
                yield
        k.act((soT, soT[:].rearrange("p a b -> p (a b)")), (pM[2], pM[2][:, :]), AF.Sigmoid)

    def mixer_tile(ti):
        smp = ti == NTP
        mi = 1 if smp else 0
        NB = NSB if smp else 1
        LB = ST if smp else 128
        xb = xt[ti % 2]
        xd, xap = x_rows(ti)
        if smp:
            k.barrier()
            k.bot = mark_rw
            Cs = k.sbuf([128, NSB, 129], F32, "Cs")
            Csb = k.sbuf([128, NSB, 130], BF16, "Csb")
            qTm = k.sbuf([128, NSB, 128], BF16, "qTm")
            ktmb = k.sbuf([128, NSB, 128], BF16, "ktmb")
            blkF = k.sbuf([128, NSB, 128], BF16, "blkF")
            rowm = k.sbuf([128, NSB], F32, "rowm")
            k.memset("pool", (blkF, blkF[:]), 1.0)
            aff(blkF, blkF[:], [[-ST, NSB], [1, 128]], 0, 0)
            aff(blkF, blkF[:], [[ST, NSB], [-1, 128]], 0, ST - 1)
            k.memset("pool", (rowm, rowm[:]), 1.0)
            aff(rowm, rowm[:], [[-ST, NSB]], 1, 0)
            aff(rowm, rowm[:], [[ST, NSB]], -1, ST - 1)
            smc = cv.view(cv[:].rearrange("p a b -> p (a b)")[0:NSB * 3, :], "smc")
            ext_s = xm.view(xm[:].rearrange("p a b -> p (a b)")[:, 0:8 * NSB * 11].rearrange("p (c b t) -> p c b t", c=8, b=NSB), "ext_s")
            k.dma("sp", smc[:], st_mconv[:, :], reads=[st_mconv], writes=[smc])
            for c in range(8):
                k.tr((pM[0], pM[0][:, c * 48:(c + 1) * 48]), (smc, smc[:, c * 128:(c + 1) * 128]), (identf, identf[0:48, 0:48]))
            k.copy("act", (ext_s, ext_s[:, :, :, 0:3]), (pM[0], pM[0][:, 0:384].rearrange("p (c b j) -> p c b j", c=8, b=NSB)))
            k.dma("sp", mst[:, 0:NSB], st_mm[:, :].rearrange("b h -> h b"), reads=[st_mm], writes=[mst], allow_slow_non_contiguous=True)
        if ti not in prefetched:
            k.dma("sp", xb[:], xap, reads=[xd], writes=[xb])
            norm_to_hT(xb, g1bc)

        if smp:
            for g in range(2):
                for c in range(4):
                    proj_fm(pF[g], pF[g][:, c * 128:(c + 1) * 128], C_QK + (4 * g + c) * 128)
                k.copy("act", (ext_s, ext_s[:, 4 * g:4 * g + 4, :, 3:11]), (pF[g], pF[g][:].rearrange("p (c b t) -> p c b t", c=4, b=NSB)))
            for c in range(8):
                cvv = cv[:, c, :].rearrange("p (b t) -> p b t", t=ST)
                k.ts("dve", (cv, cvv), (ext_s, ext_s[:, c, :, 3:11]), pcol(PT_MCW + 3 * 8 + c), pcol(PT_MCB + c),
                     op0=ALU.mult, op1=ALU.add)
                for j in range(3):
                    k.stt("dve", (cv, cvv), (ext_s, ext_s[:, c, :, j:j + ST]), pcol(PT_MCW + j * 8 + c), (cv, cvv),
                          ALU.mult, ALU.add)
        if not smp:
            if ti not in prefetched:
                k.copy("dve", (ext_q, ext_q[:, :, 0:3]), (cq, cq[:]))
                for g in range(2):
                    for c in range(4):
                        proj_fm(pF[g], pF[g][:, c * 128:(c + 1) * 128], C_QK + (4 * g + c) * 128)
                    k.copy("act", (ext_q, ext_q[:, 4 * g:4 * g + 4, 3:131]), (pF[g], pF[g][:].rearrange("p (c t) -> p c t", c=4)))
                k.copy("dve", (cq, cq[:]), (ext_q, ext_q[:, :, 128:131]))
            for c in range(8):
                k.ts("dve", (cv, cv[:, c, :]), (ext_q, ext_q[:, c, 3:131]), pcol(PT_MCW + 3 * 8 + c), pcol(PT_MCB + c),
                     op0=ALU.mult, op1=ALU.add)
                for j in range(3):
                    k.stt("dve", (cv, cv[:, c, :]), (ext_q, ext_q[:, c, j:j + 128]), pcol(PT_MCW + j * 8 + c), (cv, cv[:, c, :]),
                          ALU.mult, ALU.add)
        k.act((qkT, qkT[:].rearrange("p a b -> p (a b)")), (cv, cv[:].rearrange("p a b -> p (a b)")), AF.Silu)

        if ti not in prefetched:
            proj_tm(pR[0], pR[0][:, :], C_V, 512)
            k.copy("act", (vaug, vaug[:, :, 0:128]), (pR[0], pR[0][:].rearrange("p (h c) -> p h c", h=4)))
            for c in range(4):
                proj_fm(pF[0], pF[0][:, c * 128:(c + 1) * 128], C_O + c * 128)
            k.act((soT, soT[:].rearrange("p a b -> p (a b)")), (pF[0], pF[0][:, :]), AF.Sigmoid)
        if not smp and stage >= 2:
            rwkv_front_proj(ti)
        proj_fm(pM[0], pM[0][0:4, 0:128], C_I, M=4)
        proj_fm(pM[0], pM[0][0:4, 128:256], C_F, M=4)
        liT, nlf, ncum, gT_, t0, t1 = gsm[0], gsm[1], gsm[2], gsm[3], gsm[4], gsm[5]
        k.ts("dve", (liT, liT[:]), (pM[0], pM[0][0:4, 0:128]), (gb, gb[:, 0:1]), None, op0=ALU.add)
        k.act((t0, t0[:]), (pM[0], pM[0][0:4, 128:256]), AF.Exp, bias=(nbf, nbf[:, 0:1]), scale=-1.0)
        k.act((nlf, nlf[:]), (t0, t0[:]), AF.Ln, bias=1.0)
        k.scan("dve", (ncum, ncum[:]), (resets[mi], resets[mi][0:4, 0:128]), (nlf, nlf[:]), 0.0, ALU.mult, ALU.add)
        k.tt("dve", (gT_, gT_[:]), (liT, liT[:]), (ncum, ncum[:]), ALU.add)
        b3 = lambda buf: buf[:].rearrange("p (b l) -> p b l", l=LB)
        mcb_ = mst[:, 0:NB].unsqueeze(2).to_broadcast([4, NB, LB])
        nlast = ncum[:].rearrange("p (b l) -> p b l", l=LB)[:, :, LB - 1:LB]
        k.stt("dve", (t0, b3(t0)), (gT_, b3(gT_)), LNK, (mst, mcb_), ALU.add, ALU.subtract)
        k.act((gpk, gpk[:, 0, :]), (t0, t0[:]), AF.Exp)
        k.tt("dve", (t1, b3(t1)), (ncum, b3(ncum)), (mst, mcb_), ALU.subtract)
        k.act((gpk, gpk[:, 1, :]), (t1, t1[:]), AF.Exp)
        k.tt("dve", (t1, b3(t1)), (gT_, b3(gT_)), (ncum, nlast.to_broadcast([4, NB, LB])), ALU.subtract)
        k.red("dve", (gt[0], gt[0][:, 0:NB]), (t1, b3(t1)), ALU.max)
        k.tt("dve", (gt[1], gt[1][:, 0:NB]), (mst, mst[:, 0:NB]), (ncum, nlast.rearrange("p b o -> p (b o)")), ALU.subtract)
        k.tt("dve", (mnew, mnew[:, 0:NB]), (gt[1], gt[1][:, 0:NB]), (gt[0], gt[0][:, 0:NB]), ALU.max)
        k.tt("dve", (gt[2], gt[2][:, 0:NB]), (gt[1], gt[1][:, 0:NB]), (mnew, mnew[:, 0:NB]), ALU.subtract)
        k.act((gt[3], gt[3][:, 0:NB]), (gt[2], gt[2][:, 0:NB]), AF.Exp)
        k.tt("dve", (gpk, gpk[:, 2, :].rearrange("p (b l) -> p b l", l=LB)), (gpk, gpk[:, 0, :].rearrange("p (b l) -> p b l", l=LB)),
             (gt[3], gt[3][:, 0:NB].unsqueeze(2).to_broadcast([4, NB, LB])), ALU.mult)
        for j in range(3):
            k.tr((pM[1], pM[1][:, 4 * j:4 * j + 4]), (gpk, gpk[:, j, :]), (identf, identf[0:4, 0:4]))
        k.copy("dve", (tokS, tokS[:]), (pM[1], pM[1][:, 0:12]))
        k.tt("dve", (s0d, s0d[:, :, 0:NB]), (identf, identf[0:4, 0:4].unsqueeze(2).to_broadcast([4, 4, NB])),
             (gt[3], gt[3][:, 0:NB].unsqueeze(1).to_broadcast([4, 4, NB])), ALU.mult)
        k.mm((pM[1], pM[1][:, 16:16 + 4 * NB]), (ones4, ones4[:]), (s0d, s0d[:, :, 0:NB].rearrange("p a b -> p (a b)")))
        k.copy("dve", (s0bc, s0bc[:, 0:4 * NB]), (pM[1], pM[1][:, 16:16 + 4 * NB]))

        for h in range(4):
            k.mm((pM[0], pM[0][:, h * 128:(h + 1) * 128]), (qkT, qkT[:, 4 + h, :]), (qkT, qkT[:, h, :]))
        for h in range(4):
            k.stt("dve", (PTm, PTm[:, h, :]), (pM[0], pM[0][:, h * 128:(h + 1) * 128]), (tokS, tokS[:, h:h + 1]),
                  (mU_in[mi], mU_in[mi][:]), ALU.mult, ALU.mult)
        pO = [pM[1], pM[2]]
        oap = lambda h: pO[h // 2][:, 256 * (h % 2):256 * (h % 2) + 129]
        if not smp:
            for h in range(4):
                k.mm((pO[h // 2], oap(h)), (qkT, qkT[:, h, :]), (Cb, Cb[:, h, 0:129]), start=True, stop=False)
                k.mm((pO[h // 2], oap(h)), (PTm, PTm[:, h, :]), (vaug, vaug[:, h, 0:129]), start=False, stop=True)
        else:
            for h in range(4):
                k.tr((pT, pT[:, h * 128:(h + 1) * 128]), (qkT, qkT[:, 4 + h, :]), (identb, identb[:]))
            for h in range(4):
                k.ts("dve", (ktm, ktm[:, h, :]), (pT, pT[:, h * 128:(h + 1) * 128]), (tokS, tokS[:, 8 + h:9 + h]), None, op0=ALU.mult)
            for h in range(4):
                k.dma("sp", Cs[:, :, 0:128], st_mC[:, h, :, :].rearrange("b d v -> d b v"), reads=[st_mC], writes=[Cs])
                k.dma("sp", Cs[:, :, 128], st_mn[:, h, :].rearrange("b d -> d b"), reads=[st_mn], writes=[Cs], allow_slow_non_contiguous=True)
                k.copy("act", (Csb, Csb[:, :, 0:129]), (Cs, Cs[:]))
                k.tt("dve", (qTm, qTm[:]), (qkT, qkT[:, h, :].unsqueeze(1).to_broadcast([128, NSB, 128])), (blkF, blkF[:]), ALU.mult)
                for b in range(NSB):
                    k.mm((pO[h // 2], oap(h)), (qTm, qTm[:, b, :]), (Csb, Csb[:, b, 0:129]), start=(b == 0), stop=False)
                k.mm((pO[h // 2], oap(h)), (PTm, PTm[:, h, :]), (vaug, vaug[:, h, 0:129]), start=False, stop=True)
                k.tt("dve", (ktmb, ktmb[:]), (ktm, ktm[:, h, :].unsqueeze(1).to_broadcast([128, NSB, 128])),
                     (rowm, rowm[:].unsqueeze(2).to_broadcast([128, NSB, 128])), ALU.mult)
                for grp in range(4):
                    bank = pF[grp % 2]
                    for bi in range(4):
                        b = 4 * grp + bi
                        k.mm((bank, bank[:, bi * 128:(bi + 1) * 128]), (ktmb, ktmb[:, b, :]), (vaug, vaug[:, h, 0:128]))
                    for bi in range(4):
                        b = 4 * grp + bi
                        k.stt("dve", (Cs, Cs[:, b, 0:128]), (Cs, Cs[:, b, 0:128]), (s0bc, s0bc[:, h * NSB + b:h * NSB + b + 1]),
                              (bank, bank[:, bi * 128:(bi + 1) * 128]), ALU.mult, ALU.add)
                for b in range(NSB):
                    k.mm((pR[0], pR[0][:, b:b + 1]), (ktmb, ktmb[:, b, :]), (vaug, vaug[:, h, 128:129]))
                k.tt("dve", (Cs, Cs[:, :, 128]), (Cs, Cs[:, :, 128]), (s0bc, s0bc[:, h * NSB:(h + 1) * NSB]), ALU.mult)
                k.tt("dve", (Cs, Cs[:, :, 128]), (Cs, Cs[:, :, 128]), (pR[0], pR[0][:, 0:NSB]), ALU.add)
                k.dma("sp", o_sC[:, h, :, :].rearrange("b d v -> d b v"), Cs[:, :, 0:128], reads=[Cs], writes=[o_sC])
                k.dma("sp", o_sn[:, h, :].rearrange("b d -> d b"), Cs[:, :, 128], reads=[Cs], writes=[o_sn], allow_slow_non_contiguous=True)
        for h in range(4):
            k.copy("act", (dn, dn[:, h:h + 1]), (pO[h // 2], oap(h)[:, 128:129]))
        k.stt("dve", (dn, dn[:]), (dn, dn[:]), -1.0, (dn, dn[:]), ALU.mult, ALU.max)
        k.tt("dve", (dn, dn[:]), (dn, dn[:]), (tokS, tokS[:, 4:8]), ALU.max)
        k.op("dve", lambda e: e.reciprocal(out=dn[:], in_=dn[:]), reads=[dn], writes=[dn])
        for h in range(4):
            k.act((hm, hm[:, h, :]), (pO[h // 2], oap(h)[:, 0:128]), AF.Copy, scale=(dn, dn[:, h:h + 1]))
        for h in range(4):
            k.op("dve", lambda e, h=h: e.bn_stats(out=bst[:, h, :], in_=hm[:, h, :]), reads=[hm], writes=[(bst, h)])
        for h in range(4):
            k.op("dve", lambda e, h=h: e.bn_aggr(out=bag[:, h, :], in_=bst[:, h, :]), reads=[(bst, h)], writes=[(bag, h)])
        k.act((bag, bag[:, :, 1:2]), (bag, bag[:, :, 1:2]), AF.Ln, bias=EPS)
        k.act((bag, bag[:, :, 1:2]), (bag, bag[:, :, 1:2]), AF.Exp, scale=-0.5)
        for h in range(4):
            k.ts("dve", (hn, hn[:, h, :]), (hm, hm[:, h, :]), (bag, bag[:, h, 0:1]), (bag, bag[:, h, 1:2]),
                 op0=ALU.subtract, op1=ALU.mult)
        for h in range(4):
            k.tr((pM[0], pM[0][:, h * 128:(h + 1) * 128]), (hn, hn[:, h, :]), (identf, identf[:]))
        for h in range(4):
            k.stt("dve", (hmT_all, hmT_all[:, ti, h, :], ti), (pM[0], pM[0][:, h * 128:(h + 1) * 128]), pcol(PT_MNG + h),
                  (soT, soT[:, h, :]), ALU.mult, ALU.mult)
        if not smp:
            for h in range(4):
                k.tr((pT, pT[:, h * 128:(h + 1) * 128]), (qkT, qkT[:, 4 + h, :]), (identb, identb[:]))
            for h in range(4):
                k.ts("dve", (ktm, ktm[:, h, :]), (pT, pT[:, h * 128:(h + 1) * 128]), (tokS, tokS[:, 8 + h:9 + h]), None, op0=ALU.mult)
            for h in range(4):
                k.mm((pO[h // 2], oap(h)), (ktm, ktm[:, h, :]), (vaug, vaug[:, h, 0:129]))
            for h in range(4):
                k.stt("dve", (Cst, Cst[:, h, :]), (Cst, Cst[:, h, :]), (s0bc, s0bc[:, h:h + 1]), (pO[h // 2], oap(h)),
                      ALU.mult, ALU.add)
            k.copy("act", (Cb, Cb[:, :, 0:129]), (Cst, Cst[:]))
            k.copy("dve", (mst, mst[:, 0:1]), (mnew, mnew[:, 0:1]))
        if ti == NTP - 1:
            for h in range(4):
                k.dma("sp", o_pC[h], Cst[:, h, 0:128], reads=[Cst], writes=[o_pC])
            k.dma("sp", o_pn[:].rearrange("h d -> d h"), Cst[:, :, 128], reads=[Cst], writes=[o_pn], allow_slow_non_contiguous=True)
            k.dma("sp", o_pm[:].rearrange("o h -> h o"), mnew[:, 0:1], reads=[mnew], writes=[o_pm], allow_slow_non_contiguous=True)
            for blk in range(2):
                proj_tm(pR[blk], pR[blk][:, :], C_QK + blk * 512, 512)
                k.copy("act", (zq_tm, zq_tm[:, blk * 512:(blk + 1) * 512]), (pR[blk], pR[blk][:, :]))
            k.dma("sp", o_pconv[:], zq_tm[125:128, :], reads=[zq_tm], writes=[o_pconv])
        if smp:
            k.dma("sp", o_sm[:, :].rearrange("b h -> h b"), mnew[:, 0:NSB], reads=[mnew], writes=[o_sm], allow_slow_non_contiguous=True)
            for blk in range(2):
                proj_tm(pR[blk], pR[blk][:, :], C_QK + blk * 512, 512)
                k.copy("act", (zq_tm, zq_tm[:, blk * 512:(blk + 1) * 512]), (pR[blk], pR[blk][:, :]))
            for b in range(NSB):
                k.dma("sp", o_sconv[3 * b:3 * b + 3, :], zq_tm[ST * b + 5:ST * b + 8, :], reads=[zq_tm], writes=[o_sconv])

    k.fence_mm = (pT, identb)
    BK = [pM[0], pM[1], pF[0], pF[1], pR[0], pR[1]]

    _rw_stop = int(_os.environ.get("KDBG_RW", "99"))

    def rwkv_front_proj(ti):
        k.copy("dve", (ext_r, ext_r[:, :, 0:1]), (cr, cr[:]))
        for g in range(4):
            n = min(4, 14 - 4 * g)
            for c in range(n):
                proj_fm(pF[g % 2], pF[g % 2][:, c * 128:(c + 1) * 128], C_R + (4 * g + c) * 128)
            k.copy("act", (ext_r, ext_r[:, 4 * g:4 * g + n, 1:129]),
                   (pF[g % 2], pF[g % 2][:, 0:n * 128].rearrange("p (c t) -> p c t", c=n)))
        k.copy("dve", (cr, cr[:]), (ext_r, ext_r[:, :, 128:129]))
        k.tt("dve", (xm, xm[:]), (ext_r, ext_r[:, :, 0:128]), (ext_r, ext_r[:, :, 1:129]), ALU.subtract)
        for c in range(14):
            k.stt("dve", (xm, xm[:, c, :]), (xm, xm[:, c, :]), pcol(PT_RMIX + c), (ext_r, ext_r[:, c, 1:129]), ALU.mult, ALU.add)

    def rwkv_tile(ti):
        smp = ti == NTP
        mi = 0
        NLV = 7
        rtl = rt
        if not smp:
            pass
        else:
            k.barrier()
            k.bot = mark_rw
            ext_rs = k.sbuf([128, 14, NSB, ST + 1], F32, "ext_rs")
            rtl = [k.sbuf([128, 4, 128], F32, f"rts{i}") for i in range(7)] + [rt[7], rt[8]]
            stg = k.sbuf([128, 512], F32, "stg")
            srs = xm.view(xm[:].rearrange("p a b -> p (a b)")[0:NSB, :], "srs")
            k.dma("sp", srs[:], st_rshift[:, :], reads=[st_rshift], writes=[srs])
            for c in range(14):
                k.tr((pM[0], pM[0][:, c * NSB:(c + 1) * NSB]), (srs, srs[:, c * 128:(c + 1) * 128]), (identf, identf[0:NSB, 0:NSB]))
            k.copy("act", (ext_rs, ext_rs[:, :, :, 0]), (pM[0], pM[0][:, 0:14 * NSB].rearrange("p (c b) -> p c b", c=14)))
            for g in range(4):
                n = min(4, 14 - 4 * g)
                for c in range(n):
                    proj_fm(pF[g % 2], pF[g % 2][:, c * 128:(c + 1) * 128], C_R + (4 * g + c) * 128)
                k.copy("act", (ext_rs, ext_rs[:, 4 * g:4 * g + n, :, 1:ST + 1]),
                       (pF[g % 2], pF[g % 2][:, 0:n * 128].rearrange("p (c b t) -> p c b t", c=n, b=NSB)))
            xm4 = xm[:].rearrange("p c (b t) -> p c b t", t=ST)
            k.tt("pool", (xm, xm4), (ext_rs, ext_rs[:, :, :, 0:ST]), (ext_rs, ext_rs[:, :, :, 1:ST + 1]), ALU.subtract)
            for c in range(14):
                k.stt("dve", (xm, xm4[:, c]), (xm, xm4[:, c]), pcol(PT_RMIX + c), (ext_rs, ext_rs[:, c, :, 1:ST + 1]), ALU.mult, ALU.add)
        rT, krT, vrT = xm[:, 0:4, :], xm[:, 4:8, :], xm[:, 8:12, :]
        sig, cums, gam, ginv, gexc, a_, kk, tmp, kr2 = rtl
        if _rw_stop <= 1:
            return
        k.act((thad, thad[0:64, :]), (xm, xm[0:64, 12, :]), AF.Tanh)
        k.copy("act", (thad, thad[64:128, :]), (xm, xm[64:128, 12, :]))
        k.act((sgd, sgd[:]), (xm, xm[:, 13, :]), AF.Sigmoid)
        for c in range(4):
            k.mm((pM[0], pM[0][:, c * 128:(c + 1) * 128]), (Wl_w2, Wl_w2[0:64, c * 128:(c + 1) * 128]), (thad, thad[0:64, :]))
        for c in range(4):
            k.act((sig, sig[:, c, :]), (pM[0], pM[0][:, c * 128:(c + 1) * 128]), AF.Sigmoid, bias=pcol(PT_RW0 + c))
        k.pe_fence()
        for c in range(4):
            k.mm((pM[1], pM[1][:, c * 128:(c + 1) * 128]), (Wl_a2, Wl_a2[64:128, c * 128:(c + 1) * 128]), (thad, thad[64:128, :]))
        k.pe_fence()
        for c in range(4):
            k.act((a_, a_[:, c, :]), (pM[1], pM[1][:, c * 128:(c + 1) * 128]), AF.Sigmoid, bias=pcol(PT_RA0 + c))
        for c in range(4):
            k.mm((pM[2], pM[2][:, c * 128:(c + 1) * 128]), (Wl_g2, Wl_g2[:, c * 128:(c + 1) * 128]), (sgd, sgd[:]))
        k.copy("act", (gTs, gTs[:].rearrange("p a b -> p (a b)")), (pM[2], pM[2][:, :]))
        if _rw_stop <= 2:
            return
        fl = lambda b: b[:].rearrange("p a b -> p (a b)")
        if not smp:
            k.scan("dve", (cums, fl(cums)), (resets[mi], resets[mi][:]), (sig, fl(sig)), 0.0, ALU.mult, ALU.add)
            k.act((gam, fl(gam)), (cums, fl(cums)), AF.Exp, scale=WSCALE)
            k.act((ginv, fl(ginv)), (cums, fl(cums)), AF.Exp, scale=-WSCALE)
            k.tt("dve", (tmp, tmp[:]), (cums, cums[:]), (sig, sig[:]), ALU.subtract)
            k.act((gexc, fl(gexc)), (tmp, fl(tmp)), AF.Exp, scale=WSCALE)
        else:
            k.act((gam, fl(gam)), (sig, fl(sig)), AF.Exp, scale=WSCALE)
        if _rw_stop <= 3:
            return
        for c in range(4):
            k.ts("dve", (kk, kk[:, c, :]), (xm, xm[:, 4 + c, :]), pcol(PT_RKK + c), None, op0=ALU.mult)
        k.tt("dve", (tmp, tmp[:]), (kk, kk[:]), (kk, kk[:]), ALU.mult)
        for c in range(4):
            k.mm((pM[0], pM[0][:, c * 128:(c + 1) * 128]), (bones, bones[:]), (tmp, tmp[:, c, :]))
        k.ts("dve", (tmp, fl(tmp)), (pM[0], pM[0][:, :]), 1e-24, None, op0=ALU.max)
        k.act((tmp, fl(tmp)), (tmp, fl(tmp)), AF.Ln)
        k.act((tmp, fl(tmp)), (tmp, fl(tmp)), AF.Exp, scale=-0.5)
        k.tt("dve", (kk, kk[:]), (kk, kk[:]), (tmp, tmp[:]), ALU.mult)
        for c in range(4):
            k.ts("dve", (tmp, tmp[:, c, :]), (a_, a_[:, c, :]), -1.0, pcol(PT_RKA + c), op0=ALU.add, op1=ALU.mult)
        k.stt("dve", (kr2, kr2[:]), (tmp, tmp[:]), 1.0, (xm, krT), ALU.add, ALU.mult)
        k.tt("dve", (tmp, tmp[:]), (xm, rT), (kr2, kr2[:]), ALU.mult)
        for c in range(4):
            k.ts("dve", (tmp, tmp[:, c, :]), (tmp, tmp[:, c, :]), pcol(PT_RRK + c), None, op0=ALU.mult)
        for c in range(4):
            k.mm((pM[1], pM[1][:, c * 128:(c + 1) * 128]), (bones, bones[:]), (tmp, tmp[:, c, :]))
        k.tt("dve", (bonT, fl(bonT)), (pM[1], pM[1][:, :]), (xm, vrT.rearrange("p a b -> p (a b)") if False else xm[:, 8:12, :].rearrange("p a b -> p (a b)")), ALU.mult)
        if _rw_stop <= 4:
            return
        if smp:
            rwkv_sample_core(xm, gam, kr2, kk, a_, tmp, stg, gTs, bonT)
            return
        k.stt("dve", (ART, ART[:, :, 0, :]), (kk, kk[:]), -1.0, (gexc, gexc[:]), ALU.mult, ALU.mult)
        k.tt("dve", (ART, ART[:, :, 1, :]), (xm, rT), (gam, gam[:]), ALU.mult)
        k.tt("dve", (tmp, tmp[:]), (kk, kk[:]), (a_, a_[:]), ALU.mult)
        k.tt("dve", (BTb, BTb[:]), (tmp, tmp[:]), (ginv, ginv[:]), ALU.mult)
        k.tt("dve", (KTb, KTb[:]), (kr2, kr2[:]), (ginv, ginv[:]), ALU.mult)
        k.copy("act", (VTb, VTb[:]), (xm, vrT))
        if _rw_stop <= 5:
            return
        for c in range(4):
            k.tr((pT, pT[:, c * 128:(c + 1) * 128]), (ART, ART[:, c, 0, :]), (identb, identb[:]))
            k.tr((pT, pT[:, 512 + c * 128:512 + (c + 1) * 128]), (BTb, BTb[:, c, :]), (identb, identb[:]))
        k.copy("act", (AB_tm, AB_tm[:].rearrange("p a b -> p (a b)")), (pT, pT[:, :]))
        for c in range(4):
            k.tr((pT, pT[:, c * 128:(c + 1) * 128]), (KTb, KTb[:, c, :]), (identb, identb[:]))
            k.tr((pT, pT[:, 512 + c * 128:512 + (c + 1) * 128]), (VTb, VTb[:, c, :]), (identb, identb[:]))
        k.copy("dve", (KV_tm, KV_tm[:].rearrange("p a b -> p (a b)")), (pT, pT[:, :]))
        if _rw_stop <= 6:
            return
        A_tm = lambda h: (AB_tm, AB_tm[:, 0, h * 64:(h + 1) * 64])
        B_tm = lambda h: (AB_tm, AB_tm[:, 1, h * 64:(h + 1) * 64])
        K_tm = lambda h: (KV_tm, KV_tm[:, 0, h * 64:(h + 1) * 64])
        V_tm = lambda h: (KV_tm, KV_tm[:, 1, h * 64:(h + 1) * 64])
        m2b = mask2[mi][:].unsqueeze(1).to_broadcast([128, 2, 256])
        GB2, GK2, Nn2, PP2, XX2 = [GBm, GBm_b], [GKm, GKm_b], [Nn, Nn_b], [PP, PP_b], [XX, XX_b]
        LB3 = [[pM[0], pM[1], pR[0]], [pF[0], pF[1], pR[1]]]
        for g in range(2):
            GBm_, GKm_, Nn_, XX_ = GB2[g], GK2[g], Nn2[g], XX2[g]
            heads = [4 * g + i for i in range(4)]
            HO = [(pbs, [(i, h) for i, h in enumerate(heads) if 64 * (h % 2) == pbs]) for pbs in (0, 64)]
            for pbs, hl in HO:
                for i, h in hl:
                    c, pb = h // 2, 64 * (h % 2)
                    off = (i % 2) * 256
                    rAR = (ART, ART[pb:pb + 64, c, :, :].rearrange("p a t -> p (a t)"))
                    k.mm((BK[i // 2], BK[i // 2][:, off:off + 256]), (BTb, BTb[pb:pb + 64, c, :]), rAR)
                    k.mm((BK[2 + i // 2], BK[2 + i // 2][:, off:off + 256]), (KTb, KTb[pb:pb + 64, c, :]), rAR)
                    k.mm((BK[4], BK[4][:, i * 128:(i + 1) * 128]), (ART, ART[pb:pb + 64, c, 0, :]), (BTb, BTb[pb:pb + 64, c, :]))
                k.pe_fence()
            for hf in range(2):
                k.tt("dve", (GBm_, GBm_[:, 2 * hf:2 * hf + 2, :]), (BK[hf], BK[hf][:].rearrange("p (a b) -> p a b", a=2)), (mask2[mi], m2b), ALU.mult)
                k.tt("dve", (GKm_, GKm_[:, 2 * hf:2 * hf + 2, :]), (BK[2 + hf], BK[2 + hf][:].rearrange("p (a b) -> p a b", a=2)), (mask2[mi], m2b), ALU.mult)
            k.tt("dve", (Nn_, Nn_[:]), (BK[4], BK[4][:].rearrange("p (a b) -> p a b", a=4)),
                 (mL_st[mi], mL_st[mi][:].unsqueeze(1).to_broadcast([128, 4, 128])), ALU.mult)
            for i, h in enumerate(heads):
                k.mm((BK[5], BK[5][:, i * 64:(i + 1) * 64]), (GKm_, GKm_[:, i, 0:128]), V_tm(h))
            k.copy("act", (XX_[0], XX_[0][:, :, 64:128]), (BK[5], BK[5][:, 0:256].rearrange("p (a b) -> p a b", a=4)))
            k.copy("dve", (XX_[0], XX_[0][:, :, 0:64]), (AB_tm, AB_tm[:, 0, 256 * g:256 * g + 256].rearrange("p (a b) -> p a b", a=4)))

        xfinal = [None, None]

        def levels_gen(g):
            GBm_, Nn_, PP_, XX_ = GB2[g], Nn2[g], PP2[g], XX2[g]
            bP, bQ, bX = LB3[g]
            Pc = lambda i: (Nn_, Nn_[:, i, :])
            PTc = lambda i: (GBm_, GBm_[:, i, 0:128])
            xi = 0
            for lvl in range(NLV):
                Xc, Xn = XX_[xi], XX_[1 - xi]
                for i in range(4):
                    o = (bX, bX[:, i * 128:(i + 1) * 128])
                    k.mm(o, (identb, identb[:]), (Xc, Xc[:, i, :]), start=True, stop=False)
                    k.mm(o, PTc(i), (Xc, Xc[:, i, :]), start=False, stop=True)
                k.copy("act", (Xn, Xn[:].rearrange("p a b -> p (a b)")), (bX, bX[:, :]))
                xi = 1 - xi
                yield
                if lvl < NLV - 1:
                    bb = [bP, bQ]
                    for i in range(4):
                        off = (i % 2) * 256
                        if lvl < NLV - 2:
                            k.mm((bb[i // 2], bb[i // 2][:, off:off + 128]), PTc(i), Pc(i))
                        k.mm((bb[i // 2], bb[i // 2][:, off + 128:off + 256]), Pc(i), PTc(i))
                    PPn = PP_[lvl % 2]
                    for hf in range(2):
                        if lvl < NLV - 2:
                            k.copy("dve", (PPn, PPn[:, 2 * hf:2 * hf + 2, :]), (bb[hf], bb[hf][:].rearrange("p (a b) -> p a b", a=2)))
                        else:
                            k.copy("dve", (PPn, PPn[:, 2 * hf:2 * hf + 2, 128:256]),
                                   (bb[hf], bb[hf][:].rearrange("p (a b) -> p a b", a=2)[:, :, 128:256]))
                    Pc = lambda i, PPn=PPn: (PPn, PPn[:, i, 0:128])
                    PTc = lambda i, PPn=PPn: (PPn, PPn[:, i, 128:256])
                    yield
            xfinal[g] = XX_[xi]

        gens = [levels_gen(0), levels_gen(1)]
        if ti + 1 < NTP and (ti + 1) in tiles_run:
            gens.append(prefetch_gen(ti + 1))
            prefetched.add(ti + 1)
        while gens:
            for g_ in list(gens):
                try:
                    next(g_)
                except StopIteration:
                    gens.remove(g_)

        for g in range(2):
            heads = [4 * g + i for i in range(4)]
            HO = [(pbs, [(i, h) for i, h in enumerate(heads) if 64 * (h % 2) == pbs]) for pbs in (0, 64)]
            GBt, GKt = GB2[g], GK2[g]
            Xf = xfinal[g]
            if _rw_stop <= 8:
                continue
            k.pe_fence()
            for pbs, hl in HO:
                for i, h in hl:
                    c, pb = h // 2, 64 * (h % 2)
                    ci = i // 2
                    o = (BK[0], BK[0][pb:pb + 64, ci * 128:(ci + 1) * 128])
                    k.mm(o, (Xf, Xf[:, i, 0:64]), (GBt, GBt[:, i, 128:256]), start=True, stop=False)
                    k.pe_fence()
                    k.mm(o, (identb, identb[pb:pb + 64, pb:pb + 64]), (ART, ART[pb:pb + 64, c, 1, :]), start=False, stop=True)
                    k.pe_fence()
            k.copy("act", (QT, QT[:].rearrange("p a b -> p (a b)")), (BK[0], BK[0][:, 0:256]))
            for pbs, hl in HO:
                for i, h in hl:
                    c, pb = h // 2, 64 * (h % 2)
                    ci = i // 2
                    o = (pM[2], pM[2][:, h * 64:(h + 1) * 64])
                    k.mm(o, (QT, QT[pb:pb + 64, ci, :]), (STb, STb[pb:pb + 64, c, :]), start=True, stop=False)
                    k.pe_fence()
                    k.mm(o, (GBt, GBt[:, i, 128:256]), (Xf, Xf[:, i, 64:128]), start=False, stop=False)
                    k.mm(o, (GKt, GKt[:, i, 128:256]), V_tm(h), start=False, stop=True)
                    k.pe_fence()
            if _rw_stop <= 9:
                continue
            for pbs, hl in HO:
                for i, h in hl:
                    c, pb = h // 2, 64 * (h % 2)
                    ci = i // 2
                    k.mm((BK[1], BK[1][pb:pb + 64, ci * 64:(ci + 1) * 64]), (Xf, Xf[:, i, 0:64]), B_tm(h))
                k.pe_fence()
            k.tt("dve", (IE, IE[:, 2 * g:2 * g + 2, :]), (BK[1], BK[1][:, 0:128].rearrange("p (a b) -> p a b", a=2)),
                 (I2, I2[:].unsqueeze(1).to_broadcast([128, 2, 64])), ALU.add)
            for pbs, hl in HO:
                for i, h in hl:
                    c, pb = h // 2, 64 * (h % 2)
                    ci = i // 2
                    o = (BK[2], BK[2][pb:pb + 64, ci * 64:(ci + 1) * 64])
                    k.mm(o, (IE, IE[pb:pb + 64, c, :]), (STf, STf[pb:pb + 64, c, :]), start=True, stop=False)
                    k.pe_fence()
                    k.mm(o, B_tm(h), (Xf, Xf[:, i, 64:128]), start=False, stop=False)
                    k.mm(o, K_tm(h), V_tm(h), start=False, stop=True)
                    k.pe_fence()
            for ci in range(2):
                c = 2 * g + ci
                k.ts("dve", (STf, STf[:, c, :]), (BK[2], BK[2][:, ci * 64:(ci + 1) * 64]), (gam, gam[:, c, 127:128]), None, op0=ALU.mult)
            k.copy("act", (STb, STb[:, 2 * g:2 * g + 2, :]), (STf, STf[:, 2 * g:2 * g + 2, :]))
        if _rw_stop <= 10:
            return
        rwkv_epilogue(ti, pM[2], tmp)
        if ti == NTP - 1:
            for c in range(4):
                k.tr((pM[0], pM[0][0:64, c * 128:(c + 1) * 128]), (STf, STf[:, c, :]), (identf, identf[:]))
            k.copy("act", (rt[0], rt[0][0:64, :, :]), (pM[0], pM[0][0:64, :].rearrange("p (a b) -> p a b", a=4)))
            k.dma("sp", o_pS[:].rearrange("(h i) j -> i h j", h=8), rt[0][0:64, :, :].rearrange("p c (f j) -> p (c f) j", f=2),
                  reads=[rt[0]], writes=[o_pS])

    def rwkv_epilogue(ti, Yb, tmp):
        pM2 = [None, None, Yb]
        for h in range(8):
            k.op("dve", lambda e, h=h: e.bn_stats(out=bst8[:, h, :], in_=Yb[:, h * 64:(h + 1) * 64]), reads=[Yb], writes=[(bst8, h)])
        for h in range(8):
            k.op("dve", lambda e, h=h: e.bn_aggr(out=bag8[:, h, :], in_=bst8[:, h, :]), reads=[(bst8, h)], writes=[(bag8, h)])
        k.act((bag8, bag8[:, :, 1:2]), (bag8, bag8[:, :, 1:2]), AF.Ln, bias=GN_EPS)
        k.act((bag8, bag8[:, :, 1:2]), (bag8, bag8[:, :, 1:2]), AF.Exp, scale=-0.5)
        for h in range(8):
            k.ts("dve", (yn, yn[:, h, :]), (Yb, Yb[:, h * 64:(h + 1) * 64]), (bag8, bag8[:, h, 0:1]), (bag8, bag8[:, h, 1:2]),
                 op0=ALU.subtract, op1=ALU.mult)
        for c in range(4):
            k.tr((pM[0], pM[0][:, c * 128:(c + 1) * 128]), (yn, yn[:, 2 * c:2 * c + 2, :].rearrange("p a b -> p (a b)")), (identf, identf[:]))
        for c in range(4):
            k.ts("dve", (tmp, tmp[:, c, :]), (pM[0], pM[0][:, c * 128:(c + 1) * 128]), pcol(PT_RLNG + c), pcol(PT_RLNB + c),
                 op0=ALU.mult, op1=ALU.add)
        k.tt("dve", (tmp, tmp[:]), (tmp, tmp[:]), (bonT, bonT[:]), ALU.add)
        k.tt("dve", (yrgT_all, yrgT_all[:, ti, :, :], ti), (tmp, tmp[:]), (gTs, gTs[:]), ALU.mult)

    rsc = k.dram("rw_scratch", [6, 128, RW], F32)
    ysc = k.dram("ry_scratch", [128, RW], F32)

    def rwkv_sample_core(xm, dec, kr2, kk, a_, tmp, stg, gTs, bonT):
        ti = NTP
        srcs = []
        srcs.append((xm, lambda c: xm[:, c, :]))
        srcs.append((dec, lambda c: dec[:, c, :]))
        srcs.append((kr2, lambda c: kr2[:, c, :]))
        srcs.append((xm, lambda c: xm[:, 8 + c, :]))
        for q in range(6):
            if q == 4:
                k.ts("dve", (tmp, tmp[:]), (kk, kk[:]), -1.0, None, op0=ALU.mult)
                sb_, fn = tmp, (lambda c: tmp[:, c, :])
            elif q == 5:
                k.tt("dve", (tmp, tmp[:]), (kk, kk[:]), (a_, a_[:]), ALU.mult)
                sb_, fn = tmp, (lambda c: tmp[:, c, :])
            else:
                sb_, fn = srcs[q]
            pb_ = pM[q % 2]
            for c in range(4):
                k.tr((pb_, pb_[:, c * 128:(c + 1) * 128]), (sb_, fn(c)), (identf, identf[:]))
            k.copy("act", (stg, stg[:]), (pb_, pb_[:, :]))
            k.dma("sp", rsc[q], stg[:], reads=[stg], writes=[(rsc, q)])
        for blk, (c0, n) in enumerate(((0, 512), (512, 512), (1024, 512), (1536, 256))):
            proj_tm(pR[blk % 2], pR[blk % 2][:, 0:n], C_R + c0, n)
            k.copy("act", (stg, stg[:, 0:n]), (pR[blk % 2], pR[blk % 2][:, 0:n]))
            for b in range(NSB):
                k.dma("sp", o_sshift[b:b + 1, c0:c0 + n], stg[ST * b + ST - 1:ST * b + ST, 0:n], reads=[stg], writes=[o_sshift])
        k.barrier()
        k.bot = mark_rw
        vec6 = k.sbuf([128, 6, ST, RN], F32, "vec6")
        Ssb = k.sbuf([128, RN, RN], F32, "Ssb")
        tmpS = k.sbuf([128, RN, RN], F32, "tmpS")
        sa = k.sbuf([128, RN], F32, "sa")
        ys = k.sbuf([128, ST, RN], F32, "ys")
        Ytm = k.sbuf([128, RW], F32, "Ytm")
        k.dma("sp", Ssb[:].rearrange("p a b -> p (a b)"), st_rS[:, :], reads=[st_rS], writes=[Ssb])
        for q in range(6):
            for b in range(NSB):
                k.dma("sp", vec6[RH * b:RH * b + RH, q, :, :], rsc[q, ST * b:ST * b + ST, :].rearrange("t (h j) -> h t j", h=RH),
                      reads=[(rsc, q)], writes=[(vec6, q)])
        HV = RN // 2

        def rec_gen(hf):
            i0 = hf * HV
            S_ = (Ssb, Ssb[:, i0:i0 + HV, :], hf)
            T_ = (tmpS, tmpS[:, i0:i0 + HV, :], hf)
            bc = lambda q, t: (vec6, vec6[:, q, t, :].unsqueeze(1).to_broadcast([128, HV, RN]), q)
            for t in range(ST):
                k.tt("dve", T_, S_, bc(4, t), ALU.mult)
                yield
                k.red("dve", (sa, sa[:, i0:i0 + HV], hf), T_, ALU.add)
                yield
                k.tt("pool", S_, S_, bc(1, t), ALU.mult)
                yield
                k.tt("dve", T_, (sa, sa[:, i0:i0 + HV].unsqueeze(2).to_broadcast([128, HV, RN]), hf), bc(5, t), ALU.mult)
                yield
                k.tt("dve", S_, S_, T_, ALU.add)
                yield
                k.tt("pool", T_, (vec6, vec6[:, 3, t, i0:i0 + HV].unsqueeze(2).to_broadcast([128, HV, RN]), 3), bc(2, t), ALU.mult)
                yield
                k.tt("dve", S_, S_, T_, ALU.add)
                yield
                k.tt("pool", T_, S_, bc(0, t), ALU.mult)
                yield
                k.red("dve", (ys, ys[:, t, i0:i0 + HV], (hf, t)), T_, ALU.add)
                yield

        gens_ = [rec_gen(0), rec_gen(1)]
        while gens_:
            for g_ in list(gens_):
                try:
                    next(g_)
                except StopIteration:
                    gens_.remove(g_)
        k.dma("sp", o_sS[:, :], Ssb[:].rearrange("p a b -> p (a b)"), reads=[Ssb], writes=[o_sS])
        k.dma("sp", ysc[:, :], ys[:].rearrange("p a b -> p (a b)"), reads=[ys], writes=[ysc])
        for b in range(NSB):
            k.dma("sp", Ytm[ST * b:ST * b + ST, :].rearrange("t (h i) -> t h i", h=RH),
                  ysc[RH * b:RH * b + RH, :].rearrange("h (t i) -> t h i", t=ST), reads=[ysc], writes=[Ytm])
        rwkv_epilogue(ti, Ytm, rt[7])

    def tail_rows(ti):
        if ti != NTP - 1:
            return
        for blk, (c0, n) in enumerate(((0, 512), (512, 512), (1024, 512), (1536, 256))):
            proj_tm(pR[blk % 2], pR[blk % 2][:, 0:n], C_R + c0, n)
            k.copy("act", (rt[1], rt[1][96:128, :, :].rearrange("p a b -> p (a b)")[:, 0:n]), (pR[blk % 2], pR[blk % 2][96:128, 0:n]))
            k.dma("sp", o_pshift[0:1, c0:c0 + n], rt[1][127:128, :, :].rearrange("p a b -> p (a b)")[:, 0:n], reads=[rt[1]], writes=[o_pshift])

    tiles_all = list(range(NT)) if stage >= 5 else list(range(NTP))

    def phase_1b(k):
        k.barrier()
        k.bot = mark_1a
        W_g = k.sbuf([128, 8, 2048], BF16, "W_g")
        W_bm = k.sbuf([128, 4, D], BF16, "W_bm")
        W_br = k.sbuf([128, 4, D], BF16, "W_br")
        W_out = k.sbuf([128, 8, D], BF16, "W_out")
        k.dma("pool", W_bm[:], d_w_bm[:], reads=[d_w_bm], writes=[W_bm])
        for kh in range(2):
            k.dma("pool", W_g[:, 4 * kh:4 * kh + 4, 0:1024], d_w_in[:, 4 * kh:4 * kh + 4, C_G:C_G + 1024], reads=[d_w_in], writes=[(W_g, "a")])
        k.dma("pool", W_br[:], d_w_br[:], reads=[d_w_br], writes=[W_br])
        for kh in range(2):
            k.dma("pool", W_g[:, 4 * kh:4 * kh + 4, 1024:2048], d_w_in[:, 4 * kh:4 * kh + 4, C_G + 1024:C_G + 2048], reads=[d_w_in], writes=[(W_g, "b")])
        k.dma("pool", W_out[:], d_w_out[:], reads=[d_w_out], writes=[W_out])
        xtb = [k.sbuf([128, D], F32, f"xtb{i}") for i in range(2)]
        hb2 = k.sbuf([128, D], BF16, "hb2")
        hT2 = k.sbuf([128, 8, 128], BF16, "hT2")
        ss2 = k.sbuf([128, 1], F32, "ss2")
        rs2 = k.sbuf([128, 1], F32, "rs2")
        sgb = [k.sbuf([128, 512], F32, f"sgb{i}") for i in range(2)]
        yab = k.sbuf([128, D], F32, "yab")
        mg = [k.sbuf([128, D], BF16, f"mg{i}") for i in range(2)]
        mT = k.sbuf([128, 8, 128], BF16, "mT")
        def head_gen(ti):
            xb = xtb[ti % 2]
            xd, xap = x_rows(ti)
            k.dma("sp", xb[:], xap, reads=[xd], writes=[xb])
            norm_generic(xb, g1bc, hb2, hT2, ss2, rs2)
            yield
            for half, (Wb, src, key) in enumerate(((W_bm, hmT_all, "a"), (W_br, yrgT_all, "b"))):
                for blk in range(2):
                    for kc in range(4):
                        k.mm((pR[blk], pR[blk][:, :]), (src, src[:, ti, kc, :], ti), (Wb, Wb[:, kc, blk * 512:(blk + 1) * 512]),
                             start=(kc == 0), stop=(kc == 3))
                    col = half * 1024 + blk * 512
                    for kc in range(8):
                        k.mm((pF[blk], pF[blk][:, :]), (hT2, hT2[:, kc, :]), (W_g, W_g[:, kc, col:col + 512], key),
                             start=(kc == 0), stop=(kc == 7))
                    k.act((sgb[blk], sgb[blk][:]), (pF[blk], pF[blk][:, :]), AF.Sigmoid)
                    if half == 0:
                        k.tt("dve", (yab, yab[:, blk * 512:(blk + 1) * 512]), (sgb[blk], sgb[blk][:]), (pR[blk], pR[blk][:, :]), ALU.mult)
                    else:
                        k.tt("dve", (sgb[blk], sgb[blk][:]), (sgb[blk], sgb[blk][:]), (pR[blk], pR[blk][:, :]), ALU.mult)
                        k.tt("dve", (mg[ti % 2], mg[ti % 2][:, blk * 512:(blk + 1) * 512]), (sgb[blk], sgb[blk][:]), (yab, yab[:, blk * 512:(blk + 1) * 512]), ALU.add)
                    yield

        def tailb_gen(ti):
            xb = xtb[ti % 2]
            mgt = mg[ti % 2]
            for kc in range(8):
                k.tr((pT, pT[:, kc * 128:(kc + 1) * 128]), (mgt, mgt[:, kc * 128:(kc + 1) * 128]), (identb, identb[:]))
            k.copy("act", (mT, mT[:].rearrange("p a b -> p (a b)")), (pT, pT[:, :]))
            yield
            for blk in range(2):
                for kc in range(8):
                    k.mm((pM[blk], pM[blk][:, :]), (mT, mT[:, kc, :]), (W_out, W_out[:, kc, blk * 512:(blk + 1) * 512]),
                         start=(kc == 0), stop=(kc == 7))
                k.tt("dve", (xb, xb[:, blk * 512:(blk + 1) * 512]), (xb, xb[:, blk * 512:(blk + 1) * 512]), (pM[blk], pM[blk][:, :]), ALU.add)
                yield
            k.dma("sp", x1s[ti * 128:(ti + 1) * 128, :], xb[:], reads=[xb], writes=[(x1s, ti)])

        def rr(gens):
            gens = list(gens)
            while gens:
                for g_ in list(gens):
                    try:
                        next(g_)
                    except StopIteration:
                        gens.remove(g_)

        tlb = list(tiles_all)
        rr([head_gen(tlb[0])])
        for idx, ti in enumerate(tlb):
            gl = [tailb_gen(ti)]
            if idx + 1 < len(tlb):
                gl.append(head_gen(tlb[idx + 1]))
            rr(gl)

    def norm_generic(xbuf, gbc, hb_, hT_, ss_, rs_):
        k.act((hb_, hb_[:]), (xbuf, xbuf[:]), AF.Square, accum=(ss_, ss_[:]))
        k.ts("dve", (rs_, rs_[:]), (ss_, ss_[:]), 1.0 / D, EPS, op0=ALU.mult, op1=ALU.add)
        k.act((rs_, rs_[:]), (rs_, rs_[:]), AF.Ln)
        k.act((rs_, rs_[:]), (rs_, rs_[:]), AF.Exp, scale=-0.5)
        k.stt("dve", (hb_, hb_[:]), (xbuf, xbuf[:]), (rs_, rs_[:, 0:1]), (gbc, gbc[:]), ALU.mult, ALU.mult)
        for kc in range(8):
            k.tr((pT, pT[:, kc * 128:(kc + 1) * 128]), (hb_, hb_[:, kc * 128:(kc + 1) * 128]), (identb, identb[:]))
        k.copy("act", (hT_, hT_[:].rearrange("p a b -> p (a b)")), (pT, pT[:, :]))

    def phase_2(k):
        k.barrier()
        k.bot = mark_phase
        F_up = k.sbuf([128, 8, 2 * DFF], BF16, "F_up")
        F_dn = k.sbuf([128, NFC, D], BF16, "F_dn")
        PGW = k.sbuf([128, 8, D], BF16, "PGW")
        PPJ = k.sbuf([128, 2, D], BF16, "PPJ")
        NG = 4
        CW = DFF // NG
        for g in range(NG):
            for part in range(2):
                k.dma("pool", F_up[:, :, part * DFF + g * CW:part * DFF + (g + 1) * CW], d_f_up[:, :, part * DFF + g * CW:part * DFF + (g + 1) * CW],
                      reads=[d_f_up], writes=[(F_up, g)])
        for g in range(2):
            k.dma("pool", F_dn[:, 11 * g:11 * g + 11, :], d_f_down[:, 11 * g:11 * g + 11, :], reads=[d_f_down], writes=[(F_dn, g)])
        k.dma("pool", PGW[:], d_pgw[:], reads=[d_pgw], writes=[PGW])
        k.dma("pool", PPJ[:], d_ppj[:], reads=[d_ppj], writes=[PPJ])
        g2bc = k.sbuf([128, D], F32, "g2bc")
        g3bc = k.sbuf([128, D], F32, "g3bc")
        g4bc = k.sbuf([128, D], F32, "g4bc")
        fct = k.sbuf([128, 4 * NFC], F32, "fct")
        k.dma("sp", g2bc[:], d_g2[:], reads=[d_g2], writes=[g2bc])
        k.dma("sp", g3bc[:], d_g3[:], reads=[d_g3], writes=[g3bc])
        k.dma("sp", g4bc[:], d_g4[:], reads=[d_g4], writes=[g4bc])
        k.dma("sp", fct[:], d_fctab[:], reads=[d_fctab], writes=[fct])
        fcol = lambda c: (fct, fct[:, c:c + 1])
        xq = [k.sbuf([128, D], F32, f"xq{i}") for i in range(2)]
        hb3 = k.sbuf([128, D], BF16, "hb3")
        hT3s = [k.sbuf([128, 8, 128], BF16, f"hT3a{i}") for i in range(2)]
        ss3 = k.sbuf([128, 1], F32, "ss3")
        rs3 = k.sbuf([128, 1], F32, "rs3")
        gT = k.sbuf([128, NFC, 128], BF16, "gT")
        cf = k.sbuf([128, NFC, 2], F32, "cf")
        GS = 4
        EXW = NSB * (ST + 2)
        ex4 = [k.sbuf([128, GS, EXW], F32, f"ex4_{i}") for i in range(2)]
        cc4 = [k.sbuf([128, GS, 128], F32, f"cc4_{i}") for i in range(2)]
        t14 = [k.sbuf([128, GS, 128], F32, "t14_0")] * 2
        up4 = [k.sbuf([128, GS, 128], F32, f"up4_{i}") for i in range(2)]
        sg3 = [k.sbuf([128, 512], F32, "sg30")] * 2
        ppt = k.sbuf([128, PLE], F32, "ppt")
        ppb = k.sbuf([128, PLE], BF16, "ppb")
        peT = k.sbuf([128, 2, 128], BF16, "peT")
        utm = sg3[0]
        k.memset("pool", (cf, cf[:]), 0.0)
        cfs = k.sbuf([128, NFC, 2 * NSB], F32, "cfs")
        GC = 1.5957691216057308
        BLK6 = ((0, 512), (512, 512), (1024, 512), (1536, 512), (2048, 512), (2560, 256))
        groups = [list(range(g0, min(g0 + GS, NFC))) for g0 in range(0, NFC, GS)]
        gbank = [pF[0], pF[1]]
        ubank = [pM[0], pM[1]]

        def fup_w(col):
            g = col // CW
            g_hi = (col + 127) // CW
            return g, g_hi

        def stage_A(ti, gi, hT3):
            smp = ti == NTP
            p = gi % 2
            chunks = groups[gi]
            n = len(chunks)
            c0 = chunks[0]
            for part, bank in ((0, gbank[p]), (1, ubank[p])):
                for ci, c in enumerate(chunks):
                    g, g_hi = fup_w(c * 128)
                    for kc in range(8):
                        k.mm((bank, bank[:, ci * 128:(ci + 1) * 128]), (F_up, F_up[:, kc, part * DFF + c * 128:part * DFF + (c + 1) * 128], g),
                             (hT3, hT3[:, kc, :]), start=(kc == 0), stop=(kc == 7))
                        if g_hi != g and g_hi in F_up.subs and F_up.subs[g_hi].w is not None:
                            k.streams["pe"][-1].deps.add(F_up.subs[g_hi].w)
            ex = ex4[p]
            if not smp:
                k.copy("dve", (ex, ex[:, 0:n, 0:2]), (cf, cf[:, c0:c0 + n, :]))
                k.copy("act", (ex, ex[:, 0:n, 2:130]), (gbank[p], gbank[p][:, 0:n * 128].rearrange("p (c t) -> p c t", c=n)))
                k.copy("dve", (cf, cf[:, c0:c0 + n, :]), (ex, ex[:, 0:n, 128:130]))
            else:
                exs = ex[:, 0:n, :].rearrange("p c (b t) -> p c b t", t=ST + 2)
                k.copy("pool", (ex, exs[:, :, :, 0:2]), (cfs, cfs[:, c0:c0 + n, :].rearrange("p c (b j) -> p c b j", j=2)))
                k.copy("act", (ex, exs[:, :, :, 2:ST + 2]), (gbank[p], gbank[p][:, 0:n * 128].rearrange("p (c b t) -> p c b t", c=n, b=NSB)))
            k.copy("act", (up4[p], up4[p][:, 0:n, :]), (ubank[p], ubank[p][:, 0:n * 128].rearrange("p (c t) -> p c t", c=n)))
            for ci, c in enumerate(chunks):
                if not smp:
                    tap = lambda j: ex[:, ci, j:j + 128]
                    ccv = cc4[p][:, ci, :]
                else:
                    e3 = ex[:, ci, :].rearrange("p (b t) -> p b t", t=ST + 2)
                    tap = lambda j, e3=e3: e3[:, :, j:j + ST]
                    ccv = cc4[p][:, ci, :].rearrange("p (b t) -> p b t", t=ST)
                cb = cc4[p]
                k.ts("dve", (cb, ccv), (ex, tap(2)), fcol(2 * NFC + c), fcol(3 * NFC + c), op0=ALU.mult, op1=ALU.add)
                k.stt("dve", (cb, ccv), (ex, tap(1)), fcol(1 * NFC + c), (cb, ccv), ALU.mult, ALU.add)
                k.stt("dve", (cb, ccv), (ex, tap(0)), fcol(0 * NFC + c), (cb, ccv), ALU.mult, ALU.add)

        def stage_B(ti, gi):
            p = gi % 2
            chunks = groups[gi]
            n = len(chunks)
            c0 = chunks[0]
            cb = (cc4[p], cc4[p][:, 0:n, :])
            ta = (t14[p], t14[p][:, 0:n, :])
            k.tt("dve", ta, cb, cb, ALU.mult)
            k.ts("dve", ta, ta, 0.044715, 1.0, op0=ALU.mult, op1=ALU.add)
            k.tt("dve", ta, ta, cb, ALU.mult)
            k.act(ta, ta, AF.Sigmoid, scale=GC)
            k.tt("dve", ta, ta, cb, ALU.mult)
            k.tt("dve", (gT, gT[:, c0:c0 + n, :], ("g", gi)), ta, (up4[p], up4[p][:, 0:n, :]), ALU.mult)

        def load_norm2(ti):
            xb = xq[ti % 2]
            k.dma("sp", xb[:], x1s[ti * 128:(ti + 1) * 128, :], reads=[(x1s, ti)], writes=[xb])
            if ti == NTP:
                for b6, (c0, n) in enumerate(BLK6):
                    k.dma("sp", utm[0:2 * NSB, 0:n], st_fconv[:, c0:c0 + n], reads=[st_fconv], writes=[utm])
                    nch = n // 128
                    for ci in range(nch):
                        k.tr((pR[b6 % 2], pR[b6 % 2][:, ci * 32:(ci + 1) * 32]), (utm, utm[0:2 * NSB, ci * 128:(ci + 1) * 128]),
                             (identf, identf[0:2 * NSB, 0:2 * NSB]))
                    k.copy("act", (cfs, cfs[:, 4 * b6:4 * b6 + nch, :]), (pR[b6 % 2], pR[b6 % 2][:, 0:nch * 32].rearrange("p (c x) -> p c x", c=nch)))
            norm_generic(xb, g2bc, hb3, hT3s[ti % 2], ss3, rs3)

        hb3b = hb3

        def groups_gen(ti):
            hT3 = hT3s[ti % 2]
            for gi in range(len(groups) + 1):
                if gi < len(groups):
                    stage_A(ti, gi, hT3)
                    yield
                if gi >= 1:
                    stage_B(ti, gi - 1)
                    yield

        def tail_gen(ti):
            smp = ti == NTP
            xb = xq[ti % 2]
            hT3 = hT3s[ti % 2]
            for blk in range(2):
                for c in range(NFC):
                    k.mm((pR[blk], pR[blk][:, :]), (gT, gT[:, c, :], ("g", c // GS)), (F_dn, F_dn[:, c, blk * 512:(blk + 1) * 512], c // 11),
                         start=(c == 0), stop=(c == NFC - 1))
                k.tt("dve", (xb, xb[:, blk * 512:(blk + 1) * 512]), (xb, xb[:, blk * 512:(blk + 1) * 512]), (pR[blk], pR[blk][:, :]), ALU.add)
                yield
            if ti == NTP - 1 or smp:
                for b6, (c0, n) in enumerate(BLK6):
                    for kc in range(8):
                        k.mm((pM[2], pM[2][:, 0:n]), (hT3, hT3[:, kc, :]), (F_up, F_up[:, kc, c0:c0 + n]),
                             start=(kc == 0), stop=(kc == 7))
                    if not smp:
                        k.copy("act", (utm, utm[96:128, 0:n]), (pM[2], pM[2][96:128, 0:n]))
                        k.dma("sp", o_pfconv[:, c0:c0 + n], utm[126:128, 0:n], reads=[utm], writes=[o_pfconv])
                    else:
                        k.copy("act", (utm, utm[:, 0:n]), (pM[2], pM[2][:, 0:n]))
                        for b in range(NSB):
                            k.dma("sp", o_sfconv[2 * b:2 * b + 2, c0:c0 + n], utm[ST * b + ST - 2:ST * b + ST, 0:n], reads=[utm], writes=[o_sfconv])
                    yield
            norm_generic(xb, g3bc, hb3b, hT3, ss3, rs3)
            yield
            pd, pap = (pp, pp[ti * 128:(ti + 1) * 128, :]) if not smp else (psm, psm[:, :])
            k.dma("sp", ppt[:], pap, reads=[pd], writes=[ppt])
            k.copy("act", (ppb, ppb[:]), (ppt, ppt[:]))
            for kc in range(2):
                k.tr((pT, pT[:, kc * 128:(kc + 1) * 128]), (ppb, ppb[:, kc * 128:(kc + 1) * 128]), (identb, identb[:]))
            k.copy("act", (peT, peT[:].rearrange("p a b -> p (a b)")), (pT, pT[:, 0:256]))
            yield
            for blk in range(2):
                for kc in range(8):
                    k.mm((pR[blk], pR[blk][:, :]), (hT3, hT3[:, kc, :]), (PGW, PGW[:, kc, blk * 512:(blk + 1) * 512]),
                         start=(kc == 0), stop=(kc == 7))
                yield
                k.act((sg3[blk], sg3[blk][:]), (pR[blk], pR[blk][:, :]), AF.Sigmoid)
                for kc in range(2):
                    k.mm((pM[2], pM[2][:, :]), (peT, peT[:, kc, :]), (PPJ, PPJ[:, kc, blk * 512:(blk + 1) * 512]),
                         start=(kc == 0), stop=(kc == 1))
                k.tt("dve", (sg3[blk], sg3[blk][:]), (sg3[blk], sg3[blk][:]), (pM[2], pM[2][:, :]), ALU.mult)
                k.tt("dve", (xb, xb[:, blk * 512:(blk + 1) * 512]), (xb, xb[:, blk * 512:(blk + 1) * 512]), (sg3[blk], sg3[blk][:]), ALU.add)
                yield
            k.act((hb3b, hb3b[:]), (xb, xb[:]), AF.Square, accum=(ss3, ss3[:]))
            k.ts("dve", (rs3, rs3[:]), (ss3, ss3[:]), 1.0 / D, EPS, op0=ALU.mult, op1=ALU.add)
            k.act((rs3, rs3[:]), (rs3, rs3[:]), AF.Ln)
            k.act((rs3, rs3[:]), (rs3, rs3[:]), AF.Exp, scale=-0.5)
            yield
            k.stt("dve", (xb, xb[:]), (xb, xb[:]), (rs3, rs3[:, 0:1]), (g4bc, g4bc[:]), ALU.mult, ALU.mult)
            if not smp:
                k.dma("sp", y_p[ti * 128:(ti + 1) * 128, :], xb[:], reads=[xb], writes=[y_p])
            else:
                k.dma("sp", y_s[:, :], xb[:], reads=[xb], writes=[y_s])
            if ti in nxt2:
                yield
                load_norm2(nxt2[ti])

        def run_rr(gens):
            gens = list(gens)
            while gens:
                for g_ in list(gens):
                    try:
                        next(g_)
                    except StopIteration:
                        gens.remove(g_)

        tl = list(tiles_all)
        nxt2 = {tl[i]: tl[i + 2] for i in range(len(tl) - 2)}
        load_norm2(tl[0])
        if len(tl) > 1:
            load_norm2(tl[1])
        run_rr([groups_gen(tl[0])])
        for idx, ti in enumerate(tl):
            tg = tail_gen(ti)
            if idx + 1 < len(tl):
                gg = groups_gen(tl[idx + 1])
                next(gg)
                next(gg)
                next(tg)
                next(tg)
                run_rr([gg, tg])
            else:
                run_rr([tg])

    _nt_dbg = int(_os.environ.get("KDBG_NT", "0"))
    tiles_run = (tiles_all if not _nt_dbg else list(range(_nt_dbg)))
    for ti in tiles_run:
        mixer_tile(ti)
        if stage >= 2:
            rwkv_tile(ti)
            tail_rows(ti)

    if stage >= 3:
        phase_1b(k)
    if stage >= 4:
        phase_2(k)
    k.emit()
    k.stats["sbuf_hiwater"] = k.hiwater
    k.stats["arena_bytes"] = k.arena_bytes
    return nc, k


def _chunk_rows(w, nk):
    return np.ascontiguousarray(w.reshape(nk, 128, w.shape[1]).transpose(1, 0, 2))


def _pcols(v, nc_):
    return v.reshape(nc_, 128).T


_PROG = {}


def _get_prog(stage=99, dbg=False):
    key = (stage, dbg)
    if key not in _PROG:
        _PROG[key] = build_program(stage, dbg)
    return _PROG[key]


def make_in_maps(inp):
    f = lambda a: np.ascontiguousarray(np.asarray(a, dtype=np.float32))
    ptab = np.zeros((128, 128), np.float32)
    mcw = f(inp["m_conv_w"])[0]
    for j in range(4):
        ptab[:, j * 8:(j + 1) * 8] = _pcols(mcw[j], 8)
    ptab[:, 32:40] = _pcols(f(inp["m_conv_b"])[0], 8)
    ptab[:, 40:54] = _pcols(f(inp["r_mix"])[0], 14)
    ptab[:, 54:58] = _pcols(f(inp["r_w0"])[0], 4)
    ptab[:, 58:62] = _pcols(f(inp["r_a0"])[0], 4)
    ptab[:, 62:66] = _pcols(f(inp["r_kk"])[0], 4)
    ptab[:, 66:70] = _pcols(f(inp["r_ka"])[0], 4)
    ptab[:, 70:74] = _pcols(f(inp["r_rk"])[0].reshape(-1), 4)
    ptab[:, 74:78] = _pcols(f(inp["r_ln_g"])[0], 4)
    ptab[:, 78:82] = _pcols(f(inp["r_ln_b"])[0], 4)
    ptab[:, 82:86] = _pcols(f(inp["m_norm_g"])[0], 4)
    fct = np.zeros((128, 4 * NFC), np.float32)
    fcw = f(inp["f_conv_w"])[0]
    for j in range(3):
        fct[:, j * NFC:(j + 1) * NFC] = _pcols(fcw[j], NFC)
    fct[:, 3 * NFC:4 * NFC] = _pcols(f(inp["f_conv_b"])[0], NFC)
    gbias = np.stack([f(inp["m_i_bias"])[0], f(inp["m_f_bias"])[0]], axis=1)
    ra2 = np.zeros((128, RW), np.float32)
    ra2[64:128] = f(inp["r_a2"])[0]
    bc = lambda v: np.ascontiguousarray(np.broadcast_to(f(v).reshape(1, D), (128, D)))
    shared = {
        "w_in": _chunk_rows(f(inp["w_in"])[0], 8),
        "w_bm": _chunk_rows(f(inp["w_branch_m"])[0], 4),
        "w_br": _chunk_rows(f(inp["w_branch_r"])[0], 4),
        "w_out": _chunk_rows(f(inp["w_out"])[0], 8),
        "f_up": _chunk_rows(f(inp["f_up"])[0], 8),
        "f_down": _chunk_rows(f(inp["f_down"])[0], NFC),
        "ple_gate_w": _chunk_rows(f(inp["ple_gate_w"])[0], 8),
        "ple_proj": _chunk_rows(f(inp["ple_proj"])[0], 2),
        "r_w2": f(inp["r_w2"])[0], "r_a2": ra2, "r_g2": f(inp["r_g2"])[0],
        "norm1_g": bc(inp["norm1_g"]), "norm2_g": bc(inp["norm2_g"]),
        "ple_norm_g": bc(inp["ple_norm_g"]), "final_norm_g": bc(inp["final_norm_g"]),
        "ptab": ptab, "fctab": fct, "gate_bias": np.ascontiguousarray(gbias),
    }
    maps = []
    for c in range(NCORES):
        sl = slice(c * NSB, (c + 1) * NSB)
        m = dict(shared)
        m["xp"] = f(inp["x_prompt"][c])
        m["xs"] = f(inp["x_sample"][sl]).reshape(128, D)
        m["pp"] = f(inp["p_prompt"][0, c])
        m["psm"] = f(inp["p_sample"][0, sl]).reshape(128, PLE)
        m["st_mconv"] = f(inp["state_mlstm_conv"][0, sl]).reshape(NSB * 3, 2 * MW)
        m["st_mC"] = f(inp["state_mlstm_C"][0, sl])
        m["st_mn"] = f(inp["state_mlstm_n"][0, sl])
        m["st_mm"] = f(inp["state_mlstm_m"][0, sl])
        m["st_rshift"] = f(inp["state_rwkv_shift"][0, sl])
        m["st_rS"] = f(inp["state_rwkv_S"][0, sl]).reshape(NSB * RH, RN * RN)
        m["st_fconv"] = f(inp["state_ffn_conv"][0, sl]).reshape(NSB * 2, DFF)
        maps.append(m)
    return maps


def assemble(results):
    g = lambda name: [np.asarray(r[name], dtype=np.float32) for r in results]
    y_p = np.stack(g("y_p"), 0)
    y_s = np.concatenate([a.reshape(NSB, ST, D) for a in g("y_s")], 0)
    p_conv = np.stack(g("p_conv"), 0)[None]
    p_C = np.stack(g("p_C"), 0)[None]
    p_n = np.stack(g("p_n"), 0)[None]
    p_m = np.stack([a.reshape(MH) for a in g("p_m")], 0)[None]
    p_shift = np.stack([a.reshape(RCOLS) for a in g("p_shift")], 0)[None]
    p_S = np.stack([a.reshape(RH, RN, RN) for a in g("p_S")], 0)[None]
    p_fconv = np.stack(g("p_fconv"), 0)[None]
    s_conv = np.concatenate([a.reshape(NSB, 3, 2 * MW) for a in g("s_conv")], 0)[None]
    s_C = np.concatenate(g("s_C"), 0)[None]
    s_n = np.concatenate(g("s_n"), 0)[None]
    s_m = np.concatenate(g("s_m"), 0)[None]
    s_shift = np.concatenate(g("s_shift"), 0)[None]
    s_S = np.concatenate([a.reshape(NSB, RH, RN, RN) for a in g("s_S")], 0)[None]
    s_fconv = np.concatenate([a.reshape(NSB, 2, DFF) for a in g("s_fconv")], 0)[None]
    return (y_p, y_s, p_conv, p_C, p_n, p_m, p_shift, p_S, p_fconv,
            s_conv, s_C, s_n, s_m, s_shift, s_S, s_fconv)


def kernel(**inputs):
    nc, _ = _get_prog()
    maps = make_in_maps(inputs)
    res = run_bass_kernel_spmd(nc, maps, core_ids=list(range(NCORES)))
    return assemble(res.results)
```

```python
import math
from contextlib import ExitStack

import numpy as np
import concourse.bass as bass
import concourse.mybir as mybir
from concourse.bass_utils import run_bass_kernel_spmd

F32 = mybir.dt.float32
BF16 = mybir.dt.bfloat16
AF = mybir.ActivationFunctionType
ALU = mybir.AluOpType
AX = mybir.AxisListType

ENGS = ("pe", "act", "dve", "pool", "sp")
N_DMA_SEMS = 8
SAME_ENG_DIST = 2

D = 1024
SEQ = 2048
NCORES = 8
NTP = SEQ // 128
NSB = 16
ST = 8
MW = 512
MH = 4
RW = 512
RH = 8
RN = 64
RCOLS = 1792
DFF = 2816
NFC = DFF // 128
PLE = 256
N_IN = 5896
C_QK, C_V, C_O, C_I, C_F, C_R, C_G = 0, 1024, 1536, 2048, 2052, 2056, 3848
EPS = 1e-6
GN_EPS = 64e-5
KSCALE = 128 ** -0.5
WSCALE = -math.exp(-0.5)


class _Trk:
    __slots__ = ("w", "r")

    def __init__(self):
        self.w = None
        self.r = []


class Buf:
    def __init__(self, t, name):
        self.t = t
        self.name = name
        self.whole = _Trk()
        self.subs = {}

    def __getitem__(self, idx):
        return self.t[idx]

    def view(self, ap, name=None):
        b = Buf(ap, name or self.name + "_v")
        b.whole = self.whole
        b.subs = self.subs
        return b


class _Op:
    __slots__ = ("eng", "fn", "deps", "needs_inc", "is_dma", "sem", "val", "pos", "force")


class K:
    def __init__(self, nc):
        self.nc = nc
        self.es = ExitStack()
        self.streams = {e: [] for e in ENGS}
        self.dma_rr = {e: 0 for e in ENGS}
        self.dma_last = {}
        self.nbuf = 0
        self.ops = []

    def _init_arena(self):
        nbytes = (int(self.nc.sbuf_bytes_remaining) - 512) // 64 * 64
        self.arena_bytes = nbytes
        self.arena = self.es.enter_context(self.nc.sbuf_tensor("arena", [128, nbytes // 2], BF16))
        self.bot = 0
        self.top = nbytes
        self.hiwater = 0

    def _view(self, off, shape, dtype):
        n = 1
        for d in shape[1:]:
            n *= d
        esz = 4 if dtype == F32 else 2
        v = self.arena[:, off // 2:(off + n * esz) // 2]
        if dtype == F32:
            v = v.bitcast(F32)
        if len(shape) > 2:
            names = " ".join(f"d{i}" for i in range(len(shape) - 1))
            v = v.rearrange(f"p ({names}) -> p {names}", **{f"d{i}": shape[i + 1] for i in range(len(shape) - 1)})
        if shape[0] < 128:
            v = v[0:shape[0]]
        return v, n * esz

    def sbuf(self, shape, dtype, name=None, top=False):
        if not hasattr(self, "arena"):
            self._init_arena()
        self.nbuf += 1
        name = name or f"sb{self.nbuf}"
        n = 1
        for d in shape[1:]:
            n *= d
        nb = (n * (4 if dtype == F32 else 2) + 63) // 64 * 64
        if top:
            self.top -= nb
            off = self.top
        else:
            off = self.bot
            self.bot += nb
        assert self.bot <= self.top, f"SBUF arena overflow allocating {name}: bot={self.bot} top={self.top}"
        self.hiwater = max(self.hiwater, self.bot + (self.arena_bytes - self.top))
        v, _ = self._view(off, list(shape), dtype)
        return Buf(v, name)

    def pe_fence(self):
        st = self.streams["pe"]
        if not st:
            return
        last = st[-1]
        o = self.op("pe", lambda h: h.nop(), (), ())
        o.deps.add(last)
        o.force = {last}
        if getattr(self, "fence_mm", None) is not None:
            fb, fi = self.fence_mm
            self.tr((fb, fb[:, 0:128]), (fi, fi[:]), (fi, fi[:]))
            last = self.streams["pe"][-1]
            o = self.op("pe", lambda h: h.nop(), (), ())
            o.deps.add(last)
            o.force = {last}

    def barrier(self):
        lasts = [st[-1] for st in self.streams.values() if st]
        lasts += list(self.dma_last.values())
        for e in ENGS:
            o = self.op(e, lambda h: h.nop(), (), ())
            o.deps.update(x for x in lasts if x is not o)

    def psum(self, shape, dtype, name=None):
        self.nbuf += 1
        name = "ps_" + (name or f"{self.nbuf}")
        t = self.es.enter_context(self.nc.psum_tensor(name, list(shape), dtype))
        return Buf(t, name)

    def dram(self, name, shape, dtype, kind="Internal"):
        t = self.nc.dram_tensor(name, list(shape), dtype, kind=kind)
        return Buf(t.ap(), name)

    def _touch(self, op, item, is_write):
        if isinstance(item, tuple):
            buf, key = item
        else:
            buf, key = item, None
        if key is None:
            trks = [buf.whole] + list(buf.subs.values())
        else:
            if key not in buf.subs:
                buf.subs[key] = _Trk()
            trks = [buf.whole, buf.subs[key]]
        for t in trks:
            if t.w is not None:
                op.deps.add(t.w)
            if is_write:
                op.deps.update(t.r)
        return buf, key

    def _commit(self, op, buf, key, is_write):
        if key is None:
            if is_write:
                buf.whole.w = op
                buf.whole.r = []
                buf.subs.clear()
            else:
                self._add_reader(buf.whole, op)
        else:
            t = buf.subs[key]
            if is_write:
                t.w = op
                t.r = []
            else:
                self._add_reader(t, op)

    @staticmethod
    def _add_reader(t, op):
        if not op.is_dma:
            t.r = [o for o in t.r if o.is_dma or o.eng != op.eng]
        t.r.append(op)

    def op(self, eng, fn, reads=(), writes=(), dma=False):
        o = _Op()
        o.eng = eng
        o.fn = fn
        o.deps = set()
        o.needs_inc = False
        o.is_dma = dma
        o.sem = None
        o.val = None
        o.force = None
        touched = []
        for it in reads:
            touched.append(self._touch(o, it, False) + (False,))
        for it in writes:
            touched.append(self._touch(o, it, True) + (True,))
        o.deps.discard(o)
        for buf, key, w in touched:
            self._commit(o, buf, key, w)
        if dma:
            kk = (eng, self.dma_rr[eng] % N_DMA_SEMS)
            self.dma_rr[eng] += 1
            prev = self.dma_last.get(kk)
            if prev is not None:
                o.deps.add(prev)
            self.dma_last[kk] = o
            o.sem = kk
            o.needs_inc = True
        o.pos = len(self.streams[eng])
        self.streams[eng].append(o)
        self.ops.append(o)
        return o

    def dma(self, eng, out, in_, reads=(), writes=(), **kw):
        return self.op(eng, lambda e: e.dma_start(out=out, in_=in_, **kw), reads, writes, dma=True)

    def emit(self):
        nc = self.nc
        for o in self.ops:
            real = []
            for d in o.deps:
                if (not d.is_dma) and (not o.is_dma) and d.eng == o.eng and o.eng == "pe":
                    if not (o.force and d in o.force):
                        continue
                d.needs_inc = True
                real.append(d)
            o.deps = real
        for e in ENGS:
            cs = [o for o in self.streams[e] if not o.is_dma]
            if cs:
                cs[-1].needs_inc = True
        es = self.es
        esem = {e: es.enter_context(nc.semaphore(f"s_{e}")) for e in ENGS}
        dsem = {}
        for e in ENGS:
            for i in range(min(N_DMA_SEMS, self.dma_rr[e])):
                dsem[(e, i)] = es.enter_context(nc.semaphore(f"d_{e}{i}"))
        dcount = {kk: 0 for kk in dsem}
        for e in ENGS:
            c = 0
            for o in self.streams[e]:
                if o.is_dma:
                    dcount[o.sem] += 16
                    o.val = dcount[o.sem]
                    o.sem = dsem[o.sem]
                elif o.needs_inc:
                    c += 1
                    o.val = c
                    o.sem = esem[e]
        final_waits = [(s, dcount[kk]) for kk, s in dsem.items() if dcount[kk] > 0]
        for e in ENGS:
            if e == "sp":
                continue
            cs = [o for o in self.streams[e] if not o.is_dma and o.needs_inc]
            if cs:
                final_waits.append((esem[e], cs[-1].val))
        streams = self.streams
        nwaits = [0]

        def run(e, handle):
            waited = {}
            for o in streams[e]:
                need = {}
                for d in o.deps:
                    if need.get(d.sem, (None, 0))[1] < d.val:
                        need[d.sem] = (d.sem, d.val)
                for s, v in need.values():
                    if waited.get(s, 0) >= v:
                        continue
                    handle.wait_ge(s, v)
                    nwaits[0] += 1
                    waited[s] = v
                ins = o.fn(handle)
                if o.is_dma:
                    ins.then_inc(o.sem, 16)
                elif o.needs_inc:
                    ins.then_inc(o.sem, 1)
            if e == "sp":
                for s, v in final_waits:
                    handle.wait_ge(s, v)

        with nc.Block() as block:
            @block.tensor
            def _(h):
                run("pe", h)

            @block.scalar
            def _(h):
                run("act", h)

            @block.vector
            def _(h):
                run("dve", h)

            @block.gpsimd
            def _(h):
                run("pool", h)

            @block.sync
            def _(h):
                run("sp", h)
        self.stats = dict(n_ops={e: len(streams[e]) for e in ENGS}, n_waits=nwaits[0])
        self.es.close()

    @staticmethod
    def _it(x):
        return (x[0], x[2]) if len(x) > 2 else x[0]

    def mm(self, out, lhsT, rhs, start=True, stop=True):
        return self.op("pe", lambda e: e.matmul(out[1], lhsT=lhsT[1], rhs=rhs[1], start=start, stop=stop),
                       reads=[self._it(lhsT), self._it(rhs)], writes=[self._it(out)])

    def tr(self, out, in_, ident):
        return self.op("pe", lambda e: e.transpose(out[1], in_[1], ident[1]),
                       reads=[self._it(in_), self._it(ident)], writes=[self._it(out)])

    def act(self, out, in_, func, bias=None, scale=None, accum=None, eng="act"):
        reads = [self._it(in_)]
        kw = {}
        if bias is not None:
            if isinstance(bias, tuple):
                reads.append(self._it(bias))
                kw["bias"] = bias[1]
            else:
                kw["bias"] = bias
        if scale is not None:
            if isinstance(scale, tuple):
                reads.append(self._it(scale))
                kw["scale"] = scale[1]
            else:
                kw["scale"] = scale
        writes = [self._it(out)]
        if accum is not None:
            writes.append(self._it(accum))
            kw["accum_out"] = accum[1]
        return self.op(eng, lambda e: e.activation(out=out[1], in_=in_[1], func=func, **kw), reads, writes)

    def tt(self, eng, out, in0, in1, op):
        return self.op(eng, lambda e: e.tensor_tensor(out=out[1], in0=in0[1], in1=in1[1], op=op),
                       reads=[self._it(in0), self._it(in1)], writes=[self._it(out)])

    def ts(self, eng, out, in0, s1, s2=None, op0=ALU.mult, op1=None, accum=None):
        reads = [self._it(in0)]
        a1 = s1
        a2 = s2
        if isinstance(s1, tuple):
            reads.append(self._it(s1))
            a1 = s1[1]
        if isinstance(s2, tuple):
            reads.append(self._it(s2))
            a2 = s2[1]
        kw = {}
        if op1 is not None:
            kw["op1"] = op1
        writes = [self._it(out)]
        if accum is not None:
            writes.append(self._it(accum))
            kw["accum_out"] = accum[1]
        return self.op(eng, lambda e: e.tensor_scalar(out=out[1], in0=in0[1], scalar1=a1, scalar2=a2, op0=op0, **kw),
                       reads, writes)

    def stt(self, eng, out, in0, scalar, in1, op0, op1):
        reads = [self._it(in0), self._it(in1)]
        a = scalar
        if isinstance(scalar, tuple):
            reads.append(self._it(scalar))
            a = scalar[1]
        return self.op(eng, lambda e: e.scalar_tensor_tensor(out=out[1], in0=in0[1], scalar=a, in1=in1[1], op0=op0, op1=op1),
                       reads, [self._it(out)])

    def copy(self, eng, out, in_):
        if eng == "act":
            return self.op(eng, lambda e: e.activation(out=out[1], in_=in_[1], func=AF.Copy),
                           reads=[self._it(in_)], writes=[self._it(out)])
        return self.op(eng, lambda e: e.tensor_copy(out=out[1], in_=in_[1]),
                       reads=[self._it(in_)], writes=[self._it(out)])

    def red(self, eng, out, in_, op, axis=AX.X):
        return self.op(eng, lambda e: e.tensor_reduce(out=out[1], in_=in_[1], axis=axis, op=op),
                       reads=[self._it(in_)], writes=[self._it(out)])

    def memset(self, eng, out, val):
        return self.op(eng, lambda e: e.memset(out[1], val), reads=[], writes=[self._it(out)])

    def scan(self, eng, out, d0, d1, init, op0, op1):
        return self.op(eng, lambda e: e.tensor_tensor_scan(out=out[1], data0=d0[1], data1=d1[1], initial=init, op0=op0, op1=op1),
                       reads=[self._it(d0), self._it(d1)], writes=[self._it(out)])


def build_program(stage=99, dbg=False):
    import os as _os
    nc = bass.Bass("TRN2", target_bir_lowering=False)
    k = K(nc)
    NT = NTP + 1

    def din(name, shape):
        return k.dram(name, shape, F32, "ExternalInput")

    def dout(name, shape):
        return k.dram(name, shape, F32, "ExternalOutput")

    xp = din("xp", [SEQ, D]); xs = din("xs", [128, D])
    pp = din("pp", [SEQ, PLE]); psm = din("psm", [128, PLE])
    st_mconv = din("st_mconv", [NSB * 3, 2 * MW])
    st_mC = din("st_mC", [NSB, MH, 128, 128])
    st_mn = din("st_mn", [NSB, MH, 128])
    st_mm = din("st_mm", [NSB, MH])
    st_rshift = din("st_rshift", [NSB, RCOLS])
    st_rS = din("st_rS", [NSB * RH, RN * RN])
    st_fconv = din("st_fconv", [NSB * 2, DFF])
    d_w_in = din("w_in", [128, 8, N_IN])
    d_w_bm = din("w_bm", [128, 4, D]); d_w_br = din("w_br", [128, 4, D])
    d_w_out = din("w_out", [128, 8, D])
    d_f_up = din("f_up", [128, 8, 2 * DFF]); d_f_down = din("f_down", [128, NFC, D])
    d_pgw = din("ple_gate_w", [128, 8, D]); d_ppj = din("ple_proj", [128, 2, D])
    d_rw2 = din("r_w2", [64, RW]); d_ra2 = din("r_a2", [128, RW]); d_rg2 = din("r_g2", [128, RW])
    d_g1 = din("norm1_g", [128, D]); d_g2 = din("norm2_g", [128, D])
    d_g3 = din("ple_norm_g", [128, D]); d_g4 = din("final_norm_g", [128, D])
    d_ptab = din("ptab", [128, 128])
    d_fctab = din("fctab", [128, 4 * NFC])
    d_gb = din("gate_bias", [4, 2])
    y_p = dout("y_p", [SEQ, D]); y_s = dout("y_s", [128, D])
    o_pconv = dout("p_conv", [3, 2 * MW]); o_pC = dout("p_C", [MH, 128, 128]); o_pn = dout("p_n", [MH, 128])
    o_pm = dout("p_m", [1, MH]); o_pshift = dout("p_shift", [1, RCOLS]); o_pS = dout("p_S", [RH * RN, RN])
    o_pfconv = dout("p_fconv", [2, DFF])
    o_sconv = dout("s_conv", [NSB * 3, 2 * MW]); o_sC = dout("s_C", [NSB, MH, 128, 128]); o_sn = dout("s_n", [NSB, MH, 128])
    o_sm = dout("s_m", [NSB, MH]); o_sshift = dout("s_shift", [NSB, RCOLS]); o_sS = dout("s_S", [NSB * RH, RN * RN])
    o_sfconv = dout("s_fconv", [NSB * 2, DFF])
    x1s = k.dram("x1_scratch", [NT * 128, D], F32)
    dbgs = {}

    def dbg_out(name, src_buf, src_ap, shape):
        if not dbg:
            return
        t = dout("dbg_" + name, shape)
        dbgs[name] = t
        k.dma("sp", t[:], src_ap, reads=[src_buf], writes=[t])

    identf = k.sbuf([128, 128], F32, "identf")
    identb = k.sbuf([128, 128], BF16, "identb")
    mark_phase = k.bot
    mU_in = [k.sbuf([128, 128], F32, f"mUin{i}") for i in range(2)]
    mU_st = [k.sbuf([128, 128], F32, f"mUst{i}") for i in range(2)]
    mL_st = [k.sbuf([128, 128], F32, f"mLst{i}") for i in range(2)]
    resets = [k.sbuf([128, 512], F32, f"resets{i}") for i in range(2)]
    ones4 = k.sbuf([4, 128], F32, "ones4")

    def aff(out_buf, out_ap, pattern, cm, base, op=ALU.is_ge):
        k.op("pool", lambda e: e.affine_select(out=out_ap, in_=out_ap, pattern=pattern, compare_op=op,
                                               fill=0.0, base=base, channel_multiplier=cm),
             reads=[out_buf], writes=[out_buf])

    k.memset("pool", (identf, identf[:]), 1.0)
    aff(identf, identf[:], [[-1, 128]], 1, 0)
    aff(identf, identf[:], [[1, 128]], -1, 0)
    k.copy("pool", (identb, identb[:]), (identf, identf[:]))
    for i in range(2):
        k.memset("pool", (mU_in[i], mU_in[i][:]), 1.0)
        aff(mU_in[i], mU_in[i][:], [[1, 128]], -1, 0)
        k.memset("pool", (mU_st[i], mU_st[i][:]), 1.0)
        aff(mU_st[i], mU_st[i][:], [[1, 128]], -1, -1)
        k.memset("pool", (mL_st[i], mL_st[i][:]), 1.0)
        aff(mL_st[i], mL_st[i][:], [[-1, 128]], 1, -1)
        k.memset("pool", (resets[i], resets[i][:]), 1.0)
    v3 = lambda b: b[:].rearrange("p (a c) -> p a c", c=ST)
    aff(mU_in[1], v3(mU_in[1]), [[-ST, 16], [0, ST]], 1, 0)
    aff(mU_st[1], v3(mU_st[1]), [[-ST, 16], [0, ST]], 1, 0)
    aff(mL_st[1], v3(mL_st[1]), [[ST, 16], [0, ST]], -1, ST - 1)
    k.memset("pool", (resets[0], resets[0][:].rearrange("p (a c) -> p a c", c=128)[:, :, 0:1]), 0.0)
    k.memset("pool", (resets[1], resets[1][:].rearrange("p (a c) -> p a c", c=ST)[:, :, 0:1]), 0.0)
    k.memset("pool", (ones4, ones4[:]), 1.0)
    mask2 = [k.sbuf([128, 256], F32, f"mask2_{i}") for i in range(2)]
    for i in range(2):
        k.copy("pool", (mask2[i], mask2[i][:, 0:128]), (mU_st[i], mU_st[i][:]))
        k.copy("pool", (mask2[i], mask2[i][:, 128:256]), (mU_in[i], mU_in[i][:]))
    I2 = k.sbuf([128, 64], F32, "I2")
    k.tt("pool", (I2, I2[:]), (identf, identf[:, 0:64]), (identf, identf[:, 64:128]), ALU.add)
    bones = k.sbuf([128, 128], F32, "bones")
    k.memset("pool", (bones, bones[:]), 0.0)
    k.memset("pool", (bones, bones[0:64, 0:64]), 1.0)
    k.memset("pool", (bones, bones[64:128, 64:128]), 1.0)

    ptab = k.sbuf([128, 128], F32, "ptab")
    k.dma("sp", ptab[:], d_ptab[:], reads=[d_ptab], writes=[ptab])
    PT_MCW, PT_MCB, PT_RMIX, PT_RW0, PT_RA0, PT_RKK, PT_RKA, PT_RRK, PT_RLNG, PT_RLNB, PT_MNG = 0, 32, 40, 54, 58, 62, 66, 70, 74, 78, 82
    pcol = lambda c: (ptab, ptab[:, c:c + 1])
    gb = k.sbuf([4, 2], F32, "gb")
    k.dma("sp", gb[:], d_gb[:], reads=[d_gb], writes=[gb])
    nbf = k.sbuf([4, 1], F32, "nbf")
    k.ts("dve", (nbf, nbf[:]), (gb, gb[:, 1:2]), -1.0, None, op0=ALU.mult)
    g1bc = k.sbuf([128, D], F32, "g1bc")
    k.dma("sp", g1bc[:], d_g1[:], reads=[d_g1], writes=[g1bc])

    NA = C_G
    hmT_all = k.sbuf([128, NT, 4, 128], BF16, "hmT_all")
    yrgT_all = k.sbuf([128, NT, 4, 128], BF16, "yrgT_all")
    mark_1a = k.bot
    W_in = k.sbuf([128, 8, NA], BF16, "W_in")
    Wl_w2 = k.sbuf([64, RW], BF16, "Wl_w2")
    Wl_a2 = k.sbuf([128, RW], BF16, "Wl_a2")
    Wl_g2 = k.sbuf([128, RW], BF16, "Wl_g2")
    GRP = {"g0": (0, 1024), "g1": (1024, 2056), "g2": (2056, 3848)}
    for g in ("g0", "g1", "g2"):
        a, b = GRP[g]
        for kh in range(2):
            k.dma("pool", W_in[:, 4 * kh:4 * kh + 4, a:b], d_w_in[:, 4 * kh:4 * kh + 4, a:b], reads=[d_w_in], writes=[(W_in, g)])
        if g == "g1":
            k.dma("pool", Wl_w2[:], d_rw2[:], reads=[d_rw2], writes=[Wl_w2])
            k.dma("pool", Wl_a2[:], d_ra2[:], reads=[d_ra2], writes=[Wl_a2])
            k.dma("pool", Wl_g2[:], d_rg2[:], reads=[d_rg2], writes=[Wl_g2])

    def wgrp(col):
        for g, (a, b) in GRP.items():
            if a <= col < b:
                return g

    pF = [k.psum([128, 512], F32, f"pF{i}") for i in range(2)]
    pR = [k.psum([128, 512], F32, f"pR{i}") for i in range(2)]
    pT = k.psum([128, 1024], BF16, "pT")
    pM = [k.psum([128, 512], F32, f"pM{i}") for i in range(3)]

    xt = [k.sbuf([128, D], F32, "xt0")] * 2
    hb = k.sbuf([128, D], BF16, "hb")
    hT = k.sbuf([128, 8, 128], BF16, "hT")
    ss = k.sbuf([128, 1], F32, "ss")
    rs = k.sbuf([128, 1], F32, "rs")
    ext_q = k.sbuf([128, 8, 131], F32, "ext_q")
    cq = k.sbuf([128, 8, 3], F32, "cq")
    _eqf = ext_q[:].rearrange("p a b -> p (a b)")
    cv = k.sbuf([128, 8, 128], F32, "cv")
    qkT = k.sbuf([128, 8, 128], BF16, "qkT")
    soT = k.sbuf([128, 4, 128], F32, "soT")
    vaug = k.sbuf([128, 4, 130], BF16, "vaug")
    Cst = k.sbuf([128, 4, 129], F32, "Cst")
    Cb = k.sbuf([128, 4, 130], BF16, "Cb")
    gsm = [k.sbuf([4, 128], F32, f"gsm{i}") for i in range(8)]
    gpk = k.sbuf([4, 3, 128], F32, "gpk")
    mst = k.sbuf([4, 16], F32, "mst")
    mnew = k.sbuf([4, 16], F32, "mnew")
    gt = [k.sbuf([4, 16], F32, f"gt{i}") for i in range(4)]
    s0d = k.sbuf([4, 4, 16], F32, "s0d")
    tokS = k.sbuf([128, 12], F32, "tokS")
    s0bc = k.sbuf([128, 64], F32, "s0bc")
    _pk = _eqf[:, 512:1024].bitcast(BF16).rearrange("p (a b c) -> p a b c", a=2, b=4)
    PTm = ext_q.view(_pk[:, 0, :, :], "PTm")
    ktm = ext_q.view(_pk[:, 1, :, :], "ktm")
    dn = k.sbuf([128, 4], F32, "dn")
    hm = ext_q.view(_eqf[:, 0:512].rearrange("p (a b) -> p a b", a=4), "hm")
    hn = hm
    bst = k.sbuf([128, 4, 6], F32, "bst")
    bag = k.sbuf([128, 4, 2], F32, "bag")
    zq_tm = cv.view(cv[:].rearrange("p a b -> p (a b)"), "zq_tm")

    ext_r = k.sbuf([128, 14, 129], F32, "ext_r")
    cr = k.sbuf([128, 14, 1], F32, "cr")
    _erf = ext_r[:].rearrange("p a b -> p (a b)")
    xm = k.sbuf([128, 14, 128], F32, "xm")
    thad = k.sbuf([128, 128], BF16, "thad")
    sgd = k.sbuf([128, 128], BF16, "sgd")
    bst8 = k.sbuf([128, 8, 6], F32, "bst8")
    bag8 = k.sbuf([128, 8, 2], F32, "bag8")
    mark_rw = k.bot
    rt = [k.sbuf([128, 4, 128], F32, f"rt{i}") for i in range(7)]
    rt.append(cv.view(cv[:, 0:4, :], "rt7"))
    rt.append(cv.view(cv[:, 4:8, :], "rt8"))
    gTs = ext_r.view(_erf[:, 0:512].rearrange("p (a b) -> p a b", a=4), "gTs")
    bonT = ext_r.view(_erf[:, 512:1024].rearrange("p (a b) -> p a b", a=4), "bonT")
    ART = k.sbuf([128, 4, 2, 128], BF16, "ART")
    BTb = k.sbuf([128, 4, 128], BF16, "BTb")
    KTb = k.sbuf([128, 4, 128], BF16, "KTb")
    VTb = k.sbuf([128, 4, 128], BF16, "VTb")
    AB_tm = k.sbuf([128, 2, 512], BF16, "AB_tm")
    KV_tm = k.sbuf([128, 2, 512], BF16, "KV_tm")
    GBm = k.sbuf([128, 4, 256], BF16, "GBm")
    GKm = k.sbuf([128, 4, 256], BF16, "GKm")
    Nn = k.sbuf([128, 4, 128], BF16, "Nn")
    GBm_b = k.sbuf([128, 4, 256], BF16, "GBm_b")
    GKm_b = k.sbuf([128, 4, 256], BF16, "GKm_b")
    Nn_b = k.sbuf([128, 4, 128], BF16, "Nn_b")
    PP_b = [k.sbuf([128, 4, 256], BF16, f"PPb{i}") for i in range(2)]
    XX_b = [k.sbuf([128, 4, 128], BF16, f"XXb{i}") for i in range(2)]
    PP = [k.sbuf([128, 4, 256], BF16, f"PP{i}") for i in range(2)]
    XX = [k.sbuf([128, 4, 128], BF16, f"XX{i}") for i in range(2)]
    QT = k.sbuf([128, 2, 128], BF16, "QT")
    IE = k.sbuf([128, 4, 64], F32, "IE")
    STf = k.sbuf([128, 4, 64], F32, "STf")
    STb = k.sbuf([128, 4, 64], BF16, "STb")
    yn = ext_r.view(_erf[:, 1024:1536].rearrange("p (a b) -> p a b", a=8), "yn")
    k.memset("pool", (STf, STf[:]), 0.0)
    k.memset("pool", (STb, STb[:]), 0.0)
    k.memset("pool", (cr, cr[:]), 0.0)
    k.memset("pool", (vaug, vaug[:]), 1.0)
    k.memset("pool", (Cst, Cst[:]), 0.0)
    k.memset("pool", (Cb, Cb[:]), 0.0)
    k.memset("pool", (mst, mst[:]), 0.0)
    k.memset("pool", (cq, cq[:]), 0.0)
    LNK = math.log(KSCALE)

    def x_rows(ti):
        if ti < NTP:
            return xp, xp[ti * 128:(ti + 1) * 128, :]
        return xs, xs[:, :]

    def norm_to_hT(xbuf, gbc):
        k.act((hb, hb[:]), (xbuf, xbuf[:]), AF.Square, accum=(ss, ss[:]))
        k.ts("dve", (rs, rs[:]), (ss, ss[:]), 1.0 / D, EPS, op0=ALU.mult, op1=ALU.add)
        k.act((rs, rs[:]), (rs, rs[:]), AF.Ln)
        k.act((rs, rs[:]), (rs, rs[:]), AF.Exp, scale=-0.5)
        k.stt("dve", (hb, hb[:]), (xbuf, xbuf[:]), (rs, rs[:, 0:1]), (gbc, gbc[:]), ALU.mult, ALU.mult)
        for kc in range(8):
            k.tr((pT, pT[:, kc * 128:(kc + 1) * 128]), (hb, hb[:, kc * 128:(kc + 1) * 128]), (identb, identb[:]))
        k.copy("act", (hT, hT[:].rearrange("p a b -> p (a b)")), (pT, pT[:, :]))

    def proj_fm(ps, ps_ap, col, M=128):
        g = wgrp(col)
        for kc in range(8):
            k.mm((ps, ps_ap), (W_in, W_in[:, kc, col:col + M], g), (hT, hT[:, kc, :]), start=(kc == 0), stop=(kc == 7))

    def proj_tm(ps, ps_ap, col, N):
        g = wgrp(col)
        for kc in range(8):
            k.mm((ps, ps_ap), (hT, hT[:, kc, :]), (W_in, W_in[:, kc, col:col + N], g), start=(kc == 0), stop=(kc == 7))

    prefetched = set()

    def prefetch_gen(tn):
        xb = xt[tn % 2]
        xd, xap = x_rows(tn)
        k.dma("sp", xb[:], xap, reads=[xd], writes=[xb])
        norm_to_hT(xb, g1bc)
        yield
        k.copy("pool", (ext_q, ext_q[:, :, 0:3]), (cq, cq[:]))
        for g in range(2):
            for c in range(4):
                proj_fm(pM[2], pM[2][:, c * 128:(c + 1) * 128], C_QK + (4 * g + c) * 128)
                if c == 1:
                    yield
            k.copy("act", (ext_q, ext_q[:, 4 * g:4 * g + 4, 3:131]), (pM[2], pM[2][:].rearrange("p (c t) -> p c t", c=4)))
            yield
        k.copy("pool", (cq, cq[:]), (ext_q, ext_q[:, :, 128:131]))
        yield
        proj_tm(pM[2], pM[2][:, :], C_V, 512)
        k.copy("act", (vaug, vaug[:, :, 0:128]), (pM[2], pM[2][:].rearrange("p (h c) -> p h c", h=4)))
        yield
        for c in range(4):
            proj_fm(pM[2], pM[2][:, c * 128:(c + 1) * 128], C_O + c * 128)
            if c == 1:
                yield
        k.act((soT, soT[:].rearrange("p a b -> p (a b)")), (pM[2], pM[2][:, :]), AF.Sigmoid)

    def mixer_tile(ti):
        smp = ti == NTP
        mi = 1 if smp else 0
        NB = NSB if smp else 1
        LB = ST if smp else 128
        xb = xt[ti % 2]
        xd, xap = x_rows(ti)
        if smp:
            k.barrier()
            k.bot = mark_rw
            Cs = k.sbuf([128, NSB, 129], F32, "Cs")
            Csb = k.sbuf([128, NSB, 130], BF16, "Csb")
            qTm = k.sbuf([128, NSB, 128], BF16, "qTm")
            ktmb = k.sbuf([128, NSB, 128], BF16, "ktmb")
            blkF = k.sbuf([128, NSB, 128], BF16, "blkF")
            rowm = k.sbuf([128, NSB], F32, "rowm")
            k.memset("pool", (blkF, blkF[:]), 1.0)
            aff(blkF, blkF[:], [[-ST, NSB], [1, 128]], 0, 0)
            aff(blkF, blkF[:], [[ST, NSB], [-1, 128]], 0, ST - 1)
            k.memset("pool", (rowm, rowm[:]), 1.0)
            aff(rowm, rowm[:], [[-ST, NSB]], 1, 0)
            aff(rowm, rowm[:], [[ST, NSB]], -1, ST - 1)
            smc = cv.view(cv[:].rearrange("p a b -> p (a b)")[0:NSB * 3, :], "smc")
            ext_s = xm.view(xm[:].rearrange("p a b -> p (a b)")[:, 0:8 * NSB * 11].rearrange("p (c b t) -> p c b t", c=8, b=NSB), "ext_s")
            k.dma("sp", smc[:], st_mconv[:, :], reads=[st_mconv], writes=[smc])
            for c in range(8):
                k.tr((pM[0], pM[0][:, c * 48:(c + 1) * 48]), (smc, smc[:, c * 128:(c + 1) * 128]), (identf, identf[0:48, 0:48]))
            k.copy("act", (ext_s, ext_s[:, :, :, 0:3]), (pM[0], pM[0][:, 0:384].rearrange("p (c b j) -> p c b j", c=8, b=NSB)))
            k.dma("sp", mst[:, 0:NSB], st_mm[:, :].rearrange("b h -> h b"), reads=[st_mm], writes=[mst], allow_slow_non_contiguous=True)
        if ti not in prefetched:
            k.dma("sp", xb[:], xap, reads=[xd], writes=[xb])
            norm_to_hT(xb, g1bc)

        if smp:
            for g in range(2):
                for c in range(4):
                    proj_fm(pF[g], pF[g][:, c * 128:(c + 1) * 128], C_QK + (4 * g + c) * 128)
                k.copy("act", (ext_s, ext_s[:, 4 * g:4 * g + 4, :, 3:11]), (pF[g], pF[g][:].rearrange("p (c b t) -> p c b t", c=4, b=NSB)))
            for c in range(8):
                cvv = cv[:, c, :].rearrange("p (b t) -> p b t", t=ST)
                k.ts("dve", (cv, cvv), (ext_s, ext_s[:, c, :, 3:11]), pcol(PT_MCW + 3 * 8 + c), pcol(PT_MCB + c),
                     op0=ALU.mult, op1=ALU.add)
                for j in range(3):
                    k.stt("dve", (cv, cvv), (ext_s, ext_s[:, c, :, j:j + ST]), pcol(PT_MCW + j * 8 + c), (cv, cvv),
                          ALU.mult, ALU.add)
        if not smp:
            if ti not in prefetched:
                k.copy("pool", (ext_q, ext_q[:, :, 0:3]), (cq, cq[:]))
                for g in range(2):
                    for c in range(4):
                        proj_fm(pF[g], pF[g][:, c * 128:(c + 1) * 128], C_QK + (4 * g + c) * 128)
                    k.copy("act", (ext_q, ext_q[:, 4 * g:4 * g + 4, 3:131]), (pF[g], pF[g][:].rearrange("p (c t) -> p c t", c=4)))
                k.copy("pool", (cq, cq[:]), (ext_q, ext_q[:, :, 128:131]))
            for c in range(8):
                k.ts("dve", (cv, cv[:, c, :]), (ext_q, ext_q[:, c, 3:131]), pcol(PT_MCW + 3 * 8 + c), pcol(PT_MCB + c),
                     op0=ALU.mult, op1=ALU.add)
                for j in range(3):
                    k.stt("dve", (cv, cv[:, c, :]), (ext_q, ext_q[:, c, j:j + 128]), pcol(PT_MCW + j * 8 + c), (cv, cv[:, c, :]),
                          ALU.mult, ALU.add)
        k.act((qkT, qkT[:].rearrange("p a b -> p (a b)")), (cv, cv[:].rearrange("p a b -> p (a b)")), AF.Silu)

        if ti not in prefetched:
            proj_tm(pR[0], pR[0][:, :], C_V, 512)
            k.copy("act", (vaug, vaug[:, :, 0:128]), (pR[0], pR[0][:].rearrange("p (h c) -> p h c", h=4)))
            for c in range(4):
                proj_fm(pF[0], pF[0][:, c * 128:(c + 1) * 128], C_O + c * 128)
            k.act((soT, soT[:].rearrange("p a b -> p (a b)")), (pF[0], pF[0][:, :]), AF.Sigmoid)
        if not smp and stage >= 2:
            rwkv_front_proj(ti)
        proj_fm(pM[0], pM[0][0:4, 0:128], C_I, M=4)
        proj_fm(pM[0], pM[0][0:4, 128:256], C_F, M=4)
        liT, nlf, ncum, gT_, t0, t1 = gsm[0], gsm[1], gsm[2], gsm[3], gsm[4], gsm[5]
        k.ts("dve", (liT, liT[:]), (pM[0], pM[0][0:4, 0:128]), (gb, gb[:, 0:1]), None, op0=ALU.add)
        k.act((t0, t0[:]), (pM[0], pM[0][0:4, 128:256]), AF.Exp, bias=(nbf, nbf[:, 0:1]), scale=-1.0)
        k.act((nlf, nlf[:]), (t0, t0[:]), AF.Ln, bias=1.0)
        k.scan("dve", (ncum, ncum[:]), (resets[mi], resets[mi][0:4, 0:128]), (nlf, nlf[:]), 0.0, ALU.mult, ALU.add)
        k.tt("dve", (gT_, gT_[:]), (liT, liT[:]), (ncum, ncum[:]), ALU.add)
        b3 = lambda buf: buf[:].rearrange("p (b l) -> p b l", l=LB)
        mcb_ = mst[:, 0:NB].unsqueeze(2).to_broadcast([4, NB, LB])
        nlast = ncum[:].rearrange("p (b l) -> p b l", l=LB)[:, :, LB - 1:LB]
        k.stt("dve", (t0, b3(t0)), (gT_, b3(gT_)), LNK, (mst, mcb_), ALU.add, ALU.subtract)
        k.act((gpk, gpk[:, 0, :]), (t0, t0[:]), AF.Exp)
        k.tt("dve", (t1, b3(t1)), (ncum, b3(ncum)), (mst, mcb_), ALU.subtract)
        k.act((gpk, gpk[:, 1, :]), (t1, t1[:]), AF.Exp)
        k.tt("dve", (t1, b3(t1)), (gT_, b3(gT_)), (ncum, nlast.to_broadcast([4, NB, LB])), ALU.subtract)
        k.red("dve", (gt[0], gt[0][:, 0:NB]), (t1, b3(t1)), ALU.max)
        k.tt("dve", (gt[1], gt[1][:, 0:NB]), (mst, mst[:, 0:NB]), (ncum, nlast.rearrange("p b o -> p (b o)")), ALU.subtract)
        k.tt("dve", (mnew, mnew[:, 0:NB]), (gt[1], gt[1][:, 0:NB]), (gt[0], gt[0][:, 0:NB]), ALU.max)
        k.tt("dve", (gt[2], gt[2][:, 0:NB]), (gt[1], gt[1][:, 0:NB]), (mnew, mnew[:, 0:NB]), ALU.subtract)
        k.act((gt[3], gt[3][:, 0:NB]), (gt[2], gt[2][:, 0:NB]), AF.Exp)
        k.tt("dve", (gpk, gpk[:, 2, :].rearrange("p (b l) -> p b l", l=LB)), (gpk, gpk[:, 0, :].rearrange("p (b l) -> p b l", l=LB)),
             (gt[3], gt[3][:, 0:NB].unsqueeze(2).to_broadcast([4, NB, LB])), ALU.mult)
        for j in range(3):
            k.tr((pM[1], pM[1][:, 4 * j:4 * j + 4]), (gpk, gpk[:, j, :]), (identf, identf[0:4, 0:4]))
        k.copy("dve", (tokS, tokS[:]), (pM[1], pM[1][:, 0:12]))
        k.tt("dve", (s0d, s0d[:, :, 0:NB]), (identf, identf[0:4, 0:4].unsqueeze(2).to_broadcast([4, 4, NB])),
             (gt[3], gt[3][:, 0:NB].unsqueeze(1).to_broadcast([4, 4, NB])), ALU.mult)
        k.mm((pM[1], pM[1][:, 16:16 + 4 * NB]), (ones4, ones4[:]), (s0d, s0d[:, :, 0:NB].rearrange("p a b -> p (a b)")))
        k.copy("dve", (s0bc, s0bc[:, 0:4 * NB]), (pM[1], pM[1][:, 16:16 + 4 * NB]))

        for h in range(4):
            k.mm((pM[0], pM[0][:, h * 128:(h + 1) * 128]), (qkT, qkT[:, 4 + h, :]), (qkT, qkT[:, h, :]))
        for h in range(4):
            k.stt("dve", (PTm, PTm[:, h, :]), (pM[0], pM[0][:, h * 128:(h + 1) * 128]), (tokS, tokS[:, h:h + 1]),
                  (mU_in[mi], mU_in[mi][:]), ALU.mult, ALU.mult)
        pO = [pM[1], pM[2]]
        oap = lambda h: pO[h // 2][:, 256 * (h % 2):256 * (h % 2) + 129]
        if not smp:
            for h in range(4):
                k.mm((pO[h // 2], oap(h)), (qkT, qkT[:, h, :]), (Cb, Cb[:, h, 0:129]), start=True, stop=False)
                k.mm((pO[h // 2], oap(h)), (PTm, PTm[:, h, :]), (vaug, vaug[:, h, 0:129]), start=False, stop=True)
        else:
            for h in range(4):
                k.tr((pT, pT[:, h * 128:(h + 1) * 128]), (qkT, qkT[:, 4 + h, :]), (identb, identb[:]))
            for h in range(4):
                k.ts("dve", (ktm, ktm[:, h, :]), (pT, pT[:, h * 128:(h + 1) * 128]), (tokS, tokS[:, 8 + h:9 + h]), None, op0=ALU.mult)
            for h in range(4):
                k.dma("sp", Cs[:, :, 0:128], st_mC[:, h, :, :].rearrange("b d v -> d b v"), reads=[st_mC], writes=[Cs])
                k.dma("sp", Cs[:, :, 128], st_mn[:, h, :].rearrange("b d -> d b"), reads=[st_mn], writes=[Cs], allow_slow_non_contiguous=True)
                k.copy("act", (Csb, Csb[:, :, 0:129]), (Cs, Cs[:]))
                k.tt("dve", (qTm, qTm[:]), (qkT, qkT[:, h, :].unsqueeze(1).to_broadcast([128, NSB, 128])), (blkF, blkF[:]), ALU.mult)
                for b in range(NSB):
                    k.mm((pO[h // 2], oap(h)), (qTm, qTm[:, b, :]), (Csb, Csb[:, b, 0:129]), start=(b == 0), stop=False)
                k.mm((pO[h // 2], oap(h)), (PTm, PTm[:, h, :]), (vaug, vaug[:, h, 0:129]), start=False, stop=True)
                k.tt("dve", (ktmb, ktmb[:]), (ktm, ktm[:, h, :].unsqueeze(1).to_broadcast([128, NSB, 128])),
                     (rowm, rowm[:].unsqueeze(2).to_broadcast([128, NSB, 128])), ALU.mult)
                for grp in range(4):
                    bank = pF[grp % 2]
                    for bi in range(4):
                        b = 4 * grp + bi
                        k.mm((bank, bank[:, bi * 128:(bi + 1) * 128]), (ktmb, ktmb[:, b, :]), (vaug, vaug[:, h, 0:128]))
                    for bi in range(4):
                        b = 4 * grp + bi
                        k.stt("dve", (Cs, Cs[:, b, 0:128]), (Cs, Cs[:, b, 0:128]), (s0bc, s0bc[:, h * NSB + b:h * NSB + b + 1]),
                              (bank, bank[:, bi * 128:(bi + 1) * 128]), ALU.mult, ALU.add)
                for b in range(NSB):
                    k.mm((pR[0], pR[0][:, b:b + 1]), (ktmb, ktmb[:, b, :]), (vaug, vaug[:, h, 128:129]))
                k.tt("dve", (Cs, Cs[:, :, 128]), (Cs, Cs[:, :, 128]), (s0bc, s0bc[:, h * NSB:(h + 1) * NSB]), ALU.mult)
                k.tt("dve", (Cs, Cs[:, :, 128]), (Cs, Cs[:, :, 128]), (pR[0], pR[0][:, 0:NSB]), ALU.add)
                k.dma("sp", o_sC[:, h, :, :].rearrange("b d v -> d b v"), Cs[:, :, 0:128], reads=[Cs], writes=[o_sC])
                k.dma("sp", o_sn[:, h, :].rearrange("b d -> d b"), Cs[:, :, 128], reads=[Cs], writes=[o_sn], allow_slow_non_contiguous=True)
        for h in range(4):
            k.copy("act", (dn, dn[:, h:h + 1]), (pO[h // 2], oap(h)[:, 128:129]))
        k.stt("dve", (dn, dn[:]), (dn, dn[:]), -1.0, (dn, dn[:]), ALU.mult, ALU.max)
        k.tt("dve", (dn, dn[:]), (dn, dn[:]), (tokS, tokS[:, 4:8]), ALU.max)
        k.op("dve", lambda e: e.reciprocal(out=dn[:], in_=dn[:]), reads=[dn], writes=[dn])
        for h in range(4):
            k.act((hm, hm[:, h, :]), (pO[h // 2], oap(h)[:, 0:128]), AF.Copy, scale=(dn, dn[:, h:h + 1]))
        for h in range(4):
            k.op("dve", lambda e, h=h: e.bn_stats(out=bst[:, h, :], in_=hm[:, h, :]), reads=[hm], writes=[(bst, h)])
        for h in range(4):
            k.op("dve", lambda e, h=h: e.bn_aggr(out=bag[:, h, :], in_=bst[:, h, :]), reads=[(bst, h)], writes=[(bag, h)])
        k.act((bag, bag[:, :, 1:2]), (bag, bag[:, :, 1:2]), AF.Ln, bias=EPS)
        k.act((bag, bag[:, :, 1:2]), (bag, bag[:, :, 1:2]), AF.Exp, scale=-0.5)
        for h in range(4):
            k.ts("dve", (hn, hn[:, h, :]), (hm, hm[:, h, :]), (bag, bag[:, h, 0:1]), (bag, bag[:, h, 1:2]),
                 op0=ALU.subtract, op1=ALU.mult)
        for h in range(4):
            k.tr((pM[0], pM[0][:, h * 128:(h + 1) * 128]), (hn, hn[:, h, :]), (identf, identf[:]))
        for h in range(4):
            k.stt("dve", (hmT_all, hmT_all[:, ti, h, :], ti), (pM[0], pM[0][:, h * 128:(h + 1) * 128]), pcol(PT_MNG + h),
                  (soT, soT[:, h, :]), ALU.mult, ALU.mult)
        if not smp:
            for h in range(4):
                k.tr((pT, pT[:, h * 128:(h + 1) * 128]), (qkT, qkT[:, 4 + h, :]), (identb, identb[:]))
            for h in range(4):
                k.ts("dve", (ktm, ktm[:, h, :]), (pT, pT[:, h * 128:(h + 1) * 128]), (tokS, tokS[:, 8 + h:9 + h]), None, op0=ALU.mult)
            for h in range(4):
                k.mm((pO[h // 2], oap(h)), (ktm, ktm[:, h, :]), (vaug, vaug[:, h, 0:129]))
            for h in range(4):
                k.stt("dve", (Cst, Cst[:, h, :]), (Cst, Cst[:, h, :]), (s0bc, s0bc[:, h:h + 1]), (pO[h // 2], oap(h)),
                      ALU.mult, ALU.add)
            k.copy("act", (Cb, Cb[:, :, 0:129]), (Cst, Cst[:]))
            k.copy("dve", (mst, mst[:, 0:1]), (mnew, mnew[:, 0:1]))
        if ti == NTP - 1:
            for h in range(4):
                k.dma("sp", o_pC[h], Cst[:, h, 0:128], reads=[Cst], writes=[o_pC])
            k.dma("sp", o_pn[:].rearrange("h d -> d h"), Cst[:, :, 128], reads=[Cst], writes=[o_pn], allow_slow_non_contiguous=True)
            k.dma("sp", o_pm[:].rearrange("o h -> h o"), mnew[:, 0:1], reads=[mnew], writes=[o_pm], allow_slow_non_contiguous=True)
            for blk in range(2):
                proj_tm(pR[blk], pR[blk][:, :], C_QK + blk * 512, 512)
                k.copy("act", (zq_tm, zq_tm[:, blk * 512:(blk + 1) * 512]), (pR[blk], pR[blk][:, :]))
            k.dma("sp", o_pconv[:], zq_tm[125:128, :], reads=[zq_tm], writes=[o_pconv])
        if smp:
            k.dma("sp", o_sm[:, :].rearrange("b h -> h b"), mnew[:, 0:NSB], reads=[mnew], writes=[o_sm], allow_slow_non_contiguous=True)
            for blk in range(2):
                proj_tm(pR[blk], pR[blk][:, :], C_QK + blk * 512, 512)
                k.copy("act", (zq_tm, zq_tm[:, blk * 512:(blk + 1) * 512]), (pR[blk], pR[blk][:, :]))
            for b in range(NSB):
                k.dma("sp", o_sconv[3 * b:3 * b + 3, :], zq_tm[ST * b + 5:ST * b + 8, :], reads=[zq_tm], writes=[o_sconv])

    k.fence_mm = (pT, identb)
    BK = [pM[0], pM[1], pF[0], pF[1], pR[0], pR[1]]

    _rw_stop = int(_os.environ.get("KDBG_RW", "99"))

    def rwkv_front_proj(ti):
        k.copy("pool", (ext_r, ext_r[:, :, 0:1]), (cr, cr[:]))
        for g in range(4):
            n = min(4, 14 - 4 * g)
            for c in range(n):
                proj_fm(pF[g % 2], pF[g % 2][:, c * 128:(c + 1) * 128], C_R + (4 * g + c) * 128)
            k.copy("act", (ext_r, ext_r[:, 4 * g:4 * g + n, 1:129]),
                   (pF[g % 2], pF[g % 2][:, 0:n * 128].rearrange("p (c t) -> p c t", c=n)))
        k.copy("pool", (cr, cr[:]), (ext_r, ext_r[:, :, 128:129]))
        k.tt("pool", (xm, xm[:]), (ext_r, ext_r[:, :, 0:128]), (ext_r, ext_r[:, :, 1:129]), ALU.subtract)
        for c in range(14):
            k.stt("dve", (xm, xm[:, c, :]), (xm, xm[:, c, :]), pcol(PT_RMIX + c), (ext_r, ext_r[:, c, 1:129]), ALU.mult, ALU.add)

    def rwkv_tile(ti):
        smp = ti == NTP
        mi = 0
        NLV = 7
        rtl = rt
        if not smp:
            pass
        else:
            k.barrier()
            k.bot = mark_rw
            ext_rs = k.sbuf([128, 14, NSB, ST + 1], F32, "ext_rs")
            rtl = [k.sbuf([128, 4, 128], F32, f"rts{i}") for i in range(7)] + [rt[7], rt[8]]
            stg = k.sbuf([128, 512], F32, "stg")
            srs = xm.view(xm[:].rearrange("p a b -> p (a b)")[0:NSB, :], "srs")
            k.dma("sp", srs[:], st_rshift[:, :], reads=[st_rshift], writes=[srs])
            for c in range(14):
                k.tr((pM[0], pM[0][:, c * NSB:(c + 1) * NSB]), (srs, srs[:, c * 128:(c + 1) * 128]), (identf, identf[0:NSB, 0:NSB]))
            k.copy("act", (ext_rs, ext_rs[:, :, :, 0]), (pM[0], pM[0][:, 0:14 * NSB].rearrange("p (c b) -> p c b", c=14)))
            for g in range(4):
                n = min(4, 14 - 4 * g)
                for c in range(n):
                    proj_fm(pF[g % 2], pF[g % 2][:, c * 128:(c + 1) * 128], C_R + (4 * g + c) * 128)
                k.copy("act", (ext_rs, ext_rs[:, 4 * g:4 * g + n, :, 1:ST + 1]),
                       (pF[g % 2], pF[g % 2][:, 0:n * 128].rearrange("p (c b t) -> p c b t", c=n, b=NSB)))
            xm4 = xm[:].rearrange("p c (b t) -> p c b t", t=ST)
            k.tt("pool", (xm, xm4), (ext_rs, ext_rs[:, :, :, 0:ST]), (ext_rs, ext_rs[:, :, :, 1:ST + 1]), ALU.subtract)
            for c in range(14):
                k.stt("dve", (xm, xm4[:, c]), (xm, xm4[:, c]), pcol(PT_RMIX + c), (ext_rs, ext_rs[:, c, :, 1:ST + 1]), ALU.mult, ALU.add)
        rT, krT, vrT = xm[:, 0:4, :], xm[:, 4:8, :], xm[:, 8:12, :]
        sig, cums, gam, ginv, gexc, a_, kk, tmp, kr2 = rtl
        if _rw_stop <= 1:
            return
        k.act((thad, thad[0:64, :]), (xm, xm[0:64, 12, :]), AF.Tanh)
        k.copy("act", (thad, thad[64:128, :]), (xm, xm[64:128, 12, :]))
        k.act((sgd, sgd[:]), (xm, xm[:, 13, :]), AF.Sigmoid)
        for c in range(4):
            k.mm((pM[0], pM[0][:, c * 128:(c + 1) * 128]), (Wl_w2, Wl_w2[0:64, c * 128:(c + 1) * 128]), (thad, thad[0:64, :]))
        for c in range(4):
            k.act((sig, sig[:, c, :]), (pM[0], pM[0][:, c * 128:(c + 1) * 128]), AF.Sigmoid, bias=pcol(PT_RW0 + c))
        k.pe_fence()
        for c in range(4):
            k.mm((pM[1], pM[1][:, c * 128:(c + 1) * 128]), (Wl_a2, Wl_a2[64:128, c * 128:(c + 1) * 128]), (thad, thad[64:128, :]))
        k.pe_fence()
        for c in range(4):
            k.act((a_, a_[:, c, :]), (pM[1], pM[1][:, c * 128:(c + 1) * 128]), AF.Sigmoid, bias=pcol(PT_RA0 + c))
        for c in range(4):
            k.mm((pM[2], pM[2][:, c * 128:(c + 1) * 128]), (Wl_g2, Wl_g2[:, c * 128:(c + 1) * 128]), (sgd, sgd[:]))
        k.copy("act", (gTs, gTs[:].rearrange("p a b -> p (a b)")), (pM[2], pM[2][:, :]))
        if _rw_stop <= 2:
            return
        fl = lambda b: b[:].rearrange("p a b -> p (a b)")
        if not smp:
            k.scan("dve", (cums, fl(cums)), (resets[mi], resets[mi][:]), (sig, fl(sig)), 0.0, ALU.mult, ALU.add)
            k.act((gam, fl(gam)), (cums, fl(cums)), AF.Exp, scale=WSCALE)
            k.act((ginv, fl(ginv)), (cums, fl(cums)), AF.Exp, scale=-WSCALE)
            k.tt("dve", (tmp, tmp[:]), (cums, cums[:]), (sig, sig[:]), ALU.subtract)
            k.act((gexc, fl(gexc)), (tmp, fl(tmp)), AF.Exp, scale=WSCALE)
        else:
            k.act((gam, fl(gam)), (sig, fl(sig)), AF.Exp, scale=WSCALE)
        if _rw_stop <= 3:
            return
        for c in range(4):
            k.ts("dve", (kk, kk[:, c, :]), (xm, xm[:, 4 + c, :]), pcol(PT_RKK + c), None, op0=ALU.mult)
        k.tt("dve", (tmp, tmp[:]), (kk, kk[:]), (kk, kk[:]), ALU.mult)
        for c in range(4):
            k.mm((pM[0], pM[0][:, c * 128:(c + 1) * 128]), (bones, bones[:]), (tmp, tmp[:, c, :]))
        k.ts("dve", (tmp, fl(tmp)), (pM[0], pM[0][:, :]), 1e-24, None, op0=ALU.max)
        k.act((tmp, fl(tmp)), (tmp, fl(tmp)), AF.Ln)
        k.act((tmp, fl(tmp)), (tmp, fl(tmp)), AF.Exp, scale=-0.5)
        k.tt("dve", (kk, kk[:]), (kk, kk[:]), (tmp, tmp[:]), ALU.mult)
        for c in range(4):
            k.ts("dve", (tmp, tmp[:, c, :]), (a_, a_[:, c, :]), -1.0, pcol(PT_RKA + c), op0=ALU.add, op1=ALU.mult)
        k.stt("dve", (kr2, kr2[:]), (tmp, tmp[:]), 1.0, (xm, krT), ALU.add, ALU.mult)
        k.tt("dve", (tmp, tmp[:]), (xm, rT), (kr2, kr2[:]), ALU.mult)
        for c in range(4):
            k.ts("dve", (tmp, tmp[:, c, :]), (tmp, tmp[:, c, :]), pcol(PT_RRK + c), None, op0=ALU.mult)
        for c in range(4):
            k.mm((pM[1], pM[1][:, c * 128:(c + 1) * 128]), (bones, bones[:]), (tmp, tmp[:, c, :]))
        k.tt("dve", (bonT, fl(bonT)), (pM[1], pM[1][:, :]), (xm, vrT.rearrange("p a b -> p (a b)") if False else xm[:, 8:12, :].rearrange("p a b -> p (a b)")), ALU.mult)
        if _rw_stop <= 4:
            return
        if smp:
            rwkv_sample_core(xm, gam, kr2, kk, a_, tmp, stg, gTs, bonT)
            return
        k.stt("dve", (ART, ART[:, :, 0, :]), (kk, kk[:]), -1.0, (gexc, gexc[:]), ALU.mult, ALU.mult)
        k.tt("dve", (ART, ART[:, :, 1, :]), (xm, rT), (gam, gam[:]), ALU.mult)
        k.tt("dve", (tmp, tmp[:]), (kk, kk[:]), (a_, a_[:]), ALU.mult)
        k.tt("dve", (BTb, BTb[:]), (tmp, tmp[:]), (ginv, ginv[:]), ALU.mult)
        k.tt("dve", (KTb, KTb[:]), (kr2, kr2[:]), (ginv, ginv[:]), ALU.mult)
        k.copy("act", (VTb, VTb[:]), (xm, vrT))
        if _rw_stop <= 5:
            return
        for c in range(4):
            k.tr((pT, pT[:, c * 128:(c + 1) * 128]), (ART, ART[:, c, 0, :]), (identb, identb[:]))
            k.tr((pT, pT[:, 512 + c * 128:512 + (c + 1) * 128]), (BTb, BTb[:, c, :]), (identb, identb[:]))
        k.copy("act", (AB_tm, AB_tm[:].rearrange("p a b -> p (a b)")), (pT, pT[:, :]))
        for c in range(4):
            k.tr((pT, pT[:, c * 128:(c + 1) * 128]), (KTb, KTb[:, c, :]), (identb, identb[:]))
            k.tr((pT, pT[:, 512 + c * 128:512 + (c + 1) * 128]), (VTb, VTb[:, c, :]), (identb, identb[:]))
        k.copy("dve", (KV_tm, KV_tm[:].rearrange("p a b -> p (a b)")), (pT, pT[:, :]))
        if _rw_stop <= 6:
            return
        A_tm = lambda h: (AB_tm, AB_tm[:, 0, h * 64:(h + 1) * 64])
        B_tm = lambda h: (AB_tm, AB_tm[:, 1, h * 64:(h + 1) * 64])
        K_tm = lambda h: (KV_tm, KV_tm[:, 0, h * 64:(h + 1) * 64])
        V_tm = lambda h: (KV_tm, KV_tm[:, 1, h * 64:(h + 1) * 64])
        m2b = mask2[mi][:].unsqueeze(1).to_broadcast([128, 2, 256])
        GB2, GK2, Nn2, PP2, XX2 = [GBm, GBm_b], [GKm, GKm_b], [Nn, Nn_b], [PP, PP_b], [XX, XX_b]
        LB3 = [[pM[0], pM[1], pR[0]], [pF[0], pF[1], pR[1]]]
        for g in range(2):
            GBm_, GKm_, Nn_, XX_ = GB2[g], GK2[g], Nn2[g], XX2[g]
            heads = [4 * g + i for i in range(4)]
            HO = [(pbs, [(i, h) for i, h in enumerate(heads) if 64 * (h % 2) == pbs]) for pbs in (0, 64)]
            for pbs, hl in HO:
                for i, h in hl:
                    c, pb = h // 2, 64 * (h % 2)
                    off = (i % 2) * 256
                    rAR = (ART, ART[pb:pb + 64, c, :, :].rearrange("p a t -> p (a t)"))
                    k.mm((BK[i // 2], BK[i // 2][:, off:off + 256]), (BTb, BTb[pb:pb + 64, c, :]), rAR)
                    k.mm((BK[2 + i // 2], BK[2 + i // 2][:, off:off + 256]), (KTb, KTb[pb:pb + 64, c, :]), rAR)
                    k.mm((BK[4], BK[4][:, i * 128:(i + 1) * 128]), (ART, ART[pb:pb + 64, c, 0, :]), (BTb, BTb[pb:pb + 64, c, :]))
                k.pe_fence()
            for hf in range(2):
                k.tt("dve", (GBm_, GBm_[:, 2 * hf:2 * hf + 2, :]), (BK[hf], BK[hf][:].rearrange("p (a b) -> p a b", a=2)), (mask2[mi], m2b), ALU.mult)
                k.tt("dve", (GKm_, GKm_[:, 2 * hf:2 * hf + 2, :]), (BK[2 + hf], BK[2 + hf][:].rearrange("p (a b) -> p a b", a=2)), (mask2[mi], m2b), ALU.mult)
            k.tt("dve", (Nn_, Nn_[:]), (BK[4], BK[4][:].rearrange("p (a b) -> p a b", a=4)),
                 (mL_st[mi], mL_st[mi][:].unsqueeze(1).to_broadcast([128, 4, 128])), ALU.mult)
            for i, h in enumerate(heads):
                k.mm((BK[5], BK[5][:, i * 64:(i + 1) * 64]), (GKm_, GKm_[:, i, 0:128]), V_tm(h))
            k.copy("act", (XX_[0], XX_[0][:, :, 64:128]), (BK[5], BK[5][:, 0:256].rearrange("p (a b) -> p a b", a=4)))
            k.copy("pool", (XX_[0], XX_[0][:, :, 0:64]), (AB_tm, AB_tm[:, 0, 256 * g:256 * g + 256].rearrange("p (a b) -> p a b", a=4)))

        xfinal = [None, None]

        def levels_gen(g):
            GBm_, Nn_, PP_, XX_ = GB2[g], Nn2[g], PP2[g], XX2[g]
            bP, bQ, bX = LB3[g]
            Pc = lambda i: (Nn_, Nn_[:, i, :])
            PTc = lambda i: (GBm_, GBm_[:, i, 0:128])
            xi = 0
            for lvl in range(NLV):
                Xc, Xn = XX_[xi], XX_[1 - xi]
                for i in range(4):
                    o = (bX, bX[:, i * 128:(i + 1) * 128])
                    k.mm(o, (identb, identb[:]), (Xc, Xc[:, i, :]), start=True, stop=False)
                    k.mm(o, PTc(i), (Xc, Xc[:, i, :]), start=False, stop=True)
                k.copy("act", (Xn, Xn[:].rearrange("p a b -> p (a b)")), (bX, bX[:, :]))
                xi = 1 - xi
                yield
                if lvl < NLV - 1:
                    bb = [bP, bQ]
                    for i in range(4):
                        off = (i % 2) * 256
                        if lvl < NLV - 2:
                            k.mm((bb[i // 2], bb[i // 2][:, off:off + 128]), PTc(i), Pc(i))
                        k.mm((bb[i // 2], bb[i // 2][:, off + 128:off + 256]), Pc(i), PTc(i))
                    PPn = PP_[lvl % 2]
                    for hf in range(2):
                        if lvl < NLV - 2:
                            k.copy(("dve", "act")[hf], (PPn, PPn[:, 2 * hf:2 * hf + 2, :]), (bb[hf], bb[hf][:].rearrange("p (a b) -> p a b", a=2)))
                        else:
                            k.copy(("dve", "act")[hf], (PPn, PPn[:, 2 * hf:2 * hf + 2, 128:256]),
                                   (bb[hf], bb[hf][:].rearrange("p (a b) -> p a b", a=2)[:, :, 128:256]))
                    Pc = lambda i, PPn=PPn: (PPn, PPn[:, i, 0:128])
                    PTc = lambda i, PPn=PPn: (PPn, PPn[:, i, 128:256])
                    yield
            xfinal[g] = XX_[xi]

        gens = [levels_gen(0), levels_gen(1)]
        if ti + 1 < NTP and (ti + 1) in tiles_run:
            gens.append(prefetch_gen(ti + 1))
            prefetched.add(ti + 1)
        while gens:
            for g_ in list(gens):
                try:
                    next(g_)
                except StopIteration:
                    gens.remove(g_)

        for g in range(2):
            heads = [4 * g + i for i in range(4)]
            HO = [(pbs, [(i, h) for i, h in enumerate(heads) if 64 * (h % 2) == pbs]) for pbs in (0, 64)]
            GBt, GKt = GB2[g], GK2[g]
            Xf = xfinal[g]
            if _rw_stop <= 8:
                continue
            k.pe_fence()
            for pbs, hl in HO:
                for i, h in hl:
                    c, pb = h // 2, 64 * (h % 2)
                    ci = i // 2
                    o = (BK[0], BK[0][pb:pb + 64, ci * 128:(ci + 1) * 128])
                    k.mm(o, (Xf, Xf[:, i, 0:64]), (GBt, GBt[:, i, 128:256]), start=True, stop=False)
                    k.pe_fence()
                    k.mm(o, (identb, identb[pb:pb + 64, pb:pb + 64]), (ART, ART[pb:pb + 64, c, 1, :]), start=False, stop=True)
                    k.pe_fence()
            k.copy("act", (QT, QT[:].rearrange("p a b -> p (a b)")), (BK[0], BK[0][:, 0:256]))
            for pbs, hl in HO:
                for i, h in hl:
                    c, pb = h // 2, 64 * (h % 2)
                    ci = i // 2
                    o = (pM[2], pM[2][:, h * 64:(h + 1) * 64])
                    k.mm(o, (QT, QT[pb:pb + 64, ci, :]), (STb, STb[pb:pb + 64, c, :]), start=True, stop=False)
                    k.pe_fence()
                    k.mm(o, (GBt, GBt[:, i, 128:256]), (Xf, Xf[:, i, 64:128]), start=False, stop=False)
                    k.mm(o, (GKt, GKt[:, i, 128:256]), V_tm(h), start=False, stop=True)
                    k.pe_fence()
            if _rw_stop <= 9:
                continue
            for pbs, hl in HO:
                for i, h in hl:
                    c, pb = h // 2, 64 * (h % 2)
                    ci = i // 2
                    k.mm((BK[1], BK[1][pb:pb + 64, ci * 64:(ci + 1) * 64]), (Xf, Xf[:, i, 0:64]), B_tm(h))
                k.pe_fence()
            k.tt("dve", (IE, IE[:, 2 * g:2 * g + 2, :]), (BK[1], BK[1][:, 0:128].rearrange("p (a b) -> p a b", a=2)),
                 (I2, I2[:].unsqueeze(1).to_broadcast([128, 2, 64])), ALU.add)
            for pbs, hl in HO:
                for i, h in hl:
                    c, pb = h // 2, 64 * (h % 2)
                    ci = i // 2
                    o = (BK[2], BK[2][pb:pb + 64, ci * 64:(ci + 1) * 64])
                    k.mm(o, (IE, IE[pb:pb + 64, c, :]), (STf, STf[pb:pb + 64, c, :]), start=True, stop=False)
                    k.pe_fence()
                    k.mm(o, B_tm(h), (Xf, Xf[:, i, 64:128]), start=False, stop=False)
                    k.mm(o, K_tm(h), V_tm(h), start=False, stop=True)
                    k.pe_fence()
            for ci in range(2):
                c = 2 * g + ci
                k.ts("dve", (STf, STf[:, c, :]), (BK[2], BK[2][:, ci * 64:(ci + 1) * 64]), (gam, gam[:, c, 127:128]), None, op0=ALU.mult)
            k.copy("act", (STb, STb[:, 2 * g:2 * g + 2, :]), (STf, STf[:, 2 * g:2 * g + 2, :]))
        if _rw_stop <= 10:
            return
        rwkv_epilogue(ti, pM[2], tmp)
        if ti == NTP - 1:
            for c in range(4):
                k.tr((pM[0], pM[0][0:64, c * 128:(c + 1) * 128]), (STf, STf[:, c, :]), (identf, identf[:]))
            k.copy("act", (rt[0], rt[0][0:64, :, :]), (pM[0], pM[0][0:64, :].rearrange("p (a b) -> p a b", a=4)))
            k.dma("sp", o_pS[:].rearrange("(h i) j -> i h j", h=8), rt[0][0:64, :, :].rearrange("p c (f j) -> p (c f) j", f=2),
                  reads=[rt[0]], writes=[o_pS])

    def rwkv_epilogue(ti, Yb, tmp):
        pM2 = [None, None, Yb]
        for h in range(8):
            k.op("dve", lambda e, h=h: e.bn_stats(out=bst8[:, h, :], in_=Yb[:, h * 64:(h + 1) * 64]), reads=[Yb], writes=[(bst8, h)])
        for h in range(8):
            k.op("dve", lambda e, h=h: e.bn_aggr(out=bag8[:, h, :], in_=bst8[:, h, :]), reads=[(bst8, h)], writes=[(bag8, h)])
        k.act((bag8, bag8[:, :, 1:2]), (bag8, bag8[:, :, 1:2]), AF.Ln, bias=GN_EPS)
        k.act((bag8, bag8[:, :, 1:2]), (bag8, bag8[:, :, 1:2]), AF.Exp, scale=-0.5)
        for h in range(8):
            k.ts("dve", (yn, yn[:, h, :]), (Yb, Yb[:, h * 64:(h + 1) * 64]), (bag8, bag8[:, h, 0:1]), (bag8, bag8[:, h, 1:2]),
                 op0=ALU.subtract, op1=ALU.mult)
        for c in range(4):
            k.tr((pM[0], pM[0][:, c * 128:(c + 1) * 128]), (yn, yn[:, 2 * c:2 * c + 2, :].rearrange("p a b -> p (a b)")), (identf, identf[:]))
        for c in range(4):
            k.ts("dve", (tmp, tmp[:, c, :]), (pM[0], pM[0][:, c * 128:(c + 1) * 128]), pcol(PT_RLNG + c), pcol(PT_RLNB + c),
                 op0=ALU.mult, op1=ALU.add)
        k.tt("pool", (tmp, tmp[:]), (tmp, tmp[:]), (bonT, bonT[:]), ALU.add)
        k.tt("dve", (yrgT_all, yrgT_all[:, ti, :, :], ti), (tmp, tmp[:]), (gTs, gTs[:]), ALU.mult)

    rsc = k.dram("rw_scratch", [6, 128, RW], F32)
    ysc = k.dram("ry_scratch", [128, RW], F32)

    def rwkv_sample_core(xm, dec, kr2, kk, a_, tmp, stg, gTs, bonT):
        ti = NTP
        srcs = []
        srcs.append((xm, lambda c: xm[:, c, :]))
        srcs.append((dec, lambda c: dec[:, c, :]))
        srcs.append((kr2, lambda c: kr2[:, c, :]))
        srcs.append((xm, lambda c: xm[:, 8 + c, :]))
        for q in range(6):
            if q == 4:
                k.ts("dve", (tmp, tmp[:]), (kk, kk[:]), -1.0, None, op0=ALU.mult)
                sb_, fn = tmp, (lambda c: tmp[:, c, :])
            elif q == 5:
                k.tt("dve", (tmp, tmp[:]), (kk, kk[:]), (a_, a_[:]), ALU.mult)
                sb_, fn = tmp, (lambda c: tmp[:, c, :])
            else:
                sb_, fn = srcs[q]
            pb_ = pM[q % 2]
            for c in range(4):
                k.tr((pb_, pb_[:, c * 128:(c + 1) * 128]), (sb_, fn(c)), (identf, identf[:]))
            k.copy("act", (stg, stg[:]), (pb_, pb_[:, :]))
            k.dma("sp", rsc[q], stg[:], reads=[stg], writes=[(rsc, q)])
        for blk, (c0, n) in enumerate(((0, 512), (512, 512), (1024, 512), (1536, 256))):
            proj_tm(pR[blk % 2], pR[blk % 2][:, 0:n], C_R + c0, n)
            k.copy("act", (stg, stg[:, 0:n]), (pR[blk % 2], pR[blk % 2][:, 0:n]))
            for b in range(NSB):
                k.dma("sp", o_sshift[b:b + 1, c0:c0 + n], stg[ST * b + ST - 1:ST * b + ST, 0:n], reads=[stg], writes=[o_sshift])
        k.barrier()
        k.bot = mark_rw
        vec6 = k.sbuf([128, 6, ST, RN], F32, "vec6")
        Ssb = k.sbuf([128, RN, RN], F32, "Ssb")
        tmpS = k.sbuf([128, RN, RN], F32, "tmpS")
        sa = k.sbuf([128, RN], F32, "sa")
        ys = k.sbuf([128, ST, RN], F32, "ys")
        Ytm = k.sbuf([128, RW], F32, "Ytm")
        k.dma("sp", Ssb[:].rearrange("p a b -> p (a b)"), st_rS[:, :], reads=[st_rS], writes=[Ssb])
        for q in range(6):
            for b in range(NSB):
                k.dma("sp", vec6[RH * b:RH * b + RH, q, :, :], rsc[q, ST * b:ST * b + ST, :].rearrange("t (h j) -> h t j", h=RH),
                      reads=[(rsc, q)], writes=[(vec6, q)])
        HV = RN // 2

        def rec_gen(hf):
            i0 = hf * HV
            S_ = (Ssb, Ssb[:, i0:i0 + HV, :], hf)
            T_ = (tmpS, tmpS[:, i0:i0 + HV, :], hf)
            bc = lambda q, t: (vec6, vec6[:, q, t, :].unsqueeze(1).to_broadcast([128, HV, RN]), q)
            for t in range(ST):
                k.tt("dve", T_, S_, bc(4, t), ALU.mult)
                yield
                k.red("dve", (sa, sa[:, i0:i0 + HV], hf), T_, ALU.add)
                yield
                k.tt("pool", S_, S_, bc(1, t), ALU.mult)
                yield
                k.tt("dve", T_, (sa, sa[:, i0:i0 + HV].unsqueeze(2).to_broadcast([128, HV, RN]), hf), bc(5, t), ALU.mult)
                yield
                k.tt("dve", S_, S_, T_, ALU.add)
                yield
                k.tt("pool", T_, (vec6, vec6[:, 3, t, i0:i0 + HV].unsqueeze(2).to_broadcast([128, HV, RN]), 3), bc(2, t), ALU.mult)
                yield
                k.tt("dve", S_, S_, T_, ALU.add)
                yield
                k.tt("pool", T_, S_, bc(0, t), ALU.mult)
                yield
                k.red("dve", (ys, ys[:, t, i0:i0 + HV], (hf, t)), T_, ALU.add)
                yield

        gens_ = [rec_gen(0), rec_gen(1)]
        while gens_:
            for g_ in list(gens_):
                try:
                    next(g_)
                except StopIteration:
                    gens_.remove(g_)
        k.dma("sp", o_sS[:, :], Ssb[:].rearrange("p a b -> p (a b)"), reads=[Ssb], writes=[o_sS])
        k.dma("sp", ysc[:, :], ys[:].rearrange("p a b -> p (a b)"), reads=[ys], writes=[ysc])
        for b in range(NSB):
            k.dma("sp", Ytm[ST * b:ST * b + ST, :].rearrange("t (h i) -> t h i", h=RH),
                  ysc[RH * b:RH * b + RH, :].rearrange("h (t i) -> t h i", t=ST), reads=[ysc], writes=[Ytm])
        rwkv_epilogue(ti, Ytm, rt[7])

    def tail_rows(ti):
        if ti != NTP - 1:
            return
        for blk, (c0, n) in enumerate(((0, 512), (512, 512), (1024, 512), (1536, 256))):
            proj_tm(pR[blk % 2], pR[blk % 2][:, 0:n], C_R + c0, n)
            k.copy("act", (rt[1], rt[1][96:128, :, :].rearrange("p a b -> p (a b)")[:, 0:n]), (pR[blk % 2], pR[blk % 2][96:128, 0:n]))
            k.dma("sp", o_pshift[0:1, c0:c0 + n], rt[1][127:128, :, :].rearrange("p a b -> p (a b)")[:, 0:n], reads=[rt[1]], writes=[o_pshift])

    tiles_all = list(range(NT)) if stage >= 5 else list(range(NTP))

    def phase_1b(k):
        k.barrier()
        k.bot = mark_1a
        W_g = k.sbuf([128, 8, 2048], BF16, "W_g")
        W_bm = k.sbuf([128, 4, D], BF16, "W_bm")
        W_br = k.sbuf([128, 4, D], BF16, "W_br")
        W_out = k.sbuf([128, 8, D], BF16, "W_out")
        k.dma("pool", W_bm[:], d_w_bm[:], reads=[d_w_bm], writes=[W_bm])
        for kh in range(2):
            k.dma("pool", W_g[:, 4 * kh:4 * kh + 4, 0:1024], d_w_in[:, 4 * kh:4 * kh + 4, C_G:C_G + 1024], reads=[d_w_in], writes=[(W_g, "a")])
        k.dma("pool", W_br[:], d_w_br[:], reads=[d_w_br], writes=[W_br])
        for kh in range(2):
            k.dma("pool", W_g[:, 4 * kh:4 * kh + 4, 1024:2048], d_w_in[:, 4 * kh:4 * kh + 4, C_G + 1024:C_G + 2048], reads=[d_w_in], writes=[(W_g, "b")])
        k.dma("pool", W_out[:], d_w_out[:], reads=[d_w_out], writes=[W_out])
        xtb = [k.sbuf([128, D], F32, f"xtb{i}") for i in range(2)]
        hb2 = k.sbuf([128, D], BF16, "hb2")
        hT2 = k.sbuf([128, 8, 128], BF16, "hT2")
        ss2 = k.sbuf([128, 1], F32, "ss2")
        rs2 = k.sbuf([128, 1], F32, "rs2")
        sgb = [k.sbuf([128, 512], F32, f"sgb{i}") for i in range(2)]
        yab = k.sbuf([128, D], F32, "yab")
        mg = [k.sbuf([128, D], BF16, f"mg{i}") for i in range(2)]
        mT = k.sbuf([128, 8, 128], BF16, "mT")
        def head_gen(ti):
            xb = xtb[ti % 2]
            xd, xap = x_rows(ti)
            k.dma("sp", xb[:], xap, reads=[xd], writes=[xb])
            norm_generic(xb, g1bc, hb2, hT2, ss2, rs2)
            yield
            for half, (Wb, src, key) in enumerate(((W_bm, hmT_all, "a"), (W_br, yrgT_all, "b"))):
                for blk in range(2):
                    for kc in range(4):
                        k.mm((pR[blk], pR[blk][:, :]), (src, src[:, ti, kc, :], ti), (Wb, Wb[:, kc, blk * 512:(blk + 1) * 512]),
                             start=(kc == 0), stop=(kc == 3))
                    col = half * 1024 + blk * 512
                    for kc in range(8):
                        k.mm((pF[blk], pF[blk][:, :]), (hT2, hT2[:, kc, :]), (W_g, W_g[:, kc, col:col + 512], key),
                             start=(kc == 0), stop=(kc == 7))
                    k.act((sgb[blk], sgb[blk][:]), (pF[blk], pF[blk][:, :]), AF.Sigmoid)
                    if half == 0:
                        k.tt("dve", (yab, yab[:, blk * 512:(blk + 1) * 512]), (sgb[blk], sgb[blk][:]), (pR[blk], pR[blk][:, :]), ALU.mult)
                    else:
                        k.tt("dve", (sgb[blk], sgb[blk][:]), (sgb[blk], sgb[blk][:]), (pR[blk], pR[blk][:, :]), ALU.mult)
                        k.tt("dve", (mg[ti % 2], mg[ti % 2][:, blk * 512:(blk + 1) * 512]), (sgb[blk], sgb[blk][:]), (yab, yab[:, blk * 512:(blk + 1) * 512]), ALU.add)
                    yield

        def tailb_gen(ti):
            xb = xtb[ti % 2]
            mgt = mg[ti % 2]
            for kc in range(8):
                k.tr((pT, pT[:, kc * 128:(kc + 1) * 128]), (mgt, mgt[:, kc * 128:(kc + 1) * 128]), (identb, identb[:]))
            k.copy("act", (mT, mT[:].rearrange("p a b -> p (a b)")), (pT, pT[:, :]))
            yield
            for blk in range(2):
                for kc in range(8):
                    k.mm((pM[blk], pM[blk][:, :]), (mT, mT[:, kc, :]), (W_out, W_out[:, kc, blk * 512:(blk + 1) * 512]),
                         start=(kc == 0), stop=(kc == 7))
                k.tt("dve", (xb, xb[:, blk * 512:(blk + 1) * 512]), (xb, xb[:, blk * 512:(blk + 1) * 512]), (pM[blk], pM[blk][:, :]), ALU.add)
                yield
            k.dma("sp", x1s[ti * 128:(ti + 1) * 128, :], xb[:], reads=[xb], writes=[(x1s, ti)])

        def rr(gens):
            gens = list(gens)
            while gens:
                for g_ in list(gens):
                    try:
                        next(g_)
                    except StopIteration:
                        gens.remove(g_)

        tlb = list(tiles_all)
        rr([head_gen(tlb[0])])
        for idx, ti in enumerate(tlb):
            gl = [tailb_gen(ti)]
            if idx + 1 < len(tlb):
                gl.append(head_gen(tlb[idx + 1]))
            rr(gl)

    def norm_generic(xbuf, gbc, hb_, hT_, ss_, rs_):
        k.act((hb_, hb_[:]), (xbuf, xbuf[:]), AF.Square, accum=(ss_, ss_[:]))
        k.ts("dve", (rs_, rs_[:]), (ss_, ss_[:]), 1.0 / D, EPS, op0=ALU.mult, op1=ALU.add)
        k.act((rs_, rs_[:]), (rs_, rs_[:]), AF.Ln)
        k.act((rs_, rs_[:]), (rs_, rs_[:]), AF.Exp, scale=-0.5)
        k.stt("dve", (hb_, hb_[:]), (xbuf, xbuf[:]), (rs_, rs_[:, 0:1]), (gbc, gbc[:]), ALU.mult, ALU.mult)
        for kc in range(8):
            k.tr((pT, pT[:, kc * 128:(kc + 1) * 128]), (hb_, hb_[:, kc * 128:(kc + 1) * 128]), (identb, identb[:]))
        k.copy("act", (hT_, hT_[:].rearrange("p a b -> p (a b)")), (pT, pT[:, :]))

    def phase_2(k):
        k.barrier()
        k.bot = mark_phase
        F_up = k.sbuf([128, 8, 2 * DFF], BF16, "F_up")
        F_dn = k.sbuf([128, NFC, D], BF16, "F_dn")
        PGW = k.sbuf([128, 8, D], BF16, "PGW")
        PPJ = k.sbuf([128, 2, D], BF16, "PPJ")
        NG = 4
        CW = DFF // NG
        for g in range(NG):
            for part in range(2):
                k.dma("pool", F_up[:, :, part * DFF + g * CW:part * DFF + (g + 1) * CW], d_f_up[:, :, part * DFF + g * CW:part * DFF + (g + 1) * CW],
                      reads=[d_f_up], writes=[(F_up, g)])
        for g in range(2):
            k.dma("pool", F_dn[:, 11 * g:11 * g + 11, :], d_f_down[:, 11 * g:11 * g + 11, :], reads=[d_f_down], writes=[(F_dn, g)])
        k.dma("pool", PGW[:], d_pgw[:], reads=[d_pgw], writes=[PGW])
        k.dma("pool", PPJ[:], d_ppj[:], reads=[d_ppj], writes=[PPJ])
        g2bc = k.sbuf([128, D], F32, "g2bc")
        g3bc = k.sbuf([128, D], F32, "g3bc")
        g4bc = k.sbuf([128, D], F32, "g4bc")
        fct = k.sbuf([128, 4 * NFC], F32, "fct")
        k.dma("sp", g2bc[:], d_g2[:], reads=[d_g2], writes=[g2bc])
        k.dma("sp", g3bc[:], d_g3[:], reads=[d_g3], writes=[g3bc])
        k.dma("sp", g4bc[:], d_g4[:], reads=[d_g4], writes=[g4bc])
        k.dma("sp", fct[:], d_fctab[:], reads=[d_fctab], writes=[fct])
        fcol = lambda c: (fct, fct[:, c:c + 1])
        xq = [k.sbuf([128, D], F32, f"xq{i}") for i in range(2)]
        hb3 = k.sbuf([128, D], BF16, "hb3")
        hT3s = [k.sbuf([128, 8, 128], BF16, f"hT3a{i}") for i in range(2)]
        ss3 = k.sbuf([128, 1], F32, "ss3")
        rs3 = k.sbuf([128, 1], F32, "rs3")
        gT = k.sbuf([128, NFC, 128], BF16, "gT")
        cf = k.sbuf([128, NFC, 2], F32, "cf")
        GS = 4
        EXW = NSB * (ST + 2)
        ex4 = [k.sbuf([128, GS, EXW], F32, f"ex4_{i}") for i in range(2)]
        cc4 = [k.sbuf([128, GS, 128], F32, f"cc4_{i}") for i in range(2)]
        t14 = [k.sbuf([128, GS, 128], F32, "t14_0")] * 2
        up4 = [k.sbuf([128, GS, 128], F32, f"up4_{i}") for i in range(2)]
        sg3 = [k.sbuf([128, 512], F32, "sg30")] * 2
        ppt = k.sbuf([128, PLE], F32, "ppt")
        ppb = k.sbuf([128, PLE], BF16, "ppb")
        peT = k.sbuf([128, 2, 128], BF16, "peT")
        utm = sg3[0]
        k.memset("pool", (cf, cf[:]), 0.0)
        cfs = k.sbuf([128, NFC, 2 * NSB], F32, "cfs")
        GC = 1.5957691216057308
        BLK6 = ((0, 512), (512, 512), (1024, 512), (1536, 512), (2048, 512), (2560, 256))
        groups = [list(range(g0, min(g0 + GS, NFC))) for g0 in range(0, NFC, GS)]
        gbank = [pF[0], pF[1]]
        ubank = [pM[0], pM[1]]

        def fup_w(col):
            g = col // CW
            g_hi = (col + 127) // CW
            return g, g_hi

        def stage_A(ti, gi, hT3):
            smp = ti == NTP
            p = gi % 2
            chunks = groups[gi]
            n = len(chunks)
            c0 = chunks[0]
            for part, bank in ((0, gbank[p]), (1, ubank[p])):
                for ci, c in enumerate(chunks):
                    g, g_hi = fup_w(c * 128)
                    for kc in range(8):
                        k.mm((bank, bank[:, ci * 128:(ci + 1) * 128]), (F_up, F_up[:, kc, part * DFF + c * 128:part * DFF + (c + 1) * 128], g),
                             (hT3, hT3[:, kc, :]), start=(kc == 0), stop=(kc == 7))
                        if g_hi != g and g_hi in F_up.subs and F_up.subs[g_hi].w is not None:
                            k.streams["pe"][-1].deps.add(F_up.subs[g_hi].w)
            ex = ex4[p]
            if not smp:
                k.copy("pool", (ex, ex[:, 0:n, 0:2]), (cf, cf[:, c0:c0 + n, :]))
                k.copy("act", (ex, ex[:, 0:n, 2:130]), (gbank[p], gbank[p][:, 0:n * 128].rearrange("p (c t) -> p c t", c=n)))
                k.copy("pool", (cf, cf[:, c0:c0 + n, :]), (ex, ex[:, 0:n, 128:130]))
            else:
                exs = ex[:, 0:n, :].rearrange("p c (b t) -> p c b t", t=ST + 2)
                k.copy("pool", (ex, exs[:, :, :, 0:2]), (cfs, cfs[:, c0:c0 + n, :].rearrange("p c (b j) -> p c b j", j=2)))
                k.copy("act", (ex, exs[:, :, :, 2:ST + 2]), (gbank[p], gbank[p][:, 0:n * 128].rearrange("p (c b t) -> p c b t", c=n, b=NSB)))
            k.copy("act", (up4[p], up4[p][:, 0:n, :]), (ubank[p], ubank[p][:, 0:n * 128].rearrange("p (c t) -> p c t", c=n)))
            for ci, c in enumerate(chunks):
                if not smp:
                    tap = lambda j: ex[:, ci, j:j + 128]
                    ccv = cc4[p][:, ci, :]
                else:
                    e3 = ex[:, ci, :].rearrange("p (b t) -> p b t", t=ST + 2)
                    tap = lambda j, e3=e3: e3[:, :, j:j + ST]
                    ccv = cc4[p][:, ci, :].rearrange("p (b t) -> p b t", t=ST)
                cb = cc4[p]
                k.ts("dve", (cb, ccv), (ex, tap(2)), fcol(2 * NFC + c), fcol(3 * NFC + c), op0=ALU.mult, op1=ALU.add)
                k.stt("dve", (cb, ccv), (ex, tap(1)), fcol(1 * NFC + c), (cb, ccv), ALU.mult, ALU.add)
                k.stt("dve", (cb, ccv), (ex, tap(0)), fcol(0 * NFC + c), (cb, ccv), ALU.mult, ALU.add)

        def stage_B(ti, gi):
            p = gi % 2
            chunks = groups[gi]
            n = len(chunks)
            c0 = chunks[0]
            cb = (cc4[p], cc4[p][:, 0:n, :])
            ta = (t14[p], t14[p][:, 0:n, :])
            k.tt("dve", ta, cb, cb, ALU.mult)
            k.ts("dve", ta, ta, 0.044715, 1.0, op0=ALU.mult, op1=ALU.add)
            k.tt("dve", ta, ta, cb, ALU.mult)
            k.act(ta, ta, AF.Sigmoid, scale=GC)
            k.tt("dve", ta, ta, cb, ALU.mult)
            k.tt("dve", (gT, gT[:, c0:c0 + n, :], ("g", gi)), ta, (up4[p], up4[p][:, 0:n, :]), ALU.mult)

        def load_norm2(ti):
            xb = xq[ti % 2]
            k.dma("sp", xb[:], x1s[ti * 128:(ti + 1) * 128, :], reads=[(x1s, ti)], writes=[xb])
            if ti == NTP:
                for b6, (c0, n) in enumerate(BLK6):
                    k.dma("sp", utm[0:2 * NSB, 0:n], st_fconv[:, c0:c0 + n], reads=[st_fconv], writes=[utm])
                    nch = n // 128
                    for ci in range(nch):
                        k.tr((pR[b6 % 2], pR[b6 % 2][:, ci * 32:(ci + 1) * 32]), (utm, utm[0:2 * NSB, ci * 128:(ci + 1) * 128]),
                             (identf, identf[0:2 * NSB, 0:2 * NSB]))
                    k.copy("act", (cfs, cfs[:, 4 * b6:4 * b6 + nch, :]), (pR[b6 % 2], pR[b6 % 2][:, 0:nch * 32].rearrange("p (c x) -> p c x", c=nch)))
            norm_generic(xb, g2bc, hb3, hT3s[ti % 2], ss3, rs3)

        hb3b = hb3

        def groups_gen(ti):
            hT3 = hT3s[ti % 2]
            for gi in range(len(groups) + 1):
                if gi < len(groups):
                    stage_A(ti, gi, hT3)
                    yield
                if gi >= 1:
                    stage_B(ti, gi - 1)
                    yield

        def tail_gen(ti):
            smp = ti == NTP
            xb = xq[ti % 2]
            hT3 = hT3s[ti % 2]
            for blk in range(2):
                for c in range(NFC):
                    k.mm((pR[blk], pR[blk][:, :]), (gT, gT[:, c, :], ("g", c // GS)), (F_dn, F_dn[:, c, blk * 512:(blk + 1) * 512], c // 11),
                         start=(c == 0), stop=(c == NFC - 1))
                k.tt("dve", (xb, xb[:, blk * 512:(blk + 1) * 512]), (xb, xb[:, blk * 512:(blk + 1) * 512]), (pR[blk], pR[blk][:, :]), ALU.add)
                yield
            if ti == NTP - 1 or smp:
                for b6, (c0, n) in enumerate(BLK6):
                    for kc in range(8):
                        k.mm((pM[2], pM[2][:, 0:n]), (hT3, hT3[:, kc, :]), (F_up, F_up[:, kc, c0:c0 + n]),
                             start=(kc == 0), stop=(kc == 7))
                    if not smp:
                        k.copy("act", (utm, utm[96:128, 0:n]), (pM[2], pM[2][96:128, 0:n]))
                        k.dma("sp", o_pfconv[:, c0:c0 + n], utm[126:128, 0:n], reads=[utm], writes=[o_pfconv])
                    else:
                        k.copy("act", (utm, utm[:, 0:n]), (pM[2], pM[2][:, 0:n]))
                        for b in range(NSB):
                            k.dma("sp", o_sfconv[2 * b:2 * b + 2, c0:c0 + n], utm[ST * b + ST - 2:ST * b + ST, 0:n], reads=[utm], writes=[o_sfconv])
                    yield
            norm_generic(xb, g3bc, hb3b, hT3, ss3, rs3)
            yield
            pd, pap = (pp, pp[ti * 128:(ti + 1) * 128, :]) if not smp else (psm, psm[:, :])
            k.dma("sp", ppt[:], pap, reads=[pd], writes=[ppt])
            k.copy("act", (ppb, ppb[:]), (ppt, ppt[:]))
            for kc in range(2):
                k.tr((pT, pT[:, kc * 128:(kc + 1) * 128]), (ppb, ppb[:, kc * 128:(kc + 1) * 128]), (identb, identb[:]))
            k.copy("act", (peT, peT[:].rearrange("p a b -> p (a b)")), (pT, pT[:, 0:256]))
            yield
            for blk in range(2):
                for kc in range(8):
                    k.mm((pR[blk], pR[blk][:, :]), (hT3, hT3[:, kc, :]), (PGW, PGW[:, kc, blk * 512:(blk + 1) * 512]),
                         start=(kc == 0), stop=(kc == 7))
                yield
                k.act((sg3[blk], sg3[blk][:]), (pR[blk], pR[blk][:, :]), AF.Sigmoid)
                for kc in range(2):
                    k.mm((pM[2], pM[2][:, :]), (peT, peT[:, kc, :]), (PPJ, PPJ[:, kc, blk * 512:(blk + 1) * 512]),
                         start=(kc == 0), stop=(kc == 1))
                k.tt("dve", (sg3[blk], sg3[blk][:]), (sg3[blk], sg3[blk][:]), (pM[2], pM[2][:, :]), ALU.mult)
                k.tt("dve", (xb, xb[:, blk * 512:(blk + 1) * 512]), (xb, xb[:, blk * 512:(blk + 1) * 512]), (sg3[blk], sg3[blk][:]), ALU.add)
                yield
            k.act((hb3b, hb3b[:]), (xb, xb[:]), AF.Square, accum=(ss3, ss3[:]))
            k.ts("dve", (rs3, rs3[:]), (ss3, ss3[:]), 1.0 / D, EPS, op0=ALU.mult, op1=ALU.add)
            k.act((rs3, rs3[:]), (rs3, rs3[:]), AF.Ln)
            k.act((rs3, rs3[:]), (rs3, rs3[:]), AF.Exp, scale=-0.5)
            yield
            k.stt("dve", (xb, xb[:]), (xb, xb[:]), (rs3, rs3[:, 0:1]), (g4bc, g4bc[:]), ALU.mult, ALU.mult)
            if not smp:
                k.dma("sp", y_p[ti * 128:(ti + 1) * 128, :], xb[:], reads=[xb], writes=[y_p])
            else:
                k.dma("sp", y_s[:, :], xb[:], reads=[xb], writes=[y_s])
            if ti in nxt2:
                yield
                load_norm2(nxt2[ti])

        def run_rr(gens):
            gens = list(gens)
            while gens:
                for g_ in list(gens):
                    try:
                        next(g_)
                    except StopIteration:
                        gens.remove(g_)

        tl = list(tiles_all)
        nxt2 = {tl[i]: tl[i + 2] for i in range(len(tl) - 2)}
        load_norm2(tl[0])
        if len(tl) > 1:
            load_norm2(tl[1])
        run_rr([groups_gen(tl[0])])
        for idx, ti in enumerate(tl):
            tg = tail_gen(ti)
            if idx + 1 < len(tl):
                gg = groups_gen(tl[idx + 1])
                next(gg)
                next(gg)
                next(tg)
                next(tg)
                run_rr([gg, tg])
            else:
                run_rr([tg])

    _nt_dbg = int(_os.environ.get("KDBG_NT", "0"))
    tiles_run = (tiles_all if not _nt_dbg else list(range(_nt_dbg)))
    for ti in tiles_run:
        mixer_tile(ti)
        if stage >= 2:
            rwkv_tile(ti)
            tail_rows(ti)

    if stage >= 3:
        phase_1b(k)
    if stage >= 4:
        phase_2(k)
    k.emit()
    k.stats["sbuf_hiwater"] = k.hiwater
    k.stats["arena_bytes"] = k.arena_bytes
    return nc, k


def _chunk_rows(w, nk):
    return np.ascontiguousarray(w.reshape(nk, 128, w.shape[1]).transpose(1, 0, 2))


def _pcols(v, nc_):
    return v.reshape(nc_, 128).T


_PROG = {}


def _get_prog(stage=99, dbg=False):
    key = (stage, dbg)
    if key not in _PROG:
        _PROG[key] = build_program(stage, dbg)
    return _PROG[key]


def make_in_maps(inp):
    f = lambda a: np.ascontiguousarray(np.asarray(a, dtype=np.float32))
    ptab = np.zeros((128, 128), np.float32)
    mcw = f(inp["m_conv_w"])[0]
    for j in range(4):
        ptab[:, j * 8:(j + 1) * 8] = _pcols(mcw[j], 8)
    ptab[:, 32:40] = _pcols(f(inp["m_conv_b"])[0], 8)
    ptab[:, 40:54] = _pcols(f(inp["r_mix"])[0], 14)
    ptab[:, 54:58] = _pcols(f(inp["r_w0"])[0], 4)
    ptab[:, 58:62] = _pcols(f(inp["r_a0"])[0], 4)
    ptab[:, 62:66] = _pcols(f(inp["r_kk"])[0], 4)
    ptab[:, 66:70] = _pcols(f(inp["r_ka"])[0], 4)
    ptab[:, 70:74] = _pcols(f(inp["r_rk"])[0].reshape(-1), 4)
    ptab[:, 74:78] = _pcols(f(inp["r_ln_g"])[0], 4)
    ptab[:, 78:82] = _pcols(f(inp["r_ln_b"])[0], 4)
    ptab[:, 82:86] = _pcols(f(inp["m_norm_g"])[0], 4)
    fct = np.zeros((128, 4 * NFC), np.float32)
    fcw = f(inp["f_conv_w"])[0]
    for j in range(3):
        fct[:, j * NFC:(j + 1) * NFC] = _pcols(fcw[j], NFC)
    fct[:, 3 * NFC:4 * NFC] = _pcols(f(inp["f_conv_b"])[0], NFC)
    gbias = np.stack([f(inp["m_i_bias"])[0], f(inp["m_f_bias"])[0]], axis=1)
    ra2 = np.zeros((128, RW), np.float32)
    ra2[64:128] = f(inp["r_a2"])[0]
    bc = lambda v: np.ascontiguousarray(np.broadcast_to(f(v).reshape(1, D), (128, D)))
    shared = {
        "w_in": _chunk_rows(f(inp["w_in"])[0], 8),
        "w_bm": _chunk_rows(f(inp["w_branch_m"])[0], 4),
        "w_br": _chunk_rows(f(inp["w_branch_r"])[0], 4),
        "w_out": _chunk_rows(f(inp["w_out"])[0], 8),
        "f_up": _chunk_rows(f(inp["f_up"])[0], 8),
        "f_down": _chunk_rows(f(inp["f_down"])[0], NFC),
        "ple_gate_w": _chunk_rows(f(inp["ple_gate_w"])[0], 8),
        "ple_proj": _chunk_rows(f(inp["ple_proj"])[0], 2),
        "r_w2": f(inp["r_w2"])[0], "r_a2": ra2, "r_g2": f(inp["r_g2"])[0],
        "norm1_g": bc(inp["norm1_g"]), "norm2_g": bc(inp["norm2_g"]),
        "ple_norm_g": bc(inp["ple_norm_g"]), "final_norm_g": bc(inp["final_norm_g"]),
        "ptab": ptab, "fctab": fct, "gate_bias": np.ascontiguousarray(gbias),
    }
    maps = []
    for c in range(NCORES):
        sl = slice(c * NSB, (c + 1) * NSB)
        m = dict(shared)
        m["xp"] = f(inp["x_prompt"][c])
        m["xs"] = f(inp["x_sample"][sl]).reshape(128, D)
        m["pp"] = f(inp["p_prompt"][0, c])
        m["psm"] = f(inp["p_sample"][0, sl]).reshape(128, PLE)
        m["st_mconv"] = f(inp["state_mlstm_conv"][0, sl]).reshape(NSB * 3, 2 * MW)
        m["st_mC"] = f(inp["state_mlstm_C"][0, sl])
        m["st_mn"] = f(inp["state_mlstm_n"][0, sl])
        m["st_mm"] = f(inp["state_mlstm_m"][0, sl])
        m["st_rshift"] = f(inp["state_rwkv_shift"][0, sl])
        m["st_rS"] = f(inp["state_rwkv_S"][0, sl]).reshape(NSB * RH, RN * RN)
        m["st_fconv"] = f(inp["state_ffn_conv"][0, sl]).reshape(NSB * 2, DFF)
        maps.append(m)
    return maps


def assemble(results):
    g = lambda name: [np.asarray(r[name], dtype=np.float32) for r in results]
    y_p = np.stack(g("y_p"), 0)
    y_s = np.concatenate([a.reshape(NSB, ST, D) for a in g("y_s")], 0)
    p_conv = np.stack(g("p_conv"), 0)[None]
    p_C = np.stack(g("p_C"), 0)[None]
    p_n = np.stack(g("p_n"), 0)[None]
    p_m = np.stack([a.reshape(MH) for a in g("p_m")], 0)[None]
    p_shift = np.stack([a.reshape(RCOLS) for a in g("p_shift")], 0)[None]
    p_S = np.stack([a.reshape(RH, RN, RN) for a in g("p_S")], 0)[None]
    p_fconv = np.stack(g("p_fconv"), 0)[None]
    s_conv = np.concatenate([a.reshape(NSB, 3, 2 * MW) for a in g("s_conv")], 0)[None]
    s_C = np.concatenate(g("s_C"), 0)[None]
    s_n = np.concatenate(g("s_n"), 0)[None]
    s_m = np.concatenate(g("s_m"), 0)[None]
    s_shift = np.concatenate(g("s_shift"), 0)[None]
    s_S = np.concatenate([a.reshape(NSB, RH, RN, RN) for a in g("s_S")], 0)[None]
    s_fconv = np.concatenate([a.reshape(NSB, 2, DFF) for a in g("s_fconv")], 0)[None]
    return (y_p, y_s, p_conv, p_C, p_n, p_m, p_shift, p_S, p_fconv,
            s_conv, s_C, s_n, s_m, s_shift, s_S, s_fconv)


def kernel(**inputs):
    nc, _ = _get_prog()
    maps = make_in_maps(inputs)
    res = run_bass_kernel_spmd(nc, maps, core_ids=list(range(NCORES)))
    return assemble(res.results)
```
